# Optimizing a Trainium2 kernel written in Bass

```python
import math
import jax, jax.numpy as jnp
from jax import lax
import numpy as np

D_MODEL = 1024
BATCH = 2
SEQ = 8192
DEPTH = 2

D_MIX = D_MODEL
HEAD_DIM = 64
CONV_CH = D_MODEL // 4
MOBA_HEADS = (3 * D_MODEL // 8) // HEAD_DIM
MOBA_W = MOBA_HEADS * HEAD_DIM
GDN_W = D_MIX - CONV_CH - MOBA_W
GDN_HEADS = GDN_W // HEAD_DIM
CONV_K = 31
MOBA_BLOCK = 256
MOBA_TOPK = 3
Q_BLOCK = 64
ROPE_THETA = 500000.0
ROPE_DIM = HEAD_DIM // 4
GDN_CONV_K = 4
GDN_CHUNK = 64
D_FF = ((8 * D_MODEL // 3 + 255) // 256) * 256
N_MOD = 9
EPS = 1e-6
IN_CONV = 2 * CONV_CH
IN_MOBA = 3 * MOBA_W
IN_GDN = 4 * GDN_W + 2 * GDN_HEADS
D_IN = IN_CONV + IN_MOBA + IN_GDN

kernel_name = "hymba_conformer_moba_gdn_trunk"

F32 = jnp.float32


def rmsnorm(x, g):
    xf = x.astype(F32)
    y = xf * lax.rsqrt(jnp.mean(xf * xf, axis=-1, keepdims=True) + EPS)
    return (y * g.astype(F32)).astype(x.dtype)


def modulate(x, g, shift, scale):
    return rmsnorm(x, g) * (1 + scale[:, None, :]) + shift[:, None, :]


def swiglu(h, w_gate, w_up, w_down):
    return (jax.nn.silu(h @ w_gate) * (h @ w_up)) @ w_down


def causal_depthwise_conv(x, w):
    K, C = w.shape
    xp = jnp.pad(x, ((0, 0), (K - 1, 0), (0, 0)))
    return lax.conv_general_dilated(xp, w[:, None, :].astype(x.dtype), window_strides=(1,), padding='VALID',
                                    dimension_numbers=('NWC', 'WIO', 'NWC'), feature_group_count=C)


def conformer_conv(u, w_dw, b_dw, ln_g, ln_b):
    a, gate = jnp.split(u, 2, axis=-1)
    h = a * jax.nn.sigmoid(gate)
    h = (causal_depthwise_conv(h, w_dw) + b_dw).astype(F32)
    mu = jnp.mean(h, axis=-1, keepdims=True)
    var = jnp.mean(jnp.square(h - mu), axis=-1, keepdims=True)
    h = (h - mu) * lax.rsqrt(var + EPS) * ln_g.astype(F32) + ln_b.astype(F32)
    return jax.nn.silu(h).astype(u.dtype)


def partial_rope(x, pos):
    half = ROPE_DIM // 2
    inv = jnp.exp(-math.log(ROPE_THETA) * jnp.arange(0, ROPE_DIM, 2, dtype=F32) / ROPE_DIM)
    ang = pos.astype(F32)[:, None] * inv[None, :]
    cos = jnp.cos(ang)[None, :, None, :]
    sin = jnp.sin(ang)[None, :, None, :]
    xr = x[..., :ROPE_DIM].astype(F32)
    x1, x2 = xr[..., :half], xr[..., half:]
    rot = jnp.concatenate([x1 * cos - x2 * sin, x2 * cos + x1 * sin], axis=-1)
    return jnp.concatenate([rot.astype(x.dtype), x[..., ROPE_DIM:]], axis=-1)


def moba_attention(q, k, v):
    B, S, H, dh = q.shape
    nb = -(-S // MOBA_BLOCK)
    s_pad = nb * MOBA_BLOCK
    q = q.transpose(0, 2, 1, 3)
    pad = ((0, 0), (0, 0), (0, s_pad - S), (0, 0))
    kb = jnp.pad(k.transpose(0, 2, 1, 3), pad).reshape(B, H, nb, MOBA_BLOCK, dh)
    vb = jnp.pad(v.transpose(0, 2, 1, 3), pad).reshape(B, H, nb, MOBA_BLOCK, dh)
    k_mean = jnp.mean(kb.astype(F32), axis=3)
    gate_s = jnp.einsum('bhsd,bhnd->bhsn', q.astype(F32), k_mean)
    q_blk = jnp.arange(S) // MOBA_BLOCK
    past = jnp.arange(nb)[None, :] < q_blk[:, None]
    gate_s = jnp.where(past, gate_s, -jnp.inf)
    topk = min(MOBA_TOPK, nb)
    _, sel = lax.top_k(gate_s, topk)
    sel_valid = sel < q_blk[:, None]
    scale = dh ** -0.5
    bidx = jnp.arange(B)[:, None, None, None]
    hidx = jnp.arange(H)[None, :, None, None]

    def one_block(i):
        start = i * Q_BLOCK
        qi = lax.dynamic_slice_in_dim(q, start, Q_BLOCK, axis=2)
        seli = lax.dynamic_slice_in_dim(sel, start, Q_BLOCK, axis=2)
        vali = lax.dynamic_slice_in_dim(sel_valid, start, Q_BLOCK, axis=2)
        ksel = kb[bidx, hidx, seli]
        vsel = vb[bidx, hidx, seli]
        own = start // MOBA_BLOCK
        kown = lax.dynamic_index_in_dim(kb, own, axis=2, keepdims=False)
        vown = lax.dynamic_index_in_dim(vb, own, axis=2, keepdims=False)
        s_sel = jnp.einsum('bhqd,bhqnkd->bhqnk', qi, ksel, preferred_element_type=F32) * scale
        s_sel = jnp.where(vali[..., None], s_sel, -jnp.inf).reshape(B, H, Q_BLOCK, topk * MOBA_BLOCK)
        s_own = jnp.einsum('bhqd,bhkd->bhqk', qi, kown, preferred_element_type=F32) * scale
        qpos = start + jnp.arange(Q_BLOCK)
        kpos = own * MOBA_BLOCK + jnp.arange(MOBA_BLOCK)
        s_own = jnp.where(kpos[None, :] <= qpos[:, None], s_own, -jnp.inf)
        p = jax.nn.softmax(jnp.concatenate([s_sel, s_own], axis=-1), axis=-1)
        p_sel = p[..., :topk * MOBA_BLOCK].reshape(B, H, Q_BLOCK, topk, MOBA_BLOCK)
        p_own = p[..., topk * MOBA_BLOCK:]
        o = (jnp.einsum('bhqnk,bhqnkd->bhqd', p_sel, vsel.astype(F32))
             + jnp.einsum('bhqk,bhkd->bhqd', p_own, vown.astype(F32)))
        return o.astype(q.dtype)

    out = lax.map(one_block, jnp.arange(S // Q_BLOCK))
    return out.transpose(1, 0, 3, 2, 4).reshape(B, S, H * dh)


def l2norm(x):
    return x * lax.rsqrt(jnp.sum(x * x, axis=-1, keepdims=True) + EPS)


def gated_delta_net(q, k, v, g, beta):
    B, S, H, dk = q.shape
    dv = v.shape[-1]
    C = GDN_CHUNK
    N = S // C
    q = q * dk ** -0.5
    tr = lambda t: t.transpose(0, 2, 1, 3).reshape(B, H, N, C, t.shape[-1])
    q, k, v = tr(q), tr(k), tr(v)
    g = jnp.cumsum(g.transpose(0, 2, 1).reshape(B, H, N, C), axis=-1)
    beta = beta.transpose(0, 2, 1).reshape(B, H, N, C)
    k_beta = k * beta[..., None]
    v_beta = v * beta[..., None]
    tril = jnp.tril(jnp.ones((C, C), dtype=bool))
    strict = jnp.tril(jnp.ones((C, C), dtype=bool), -1)
    decay = jnp.exp(jnp.where(tril, g[..., :, None] - g[..., None, :], -jnp.inf))
    L = jnp.where(strict, jnp.einsum('bhncd,bhnsd->bhncs', k_beta, k) * decay, 0.0)
    eye = jnp.eye(C, dtype=F32)
    T = lax.linalg.triangular_solve(eye + L, jnp.broadcast_to(eye, L.shape), left_side=True, lower=True)
    u = T @ v_beta
    w = T @ (k_beta * jnp.exp(g)[..., None])
    qk = jnp.where(tril, jnp.einsum('bhncd,bhnsd->bhncs', q, k) * decay, 0.0)
    g_last = g[..., -1]
    k_dec = k * jnp.exp(g_last[..., None] - g)[..., None]
    q_dec = q * jnp.exp(g)[..., None]

    def step(state, xs):
        q_i, qk_i, k_i, u_i, w_i, gl_i = xs
        v_new = u_i - w_i @ state
        o = q_i @ state + qk_i @ v_new
        state = state * jnp.exp(gl_i)[..., None, None] + jnp.swapaxes(k_i, -1, -2) @ v_new
        return state, o

    mv = lambda t: jnp.moveaxis(t, 2, 0)
    xs = (mv(q_dec), mv(qk), mv(k_dec), mv(u), mv(w), mv(g_last))
    state0 = jnp.zeros((B, H, dk, dv), F32)
    _, o = lax.scan(step, state0, xs)
    return o.transpose(1, 0, 3, 2, 4).reshape(B, S, H, dv)


def token_mixing(h, w_in, conv_w, conv_b, conv_ln_g, conv_ln_b, gdn_conv_w, gdn_a_log, gdn_dt_bias, gdn_norm_g, w_out):
    B, S, _ = h.shape
    u = h @ w_in
    u_conv, u_moba, u_gdn = jnp.split(u, [IN_CONV, IN_CONV + IN_MOBA], axis=-1)
    y_conv = conformer_conv(u_conv, conv_w, conv_b, conv_ln_g, conv_ln_b)
    mq, mk, mv = [t.reshape(B, S, MOBA_HEADS, HEAD_DIM) for t in jnp.split(u_moba, 3, axis=-1)]
    pos = jnp.arange(S)
    y_moba = moba_attention(partial_rope(mq, pos), partial_rope(mk, pos), mv)
    qkv = u_gdn[..., :3 * GDN_W]
    z = u_gdn[..., 3 * GDN_W:4 * GDN_W].reshape(B, S, GDN_HEADS, HEAD_DIM).astype(F32)
    a = u_gdn[..., 4 * GDN_W:4 * GDN_W + GDN_HEADS].astype(F32)
    b = u_gdn[..., 4 * GDN_W + GDN_HEADS:].astype(F32)
    qkv = jax.nn.silu(causal_depthwise_conv(qkv, gdn_conv_w)).astype(F32)
    gq, gk, gv = [t.reshape(B, S, GDN_HEADS, HEAD_DIM) for t in jnp.split(qkv, 3, axis=-1)]
    beta = jax.nn.sigmoid(b)
    g = -jnp.exp(gdn_a_log.astype(F32)) * jax.nn.softplus(a + gdn_dt_bias.astype(F32))
    o = gated_delta_net(l2norm(gq), l2norm(gk), gv, g, beta)
    o = rmsnorm(o, gdn_norm_g) * jax.nn.silu(z)
    y_gdn = o.reshape(B, S, GDN_W).astype(h.dtype)
    y = jnp.concatenate([y_conv, y_moba, y_gdn], axis=-1)
    return y @ w_out


def setup_inputs(seed: int = 0) -> dict:
    key = jax.random.key(seed)
    ks = jax.random.split(key, 24)
    nrm = lambda k, shape, s: jax.random.normal(k, shape, F32) * s
    gain = lambda k, shape: 1.0 + 0.02 * jax.random.normal(k, shape, F32)
    dt = jnp.exp(jax.random.uniform(ks[16], (DEPTH, GDN_HEADS), F32) * (math.log(0.1) - math.log(0.001)) + math.log(0.001))
    return {
        "x": jax.random.normal(ks[0], (BATCH, SEQ, D_MODEL), F32),
        "c": jax.random.normal(ks[1], (BATCH, D_MODEL), F32),
        "w_ada": nrm(ks[2], (DEPTH, D_MODEL, N_MOD * D_MODEL), 0.5 * D_MODEL ** -0.5),
        "b_ada": nrm(ks[3], (DEPTH, N_MOD * D_MODEL), 0.02),
        "ln_ffn1_g": gain(ks[4], (DEPTH, D_MODEL)),
        "ffn1_w_gate": nrm(ks[5], (DEPTH, D_MODEL, D_FF), D_MODEL ** -0.5),
        "ffn1_w_up": nrm(ks[6], (DEPTH, D_MODEL, D_FF), D_MODEL ** -0.5),
        "ffn1_w_down": nrm(ks[7], (DEPTH, D_FF, D_MODEL), D_FF ** -0.5),
        "ln_mix_g": gain(ks[8], (DEPTH, D_MODEL)),
        "w_in": nrm(ks[9], (DEPTH, D_MODEL, D_IN), D_MODEL ** -0.5),
        "conv_w": nrm(ks[10], (DEPTH, CONV_K, CONV_CH), CONV_K ** -0.5),
        "conv_b": nrm(ks[11], (DEPTH, CONV_CH), 0.02),
        "conv_ln_g": gain(ks[12], (DEPTH, CONV_CH)),
        "conv_ln_b": nrm(ks[13], (DEPTH, CONV_CH), 0.02),
        "gdn_conv_w": nrm(ks[14], (DEPTH, GDN_CONV_K, 3 * GDN_W), GDN_CONV_K ** -0.5),
        "gdn_a_log": jnp.log(jax.random.uniform(ks[15], (DEPTH, GDN_HEADS), F32, 1.0, 16.0)),
        "gdn_dt_bias": dt + jnp.log(-jnp.expm1(-dt)),
        "gdn_norm_g": gain(ks[17], (DEPTH, HEAD_DIM)),
        "w_out": nrm(ks[18], (DEPTH, D_MIX, D_MODEL), D_MIX ** -0.5),
        "ln_ffn2_g": gain(ks[19], (DEPTH, D_MODEL)),
        "ffn2_w_gate": nrm(ks[20], (DEPTH, D_MODEL, D_FF), D_MODEL ** -0.5),
        "ffn2_w_up": nrm(ks[21], (DEPTH, D_MODEL, D_FF), D_MODEL ** -0.5),
        "ffn2_w_down": nrm(ks[22], (DEPTH, D_FF, D_MODEL), D_FF ** -0.5),
        "final_g": gain(ks[23], (D_MODEL,)),
    }


def reference(x, c, w_ada, b_ada, ln_ffn1_g, ffn1_w_gate, ffn1_w_up, ffn1_w_down, ln_mix_g, w_in, conv_w, conv_b,
              conv_ln_g, conv_ln_b, gdn_conv_w, gdn_a_log, gdn_dt_bias, gdn_norm_g, w_out, ln_ffn2_g,
              ffn2_w_gate, ffn2_w_up, ffn2_w_down, final_g):
    c_act = jax.nn.silu(c)
    for l in range(DEPTH):
        mod = c_act @ w_ada[l] + b_ada[l]
        sh1, sc1, gt1, sh2, sc2, gt2, sh3, sc3, gt3 = jnp.split(mod, N_MOD, axis=-1)
        h = modulate(x, ln_ffn1_g[l], sh1, sc1)
        x = x + 0.5 * gt1[:, None, :] * swiglu(h, ffn1_w_gate[l], ffn1_w_up[l], ffn1_w_down[l])
        h = modulate(x, ln_mix_g[l], sh2, sc2)
        x = x + gt2[:, None, :] * token_mixing(h, w_in[l], conv_w[l], conv_b[l], conv_ln_g[l], conv_ln_b[l],
                                                gdn_conv_w[l], gdn_a_log[l], gdn_dt_bias[l], gdn_norm_g[l], w_out[l])
        h = modulate(x, ln_ffn2_g[l], sh3, sc3)
        x = x + 0.5 * gt3[:, None, :] * swiglu(h, ffn2_w_gate[l], ffn2_w_up[l], ffn2_w_down[l])
    return rmsnorm(x, final_g)
```

```python
import numpy as np
from contextlib import ExitStack
import concourse.bass as bass
import concourse.mybir as mybir
from concourse.bass_utils import run_bass_kernel_spmd

F32 = mybir.dt.float32
BF16 = mybir.dt.bfloat16
AF = mybir.ActivationFunctionType
ALU = mybir.AluOpType

D = 1024
KC = 8
DFF = 2816
FC = 22
DIN = 3212
B = 2
S = 8192
NCORES = 8
TOK = 2048
EPS = 1e-6

SAME_ENG_SYNC = True


class Buf:
    __slots__ = ("name", "lw", "rd")

    def __init__(self, name=""):
        self.name = name
        self.lw = None
        self.rd = {}


class Sched:
    ENGS = ("tensor", "vector", "scalar", "gpsimd", "sync")
    EPOCH = 20000

    def __init__(self, nc, es):
        self.nc = nc
        self.es = es
        self.q = {e: [] for e in self.ENGS}
        self.cnt = {e: 0 for e in self.ENGS}
        self.seen = {e: {} for e in self.ENGS}
        self.esem = {}
        self.dsem = {}
        self.dcnt = {}
        self.cckeys = set()

    def _get_esem(self, eng, epoch):
        k = (eng, epoch)
        if k not in self.esem:
            self.esem[k] = self.es.enter_context(self.nc.semaphore(f"se_{eng}_{epoch}"))
        return self.esem[k]

    def _get_dsem(self, key):
        if key not in self.dsem:
            self.dsem[key] = self.es.enter_context(self.nc.semaphore(f"sd_{key}"))
            self.dcnt[key] = 0
        return self.dsem[key]

    def _need(self, eng, tok, waits):
        if tok is None:
            return
        kind, k, val = tok
        if kind == "e":
            if k == eng and (eng == "tensor" or not SAME_ENG_SYNC):
                return
        key = (kind, k)
        if self.seen[eng].get(key, 0) >= val:
            return
        self.seen[eng][key] = val
        waits.append(tok)

    def _deps(self, eng, reads, writes):
        waits = []
        for b in reads:
            self._need(eng, b.lw, waits)
        for b in writes:
            self._need(eng, b.lw, waits)
            for k, v in b.rd.items():
                self._need(eng, (k[0], k[1], v), waits)
        return waits

    def _mark(self, tok, reads, writes):
        key = (tok[0], tok[1])
        for b in reads:
            if b.rd.get(key, 0) < tok[2]:
                b.rd[key] = tok[2]
        for b in writes:
            b.lw = tok
            b.rd = {}

    def op(self, eng, fn, reads=(), writes=(), inc=True):
        waits = self._deps(eng, reads, writes)
        idx = self.cnt[eng] + 1
        if inc:
            self.cnt[eng] = idx
        tok = ("e", eng, idx)
        self._mark(tok, reads, writes)
        self.q[eng].append((waits, fn, tok if inc else None))

    def dma(self, qeng, fn, reads=(), writes=(), key="d"):
        waits = self._deps(qeng, reads, writes)
        self._get_dsem(key)
        self.dcnt[key] += 1
        tok = ("d", key, 16 * self.dcnt[key])
        self._mark(tok, reads, writes)
        self.q[qeng].append((waits, fn, tok))

    def cc(self, fn, reads=(), writes=(), key="cc"):
        waits = self._deps("gpsimd", reads, writes)
        self._get_dsem(key)
        self.cckeys.add(key)
        self.dcnt[key] += 1
        tok = ("c", key, self.dcnt[key])
        self._mark(tok, reads, writes)
        self.q["gpsimd"].append((waits, fn, tok))

    def barrier(self):
        for e in self.ENGS:
            waits = []
            for e2 in self.ENGS:
                if e2 != e and self.cnt[e2] > 0:
                    self._need(e, ("e", e2, self.cnt[e2]), waits)
            for key, n in self.dcnt.items():
                if n > 0:
                    kind = "c" if key in self.cckeys else "d"
                    self._need(e, (kind, key, n if kind == "c" else 16 * n), waits)
            self.q[e].append((waits, None, None))

    def final_wait(self, eng, toks_bufs):
        waits = self._deps(eng, (), toks_bufs)
        self.q[eng].append((waits, None, None))

    def emit(self, block):
        nc = self.nc

        def run(engname):
            def body(eng):
                for waits, fn, tok in self.q[engname]:
                    for (kind, k, val) in waits:
                        if kind == "e":
                            epoch = (val - 1) // self.EPOCH
                            eng.wait_ge(self._get_esem(k, epoch), val - epoch * self.EPOCH)
                        else:
                            eng.wait_ge(self.dsem[k], val)
                    if fn is None:
                        continue
                    ins = fn(eng)
                    if tok is not None:
                        if tok[0] == "e":
                            epoch = (tok[2] - 1) // self.EPOCH
                            ins.then_inc(self._get_esem(tok[1], epoch), 1)
                        elif tok[0] == "c":
                            ins.then_inc(self.dsem[tok[1]])
                        else:
                            ins.then_inc(self.dsem[tok[1]], 16)
                self.q[engname] = []
            return body

        for e in self.ENGS:
            for ep in range((self.cnt[e] - 1) // self.EPOCH + 1 if self.cnt[e] else 0):
                self._get_esem(e, ep)
        block.tensor(run("tensor"))
        block.vector(run("vector"))
        block.scalar(run("scalar"))
        block.gpsimd(run("gpsimd"))
        block.sync(run("sync"))


class TokProg:
    def __init__(self, stages, tok=TOK, fused=None):
        self.stages = stages
        self.tok = tok
        self.fused = fused
        if fused is None:
            self.nc = bass.Bass("TRN2", target_bir_lowering=False)
            self.es = ExitStack()
        else:
            self.nc = fused.nc
            self.es = fused.es
        self.in_names = []
        self.out_names = []

    def din(self, name, shape, dt=F32):
        if self.fused is not None:
            return self.fused.din(name, shape, dt)
        self.in_names.append(name)
        return self.nc.dram_tensor(name, list(shape), dt, kind="ExternalInput").ap()

    def dout(self, name, shape, dt=F32):
        if self.fused is not None:
            return self.fused.dout(name, shape, dt)
        self.out_names.append(name)
        return self.nc.dram_tensor(name, list(shape), dt, kind="ExternalOutput").ap()

    def sb(self, name, shape, dt):
        if self.fused is not None:
            return self.fused.sb(name, shape, dt)
        return self.es.enter_context(self.nc.sbuf_tensor(name, list(shape), dt))

    def build(self):
        nc, es = self.nc, self.es
        fz = self.fused
        T = self.tok
        NH = T // 1024
        stages = self.stages
        layers = sorted({s[1] for s in stages if len(s) > 1})
        need_v = {}
        for s in stages:
            if s[0] == "ffn1":
                need_v.setdefault(s[1], set()).update([0, 1, 2])
            elif s[0] == "uproj":
                need_v.setdefault(s[1], set()).update([3, 4])
            elif s[0] == "wout":
                need_v.setdefault(s[1], set()).update([5])
            elif s[0] == "ffn2":
                need_v.setdefault(s[1], set()).update([6, 7, 8])

        xT_d = self.din("xT", [D, T]) if (fz is None or fz.load_x) else None
        cT_d = self.din("cT", [128, KC])
        W = {}
        for l in layers:
            W[("w_ada", l)] = self.din(f"w_ada{l}", [D, 9 * D])
            W[("b_ada", l)] = self.din(f"b_ada{l}", [128, 72])
        for s in stages:
            if s[0] in ("ffn1", "ffn2"):
                l = s[1]
                n = s[0]
                W[(n + "_g", l)] = self.din(f"ln_{n}_g{l}", [128, KC])
                W[(n + "_wg", l)] = self.din(f"{n}_w_gate{l}", [D, DFF])
                W[(n + "_wu", l)] = self.din(f"{n}_w_up{l}", [D, DFF])
                W[(n + "_wd", l)] = self.din(f"{n}_w_down{l}", [DFF, D])
            elif s[0] == "uproj":
                l = s[1]
                W[("mix_g", l)] = self.din(f"ln_mix_g{l}", [128, KC])
                W[("w_in", l)] = self.din(f"w_in{l}", [D, DIN])
                W[("uT", l)] = self.dout(f"uT{l}", [DIN, T]) if fz is None else fz.uT_dst
            elif s[0] == "wout":
                l = s[1]
                W[("w_out", l)] = self.din(f"w_out{l}", [D, D])
                W[("yT", l)] = self.din(f"yT{l}", [D, T]) if fz is None else fz.yT_src
            elif s[0] == "final":
                W[("final_g",)] = self.din("final_g", [128, KC])
        xo_d = self.dout("xoT", [D, T]) if (fz is None or fz.store_x) else None

        x = self.sb("x", [128, KC, T], F32) if fz is None else fz.x
        h = self.sb("h", [128, KC, 1024], BF16)
        act = self.sb("act", [128, FC, 1024], BF16)
        wd = self.sb("wd", [128, FC, D], BF16)
        NSLOT = 4
        SLOTW = 256
        wslot = [self.sb(f"ws{i}", [128, KC, SLOTW], BF16) for i in range(NSLOT)]
        tmpA = [self.sb(f"tmpA{i}", [128, 512], F32) for i in range(2)]
        tmpB = [self.sb(f"tmpB{i}", [128, 512], F32) for i in range(2)]
        sqb = [self.sb(f"sq{i}", [128, 512], BF16) for i in range(2)]
        rstd = self.sb("rstd", [128, 512], F32)
        ones = self.sb("ones", [128, 128], BF16)
        cT = self.sb("cT_sb", [128, KC], F32)
        cact = self.sb("cact", [128, KC], BF16)
        bada = {l: self.sb(f"bada{l}", [128, 72], F32) for l in layers}
        mod = {l: self.sb(f"mod{l}", [128, 72], F32) for l in layers}
        gains = {}
        for k in W:
            if k[0] in ("ffn1_g", "ffn2_g", "mix_g", "final_g"):
                gains[k] = self.sb("g_" + "_".join(map(str, k)), [128, KC], F32)
        coefA = {}
        coefG = {}
        ps = es.enter_context(nc.psum_tensor("ps", [128, 8, 512], F32)) if fz is None else fz.ps

        sc = Sched(nc, es) if fz is None else fz.sc
        bx = [[Buf(f"x{c}_{t}") for t in range(T // 512)] for c in range(KC)] if fz is None else fz.bx
        bU = [] if fz is None else [fz.bU]
        bY = [] if fz is None else [fz.bY]
        bh = [Buf(f"h{t}") for t in range(2)]
        bact = [[Buf(f"act{f}_{t}") for t in range(2)] for f in range(FC)]
        WD_PIECES = ((0, 6), (6, 12), (12, 17), (17, 22))
        bwd = [Buf(f"wd{i}") for i in range(4)]
        wd_piece = {}
        for i, (f0, f1) in enumerate(WD_PIECES):
            for f in range(f0, f1):
                wd_piece[f] = i
        bws = [Buf(f"ws{i}") for i in range(NSLOT)]
        btA = [Buf() for _ in range(2)]
        btB = [Buf() for _ in range(2)]
        bsq = [Buf() for _ in range(2)]
        brstd = Buf()
        bones = Buf()
        bps = [Buf(f"ps{i}") for i in range(8)]
        bmisc = Buf("misc")
        bmod = Buf("mod")

        if xT_d is not None:
            xT_v = xT_d.rearrange("(c p) t -> p c t", p=128)
            for c in range(KC):
                sc.dma("sync", lambda e, c=c: e.dma_start(out=x[:, c, :], in_=xT_v[:, c, :]),
                       writes=bx[c], key=f"x{c}")
        sc.dma("sync", lambda e: e.dma_start(out=cT[:], in_=cT_d[:, :]), writes=[bmisc], key="misc")
        for l in layers:
            sc.dma("sync", lambda e, l=l: e.dma_start(out=bada[l][:], in_=W[("b_ada", l)][:, :]),
                   writes=[bmisc], key="misc")
        for k, t in gains.items():
            sc.dma("sync", lambda e, k=k, t=t: e.dma_start(out=t[:], in_=W[k][:, :]), writes=[bmisc], key="misc")
        sc.op("vector", lambda e: e.memset(ones[:], 1.0), writes=[bones])
        sc.op("scalar", lambda e: e.activation(out=cact[:], in_=cT[:], func=AF.Silu), reads=[bmisc], writes=[bmod])

        wslot_i = [0]

        def next_slot():
            i = wslot_i[0] % NSLOT
            wslot_i[0] += 1
            return i

        def load_cols(Wd, c0, ncols, nk=KC):
            i = next_slot()
            src = Wd.rearrange("(k p) n -> p k n", p=128)
            sc.dma("gpsimd", lambda e, i=i: e.dma_start(out=wslot[i][:, 0:nk, 0:ncols], in_=src[:, :, c0:c0 + ncols]),
                   writes=[bws[i]], key=f"ws{i}")
            return i

        mod_ps = ps[:, 7, 0:72]
        for l in layers:
            for v in sorted(need_v[l]):
                for hh in range(4):
                    si = load_cols(W[("w_ada", l)], v * 1024 + hh * 256, 256)
                    for jj in range(2):
                        j = hh * 2 + jj
                        col = v * 8 + j
                        for kc in range(KC):
                            sc.op("tensor",
                                  lambda e, si=si, jj=jj, kc=kc, col=col: e.matmul(
                                      ps[:, 7, col:col + 1], lhsT=wslot[si][:, kc, jj * 128:(jj + 1) * 128],
                                      rhs=cact[:, kc:kc + 1], start=(kc == 0), stop=(kc == KC - 1)),
                                  reads=[bws[si], bmod], writes=[bps[7]], inc=(kc == KC - 1))
            sc.op("vector", lambda e, l=l: e.tensor_tensor(out=mod[l][:], in0=mod_ps, in1=bada[l][:], op=ALU.add),
                  reads=[bps[7], bmisc], writes=[bmod])
            for (gk, vs, vg, half) in ((("ffn1_g", l), 1, 2, 0.5), (("mix_g", l), 4, None, None),
                                       (("ffn2_g", l), 7, 8, 0.5)):
                if gk in gains:
                    a = self.sb("cA_" + "_".join(map(str, gk)), [128, KC], F32)
                    coefA[gk] = a
                    sc.op("vector", lambda e, a=a, gk=gk, vs=vs, l=l: e.scalar_tensor_tensor(
                        out=a[:], in0=mod[l][:, vs * 8:vs * 8 + 8], scalar=1.0, in1=gains[gk][:],
                        op0=ALU.add, op1=ALU.mult), reads=[bmod, bmisc], writes=[bmod])
                    if vg is not None:
                        g = self.sb("cG_" + "_".join(map(str, gk)), [128, KC], F32)
                        coefG[gk] = g
                        sc.op("vector", lambda e, g=g, vg=vg, l=l: e.tensor_scalar(
                            out=g[:], in0=mod[l][:, vg * 8:vg * 8 + 8], scalar1=0.5, scalar2=None, op0=ALU.mult),
                            reads=[bmod], writes=[bmod])

        psi = [0]

        def next_ps(pool):
            i = pool[psi[0] % len(pool)]
            psi[0] += 1
            return i

        def norm_mod(half, A_ap, sh_ap):
            for tt in range(2):
                t0 = half * 1024 + tt * 512
                ti = t0 // 512
                pb = 6
                for c in range(KC):
                    s = c % 2
                    sc.op("scalar", lambda e, c=c, s=s, t0=t0: e.activation(out=sqb[s][:], in_=x[:, c, t0:t0 + 512],
                                                                            func=AF.Square),
                          reads=[bx[c][ti]], writes=[bsq[s]])
                    sc.op("tensor", lambda e, c=c, s=s: e.matmul(ps[:, pb, :], lhsT=ones[:], rhs=sqb[s][:],
                                                                 start=(c == 0), stop=(c == KC - 1)),
                          reads=[bones, bsq[s]], writes=[bps[pb]])
                sc.op("scalar", lambda e: e.activation(out=tmpA[0][:], in_=ps[:, pb, :], func=AF.Sqrt,
                                                       bias=eps_t[:, 0:1], scale=1.0 / D),
                      reads=[bps[pb], bmisc], writes=[btA[0]])
                sc.op("vector", lambda e: e.reciprocal(out=rstd[:], in_=tmpA[0][:]), reads=[btA[0]], writes=[brstd])
                for c in range(KC):
                    s = c % 2
                    sc.op("vector", lambda e, c=c, s=s, t0=t0: e.scalar_tensor_tensor(
                        out=tmpB[s][:], in0=x[:, c, t0:t0 + 512], scalar=A_ap[:, c:c + 1], in1=rstd[:],
                        op0=ALU.mult, op1=ALU.mult), reads=[bx[c][ti], brstd, bmod], writes=[btB[s]])
                    if sh_ap is not None:
                        sc.op("scalar", lambda e, c=c, s=s, tt=tt: e.activation(
                            out=h[:, c, tt * 512:(tt + 1) * 512], in_=tmpB[s][:], func=AF.Identity,
                            bias=sh_ap[:, c:c + 1], scale=1.0), reads=[btB[s], bmod], writes=[bh[tt]])

        def ffn(half, n, l):
            A = coefA[(n + "_g", l)]
            G = coefG[(n + "_g", l)]
            vsh = 0 if n == "ffn1" else 6
            sh = mod[l][:, vsh * 8:vsh * 8 + 8]
            norm_mod(half, A, sh)
            wdv = W[(n + "_wd", l)].rearrange("(f p) n -> p f n", p=128)
            for i, (f0, f1) in enumerate(WD_PIECES):
                sc.dma("gpsimd", lambda e, f0=f0, f1=f1: e.dma_start(out=wd[:, f0:f1, :], in_=wdv[:, f0:f1, :]),
                       writes=[bwd[i]], key=f"wd{i}")
            groups = [(g * 2, 2) for g in range(11)]
            loaded = {}

            def load_group(gi):
                f0, nf = groups[gi]
                loaded[gi] = (load_cols(W[(n + "_wg", l)], f0 * 128, nf * 128),
                              load_cols(W[(n + "_wu", l)], f0 * 128, nf * 128))
            load_group(0)
            for gi, (f0, nf) in enumerate(groups):
                if gi + 1 < len(groups):
                    load_group(gi + 1)
                sg, su = loaded[gi]
                for fi in range(nf):
                    f = f0 + fi
                    for tt in range(2):
                        pg = next_ps([0, 1])
                        pu = pg + 2
                        for kc in range(KC):
                            sc.op("tensor", lambda e, sg=sg, fi=fi, kc=kc, tt=tt, pg=pg: e.matmul(
                                ps[:, pg, :], lhsT=wslot[sg][:, kc, fi * 128:(fi + 1) * 128],
                                rhs=h[:, kc, tt * 512:(tt + 1) * 512], start=(kc == 0), stop=(kc == KC - 1)),
                                reads=[bws[sg], bh[tt]], writes=[bps[pg]], inc=(kc == KC - 1))
                        for kc in range(KC):
                            sc.op("tensor", lambda e, su=su, fi=fi, kc=kc, tt=tt, pu=pu: e.matmul(
                                ps[:, pu, :], lhsT=wslot[su][:, kc, fi * 128:(fi + 1) * 128],
                                rhs=h[:, kc, tt * 512:(tt + 1) * 512], start=(kc == 0), stop=(kc == KC - 1)),
                                reads=[bws[su], bh[tt]], writes=[bps[pu]], inc=(kc == KC - 1))
                        s = pg
                        sc.op("scalar", lambda e, s=s, pg=pg: e.activation(out=tmpA[s][:], in_=ps[:, pg, :],
                                                                           func=AF.Silu),
                              reads=[bps[pg]], writes=[btA[s]])
                        sc.op("vector", lambda e, s=s, pu=pu, f=f, tt=tt: e.tensor_tensor(
                            out=act[:, f, tt * 512:(tt + 1) * 512], in0=tmpA[s][:], in1=ps[:, pu, :], op=ALU.mult),
                            reads=[btA[s], bps[pu]], writes=[bact[f][tt]])
            for tt in range(2):
                t0 = half * 1024 + tt * 512
                ti = t0 // 512
                for d in range(KC):
                    pd = next_ps([4, 5])
                    for f in range(FC):
                        sc.op("tensor", lambda e, f=f, d=d, tt=tt, pd=pd: e.matmul(
                            ps[:, pd, :], lhsT=wd[:, f, d * 128:(d + 1) * 128], rhs=act[:, f, tt * 512:(tt + 1) * 512],
                            start=(f == 0), stop=(f == FC - 1)),
                            reads=[bwd[wd_piece[f]], bact[f][tt]], writes=[bps[pd]], inc=(f == FC - 1))
                    sc.op("vector", lambda e, d=d, t0=t0, pd=pd: e.scalar_tensor_tensor(
                        out=x[:, d, t0:t0 + 512], in0=ps[:, pd, :], scalar=G[:, d:d + 1], in1=x[:, d, t0:t0 + 512],
                        op0=ALU.mult, op1=ALU.add), reads=[bps[pd], bx[d][ti], bmod], writes=[bx[d][ti]])

        ostage = [self.sb(f"ost{i}", [128, 512], F32) for i in range(2)]
        bost = [Buf() for _ in range(2)]
        osi = [0]

        def uproj(half, l):
            A = coefA[("mix_g", l)]
            sh = mod[l][:, 3 * 8:3 * 8 + 8]
            norm_mod(half, A, sh)
            uT = W[("uT", l)]
            ngr = (DIN + 255) // 256
            loaded = {}

            def load_group(gi):
                c0 = gi * 256
                loaded[gi] = load_cols(W[("w_in", l)], c0, min(256, DIN - c0))
            load_group(0)
            for gi in range(ngr):
                if gi + 1 < ngr:
                    load_group(gi + 1)
                si = loaded[gi]
                c0 = gi * 256
                ncol = min(256, DIN - c0)
                for fi in range((ncol + 127) // 128):
                    m = min(128, ncol - fi * 128)
                    for tt in range(2):
                        t0 = half * 1024 + tt * 512
                        pg = next_ps([0, 1, 2, 3])
                        for kc in range(KC):
                            sc.op("tensor", lambda e, si=si, fi=fi, kc=kc, tt=tt, pg=pg, m=m: e.matmul(
                                ps[0:m, pg, :], lhsT=wslot[si][:, kc, fi * 128:fi * 128 + m],
                                rhs=h[:, kc, tt * 512:(tt + 1) * 512], start=(kc == 0), stop=(kc == KC - 1)),
                                reads=[bws[si], bh[tt]], writes=[bps[pg]], inc=(kc == KC - 1))
                        o = osi[0] % 2
                        osi[0] += 1
                        eng = "scalar" if o == 0 else "vector"
                        if eng == "scalar":
                            sc.op("scalar", lambda e, o=o, pg=pg, m=m: e.copy(out=ostage[o][0:m, :], in_=ps[0:m, pg, :]),
                                  reads=[bps[pg]], writes=[bost[o]])
                        else:
                            sc.op("vector", lambda e, o=o, pg=pg, m=m: e.tensor_copy(out=ostage[o][0:m, :],
                                                                                     in_=ps[0:m, pg, :]),
                                  reads=[bps[pg]], writes=[bost[o]])
                        r0 = c0 + fi * 128
                        sc.dma("sync", lambda e, o=o, m=m, r0=r0, t0=t0: e.dma_start(
                            out=uT[r0:r0 + m, t0:t0 + 512], in_=ostage[o][0:m, :]), reads=[bost[o]] + bU, key=f"ost{o}")

        ystage = [act[:, i * 8:(i + 1) * 8, 0:512] for i in range(2)]
        byst = [[bact[f][0] for f in range(i * 8, (i + 1) * 8)] for i in range(2)]

        def wout(half, l):
            yT = W[("yT", l)].rearrange("(c p) t -> p c t", p=128)
            wsl = [load_cols(W[("w_out", l)], q * 256, 256) for q in range(4)]
            G = mod[l][:, 5 * 8:5 * 8 + 8]
            for tt in range(2):
                t0 = half * 1024 + tt * 512
                ti = t0 // 512
                sc.dma("gpsimd", lambda e, tt=tt, t0=t0: e.dma_start(out=ystage[tt], in_=yT[:, :, t0:t0 + 512]),
                       reads=bY, writes=byst[tt], key=f"yst{tt}")
                for d in range(KC):
                    si = wsl[d // 2]
                    dj = d % 2
                    pd = next_ps([4, 5])
                    for kc in range(KC):
                        sc.op("tensor", lambda e, si=si, dj=dj, kc=kc, tt=tt, pd=pd: e.matmul(
                            ps[:, pd, :], lhsT=wslot[si][:, kc, dj * 128:(dj + 1) * 128], rhs=ystage[tt][:, kc, :],
                            start=(kc == 0), stop=(kc == KC - 1)),
                            reads=[bws[si]] + byst[tt], writes=[bps[pd]], inc=(kc == KC - 1))
                    sc.op("vector", lambda e, d=d, t0=t0, pd=pd: e.scalar_tensor_tensor(
                        out=x[:, d, t0:t0 + 512], in0=ps[:, pd, :], scalar=G[:, d:d + 1], in1=x[:, d, t0:t0 + 512],
                        op0=ALU.mult, op1=ALU.add), reads=[bps[pd], bx[d][ti], bmod], writes=[bx[d][ti]])

        def final(half):
            g = gains[("final_g",)]
            for tt in range(2):
                t0 = half * 1024 + tt * 512
                ti = t0 // 512
                pb = 6
                for c in range(KC):
                    s = c % 2
                    sc.op("scalar", lambda e, c=c, s=s, t0=t0: e.activation(out=sqb[s][:], in_=x[:, c, t0:t0 + 512],
                                                                            func=AF.Square),
                          reads=[bx[c][ti]], writes=[bsq[s]])
                    sc.op("tensor", lambda e, c=c, s=s: e.matmul(ps[:, pb, :], lhsT=ones[:], rhs=sqb[s][:],
                                                                 start=(c == 0), stop=(c == KC - 1)),
                          reads=[bones, bsq[s]], writes=[bps[pb]])
                sc.op("scalar", lambda e: e.activation(out=tmpA[0][:], in_=ps[:, pb, :], func=AF.Sqrt,
                                                       bias=eps_t[:, 0:1], scale=1.0 / D),
                      reads=[bps[pb], bmisc], writes=[btA[0]])
                sc.op("vector", lambda e: e.reciprocal(out=rstd[:], in_=tmpA[0][:]), reads=[btA[0]], writes=[brstd])
                for c in range(KC):
                    sc.op("vector", lambda e, c=c, t0=t0: e.scalar_tensor_tensor(
                        out=x[:, c, t0:t0 + 512], in0=x[:, c, t0:t0 + 512], scalar=g[:, c:c + 1], in1=rstd[:],
                        op0=ALU.mult, op1=ALU.mult), reads=[bx[c][ti], brstd, bmisc], writes=[bx[c][ti]])

        eps_t = self.sb("eps_t", [128, 1], F32)
        sc.op("vector", lambda e: e.memset(eps_t[:], EPS), writes=[bmisc])

        for half in range(NH):
            for s in stages:
                if s[0] in ("ffn1", "ffn2"):
                    ffn(half, s[0], s[1])
                elif s[0] == "uproj":
                    uproj(half, s[1])
                elif s[0] == "wout":
                    wout(half, s[1])
                elif s[0] == "final":
                    final(half)

        allb = []
        if xo_d is not None:
            xo_v = xo_d.rearrange("(c p) t -> p c t", p=128)
            for c in range(KC):
                sc.dma("sync", lambda e, c=c: e.dma_start(out=xo_v[:, c, :], in_=x[:, c, :]), reads=bx[c], key="xo")
                allb += bx[c]
        if fz is not None:
            return allb
        sc.final_wait("sync", allb + bost)

        with nc.Block() as block:
            sc.emit(block)
        es.close()
        return nc


NBLK = 32
MOBA_SLOTS = 16
HALF_BLOCKS = ([b for b in range(NBLK) if b % 4 in (0, 3)], [b for b in range(NBLK) if b % 4 in (1, 2)])
NEG = -30000.0


def moba_emit(P, sc, nu, pfx="", fz=None):
    nc, es = P.nc, P.es
    NQ = MOBA_SLOTS * 256
    if fz is None:
        mq = P.din(pfx + "mq", [nu, 64, NQ])
        mqs = P.din(pfx + "mqs", [nu, 16, NQ])
        mk = P.din(pfx + "mk", [nu, 64, S])
        mks = P.din(pfx + "mks", [nu, 16, S])
        mv = P.din(pfx + "mv", [nu, S, 64])
        yo = P.dout(pfx + "moT", [nu, 64, NQ])
    cq = P.din("ropeq", [nu, 2, 16, NQ])
    ck = P.din("ropek", [2, 16, S])
    pm_d = P.din("pm", [nu, 128, MOBA_SLOTS * NBLK])
    oh_d = P.din("oh", [nu, 128, MOBA_SLOTS * NBLK])
    cm_d = P.din("cm", [nu, 2, 4, 128, 256])
    boh_d = P.din("boh", [32, S])
    id_d = P.din("ident", [128, 128])

    qaug = P.sb(pfx + "qaug", [128, NQ], BF16)
    kaug = P.sb(pfx + "kaug", [128, S], BF16)
    vaug = P.sb(pfx + "vaug", [128, 64, 128], BF16)
    qf = P.sb(pfx + "qf", [64, NQ], F32)
    xt = [P.sb(pfx + f"xt{i}", [64, 1024], F32) for i in range(2)]
    xs = [P.sb(pfx + f"xs{i}", [16, 1024], F32) for i in range(2)]
    ct = [P.sb(pfx + f"ct{i}", [16, 2, 1024], F32) for i in range(2)]
    t16 = P.sb(pfx + "t16", [16, 1024], F32)
    sqf = P.sb(pfx + "sqf", [64, 1024], F32)
    kmean = P.sb(pfx + "kmean", [64, NBLK], F32)
    mx = P.sb(pfx + "mx", [128, 4], F32)
    nbias = P.sb(pfx + "nbias", [128, 1], F32)
    onesf = P.sb(pfx + "onesf", [64, 128], F32)
    ident = P.sb(pfx + "ident_sb", [128, 128], F32)
    pm = P.sb(pfx + "pm_sb", [128, MOBA_SLOTS * NBLK], F32)
    oh = P.sb(pfx + "oh_sb", [128, MOBA_SLOTS * NBLK], F32)
    cm = P.sb(pfx + "cm_sb", [128, 8, 256], F32)
    gs = P.sb(pfx + "gs", [128, NBLK], F32)
    g8 = P.sb(pfx + "g8", [128, 8], F32)
    m1 = P.sb(pfx + "m1", [128, NBLK], F32)
    m2 = P.sb(pfx + "m2", [128, NBLK], F32)
    stm = [P.sb(pfx + f"stm{i}", [128, 256], F32) for i in range(2)]
    pt = [P.sb(pfx + f"pt{i}", [128, 256], BF16) for i in range(3)]
    rec = P.sb(pfx + "rec", [64, 256], F32)
    yst = [P.sb(pfx + f"yst{i}", [64, 256], F32) for i in range(2)]
    ps = es.enter_context(nc.psum_tensor(pfx + "mps", [128, 8, 512], F32)) if fz is None else fz.ps
    if fz is not None:
        vt = P.sb(pfx + "vt", [64, 1024], F32)
        bvt = Buf()

    def rows6(e, u, r0, nr, cols):
        return fz.Urecv[r0:r0 + 5 * 64 + nr, cols][bass.ds(fz.dyn(e, "sync", ("mhr", u)), nr), :]

    bq, bk, bv, bqf = Buf(), Buf(), Buf(), Buf()
    bxt = [Buf(), Buf()]
    bxs = [Buf(), Buf()]
    bct = [Buf(), Buf()]
    bt16, bsqf, bkm, bmx, bnb, bconst, bmask = Buf(), Buf(), Buf(), Buf(), Buf(), Buf(), Buf()
    bgs, bg8, bm1, bm2 = Buf(), Buf(), Buf(), Buf()
    bstm = [Buf(), Buf()]
    bpt = [Buf(), Buf(), Buf()]
    brec = Buf()
    byst = [Buf(), Buf()]
    bps = [Buf() for _ in range(8)]

    sc.dma("sync", lambda e: e.dma_start(out=ident[:], in_=id_d[:, :]), writes=[bconst], key=pfx + "mconst")
    sc.op("vector", lambda e: e.memset(onesf[:], 1.0), writes=[bconst])
    sc.op("vector", lambda e: e.memset(kaug[32:64, :], 0.0), writes=[bk])
    sc.op("vector", lambda e: e.memset(kaug[32:33, :], 1.0), writes=[bk])
    sc.dma("gpsimd", lambda e: e.dma_start(out=kaug[0:32, :], in_=boh_d[:, :]), writes=[bk], key=pfx + "mk0")
    sc.op("vector", lambda e: e.memset(qaug[32:64, :], 0.0), writes=[bq])
    sc.op("vector", lambda e: e.memset(vaug[:, :, 64:128], 1.0), writes=[bv])

    cnt = [0]
    for u in range(nu):
        sc.dma("sync", lambda e, u=u: e.dma_start(out=pm[:], in_=pm_d[u]), writes=[bmask], key=pfx + "mmask")
        sc.dma("sync", lambda e, u=u: e.dma_start(out=oh[:], in_=oh_d[u]), writes=[bmask], key=pfx + "mmask")
        sc.dma("sync", lambda e, u=u: e.dma_start(out=cm[:], in_=cm_d[u].rearrange("a k p q -> p (a k) q")),
               writes=[bmask], key=pfx + "mmask")
        if fz is None:
            for k0 in range(0, 64, 16):
                sc.dma("gpsimd", lambda e, u=u, k0=k0: e.dma_start(
                    out=vaug[:, k0:k0 + 16, 0:64], in_=mv[u].rearrange("(k p) d -> p k d", p=128)[:, k0:k0 + 16, :]),
                    writes=[bv], key=pfx + "mv")
        else:
            for c0 in range(0, S, 1024):
                rr, t0 = c0 // TOK, c0 % TOK

                def vsrc(e, u=u, c0=c0):
                    return fz.LV[u, :, c0:c0 + 1024]
                sc.dma("sync", lambda e, vsrc=vsrc: e.dma_start(out=vt[:], in_=vsrc(e)), reads=[fz.bUr], writes=[bvt],
                       key=pfx + "mvt")
                for cj in range(8):
                    sc.op("tensor", lambda e, cj=cj: e.transpose(ps[:, 6, cj * 64:(cj + 1) * 64], vt[:, cj * 128:(cj + 1) * 128],
                                                                 ident[0:64, 0:64]),
                          reads=[bvt, bconst], writes=[bps[6]], inc=(cj == 7))
                k0 = c0 // 128
                sc.op("vector", lambda e, k0=k0: e.tensor_copy(out=vaug[:, k0:k0 + 8, 0:64],
                                                               in_=ps[:, 6, :].rearrange("p (a b) -> p a b", b=64)),
                      reads=[bps[6]], writes=[bv])
        sc.op("vector", lambda e: e.memset(mx[:], 0.0), writes=[bmx])
        srcs_ = ((mk, mks, None, S), (mq, mqs, cq, NQ)) if fz is None else ((None, None, None, S), (None, None, cq, NQ))
        for which, (src, srcs, tab, ncols) in enumerate(srcs_):
            for c0 in range(0, ncols, 1024):
                i = cnt[0] % 2
                cnt[0] += 1
                if fz is None:
                    sc.dma("sync", lambda e, i=i, c0=c0, src=src, u=u: e.dma_start(out=xt[i][:], in_=src[u, :, c0:c0 + 1024]),
                           writes=[bxt[i]], key=pfx + f"mxt{i}")
                    sc.dma("sync", lambda e, i=i, c0=c0, srcs=srcs, u=u: e.dma_start(out=xs[i][:], in_=srcs[u, :, c0:c0 + 1024]),
                           writes=[bxs[i]], key=pfx + f"mxs{i}")
                elif which == 0:
                    rr, t0 = c0 // TOK, c0 % TOK

                    def ksrc(e, ro, nr, u=u, c0=c0):
                        return fz.LK[u, ro:ro + nr, c0:c0 + 1024]
                    sc.dma("sync", lambda e, i=i, ksrc=ksrc: e.dma_start(out=xt[i][:], in_=ksrc(e, 0, 64)),
                           reads=[fz.bUr], writes=[bxt[i]], key=pfx + f"mxt{i}")
                    sc.dma("sync", lambda e, i=i, ksrc=ksrc: e.dma_start(out=xs[i][0:8, :], in_=ksrc(e, 8, 8)),
                           reads=[fz.bUr], writes=[bxs[i]], key=pfx + f"mxs{i}")
                    sc.dma("sync", lambda e, i=i, ksrc=ksrc: e.dma_start(out=xs[i][8:16, :], in_=ksrc(e, 0, 8)),
                           reads=[fz.bUr], writes=[bxs[i]], key=pfx + f"mxs{i}")
                else:
                    def qsrc(e, ro, nr, u=u, c0=c0):
                        return fz.LQ[u, ro:ro + nr, c0:c0 + 1024]
                    sc.dma("sync", lambda e, i=i, qsrc=qsrc: e.dma_start(out=xt[i][:], in_=qsrc(e, 0, 64)),
                           reads=[fz.bUr], writes=[bxt[i]], key=pfx + f"mxt{i}")
                    sc.dma("sync", lambda e, i=i, qsrc=qsrc: e.dma_start(out=xs[i][0:8, :], in_=qsrc(e, 8, 8)),
                           reads=[fz.bUr], writes=[bxs[i]], key=pfx + f"mxs{i}")
                    sc.dma("sync", lambda e, i=i, qsrc=qsrc: e.dma_start(out=xs[i][8:16, :], in_=qsrc(e, 0, 8)),
                           reads=[fz.bUr], writes=[bxs[i]], key=pfx + f"mxs{i}")
                if which == 0:
                    sc.dma("sync", lambda e, i=i, c0=c0: e.dma_start(
                        out=ct[i][:], in_=ck[:, :, c0:c0 + 1024].rearrange("a p t -> p a t")),
                        writes=[bct[i]], key=pfx + f"mct{i}")
                else:
                    sc.dma("sync", lambda e, i=i, c0=c0, u=u: e.dma_start(
                        out=ct[i][:], in_=cq[u, :, :, c0:c0 + 1024].rearrange("a p t -> p a t")),
                        writes=[bct[i]], key=pfx + f"mct{i}")
                sc.op("vector", lambda e, i=i: e.tensor_tensor(out=t16[:], in0=xs[i][:], in1=ct[i][:, 1, :], op=ALU.mult),
                      reads=[bxs[i], bct[i]], writes=[bt16])
                sc.op("vector", lambda e, i=i: e.tensor_tensor(out=xt[i][0:16, :], in0=xt[i][0:16, :], in1=ct[i][:, 0, :],
                                                               op=ALU.mult), reads=[bxt[i], bct[i]], writes=[bxt[i]])
                sc.op("vector", lambda e, i=i: e.tensor_tensor(out=xt[i][0:16, :], in0=xt[i][0:16, :], in1=t16[:],
                                                               op=ALU.add), reads=[bxt[i], bt16], writes=[bxt[i]])
                sc.op("scalar", lambda e, i=i: e.activation(out=sqf[:], in_=xt[i][:], func=AF.Square),
                      reads=[bxt[i]], writes=[bsqf])
                for hh in range(2):
                    sc.op("tensor", lambda e, hh=hh: e.matmul(ps[:, 6, :], lhsT=onesf[:], rhs=sqf[:, hh * 512:(hh + 1) * 512],
                                                              start=True, stop=True), reads=[bconst, bsqf], writes=[bps[6]])
                    sc.op("vector", lambda e, which=which: e.tensor_reduce(out=mx[:, 2:3], in_=ps[:, 6, :], axis=mybir.AxisListType.X,
                                                                           op=ALU.max), reads=[bps[6]], writes=[bmx])
                    sc.op("vector", lambda e, which=which: e.tensor_tensor(out=mx[:, which:which + 1], in0=mx[:, which:which + 1],
                                                                           in1=mx[:, 2:3], op=ALU.max), reads=[bmx], writes=[bmx])
                if which == 0:
                    nb0 = c0 // 256
                    sc.op("vector", lambda e, i=i, nb0=nb0: e.tensor_reduce(
                        out=kmean[:, nb0:nb0 + 4], in_=xt[i][:].rearrange("p (n t) -> p n t", t=256),
                        axis=mybir.AxisListType.X, op=ALU.add), reads=[bxt[i]], writes=[bkm])
                    sc.op("scalar", lambda e, i=i, c0=c0: e.copy(out=kaug[64:128, c0:c0 + 1024], in_=xt[i][:]),
                          reads=[bxt[i]], writes=[bk])
                else:
                    sc.op("scalar", lambda e, i=i, c0=c0: e.mul(out=qaug[64:128, c0:c0 + 1024], in_=xt[i][:], mul=0.125),
                          reads=[bxt[i]], writes=[bq])
                    sc.op("vector", lambda e, i=i, c0=c0: e.tensor_copy(out=qf[:, c0:c0 + 1024], in_=xt[i][:]),
                          reads=[bxt[i]], writes=[bqf])
        sc.op("vector", lambda e: e.tensor_tensor(out=mx[:, 3:4], in0=mx[:, 0:1], in1=mx[:, 1:2], op=ALU.mult),
              reads=[bmx], writes=[bmx])
        sc.op("scalar", lambda e: e.activation(out=mx[:, 3:4], in_=mx[:, 3:4], func=AF.Sqrt), reads=[bmx], writes=[bmx])
        sc.op("vector", lambda e: e.tensor_scalar(out=nbias[:], in0=mx[:, 3:4], scalar1=-0.125, scalar2=None, op0=ALU.mult),
              reads=[bmx], writes=[bnb])
        for t in range(NQ // 128):
            r = t // 2
            sc.op("tensor", lambda e, t=t: e.matmul(ps[:, 7, 0:NBLK], lhsT=qf[:, t * 128:(t + 1) * 128], rhs=kmean[:],
                                                    start=True, stop=True), reads=[bqf, bkm], writes=[bps[7]])
            sc.op("vector", lambda e, r=r: e.tensor_tensor(out=gs[:], in0=ps[:, 7, 0:NBLK], in1=pm[:, r * NBLK:(r + 1) * NBLK],
                                                           op=ALU.add), reads=[bps[7], bmask], writes=[bgs])
            sc.op("vector", lambda e: e.max(out=g8[:], in_=gs[:]), reads=[bgs], writes=[bg8])
            sc.op("vector", lambda e: e.tensor_scalar(out=m1[:], in0=gs[:], scalar1=g8[:, 2:3], scalar2=None, op0=ALU.is_ge),
                  reads=[bgs, bg8], writes=[bm1])
            sc.op("vector", lambda e: e.tensor_scalar(out=m2[:], in0=gs[:], scalar1=-1e29, scalar2=None, op0=ALU.is_gt),
                  reads=[bgs], writes=[bm2])
            sc.op("vector", lambda e: e.tensor_tensor(out=m1[:], in0=m1[:], in1=m2[:], op=ALU.mult),
                  reads=[bm1, bm2], writes=[bm1])
            sc.op("vector", lambda e, r=r: e.tensor_tensor(out=m1[:], in0=m1[:], in1=oh[:, r * NBLK:(r + 1) * NBLK], op=ALU.add),
                  reads=[bm1, bmask], writes=[bm1])
            sc.op("vector", lambda e: e.tensor_scalar(out=m2[:], in0=m1[:], scalar1=-1.0, scalar2=-NEG, op0=ALU.add, op1=ALU.mult),
                  reads=[bm1], writes=[bm2])
            sc.op("tensor", lambda e: e.transpose(ps[0:NBLK, 7, 128:256], m2[:], ident[:]),
                  reads=[bm2, bconst], writes=[bps[7]])
            sc.op("vector", lambda e, t=t: e.tensor_copy(out=qaug[0:32, t * 128:(t + 1) * 128], in_=ps[0:NBLK, 7, 128:256]),
                  reads=[bps[7]], writes=[bq])
        pi = [0]
        for r in range(MOBA_SLOTS):
            KT = 4 * r + 4
            po = 4 + (r % 2)
            for kt in range(KT):
                p = pi[0] % 3
                pi[0] += 1
                sc.op("tensor", lambda e, kt=kt, r=r, p=p: e.matmul(ps[:, p, 0:256], lhsT=kaug[:, kt * 128:(kt + 1) * 128],
                                                                   rhs=qaug[:, r * 256:(r + 1) * 256], start=True, stop=True),
                      reads=[bk, bq], writes=[bps[p]])
                if kt >= KT - 4:
                    j = kt - (KT - 4) + 4 * (r % 2)
                    s = j % 2
                    sc.op("vector", lambda e, p=p, j=j, s=s: e.tensor_tensor(out=stm[s][:], in0=ps[:, p, 0:256], in1=cm[:, j, :],
                                                                            op=ALU.add), reads=[bps[p], bmask], writes=[bstm[s]])
                    sc.op("scalar", lambda e, p=p, s=s: e.activation(out=pt[p][:], in_=stm[s][:], func=AF.Exp, bias=nbias[:, 0:1],
                                                                     scale=1.0), reads=[bstm[s], bnb], writes=[bpt[p]])
                else:
                    sc.op("scalar", lambda e, p=p: e.activation(out=pt[p][:], in_=ps[:, p, 0:256], func=AF.Exp, bias=nbias[:, 0:1],
                                                                scale=1.0), reads=[bps[p], bnb], writes=[bpt[p]])
                sc.op("tensor", lambda e, kt=kt, p=p, po=po, KT=KT: e.matmul(ps[:, po, 0:256], lhsT=vaug[:, kt, :], rhs=pt[p][:],
                                                                            start=(kt == 0), stop=(kt == KT - 1)),
                      reads=[bv, bpt[p]], writes=[bps[po]])
            sc.op("vector", lambda e, po=po: e.reciprocal(out=rec[:], in_=ps[64:128, po, 0:256]), reads=[bps[po]], writes=[brec])
            ys = r % 2
            sc.op("vector", lambda e, po=po, ys=ys: e.tensor_tensor(out=yst[ys][:], in0=ps[0:64, po, 0:256], in1=rec[:], op=ALU.mult),
                  reads=[bps[po], brec], writes=[byst[ys]])
            if fz is None:
                sc.dma("sync", lambda e, u=u, r=r, ys=ys: e.dma_start(out=yo[u, :, r * 256:(r + 1) * 256], in_=yst[ys][:]),
                       reads=[byst[ys]], key=pfx + f"myo{ys}")
            else:
                row0 = (r // 4) * 512 + u * 64
                sc.dma("sync", lambda e, row0=row0, r=r, ys=ys: e.dma_start(
                    out=fz.Ysend[row0:row0 + 64, (r % 4) * 256:(r % 4 + 1) * 256], in_=yst[ys][:]),
                    reads=[byst[ys], fz.bYs], key=pfx + f"myo{ys}")
    return byst


class SimpleProg:
    def __init__(self):
        self.nc = bass.Bass("TRN2", target_bir_lowering=False)
        self.es = ExitStack()
        self.in_names = []
        self.out_names = []

    din = TokProg.din
    dout = TokProg.dout
    sb = TokProg.sb

    def finish(self, sc, outbufs):
        sc.final_wait("sync", outbufs)
        with self.nc.Block() as block:
            sc.emit(block)
        self.es.close()
        return self.nc


def build_moba(nu=3):
    P = SimpleProg()
    sc = Sched(P.nc, P.es)
    ob = moba_emit(P, sc, nu)
    return P, P.finish(sc, ob)


def rope_tables():
    inv = np.exp(np.float32(-np.log(500000.0)) * np.arange(0, 16, 2, dtype=np.float32) / np.float32(16)).astype(np.float32)
    ang = (np.arange(S, dtype=np.float32)[:, None] * inv[None, :]).astype(np.float32)
    cos = np.cos(ang).astype(np.float32).T
    sin = np.sin(ang).astype(np.float32).T
    tab = np.zeros((2, 16, S), np.float32)
    tab[0, 0:8] = cos
    tab[0, 8:16] = cos
    tab[1, 0:8] = -sin
    tab[1, 8:16] = sin
    return tab


def moba_unit_inputs(uq, uk, uv, half, tab):
    blocks = HALF_BLOCKS[half]
    qpos = np.concatenate([np.arange(b * 256, (b + 1) * 256) for b in blocks])
    qT = np.ascontiguousarray(uq[qpos].T)
    kT = np.ascontiguousarray(uk.T)
    sw = np.r_[8:16, 0:8]
    return dict(mq=qT, mqs=np.ascontiguousarray(qT[sw]), mk=kT, mks=np.ascontiguousarray(kT[sw]), mv=np.ascontiguousarray(uv),
                ropeq=np.ascontiguousarray(tab[:, :, qpos]))


def moba_const_inputs(half):
    blocks = HALF_BLOCKS[half]
    pm = np.zeros((MOBA_SLOTS, NBLK), np.float32)
    oh = np.zeros((MOBA_SLOTS, NBLK), np.float32)
    for r, b in enumerate(blocks):
        pm[r, b:] = -1e30
        oh[r, b] = 1.0
    kk = np.arange(128)[:, None]
    qq = np.arange(256)[None, :]
    M0 = np.where(kk <= qq, 0.0, NEG).astype(np.float32)
    M1 = np.where(kk + 128 <= qq, 0.0, NEG).astype(np.float32)
    Z = np.zeros((128, 256), np.float32)
    cm = np.zeros((2, 4, 128, 256), np.float32)
    for par in range(2):
        r = par
        b = blocks[r]
        if b == 2 * r + 1:
            cm[par] = np.stack([Z, Z, M0, M1])
        else:
            cm[par] = np.stack([M0, M1, Z, Z])
    pmb = np.ascontiguousarray(np.broadcast_to(pm.reshape(1, -1), (128, MOBA_SLOTS * NBLK)))
    ohb = np.ascontiguousarray(np.broadcast_to(oh.reshape(1, -1), (128, MOBA_SLOTS * NBLK)))
    return dict(pm=pmb, oh=ohb, cm=cm)


def moba_shared_inputs(tab):
    boh = np.zeros((32, S), np.float32)
    for n in range(32):
        boh[n, n * 256:(n + 1) * 256] = 1.0
    return dict(ropek=tab, boh=boh, ident=np.eye(128, dtype=np.float32))


CH = 32


def conv_emit(P, sc, pfx="", fz=None):
    nc, es = P.nc, P.es
    T = TOK
    if fz is None:
        uc = P.din(pfx + "uc", [512, T + CH])
        yc = P.dout(pfx + "ycT", [256, T])
    else:
        yc = fz.Yfull
        flag_d = P.din("cflag", [128, 1])
        flag = P.sb(pfx + "cflag_sb", [128, 1], F32)
    cw = P.din(pfx + "cw", [128, 2, 31])
    cp = P.din(pfx + "cp", [128, 2, 3])
    idb = P.din("cident", [128, 128])
    a_t = [P.sb(pfx + f"ca{c}", [128, T + CH], F32) for c in range(2)]
    g_t = [P.sb(pfx + f"cg{c}", [128, T + CH], F32) for c in range(2)]
    hg = [P.sb(pfx + f"chg{c}", [128, T + CH], BF16) for c in range(2)]
    dg = [P.sb(pfx + f"cdg{c}", [128, 31, 128], BF16) for c in range(2)]
    cws = P.sb(pfx + "cws", [128, 2, 31], F32)
    cps = P.sb(pfx + "cps", [128, 2, 3], F32)
    idt = P.sb(pfx + "cidt", [128, 128], F32)
    onesf = P.sb(pfx + "cones", [128, 128], F32)
    epsc = P.sb(pfx + "ceps", [128, 1], F32)
    hc = [P.sb(pfx + f"chc{c}", [128, 512], F32) for c in range(2)]
    sq = [P.sb(pfx + f"csq{c}", [128, 512], F32) for c in range(2)]
    mean = P.sb(pfx + "cmean", [128, 512], F32)
    msq = P.sb(pfx + "cmsq", [128, 512], F32)
    var = P.sb(pfx + "cvar", [128, 512], F32)
    rstd = P.sb(pfx + "crstd", [128, 512], F32)
    tt_ = [P.sb(pfx + f"ctt{c}", [128, 512], F32) for c in range(2)]
    yo = [P.sb(pfx + f"cyo{c}", [128, 512], F32) for c in range(2)]
    ps = es.enter_context(nc.psum_tensor(pfx + "cps_", [128, 4, 512], F32)) if fz is None else fz.ps
    ba = [Buf(), Buf()]
    bg = [Buf(), Buf()]
    bhg = [Buf(), Buf()]
    bdg = [Buf(), Buf()]
    bc, bhc, bsq = Buf(), [Buf(), Buf()], [Buf(), Buf()]
    bmean, bmsq, bvar, brstd = Buf(), Buf(), Buf(), Buf()
    btt = [Buf(), Buf()]
    byo = [Buf(), Buf()]
    bps = [Buf() for _ in range(4)]

    sc.dma("sync", lambda e: e.dma_start(out=cws[:], in_=cw[:, :, :]), writes=[bc], key=pfx + "cc")
    sc.dma("sync", lambda e: e.dma_start(out=cps[:], in_=cp[:, :, :]), writes=[bc], key=pfx + "cc")
    sc.dma("sync", lambda e: e.dma_start(out=idt[:], in_=idb[:, :]), writes=[bc], key=pfx + "cc")
    sc.op("vector", lambda e: e.memset(onesf[:], 1.0), writes=[bc])
    sc.op("vector", lambda e: e.memset(epsc[:], EPS), writes=[bc])
    if fz is not None:
        sc.dma("sync", lambda e: e.dma_start(out=flag[:], in_=flag_d[:, :]), writes=[bc], key=pfx + "cc")

    def prev_rows(e, row0):
        return fz.LH[row0:row0 + 128, :]

    for c in range(2):
        if fz is None:
            sc.dma("sync", lambda e, c=c: e.dma_start(out=a_t[c][:], in_=uc[c * 128:(c + 1) * 128, :]), writes=[ba[c]],
                   key=pfx + f"ca{c}")
            sc.dma("sync", lambda e, c=c: e.dma_start(out=g_t[c][:], in_=uc[256 + c * 128:256 + (c + 1) * 128, :]),
                   writes=[bg[c]], key=pfx + f"cg{c}")
        else:
            sc.dma("sync", lambda e, c=c: e.dma_start(out=a_t[c][:, CH:], in_=fz.Usend[c * 128:(c + 1) * 128, :]),
                   reads=[fz.bU], writes=[ba[c]], key=pfx + f"ca{c}")
            sc.dma("sync", lambda e, c=c: e.dma_start(out=a_t[c][:, 0:CH], in_=prev_rows(e, c * 128)),
                   reads=[fz.bUr], writes=[ba[c]], key=pfx + f"ca{c}")
            sc.dma("sync", lambda e, c=c: e.dma_start(out=g_t[c][:, CH:], in_=fz.Usend[256 + c * 128:256 + (c + 1) * 128, :]),
                   reads=[fz.bU], writes=[bg[c]], key=pfx + f"cg{c}")
            sc.dma("sync", lambda e, c=c: e.dma_start(out=g_t[c][:, 0:CH], in_=prev_rows(e, 256 + c * 128)),
                   reads=[fz.bUr], writes=[bg[c]], key=pfx + f"cg{c}")
        sc.op("scalar", lambda e, c=c: e.activation(out=g_t[c][:], in_=g_t[c][:], func=AF.Sigmoid),
              reads=[bg[c]], writes=[bg[c]])
        sc.op("vector", lambda e, c=c: e.tensor_tensor(out=hg[c][:], in0=a_t[c][:], in1=g_t[c][:], op=ALU.mult),
              reads=[ba[c], bg[c]], writes=[bhg[c]])
        if fz is not None:
            sc.op("vector", lambda e, c=c: e.tensor_scalar(out=hg[c][:, 0:CH], in0=hg[c][:, 0:CH], scalar1=flag[:, 0:1],
                                                           scalar2=None, op0=ALU.mult), reads=[bhg[c], bc], writes=[bhg[c]])
        for k in range(31):
            sc.op("gpsimd", lambda e, c=c, k=k: e.tensor_scalar(out=dg[c][:, k, :], in0=idt[:], scalar1=cws[:, c, k:k + 1],
                                                                scalar2=None, op0=ALU.mult), reads=[bc], writes=[bdg[c]])
    for tt in range(T // 512):
        for c in range(2):
            for k in range(31):
                o = tt * 512 + 2 + k
                sc.op("tensor", lambda e, c=c, k=k, o=o: e.matmul(ps[:, c, :], lhsT=dg[c][:, k, :], rhs=hg[c][:, o:o + 512],
                                                                 start=(k == 0), stop=(k == 30)),
                      reads=[bdg[c], bhg[c]], writes=[bps[c]], inc=(k == 30))
            sc.op("scalar", lambda e, c=c: e.activation(out=hc[c][:], in_=ps[:, c, :], func=AF.Identity, bias=cps[:, c, 0:1],
                                                        scale=1.0), reads=[bps[c], bc], writes=[bhc[c]])
            sc.op("scalar", lambda e, c=c: e.activation(out=sq[c][:], in_=hc[c][:], func=AF.Square), reads=[bhc[c]],
                  writes=[bsq[c]])
        for c in range(2):
            sc.op("tensor", lambda e, c=c: e.matmul(ps[:, 2, :], lhsT=onesf[:], rhs=hc[c][:], start=(c == 0), stop=(c == 1)),
                  reads=[bc, bhc[c]], writes=[bps[2]])
        for c in range(2):
            sc.op("tensor", lambda e, c=c: e.matmul(ps[:, 3, :], lhsT=onesf[:], rhs=sq[c][:], start=(c == 0), stop=(c == 1)),
                  reads=[bc, bsq[c]], writes=[bps[3]])
        sc.op("vector", lambda e: e.tensor_scalar(out=mean[:], in0=ps[:, 2, :], scalar1=1.0 / 256, scalar2=None, op0=ALU.mult),
              reads=[bps[2]], writes=[bmean])
        sc.op("vector", lambda e: e.tensor_tensor(out=msq[:], in0=mean[:], in1=mean[:], op=ALU.mult), reads=[bmean], writes=[bmsq])
        sc.op("vector", lambda e: e.scalar_tensor_tensor(out=var[:], in0=ps[:, 3, :], scalar=1.0 / 256, in1=msq[:],
                                                         op0=ALU.mult, op1=ALU.subtract), reads=[bps[3], bmsq], writes=[bvar])
        sc.op("scalar", lambda e: e.activation(out=var[:], in_=var[:], func=AF.Sqrt, bias=epsc[:, 0:1], scale=1.0),
              reads=[bvar, bc], writes=[bvar])
        sc.op("vector", lambda e: e.reciprocal(out=rstd[:], in_=var[:]), reads=[bvar], writes=[brstd])
        for c in range(2):
            sc.op("vector", lambda e, c=c: e.tensor_tensor(out=tt_[c][:], in0=hc[c][:], in1=mean[:], op=ALU.subtract),
                  reads=[bhc[c], bmean], writes=[btt[c]])
            sc.op("vector", lambda e, c=c: e.tensor_tensor(out=tt_[c][:], in0=tt_[c][:], in1=rstd[:], op=ALU.mult),
                  reads=[btt[c], brstd], writes=[btt[c]])
            sc.op("scalar", lambda e, c=c: e.activation(out=yo[c][:], in_=tt_[c][:], func=AF.Silu, bias=cps[:, c, 2:3],
                                                        scale=cps[:, c, 1:2]), reads=[btt[c], bc], writes=[byo[c]])
            sc.dma("sync", lambda e, c=c, tt=tt: e.dma_start(out=yc[c * 128:(c + 1) * 128, tt * 512:(tt + 1) * 512], in_=yo[c][:]),
                   reads=[byo[c]] + ([] if fz is None else [fz.bY]), key=pfx + f"cyo{c}")
    return byo


def build_conv():
    P = SimpleProg()
    sc = Sched(P.nc, P.es)
    ob = conv_emit(P, sc)
    return P, P.finish(sc, ob)


def conv_inputs(u_b, j, conv_w, conv_b, ln_g, ln_b):
    t0 = j * TOK
    uc = np.zeros((512, TOK + CH), np.float32)
    lo = max(0, t0 - CH)
    uc[:, CH - (t0 - lo):] = u_b[lo:t0 + TOK, 0:512].T
    lay = lambda v: np.ascontiguousarray(v.reshape(2, 128).T)
    cw = np.ascontiguousarray(conv_w.T.reshape(2, 128, 31).transpose(1, 0, 2))
    cp = np.ascontiguousarray(np.stack([lay(conv_b), lay(ln_g), lay(ln_b)], axis=-1))
    return dict(uc=uc, cw=cw, cp=cp, cident=np.eye(128, dtype=np.float32))


GC = 64
NCH = S // GC
GSEG = 4
AX = mybir.AxisListType


def gdn_emit(P, sc, nu, pfx="", fz=None):
    import os
    STOP = float(os.environ.get("GDN_STOP", "99"))
    nc, es = P.nc, P.es
    if fz is None:
        raw_d = P.din(pfx + "graw", [nu, 3, 64, S + 3])
        gz_d = P.din(pfx + "gz", [nu, S, 64])
        ga_d = P.din(pfx + "ga", [nu, 64, NCH])
        gb_d = P.din(pfx + "gb", [nu, 64, NCH])
        go_d = P.dout(pfx + "go", [nu, S, 64])
    gcw_d = P.din(pfx + "gcw", [nu, 64, 12])
    gpar_d = P.din(pfx + "gpar", [nu, 64, 2])
    gng_d = P.din(pfx + "gng", [nu, 64, 64])
    gcst_d = P.din("gcst", [3, 64, 64])

    def unit_h(e, u):
        return fz.dyn(e, "gpsimd", ("gh", u))

    f = lambda n, shp: P.sb(pfx + n, shp, F32)
    cst = f("gcst_sb", [64, 3, 64])
    TriB = f("gTriB", [64, 8, 64])
    MB = f("gMB", [64, 8, 64])
    IB = f("gIB", [64, 8, 64])
    ones64 = f("gones", [64, 64])
    epsg = f("geps", [64, 1])
    gcw = f("gcw_sb", [64, 12])
    dgw = f("gdgw", [64, 12, 64])
    par = f("gpar_sb", [64, 2])
    negA = f("gnegA", [64, 1])
    ngb = f("gngb", [64, 64])
    a_t = f("ga_sb", [64, NCH])
    b_t = f("gb_sb", [64, NCH])
    g_t = f("gg", [64, NCH])
    beta = f("gbeta", [64, NCH])
    gc = f("ggc", [64, NCH])
    egc = f("gegc", [64, NCH])
    eglb = f("geglb", [64, NCH])
    edec = f("gedec", [64, NCH])
    bgk = f("gbgk", [64, NCH])
    SEGT = S // GSEG
    SEGC = NCH // GSEG
    raw = [f(f"graw{i}", [64, 515]) for i in range(2)]
    xa = [f(f"gxa{i}", [64, 512]) for i in range(2)]
    xq = f("gxq", [64, 512])
    rn = f("grn", [64, 512])
    qnT = f("gqnT", [64, SEGT])
    knT = f("gknT", [64, SEGT])
    Kt = f("gKt", [64, SEGC, 64])
    Vt = f("gVt", [64, SEGC, 64])
    oseg = f("goseg", [64, SEGC, 64])
    zseg = f("gzseg", [64, SEGC, 64])
    osq = f("gosq", [64, SEGC, 64])
    oss = f("goss", [64, SEGC])
    rhsD = f("grhsD", [64, 8, 64])
    ED = f("gED", [64, 8, 64])
    EDT = f("gEDT", [64, 8, 64])
    Lp = [f(f"gL{i}", [64, 8, 64]) for i in range(2)]
    Np = [f(f"gN{i}", [64, 8, 64]) for i in range(2)]
    Pm = f("gP", [64, 8, 64])
    Kbg = f("gKbg", [64, 8, 64])
    Vb = f("gVb", [64, 8, 64])
    kdec = f("gkdec", [64, 8, 64])
    u_sb = f("gu", [64, 8, 64])
    wT = f("gwT", [64, 8, 64])
    qkT = f("gqkT", [64, 8, 64])
    St = f("gS", [64, 64])
    vn = [f(f"gvn{i}", [64, 64]) for i in range(2)]
    As = [f(f"gAs{i}", [64, 64]) for i in range(2)]
    ps = es.enter_context(nc.psum_tensor(pfx + "gps", [64, 8, 512], F32)) if fz is None else fz.ps[0:64, :, :]

    B_ = lambda: Buf()
    bcst, bpar, bg = B_(), B_(), B_()
    braw = [B_(), B_()]
    bxa = [B_(), B_()]
    bxq, brn, bqn, bkn, bKt, bVt, boseg, bz, bosq, boss = (B_() for _ in range(10))
    brhsD, bED, bEDT, bP, bKbg, bVb, bkdec, bu, bwT, bqkT, bS = (B_() for _ in range(11))
    bL = [B_(), B_()]
    bN = [B_(), B_()]
    bvn = [B_(), B_()]
    bAs = [B_(), B_()]
    bps = [B_() for _ in range(8)]
    wk = [0]

    def nps():
        i = wk[0] % 4
        wk[0] += 1
        return i

    sc.dma("sync", lambda e: e.dma_start(out=cst[:], in_=gcst_d.rearrange("a p q -> p a q")), writes=[bcst], key=pfx + "gc")
    sc.op("vector", lambda e: e.memset(ones64[:], 1.0), writes=[bcst])
    sc.op("vector", lambda e: e.memset(epsg[:], EPS), writes=[bcst])
    for j in range(8):
        sc.op("vector", lambda e, j=j: e.tensor_copy(out=TriB[:, j, :], in_=cst[:, 0, :]), reads=[bcst], writes=[bcst])
        sc.op("vector", lambda e, j=j: e.tensor_copy(out=MB[:, j, :], in_=cst[:, 1, :]), reads=[bcst], writes=[bcst])
        sc.op("vector", lambda e, j=j: e.tensor_copy(out=IB[:, j, :], in_=cst[:, 2, :]), reads=[bcst], writes=[bcst])
    Tri = cst[:, 0, :]
    I64 = cst[:, 2, :]

    def bc_n(t, n0):
        return t[:, n0:n0 + 8].unsqueeze(2).to_broadcast([64, 8, 64])

    for u in range(nu):
        sc.dma("sync", lambda e, u=u: e.dma_start(out=gcw[:], in_=gcw_d[u]), writes=[bpar], key=pfx + "gp")
        sc.dma("sync", lambda e, u=u: e.dma_start(out=par[:], in_=gpar_d[u]), writes=[bpar], key=pfx + "gp")
        sc.dma("sync", lambda e, u=u: e.dma_start(out=ngb[:], in_=gng_d[u]), writes=[bpar], key=pfx + "gp")
        if fz is None:
            sc.dma("sync", lambda e, u=u: e.dma_start(out=a_t[:], in_=ga_d[u]), writes=[bpar], key=pfx + "gp")
            sc.dma("sync", lambda e, u=u: e.dma_start(out=b_t[:], in_=gb_d[u]), writes=[bpar], key=pfx + "gp")
        else:
            for rr in range(4):
                for (dst, ro) in ((a_t, 3200), (b_t, 3206)):
                    def absrc(e, u=u, rr=rr, ro=ro):
                        return fz.LAB[u, (0 if ro == 3200 else 1):(1 if ro == 3200 else 2), rr * TOK:(rr + 1) * TOK].rearrange(
                            "o (n s) -> s (o n)", s=64)
                    sc.dma("gpsimd", lambda e, dst=dst, rr=rr, absrc=absrc: e.dma_start(
                        out=dst[:, rr * 32:(rr + 1) * 32], in_=absrc(e), allow_slow_non_contiguous=True),
                        reads=[fz.bUr], writes=[bpar], key=pfx + "gp")
        for k in range(12):
            sc.op("gpsimd", lambda e, k=k: e.tensor_scalar(out=dgw[:, k, :], in0=I64, scalar1=gcw[:, k:k + 1], scalar2=None,
                                                           op0=ALU.mult), reads=[bcst, bpar], writes=[bpar])
        sc.op("scalar", lambda e: e.activation(out=negA[:], in_=par[:, 0:1], func=AF.Exp), reads=[bpar], writes=[bg])
        sc.op("vector", lambda e: e.tensor_scalar(out=negA[:], in0=negA[:], scalar1=-1.0, scalar2=None, op0=ALU.mult),
              reads=[bg], writes=[bg])
        sc.op("scalar", lambda e: e.activation(out=g_t[:], in_=a_t[:], func=AF.Exp, bias=par[:, 1:2], scale=1.0),
              reads=[bpar, bg], writes=[bg])
        sc.op("scalar", lambda e: e.activation(out=g_t[:], in_=g_t[:], func=AF.Ln, bias=1.0, scale=1.0), reads=[bg], writes=[bg])
        sc.op("vector", lambda e: e.tensor_scalar(out=g_t[:], in0=g_t[:], scalar1=negA[:, 0:1], scalar2=None, op0=ALU.mult),
              reads=[bg], writes=[bg])
        sc.op("scalar", lambda e: e.activation(out=beta[:], in_=b_t[:], func=AF.Sigmoid), reads=[bpar, bg], writes=[bg])
        sc.op("tensor", lambda e: e.matmul(ps[:, 7, 0:NCH], lhsT=Tri, rhs=g_t[:], start=True, stop=True),
              reads=[bcst, bg], writes=[bps[7]])
        sc.op("tensor", lambda e: e.matmul(ps[:, 7, NCH:2 * NCH], lhsT=ones64[:], rhs=g_t[:], start=True, stop=True),
              reads=[bcst, bg], writes=[bps[7]])
        sc.op("vector", lambda e: e.tensor_copy(out=gc[:], in_=ps[:, 7, 0:NCH]), reads=[bps[7], bg], writes=[bg])
        sc.op("vector", lambda e: e.tensor_copy(out=eglb[:], in_=ps[:, 7, NCH:2 * NCH]), reads=[bps[7], bg], writes=[bg])
        sc.op("vector", lambda e: e.tensor_tensor(out=edec[:], in0=eglb[:], in1=gc[:], op=ALU.subtract), reads=[bg], writes=[bg])
        sc.op("scalar", lambda e: e.activation(out=egc[:], in_=gc[:], func=AF.Exp), reads=[bg], writes=[bg])
        sc.op("scalar", lambda e: e.activation(out=eglb[:], in_=eglb[:], func=AF.Exp), reads=[bg], writes=[bg])
        sc.op("scalar", lambda e: e.activation(out=edec[:], in_=edec[:], func=AF.Exp), reads=[bg], writes=[bg])
        sc.op("vector", lambda e: e.tensor_tensor(out=bgk[:], in0=beta[:], in1=egc[:], op=ALU.mult), reads=[bg], writes=[bg])
        sc.op("vector", lambda e: e.memset(St[:], 0.0), writes=[bS])
        if STOP <= 1:
            return [bS]

        for seg in range(GSEG):
            for tt in range(SEGT // 512):
                c0 = seg * SEGT + tt * 512
                for j in range(3):
                    ri = (tt * 3 + j) % 2
                    if fz is None:
                        sc.dma("sync", lambda e, u=u, j=j, ri=ri, c0=c0: e.dma_start(out=raw[ri][:], in_=raw_d[u, j, :, c0:c0 + 515]),
                               writes=[braw[ri]], key=pfx + f"graw{ri}")
                    else:
                        rr, t0 = c0 // TOK, c0 % TOK

                        def rsrc(e, rr_, ta, tb, u=u, j=j):
                            return fz.LG[u, j, :, rr_ * TOK + ta:rr_ * TOK + tb]
                        sc.dma("gpsimd", lambda e, ri=ri, rr=rr, t0=t0, rsrc=rsrc: e.dma_start(out=raw[ri][:, 3:515],
                                                                                            in_=rsrc(e, rr, t0, t0 + 512)),
                               reads=[fz.bUr], writes=[braw[ri]], key=pfx + f"graw{ri}")
                        if t0 >= 3:
                            sc.dma("gpsimd", lambda e, ri=ri, rr=rr, t0=t0, rsrc=rsrc: e.dma_start(out=raw[ri][:, 0:3],
                                                                                                in_=rsrc(e, rr, t0 - 3, t0)),
                                   reads=[fz.bUr], writes=[braw[ri]], key=pfx + f"graw{ri}")
                        elif rr > 0:
                            sc.dma("gpsimd", lambda e, ri=ri, rr=rr, rsrc=rsrc: e.dma_start(out=raw[ri][:, 0:3],
                                                                                         in_=rsrc(e, rr - 1, TOK - 3, TOK)),
                                   reads=[fz.bUr], writes=[braw[ri]], key=pfx + f"graw{ri}")
                        else:
                            sc.op("vector", lambda e, ri=ri: e.memset(raw[ri][:, 0:3], 0.0), writes=[braw[ri]])
                    p1 = nps()
                    for k in range(4):
                        sc.op("tensor", lambda e, j=j, k=k, ri=ri, p1=p1: e.matmul(ps[:, p1, :], lhsT=dgw[:, j * 4 + k, :],
                                                                                 rhs=raw[ri][:, k:k + 512], start=(k == 0), stop=(k == 3)),
                              reads=[bpar, braw[ri]], writes=[bps[p1]], inc=(k == 3))
                    xi = j % 2
                    sc.op("scalar", lambda e, xi=xi, p1=p1: e.activation(out=xa[xi][:], in_=ps[:, p1, :], func=AF.Silu),
                          reads=[bps[p1]], writes=[bxa[xi]])
                    if j < 2:
                        sc.op("scalar", lambda e, xi=xi: e.activation(out=xq[:], in_=xa[xi][:], func=AF.Square),
                              reads=[bxa[xi]], writes=[bxq])
                        p2 = nps()
                        sc.op("tensor", lambda e, p2=p2: e.matmul(ps[:, p2, :], lhsT=ones64[:], rhs=xq[:], start=True, stop=True),
                              reads=[bcst, bxq], writes=[bps[p2]])
                        sc.op("scalar", lambda e, p2=p2: e.activation(out=rn[:], in_=ps[:, p2, :], func=AF.Sqrt, bias=epsg[:, 0:1],
                                                                      scale=1.0), reads=[bps[p2], bcst], writes=[brn])
                        sc.op("vector", lambda e: e.reciprocal(out=rn[:], in_=rn[:]), reads=[brn], writes=[brn])
                        dst, bd = (qnT, bqn) if j == 0 else (knT, bkn)
                        scl = 0.125 if j == 0 else 1.0
                        sc.op("vector", lambda e, xi=xi, dst=dst, tt=tt, scl=scl: e.scalar_tensor_tensor(
                            out=dst[:, tt * 512:(tt + 1) * 512], in0=xa[xi][:], scalar=scl, in1=rn[:], op0=ALU.mult, op1=ALU.mult),
                            reads=[bxa[xi], brn], writes=[bd])
                    if j >= 1:
                        srcT = knT[:, tt * 512:(tt + 1) * 512] if j == 1 else xa[xi][:]
                        bsrc = bkn if j == 1 else bxa[xi]
                        p3 = nps()
                        for cj in range(8):
                            sc.op("tensor", lambda e, srcT=srcT, cj=cj, p3=p3: e.transpose(ps[:, p3, cj * 64:(cj + 1) * 64],
                                                                                         srcT[:, cj * 64:(cj + 1) * 64], I64),
                                  reads=[bsrc, bcst], writes=[bps[p3]], inc=(cj == 7))
                        dstT, bdt = (Kt, bKt) if j == 1 else (Vt, bVt)
                        sc.op("vector", lambda e, dstT=dstT, tt=tt, p3=p3: e.tensor_copy(
                            out=dstT[:, tt * 8:(tt + 1) * 8, :], in_=ps[:, p3, :].rearrange("p (a b) -> p a b", b=64)),
                            reads=[bps[p3]], writes=[bdt])
            if fz is None:
                sc.dma("sync", lambda e, u=u, seg=seg: e.dma_start(
                    out=zseg[:], in_=gz_d[u, seg * SEGT:(seg + 1) * SEGT, :].rearrange("(n s) d -> s n d", s=64)),
                    writes=[bz], key=pfx + "gz")
            else:
                def zsrc(e, u=u, seg=seg):
                    return fz.LZ[u, :, seg * TOK:(seg + 1) * TOK]
                sc.dma("gpsimd", lambda e, zsrc=zsrc: e.dma_start(out=zseg[:].rearrange("p a b -> p (a b)"), in_=zsrc(e)),
                       reads=[fz.bUr], writes=[bz], key=pfx + "gz")
            if STOP <= 2:
                return [bz, bKt, bVt, bqn]
            for gi in range(SEGC // 8):
                l0 = gi * 8
                n0 = seg * SEGC + l0
                v3 = lambda t: t[:]
                pk, pd, pdt = nps(), nps(), nps()
                for j in range(8):
                    cs = slice((l0 + j) * 64, (l0 + j + 1) * 64)
                    sc.op("tensor", lambda e, j=j, cs=cs, pk=pk: e.matmul(ps[:, pk, j * 64:(j + 1) * 64], lhsT=knT[:, cs], rhs=knT[:, cs],
                                                                         start=True, stop=True), reads=[bkn], writes=[bps[pk]], inc=(j == 7))
                sc.op("vector", lambda e, n0=n0: e.tensor_tensor(out=rhsD[:], in0=MB[:], in1=bc_n(g_t, n0), op=ALU.mult),
                      reads=[bcst, bg], writes=[brhsD])
                if STOP <= 2.1:
                    return [brhsD, bps[pk]]
                sc.op("tensor", lambda e, pd=pd: e.matmul(ps[:, pd, :], lhsT=Tri, rhs=rhsD[:].rearrange("p a b -> p (a b)"),
                                                          start=True, stop=True), reads=[bcst, brhsD], writes=[bps[pd]])
                for j in range(8):
                    sc.op("tensor", lambda e, j=j, pdt=pdt: e.matmul(ps[:, pdt, j * 64:(j + 1) * 64], lhsT=rhsD[:, j, :], rhs=Tri,
                                                                    start=True, stop=True), reads=[bcst, brhsD], writes=[bps[pdt]], inc=(j == 7))
                r3 = lambda ap: ap.rearrange("p (a b) -> p a b", b=64)
                sc.op("scalar", lambda e, pd=pd: e.activation(out=ED[:], in_=r3(ps[:, pd, :]), func=AF.Exp), reads=[bps[pd]], writes=[bED])
                sc.op("scalar", lambda e, pdt=pdt: e.activation(out=EDT[:], in_=r3(ps[:, pdt, :]), func=AF.Exp), reads=[bps[pdt]], writes=[bEDT])
                if STOP <= 2.2:
                    return [bED, bEDT]
                sc.op("vector", lambda e, pk=pk: e.tensor_tensor(out=Lp[0][:], in0=r3(ps[:, pk, :]), in1=ED[:], op=ALU.mult),
                      reads=[bps[pk], bED], writes=[bL[0]])
                sc.op("vector", lambda e, n0=n0: e.tensor_tensor(out=Lp[0][:], in0=Lp[0][:], in1=bc_n(beta, n0), op=ALU.mult),
                      reads=[bL[0], bg], writes=[bL[0]])
                sc.op("vector", lambda e: e.tensor_tensor(out=Lp[0][:], in0=Lp[0][:], in1=MB[:], op=ALU.mult),
                      reads=[bL[0], bcst], writes=[bL[0]])
                if STOP <= 2.3:
                    return [bL[0]]
                pn = nps()
                for j in range(8):
                    sc.op("tensor", lambda e, j=j, pn=pn: e.matmul(ps[:, pn, j * 64:(j + 1) * 64], lhsT=Lp[0][:, j, :], rhs=I64,
                                                                  start=True, stop=True),
                          reads=[bL[0], bcst], writes=[bps[pn]], inc=(j == 7))
                sc.op("scalar", lambda e, pn=pn: e.copy(out=Np[0][:], in_=r3(ps[:, pn, :])), reads=[bps[pn]], writes=[bN[0]])
                sc.op("vector", lambda e: e.tensor_tensor(out=Pm[:], in0=IB[:], in1=Np[0][:], op=ALU.subtract),
                      reads=[bN[0], bcst], writes=[bP])
                if STOP <= 2.4:
                    return [bP, bN[0]]
                cur = 0
                for lvl in range(5):
                    nxt = 1 - cur
                    pl = nps()
                    for j in range(8):
                        sc.op("tensor", lambda e, j=j, pl=pl, cur=cur: e.matmul(ps[:, pl, j * 64:(j + 1) * 64], lhsT=Np[cur][:, j, :],
                                                                               rhs=Lp[cur][:, j, :], start=True, stop=True),
                              reads=[bN[cur], bL[cur]], writes=[bps[pl]], inc=(j == 7))
                    if lvl < 4:
                        pn2 = nps()
                        for j in range(8):
                            sc.op("tensor", lambda e, j=j, pn2=pn2, cur=cur: e.matmul(ps[:, pn2, j * 64:(j + 1) * 64], lhsT=Lp[cur][:, j, :],
                                                                                     rhs=Np[cur][:, j, :], start=True, stop=True),
                                  reads=[bN[cur], bL[cur]], writes=[bps[pn2]], inc=(j == 7))
                    sc.op("scalar", lambda e, pl=pl, nxt=nxt: e.copy(out=Lp[nxt][:], in_=r3(ps[:, pl, :])), reads=[bps[pl]], writes=[bL[nxt]])
                    if lvl < 4:
                        sc.op("vector", lambda e, pn2=pn2, nxt=nxt: e.tensor_copy(out=Np[nxt][:], in_=r3(ps[:, pn2, :])),
                              reads=[bps[pn2]], writes=[bN[nxt]])
                    pu = nps()
                    for j in range(8):
                        sc.op("tensor", lambda e, j=j, pu=pu, nxt=nxt: e.matmul(ps[:, pu, j * 64:(j + 1) * 64], lhsT=Lp[nxt][:, j, :],
                                                                               rhs=Pm[:, j, :], start=True, stop=True),
                              reads=[bL[nxt], bP], writes=[bps[pu]], inc=(j == 7))
                    sc.op("vector", lambda e, pu=pu: e.tensor_tensor(out=Pm[:], in0=Pm[:], in1=r3(ps[:, pu, :]), op=ALU.add),
                          reads=[bps[pu], bP], writes=[bP])
                    cur = nxt
                if STOP <= 2.5:
                    return [bP]
                sc.op("vector", lambda e, l0=l0, n0=n0: e.tensor_tensor(out=Kbg[:], in0=Kt[:, l0:l0 + 8, :], in1=bc_n(bgk, n0), op=ALU.mult),
                      reads=[bKt, bg], writes=[bKbg])
                sc.op("vector", lambda e, l0=l0, n0=n0: e.tensor_tensor(out=Vb[:], in0=Vt[:, l0:l0 + 8, :], in1=bc_n(beta, n0), op=ALU.mult),
                      reads=[bVt, bg], writes=[bVb])
                sc.op("vector", lambda e, l0=l0, n0=n0: e.tensor_tensor(out=kdec[:], in0=Kt[:, l0:l0 + 8, :], in1=bc_n(edec, n0), op=ALU.mult),
                      reads=[bKt, bg], writes=[bkdec])
                p_u, p_w, p_q = nps(), nps(), nps()
                for j in range(8):
                    sc.op("tensor", lambda e, j=j, p_u=p_u: e.matmul(ps[:, p_u, j * 64:(j + 1) * 64], lhsT=Pm[:, j, :], rhs=Vb[:, j, :],
                                                                    start=True, stop=True), reads=[bP, bVb], writes=[bps[p_u]], inc=(j == 7))
                for j in range(8):
                    sc.op("tensor", lambda e, j=j, p_w=p_w: e.matmul(ps[:, p_w, j * 64:(j + 1) * 64], lhsT=Kbg[:, j, :], rhs=Pm[:, j, :],
                                                                    start=True, stop=True), reads=[bP, bKbg], writes=[bps[p_w]], inc=(j == 7))
                for j in range(8):
                    cs = slice((l0 + j) * 64, (l0 + j + 1) * 64)
                    sc.op("tensor", lambda e, j=j, cs=cs, p_q=p_q: e.matmul(ps[:, p_q, j * 64:(j + 1) * 64], lhsT=knT[:, cs], rhs=qnT[:, cs],
                                                                           start=True, stop=True), reads=[bkn, bqn], writes=[bps[p_q]], inc=(j == 7))
                sc.op("scalar", lambda e, p_u=p_u: e.copy(out=u_sb[:], in_=r3(ps[:, p_u, :])), reads=[bps[p_u]], writes=[bu])
                sc.op("scalar", lambda e, p_w=p_w: e.copy(out=wT[:], in_=r3(ps[:, p_w, :])), reads=[bps[p_w]], writes=[bwT])
                sc.op("vector", lambda e, p_q=p_q: e.tensor_tensor(out=qkT[:], in0=r3(ps[:, p_q, :]), in1=EDT[:], op=ALU.mult),
                      reads=[bps[p_q], bEDT], writes=[bqkT])
                sc.op("vector", lambda e: e.tensor_tensor(out=qkT[:], in0=qkT[:], in1=TriB[:], op=ALU.mult), reads=[bqkT, bcst], writes=[bqkT])
                if STOP <= 3:
                    return [bqkT, bu, bwT]
                for j in range(8):
                    n = n0 + j
                    l = l0 + j
                    cs = slice(l * 64, (l + 1) * 64)
                    i2 = j % 2
                    bx_, by_ = 4 + i2, 6 + i2
                    sc.op("tensor", lambda e, j=j, bx_=bx_: e.matmul(ps[:, bx_, 0:64], lhsT=wT[:, j, :], rhs=St[:], start=True, stop=True),
                          reads=[bwT, bS], writes=[bps[bx_]])
                    sc.op("tensor", lambda e, cs=cs, by_=by_: e.matmul(ps[:, by_, 0:64], lhsT=qnT[:, cs], rhs=St[:], start=True, stop=True),
                          reads=[bqn, bS], writes=[bps[by_]])
                    sc.op("vector", lambda e, j=j, bx_=bx_, i2=i2: e.tensor_tensor(out=vn[i2][:], in0=u_sb[:, j, :], in1=ps[:, bx_, 0:64],
                                                                                 op=ALU.subtract), reads=[bu, bps[bx_]], writes=[bvn[i2]])
                    sc.op("scalar", lambda e, by_=by_, i2=i2, n=n: e.activation(out=As[i2][:], in_=ps[:, by_, 0:64], func=AF.Copy,
                                                                               scale=egc[:, n:n + 1]), reads=[bps[by_], bg], writes=[bAs[i2]])
                    sc.op("tensor", lambda e, j=j, bx_=bx_, i2=i2: e.matmul(ps[:, bx_, 64:128], lhsT=qkT[:, j, :], rhs=vn[i2][:],
                                                                          start=True, stop=True), reads=[bqkT, bvn[i2]], writes=[bps[bx_]], inc=False)
                    sc.op("tensor", lambda e, j=j, bx_=bx_, i2=i2: e.matmul(ps[:, bx_, 128:192], lhsT=kdec[:, j, :], rhs=vn[i2][:],
                                                                          start=True, stop=True), reads=[bkdec, bvn[i2]], writes=[bps[bx_]])
                    sc.op("vector", lambda e, bx_=bx_, n=n: e.scalar_tensor_tensor(out=St[:], in0=St[:], scalar=eglb[:, n:n + 1],
                                                                                  in1=ps[:, bx_, 128:192], op0=ALU.mult, op1=ALU.add),
                          reads=[bS, bg, bps[bx_]], writes=[bS])
                    sc.op("vector", lambda e, bx_=bx_, i2=i2, l=l: e.tensor_tensor(out=oseg[:, l, :], in0=As[i2][:], in1=ps[:, bx_, 64:128],
                                                                                 op=ALU.add), reads=[bAs[i2], bps[bx_]], writes=[boseg])
                if STOP <= 4:
                    return [boseg, bS]
            sc.op("gpsimd", lambda e: e.tensor_tensor(out=osq[:], in0=oseg[:], in1=oseg[:], op=ALU.mult), reads=[boseg], writes=[bosq])
            sc.op("vector", lambda e: e.tensor_reduce(out=oss[:], in_=osq[:], axis=AX.X, op=ALU.add), reads=[bosq], writes=[boss])
            sc.op("scalar", lambda e: e.activation(out=oss[:], in_=oss[:], func=AF.Sqrt, bias=epsg[:, 0:1], scale=1.0 / 64),
                  reads=[boss, bcst], writes=[boss])
            sc.op("vector", lambda e: e.reciprocal(out=oss[:], in_=oss[:]), reads=[boss], writes=[boss])
            sc.op("vector", lambda e: e.tensor_tensor(out=osq[:], in0=oseg[:], in1=oss[:].unsqueeze(2).to_broadcast([64, SEGC, 64]),
                                                      op=ALU.mult), reads=[boseg, boss], writes=[bosq])
            sc.op("gpsimd", lambda e: e.tensor_tensor(out=osq[:], in0=osq[:], in1=ngb[:].unsqueeze(1).to_broadcast([64, SEGC, 64]),
                                                      op=ALU.mult), reads=[bosq, bpar], writes=[bosq])
            sc.op("scalar", lambda e: e.activation(out=zseg[:], in_=zseg[:], func=AF.Silu), reads=[bz], writes=[bz])
            if fz is None:
                sc.op("vector", lambda e: e.tensor_tensor(out=osq[:], in0=osq[:], in1=zseg[:], op=ALU.mult), reads=[bosq, bz], writes=[bosq])
                sc.dma("sync", lambda e, u=u, seg=seg: e.dma_start(
                    out=go_d[u, seg * SEGT:(seg + 1) * SEGT, :].rearrange("(n s) d -> s n d", s=64), in_=osq[:]),
                    reads=[bosq], key=pfx + "go")
            else:
                oT = oseg[:].rearrange("p a b -> p (a b)")
                zT = zseg[:].rearrange("p a b -> p (a b)")
                for g4 in range(SEGC // 8):
                    pt_ = nps()
                    for j in range(8):
                        sc.op("tensor", lambda e, j=j, g4=g4, pt_=pt_: e.transpose(ps[:, pt_, j * 64:(j + 1) * 64], osq[:, g4 * 8 + j, :], I64),
                              reads=[bosq, bcst], writes=[bps[pt_]], inc=(j == 7))
                    sc.op("vector", lambda e, g4=g4, pt_=pt_: e.tensor_tensor(out=oT[:, g4 * 512:(g4 + 1) * 512], in0=ps[:, pt_, :],
                                                                            in1=zT[:, g4 * 512:(g4 + 1) * 512], op=ALU.mult),
                          reads=[bps[pt_], bz, bosq], writes=[boseg])
                for kk in range(2):
                    row0 = seg * 512 + 192 + (u * 2 + kk) * 64
                    sc.dma("sync", lambda e, row0=row0, kk=kk: e.dma_start(out=fz.Ysend[row0:row0 + 64, :],
                                                                          in_=oT[:, kk * 1024:(kk + 1) * 1024]),
                           reads=[boseg, fz.bYs], key=pfx + "go")
    return [bosq, boseg]


def build_gdn(nu=2):
    P = SimpleProg()
    sc = Sched(P.nc, P.es)
    ob = gdn_emit(P, sc, nu)
    return P, P.finish(sc, ob)


def gdn_const_inputs():
    i = np.arange(64)
    tri = (i[:, None] <= i[None, :]).astype(np.float32)
    ms = (i[:, None] > i[None, :]).astype(np.float32)
    return dict(gcst=np.stack([tri, ms, np.eye(64, dtype=np.float32)]))


def gdn_unit_inputs(ug, h, gdn_conv_w, a_log, dt_bias, norm_g):
    GW = 384
    raw = np.zeros((3, 64, S + 3), np.float32)
    cw = np.zeros((64, 12), np.float32)
    for j in range(3):
        cols = slice(j * GW + h * 64, j * GW + (h + 1) * 64)
        raw[j, :, 3:] = ug[:, cols].T
        cw[:, j * 4:(j + 1) * 4] = gdn_conv_w[:, cols].T
    z = np.ascontiguousarray(ug[:, 3 * GW + h * 64:3 * GW + (h + 1) * 64])
    a = np.ascontiguousarray(ug[:, 4 * GW + h].reshape(NCH, 64).T)
    b = np.ascontiguousarray(ug[:, 4 * GW + 6 + h].reshape(NCH, 64).T)
    par = np.zeros((64, 2), np.float32)
    par[:, 0] = a_log[h]
    par[:, 1] = dt_bias[h]
    ng = np.ascontiguousarray(np.broadcast_to(norm_g[None, :], (64, 64))).astype(np.float32)
    return dict(graw=raw, gcw=cw, gz=z, ga=a, gb=b, gpar=par, gng=ng)


def _lay(v):
    return np.ascontiguousarray(np.asarray(v, np.float32).reshape(-1, 128).T)


_PROGS = {}


def _prog(key, builder):
    if key not in _PROGS:
        _PROGS[key] = builder()
    return _PROGS[key]


def _run(nc, in_maps):
    res = run_bass_kernel_spmd(nc, in_maps, core_ids=list(range(NCORES)))
    return res.results


def _tok_launch(key, stages, inp, xT_list, yT_list=None):
    def mk():
        p = TokProg(stages)
        return p, p.build()
    P, nc = _prog(key, mk)
    maps = []
    for c in range(NCORES):
        b = c // 4
        m = {"xT": xT_list[c], "cT": _lay(inp["c"][b])}
        for name in P.in_names:
            if name in m:
                continue
            if name.startswith("yT"):
                m[name] = yT_list[c]
            elif name == "final_g":
                m[name] = _lay(inp["final_g"])
            else:
                base, l = name[:-1], int(name[-1])
                arr = np.asarray(inp[base][l], np.float32)
                if base == "b_ada" or base.startswith("ln_"):
                    arr = _lay(arr)
                m[name] = np.ascontiguousarray(arr)
        maps.append(m)
    return _run(nc, maps)


def _mixer(inp, l, u):
    y = np.zeros((B, S, D), np.float32)
    P, nc = _prog("conv", build_conv)
    maps = []
    for c in range(NCORES):
        b, j = c // 4, c % 4
        maps.append(conv_inputs(u[b], j, np.asarray(inp["conv_w"][l]), np.asarray(inp["conv_b"][l]),
                                np.asarray(inp["conv_ln_g"][l]), np.asarray(inp["conv_ln_b"][l])))
    res = _run(nc, maps)
    for c in range(NCORES):
        b, j = c // 4, c % 4
        y[b, j * TOK:(j + 1) * TOK, 0:256] = res[c]["ycT"].T
    P, nc = _prog("moba", lambda: build_moba(3))
    tab = rope_tables()
    shared = moba_shared_inputs(tab)
    consts = [moba_const_inputs(0), moba_const_inputs(1)]
    maps = []
    for c in range(NCORES):
        b, cc = c // 4, c % 4
        units = []
        for s in range(3):
            combo = 3 * cc + s
            h, half = combo // 2, combo % 2
            q = u[b, :, 512 + h * 64:512 + (h + 1) * 64]
            k = u[b, :, 512 + 384 + h * 64:512 + 384 + (h + 1) * 64]
            v = u[b, :, 512 + 768 + h * 64:512 + 768 + (h + 1) * 64]
            d = moba_unit_inputs(q, k, v, half, tab)
            d.update(consts[half])
            units.append(d)
        m = {k_: np.ascontiguousarray(np.stack([un[k_] for un in units])) for k_ in units[0]}
        m.update(shared)
        maps.append(m)
    res = _run(nc, maps)
    for c in range(NCORES):
        b, cc = c // 4, c % 4
        for s in range(3):
            combo = 3 * cc + s
            h, half = combo // 2, combo % 2
            qpos = np.concatenate([np.arange(bl * 256, (bl + 1) * 256) for bl in HALF_BLOCKS[half]])
            y[b, qpos, 256 + h * 64:256 + (h + 1) * 64] = res[c]["moT"][s].T
    P, nc = _prog("gdn", lambda: build_gdn(2))
    gconst = gdn_const_inputs()
    allu = [(b, h) for b in range(B) for h in range(6)]
    maps = []
    assign = []
    for c in range(NCORES):
        us = [allu[i] if i < len(allu) else allu[0] for i in (2 * c, 2 * c + 1)]
        assign.append([(i < len(allu)) for i in (2 * c, 2 * c + 1)])
        units = [gdn_unit_inputs(u[b, :, 512 + 1152:], h, np.asarray(inp["gdn_conv_w"][l]), np.asarray(inp["gdn_a_log"][l]),
                                 np.asarray(inp["gdn_dt_bias"][l]), np.asarray(inp["gdn_norm_g"][l])) for (b, h) in us]
        m = {k_: np.ascontiguousarray(np.stack([un[k_] for un in units])) for k_ in units[0]}
        m.update(gconst)
        maps.append(m)
    res = _run(nc, maps)
    for c in range(NCORES):
        for s in range(2):
            i = 2 * c + s
            if i < len(allu):
                b, h = allu[i]
                y[b, :, 640 + h * 64:640 + (h + 1) * 64] = res[c]["go"][s]
    return y


YROWS = 768 + 1024
RG = [[0, 1, 2, 3], [4, 5, 6, 7]]


def moba_unit(cc, su):
    return (cc, su) if su < 2 else (4 + cc // 2, cc % 2)


def moba_owner(h, half):
    return (h, half) if h < 4 else (2 * (h - 4) + half, 2)


class Fused:
    def __init__(self):
        self.nc = bass.Bass("TRN2", target_bir_lowering=False)
        self.es = ExitStack()
        self.cur = self.es
        self.dins = {}
        self.in_names = []
        self.out_names = []
        self.phase_i = 0
        self.load_x = False
        self.store_x = False
        self._dyn = {}

    def din(self, name, shape, dt=F32):
        if name not in self.dins:
            self.in_names.append(name)
            self.dins[name] = self.nc.dram_tensor(name, list(shape), dt, kind="ExternalInput").ap()
        return self.dins[name]

    def dout(self, name, shape, dt=F32):
        if name not in self.dins:
            self.out_names.append(name)
            self.dins[name] = self.nc.dram_tensor(name, list(shape), dt, kind="ExternalOutput").ap()
        return self.dins[name]

    AW = 36800

    def sb(self, name, shape, dt):
        p = shape[0]
        n = int(np.prod(shape[1:]))
        n32 = n if dt == F32 else (n + 1) // 2
        n32 = (n32 + 7) // 8 * 8
        off = self.aoff
        self.aoff += n32
        assert self.aoff <= self.AW, (name, self.aoff)
        v = self.arena[0:p, off:off + n32]
        if dt != F32:
            v = v.bitcast(dt)
        v = v[:, 0:n]
        if len(shape) == 3:
            v = v.rearrange("p (a b) -> p a b", a=shape[1])
        return v

    def dyn(self, e, engname, key):
        c = self._dyn.setdefault(engname, {})
        if "cc" not in c:
            c["cc"] = e.snap(e.partition_id() % 4)
        if key not in c:
            cc = c["cc"]
            doff = lambda h: (h // 2) * 512 + (h % 2) * 64
            v = {"c2048": lambda: cc * 2048, "prev": lambda: (cc + 3) % 4,
                 "D0": lambda: doff(cc), "D2": lambda: (cc // 2) * 64 + 1024, "mha2": lambda: cc % 2, "mhb2": lambda: 3 - cc % 2,
                 "gh1": lambda: (cc + 4) % 6, "Dg1": lambda: doff((cc + 4) % 6)}[key]()
            c[key] = e.snap(v)
        return c[key]

    def build(self):
        nc, es = self.nc, self.es
        sc = self.sc = Sched(nc, es)
        self.x = es.enter_context(nc.sbuf_tensor("x_res", [128, KC, TOK], F32))
        self.arena = es.enter_context(nc.sbuf_tensor("arena", [128, self.AW], F32))
        self.aoff = 0
        self.bx = [[Buf(f"x{c}_{t}") for t in range(TOK // 512)] for c in range(KC)]
        self.ps = es.enter_context(nc.psum_tensor("ps_all", [128, 8, 512], F32))
        NUC = (DIN + 127) // 128
        Usend_t = nc.dram_tensor("Usend", [NUC * 128, TOK], F32)
        Urecv_t = nc.dram_tensor("Urecv", [NUC * 512 + 128, TOK], F32)
        Ysend_t = nc.dram_tensor("Ysend", [2048, 1024], F32)
        Yrecv_t = nc.dram_tensor("Yrecv", [8192, 1024], F32)
        Yfull_t = nc.dram_tensor("Yfull", [D, TOK], F32)
        self.Usend, self.Urecv, self.Ysend, self.Yrecv, self.Yfull = (t.ap() for t in (Usend_t, Urecv_t, Ysend_t, Yrecv_t, Yfull_t))
        self.LK = nc.dram_tensor("LK", [3, 64, S], F32).ap()
        self.LV = nc.dram_tensor("LV", [3, 64, S], F32).ap()
        self.LQ = nc.dram_tensor("LQ", [3, 64, MOBA_SLOTS * 256], F32).ap()
        self.LQF = nc.dram_tensor("LQF", [3, 64, S], F32).ap()
        self.LG = nc.dram_tensor("LG", [2, 3, 64, S], F32).ap()
        self.LZ = nc.dram_tensor("LZ", [2, 64, S], F32).ap()
        self.LAB = nc.dram_tensor("LAB", [2, 2, S], F32).ap()
        self.LH = nc.dram_tensor("LH", [512, CH], F32).ap()
        self.Yloc = nc.dram_tensor("Yloc", [4, 512, 1024], F32).ap()
        self.uT_dst = self.Usend
        self.yT_src = self.Yfull
        self.bU, self.bUr, self.bYs, self.bYr, self.bY = Buf("U"), Buf("Ur"), Buf("Ys"), Buf("Yr"), Buf("Yf")
        self.bL, self.bLq, self.bYl = Buf("L"), Buf("Lq"), Buf("Yl")
        outb = []

        def run_phase(fn):
            self.aoff = 0
            r = fn()
            sc.barrier()
            self.phase_i += 1
            return r

        def tok_phase(stages, load_x=False, store_x=False, ag=True):
            def fn():
                self.load_x, self.store_x = load_x, store_x
                r = TokProg(stages, fused=self).build()
                if ag:
                    for ci in range(NUC):
                        sc.cc(lambda e, ci=ci: e.collective_compute(
                            "AllGather", ALU.bypass, replica_groups=RG,
                            ins=[Usend_t.ap()[ci * 128:(ci + 1) * 128, :]], outs=[Urecv_t.ap()[ci * 512:(ci + 1) * 512, :]]),
                            writes=[self.bU, self.bUr], key="agU")
                return r
            return run_phase(fn)

        def y_exchange():
            for ci in range(8):
                sc.cc(lambda e, ci=ci: e.collective_compute(
                    "AllGather", ALU.bypass, replica_groups=RG,
                    ins=[Ysend_t.ap()[ci * 256:(ci + 1) * 256, :]], outs=[Yrecv_t.ap()[ci * 1024:(ci + 1) * 1024, :]]),
                    writes=[self.bYs, self.bYr], key="agY")
            LB = ([0, 3, 4, 7], [1, 2, 5, 6])
            for c2 in range(2):
                sc.dma("scalar", lambda e, c2=c2: e.dma_start(
                    out=self.Yloc[:, c2 * 256:(c2 + 1) * 256, :],
                    in_=self.Yrecv[c2 * 1024:c2 * 1024 + 7168, :][bass.ds(self.dyn(e, "scalar", "c2048"), 1024), :].rearrange(
                        "(r f) t -> r f t", r=4)),
                    reads=[self.bYr], writes=[self.bYl], key="yloc")
            for h in range(6):
                for half in range(2):
                    rs, su = moba_owner(h, half)
                    for q4 in range(4):
                        lb = LB[half][q4]
                        sc.dma("sync", lambda e, h=h, lb=lb, rs=rs, su=su, q4=q4: e.dma_start(
                            out=self.Yfull[256 + h * 64:256 + (h + 1) * 64, lb * 256:(lb + 1) * 256],
                            in_=self.Yloc[rs, su * 64:(su + 1) * 64, q4 * 256:(q4 + 1) * 256]),
                            reads=[self.bYl, self.bY], key="yasm")
            for h in range(6):
                rs, g = (h, 0) if h < 4 else (h - 4, 1)
                for kk in range(2):
                    r0 = 192 + (g * 2 + kk) * 64
                    sc.dma("sync", lambda e, h=h, kk=kk, rs=rs, r0=r0: e.dma_start(
                        out=self.Yfull[640 + h * 64:640 + (h + 1) * 64, kk * 1024:(kk + 1) * 1024],
                        in_=self.Yloc[rs, r0:r0 + 64, :]),
                        reads=[self.bYl, self.bY], key="yasm")

        def localize():
            Ur = self.Urecv
            rk = lambda ap: ap.rearrange("d (r t) -> d r t", r=4)
            LQF = self.LQF

            def blk(e, q, dkey, B, n=64):
                R0 = (B // 128) * 512 + B % 128
                R1 = min(R0 + 2048, NUC * 512 + 128)
                return Ur[R0:R1, :][bass.ds(self.dyn(e, q, dkey), 512), :].rearrange("(r f) t -> f r t", r=4)[0:n]

            for u in range(3):
                q = "sync" if u < 2 else "scalar"
                dk = "D0" if u < 2 else "D2"
                for (dst, B) in ((self.LK, 896), (self.LV, 1280), (LQF, 512)):
                    sc.dma(q, lambda e, u=u, q=q, dk=dk, dst=dst, B=B: e.dma_start(out=rk(dst[u]), in_=blk(e, q, dk, B)),
                           reads=[self.bUr], writes=[self.bL], key=f"loc{q}{u}")
                for ab in range(2):
                    dstq = self.LQ[u].rearrange("d (G ab i) -> d G ab i", G=8, ab=2)[:, :, ab:ab + 1, :]
                    srcv = LQF[u].rearrange("d (G b i) -> d G b i", G=8, b=4)
                    if u < 2:
                        b = u if ab == 0 else 3 - u
                        sc.dma(q, lambda e, dstq=dstq, srcv=srcv, b=b: e.dma_start(out=dstq, in_=srcv[:, :, b:b + 1, :]),
                               reads=[self.bL], writes=[self.bLq], key=f"locq{u}")
                    else:
                        kn = "mha2" if ab == 0 else "mhb2"
                        sc.dma(q, lambda e, dstq=dstq, srcv=srcv, kn=kn: e.dma_start(
                            out=dstq, in_=srcv[:, :, bass.ds(self.dyn(e, "scalar", kn), 1), :]),
                            reads=[self.bL], writes=[self.bLq], key=f"locq{u}")
            for u in range(2):
                dk, gk = ("D0", "cc") if u == 0 else ("Dg1", "gh1")
                for j in range(3):
                    sc.dma("gpsimd", lambda e, u=u, j=j, dk=dk: e.dma_start(
                        out=rk(self.LG[u, j]), in_=blk(e, "gpsimd", dk, 1664 + j * 384)),
                        reads=[self.bUr], writes=[self.bL], key="locg")
                q2 = "gpsimd" if u == 0 else "sync"
                sc.dma(q2, lambda e, u=u, dk=dk, q2=q2: e.dma_start(out=rk(self.LZ[u]), in_=blk(e, q2, dk, 2816)),
                       reads=[self.bUr], writes=[self.bL], key=f"locz{u}")
                for ab, B in ((0, 3200), (1, 3206)):
                    sc.dma(q2, lambda e, u=u, ab=ab, B=B, gk=gk, q2=q2: e.dma_start(
                        out=self.LAB[u, ab:ab + 1].rearrange("o (r t) -> o r t", r=4), in_=blk(e, q2, gk, B, 1)),
                        reads=[self.bUr], writes=[self.bL], key=f"locz{u}")
            sc.dma("scalar", lambda e: e.dma_start(
                out=self.LH.rearrange("(c f) t -> c f t", c=4),
                in_=Ur[0:2048, TOK - CH:TOK].rearrange("(c r f) t -> r c f t", r=4, f=128)[bass.ds(self.dyn(e, "scalar", "prev"), 1)]),
                reads=[self.bUr], writes=[self.bL], key="loch")

        def mixer(l):
            pfx = f"L{l}_"
            run_phase(localize)
            run_phase(lambda: conv_emit(self, sc, pfx, fz=self))
            run_phase(lambda: moba_emit(self, sc, 3, pfx, fz=self))

            def g():
                gdn_emit(self, sc, 2, pfx, fz=self)
                y_exchange()
            run_phase(g)

        tok_phase([("ffn1", 0), ("uproj", 0)], load_x=True)
        mixer(0)
        tok_phase([("wout", 0), ("ffn2", 0), ("ffn1", 1), ("uproj", 1)])
        mixer(1)
        outb = tok_phase([("wout", 1), ("ffn2", 1), ("final",)], store_x=True, ag=False)
        with nc.Block() as block:
            sc.emit(block)
        es.close()
        return nc


_FUSED = {}


def kernel(**inp):
    if "p" not in _FUSED:
        F = Fused()
        _FUSED["p"] = (F, F.build())
    F, nc = _FUSED["p"]
    x = np.asarray(inp["x"], np.float32)
    tab = rope_tables()
    shared = moba_shared_inputs(tab)
    mconst = [moba_const_inputs(0), moba_const_inputs(1)]
    gconst = gdn_const_inputs()
    wnames = ("w_ada", "ffn1_w_gate", "ffn1_w_up", "ffn1_w_down", "w_in", "w_out", "ffn2_w_gate", "ffn2_w_up", "ffn2_w_down")
    lnames = ("b_ada", "ln_ffn1_g", "ln_mix_g", "ln_ffn2_g")
    common = {}
    for l in range(2):
        for n in wnames:
            common[f"{n}{l}"] = np.ascontiguousarray(np.asarray(inp[n][l], np.float32))
        for n in lnames:
            common[f"{n}{l}"] = _lay(inp[n][l])
        cw = np.asarray(inp["conv_w"][l], np.float32)
        lay2 = lambda v: np.ascontiguousarray(np.asarray(v, np.float32).reshape(2, 128).T)
        common[f"L{l}_cw"] = np.ascontiguousarray(cw.T.reshape(2, 128, 31).transpose(1, 0, 2))
        common[f"L{l}_cp"] = np.ascontiguousarray(np.stack([lay2(inp["conv_b"][l]), lay2(inp["conv_ln_g"][l]),
                                                            lay2(inp["conv_ln_b"][l])], axis=-1))
    common["final_g"] = _lay(inp["final_g"])
    common["cident"] = np.eye(128, dtype=np.float32)
    common.update(shared)
    common.update(gconst)
    maps = []
    for c in range(NCORES):
        b, cc = c // 4, c % 4
        m = dict(common)
        m["xT"] = np.ascontiguousarray(x[b, cc * TOK:(cc + 1) * TOK].T)
        m["cT"] = _lay(inp["c"][b])
        m["cflag"] = np.full((128, 1), 0.0 if cc == 0 else 1.0, np.float32)
        units = []
        for su in range(3):
            half = moba_unit(cc, su)[1]
            qpos = np.concatenate([np.arange(bl * 256, (bl + 1) * 256) for bl in HALF_BLOCKS[half]])
            d = dict(mconst[half])
            d["ropeq"] = np.ascontiguousarray(tab[:, :, qpos])
            units.append(d)
        for k_ in units[0]:
            m[k_] = np.ascontiguousarray(np.stack([un[k_] for un in units]))
        for l in range(2):
            heads = [cc, (cc + 4) % 6]
            gw = np.asarray(inp["gdn_conv_w"][l], np.float32)
            gcw = np.zeros((2, 64, 12), np.float32)
            gpar = np.zeros((2, 64, 2), np.float32)
            gng = np.zeros((2, 64, 64), np.float32)
            for g, h in enumerate(heads):
                for j in range(3):
                    gcw[g, :, j * 4:(j + 1) * 4] = gw[:, j * 384 + h * 64:j * 384 + (h + 1) * 64].T
                gpar[g, :, 0] = np.asarray(inp["gdn_a_log"][l], np.float32)[h]
                gpar[g, :, 1] = np.asarray(inp["gdn_dt_bias"][l], np.float32)[h]
                gng[g] = np.asarray(inp["gdn_norm_g"][l], np.float32)[None, :]
            m[f"L{l}_gcw"], m[f"L{l}_gpar"], m[f"L{l}_gng"] = gcw, gpar, gng
        maps.append({k_: m[k_] for k_ in F.in_names})
    res = run_bass_kernel_spmd(nc, maps, core_ids=list(range(NCORES))).results
    out = np.zeros((B, S, D), np.float32)
    for c in range(NCORES):
        out[c // 4, (c % 4) * TOK:(c % 4 + 1) * TOK] = res[c]["xoT"].T
    return out
```

```python
import numpy as np
from contextlib import ExitStack
import concourse.bass as bass
import concourse.mybir as mybir
from concourse.bass_utils import run_bass_kernel_spmd

F32 = mybir.dt.float32
BF16 = mybir.dt.bfloat16
AF = mybir.ActivationFunctionType
ALU = mybir.AluOpType

D = 1024
KC = 8
DFF = 2816
FC = 22
DIN = 3212
B = 2
S = 8192
NCORES = 8
TOK = 2048
EPS = 1e-6

SAME_ENG_SYNC = True


class Buf:
    __slots__ = ("name", "lw", "rd")

    def __init__(self, name=""):
        self.name = name
        self.lw = None
        self.rd = {}


class Sched:
    ENGS = ("tensor", "vector", "scalar", "gpsimd", "sync")
    EPOCH = 20000

    def __init__(self, nc, es):
        self.nc = nc
        self.es = es
        self.q = {e: [] for e in self.ENGS}
        self.cnt = {e: 0 for e in self.ENGS}
        self.seen = {e: {} for e in self.ENGS}
        self.esem = {}
        self.dsem = {}
        self.dcnt = {}
        self.cckeys = set()

    def _get_esem(self, eng, epoch):
        k = (eng, epoch)
        if k not in self.esem:
            self.esem[k] = self.es.enter_context(self.nc.semaphore(f"se_{eng}_{epoch}"))
        return self.esem[k]

    def _get_dsem(self, key):
        if key not in self.dsem:
            self.dsem[key] = self.es.enter_context(self.nc.semaphore(f"sd_{key}"))
            self.dcnt[key] = 0
        return self.dsem[key]

    def _need(self, eng, tok, waits):
        if tok is None:
            return
        kind, k, val = tok
        if kind == "e":
            if k == eng and (eng == "tensor" or not SAME_ENG_SYNC):
                return
        key = (kind, k)
        if self.seen[eng].get(key, 0) >= val:
            return
        self.seen[eng][key] = val
        waits.append(tok)

    def _deps(self, eng, reads, writes):
        waits = []
        for b in reads:
            self._need(eng, b.lw, waits)
        for b in writes:
            self._need(eng, b.lw, waits)
            for k, v in b.rd.items():
                self._need(eng, (k[0], k[1], v), waits)
        return waits

    def _mark(self, tok, reads, writes):
        key = (tok[0], tok[1])
        for b in reads:
            if b.rd.get(key, 0) < tok[2]:
                b.rd[key] = tok[2]
        for b in writes:
            b.lw = tok
            b.rd = {}

    def op(self, eng, fn, reads=(), writes=(), inc=True):
        waits = self._deps(eng, reads, writes)
        idx = self.cnt[eng] + 1
        if inc:
            self.cnt[eng] = idx
        tok = ("e", eng, idx)
        self._mark(tok, reads, writes)
        self.q[eng].append((waits, fn, tok if inc else None))

    def dma(self, qeng, fn, reads=(), writes=(), key="d"):
        waits = self._deps(qeng, reads, writes)
        self._get_dsem(key)
        self.dcnt[key] += 1
        tok = ("d", key, 16 * self.dcnt[key])
        self._mark(tok, reads, writes)
        self.q[qeng].append((waits, fn, tok))

    def cc(self, fn, reads=(), writes=(), key="cc"):
        waits = self._deps("gpsimd", reads, writes)
        self._get_dsem(key)
        self.cckeys.add(key)
        self.dcnt[key] += 1
        tok = ("c", key, self.dcnt[key])
        self._mark(tok, reads, writes)
        self.q["gpsimd"].append((waits, fn, tok))

    def barrier(self):
        for e in self.ENGS:
            waits = []
            for e2 in self.ENGS:
                if e2 != e and self.cnt[e2] > 0:
                    self._need(e, ("e", e2, self.cnt[e2]), waits)
            for key, n in self.dcnt.items():
                if n > 0:
                    kind = "c" if key in self.cckeys else "d"
                    self._need(e, (kind, key, n if kind == "c" else 16 * n), waits)
            self.q[e].append((waits, None, None))

    def final_wait(self, eng, toks_bufs):
        waits = self._deps(eng, (), toks_bufs)
        self.q[eng].append((waits, None, None))

    def emit(self, block):
        nc = self.nc

        def run(engname):
            def body(eng):
                for waits, fn, tok in self.q[engname]:
                    for (kind, k, val) in waits:
                        if kind == "e":
                            epoch = (val - 1) // self.EPOCH
                            eng.wait_ge(self._get_esem(k, epoch), val - epoch * self.EPOCH)
                        else:
                            eng.wait_ge(self.dsem[k], val)
                    if fn is None:
                        continue
                    ins = fn(eng)
                    if tok is not None:
                        if tok[0] == "e":
                            epoch = (tok[2] - 1) // self.EPOCH
                            ins.then_inc(self._get_esem(tok[1], epoch), 1)
                        elif tok[0] == "c":
                            ins.then_inc(self.dsem[tok[1]])
                        else:
                            ins.then_inc(self.dsem[tok[1]], 16)
                self.q[engname] = []
            return body

        for e in self.ENGS:
            for ep in range((self.cnt[e] - 1) // self.EPOCH + 1 if self.cnt[e] else 0):
                self._get_esem(e, ep)
        block.tensor(run("tensor"))
        block.vector(run("vector"))
        block.scalar(run("scalar"))
        block.gpsimd(run("gpsimd"))
        block.sync(run("sync"))


class TokProg:
    def __init__(self, stages, tok=TOK, fused=None):
        self.stages = stages
        self.tok = tok
        self.fused = fused
        if fused is None:
            self.nc = bass.Bass("TRN2", target_bir_lowering=False)
            self.es = ExitStack()
        else:
            self.nc = fused.nc
            self.es = fused.es
        self.in_names = []
        self.out_names = []

    def din(self, name, shape, dt=F32):
        if self.fused is not None:
            return self.fused.din(name, shape, dt)
        self.in_names.append(name)
        return self.nc.dram_tensor(name, list(shape), dt, kind="ExternalInput").ap()

    def dout(self, name, shape, dt=F32):
        if self.fused is not None:
            return self.fused.dout(name, shape, dt)
        self.out_names.append(name)
        return self.nc.dram_tensor(name, list(shape), dt, kind="ExternalOutput").ap()

    def sb(self, name, shape, dt):
        if self.fused is not None:
            return self.fused.sb(name, shape, dt)
        return self.es.enter_context(self.nc.sbuf_tensor(name, list(shape), dt))

    def build(self):
        nc, es = self.nc, self.es
        fz = self.fused
        T = self.tok
        NH = T // 1024
        stages = self.stages
        layers = sorted({s[1] for s in stages if len(s) > 1})
        need_v = {}
        for s in stages:
            if s[0] == "ffn1":
                need_v.setdefault(s[1], set()).update([0, 1, 2])
            elif s[0] == "uproj":
                need_v.setdefault(s[1], set()).update([3, 4])
            elif s[0] == "wout":
                need_v.setdefault(s[1], set()).update([5])
            elif s[0] == "ffn2":
                need_v.setdefault(s[1], set()).update([6, 7, 8])

        xT_d = self.din("xT", [D, T]) if (fz is None or fz.load_x) else None
        cT_d = self.din("cT", [128, KC])
        W = {}
        for l in layers:
            W[("w_ada", l)] = self.din(f"w_ada{l}", [D, 9 * D])
            W[("b_ada", l)] = self.din(f"b_ada{l}", [128, 72])
        for s in stages:
            if s[0] in ("ffn1", "ffn2"):
                l = s[1]
                n = s[0]
                W[(n + "_g", l)] = self.din(f"ln_{n}_g{l}", [128, KC])
                W[(n + "_wg", l)] = self.din(f"{n}_w_gate{l}", [D, DFF])
                W[(n + "_wu", l)] = self.din(f"{n}_w_up{l}", [D, DFF])
                W[(n + "_wd", l)] = self.din(f"{n}_w_down{l}", [DFF, D])
            elif s[0] == "uproj":
                l = s[1]
                W[("mix_g", l)] = self.din(f"ln_mix_g{l}", [128, KC])
                W[("w_in", l)] = self.din(f"w_in{l}", [D, DIN])
                W[("uT", l)] = self.dout(f"uT{l}", [DIN, T]) if fz is None else fz.uT_dst
            elif s[0] == "wout":
                l = s[1]
                W[("w_out", l)] = self.din(f"w_out{l}", [D, D])
                W[("yT", l)] = self.din(f"yT{l}", [D, T]) if fz is None else fz.yT_src
            elif s[0] == "final":
                W[("final_g",)] = self.din("final_g", [128, KC])
        xo_d = self.dout("xoT", [D, T]) if (fz is None or fz.store_x) else None

        x = self.sb("x", [128, KC, T], F32) if fz is None else fz.x
        h = self.sb("h", [128, KC, 1024], BF16)
        act = self.sb("act", [128, FC, 1024], BF16)
        wd = self.sb("wd", [128, FC, D], BF16)
        NSLOT = 4
        SLOTW = 256
        wslot = [self.sb(f"ws{i}", [128, KC, SLOTW], BF16) for i in range(NSLOT)]
        tmpA = [self.sb(f"tmpA{i}", [128, 512], F32) for i in range(2)]
        tmpB = [self.sb(f"tmpB{i}", [128, 512], F32) for i in range(2)]
        sqb = [self.sb(f"sq{i}", [128, 512], BF16) for i in range(2)]
        rstd = self.sb("rstd", [128, 512], F32)
        ones = self.sb("ones", [128, 128], BF16)
        cT = self.sb("cT_sb", [128, KC], F32)
        cact = self.sb("cact", [128, KC], BF16)
        bada = {l: self.sb(f"bada{l}", [128, 72], F32) for l in layers}
        mod = {l: self.sb(f"mod{l}", [128, 72], F32) for l in layers}
        gains = {}
        for k in W:
            if k[0] in ("ffn1_g", "ffn2_g", "mix_g", "final_g"):
                gains[k] = self.sb("g_" + "_".join(map(str, k)), [128, KC], F32)
        coefA = {}
        coefG = {}
        ps = es.enter_context(nc.psum_tensor("ps", [128, 8, 512], F32)) if fz is None else fz.ps

        sc = Sched(nc, es) if fz is None else fz.sc
        bx = [[Buf(f"x{c}_{t}") for t in range(T // 512)] for c in range(KC)] if fz is None else fz.bx
        bU = [] if fz is None else [fz.bU]
        bY = [] if fz is None else [fz.bY]
        bh = [Buf(f"h{t}") for t in range(2)]
        bact = [[Buf(f"act{f}_{t}") for t in range(2)] for f in range(FC)]
        WD_PIECES = ((0, 6), (6, 12), (12, 17), (17, 22))
        bwd = [Buf(f"wd{i}") for i in range(4)]
        wd_piece = {}
        for i, (f0, f1) in enumerate(WD_PIECES):
            for f in range(f0, f1):
                wd_piece[f] = i
        bws = [Buf(f"ws{i}") for i in range(NSLOT)]
        btA = [Buf() for _ in range(2)]
        btB = [Buf() for _ in range(2)]
        bsq = [Buf() for _ in range(2)]
        brstd = Buf()
        bones = Buf()
        bps = [Buf(f"ps{i}") for i in range(8)]
        bmisc = Buf("misc")
        bmod = Buf("mod")

        if xT_d is not None:
            xT_v = xT_d.rearrange("(c p) t -> p c t", p=128)
            for c in range(KC):
                sc.dma("sync", lambda e, c=c: e.dma_start(out=x[:, c, :], in_=xT_v[:, c, :]),
                       writes=bx[c], key=f"x{c}")
        sc.dma("sync", lambda e: e.dma_start(out=cT[:], in_=cT_d[:, :]), writes=[bmisc], key="misc")
        for l in layers:
            sc.dma("sync", lambda e, l=l: e.dma_start(out=bada[l][:], in_=W[("b_ada", l)][:, :]),
                   writes=[bmisc], key="misc")
        for k, t in gains.items():
            sc.dma("sync", lambda e, k=k, t=t: e.dma_start(out=t[:], in_=W[k][:, :]), writes=[bmisc], key="misc")
        sc.op("vector", lambda e: e.memset(ones[:], 1.0), writes=[bones])
        sc.op("scalar", lambda e: e.activation(out=cact[:], in_=cT[:], func=AF.Silu), reads=[bmisc], writes=[bmod])

        wslot_i = [0]

        def next_slot():
            i = wslot_i[0] % NSLOT
            wslot_i[0] += 1
            return i

        def load_cols(Wd, c0, ncols, nk=KC):
            i = next_slot()
            src = Wd.rearrange("(k p) n -> p k n", p=128)
            sc.dma("gpsimd", lambda e, i=i: e.dma_start(out=wslot[i][:, 0:nk, 0:ncols], in_=src[:, :, c0:c0 + ncols]),
                   writes=[bws[i]], key=f"ws{i}")
            return i

        mod_ps = ps[:, 7, 0:72]
        for l in layers:
            for v in sorted(need_v[l]):
                for hh in range(4):
                    si = load_cols(W[("w_ada", l)], v * 1024 + hh * 256, 256)
                    for jj in range(2):
                        j = hh * 2 + jj
                        col = v * 8 + j
                        for kc in range(KC):
                            sc.op("tensor",
                                  lambda e, si=si, jj=jj, kc=kc, col=col: e.matmul(
                                      ps[:, 7, col:col + 1], lhsT=wslot[si][:, kc, jj * 128:(jj + 1) * 128],
                                      rhs=cact[:, kc:kc + 1], start=(kc == 0), stop=(kc == KC - 1)),
                                  reads=[bws[si], bmod], writes=[bps[7]], inc=(kc == KC - 1))
            sc.op("vector", lambda e, l=l: e.tensor_tensor(out=mod[l][:], in0=mod_ps, in1=bada[l][:], op=ALU.add),
                  reads=[bps[7], bmisc], writes=[bmod])
            for (gk, vs, vg, half) in ((("ffn1_g", l), 1, 2, 0.5), (("mix_g", l), 4, None, None),
                                       (("ffn2_g", l), 7, 8, 0.5)):
                if gk in gains:
                    a = self.sb("cA_" + "_".join(map(str, gk)), [128, KC], F32)
                    coefA[gk] = a
                    sc.op("vector", lambda e, a=a, gk=gk, vs=vs, l=l: e.scalar_tensor_tensor(
                        out=a[:], in0=mod[l][:, vs * 8:vs * 8 + 8], scalar=1.0, in1=gains[gk][:],
                        op0=ALU.add, op1=ALU.mult), reads=[bmod, bmisc], writes=[bmod])
                    if vg is not None:
                        g = self.sb("cG_" + "_".join(map(str, gk)), [128, KC], F32)
                        coefG[gk] = g
                        sc.op("vector", lambda e, g=g, vg=vg, l=l: e.tensor_scalar(
                            out=g[:], in0=mod[l][:, vg * 8:vg * 8 + 8], scalar1=0.5, scalar2=None, op0=ALU.mult),
                            reads=[bmod], writes=[bmod])

        psi = [0]

        def next_ps(pool):
            i = pool[psi[0] % len(pool)]
            psi[0] += 1
            return i

        def norm_mod(half, A_ap, sh_ap):
            for tt in range(2):
                t0 = half * 1024 + tt * 512
                ti = t0 // 512
                pb = 6
                for c in range(KC):
                    s = c % 2
                    sc.op("scalar", lambda e, c=c, s=s, t0=t0: e.activation(out=sqb[s][:], in_=x[:, c, t0:t0 + 512],
                                                                            func=AF.Square),
                          reads=[bx[c][ti]], writes=[bsq[s]])
                    sc.op("tensor", lambda e, c=c, s=s: e.matmul(ps[:, pb, :], lhsT=ones[:], rhs=sqb[s][:],
                                                                 start=(c == 0), stop=(c == KC - 1)),
                          reads=[bones, bsq[s]], writes=[bps[pb]])
                sc.op("scalar", lambda e: e.activation(out=tmpA[0][:], in_=ps[:, pb, :], func=AF.Sqrt,
                                                       bias=eps_t[:, 0:1], scale=1.0 / D),
                      reads=[bps[pb], bmisc], writes=[btA[0]])
                sc.op("vector", lambda e: e.reciprocal(out=rstd[:], in_=tmpA[0][:]), reads=[btA[0]], writes=[brstd])
                for c in range(KC):
                    s = c % 2
                    sc.op("vector", lambda e, c=c, s=s, t0=t0: e.scalar_tensor_tensor(
                        out=tmpB[s][:], in0=x[:, c, t0:t0 + 512], scalar=A_ap[:, c:c + 1], in1=rstd[:],
                        op0=ALU.mult, op1=ALU.mult), reads=[bx[c][ti], brstd, bmod], writes=[btB[s]])
                    if sh_ap is not None:
                        sc.op("scalar", lambda e, c=c, s=s, tt=tt: e.activation(
                            out=h[:, c, tt * 512:(tt + 1) * 512], in_=tmpB[s][:], func=AF.Identity,
                            bias=sh_ap[:, c:c + 1], scale=1.0), reads=[btB[s], bmod], writes=[bh[tt]])

        def ffn(half, n, l):
            A = coefA[(n + "_g", l)]
            G = coefG[(n + "_g", l)]
            vsh = 0 if n == "ffn1" else 6
            sh = mod[l][:, vsh * 8:vsh * 8 + 8]
            norm_mod(half, A, sh)
            wdv = W[(n + "_wd", l)].rearrange("(f p) n -> p f n", p=128)
            for i, (f0, f1) in enumerate(WD_PIECES):
                sc.dma("gpsimd", lambda e, f0=f0, f1=f1: e.dma_start(out=wd[:, f0:f1, :], in_=wdv[:, f0:f1, :]),
                       writes=[bwd[i]], key=f"wd{i}")
            groups = [(g * 2, 2) for g in range(11)]
            loaded = {}

            def load_group(gi):
                f0, nf = groups[gi]
                loaded[gi] = (load_cols(W[(n + "_wg", l)], f0 * 128, nf * 128),
                              load_cols(W[(n + "_wu", l)], f0 * 128, nf * 128))
            load_group(0)
            for gi, (f0, nf) in enumerate(groups):
                if gi + 1 < len(groups):
                    load_group(gi + 1)
                sg, su = loaded[gi]
                for fi in range(nf):
                    f = f0 + fi
                    for tt in range(2):
                        pg = next_ps([0, 1])
                        pu = pg + 2
                        for kc in range(KC):
                            sc.op("tensor", lambda e, sg=sg, fi=fi, kc=kc, tt=tt, pg=pg: e.matmul(
                                ps[:, pg, :], lhsT=wslot[sg][:, kc, fi * 128:(fi + 1) * 128],
                                rhs=h[:, kc, tt * 512:(tt + 1) * 512], start=(kc == 0), stop=(kc == KC - 1)),
                                reads=[bws[sg], bh[tt]], writes=[bps[pg]], inc=(kc == KC - 1))
                        for kc in range(KC):
                            sc.op("tensor", lambda e, su=su, fi=fi, kc=kc, tt=tt, pu=pu: e.matmul(
                                ps[:, pu, :], lhsT=wslot[su][:, kc, fi * 128:(fi + 1) * 128],
                                rhs=h[:, kc, tt * 512:(tt + 1) * 512], start=(kc == 0), stop=(kc == KC - 1)),
                                reads=[bws[su], bh[tt]], writes=[bps[pu]], inc=(kc == KC - 1))
                        s = pg
                        sc.op("scalar", lambda e, s=s, pg=pg: e.activation(out=tmpA[s][:], in_=ps[:, pg, :],
                                                                           func=AF.Silu),
                              reads=[bps[pg]], writes=[btA[s]])
                        sc.op("vector", lambda e, s=s, pu=pu, f=f, tt=tt: e.tensor_tensor(
                            out=act[:, f, tt * 512:(tt + 1) * 512], in0=tmpA[s][:], in1=ps[:, pu, :], op=ALU.mult),
                            reads=[btA[s], bps[pu]], writes=[bact[f][tt]])
            for tt in range(2):
                t0 = half * 1024 + tt * 512
                ti = t0 // 512
                for d in range(KC):
                    pd = next_ps([4, 5])
                    for f in range(FC):
                        sc.op("tensor", lambda e, f=f, d=d, tt=tt, pd=pd: e.matmul(
                            ps[:, pd, :], lhsT=wd[:, f, d * 128:(d + 1) * 128], rhs=act[:, f, tt * 512:(tt + 1) * 512],
                            start=(f == 0), stop=(f == FC - 1)),
                            reads=[bwd[wd_piece[f]], bact[f][tt]], writes=[bps[pd]], inc=(f == FC - 1))
                    sc.op("vector", lambda e, d=d, t0=t0, pd=pd: e.scalar_tensor_tensor(
                        out=x[:, d, t0:t0 + 512], in0=ps[:, pd, :], scalar=G[:, d:d + 1], in1=x[:, d, t0:t0 + 512],
                        op0=ALU.mult, op1=ALU.add), reads=[bps[pd], bx[d][ti], bmod], writes=[bx[d][ti]])

        ostage = [self.sb(f"ost{i}", [128, 512], F32) for i in range(2)]
        bost = [Buf() for _ in range(2)]
        osi = [0]

        def uproj(half, l):
            A = coefA[("mix_g", l)]
            sh = mod[l][:, 3 * 8:3 * 8 + 8]
            norm_mod(half, A, sh)
            uT = W[("uT", l)]
            ngr = (DIN + 255) // 256
            loaded = {}

            def load_group(gi):
                c0 = gi * 256
                loaded[gi] = load_cols(W[("w_in", l)], c0, min(256, DIN - c0))
            load_group(0)
            for gi in range(ngr):
                if gi + 1 < ngr:
                    load_group(gi + 1)
                si = loaded[gi]
                c0 = gi * 256
                ncol = min(256, DIN - c0)
                for fi in range((ncol + 127) // 128):
                    m = min(128, ncol - fi * 128)
                    for tt in range(2):
                        t0 = half * 1024 + tt * 512
                        pg = next_ps([0, 1, 2, 3])
                        for kc in range(KC):
                            sc.op("tensor", lambda e, si=si, fi=fi, kc=kc, tt=tt, pg=pg, m=m: e.matmul(
                                ps[0:m, pg, :], lhsT=wslot[si][:, kc, fi * 128:fi * 128 + m],
                                rhs=h[:, kc, tt * 512:(tt + 1) * 512], start=(kc == 0), stop=(kc == KC - 1)),
                                reads=[bws[si], bh[tt]], writes=[bps[pg]], inc=(kc == KC - 1))
                        o = osi[0] % 2
                        osi[0] += 1
                        eng = "scalar" if o == 0 else "vector"
                        if eng == "scalar":
                            sc.op("scalar", lambda e, o=o, pg=pg, m=m: e.copy(out=ostage[o][0:m, :], in_=ps[0:m, pg, :]),
                                  reads=[bps[pg]], writes=[bost[o]])
                        else:
                            sc.op("vector", lambda e, o=o, pg=pg, m=m: e.tensor_copy(out=ostage[o][0:m, :],
                                                                                     in_=ps[0:m, pg, :]),
                                  reads=[bps[pg]], writes=[bost[o]])
                        r0 = c0 + fi * 128
                        sc.dma("sync", lambda e, o=o, m=m, r0=r0, t0=t0: e.dma_start(
                            out=uT[r0:r0 + m, t0:t0 + 512], in_=ostage[o][0:m, :]), reads=[bost[o]] + bU, key=f"ost{o}")

        ystage = [act[:, i * 8:(i + 1) * 8, 0:512] for i in range(2)]
        byst = [[bact[f][0] for f in range(i * 8, (i + 1) * 8)] for i in range(2)]

        def wout(half, l):
            yT = W[("yT", l)].rearrange("(c p) t -> p c t", p=128)
            wsl = [load_cols(W[("w_out", l)], q * 256, 256) for q in range(4)]
            G = mod[l][:, 5 * 8:5 * 8 + 8]
            for tt in range(2):
                t0 = half * 1024 + tt * 512
                ti = t0 // 512
                sc.dma("gpsimd", lambda e, tt=tt, t0=t0: e.dma_start(out=ystage[tt], in_=yT[:, :, t0:t0 + 512]),
                       reads=bY, writes=byst[tt], key=f"yst{tt}")
                for d in range(KC):
                    si = wsl[d // 2]
                    dj = d % 2
                    pd = next_ps([4, 5])
                    for kc in range(KC):
                        sc.op("tensor", lambda e, si=si, dj=dj, kc=kc, tt=tt, pd=pd: e.matmul(
                            ps[:, pd, :], lhsT=wslot[si][:, kc, dj * 128:(dj + 1) * 128], rhs=ystage[tt][:, kc, :],
                            start=(kc == 0), stop=(kc == KC - 1)),
                            reads=[bws[si]] + byst[tt], writes=[bps[pd]], inc=(kc == KC - 1))
                    sc.op("vector", lambda e, d=d, t0=t0, pd=pd: e.scalar_tensor_tensor(
                        out=x[:, d, t0:t0 + 512], in0=ps[:, pd, :], scalar=G[:, d:d + 1], in1=x[:, d, t0:t0 + 512],
                        op0=ALU.mult, op1=ALU.add), reads=[bps[pd], bx[d][ti], bmod], writes=[bx[d][ti]])

        def final(half):
            g = gains[("final_g",)]
            for tt in range(2):
                t0 = half * 1024 + tt * 512
                ti = t0 // 512
                pb = 6
                for c in range(KC):
                    s = c % 2
                    sc.op("scalar", lambda e, c=c, s=s, t0=t0: e.activation(out=sqb[s][:], in_=x[:, c, t0:t0 + 512],
                                                                            func=AF.Square),
                          reads=[bx[c][ti]], writes=[bsq[s]])
                    sc.op("tensor", lambda e, c=c, s=s: e.matmul(ps[:, pb, :], lhsT=ones[:], rhs=sqb[s][:],
                                                                 start=(c == 0), stop=(c == KC - 1)),
                          reads=[bones, bsq[s]], writes=[bps[pb]])
                sc.op("scalar", lambda e: e.activation(out=tmpA[0][:], in_=ps[:, pb, :], func=AF.Sqrt,
                                                       bias=eps_t[:, 0:1], scale=1.0 / D),
                      reads=[bps[pb], bmisc], writes=[btA[0]])
                sc.op("vector", lambda e: e.reciprocal(out=rstd[:], in_=tmpA[0][:]), reads=[btA[0]], writes=[brstd])
                for c in range(KC):
                    sc.op("vector", lambda e, c=c, t0=t0: e.scalar_tensor_tensor(
                        out=x[:, c, t0:t0 + 512], in0=x[:, c, t0:t0 + 512], scalar=g[:, c:c + 1], in1=rstd[:],
                        op0=ALU.mult, op1=ALU.mult), reads=[bx[c][ti], brstd, bmisc], writes=[bx[c][ti]])

        eps_t = self.sb("eps_t", [128, 1], F32)
        sc.op("vector", lambda e: e.memset(eps_t[:], EPS), writes=[bmisc])

        for half in range(NH):
            for s in stages:
                if s[0] in ("ffn1", "ffn2"):
                    ffn(half, s[0], s[1])
                elif s[0] == "uproj":
                    uproj(half, s[1])
                elif s[0] == "wout":
                    wout(half, s[1])
                elif s[0] == "final":
                    final(half)

        allb = []
        if xo_d is not None:
            xo_v = xo_d.rearrange("(c p) t -> p c t", p=128)
            for c in range(KC):
                sc.dma("sync", lambda e, c=c: e.dma_start(out=xo_v[:, c, :], in_=x[:, c, :]), reads=bx[c], key="xo")
                allb += bx[c]
        if fz is not None:
            return allb
        sc.final_wait("sync", allb + bost)

        with nc.Block() as block:
            sc.emit(block)
        es.close()
        return nc


NBLK = 32
MOBA_SLOTS = 16
HALF_BLOCKS = ([b for b in range(NBLK) if b % 4 in (0, 3)], [b for b in range(NBLK) if b % 4 in (1, 2)])
NEG = -30000.0


def moba_emit(P, sc, nu, pfx="", fz=None):
    nc, es = P.nc, P.es
    NQ = MOBA_SLOTS * 256
    if fz is None:
        mq = P.din(pfx + "mq", [nu, 64, NQ])
        mqs = P.din(pfx + "mqs", [nu, 16, NQ])
        mk = P.din(pfx + "mk", [nu, 64, S])
        mks = P.din(pfx + "mks", [nu, 16, S])
        mv = P.din(pfx + "mv", [nu, S, 64])
        yo = P.dout(pfx + "moT", [nu, 64, NQ])
    cq = P.din("ropeq", [nu, 2, 16, NQ])
    ck = P.din("ropek", [2, 16, S])
    pm_d = P.din("pm", [nu, 128, MOBA_SLOTS * NBLK])
    oh_d = P.din("oh", [nu, 128, MOBA_SLOTS * NBLK])
    cm_d = P.din("cm", [nu, 2, 4, 128, 256])
    boh_d = P.din("boh", [32, S])
    id_d = P.din("ident", [128, 128])

    qaug = P.sb(pfx + "qaug", [128, NQ], BF16)
    kaug = P.sb(pfx + "kaug", [128, S], BF16)
    vaug = P.sb(pfx + "vaug", [128, 64, 128], BF16)
    qf = P.sb(pfx + "qf", [64, NQ], F32)
    xt = [P.sb(pfx + f"xt{i}", [64, 1024], F32) for i in range(2)]
    xs = [P.sb(pfx + f"xs{i}", [16, 1024], F32) for i in range(2)]
    ct = [P.sb(pfx + f"ct{i}", [16, 2, 1024], F32) for i in range(2)]
    t16 = P.sb(pfx + "t16", [16, 1024], F32)
    sqf = P.sb(pfx + "sqf", [64, 1024], F32)
    kmean = P.sb(pfx + "kmean", [64, NBLK], F32)
    mx = P.sb(pfx + "mx", [128, 4], F32)
    nbias = P.sb(pfx + "nbias", [128, 1], F32)
    onesf = P.sb(pfx + "onesf", [64, 128], F32)
    ident = P.sb(pfx + "ident_sb", [128, 128], F32)
    pm = P.sb(pfx + "pm_sb", [128, MOBA_SLOTS * NBLK], F32)
    oh = P.sb(pfx + "oh_sb", [128, MOBA_SLOTS * NBLK], F32)
    cm = P.sb(pfx + "cm_sb", [128, 8, 256], F32)
    gs = P.sb(pfx + "gs", [128, NBLK], F32)
    g8 = P.sb(pfx + "g8", [128, 8], F32)
    m1 = P.sb(pfx + "m1", [128, NBLK], F32)
    m2 = P.sb(pfx + "m2", [128, NBLK], F32)
    stm = [P.sb(pfx + f"stm{i}", [128, 256], F32) for i in range(2)]
    pt = [P.sb(pfx + f"pt{i}", [128, 256], BF16) for i in range(4)]
    rec = P.sb(pfx + "rec", [64, 256], F32)
    yst = [P.sb(pfx + f"yst{i}", [64, 256], F32) for i in range(2)]
    ps = es.enter_context(nc.psum_tensor(pfx + "mps", [128, 8, 512], F32)) if fz is None else fz.ps
    if fz is not None:
        vt = P.sb(pfx + "vt", [64, 1024], F32)
        bvt = Buf()

    def rows6(e, u, r0, nr, cols):
        return fz.Urecv[r0:r0 + 5 * 64 + nr, cols][bass.ds(fz.dyn(e, "sync", ("mhr", u)), nr), :]

    bq, bk, bv, bqf = Buf(), Buf(), Buf(), Buf()
    bxt = [Buf(), Buf()]
    bxs = [Buf(), Buf()]
    bct = [Buf(), Buf()]
    bt16, bsqf, bkm, bmx, bnb, bconst, bmask = Buf(), Buf(), Buf(), Buf(), Buf(), Buf(), Buf()
    bgs, bg8, bm1, bm2 = Buf(), Buf(), Buf(), Buf()
    bstm = [Buf(), Buf()]
    bpt = [Buf(), Buf(), Buf(), Buf()]
    brec = Buf()
    byst = [Buf(), Buf()]
    bps = [Buf() for _ in range(8)]

    sc.dma("sync", lambda e: e.dma_start(out=ident[:], in_=id_d[:, :]), writes=[bconst], key=pfx + "mconst")
    sc.op("vector", lambda e: e.memset(onesf[:], 1.0), writes=[bconst])
    sc.op("vector", lambda e: e.memset(kaug[32:64, :], 0.0), writes=[bk])
    sc.op("vector", lambda e: e.memset(kaug[32:33, :], 1.0), writes=[bk])
    sc.dma("gpsimd", lambda e: e.dma_start(out=kaug[0:32, :], in_=boh_d[:, :]), writes=[bk], key=pfx + "mk0")
    sc.op("vector", lambda e: e.memset(qaug[32:64, :], 0.0), writes=[bq])
    sc.op("vector", lambda e: e.memset(vaug[:, :, 64:128], 1.0), writes=[bv])

    cnt = [0]
    for u in range(nu):
        sc.dma("sync", lambda e, u=u: e.dma_start(out=pm[:], in_=pm_d[u]), writes=[bmask], key=pfx + "mmask")
        sc.dma("sync", lambda e, u=u: e.dma_start(out=oh[:], in_=oh_d[u]), writes=[bmask], key=pfx + "mmask")
        sc.dma("sync", lambda e, u=u: e.dma_start(out=cm[:], in_=cm_d[u].rearrange("a k p q -> p (a k) q")),
               writes=[bmask], key=pfx + "mmask")
        if fz is None:
            for k0 in range(0, 64, 16):
                sc.dma("gpsimd", lambda e, u=u, k0=k0: e.dma_start(
                    out=vaug[:, k0:k0 + 16, 0:64], in_=mv[u].rearrange("(k p) d -> p k d", p=128)[:, k0:k0 + 16, :]),
                    writes=[bv], key=pfx + "mv")
        else:
            for c0 in range(0, S, 1024):
                rr, t0 = c0 // TOK, c0 % TOK

                def vsrc(e, u=u, c0=c0):
                    return fz.LV[u, :, c0:c0 + 1024]
                sc.dma("sync", lambda e, vsrc=vsrc: e.dma_start(out=vt[:], in_=vsrc(e)), reads=[fz.bUr], writes=[bvt],
                       key=pfx + "mvt")
                for cj in range(8):
                    sc.op("tensor", lambda e, cj=cj: e.transpose(ps[:, 6, cj * 64:(cj + 1) * 64], vt[:, cj * 128:(cj + 1) * 128],
                                                                 ident[0:64, 0:64]),
                          reads=[bvt, bconst], writes=[bps[6]], inc=(cj == 7))
                k0 = c0 // 128
                sc.op("vector", lambda e, k0=k0: e.tensor_copy(out=vaug[:, k0:k0 + 8, 0:64],
                                                               in_=ps[:, 6, :].rearrange("p (a b) -> p a b", b=64)),
                      reads=[bps[6]], writes=[bv])
        sc.op("vector", lambda e: e.memset(mx[:], 0.0), writes=[bmx])
        srcs_ = ((mk, mks, None, S), (mq, mqs, cq, NQ)) if fz is None else ((None, None, None, S), (None, None, cq, NQ))
        for which, (src, srcs, tab, ncols) in enumerate(srcs_):
            for c0 in range(0, ncols, 1024):
                i = cnt[0] % 2
                cnt[0] += 1
                if fz is None:
                    sc.dma("sync", lambda e, i=i, c0=c0, src=src, u=u: e.dma_start(out=xt[i][:], in_=src[u, :, c0:c0 + 1024]),
                           writes=[bxt[i]], key=pfx + f"mxt{i}")
                    sc.dma("sync", lambda e, i=i, c0=c0, srcs=srcs, u=u: e.dma_start(out=xs[i][:], in_=srcs[u, :, c0:c0 + 1024]),
                           writes=[bxs[i]], key=pfx + f"mxs{i}")
                elif which == 0:
                    rr, t0 = c0 // TOK, c0 % TOK

                    def ksrc(e, ro, nr, u=u, c0=c0):
                        return fz.LK[u, ro:ro + nr, c0:c0 + 1024]
                    sc.dma("sync", lambda e, i=i, ksrc=ksrc: e.dma_start(out=xt[i][:], in_=ksrc(e, 0, 64)),
                           reads=[fz.bUr], writes=[bxt[i]], key=pfx + f"mxt{i}")
                    sc.dma("sync", lambda e, i=i, ksrc=ksrc: e.dma_start(out=xs[i][0:8, :], in_=ksrc(e, 8, 8)),
                           reads=[fz.bUr], writes=[bxs[i]], key=pfx + f"mxs{i}")
                    sc.dma("sync", lambda e, i=i, ksrc=ksrc: e.dma_start(out=xs[i][8:16, :], in_=ksrc(e, 0, 8)),
                           reads=[fz.bUr], writes=[bxs[i]], key=pfx + f"mxs{i}")
                else:
                    def qsrc(e, ro, nr, u=u, c0=c0):
                        return fz.LQ[u, ro:ro + nr, c0:c0 + 1024]
                    sc.dma("sync", lambda e, i=i, qsrc=qsrc: e.dma_start(out=xt[i][:], in_=qsrc(e, 0, 64)),
                           reads=[fz.bUr], writes=[bxt[i]], key=pfx + f"mxt{i}")
                    sc.dma("sync", lambda e, i=i, qsrc=qsrc: e.dma_start(out=xs[i][0:8, :], in_=qsrc(e, 8, 8)),
                           reads=[fz.bUr], writes=[bxs[i]], key=pfx + f"mxs{i}")
                    sc.dma("sync", lambda e, i=i, qsrc=qsrc: e.dma_start(out=xs[i][8:16, :], in_=qsrc(e, 0, 8)),
                           reads=[fz.bUr], writes=[bxs[i]], key=pfx + f"mxs{i}")
                if which == 0:
                    sc.dma("sync", lambda e, i=i, c0=c0: e.dma_start(
                        out=ct[i][:], in_=ck[:, :, c0:c0 + 1024].rearrange("a p t -> p a t")),
                        writes=[bct[i]], key=pfx + f"mct{i}")
                else:
                    sc.dma("sync", lambda e, i=i, c0=c0, u=u: e.dma_start(
                        out=ct[i][:], in_=cq[u, :, :, c0:c0 + 1024].rearrange("a p t -> p a t")),
                        writes=[bct[i]], key=pfx + f"mct{i}")
                sc.op("vector", lambda e, i=i: e.tensor_tensor(out=t16[:], in0=xs[i][:], in1=ct[i][:, 1, :], op=ALU.mult),
                      reads=[bxs[i], bct[i]], writes=[bt16])
                sc.op("vector", lambda e, i=i: e.tensor_tensor(out=xt[i][0:16, :], in0=xt[i][0:16, :], in1=ct[i][:, 0, :],
                                                               op=ALU.mult), reads=[bxt[i], bct[i]], writes=[bxt[i]])
                sc.op("vector", lambda e, i=i: e.tensor_tensor(out=xt[i][0:16, :], in0=xt[i][0:16, :], in1=t16[:],
                                                               op=ALU.add), reads=[bxt[i], bt16], writes=[bxt[i]])
                sc.op("scalar", lambda e, i=i: e.activation(out=sqf[:], in_=xt[i][:], func=AF.Square),
                      reads=[bxt[i]], writes=[bsqf])
                for hh in range(2):
                    sc.op("tensor", lambda e, hh=hh: e.matmul(ps[:, 6, :], lhsT=onesf[:], rhs=sqf[:, hh * 512:(hh + 1) * 512],
                                                              start=True, stop=True), reads=[bconst, bsqf], writes=[bps[6]])
                    sc.op("vector", lambda e, which=which: e.tensor_reduce(out=mx[:, 2:3], in_=ps[:, 6, :], axis=mybir.AxisListType.X,
                                                                           op=ALU.max), reads=[bps[6]], writes=[bmx])
                    sc.op("vector", lambda e, which=which: e.tensor_tensor(out=mx[:, which:which + 1], in0=mx[:, which:which + 1],
                                                                           in1=mx[:, 2:3], op=ALU.max), reads=[bmx], writes=[bmx])
                if which == 0:
                    nb0 = c0 // 256
                    sc.op("vector", lambda e, i=i, nb0=nb0: e.tensor_reduce(
                        out=kmean[:, nb0:nb0 + 4], in_=xt[i][:].rearrange("p (n t) -> p n t", t=256),
                        axis=mybir.AxisListType.X, op=ALU.add), reads=[bxt[i]], writes=[bkm])
                    sc.op("scalar", lambda e, i=i, c0=c0: e.copy(out=kaug[64:128, c0:c0 + 1024], in_=xt[i][:]),
                          reads=[bxt[i]], writes=[bk])
                else:
                    sc.op("scalar", lambda e, i=i, c0=c0: e.mul(out=qaug[64:128, c0:c0 + 1024], in_=xt[i][:], mul=0.125),
                          reads=[bxt[i]], writes=[bq])
                    sc.op("vector", lambda e, i=i, c0=c0: e.tensor_copy(out=qf[:, c0:c0 + 1024], in_=xt[i][:]),
                          reads=[bxt[i]], writes=[bqf])
        sc.op("vector", lambda e: e.tensor_tensor(out=mx[:, 3:4], in0=mx[:, 0:1], in1=mx[:, 1:2], op=ALU.mult),
              reads=[bmx], writes=[bmx])
        sc.op("scalar", lambda e: e.activation(out=mx[:, 3:4], in_=mx[:, 3:4], func=AF.Sqrt), reads=[bmx], writes=[bmx])
        sc.op("vector", lambda e: e.tensor_scalar(out=nbias[:], in0=mx[:, 3:4], scalar1=-0.125, scalar2=None, op0=ALU.mult),
              reads=[bmx], writes=[bnb])
        for t in range(NQ // 128):
            r = t // 2
            sc.op("tensor", lambda e, t=t: e.matmul(ps[:, 7, 0:NBLK], lhsT=qf[:, t * 128:(t + 1) * 128], rhs=kmean[:],
                                                    start=True, stop=True), reads=[bqf, bkm], writes=[bps[7]])
            sc.op("vector", lambda e, r=r: e.tensor_tensor(out=gs[:], in0=ps[:, 7, 0:NBLK], in1=pm[:, r * NBLK:(r + 1) * NBLK],
                                                           op=ALU.add), reads=[bps[7], bmask], writes=[bgs])
            sc.op("vector", lambda e: e.max(out=g8[:], in_=gs[:]), reads=[bgs], writes=[bg8])
            sc.op("vector", lambda e: e.tensor_scalar(out=m1[:], in0=gs[:], scalar1=g8[:, 2:3], scalar2=None, op0=ALU.is_ge),
                  reads=[bgs, bg8], writes=[bm1])
            sc.op("vector", lambda e: e.tensor_scalar(out=m2[:], in0=gs[:], scalar1=-1e29, scalar2=None, op0=ALU.is_gt),
                  reads=[bgs], writes=[bm2])
            sc.op("vector", lambda e: e.tensor_tensor(out=m1[:], in0=m1[:], in1=m2[:], op=ALU.mult),
                  reads=[bm1, bm2], writes=[bm1])
            sc.op("vector", lambda e, r=r: e.tensor_tensor(out=m1[:], in0=m1[:], in1=oh[:, r * NBLK:(r + 1) * NBLK], op=ALU.add),
                  reads=[bm1, bmask], writes=[bm1])
            sc.op("vector", lambda e: e.tensor_scalar(out=m2[:], in0=m1[:], scalar1=-1.0, scalar2=-NEG, op0=ALU.add, op1=ALU.mult),
                  reads=[bm1], writes=[bm2])
            sc.op("tensor", lambda e: e.transpose(ps[0:NBLK, 7, 128:256], m2[:], ident[:]),
                  reads=[bm2, bconst], writes=[bps[7]])
            sc.op("vector", lambda e, t=t: e.tensor_copy(out=qaug[0:32, t * 128:(t + 1) * 128], in_=ps[0:NBLK, 7, 128:256]),
                  reads=[bps[7]], writes=[bq])
        tasks = [(r, kt) for r in range(MOBA_SLOTS) for kt in range(4 * r + 4)]
        NB, DEPTH = 4, 3

        def qk(i):
            r, kt = tasks[i]
            KT = 4 * r + 4
            p = i % NB
            sc.op("tensor", lambda e, kt=kt, r=r, p=p: e.matmul(ps[:, p, 0:256], lhsT=kaug[:, kt * 128:(kt + 1) * 128],
                                                               rhs=qaug[:, r * 256:(r + 1) * 256], start=True, stop=True),
                  reads=[bk, bq], writes=[bps[p]])
            if kt >= KT - 4:
                j = kt - (KT - 4) + 4 * (r % 2)
                s = j % 2
                sc.op("vector", lambda e, p=p, j=j, s=s: e.tensor_tensor(out=stm[s][:], in0=ps[:, p, 0:256], in1=cm[:, j, :],
                                                                        op=ALU.add), reads=[bps[p], bmask], writes=[bstm[s]])
                sc.op("scalar", lambda e, p=p, s=s: e.activation(out=pt[p][:], in_=stm[s][:], func=AF.Exp, bias=nbias[:, 0:1],
                                                                 scale=1.0), reads=[bstm[s], bnb], writes=[bpt[p]])
            else:
                sc.op("scalar", lambda e, p=p: e.activation(out=pt[p][:], in_=ps[:, p, 0:256], func=AF.Exp, bias=nbias[:, 0:1],
                                                            scale=1.0), reads=[bps[p], bnb], writes=[bpt[p]])

        def pv(i):
            r, kt = tasks[i]
            KT = 4 * r + 4
            p = i % NB
            po = 4 + (r % 2)
            sc.op("tensor", lambda e, kt=kt, p=p, po=po, KT=KT: e.matmul(ps[:, po, 0:256], lhsT=vaug[:, kt, :], rhs=pt[p][:],
                                                                        start=(kt == 0), stop=(kt == KT - 1)),
                  reads=[bv, bpt[p]], writes=[bps[po]])
            if kt < KT - 1:
                return
            sc.op("vector", lambda e, po=po: e.reciprocal(out=rec[:], in_=ps[64:128, po, 0:256]), reads=[bps[po]], writes=[brec])
            ys = r % 2
            sc.op("vector", lambda e, po=po, ys=ys: e.tensor_tensor(out=yst[ys][:], in0=ps[0:64, po, 0:256], in1=rec[:], op=ALU.mult),
                  reads=[bps[po], brec], writes=[byst[ys]])
            if fz is None:
                sc.dma("sync", lambda e, u=u, r=r, ys=ys: e.dma_start(out=yo[u, :, r * 256:(r + 1) * 256], in_=yst[ys][:]),
                       reads=[byst[ys]], key=pfx + f"myo{ys}")
            else:
                row0 = (r // 4) * 512 + u * 64
                sc.dma("sync", lambda e, row0=row0, r=r, ys=ys: e.dma_start(
                    out=fz.Ysend[row0:row0 + 64, (r % 4) * 256:(r % 4 + 1) * 256], in_=yst[ys][:]),
                    reads=[byst[ys], fz.bYs], key=pfx + f"myo{ys}")

        for i in range(min(DEPTH, len(tasks))):
            qk(i)
        for i in range(len(tasks)):
            if i + DEPTH < len(tasks):
                qk(i + DEPTH)
            pv(i)
    return byst


class SimpleProg:
    def __init__(self):
        self.nc = bass.Bass("TRN2", target_bir_lowering=False)
        self.es = ExitStack()
        self.in_names = []
        self.out_names = []

    din = TokProg.din
    dout = TokProg.dout
    sb = TokProg.sb

    def finish(self, sc, outbufs):
        sc.final_wait("sync", outbufs)
        with self.nc.Block() as block:
            sc.emit(block)
        self.es.close()
        return self.nc


def build_moba(nu=3):
    P = SimpleProg()
    sc = Sched(P.nc, P.es)
    ob = moba_emit(P, sc, nu)
    return P, P.finish(sc, ob)


def rope_tables():
    inv = np.exp(np.float32(-np.log(500000.0)) * np.arange(0, 16, 2, dtype=np.float32) / np.float32(16)).astype(np.float32)
    ang = (np.arange(S, dtype=np.float32)[:, None] * inv[None, :]).astype(np.float32)
    cos = np.cos(ang).astype(np.float32).T
    sin = np.sin(ang).astype(np.float32).T
    tab = np.zeros((2, 16, S), np.float32)
    tab[0, 0:8] = cos
    tab[0, 8:16] = cos
    tab[1, 0:8] = -sin
    tab[1, 8:16] = sin
    return tab


def moba_unit_inputs(uq, uk, uv, half, tab):
    blocks = HALF_BLOCKS[half]
    qpos = np.concatenate([np.arange(b * 256, (b + 1) * 256) for b in blocks])
    qT = np.ascontiguousarray(uq[qpos].T)
    kT = np.ascontiguousarray(uk.T)
    sw = np.r_[8:16, 0:8]
    return dict(mq=qT, mqs=np.ascontiguousarray(qT[sw]), mk=kT, mks=np.ascontiguousarray(kT[sw]), mv=np.ascontiguousarray(uv),
                ropeq=np.ascontiguousarray(tab[:, :, qpos]))


def moba_const_inputs(half):
    blocks = HALF_BLOCKS[half]
    pm = np.zeros((MOBA_SLOTS, NBLK), np.float32)
    oh = np.zeros((MOBA_SLOTS, NBLK), np.float32)
    for r, b in enumerate(blocks):
        pm[r, b:] = -1e30
        oh[r, b] = 1.0
    kk = np.arange(128)[:, None]
    qq = np.arange(256)[None, :]
    M0 = np.where(kk <= qq, 0.0, NEG).astype(np.float32)
    M1 = np.where(kk + 128 <= qq, 0.0, NEG).astype(np.float32)
    Z = np.zeros((128, 256), np.float32)
    cm = np.zeros((2, 4, 128, 256), np.float32)
    for par in range(2):
        r = par
        b = blocks[r]
        if b == 2 * r + 1:
            cm[par] = np.stack([Z, Z, M0, M1])
        else:
            cm[par] = np.stack([M0, M1, Z, Z])
    pmb = np.ascontiguousarray(np.broadcast_to(pm.reshape(1, -1), (128, MOBA_SLOTS * NBLK)))
    ohb = np.ascontiguousarray(np.broadcast_to(oh.reshape(1, -1), (128, MOBA_SLOTS * NBLK)))
    return dict(pm=pmb, oh=ohb, cm=cm)


def moba_shared_inputs(tab):
    boh = np.zeros((32, S), np.float32)
    for n in range(32):
        boh[n, n * 256:(n + 1) * 256] = 1.0
    return dict(ropek=tab, boh=boh, ident=np.eye(128, dtype=np.float32))


CH = 32


def conv_emit(P, sc, pfx="", fz=None):
    nc, es = P.nc, P.es
    T = TOK
    if fz is None:
        uc = P.din(pfx + "uc", [512, T + CH])
        yc = P.dout(pfx + "ycT", [256, T])
    else:
        yc = fz.Yfull
        flag_d = P.din("cflag", [128, 1])
        flag = P.sb(pfx + "cflag_sb", [128, 1], F32)
    cw = P.din(pfx + "cw", [128, 2, 31])
    cp = P.din(pfx + "cp", [128, 2, 3])
    idb = P.din("cident", [128, 128])
    a_t = [P.sb(pfx + f"ca{c}", [128, T + CH], F32) for c in range(2)]
    g_t = [P.sb(pfx + f"cg{c}", [128, T + CH], F32) for c in range(2)]
    hg = [P.sb(pfx + f"chg{c}", [128, T + CH], BF16) for c in range(2)]
    dg = [P.sb(pfx + f"cdg{c}", [128, 31, 128], BF16) for c in range(2)]
    cws = P.sb(pfx + "cws", [128, 2, 31], F32)
    cps = P.sb(pfx + "cps", [128, 2, 3], F32)
    idt = P.sb(pfx + "cidt", [128, 128], F32)
    onesf = P.sb(pfx + "cones", [128, 128], F32)
    epsc = P.sb(pfx + "ceps", [128, 1], F32)
    hc = [P.sb(pfx + f"chc{c}", [128, 512], F32) for c in range(2)]
    sq = [P.sb(pfx + f"csq{c}", [128, 512], F32) for c in range(2)]
    mean = P.sb(pfx + "cmean", [128, 512], F32)
    msq = P.sb(pfx + "cmsq", [128, 512], F32)
    var = P.sb(pfx + "cvar", [128, 512], F32)
    rstd = P.sb(pfx + "crstd", [128, 512], F32)
    tt_ = [P.sb(pfx + f"ctt{c}", [128, 512], F32) for c in range(2)]
    yo = [P.sb(pfx + f"cyo{c}", [128, 512], F32) for c in range(2)]
    ps = es.enter_context(nc.psum_tensor(pfx + "cps_", [128, 4, 512], F32)) if fz is None else fz.ps
    ba = [Buf(), Buf()]
    bg = [Buf(), Buf()]
    bhg = [Buf(), Buf()]
    bdg = [Buf(), Buf()]
    bc, bhc, bsq = Buf(), [Buf(), Buf()], [Buf(), Buf()]
    bmean, bmsq, bvar, brstd = Buf(), Buf(), Buf(), Buf()
    btt = [Buf(), Buf()]
    byo = [Buf(), Buf()]
    bps = [Buf() for _ in range(4)]

    sc.dma("sync", lambda e: e.dma_start(out=cws[:], in_=cw[:, :, :]), writes=[bc], key=pfx + "cc")
    sc.dma("sync", lambda e: e.dma_start(out=cps[:], in_=cp[:, :, :]), writes=[bc], key=pfx + "cc")
    sc.dma("sync", lambda e: e.dma_start(out=idt[:], in_=idb[:, :]), writes=[bc], key=pfx + "cc")
    sc.op("vector", lambda e: e.memset(onesf[:], 1.0), writes=[bc])
    sc.op("vector", lambda e: e.memset(epsc[:], EPS), writes=[bc])
    if fz is not None:
        sc.dma("sync", lambda e: e.dma_start(out=flag[:], in_=flag_d[:, :]), writes=[bc], key=pfx + "cc")

    def prev_rows(e, row0):
        return fz.LH[row0:row0 + 128, :]

    for c in range(2):
        if fz is None:
            sc.dma("sync", lambda e, c=c: e.dma_start(out=a_t[c][:], in_=uc[c * 128:(c + 1) * 128, :]), writes=[ba[c]],
                   key=pfx + f"ca{c}")
            sc.dma("sync", lambda e, c=c: e.dma_start(out=g_t[c][:], in_=uc[256 + c * 128:256 + (c + 1) * 128, :]),
                   writes=[bg[c]], key=pfx + f"cg{c}")
        else:
            sc.dma("sync", lambda e, c=c: e.dma_start(out=a_t[c][:, CH:], in_=fz.Usend[c * 128:(c + 1) * 128, :]),
                   reads=[fz.bU], writes=[ba[c]], key=pfx + f"ca{c}")
            sc.dma("sync", lambda e, c=c: e.dma_start(out=a_t[c][:, 0:CH], in_=prev_rows(e, c * 128)),
                   reads=[fz.bUr], writes=[ba[c]], key=pfx + f"ca{c}")
            sc.dma("sync", lambda e, c=c: e.dma_start(out=g_t[c][:, CH:], in_=fz.Usend[256 + c * 128:256 + (c + 1) * 128, :]),
                   reads=[fz.bU], writes=[bg[c]], key=pfx + f"cg{c}")
            sc.dma("sync", lambda e, c=c: e.dma_start(out=g_t[c][:, 0:CH], in_=prev_rows(e, 256 + c * 128)),
                   reads=[fz.bUr], writes=[bg[c]], key=pfx + f"cg{c}")
        sc.op("scalar", lambda e, c=c: e.activation(out=g_t[c][:], in_=g_t[c][:], func=AF.Sigmoid),
              reads=[bg[c]], writes=[bg[c]])
        sc.op("vector", lambda e, c=c: e.tensor_tensor(out=hg[c][:], in0=a_t[c][:], in1=g_t[c][:], op=ALU.mult),
              reads=[ba[c], bg[c]], writes=[bhg[c]])
        if fz is not None:
            sc.op("vector", lambda e, c=c: e.tensor_scalar(out=hg[c][:, 0:CH], in0=hg[c][:, 0:CH], scalar1=flag[:, 0:1],
                                                           scalar2=None, op0=ALU.mult), reads=[bhg[c], bc], writes=[bhg[c]])
        for k in range(31):
            sc.op("gpsimd", lambda e, c=c, k=k: e.tensor_scalar(out=dg[c][:, k, :], in0=idt[:], scalar1=cws[:, c, k:k + 1],
                                                                scalar2=None, op0=ALU.mult), reads=[bc], writes=[bdg[c]])
    for tt in range(T // 512):
        for c in range(2):
            for k in range(31):
                o = tt * 512 + 2 + k
                sc.op("tensor", lambda e, c=c, k=k, o=o: e.matmul(ps[:, c, :], lhsT=dg[c][:, k, :], rhs=hg[c][:, o:o + 512],
                                                                 start=(k == 0), stop=(k == 30)),
                      reads=[bdg[c], bhg[c]], writes=[bps[c]], inc=(k == 30))
            sc.op("scalar", lambda e, c=c: e.activation(out=hc[c][:], in_=ps[:, c, :], func=AF.Identity, bias=cps[:, c, 0:1],
                                                        scale=1.0), reads=[bps[c], bc], writes=[bhc[c]])
            sc.op("scalar", lambda e, c=c: e.activation(out=sq[c][:], in_=hc[c][:], func=AF.Square), reads=[bhc[c]],
                  writes=[bsq[c]])
        for c in range(2):
            sc.op("tensor", lambda e, c=c: e.matmul(ps[:, 2, :], lhsT=onesf[:], rhs=hc[c][:], start=(c == 0), stop=(c == 1)),
                  reads=[bc, bhc[c]], writes=[bps[2]])
        for c in range(2):
            sc.op("tensor", lambda e, c=c: e.matmul(ps[:, 3, :], lhsT=onesf[:], rhs=sq[c][:], start=(c == 0), stop=(c == 1)),
                  reads=[bc, bsq[c]], writes=[bps[3]])
        sc.op("vector", lambda e: e.tensor_scalar(out=mean[:], in0=ps[:, 2, :], scalar1=1.0 / 256, scalar2=None, op0=ALU.mult),
              reads=[bps[2]], writes=[bmean])
        sc.op("vector", lambda e: e.tensor_tensor(out=msq[:], in0=mean[:], in1=mean[:], op=ALU.mult), reads=[bmean], writes=[bmsq])
        sc.op("vector", lambda e: e.scalar_tensor_tensor(out=var[:], in0=ps[:, 3, :], scalar=1.0 / 256, in1=msq[:],
                                                         op0=ALU.mult, op1=ALU.subtract), reads=[bps[3], bmsq], writes=[bvar])
        sc.op("scalar", lambda e: e.activation(out=var[:], in_=var[:], func=AF.Sqrt, bias=epsc[:, 0:1], scale=1.0),
              reads=[bvar, bc], writes=[bvar])
        sc.op("vector", lambda e: e.reciprocal(out=rstd[:], in_=var[:]), reads=[bvar], writes=[brstd])
        for c in range(2):
            sc.op("vector", lambda e, c=c: e.tensor_tensor(out=tt_[c][:], in0=hc[c][:], in1=mean[:], op=ALU.subtract),
                  reads=[bhc[c], bmean], writes=[btt[c]])
            sc.op("vector", lambda e, c=c: e.tensor_tensor(out=tt_[c][:], in0=tt_[c][:], in1=rstd[:], op=ALU.mult),
                  reads=[btt[c], brstd], writes=[btt[c]])
            sc.op("scalar", lambda e, c=c: e.activation(out=yo[c][:], in_=tt_[c][:], func=AF.Silu, bias=cps[:, c, 2:3],
                                                        scale=cps[:, c, 1:2]), reads=[btt[c], bc], writes=[byo[c]])
            sc.dma("sync", lambda e, c=c, tt=tt: e.dma_start(out=yc[c * 128:(c + 1) * 128, tt * 512:(tt + 1) * 512], in_=yo[c][:]),
                   reads=[byo[c]] + ([] if fz is None else [fz.bY]), key=pfx + f"cyo{c}")
    return byo


def build_conv():
    P = SimpleProg()
    sc = Sched(P.nc, P.es)
    ob = conv_emit(P, sc)
    return P, P.finish(sc, ob)


def conv_inputs(u_b, j, conv_w, conv_b, ln_g, ln_b):
    t0 = j * TOK
    uc = np.zeros((512, TOK + CH), np.float32)
    lo = max(0, t0 - CH)
    uc[:, CH - (t0 - lo):] = u_b[lo:t0 + TOK, 0:512].T
    lay = lambda v: np.ascontiguousarray(v.reshape(2, 128).T)
    cw = np.ascontiguousarray(conv_w.T.reshape(2, 128, 31).transpose(1, 0, 2))
    cp = np.ascontiguousarray(np.stack([lay(conv_b), lay(ln_g), lay(ln_b)], axis=-1))
    return dict(uc=uc, cw=cw, cp=cp, cident=np.eye(128, dtype=np.float32))


GC = 64
NCH = S // GC
GSEG = 4
AX = mybir.AxisListType


def gdn_emit(P, sc, nu, pfx="", fz=None):
    import os
    STOP = float(os.environ.get("GDN_STOP", "99"))
    nc, es = P.nc, P.es
    if fz is None:
        raw_d = P.din(pfx + "graw", [nu, 3, 64, S + 3])
        gz_d = P.din(pfx + "gz", [nu, S, 64])
        ga_d = P.din(pfx + "ga", [nu, 64, NCH])
        gb_d = P.din(pfx + "gb", [nu, 64, NCH])
        go_d = P.dout(pfx + "go", [nu, S, 64])
    gcw_d = P.din(pfx + "gcw", [nu, 64, 12])
    gpar_d = P.din(pfx + "gpar", [nu, 64, 2])
    gng_d = P.din(pfx + "gng", [nu, 64, 64])
    gcst_d = P.din("gcst", [3, 64, 64])

    def unit_h(e, u):
        return fz.dyn(e, "gpsimd", ("gh", u))

    f = lambda n, shp: P.sb(pfx + n, shp, F32)
    cst = f("gcst_sb", [64, 3, 64])
    TriB = f("gTriB", [64, 8, 64])
    MB = f("gMB", [64, 8, 64])
    IB = f("gIB", [64, 8, 64])
    ones64 = f("gones", [64, 64])
    epsg = f("geps", [64, 1])
    gcw = f("gcw_sb", [64, 12])
    dgw = f("gdgw", [64, 12, 64])
    par = f("gpar_sb", [64, 2])
    negA = f("gnegA", [64, 1])
    ngb = f("gngb", [64, 64])
    a_t = f("ga_sb", [64, NCH])
    b_t = f("gb_sb", [64, NCH])
    g_t = f("gg", [64, NCH])
    beta = f("gbeta", [64, NCH])
    gc = f("ggc", [64, NCH])
    egc = f("gegc", [64, NCH])
    eglb = f("geglb", [64, NCH])
    edec = f("gedec", [64, NCH])
    bgk = f("gbgk", [64, NCH])
    SEGT = S // GSEG
    SEGC = NCH // GSEG
    raw = [f(f"graw{i}", [64, 515]) for i in range(2)]
    xa = [f(f"gxa{i}", [64, 512]) for i in range(2)]
    xq = f("gxq", [64, 512])
    rn = f("grn", [64, 512])
    qnT = f("gqnT", [64, SEGT])
    knT = f("gknT", [64, SEGT])
    Kt = f("gKt", [64, SEGC, 64])
    Vt = f("gVt", [64, SEGC, 64])
    oseg = f("goseg", [64, SEGC, 64])
    zseg = f("gzseg", [64, SEGC, 64])
    osq = f("gosq", [64, SEGC, 64])
    oss = f("goss", [64, SEGC])
    rhsD = f("grhsD", [64, 8, 64])
    ED = f("gED", [64, 8, 64])
    EDT = f("gEDT", [64, 8, 64])
    Lp = [f(f"gL{i}", [64, 8, 64]) for i in range(2)]
    Np = [f(f"gN{i}", [64, 8, 64]) for i in range(2)]
    Pm = f("gP", [64, 8, 64])
    Kbg = f("gKbg", [64, 8, 64])
    Vb = f("gVb", [64, 8, 64])
    kdec = f("gkdec", [64, 8, 64])
    u_sb = f("gu", [64, 8, 64])
    wT = f("gwT", [64, 8, 64])
    qkT = f("gqkT", [64, 8, 64])
    St = f("gS", [64, 64])
    vn = [f(f"gvn{i}", [64, 64]) for i in range(2)]
    As = [f(f"gAs{i}", [64, 64]) for i in range(2)]
    ps = es.enter_context(nc.psum_tensor(pfx + "gps", [64, 8, 512], F32)) if fz is None else fz.ps[0:64, :, :]

    B_ = lambda: Buf()
    bcst, bpar, bg = B_(), B_(), B_()
    braw = [B_(), B_()]
    bxa = [B_(), B_()]
    bxq, brn, bqn, bkn, bKt, bVt, boseg, bz, bosq, boss = (B_() for _ in range(10))
    brhsD, bED, bEDT, bP, bKbg, bVb, bkdec, bu, bwT, bqkT, bS = (B_() for _ in range(11))
    bL = [B_(), B_()]
    bN = [B_(), B_()]
    bvn = [B_(), B_()]
    bAs = [B_(), B_()]
    bps = [B_() for _ in range(8)]
    wk = [0]

    def nps():
        i = wk[0] % 4
        wk[0] += 1
        return i

    sc.dma("sync", lambda e: e.dma_start(out=cst[:], in_=gcst_d.rearrange("a p q -> p a q")), writes=[bcst], key=pfx + "gc")
    sc.op("vector", lambda e: e.memset(ones64[:], 1.0), writes=[bcst])
    sc.op("vector", lambda e: e.memset(epsg[:], EPS), writes=[bcst])
    for j in range(8):
        sc.op("vector", lambda e, j=j: e.tensor_copy(out=TriB[:, j, :], in_=cst[:, 0, :]), reads=[bcst], writes=[bcst])
        sc.op("vector", lambda e, j=j: e.tensor_copy(out=MB[:, j, :], in_=cst[:, 1, :]), reads=[bcst], writes=[bcst])
        sc.op("vector", lambda e, j=j: e.tensor_copy(out=IB[:, j, :], in_=cst[:, 2, :]), reads=[bcst], writes=[bcst])
    Tri = cst[:, 0, :]
    I64 = cst[:, 2, :]

    def bc_n(t, n0):
        return t[:, n0:n0 + 8].unsqueeze(2).to_broadcast([64, 8, 64])

    for u in range(nu):
        sc.dma("sync", lambda e, u=u: e.dma_start(out=gcw[:], in_=gcw_d[u]), writes=[bpar], key=pfx + "gp")
        sc.dma("sync", lambda e, u=u: e.dma_start(out=par[:], in_=gpar_d[u]), writes=[bpar], key=pfx + "gp")
        sc.dma("sync", lambda e, u=u: e.dma_start(out=ngb[:], in_=gng_d[u]), writes=[bpar], key=pfx + "gp")
        if fz is None:
            sc.dma("sync", lambda e, u=u: e.dma_start(out=a_t[:], in_=ga_d[u]), writes=[bpar], key=pfx + "gp")
            sc.dma("sync", lambda e, u=u: e.dma_start(out=b_t[:], in_=gb_d[u]), writes=[bpar], key=pfx + "gp")
        else:
            for rr in range(4):
                for (dst, ro) in ((a_t, 3200), (b_t, 3206)):
                    def absrc(e, u=u, rr=rr, ro=ro):
                        return fz.LAB[u, (0 if ro == 3200 else 1):(1 if ro == 3200 else 2), rr * TOK:(rr + 1) * TOK].rearrange(
                            "o (n s) -> s (o n)", s=64)
                    sc.dma("gpsimd", lambda e, dst=dst, rr=rr, absrc=absrc: e.dma_start(
                        out=dst[:, rr * 32:(rr + 1) * 32], in_=absrc(e), allow_slow_non_contiguous=True),
                        reads=[fz.bUr], writes=[bpar], key=pfx + "gp")
        for k in range(12):
            sc.op("gpsimd", lambda e, k=k: e.tensor_scalar(out=dgw[:, k, :], in0=I64, scalar1=gcw[:, k:k + 1], scalar2=None,
                                                           op0=ALU.mult), reads=[bcst, bpar], writes=[bpar])
        sc.op("scalar", lambda e: e.activation(out=negA[:], in_=par[:, 0:1], func=AF.Exp), reads=[bpar], writes=[bg])
        sc.op("vector", lambda e: e.tensor_scalar(out=negA[:], in0=negA[:], scalar1=-1.0, scalar2=None, op0=ALU.mult),
              reads=[bg], writes=[bg])
        sc.op("scalar", lambda e: e.activation(out=g_t[:], in_=a_t[:], func=AF.Exp, bias=par[:, 1:2], scale=1.0),
              reads=[bpar, bg], writes=[bg])
        sc.op("scalar", lambda e: e.activation(out=g_t[:], in_=g_t[:], func=AF.Ln, bias=1.0, scale=1.0), reads=[bg], writes=[bg])
        sc.op("vector", lambda e: e.tensor_scalar(out=g_t[:], in0=g_t[:], scalar1=negA[:, 0:1], scalar2=None, op0=ALU.mult),
              reads=[bg], writes=[bg])
        sc.op("scalar", lambda e: e.activation(out=beta[:], in_=b_t[:], func=AF.Sigmoid), reads=[bpar, bg], writes=[bg])
        sc.op("tensor", lambda e: e.matmul(ps[:, 7, 0:NCH], lhsT=Tri, rhs=g_t[:], start=True, stop=True),
              reads=[bcst, bg], writes=[bps[7]])
        sc.op("tensor", lambda e: e.matmul(ps[:, 7, NCH:2 * NCH], lhsT=ones64[:], rhs=g_t[:], start=True, stop=True),
              reads=[bcst, bg], writes=[bps[7]])
        sc.op("vector", lambda e: e.tensor_copy(out=gc[:], in_=ps[:, 7, 0:NCH]), reads=[bps[7], bg], writes=[bg])
        sc.op("vector", lambda e: e.tensor_copy(out=eglb[:], in_=ps[:, 7, NCH:2 * NCH]), reads=[bps[7], bg], writes=[bg])
        sc.op("vector", lambda e: e.tensor_tensor(out=edec[:], in0=eglb[:], in1=gc[:], op=ALU.subtract), reads=[bg], writes=[bg])
        sc.op("scalar", lambda e: e.activation(out=egc[:], in_=gc[:], func=AF.Exp), reads=[bg], writes=[bg])
        sc.op("scalar", lambda e: e.activation(out=eglb[:], in_=eglb[:], func=AF.Exp), reads=[bg], writes=[bg])
        sc.op("scalar", lambda e: e.activation(out=edec[:], in_=edec[:], func=AF.Exp), reads=[bg], writes=[bg])
        sc.op("vector", lambda e: e.tensor_tensor(out=bgk[:], in0=beta[:], in1=egc[:], op=ALU.mult), reads=[bg], writes=[bg])
        sc.op("vector", lambda e: e.memset(St[:], 0.0), writes=[bS])
        if STOP <= 1:
            return [bS]

        for seg in range(GSEG):
            for tt in range(SEGT // 512):
                c0 = seg * SEGT + tt * 512
                for j in range(3):
                    ri = (tt * 3 + j) % 2
                    if fz is None:
                        sc.dma("sync", lambda e, u=u, j=j, ri=ri, c0=c0: e.dma_start(out=raw[ri][:], in_=raw_d[u, j, :, c0:c0 + 515]),
                               writes=[braw[ri]], key=pfx + f"graw{ri}")
                    else:
                        rr, t0 = c0 // TOK, c0 % TOK

                        def rsrc(e, rr_, ta, tb, u=u, j=j):
                            return fz.LG[u, j, :, rr_ * TOK + ta:rr_ * TOK + tb]
                        sc.dma("gpsimd", lambda e, ri=ri, rr=rr, t0=t0, rsrc=rsrc: e.dma_start(out=raw[ri][:, 3:515],
                                                                                            in_=rsrc(e, rr, t0, t0 + 512)),
                               reads=[fz.bUr], writes=[braw[ri]], key=pfx + f"graw{ri}")
                        if t0 >= 3:
                            sc.dma("gpsimd", lambda e, ri=ri, rr=rr, t0=t0, rsrc=rsrc: e.dma_start(out=raw[ri][:, 0:3],
                                                                                                in_=rsrc(e, rr, t0 - 3, t0)),
                                   reads=[fz.bUr], writes=[braw[ri]], key=pfx + f"graw{ri}")
                        elif rr > 0:
                            sc.dma("gpsimd", lambda e, ri=ri, rr=rr, rsrc=rsrc: e.dma_start(out=raw[ri][:, 0:3],
                                                                                         in_=rsrc(e, rr - 1, TOK - 3, TOK)),
                                   reads=[fz.bUr], writes=[braw[ri]], key=pfx + f"graw{ri}")
                        else:
                            sc.op("vector", lambda e, ri=ri: e.memset(raw[ri][:, 0:3], 0.0), writes=[braw[ri]])
                    p1 = nps()
                    for k in range(4):
                        sc.op("tensor", lambda e, j=j, k=k, ri=ri, p1=p1: e.matmul(ps[:, p1, :], lhsT=dgw[:, j * 4 + k, :],
                                                                                 rhs=raw[ri][:, k:k + 512], start=(k == 0), stop=(k == 3)),
                              reads=[bpar, braw[ri]], writes=[bps[p1]], inc=(k == 3))
                    xi = j % 2
                    sc.op("scalar", lambda e, xi=xi, p1=p1: e.activation(out=xa[xi][:], in_=ps[:, p1, :], func=AF.Silu),
                          reads=[bps[p1]], writes=[bxa[xi]])
                    if j < 2:
                        sc.op("scalar", lambda e, xi=xi: e.activation(out=xq[:], in_=xa[xi][:], func=AF.Square),
                              reads=[bxa[xi]], writes=[bxq])
                        p2 = nps()
                        sc.op("tensor", lambda e, p2=p2: e.matmul(ps[:, p2, :], lhsT=ones64[:], rhs=xq[:], start=True, stop=True),
                              reads=[bcst, bxq], writes=[bps[p2]])
                        sc.op("scalar", lambda e, p2=p2: e.activation(out=rn[:], in_=ps[:, p2, :], func=AF.Sqrt, bias=epsg[:, 0:1],
                                                                      scale=1.0), reads=[bps[p2], bcst], writes=[brn])
                        sc.op("vector", lambda e: e.reciprocal(out=rn[:], in_=rn[:]), reads=[brn], writes=[brn])
                        dst, bd = (qnT, bqn) if j == 0 else (knT, bkn)
                        scl = 0.125 if j == 0 else 1.0
                        sc.op("vector", lambda e, xi=xi, dst=dst, tt=tt, scl=scl: e.scalar_tensor_tensor(
                            out=dst[:, tt * 512:(tt + 1) * 512], in0=xa[xi][:], scalar=scl, in1=rn[:], op0=ALU.mult, op1=ALU.mult),
                            reads=[bxa[xi], brn], writes=[bd])
                    if j >= 1:
                        srcT = knT[:, tt * 512:(tt + 1) * 512] if j == 1 else xa[xi][:]
                        bsrc = bkn if j == 1 else bxa[xi]
                        p3 = nps()
                        for cj in range(8):
                            sc.op("tensor", lambda e, srcT=srcT, cj=cj, p3=p3: e.transpose(ps[:, p3, cj * 64:(cj + 1) * 64],
                                                                                         srcT[:, cj * 64:(cj + 1) * 64], I64),
                                  reads=[bsrc, bcst], writes=[bps[p3]], inc=(cj == 7))
                        dstT, bdt = (Kt, bKt) if j == 1 else (Vt, bVt)
                        sc.op("vector", lambda e, dstT=dstT, tt=tt, p3=p3: e.tensor_copy(
                            out=dstT[:, tt * 8:(tt + 1) * 8, :], in_=ps[:, p3, :].rearrange("p (a b) -> p a b", b=64)),
                            reads=[bps[p3]], writes=[bdt])
            if fz is None:
                sc.dma("sync", lambda e, u=u, seg=seg: e.dma_start(
                    out=zseg[:], in_=gz_d[u, seg * SEGT:(seg + 1) * SEGT, :].rearrange("(n s) d -> s n d", s=64)),
                    writes=[bz], key=pfx + "gz")
            else:
                def zsrc(e, u=u, seg=seg):
                    return fz.LZ[u, :, seg * TOK:(seg + 1) * TOK]
                sc.dma("gpsimd", lambda e, zsrc=zsrc: e.dma_start(out=zseg[:].rearrange("p a b -> p (a b)"), in_=zsrc(e)),
                       reads=[fz.bUr], writes=[bz], key=pfx + "gz")
            if STOP <= 2:
                return [bz, bKt, bVt, bqn]
            for gi in range(SEGC // 8):
                l0 = gi * 8
                n0 = seg * SEGC + l0
                v3 = lambda t: t[:]
                pk, pd, pdt = nps(), nps(), nps()
                for j in range(8):
                    cs = slice((l0 + j) * 64, (l0 + j + 1) * 64)
                    sc.op("tensor", lambda e, j=j, cs=cs, pk=pk: e.matmul(ps[:, pk, j * 64:(j + 1) * 64], lhsT=knT[:, cs], rhs=knT[:, cs],
                                                                         start=True, stop=True), reads=[bkn], writes=[bps[pk]], inc=(j == 7))
                sc.op("vector", lambda e, n0=n0: e.tensor_tensor(out=rhsD[:], in0=MB[:], in1=bc_n(g_t, n0), op=ALU.mult),
                      reads=[bcst, bg], writes=[brhsD])
                if STOP <= 2.1:
                    return [brhsD, bps[pk]]
                sc.op("tensor", lambda e, pd=pd: e.matmul(ps[:, pd, :], lhsT=Tri, rhs=rhsD[:].rearrange("p a b -> p (a b)"),
                                                          start=True, stop=True), reads=[bcst, brhsD], writes=[bps[pd]])
                for j in range(8):
                    sc.op("tensor", lambda e, j=j, pdt=pdt: e.matmul(ps[:, pdt, j * 64:(j + 1) * 64], lhsT=rhsD[:, j, :], rhs=Tri,
                                                                    start=True, stop=True), reads=[bcst, brhsD], writes=[bps[pdt]], inc=(j == 7))
                r3 = lambda ap: ap.rearrange("p (a b) -> p a b", b=64)
                sc.op("scalar", lambda e, pd=pd: e.activation(out=ED[:], in_=r3(ps[:, pd, :]), func=AF.Exp), reads=[bps[pd]], writes=[bED])
                sc.op("scalar", lambda e, pdt=pdt: e.activation(out=EDT[:], in_=r3(ps[:, pdt, :]), func=AF.Exp), reads=[bps[pdt]], writes=[bEDT])
                if STOP <= 2.2:
                    return [bED, bEDT]
                sc.op("vector", lambda e, pk=pk: e.tensor_tensor(out=Lp[0][:], in0=r3(ps[:, pk, :]), in1=ED[:], op=ALU.mult),
                      reads=[bps[pk], bED], writes=[bL[0]])
                sc.op("vector", lambda e, n0=n0: e.tensor_tensor(out=Lp[0][:], in0=Lp[0][:], in1=bc_n(beta, n0), op=ALU.mult),
                      reads=[bL[0], bg], writes=[bL[0]])
                sc.op("vector", lambda e: e.tensor_tensor(out=Lp[0][:], in0=Lp[0][:], in1=MB[:], op=ALU.mult),
                      reads=[bL[0], bcst], writes=[bL[0]])
                if STOP <= 2.3:
                    return [bL[0]]
                pn = nps()
                for j in range(8):
                    sc.op("tensor", lambda e, j=j, pn=pn: e.matmul(ps[:, pn, j * 64:(j + 1) * 64], lhsT=Lp[0][:, j, :], rhs=I64,
                                                                  start=True, stop=True),
                          reads=[bL[0], bcst], writes=[bps[pn]], inc=(j == 7))
                sc.op("scalar", lambda e, pn=pn: e.copy(out=Np[0][:], in_=r3(ps[:, pn, :])), reads=[bps[pn]], writes=[bN[0]])
                sc.op("vector", lambda e: e.tensor_tensor(out=Pm[:], in0=IB[:], in1=Np[0][:], op=ALU.subtract),
                      reads=[bN[0], bcst], writes=[bP])
                if STOP <= 2.4:
                    return [bP, bN[0]]
                cur = 0
                for lvl in range(5):
                    nxt = 1 - cur
                    pl = nps()
                    for j in range(8):
                        sc.op("tensor", lambda e, j=j, pl=pl, cur=cur: e.matmul(ps[:, pl, j * 64:(j + 1) * 64], lhsT=Np[cur][:, j, :],
                                                                               rhs=Lp[cur][:, j, :], start=True, stop=True),
                              reads=[bN[cur], bL[cur]], writes=[bps[pl]], inc=(j == 7))
                    if lvl < 4:
                        pn2 = nps()
                        for j in range(8):
                            sc.op("tensor", lambda e, j=j, pn2=pn2, cur=cur: e.matmul(ps[:, pn2, j * 64:(j + 1) * 64], lhsT=Lp[cur][:, j, :],
                                                                                     rhs=Np[cur][:, j, :], start=True, stop=True),
                                  reads=[bN[cur], bL[cur]], writes=[bps[pn2]], inc=(j == 7))
                    sc.op("scalar", lambda e, pl=pl, nxt=nxt: e.copy(out=Lp[nxt][:], in_=r3(ps[:, pl, :])), reads=[bps[pl]], writes=[bL[nxt]])
                    if lvl < 4:
                        sc.op("vector", lambda e, pn2=pn2, nxt=nxt: e.tensor_copy(out=Np[nxt][:], in_=r3(ps[:, pn2, :])),
                              reads=[bps[pn2]], writes=[bN[nxt]])
                    pu = nps()
                    for j in range(8):
                        sc.op("tensor", lambda e, j=j, pu=pu, nxt=nxt: e.matmul(ps[:, pu, j * 64:(j + 1) * 64], lhsT=Lp[nxt][:, j, :],
                                                                               rhs=Pm[:, j, :], start=True, stop=True),
                              reads=[bL[nxt], bP], writes=[bps[pu]], inc=(j == 7))
                    sc.op("vector", lambda e, pu=pu: e.tensor_tensor(out=Pm[:], in0=Pm[:], in1=r3(ps[:, pu, :]), op=ALU.add),
                          reads=[bps[pu], bP], writes=[bP])
                    cur = nxt
                if STOP <= 2.5:
                    return [bP]
                sc.op("vector", lambda e, l0=l0, n0=n0: e.tensor_tensor(out=Kbg[:], in0=Kt[:, l0:l0 + 8, :], in1=bc_n(bgk, n0), op=ALU.mult),
                      reads=[bKt, bg], writes=[bKbg])
                sc.op("vector", lambda e, l0=l0, n0=n0: e.tensor_tensor(out=Vb[:], in0=Vt[:, l0:l0 + 8, :], in1=bc_n(beta, n0), op=ALU.mult),
                      reads=[bVt, bg], writes=[bVb])
                sc.op("vector", lambda e, l0=l0, n0=n0: e.tensor_tensor(out=kdec[:], in0=Kt[:, l0:l0 + 8, :], in1=bc_n(edec, n0), op=ALU.mult),
                      reads=[bKt, bg], writes=[bkdec])
                p_u, p_w, p_q = nps(), nps(), nps()
                for j in range(8):
                    sc.op("tensor", lambda e, j=j, p_u=p_u: e.matmul(ps[:, p_u, j * 64:(j + 1) * 64], lhsT=Pm[:, j, :], rhs=Vb[:, j, :],
                                                                    start=True, stop=True), reads=[bP, bVb], writes=[bps[p_u]], inc=(j == 7))
                for j in range(8):
                    sc.op("tensor", lambda e, j=j, p_w=p_w: e.matmul(ps[:, p_w, j * 64:(j + 1) * 64], lhsT=Kbg[:, j, :], rhs=Pm[:, j, :],
                                                                    start=True, stop=True), reads=[bP, bKbg], writes=[bps[p_w]], inc=(j == 7))
                for j in range(8):
                    cs = slice((l0 + j) * 64, (l0 + j + 1) * 64)
                    sc.op("tensor", lambda e, j=j, cs=cs, p_q=p_q: e.matmul(ps[:, p_q, j * 64:(j + 1) * 64], lhsT=knT[:, cs], rhs=qnT[:, cs],
                                                                           start=True, stop=True), reads=[bkn, bqn], writes=[bps[p_q]], inc=(j == 7))
                sc.op("scalar", lambda e, p_u=p_u: e.copy(out=u_sb[:], in_=r3(ps[:, p_u, :])), reads=[bps[p_u]], writes=[bu])
                sc.op("scalar", lambda e, p_w=p_w: e.copy(out=wT[:], in_=r3(ps[:, p_w, :])), reads=[bps[p_w]], writes=[bwT])
                sc.op("vector", lambda e, p_q=p_q: e.tensor_tensor(out=qkT[:], in0=r3(ps[:, p_q, :]), in1=EDT[:], op=ALU.mult),
                      reads=[bps[p_q], bEDT], writes=[bqkT])
                sc.op("vector", lambda e: e.tensor_tensor(out=qkT[:], in0=qkT[:], in1=TriB[:], op=ALU.mult), reads=[bqkT, bcst], writes=[bqkT])
                if STOP <= 3:
                    return [bqkT, bu, bwT]
                for j in range(8):
                    n = n0 + j
                    l = l0 + j
                    cs = slice(l * 64, (l + 1) * 64)
                    i2 = j % 2
                    bx_, by_ = 4 + i2, 6 + i2
                    sc.op("tensor", lambda e, j=j, bx_=bx_: e.matmul(ps[:, bx_, 0:64], lhsT=wT[:, j, :], rhs=St[:], start=True, stop=True),
                          reads=[bwT, bS], writes=[bps[bx_]])
                    sc.op("tensor", lambda e, cs=cs, by_=by_: e.matmul(ps[:, by_, 0:64], lhsT=qnT[:, cs], rhs=St[:], start=True, stop=True),
                          reads=[bqn, bS], writes=[bps[by_]])
                    sc.op("vector", lambda e, j=j, bx_=bx_, i2=i2: e.tensor_tensor(out=vn[i2][:], in0=u_sb[:, j, :], in1=ps[:, bx_, 0:64],
                                                                                 op=ALU.subtract), reads=[bu, bps[bx_]], writes=[bvn[i2]])
                    sc.op("scalar", lambda e, by_=by_, i2=i2, n=n: e.activation(out=As[i2][:], in_=ps[:, by_, 0:64], func=AF.Copy,
                                                                               scale=egc[:, n:n + 1]), reads=[bps[by_], bg], writes=[bAs[i2]])
                    sc.op("tensor", lambda e, j=j, bx_=bx_, i2=i2: e.matmul(ps[:, bx_, 64:128], lhsT=qkT[:, j, :], rhs=vn[i2][:],
                                                                          start=True, stop=True), reads=[bqkT, bvn[i2]], writes=[bps[bx_]], inc=False)
                    sc.op("tensor", lambda e, j=j, bx_=bx_, i2=i2: e.matmul(ps[:, bx_, 128:192], lhsT=kdec[:, j, :], rhs=vn[i2][:],
                                                                          start=True, stop=True), reads=[bkdec, bvn[i2]], writes=[bps[bx_]])
                    sc.op("vector", lambda e, bx_=bx_, n=n: e.scalar_tensor_tensor(out=St[:], in0=St[:], scalar=eglb[:, n:n + 1],
                                                                                  in1=ps[:, bx_, 128:192], op0=ALU.mult, op1=ALU.add),
                          reads=[bS, bg, bps[bx_]], writes=[bS])
                    sc.op("vector", lambda e, bx_=bx_, i2=i2, l=l: e.tensor_tensor(out=oseg[:, l, :], in0=As[i2][:], in1=ps[:, bx_, 64:128],
                                                                                 op=ALU.add), reads=[bAs[i2], bps[bx_]], writes=[boseg])
                if STOP <= 4:
                    return [boseg, bS]
            sc.op("gpsimd", lambda e: e.tensor_tensor(out=osq[:], in0=oseg[:], in1=oseg[:], op=ALU.mult), reads=[boseg], writes=[bosq])
            sc.op("vector", lambda e: e.tensor_reduce(out=oss[:], in_=osq[:], axis=AX.X, op=ALU.add), reads=[bosq], writes=[boss])
            sc.op("scalar", lambda e: e.activation(out=oss[:], in_=oss[:], func=AF.Sqrt, bias=epsg[:, 0:1], scale=1.0 / 64),
                  reads=[boss, bcst], writes=[boss])
            sc.op("vector", lambda e: e.reciprocal(out=oss[:], in_=oss[:]), reads=[boss], writes=[boss])
            sc.op("vector", lambda e: e.tensor_tensor(out=osq[:], in0=oseg[:], in1=oss[:].unsqueeze(2).to_broadcast([64, SEGC, 64]),
                                                      op=ALU.mult), reads=[boseg, boss], writes=[bosq])
            sc.op("gpsimd", lambda e: e.tensor_tensor(out=osq[:], in0=osq[:], in1=ngb[:].unsqueeze(1).to_broadcast([64, SEGC, 64]),
                                                      op=ALU.mult), reads=[bosq, bpar], writes=[bosq])
            sc.op("scalar", lambda e: e.activation(out=zseg[:], in_=zseg[:], func=AF.Silu), reads=[bz], writes=[bz])
            if fz is None:
                sc.op("vector", lambda e: e.tensor_tensor(out=osq[:], in0=osq[:], in1=zseg[:], op=ALU.mult), reads=[bosq, bz], writes=[bosq])
                sc.dma("sync", lambda e, u=u, seg=seg: e.dma_start(
                    out=go_d[u, seg * SEGT:(seg + 1) * SEGT, :].rearrange("(n s) d -> s n d", s=64), in_=osq[:]),
                    reads=[bosq], key=pfx + "go")
            else:
                oT = oseg[:].rearrange("p a b -> p (a b)")
                zT = zseg[:].rearrange("p a b -> p (a b)")
                for g4 in range(SEGC // 8):
                    pt_ = nps()
                    for j in range(8):
                        sc.op("tensor", lambda e, j=j, g4=g4, pt_=pt_: e.transpose(ps[:, pt_, j * 64:(j + 1) * 64], osq[:, g4 * 8 + j, :], I64),
                              reads=[bosq, bcst], writes=[bps[pt_]], inc=(j == 7))
                    sc.op("vector", lambda e, g4=g4, pt_=pt_: e.tensor_tensor(out=oT[:, g4 * 512:(g4 + 1) * 512], in0=ps[:, pt_, :],
                                                                            in1=zT[:, g4 * 512:(g4 + 1) * 512], op=ALU.mult),
                          reads=[bps[pt_], bz, bosq], writes=[boseg])
                for kk in range(2):
                    row0 = seg * 512 + 192 + (u * 2 + kk) * 64
                    sc.dma("sync", lambda e, row0=row0, kk=kk: e.dma_start(out=fz.Ysend[row0:row0 + 64, :],
                                                                          in_=oT[:, kk * 1024:(kk + 1) * 1024]),
                           reads=[boseg, fz.bYs], key=pfx + "go")
    return [bosq, boseg]


def build_gdn(nu=2):
    P = SimpleProg()
    sc = Sched(P.nc, P.es)
    ob = gdn_emit(P, sc, nu)
    return P, P.finish(sc, ob)


def gdn_const_inputs():
    i = np.arange(64)
    tri = (i[:, None] <= i[None, :]).astype(np.float32)
    ms = (i[:, None] > i[None, :]).astype(np.float32)
    return dict(gcst=np.stack([tri, ms, np.eye(64, dtype=np.float32)]))


def gdn_unit_inputs(ug, h, gdn_conv_w, a_log, dt_bias, norm_g):
    GW = 384
    raw = np.zeros((3, 64, S + 3), np.float32)
    cw = np.zeros((64, 12), np.float32)
    for j in range(3):
        cols = slice(j * GW + h * 64, j * GW + (h + 1) * 64)
        raw[j, :, 3:] = ug[:, cols].T
        cw[:, j * 4:(j + 1) * 4] = gdn_conv_w[:, cols].T
    z = np.ascontiguousarray(ug[:, 3 * GW + h * 64:3 * GW + (h + 1) * 64])
    a = np.ascontiguousarray(ug[:, 4 * GW + h].reshape(NCH, 64).T)
    b = np.ascontiguousarray(ug[:, 4 * GW + 6 + h].reshape(NCH, 64).T)
    par = np.zeros((64, 2), np.float32)
    par[:, 0] = a_log[h]
    par[:, 1] = dt_bias[h]
    ng = np.ascontiguousarray(np.broadcast_to(norm_g[None, :], (64, 64))).astype(np.float32)
    return dict(graw=raw, gcw=cw, gz=z, ga=a, gb=b, gpar=par, gng=ng)


def _lay(v):
    return np.ascontiguousarray(np.asarray(v, np.float32).reshape(-1, 128).T)


_PROGS = {}


def _prog(key, builder):
    if key not in _PROGS:
        _PROGS[key] = builder()
    return _PROGS[key]


def _run(nc, in_maps):
    res = run_bass_kernel_spmd(nc, in_maps, core_ids=list(range(NCORES)))
    return res.results


def _tok_launch(key, stages, inp, xT_list, yT_list=None):
    def mk():
        p = TokProg(stages)
        return p, p.build()
    P, nc = _prog(key, mk)
    maps = []
    for c in range(NCORES):
        b = c // 4
        m = {"xT": xT_list[c], "cT": _lay(inp["c"][b])}
        for name in P.in_names:
            if name in m:
                continue
            if name.startswith("yT"):
                m[name] = yT_list[c]
            elif name == "final_g":
                m[name] = _lay(inp["final_g"])
            else:
                base, l = name[:-1], int(name[-1])
                arr = np.asarray(inp[base][l], np.float32)
                if base == "b_ada" or base.startswith("ln_"):
                    arr = _lay(arr)
                m[name] = np.ascontiguousarray(arr)
        maps.append(m)
    return _run(nc, maps)


def _mixer(inp, l, u):
    y = np.zeros((B, S, D), np.float32)
    P, nc = _prog("conv", build_conv)
    maps = []
    for c in range(NCORES):
        b, j = c // 4, c % 4
        maps.append(conv_inputs(u[b], j, np.asarray(inp["conv_w"][l]), np.asarray(inp["conv_b"][l]),
                                np.asarray(inp["conv_ln_g"][l]), np.asarray(inp["conv_ln_b"][l])))
    res = _run(nc, maps)
    for c in range(NCORES):
        b, j = c // 4, c % 4
        y[b, j * TOK:(j + 1) * TOK, 0:256] = res[c]["ycT"].T
    P, nc = _prog("moba", lambda: build_moba(3))
    tab = rope_tables()
    shared = moba_shared_inputs(tab)
    consts = [moba_const_inputs(0), moba_const_inputs(1)]
    maps = []
    for c in range(NCORES):
        b, cc = c // 4, c % 4
        units = []
        for s in range(3):
            combo = 3 * cc + s
            h, half = combo // 2, combo % 2
            q = u[b, :, 512 + h * 64:512 + (h + 1) * 64]
            k = u[b, :, 512 + 384 + h * 64:512 + 384 + (h + 1) * 64]
            v = u[b, :, 512 + 768 + h * 64:512 + 768 + (h + 1) * 64]
            d = moba_unit_inputs(q, k, v, half, tab)
            d.update(consts[half])
            units.append(d)
        m = {k_: np.ascontiguousarray(np.stack([un[k_] for un in units])) for k_ in units[0]}
        m.update(shared)
        maps.append(m)
    res = _run(nc, maps)
    for c in range(NCORES):
        b, cc = c // 4, c % 4
        for s in range(3):
            combo = 3 * cc + s
            h, half = combo // 2, combo % 2
            qpos = np.concatenate([np.arange(bl * 256, (bl + 1) * 256) for bl in HALF_BLOCKS[half]])
            y[b, qpos, 256 + h * 64:256 + (h + 1) * 64] = res[c]["moT"][s].T
    P, nc = _prog("gdn", lambda: build_gdn(2))
    gconst = gdn_const_inputs()
    allu = [(b, h) for b in range(B) for h in range(6)]
    maps = []
    assign = []
    for c in range(NCORES):
        us = [allu[i] if i < len(allu) else allu[0] for i in (2 * c, 2 * c + 1)]
        assign.append([(i < len(allu)) for i in (2 * c, 2 * c + 1)])
        units = [gdn_unit_inputs(u[b, :, 512 + 1152:], h, np.asarray(inp["gdn_conv_w"][l]), np.asarray(inp["gdn_a_log"][l]),
                                 np.asarray(inp["gdn_dt_bias"][l]), np.asarray(inp["gdn_norm_g"][l])) for (b, h) in us]
        m = {k_: np.ascontiguousarray(np.stack([un[k_] for un in units])) for k_ in units[0]}
        m.update(gconst)
        maps.append(m)
    res = _run(nc, maps)
    for c in range(NCORES):
        for s in range(2):
            i = 2 * c + s
            if i < len(allu):
                b, h = allu[i]
                y[b, :, 640 + h * 64:640 + (h + 1) * 64] = res[c]["go"][s]
    return y


YROWS = 768 + 1024
RG = [[0, 1, 2, 3], [4, 5, 6, 7]]


def moba_unit(cc, su):
    return (cc, su) if su < 2 else (4 + cc // 2, cc % 2)


def moba_owner(h, half):
    return (h, half) if h < 4 else (2 * (h - 4) + half, 2)


class Fused:
    def __init__(self):
        self.nc = bass.Bass("TRN2", target_bir_lowering=False)
        self.es = ExitStack()
        self.cur = self.es
        self.dins = {}
        self.in_names = []
        self.out_names = []
        self.phase_i = 0
        self.load_x = False
        self.store_x = False
        self._dyn = {}

    def din(self, name, shape, dt=F32):
        if name not in self.dins:
            self.in_names.append(name)
            self.dins[name] = self.nc.dram_tensor(name, list(shape), dt, kind="ExternalInput").ap()
        return self.dins[name]

    def dout(self, name, shape, dt=F32):
        if name not in self.dins:
            self.out_names.append(name)
            self.dins[name] = self.nc.dram_tensor(name, list(shape), dt, kind="ExternalOutput").ap()
        return self.dins[name]

    AW = 36800

    def sb(self, name, shape, dt):
        p = shape[0]
        n = int(np.prod(shape[1:]))
        n32 = n if dt == F32 else (n + 1) // 2
        n32 = (n32 + 7) // 8 * 8
        off = self.aoff
        self.aoff += n32
        assert self.aoff <= self.AW, (name, self.aoff)
        v = self.arena[0:p, off:off + n32]
        if dt != F32:
            v = v.bitcast(dt)
        v = v[:, 0:n]
        if len(shape) == 3:
            v = v.rearrange("p (a b) -> p a b", a=shape[1])
        return v

    def dyn(self, e, engname, key):
        c = self._dyn.setdefault(engname, {})
        if "cc" not in c:
            c["cc"] = e.snap(e.partition_id() % 4)
        if key not in c:
            cc = c["cc"]
            doff = lambda h: (h // 2) * 512 + (h % 2) * 64
            v = {"c2048": lambda: cc * 2048, "prev": lambda: (cc + 3) % 4,
                 "D0": lambda: doff(cc), "D2": lambda: (cc // 2) * 64 + 1024, "mha2": lambda: cc % 2, "mhb2": lambda: 3 - cc % 2,
                 "gh1": lambda: (cc + 4) % 6, "Dg1": lambda: doff((cc + 4) % 6)}[key]()
            c[key] = e.snap(v)
        return c[key]

    def build(self):
        nc, es = self.nc, self.es
        sc = self.sc = Sched(nc, es)
        self.x = es.enter_context(nc.sbuf_tensor("x_res", [128, KC, TOK], F32))
        self.arena = es.enter_context(nc.sbuf_tensor("arena", [128, self.AW], F32))
        self.aoff = 0
        self.bx = [[Buf(f"x{c}_{t}") for t in range(TOK // 512)] for c in range(KC)]
        self.ps = es.enter_context(nc.psum_tensor("ps_all", [128, 8, 512], F32))
        NUC = (DIN + 127) // 128
        Usend_t = nc.dram_tensor("Usend", [NUC * 128, TOK], F32)
        Urecv_t = nc.dram_tensor("Urecv", [NUC * 512 + 128, TOK], F32)
        Ysend_t = nc.dram_tensor("Ysend", [2048, 1024], F32)
        Yrecv_t = nc.dram_tensor("Yrecv", [8192, 1024], F32)
        Yfull_t = nc.dram_tensor("Yfull", [D, TOK], F32)
        self.Usend, self.Urecv, self.Ysend, self.Yrecv, self.Yfull = (t.ap() for t in (Usend_t, Urecv_t, Ysend_t, Yrecv_t, Yfull_t))
        self.LK = nc.dram_tensor("LK", [3, 64, S], F32).ap()
        self.LV = nc.dram_tensor("LV", [3, 64, S], F32).ap()
        self.LQ = nc.dram_tensor("LQ", [3, 64, MOBA_SLOTS * 256], F32).ap()
        self.LQF = nc.dram_tensor("LQF", [3, 64, S], F32).ap()
        self.LG = nc.dram_tensor("LG", [2, 3, 64, S], F32).ap()
        self.LZ = nc.dram_tensor("LZ", [2, 64, S], F32).ap()
        self.LAB = nc.dram_tensor("LAB", [2, 2, S], F32).ap()
        self.LH = nc.dram_tensor("LH", [512, CH], F32).ap()
        self.Yloc = nc.dram_tensor("Yloc", [4, 512, 1024], F32).ap()
        self.uT_dst = self.Usend
        self.yT_src = self.Yfull
        self.bU, self.bUr, self.bYs, self.bYr, self.bY = Buf("U"), Buf("Ur"), Buf("Ys"), Buf("Yr"), Buf("Yf")
        self.bL, self.bLq, self.bYl = Buf("L"), Buf("Lq"), Buf("Yl")
        outb = []

        def run_phase(fn):
            self.aoff = 0
            r = fn()
            sc.barrier()
            self.phase_i += 1
            return r

        def tok_phase(stages, load_x=False, store_x=False, ag=True):
            def fn():
                self.load_x, self.store_x = load_x, store_x
                r = TokProg(stages, fused=self).build()
                if ag:
                    for ci in range(NUC):
                        sc.cc(lambda e, ci=ci: e.collective_compute(
                            "AllGather", ALU.bypass, replica_groups=RG,
                            ins=[Usend_t.ap()[ci * 128:(ci + 1) * 128, :]], outs=[Urecv_t.ap()[ci * 512:(ci + 1) * 512, :]]),
                            writes=[self.bU, self.bUr], key="agU")
                return r
            return run_phase(fn)

        def y_exchange():
            for ci in range(8):
                sc.cc(lambda e, ci=ci: e.collective_compute(
                    "AllGather", ALU.bypass, replica_groups=RG,
                    ins=[Ysend_t.ap()[ci * 256:(ci + 1) * 256, :]], outs=[Yrecv_t.ap()[ci * 1024:(ci + 1) * 1024, :]]),
                    writes=[self.bYs, self.bYr], key="agY")
            LB = ([0, 3, 4, 7], [1, 2, 5, 6])
            for c2 in range(2):
                sc.dma("scalar", lambda e, c2=c2: e.dma_start(
                    out=self.Yloc[:, c2 * 256:(c2 + 1) * 256, :],
                    in_=self.Yrecv[c2 * 1024:c2 * 1024 + 7168, :][bass.ds(self.dyn(e, "scalar", "c2048"), 1024), :].rearrange(
                        "(r f) t -> r f t", r=4)),
                    reads=[self.bYr], writes=[self.bYl], key="yloc")
            for h in range(6):
                for half in range(2):
                    rs, su = moba_owner(h, half)
                    for q4 in range(4):
                        lb = LB[half][q4]
                        sc.dma("sync", lambda e, h=h, lb=lb, rs=rs, su=su, q4=q4: e.dma_start(
                            out=self.Yfull[256 + h * 64:256 + (h + 1) * 64, lb * 256:(lb + 1) * 256],
                            in_=self.Yloc[rs, su * 64:(su + 1) * 64, q4 * 256:(q4 + 1) * 256]),
                            reads=[self.bYl, self.bY], key="yasm")
            for h in range(6):
                rs, g = (h, 0) if h < 4 else (h - 4, 1)
                for kk in range(2):
                    r0 = 192 + (g * 2 + kk) * 64
                    sc.dma("sync", lambda e, h=h, kk=kk, rs=rs, r0=r0: e.dma_start(
                        out=self.Yfull[640 + h * 64:640 + (h + 1) * 64, kk * 1024:(kk + 1) * 1024],
                        in_=self.Yloc[rs, r0:r0 + 64, :]),
                        reads=[self.bYl, self.bY], key="yasm")

        def localize():
            Ur = self.Urecv
            rk = lambda ap: ap.rearrange("d (r t) -> d r t", r=4)
            LQF = self.LQF

            def blk(e, q, dkey, B, n=64):
                R0 = (B // 128) * 512 + B % 128
                R1 = min(R0 + 2048, NUC * 512 + 128)
                return Ur[R0:R1, :][bass.ds(self.dyn(e, q, dkey), 512), :].rearrange("(r f) t -> f r t", r=4)[0:n]

            for u in range(3):
                q = "sync" if u < 2 else "scalar"
                dk = "D0" if u < 2 else "D2"
                for (dst, B) in ((self.LK, 896), (self.LV, 1280), (LQF, 512)):
                    sc.dma(q, lambda e, u=u, q=q, dk=dk, dst=dst, B=B: e.dma_start(out=rk(dst[u]), in_=blk(e, q, dk, B)),
                           reads=[self.bUr], writes=[self.bL], key=f"loc{q}{u}")
                for ab in range(2):
                    dstq = self.LQ[u].rearrange("d (G ab i) -> d G ab i", G=8, ab=2)[:, :, ab:ab + 1, :]
                    srcv = LQF[u].rearrange("d (G b i) -> d G b i", G=8, b=4)
                    if u < 2:
                        b = u if ab == 0 else 3 - u
                        sc.dma(q, lambda e, dstq=dstq, srcv=srcv, b=b: e.dma_start(out=dstq, in_=srcv[:, :, b:b + 1, :]),
                               reads=[self.bL], writes=[self.bLq], key=f"locq{u}")
                    else:
                        kn = "mha2" if ab == 0 else "mhb2"
                        sc.dma(q, lambda e, dstq=dstq, srcv=srcv, kn=kn: e.dma_start(
                            out=dstq, in_=srcv[:, :, bass.ds(self.dyn(e, "scalar", kn), 1), :]),
                            reads=[self.bL], writes=[self.bLq], key=f"locq{u}")
            for u in range(2):
                dk, gk = ("D0", "cc") if u == 0 else ("Dg1", "gh1")
                for j in range(3):
                    sc.dma("gpsimd", lambda e, u=u, j=j, dk=dk: e.dma_start(
                        out=rk(self.LG[u, j]), in_=blk(e, "gpsimd", dk, 1664 + j * 384)),
                        reads=[self.bUr], writes=[self.bL], key="locg")
                q2 = "gpsimd" if u == 0 else "sync"
                sc.dma(q2, lambda e, u=u, dk=dk, q2=q2: e.dma_start(out=rk(self.LZ[u]), in_=blk(e, q2, dk, 2816)),
                       reads=[self.bUr], writes=[self.bL], key=f"locz{u}")
                for ab, B in ((0, 3200), (1, 3206)):
                    sc.dma(q2, lambda e, u=u, ab=ab, B=B, gk=gk, q2=q2: e.dma_start(
                        out=self.LAB[u, ab:ab + 1].rearrange("o (r t) -> o r t", r=4), in_=blk(e, q2, gk, B, 1)),
                        reads=[self.bUr], writes=[self.bL], key=f"locz{u}")
            sc.dma("scalar", lambda e: e.dma_start(
                out=self.LH.rearrange("(c f) t -> c f t", c=4),
                in_=Ur[0:2048, TOK - CH:TOK].rearrange("(c r f) t -> r c f t", r=4, f=128)[bass.ds(self.dyn(e, "scalar", "prev"), 1)]),
                reads=[self.bUr], writes=[self.bL], key="loch")

        def mixer(l):
            pfx = f"L{l}_"
            run_phase(localize)
            run_phase(lambda: conv_emit(self, sc, pfx, fz=self))
            run_phase(lambda: moba_emit(self, sc, 3, pfx, fz=self))

            def g():
                gdn_emit(self, sc, 2, pfx, fz=self)
                y_exchange()
            run_phase(g)

        tok_phase([("ffn1", 0), ("uproj", 0)], load_x=True)
        mixer(0)
        tok_phase([("wout", 0), ("ffn2", 0), ("ffn1", 1), ("uproj", 1)])
        mixer(1)
        outb = tok_phase([("wout", 1), ("ffn2", 1), ("final",)], store_x=True, ag=False)
        with nc.Block() as block:
            sc.emit(block)
        es.close()
        return nc


_FUSED = {}


def kernel(**inp):
    if "p" not in _FUSED:
        F = Fused()
        _FUSED["p"] = (F, F.build())
    F, nc = _FUSED["p"]
    x = np.asarray(inp["x"], np.float32)
    tab = rope_tables()
    shared = moba_shared_inputs(tab)
    mconst = [moba_const_inputs(0), moba_const_inputs(1)]
    gconst = gdn_const_inputs()
    wnames = ("w_ada", "ffn1_w_gate", "ffn1_w_up", "ffn1_w_down", "w_in", "w_out", "ffn2_w_gate", "ffn2_w_up", "ffn2_w_down")
    lnames = ("b_ada", "ln_ffn1_g", "ln_mix_g", "ln_ffn2_g")
    common = {}
    for l in range(2):
        for n in wnames:
            common[f"{n}{l}"] = np.ascontiguousarray(np.asarray(inp[n][l], np.float32))
        for n in lnames:
            common[f"{n}{l}"] = _lay(inp[n][l])
        cw = np.asarray(inp["conv_w"][l], np.float32)
        lay2 = lambda v: np.ascontiguousarray(np.asarray(v, np.float32).reshape(2, 128).T)
        common[f"L{l}_cw"] = np.ascontiguousarray(cw.T.reshape(2, 128, 31).transpose(1, 0, 2))
        common[f"L{l}_cp"] = np.ascontiguousarray(np.stack([lay2(inp["conv_b"][l]), lay2(inp["conv_ln_g"][l]),
                                                            lay2(inp["conv_ln_b"][l])], axis=-1))
    common["final_g"] = _lay(inp["final_g"])
    common["cident"] = np.eye(128, dtype=np.float32)
    common.update(shared)
    common.update(gconst)
    maps = []
    for c in range(NCORES):
        b, cc = c // 4, c % 4
        m = dict(common)
        m["xT"] = np.ascontiguousarray(x[b, cc * TOK:(cc + 1) * TOK].T)
        m["cT"] = _lay(inp["c"][b])
        m["cflag"] = np.full((128, 1), 0.0 if cc == 0 else 1.0, np.float32)
        units = []
        for su in range(3):
            half = moba_unit(cc, su)[1]
            qpos = np.concatenate([np.arange(bl * 256, (bl + 1) * 256) for bl in HALF_BLOCKS[half]])
            d = dict(mconst[half])
            d["ropeq"] = np.ascontiguousarray(tab[:, :, qpos])
            units.append(d)
        for k_ in units[0]:
            m[k_] = np.ascontiguousarray(np.stack([un[k_] for un in units]))
        for l in range(2):
            heads = [cc, (cc + 4) % 6]
            gw = np.asarray(inp["gdn_conv_w"][l], np.float32)
            gcw = np.zeros((2, 64, 12), np.float32)
            gpar = np.zeros((2, 64, 2), np.float32)
            gng = np.zeros((2, 64, 64), np.float32)
            for g, h in enumerate(heads):
                for j in range(3):
                    gcw[g, :, j * 4:(j + 1) * 4] = gw[:, j * 384 + h * 64:j * 384 + (h + 1) * 64].T
                gpar[g, :, 0] = np.asarray(inp["gdn_a_log"][l], np.float32)[h]
                gpar[g, :, 1] = np.asarray(inp["gdn_dt_bias"][l], np.float32)[h]
                gng[g] = np.asarray(inp["gdn_norm_g"][l], np.float32)[None, :]
            m[f"L{l}_gcw"], m[f"L{l}_gpar"], m[f"L{l}_gng"] = gcw, gpar, gng
        maps.append({k_: m[k_] for k_ in F.in_names})
    res = run_bass_kernel_spmd(nc, maps, core_ids=list(range(NCORES))).results
    out = np.zeros((B, S, D), np.float32)
    for c in range(NCORES):
        out[c // 4, (c % 4) * TOK:(c % 4 + 1) * TOK] = res[c]["xoT"].T
    return out
```

```python
import numpy as np
from contextlib import ExitStack
import concourse.bass as bass
import concourse.mybir as mybir
from concourse.bass_utils import run_bass_kernel_spmd

F32 = mybir.dt.float32
BF16 = mybir.dt.bfloat16
AF = mybir.ActivationFunctionType
ALU = mybir.AluOpType

D = 1024
KC = 8
DFF = 2816
FC = 22
DIN = 3212
B = 2
S = 8192
NCORES = 8
TOK = 2048
EPS = 1e-6

SAME_ENG_SYNC = True


class Buf:
    __slots__ = ("name", "lw", "rd")

    def __init__(self, name=""):
        self.name = name
        self.lw = None
        self.rd = {}


class Sched:
    ENGS = ("tensor", "vector", "scalar", "gpsimd", "sync")
    EPOCH = 20000

    def __init__(self, nc, es):
        self.nc = nc
        self.es = es
        self.q = {e: [] for e in self.ENGS}
        self.cnt = {e: 0 for e in self.ENGS}
        self.seen = {e: {} for e in self.ENGS}
        self.esem = {}
        self.dsem = {}
        self.dcnt = {}
        self.cckeys = set()

    def _get_esem(self, eng, epoch):
        k = (eng, epoch)
        if k not in self.esem:
            self.esem[k] = self.es.enter_context(self.nc.semaphore(f"se_{eng}_{epoch}"))
        return self.esem[k]

    def _get_dsem(self, key):
        if key not in self.dsem:
            self.dsem[key] = self.es.enter_context(self.nc.semaphore(f"sd_{key}"))
            self.dcnt[key] = 0
        return self.dsem[key]

    def _need(self, eng, tok, waits):
        if tok is None:
            return
        kind, k, val = tok
        if kind == "e":
            if k == eng and (eng == "tensor" or not SAME_ENG_SYNC):
                return
        key = (kind, k)
        if self.seen[eng].get(key, 0) >= val:
            return
        self.seen[eng][key] = val
        waits.append(tok)

    def _deps(self, eng, reads, writes):
        waits = []
        for b in reads:
            self._need(eng, b.lw, waits)
        for b in writes:
            self._need(eng, b.lw, waits)
            for k, v in b.rd.items():
                self._need(eng, (k[0], k[1], v), waits)
        return waits

    def _mark(self, tok, reads, writes):
        key = (tok[0], tok[1])
        for b in reads:
            if b.rd.get(key, 0) < tok[2]:
                b.rd[key] = tok[2]
        for b in writes:
            b.lw = tok
            b.rd = {}

    def op(self, eng, fn, reads=(), writes=(), inc=True):
        waits = self._deps(eng, reads, writes)
        idx = self.cnt[eng] + 1
        if inc:
            self.cnt[eng] = idx
        tok = ("e", eng, idx)
        self._mark(tok, reads, writes)
        self.q[eng].append((waits, fn, tok if inc else None))

    def dma(self, qeng, fn, reads=(), writes=(), key="d"):
        waits = self._deps(qeng, reads, writes)
        self._get_dsem(key)
        self.dcnt[key] += 1
        tok = ("d", key, 16 * self.dcnt[key])
        self._mark(tok, reads, writes)
        self.q[qeng].append((waits, fn, tok))

    def cc(self, fn, reads=(), writes=(), key="cc"):
        waits = self._deps("gpsimd", reads, writes)
        self._get_dsem(key)
        self.cckeys.add(key)
        self.dcnt[key] += 1
        tok = ("c", key, self.dcnt[key])
        self._mark(tok, reads, writes)
        self.q["gpsimd"].append((waits, fn, tok))

    def barrier(self, exclude=()):
        for e in self.ENGS:
            waits = []
            for e2 in self.ENGS:
                if e2 != e and self.cnt[e2] > 0:
                    self._need(e, ("e", e2, self.cnt[e2]), waits)
            for key, n in self.dcnt.items():
                if n > 0 and key not in exclude:
                    kind = "c" if key in self.cckeys else "d"
                    self._need(e, (kind, key, n if kind == "c" else 16 * n), waits)
            self.q[e].append((waits, None, None))

    def final_wait(self, eng, toks_bufs):
        waits = self._deps(eng, (), toks_bufs)
        self.q[eng].append((waits, None, None))

    def emit(self, block):
        nc = self.nc

        def run(engname):
            def body(eng):
                for waits, fn, tok in self.q[engname]:
                    for (kind, k, val) in waits:
                        if kind == "e":
                            epoch = (val - 1) // self.EPOCH
                            eng.wait_ge(self._get_esem(k, epoch), val - epoch * self.EPOCH)
                        else:
                            eng.wait_ge(self.dsem[k], val)
                    if fn is None:
                        continue
                    ins = fn(eng)
                    if tok is not None:
                        if tok[0] == "e":
                            epoch = (tok[2] - 1) // self.EPOCH
                            ins.then_inc(self._get_esem(tok[1], epoch), 1)
                        elif tok[0] == "c":
                            ins.then_inc(self.dsem[tok[1]])
                        else:
                            ins.then_inc(self.dsem[tok[1]], 16)
                self.q[engname] = []
            return body

        for e in self.ENGS:
            for ep in range((self.cnt[e] - 1) // self.EPOCH + 1 if self.cnt[e] else 0):
                self._get_esem(e, ep)
        block.tensor(run("tensor"))
        block.vector(run("vector"))
        block.scalar(run("scalar"))
        block.gpsimd(run("gpsimd"))
        block.sync(run("sync"))


class TokProg:
    def __init__(self, stages, tok=TOK, fused=None):
        self.stages = stages
        self.tok = tok
        self.fused = fused
        if fused is None:
            self.nc = bass.Bass("TRN2", target_bir_lowering=False)
            self.es = ExitStack()
        else:
            self.nc = fused.nc
            self.es = fused.es
        self.in_names = []
        self.out_names = []

    def din(self, name, shape, dt=F32):
        if self.fused is not None:
            return self.fused.din(name, shape, dt)
        self.in_names.append(name)
        return self.nc.dram_tensor(name, list(shape), dt, kind="ExternalInput").ap()

    def dout(self, name, shape, dt=F32):
        if self.fused is not None:
            return self.fused.dout(name, shape, dt)
        self.out_names.append(name)
        return self.nc.dram_tensor(name, list(shape), dt, kind="ExternalOutput").ap()

    def sb(self, name, shape, dt):
        if self.fused is not None:
            return self.fused.sb(name, shape, dt)
        return self.es.enter_context(self.nc.sbuf_tensor(name, list(shape), dt))

    def build(self):
        nc, es = self.nc, self.es
        fz = self.fused
        T = self.tok
        NH = T // 1024
        stages = self.stages
        layers = sorted({s[1] for s in stages if len(s) > 1})
        need_v = {}
        for s in stages:
            if s[0] == "ffn1":
                need_v.setdefault(s[1], set()).update([0, 1, 2])
            elif s[0] == "uproj":
                need_v.setdefault(s[1], set()).update([3, 4])
            elif s[0] == "wout":
                need_v.setdefault(s[1], set()).update([5])
            elif s[0] == "ffn2":
                need_v.setdefault(s[1], set()).update([6, 7, 8])

        xT_d = self.din("xT", [D, T]) if (fz is None or fz.load_x) else None
        cT_d = self.din("cT", [128, KC])
        W = {}
        for l in layers:
            W[("w_ada", l)] = self.din(f"w_ada{l}", [D, 9 * D])
            W[("b_ada", l)] = self.din(f"b_ada{l}", [128, 72])
        for s in stages:
            if s[0] in ("ffn1", "ffn2"):
                l = s[1]
                n = s[0]
                W[(n + "_g", l)] = self.din(f"ln_{n}_g{l}", [128, KC])
                W[(n + "_wg", l)] = self.din(f"{n}_w_gate{l}", [D, DFF])
                W[(n + "_wu", l)] = self.din(f"{n}_w_up{l}", [D, DFF])
                W[(n + "_wd", l)] = self.din(f"{n}_w_down{l}", [DFF, D])
            elif s[0] == "uproj":
                l = s[1]
                W[("mix_g", l)] = self.din(f"ln_mix_g{l}", [128, KC])
                W[("w_in", l)] = self.din(f"w_in{l}", [D, DIN])
                W[("uT", l)] = self.dout(f"uT{l}", [DIN, T]) if fz is None else fz.uT_dst
            elif s[0] == "wout":
                l = s[1]
                W[("w_out", l)] = self.din(f"w_out{l}", [D, D])
                W[("yT", l)] = self.din(f"yT{l}", [D, T]) if fz is None else fz.yT_src
            elif s[0] == "final":
                W[("final_g",)] = self.din("final_g", [128, KC])
        xo_d = self.dout("xoT", [D, T]) if (fz is None or fz.store_x) else None

        x = self.sb("x", [128, KC, T], F32) if fz is None else fz.x
        h = self.sb("h", [128, KC, 1024], BF16)
        act = self.sb("act", [128, FC, 1024], BF16)
        wd = self.sb("wd", [128, FC, D], BF16)
        NSLOT = 4
        SLOTW = 256
        wslot = [self.sb(f"ws{i}", [128, KC, SLOTW], BF16) for i in range(NSLOT)]
        tmpA = [self.sb(f"tmpA{i}", [128, 512], F32) for i in range(2)]
        tmpB = [self.sb(f"tmpB{i}", [128, 512], F32) for i in range(2)]
        sqb = [self.sb(f"sq{i}", [128, 512], BF16) for i in range(2)]
        rstd = self.sb("rstd", [128, 512], F32)
        ones = self.sb("ones", [128, 128], BF16)
        cT = self.sb("cT_sb", [128, KC], F32)
        cact = self.sb("cact", [128, KC], BF16)
        bada = {l: self.sb(f"bada{l}", [128, 72], F32) for l in layers}
        mod = {l: self.sb(f"mod{l}", [128, 72], F32) for l in layers}
        gains = {}
        for k in W:
            if k[0] in ("ffn1_g", "ffn2_g", "mix_g", "final_g"):
                gains[k] = self.sb("g_" + "_".join(map(str, k)), [128, KC], F32)
        coefA = {}
        coefG = {}
        ps = es.enter_context(nc.psum_tensor("ps", [128, 8, 512], F32)) if fz is None else fz.ps

        sc = Sched(nc, es) if fz is None else fz.sc
        bx = [[Buf(f"x{c}_{t}") for t in range(T // 512)] for c in range(KC)] if fz is None else fz.bx
        bU = [] if fz is None else [fz.bU]
        bY = [] if fz is None else [fz.bY]
        bh = [Buf(f"h{t}") for t in range(2)]
        bact = [[Buf(f"act{f}_{t}") for t in range(2)] for f in range(FC)]
        WD_PIECES = ((0, 6), (6, 12), (12, 17), (17, 22))
        bwd = [Buf(f"wd{i}") for i in range(4)]
        wd_piece = {}
        for i, (f0, f1) in enumerate(WD_PIECES):
            for f in range(f0, f1):
                wd_piece[f] = i
        bws = [Buf(f"ws{i}") for i in range(NSLOT)]
        btA = [Buf() for _ in range(2)]
        btB = [Buf() for _ in range(2)]
        bsq = [Buf() for _ in range(2)]
        brstd = Buf()
        bones = Buf()
        bps = [Buf(f"ps{i}") for i in range(8)]
        bmisc = Buf("misc")
        bmod = Buf("mod")

        if xT_d is not None:
            xT_v = xT_d.rearrange("(c p) t -> p c t", p=128)
            for c in range(KC):
                sc.dma("sync", lambda e, c=c: e.dma_start(out=x[:, c, :], in_=xT_v[:, c, :]),
                       writes=bx[c], key=f"x{c}")
        sc.dma("sync", lambda e: e.dma_start(out=cT[:], in_=cT_d[:, :]), writes=[bmisc], key="misc")
        for l in layers:
            sc.dma("sync", lambda e, l=l: e.dma_start(out=bada[l][:], in_=W[("b_ada", l)][:, :]),
                   writes=[bmisc], key="misc")
        for k, t in gains.items():
            sc.dma("sync", lambda e, k=k, t=t: e.dma_start(out=t[:], in_=W[k][:, :]), writes=[bmisc], key="misc")
        sc.op("vector", lambda e: e.memset(ones[:], 1.0), writes=[bones])
        sc.op("scalar", lambda e: e.activation(out=cact[:], in_=cT[:], func=AF.Silu), reads=[bmisc], writes=[bmod])

        wslot_i = [0]

        def next_slot():
            i = wslot_i[0] % NSLOT
            wslot_i[0] += 1
            return i

        def load_cols(Wd, c0, ncols, nk=KC):
            i = next_slot()
            src = Wd.rearrange("(k p) n -> p k n", p=128)
            sc.dma("gpsimd", lambda e, i=i: e.dma_start(out=wslot[i][:, 0:nk, 0:ncols], in_=src[:, :, c0:c0 + ncols]),
                   writes=[bws[i]], key=f"ws{i}")
            return i

        mod_ps = ps[:, 7, 0:72]
        for l in layers:
            for v in sorted(need_v[l]):
                for hh in range(4):
                    si = load_cols(W[("w_ada", l)], v * 1024 + hh * 256, 256)
                    for jj in range(2):
                        j = hh * 2 + jj
                        col = v * 8 + j
                        for kc in range(KC):
                            sc.op("tensor",
                                  lambda e, si=si, jj=jj, kc=kc, col=col: e.matmul(
                                      ps[:, 7, col:col + 1], lhsT=wslot[si][:, kc, jj * 128:(jj + 1) * 128],
                                      rhs=cact[:, kc:kc + 1], start=(kc == 0), stop=(kc == KC - 1)),
                                  reads=[bws[si], bmod], writes=[bps[7]], inc=(kc == KC - 1))
            sc.op("vector", lambda e, l=l: e.tensor_tensor(out=mod[l][:], in0=mod_ps, in1=bada[l][:], op=ALU.add),
                  reads=[bps[7], bmisc], writes=[bmod])
            for (gk, vs, vg, half) in ((("ffn1_g", l), 1, 2, 0.5), (("mix_g", l), 4, None, None),
                                       (("ffn2_g", l), 7, 8, 0.5)):
                if gk in gains:
                    a = self.sb("cA_" + "_".join(map(str, gk)), [128, KC], F32)
                    coefA[gk] = a
                    sc.op("vector", lambda e, a=a, gk=gk, vs=vs, l=l: e.scalar_tensor_tensor(
                        out=a[:], in0=mod[l][:, vs * 8:vs * 8 + 8], scalar=1.0, in1=gains[gk][:],
                        op0=ALU.add, op1=ALU.mult), reads=[bmod, bmisc], writes=[bmod])
                    if vg is not None:
                        g = self.sb("cG_" + "_".join(map(str, gk)), [128, KC], F32)
                        coefG[gk] = g
                        sc.op("vector", lambda e, g=g, vg=vg, l=l: e.tensor_scalar(
                            out=g[:], in0=mod[l][:, vg * 8:vg * 8 + 8], scalar1=0.5, scalar2=None, op0=ALU.mult),
                            reads=[bmod], writes=[bmod])

        psi = [0]

        def next_ps(pool):
            i = pool[psi[0] % len(pool)]
            psi[0] += 1
            return i

        def norm_mod(half, A_ap, sh_ap):
            for tt in range(2):
                t0 = half * 1024 + tt * 512
                ti = t0 // 512
                pb = 6
                for c in range(KC):
                    s = c % 2
                    sc.op("scalar", lambda e, c=c, s=s, t0=t0: e.activation(out=sqb[s][:], in_=x[:, c, t0:t0 + 512],
                                                                            func=AF.Square),
                          reads=[bx[c][ti]], writes=[bsq[s]])
                    sc.op("tensor", lambda e, c=c, s=s: e.matmul(ps[:, pb, :], lhsT=ones[:], rhs=sqb[s][:],
                                                                 start=(c == 0), stop=(c == KC - 1)),
                          reads=[bones, bsq[s]], writes=[bps[pb]])
                sc.op("scalar", lambda e: e.activation(out=tmpA[0][:], in_=ps[:, pb, :], func=AF.Sqrt,
                                                       bias=eps_t[:, 0:1], scale=1.0 / D),
                      reads=[bps[pb], bmisc], writes=[btA[0]])
                sc.op("vector", lambda e: e.reciprocal(out=rstd[:], in_=tmpA[0][:]), reads=[btA[0]], writes=[brstd])
                for c in range(KC):
                    s = c % 2
                    sc.op("vector", lambda e, c=c, s=s, t0=t0: e.scalar_tensor_tensor(
                        out=tmpB[s][:], in0=x[:, c, t0:t0 + 512], scalar=A_ap[:, c:c + 1], in1=rstd[:],
                        op0=ALU.mult, op1=ALU.mult), reads=[bx[c][ti], brstd, bmod], writes=[btB[s]])
                    if sh_ap is not None:
                        sc.op("scalar", lambda e, c=c, s=s, tt=tt: e.activation(
                            out=h[:, c, tt * 512:(tt + 1) * 512], in_=tmpB[s][:], func=AF.Identity,
                            bias=sh_ap[:, c:c + 1], scale=1.0), reads=[btB[s], bmod], writes=[bh[tt]])

        def ffn(half, n, l):
            A = coefA[(n + "_g", l)]
            G = coefG[(n + "_g", l)]
            vsh = 0 if n == "ffn1" else 6
            sh = mod[l][:, vsh * 8:vsh * 8 + 8]
            norm_mod(half, A, sh)
            wdv = W[(n + "_wd", l)].rearrange("(f p) n -> p f n", p=128)
            for i, (f0, f1) in enumerate(WD_PIECES):
                sc.dma("gpsimd", lambda e, f0=f0, f1=f1: e.dma_start(out=wd[:, f0:f1, :], in_=wdv[:, f0:f1, :]),
                       writes=[bwd[i]], key=f"wd{i}")
            groups = [(g * 2, 2) for g in range(11)]
            loaded = {}

            def load_group(gi):
                f0, nf = groups[gi]
                loaded[gi] = (load_cols(W[(n + "_wg", l)], f0 * 128, nf * 128),
                              load_cols(W[(n + "_wu", l)], f0 * 128, nf * 128))
            load_group(0)
            for gi, (f0, nf) in enumerate(groups):
                if gi + 1 < len(groups):
                    load_group(gi + 1)
                sg, su = loaded[gi]
                for fi in range(nf):
                    f = f0 + fi
                    for tt in range(2):
                        pg = next_ps([0, 1])
                        pu = pg + 2
                        for kc in range(KC):
                            sc.op("tensor", lambda e, sg=sg, fi=fi, kc=kc, tt=tt, pg=pg: e.matmul(
                                ps[:, pg, :], lhsT=wslot[sg][:, kc, fi * 128:(fi + 1) * 128],
                                rhs=h[:, kc, tt * 512:(tt + 1) * 512], start=(kc == 0), stop=(kc == KC - 1)),
                                reads=[bws[sg], bh[tt]], writes=[bps[pg]], inc=(kc == KC - 1))
                        for kc in range(KC):
                            sc.op("tensor", lambda e, su=su, fi=fi, kc=kc, tt=tt, pu=pu: e.matmul(
                                ps[:, pu, :], lhsT=wslot[su][:, kc, fi * 128:(fi + 1) * 128],
                                rhs=h[:, kc, tt * 512:(tt + 1) * 512], start=(kc == 0), stop=(kc == KC - 1)),
                                reads=[bws[su], bh[tt]], writes=[bps[pu]], inc=(kc == KC - 1))
                        s = pg
                        sc.op("scalar", lambda e, s=s, pg=pg: e.activation(out=tmpA[s][:], in_=ps[:, pg, :],
                                                                           func=AF.Silu),
                              reads=[bps[pg]], writes=[btA[s]])
                        sc.op("vector", lambda e, s=s, pu=pu, f=f, tt=tt: e.tensor_tensor(
                            out=act[:, f, tt * 512:(tt + 1) * 512], in0=tmpA[s][:], in1=ps[:, pu, :], op=ALU.mult),
                            reads=[btA[s], bps[pu]], writes=[bact[f][tt]])
            for tt in range(2):
                t0 = half * 1024 + tt * 512
                ti = t0 // 512
                for d in range(KC):
                    pd = next_ps([4, 5])
                    for f in range(FC):
                        sc.op("tensor", lambda e, f=f, d=d, tt=tt, pd=pd: e.matmul(
                            ps[:, pd, :], lhsT=wd[:, f, d * 128:(d + 1) * 128], rhs=act[:, f, tt * 512:(tt + 1) * 512],
                            start=(f == 0), stop=(f == FC - 1)),
                            reads=[bwd[wd_piece[f]], bact[f][tt]], writes=[bps[pd]], inc=(f == FC - 1))
                    sc.op("vector", lambda e, d=d, t0=t0, pd=pd: e.scalar_tensor_tensor(
                        out=x[:, d, t0:t0 + 512], in0=ps[:, pd, :], scalar=G[:, d:d + 1], in1=x[:, d, t0:t0 + 512],
                        op0=ALU.mult, op1=ALU.add), reads=[bps[pd], bx[d][ti], bmod], writes=[bx[d][ti]])

        ostage = [self.sb(f"ost{i}", [128, 512], F32) for i in range(2)]
        bost = [Buf() for _ in range(2)]
        osi = [0]

        def uproj(half, l):
            A = coefA[("mix_g", l)]
            sh = mod[l][:, 3 * 8:3 * 8 + 8]
            norm_mod(half, A, sh)
            uT = W[("uT", l)]
            ngr = (DIN + 255) // 256
            loaded = {}

            def load_group(gi):
                c0 = gi * 256
                loaded[gi] = load_cols(W[("w_in", l)], c0, min(256, DIN - c0))
            load_group(0)
            for gi in range(ngr):
                if gi + 1 < ngr:
                    load_group(gi + 1)
                si = loaded[gi]
                c0 = gi * 256
                ncol = min(256, DIN - c0)
                for fi in range((ncol + 127) // 128):
                    m = min(128, ncol - fi * 128)
                    for tt in range(2):
                        t0 = half * 1024 + tt * 512
                        pg = next_ps([0, 1, 2, 3])
                        for kc in range(KC):
                            sc.op("tensor", lambda e, si=si, fi=fi, kc=kc, tt=tt, pg=pg, m=m: e.matmul(
                                ps[0:m, pg, :], lhsT=wslot[si][:, kc, fi * 128:fi * 128 + m],
                                rhs=h[:, kc, tt * 512:(tt + 1) * 512], start=(kc == 0), stop=(kc == KC - 1)),
                                reads=[bws[si], bh[tt]], writes=[bps[pg]], inc=(kc == KC - 1))
                        o = osi[0] % 2
                        osi[0] += 1
                        eng = "scalar" if o == 0 else "vector"
                        if eng == "scalar":
                            sc.op("scalar", lambda e, o=o, pg=pg, m=m: e.copy(out=ostage[o][0:m, :], in_=ps[0:m, pg, :]),
                                  reads=[bps[pg]], writes=[bost[o]])
                        else:
                            sc.op("vector", lambda e, o=o, pg=pg, m=m: e.tensor_copy(out=ostage[o][0:m, :],
                                                                                     in_=ps[0:m, pg, :]),
                                  reads=[bps[pg]], writes=[bost[o]])
                        r0 = c0 + fi * 128
                        sc.dma("sync", lambda e, o=o, m=m, r0=r0, t0=t0: e.dma_start(
                            out=uT[r0:r0 + m, t0:t0 + 512], in_=ostage[o][0:m, :]), reads=[bost[o]] + bU, key=f"ost{o}")

        ystage = [act[:, i * 8:(i + 1) * 8, 0:512] for i in range(2)]
        byst = [[bact[f][0] for f in range(i * 8, (i + 1) * 8)] for i in range(2)]

        def wout(half, l):
            yT = W[("yT", l)].rearrange("(c p) t -> p c t", p=128)
            wsl = [load_cols(W[("w_out", l)], q * 256, 256) for q in range(4)]
            G = mod[l][:, 5 * 8:5 * 8 + 8]
            for tt in range(2):
                t0 = half * 1024 + tt * 512
                ti = t0 // 512
                sc.dma("gpsimd", lambda e, tt=tt, t0=t0: e.dma_start(out=ystage[tt], in_=yT[:, :, t0:t0 + 512]),
                       reads=bY, writes=byst[tt], key=f"yst{tt}")
                for d in range(KC):
                    si = wsl[d // 2]
                    dj = d % 2
                    pd = next_ps([4, 5])
                    for kc in range(KC):
                        sc.op("tensor", lambda e, si=si, dj=dj, kc=kc, tt=tt, pd=pd: e.matmul(
                            ps[:, pd, :], lhsT=wslot[si][:, kc, dj * 128:(dj + 1) * 128], rhs=ystage[tt][:, kc, :],
                            start=(kc == 0), stop=(kc == KC - 1)),
                            reads=[bws[si]] + byst[tt], writes=[bps[pd]], inc=(kc == KC - 1))
                    sc.op("vector", lambda e, d=d, t0=t0, pd=pd: e.scalar_tensor_tensor(
                        out=x[:, d, t0:t0 + 512], in0=ps[:, pd, :], scalar=G[:, d:d + 1], in1=x[:, d, t0:t0 + 512],
                        op0=ALU.mult, op1=ALU.add), reads=[bps[pd], bx[d][ti], bmod], writes=[bx[d][ti]])

        def final(half):
            g = gains[("final_g",)]
            for tt in range(2):
                t0 = half * 1024 + tt * 512
                ti = t0 // 512
                pb = 6
                for c in range(KC):
                    s = c % 2
                    sc.op("scalar", lambda e, c=c, s=s, t0=t0: e.activation(out=sqb[s][:], in_=x[:, c, t0:t0 + 512],
                                                                            func=AF.Square),
                          reads=[bx[c][ti]], writes=[bsq[s]])
                    sc.op("tensor", lambda e, c=c, s=s: e.matmul(ps[:, pb, :], lhsT=ones[:], rhs=sqb[s][:],
                                                                 start=(c == 0), stop=(c == KC - 1)),
                          reads=[bones, bsq[s]], writes=[bps[pb]])
                sc.op("scalar", lambda e: e.activation(out=tmpA[0][:], in_=ps[:, pb, :], func=AF.Sqrt,
                                                       bias=eps_t[:, 0:1], scale=1.0 / D),
                      reads=[bps[pb], bmisc], writes=[btA[0]])
                sc.op("vector", lambda e: e.reciprocal(out=rstd[:], in_=tmpA[0][:]), reads=[btA[0]], writes=[brstd])
                for c in range(KC):
                    sc.op("vector", lambda e, c=c, t0=t0: e.scalar_tensor_tensor(
                        out=x[:, c, t0:t0 + 512], in0=x[:, c, t0:t0 + 512], scalar=g[:, c:c + 1], in1=rstd[:],
                        op0=ALU.mult, op1=ALU.mult), reads=[bx[c][ti], brstd, bmisc], writes=[bx[c][ti]])

        eps_t = self.sb("eps_t", [128, 1], F32)
        sc.op("vector", lambda e: e.memset(eps_t[:], EPS), writes=[bmisc])

        for half in range(NH):
            for s in stages:
                if s[0] in ("ffn1", "ffn2"):
                    ffn(half, s[0], s[1])
                elif s[0] == "uproj":
                    uproj(half, s[1])
                elif s[0] == "wout":
                    wout(half, s[1])
                elif s[0] == "final":
                    final(half)

        allb = []
        if xo_d is not None:
            xo_v = xo_d.rearrange("(c p) t -> p c t", p=128)
            for c in range(KC):
                sc.dma("sync", lambda e, c=c: e.dma_start(out=xo_v[:, c, :], in_=x[:, c, :]), reads=bx[c], key="xo")
                allb += bx[c]
        if fz is not None:
            return allb
        sc.final_wait("sync", allb + bost)

        with nc.Block() as block:
            sc.emit(block)
        es.close()
        return nc


NBLK = 32
MOBA_SLOTS = 16
HALF_BLOCKS = ([b for b in range(NBLK) if b % 4 in (0, 3)], [b for b in range(NBLK) if b % 4 in (1, 2)])
NEG = -30000.0


def moba_emit(P, sc, nu, pfx="", fz=None):
    nc, es = P.nc, P.es
    NQ = MOBA_SLOTS * 256
    if fz is None:
        mq = P.din(pfx + "mq", [nu, 64, NQ])
        mqs = P.din(pfx + "mqs", [nu, 16, NQ])
        mk = P.din(pfx + "mk", [nu, 64, S])
        mks = P.din(pfx + "mks", [nu, 16, S])
        mv = P.din(pfx + "mv", [nu, S, 64])
        yo = P.dout(pfx + "moT", [nu, 64, NQ])
    cq = P.din("ropeq", [nu, 2, 16, NQ])
    ck = P.din("ropek", [2, 16, S])
    pm_d = P.din("pm", [nu, 128, MOBA_SLOTS * NBLK])
    oh_d = P.din("oh", [nu, 128, MOBA_SLOTS * NBLK])
    cm_d = P.din("cm", [nu, 2, 4, 128, 256])
    boh_d = P.din("boh", [32, S])
    id_d = P.din("ident", [128, 128])

    qaug = P.sb(pfx + "qaug", [128, NQ], BF16)
    kaug = P.sb(pfx + "kaug", [128, S], BF16)
    vaug = P.sb(pfx + "vaug", [128, 64, 128], BF16)
    qf = P.sb(pfx + "qf", [64, NQ], F32)
    xt = [P.sb(pfx + f"xt{i}", [64, 1024], F32) for i in range(2)]
    xs = [P.sb(pfx + f"xs{i}", [16, 1024], F32) for i in range(2)]
    ct = [P.sb(pfx + f"ct{i}", [16, 2, 1024], F32) for i in range(2)]
    t16 = P.sb(pfx + "t16", [16, 1024], F32)
    sqf = P.sb(pfx + "sqf", [64, 1024], F32)
    kmean = P.sb(pfx + "kmean", [64, NBLK], F32)
    mx = P.sb(pfx + "mx", [128, 4], F32)
    nbias = P.sb(pfx + "nbias", [128, 1], F32)
    onesf = P.sb(pfx + "onesf", [64, 128], F32)
    ident = P.sb(pfx + "ident_sb", [128, 128], F32)
    pm = P.sb(pfx + "pm_sb", [128, MOBA_SLOTS * NBLK], F32)
    oh = P.sb(pfx + "oh_sb", [128, MOBA_SLOTS * NBLK], F32)
    cm = P.sb(pfx + "cm_sb", [128, 8, 256], F32)
    gs = P.sb(pfx + "gs", [128, NBLK], F32)
    g8 = P.sb(pfx + "g8", [128, 8], F32)
    m1 = P.sb(pfx + "m1", [128, NBLK], F32)
    m2 = P.sb(pfx + "m2", [128, NBLK], F32)
    stm = [P.sb(pfx + f"stm{i}", [128, 256], F32) for i in range(2)]
    pt = [P.sb(pfx + f"pt{i}", [128, 256], BF16) for i in range(4)]
    rec = P.sb(pfx + "rec", [64, 256], F32)
    yst = [P.sb(pfx + f"yst{i}", [64, 256], F32) for i in range(2)]
    ps = es.enter_context(nc.psum_tensor(pfx + "mps", [128, 8, 512], F32)) if fz is None else fz.ps
    if fz is not None:
        vt = P.sb(pfx + "vt", [64, 1024], F32)
        bvt = Buf()

    def rows6(e, u, r0, nr, cols):
        return fz.Urecv[r0:r0 + 5 * 64 + nr, cols][bass.ds(fz.dyn(e, "sync", ("mhr", u)), nr), :]

    bq, bk, bv, bqf = Buf(), Buf(), Buf(), Buf()
    bxt = [Buf(), Buf()]
    bxs = [Buf(), Buf()]
    bct = [Buf(), Buf()]
    bt16, bsqf, bkm, bmx, bnb, bconst, bmask = Buf(), Buf(), Buf(), Buf(), Buf(), Buf(), Buf()
    bgs, bg8, bm1, bm2 = Buf(), Buf(), Buf(), Buf()
    bstm = [Buf(), Buf()]
    bpt = [Buf(), Buf(), Buf(), Buf()]
    brec = Buf()
    byst = [Buf(), Buf()]
    bps = [Buf() for _ in range(8)]

    sc.dma("sync", lambda e: e.dma_start(out=ident[:], in_=id_d[:, :]), writes=[bconst], key=pfx + "mconst")
    sc.op("vector", lambda e: e.memset(onesf[:], 1.0), writes=[bconst])
    sc.op("vector", lambda e: e.memset(kaug[32:64, :], 0.0), writes=[bk])
    sc.op("vector", lambda e: e.memset(kaug[32:33, :], 1.0), writes=[bk])
    sc.dma("gpsimd", lambda e: e.dma_start(out=kaug[0:32, :], in_=boh_d[:, :]), writes=[bk], key=pfx + "mk0")
    sc.op("vector", lambda e: e.memset(qaug[32:64, :], 0.0), writes=[bq])
    sc.op("vector", lambda e: e.memset(vaug[:, :, 64:128], 1.0), writes=[bv])

    cnt = [0]
    for u in range(nu):
        sc.dma("sync", lambda e, u=u: e.dma_start(out=pm[:], in_=pm_d[u]), writes=[bmask], key=pfx + "mmask")
        sc.dma("sync", lambda e, u=u: e.dma_start(out=oh[:], in_=oh_d[u]), writes=[bmask], key=pfx + "mmask")
        sc.dma("sync", lambda e, u=u: e.dma_start(out=cm[:], in_=cm_d[u].rearrange("a k p q -> p (a k) q")),
               writes=[bmask], key=pfx + "mmask")
        if fz is None:
            for k0 in range(0, 64, 16):
                sc.dma("gpsimd", lambda e, u=u, k0=k0: e.dma_start(
                    out=vaug[:, k0:k0 + 16, 0:64], in_=mv[u].rearrange("(k p) d -> p k d", p=128)[:, k0:k0 + 16, :]),
                    writes=[bv], key=pfx + "mv")
        else:
            for c0 in range(0, S, 1024):
                rr, t0 = c0 // TOK, c0 % TOK

                def vsrc(e, u=u, c0=c0):
                    return fz.LV[u, :, c0:c0 + 1024]
                sc.dma("sync", lambda e, vsrc=vsrc: e.dma_start(out=vt[:], in_=vsrc(e)), reads=[fz.bUr], writes=[bvt],
                       key=pfx + "mvt")
                for cj in range(8):
                    sc.op("tensor", lambda e, cj=cj: e.transpose(ps[:, 6, cj * 64:(cj + 1) * 64], vt[:, cj * 128:(cj + 1) * 128],
                                                                 ident[0:64, 0:64]),
                          reads=[bvt, bconst], writes=[bps[6]], inc=(cj == 7))
                k0 = c0 // 128
                sc.op("vector", lambda e, k0=k0: e.tensor_copy(out=vaug[:, k0:k0 + 8, 0:64],
                                                               in_=ps[:, 6, :].rearrange("p (a b) -> p a b", b=64)),
                      reads=[bps[6]], writes=[bv])
        sc.op("vector", lambda e: e.memset(mx[:], 0.0), writes=[bmx])
        srcs_ = ((mk, mks, None, S), (mq, mqs, cq, NQ)) if fz is None else ((None, None, None, S), (None, None, cq, NQ))
        for which, (src, srcs, tab, ncols) in enumerate(srcs_):
            for c0 in range(0, ncols, 1024):
                i = cnt[0] % 2
                cnt[0] += 1
                if fz is None:
                    sc.dma("sync", lambda e, i=i, c0=c0, src=src, u=u: e.dma_start(out=xt[i][:], in_=src[u, :, c0:c0 + 1024]),
                           writes=[bxt[i]], key=pfx + f"mxt{i}")
                    sc.dma("sync", lambda e, i=i, c0=c0, srcs=srcs, u=u: e.dma_start(out=xs[i][:], in_=srcs[u, :, c0:c0 + 1024]),
                           writes=[bxs[i]], key=pfx + f"mxs{i}")
                elif which == 0:
                    rr, t0 = c0 // TOK, c0 % TOK

                    def ksrc(e, ro, nr, u=u, c0=c0):
                        return fz.LK[u, ro:ro + nr, c0:c0 + 1024]
                    sc.dma("sync", lambda e, i=i, ksrc=ksrc: e.dma_start(out=xt[i][:], in_=ksrc(e, 0, 64)),
                           reads=[fz.bUr], writes=[bxt[i]], key=pfx + f"mxt{i}")
                    sc.dma("sync", lambda e, i=i, ksrc=ksrc: e.dma_start(out=xs[i][0:8, :], in_=ksrc(e, 8, 8)),
                           reads=[fz.bUr], writes=[bxs[i]], key=pfx + f"mxs{i}")
                    sc.dma("sync", lambda e, i=i, ksrc=ksrc: e.dma_start(out=xs[i][8:16, :], in_=ksrc(e, 0, 8)),
                           reads=[fz.bUr], writes=[bxs[i]], key=pfx + f"mxs{i}")
                else:
                    def qsrc(e, ro, nr, u=u, c0=c0):
                        return fz.LQ[u, ro:ro + nr, c0:c0 + 1024]
                    sc.dma("sync", lambda e, i=i, qsrc=qsrc: e.dma_start(out=xt[i][:], in_=qsrc(e, 0, 64)),
                           reads=[fz.bUr], writes=[bxt[i]], key=pfx + f"mxt{i}")
                    sc.dma("sync", lambda e, i=i, qsrc=qsrc: e.dma_start(out=xs[i][0:8, :], in_=qsrc(e, 8, 8)),
                           reads=[fz.bUr], writes=[bxs[i]], key=pfx + f"mxs{i}")
                    sc.dma("sync", lambda e, i=i, qsrc=qsrc: e.dma_start(out=xs[i][8:16, :], in_=qsrc(e, 0, 8)),
                           reads=[fz.bUr], writes=[bxs[i]], key=pfx + f"mxs{i}")
                if which == 0:
                    sc.dma("sync", lambda e, i=i, c0=c0: e.dma_start(
                        out=ct[i][:], in_=ck[:, :, c0:c0 + 1024].rearrange("a p t -> p a t")),
                        writes=[bct[i]], key=pfx + f"mct{i}")
                else:
                    sc.dma("sync", lambda e, i=i, c0=c0, u=u: e.dma_start(
                        out=ct[i][:], in_=cq[u, :, :, c0:c0 + 1024].rearrange("a p t -> p a t")),
                        writes=[bct[i]], key=pfx + f"mct{i}")
                sc.op("vector", lambda e, i=i: e.tensor_tensor(out=t16[:], in0=xs[i][:], in1=ct[i][:, 1, :], op=ALU.mult),
                      reads=[bxs[i], bct[i]], writes=[bt16])
                sc.op("vector", lambda e, i=i: e.tensor_tensor(out=xt[i][0:16, :], in0=xt[i][0:16, :], in1=ct[i][:, 0, :],
                                                               op=ALU.mult), reads=[bxt[i], bct[i]], writes=[bxt[i]])
                sc.op("vector", lambda e, i=i: e.tensor_tensor(out=xt[i][0:16, :], in0=xt[i][0:16, :], in1=t16[:],
                                                               op=ALU.add), reads=[bxt[i], bt16], writes=[bxt[i]])
                sc.op("scalar", lambda e, i=i: e.activation(out=sqf[:], in_=xt[i][:], func=AF.Square),
                      reads=[bxt[i]], writes=[bsqf])
                for hh in range(2):
                    sc.op("tensor", lambda e, hh=hh: e.matmul(ps[:, 6, :], lhsT=onesf[:], rhs=sqf[:, hh * 512:(hh + 1) * 512],
                                                              start=True, stop=True), reads=[bconst, bsqf], writes=[bps[6]])
                    sc.op("vector", lambda e, which=which: e.tensor_reduce(out=mx[:, 2:3], in_=ps[:, 6, :], axis=mybir.AxisListType.X,
                                                                           op=ALU.max), reads=[bps[6]], writes=[bmx])
                    sc.op("vector", lambda e, which=which: e.tensor_tensor(out=mx[:, which:which + 1], in0=mx[:, which:which + 1],
                                                                           in1=mx[:, 2:3], op=ALU.max), reads=[bmx], writes=[bmx])
                if which == 0:
                    nb0 = c0 // 256
                    sc.op("vector", lambda e, i=i, nb0=nb0: e.tensor_reduce(
                        out=kmean[:, nb0:nb0 + 4], in_=xt[i][:].rearrange("p (n t) -> p n t", t=256),
                        axis=mybir.AxisListType.X, op=ALU.add), reads=[bxt[i]], writes=[bkm])
                    sc.op("scalar", lambda e, i=i, c0=c0: e.copy(out=kaug[64:128, c0:c0 + 1024], in_=xt[i][:]),
                          reads=[bxt[i]], writes=[bk])
                else:
                    sc.op("scalar", lambda e, i=i, c0=c0: e.mul(out=qaug[64:128, c0:c0 + 1024], in_=xt[i][:], mul=0.125),
                          reads=[bxt[i]], writes=[bq])
                    sc.op("vector", lambda e, i=i, c0=c0: e.tensor_copy(out=qf[:, c0:c0 + 1024], in_=xt[i][:]),
                          reads=[bxt[i]], writes=[bqf])
        sc.op("vector", lambda e: e.tensor_tensor(out=mx[:, 3:4], in0=mx[:, 0:1], in1=mx[:, 1:2], op=ALU.mult),
              reads=[bmx], writes=[bmx])
        sc.op("scalar", lambda e: e.activation(out=mx[:, 3:4], in_=mx[:, 3:4], func=AF.Sqrt), reads=[bmx], writes=[bmx])
        sc.op("vector", lambda e: e.tensor_scalar(out=nbias[:], in0=mx[:, 3:4], scalar1=-0.125, scalar2=None, op0=ALU.mult),
              reads=[bmx], writes=[bnb])
        for t in range(NQ // 128):
            r = t // 2
            sc.op("tensor", lambda e, t=t: e.matmul(ps[:, 7, 0:NBLK], lhsT=qf[:, t * 128:(t + 1) * 128], rhs=kmean[:],
                                                    start=True, stop=True), reads=[bqf, bkm], writes=[bps[7]])
            sc.op("vector", lambda e, r=r: e.tensor_tensor(out=gs[:], in0=ps[:, 7, 0:NBLK], in1=pm[:, r * NBLK:(r + 1) * NBLK],
                                                           op=ALU.add), reads=[bps[7], bmask], writes=[bgs])
            sc.op("vector", lambda e: e.max(out=g8[:], in_=gs[:]), reads=[bgs], writes=[bg8])
            sc.op("vector", lambda e: e.tensor_scalar(out=m1[:], in0=gs[:], scalar1=g8[:, 2:3], scalar2=None, op0=ALU.is_ge),
                  reads=[bgs, bg8], writes=[bm1])
            sc.op("vector", lambda e: e.tensor_scalar(out=m2[:], in0=gs[:], scalar1=-1e29, scalar2=None, op0=ALU.is_gt),
                  reads=[bgs], writes=[bm2])
            sc.op("vector", lambda e: e.tensor_tensor(out=m1[:], in0=m1[:], in1=m2[:], op=ALU.mult),
                  reads=[bm1, bm2], writes=[bm1])
            sc.op("vector", lambda e, r=r: e.tensor_tensor(out=m1[:], in0=m1[:], in1=oh[:, r * NBLK:(r + 1) * NBLK], op=ALU.add),
                  reads=[bm1, bmask], writes=[bm1])
            sc.op("vector", lambda e: e.tensor_scalar(out=m2[:], in0=m1[:], scalar1=-1.0, scalar2=-NEG, op0=ALU.add, op1=ALU.mult),
                  reads=[bm1], writes=[bm2])
            sc.op("tensor", lambda e: e.transpose(ps[0:NBLK, 7, 128:256], m2[:], ident[:]),
                  reads=[bm2, bconst], writes=[bps[7]])
            sc.op("vector", lambda e, t=t: e.tensor_copy(out=qaug[0:32, t * 128:(t + 1) * 128], in_=ps[0:NBLK, 7, 128:256]),
                  reads=[bps[7]], writes=[bq])
        tasks = [(r, kt) for r in range(MOBA_SLOTS) for kt in range(4 * r + 4)]
        NB, DEPTH = 4, 3

        def qk(i):
            r, kt = tasks[i]
            KT = 4 * r + 4
            p = i % NB
            sc.op("tensor", lambda e, kt=kt, r=r, p=p: e.matmul(ps[:, p, 0:256], lhsT=kaug[:, kt * 128:(kt + 1) * 128],
                                                               rhs=qaug[:, r * 256:(r + 1) * 256], start=True, stop=True),
                  reads=[bk, bq], writes=[bps[p]])
            if kt >= KT - 4:
                j = kt - (KT - 4) + 4 * (r % 2)
                s = j % 2
                sc.op("vector", lambda e, p=p, j=j, s=s: e.tensor_tensor(out=stm[s][:], in0=ps[:, p, 0:256], in1=cm[:, j, :],
                                                                        op=ALU.add), reads=[bps[p], bmask], writes=[bstm[s]])
                sc.op("scalar", lambda e, p=p, s=s: e.activation(out=pt[p][:], in_=stm[s][:], func=AF.Exp, bias=nbias[:, 0:1],
                                                                 scale=1.0), reads=[bstm[s], bnb], writes=[bpt[p]])
            else:
                sc.op("scalar", lambda e, p=p: e.activation(out=pt[p][:], in_=ps[:, p, 0:256], func=AF.Exp, bias=nbias[:, 0:1],
                                                            scale=1.0), reads=[bps[p], bnb], writes=[bpt[p]])

        def pv(i):
            r, kt = tasks[i]
            KT = 4 * r + 4
            p = i % NB
            po = 4 + (r % 2)
            sc.op("tensor", lambda e, kt=kt, p=p, po=po, KT=KT: e.matmul(ps[:, po, 0:256], lhsT=vaug[:, kt, :], rhs=pt[p][:],
                                                                        start=(kt == 0), stop=(kt == KT - 1)),
                  reads=[bv, bpt[p]], writes=[bps[po]])
            if kt < KT - 1:
                return
            sc.op("vector", lambda e, po=po: e.reciprocal(out=rec[:], in_=ps[64:128, po, 0:256]), reads=[bps[po]], writes=[brec])
            ys = r % 2
            sc.op("vector", lambda e, po=po, ys=ys: e.tensor_tensor(out=yst[ys][:], in0=ps[0:64, po, 0:256], in1=rec[:], op=ALU.mult),
                  reads=[bps[po], brec], writes=[byst[ys]])
            if fz is None:
                sc.dma("sync", lambda e, u=u, r=r, ys=ys: e.dma_start(out=yo[u, :, r * 256:(r + 1) * 256], in_=yst[ys][:]),
                       reads=[byst[ys]], key=pfx + f"myo{ys}")
            else:
                row0 = (r // 4) * 512 + u * 64
                sc.dma("sync", lambda e, row0=row0, r=r, ys=ys: e.dma_start(
                    out=fz.Ysend[row0:row0 + 64, (r % 4) * 256:(r % 4 + 1) * 256], in_=yst[ys][:]),
                    reads=[byst[ys], fz.bYs], key=pfx + f"myo{ys}")

        for i in range(min(DEPTH, len(tasks))):
            qk(i)
        for i in range(len(tasks)):
            if i + DEPTH < len(tasks):
                qk(i + DEPTH)
            pv(i)
    return byst


class SimpleProg:
    def __init__(self):
        self.fused = None
        self.nc = bass.Bass("TRN2", target_bir_lowering=False)
        self.es = ExitStack()
        self.in_names = []
        self.out_names = []

    din = TokProg.din
    dout = TokProg.dout
    sb = TokProg.sb

    def finish(self, sc, outbufs):
        sc.final_wait("sync", outbufs)
        with self.nc.Block() as block:
            sc.emit(block)
        self.es.close()
        return self.nc


def build_moba(nu=3):
    P = SimpleProg()
    sc = Sched(P.nc, P.es)
    ob = moba_emit(P, sc, nu)
    return P, P.finish(sc, ob)


def rope_tables():
    inv = np.exp(np.float32(-np.log(500000.0)) * np.arange(0, 16, 2, dtype=np.float32) / np.float32(16)).astype(np.float32)
    ang = (np.arange(S, dtype=np.float32)[:, None] * inv[None, :]).astype(np.float32)
    cos = np.cos(ang).astype(np.float32).T
    sin = np.sin(ang).astype(np.float32).T
    tab = np.zeros((2, 16, S), np.float32)
    tab[0, 0:8] = cos
    tab[0, 8:16] = cos
    tab[1, 0:8] = -sin
    tab[1, 8:16] = sin
    return tab


def moba_unit_inputs(uq, uk, uv, half, tab):
    blocks = HALF_BLOCKS[half]
    qpos = np.concatenate([np.arange(b * 256, (b + 1) * 256) for b in blocks])
    qT = np.ascontiguousarray(uq[qpos].T)
    kT = np.ascontiguousarray(uk.T)
    sw = np.r_[8:16, 0:8]
    return dict(mq=qT, mqs=np.ascontiguousarray(qT[sw]), mk=kT, mks=np.ascontiguousarray(kT[sw]), mv=np.ascontiguousarray(uv),
                ropeq=np.ascontiguousarray(tab[:, :, qpos]))


def moba_const_inputs(half):
    blocks = HALF_BLOCKS[half]
    pm = np.zeros((MOBA_SLOTS, NBLK), np.float32)
    oh = np.zeros((MOBA_SLOTS, NBLK), np.float32)
    for r, b in enumerate(blocks):
        pm[r, b:] = -1e30
        oh[r, b] = 1.0
    kk = np.arange(128)[:, None]
    qq = np.arange(256)[None, :]
    M0 = np.where(kk <= qq, 0.0, NEG).astype(np.float32)
    M1 = np.where(kk + 128 <= qq, 0.0, NEG).astype(np.float32)
    Z = np.zeros((128, 256), np.float32)
    cm = np.zeros((2, 4, 128, 256), np.float32)
    for par in range(2):
        r = par
        b = blocks[r]
        if b == 2 * r + 1:
            cm[par] = np.stack([Z, Z, M0, M1])
        else:
            cm[par] = np.stack([M0, M1, Z, Z])
    pmb = np.ascontiguousarray(np.broadcast_to(pm.reshape(1, -1), (128, MOBA_SLOTS * NBLK)))
    ohb = np.ascontiguousarray(np.broadcast_to(oh.reshape(1, -1), (128, MOBA_SLOTS * NBLK)))
    return dict(pm=pmb, oh=ohb, cm=cm)


def moba_shared_inputs(tab):
    boh = np.zeros((32, S), np.float32)
    for n in range(32):
        boh[n, n * 256:(n + 1) * 256] = 1.0
    return dict(ropek=tab, boh=boh, ident=np.eye(128, dtype=np.float32))


CH = 32


def conv_emit(P, sc, pfx="", fz=None):
    nc, es = P.nc, P.es
    T = TOK
    if fz is None:
        uc = P.din(pfx + "uc", [512, T + CH])
        yc = P.dout(pfx + "ycT", [256, T])
    else:
        yc = fz.Yfull
        flag_d = P.din("cflag", [128, 1])
        flag = P.sb(pfx + "cflag_sb", [128, 1], F32)
    cw = P.din(pfx + "cw", [128, 2, 31])
    cp = P.din(pfx + "cp", [128, 2, 3])
    idb = P.din("cident", [128, 128])
    a_t = [P.sb(pfx + f"ca{c}", [128, T + CH], F32) for c in range(2)]
    g_t = [P.sb(pfx + f"cg{c}", [128, T + CH], F32) for c in range(2)]
    hg = [P.sb(pfx + f"chg{c}", [128, T + CH], BF16) for c in range(2)]
    dg = [P.sb(pfx + f"cdg{c}", [128, 31, 128], BF16) for c in range(2)]
    cws = P.sb(pfx + "cws", [128, 2, 31], F32)
    cps = P.sb(pfx + "cps", [128, 2, 3], F32)
    idt = P.sb(pfx + "cidt", [128, 128], F32)
    onesf = P.sb(pfx + "cones", [128, 128], F32)
    epsc = P.sb(pfx + "ceps", [128, 1], F32)
    hc = [P.sb(pfx + f"chc{c}", [128, 512], F32) for c in range(2)]
    sq = [P.sb(pfx + f"csq{c}", [128, 512], F32) for c in range(2)]
    mean = P.sb(pfx + "cmean", [128, 512], F32)
    msq = P.sb(pfx + "cmsq", [128, 512], F32)
    var = P.sb(pfx + "cvar", [128, 512], F32)
    rstd = P.sb(pfx + "crstd", [128, 512], F32)
    tt_ = [P.sb(pfx + f"ctt{c}", [128, 512], F32) for c in range(2)]
    yo = [P.sb(pfx + f"cyo{c}", [128, 512], F32) for c in range(2)]
    ps = es.enter_context(nc.psum_tensor(pfx + "cps_", [128, 4, 512], F32)) if fz is None else fz.ps
    ba = [Buf(), Buf()]
    bg = [Buf(), Buf()]
    bhg = [Buf(), Buf()]
    bdg = [Buf(), Buf()]
    bc, bhc, bsq = Buf(), [Buf(), Buf()], [Buf(), Buf()]
    bmean, bmsq, bvar, brstd = Buf(), Buf(), Buf(), Buf()
    btt = [Buf(), Buf()]
    byo = [Buf(), Buf()]
    bps = [Buf() for _ in range(4)]

    sc.dma("sync", lambda e: e.dma_start(out=cws[:], in_=cw[:, :, :]), writes=[bc], key=pfx + "cc")
    sc.dma("sync", lambda e: e.dma_start(out=cps[:], in_=cp[:, :, :]), writes=[bc], key=pfx + "cc")
    sc.dma("sync", lambda e: e.dma_start(out=idt[:], in_=idb[:, :]), writes=[bc], key=pfx + "cc")
    sc.op("vector", lambda e: e.memset(onesf[:], 1.0), writes=[bc])
    sc.op("vector", lambda e: e.memset(epsc[:], EPS), writes=[bc])
    if fz is not None:
        sc.dma("sync", lambda e: e.dma_start(out=flag[:], in_=flag_d[:, :]), writes=[bc], key=pfx + "cc")

    def prev_rows(e, row0):
        return fz.LH[row0:row0 + 128, :]

    for c in range(2):
        if fz is None:
            sc.dma("sync", lambda e, c=c: e.dma_start(out=a_t[c][:], in_=uc[c * 128:(c + 1) * 128, :]), writes=[ba[c]],
                   key=pfx + f"ca{c}")
            sc.dma("sync", lambda e, c=c: e.dma_start(out=g_t[c][:], in_=uc[256 + c * 128:256 + (c + 1) * 128, :]),
                   writes=[bg[c]], key=pfx + f"cg{c}")
        else:
            sc.dma("sync", lambda e, c=c: e.dma_start(out=a_t[c][:, CH:], in_=fz.Usend[c * 128:(c + 1) * 128, :]),
                   reads=[fz.bU], writes=[ba[c]], key=pfx + f"ca{c}")
            sc.dma("sync", lambda e, c=c: e.dma_start(out=a_t[c][:, 0:CH], in_=prev_rows(e, c * 128)),
                   reads=[fz.bUr], writes=[ba[c]], key=pfx + f"ca{c}")
            sc.dma("sync", lambda e, c=c: e.dma_start(out=g_t[c][:, CH:], in_=fz.Usend[256 + c * 128:256 + (c + 1) * 128, :]),
                   reads=[fz.bU], writes=[bg[c]], key=pfx + f"cg{c}")
            sc.dma("sync", lambda e, c=c: e.dma_start(out=g_t[c][:, 0:CH], in_=prev_rows(e, 256 + c * 128)),
                   reads=[fz.bUr], writes=[bg[c]], key=pfx + f"cg{c}")
        sc.op("scalar", lambda e, c=c: e.activation(out=g_t[c][:], in_=g_t[c][:], func=AF.Sigmoid),
              reads=[bg[c]], writes=[bg[c]])
        sc.op("vector", lambda e, c=c: e.tensor_tensor(out=hg[c][:], in0=a_t[c][:], in1=g_t[c][:], op=ALU.mult),
              reads=[ba[c], bg[c]], writes=[bhg[c]])
        if fz is not None:
            sc.op("vector", lambda e, c=c: e.tensor_scalar(out=hg[c][:, 0:CH], in0=hg[c][:, 0:CH], scalar1=flag[:, 0:1],
                                                           scalar2=None, op0=ALU.mult), reads=[bhg[c], bc], writes=[bhg[c]])
        for k in range(31):
            sc.op("gpsimd", lambda e, c=c, k=k: e.tensor_scalar(out=dg[c][:, k, :], in0=idt[:], scalar1=cws[:, c, k:k + 1],
                                                                scalar2=None, op0=ALU.mult), reads=[bc], writes=[bdg[c]])
    for tt in range(T // 512):
        for c in range(2):
            for k in range(31):
                o = tt * 512 + 2 + k
                sc.op("tensor", lambda e, c=c, k=k, o=o: e.matmul(ps[:, c, :], lhsT=dg[c][:, k, :], rhs=hg[c][:, o:o + 512],
                                                                 start=(k == 0), stop=(k == 30)),
                      reads=[bdg[c], bhg[c]], writes=[bps[c]], inc=(k == 30))
            sc.op("scalar", lambda e, c=c: e.activation(out=hc[c][:], in_=ps[:, c, :], func=AF.Identity, bias=cps[:, c, 0:1],
                                                        scale=1.0), reads=[bps[c], bc], writes=[bhc[c]])
            sc.op("scalar", lambda e, c=c: e.activation(out=sq[c][:], in_=hc[c][:], func=AF.Square), reads=[bhc[c]],
                  writes=[bsq[c]])
        for c in range(2):
            sc.op("tensor", lambda e, c=c: e.matmul(ps[:, 2, :], lhsT=onesf[:], rhs=hc[c][:], start=(c == 0), stop=(c == 1)),
                  reads=[bc, bhc[c]], writes=[bps[2]])
        for c in range(2):
            sc.op("tensor", lambda e, c=c: e.matmul(ps[:, 3, :], lhsT=onesf[:], rhs=sq[c][:], start=(c == 0), stop=(c == 1)),
                  reads=[bc, bsq[c]], writes=[bps[3]])
        sc.op("vector", lambda e: e.tensor_scalar(out=mean[:], in0=ps[:, 2, :], scalar1=1.0 / 256, scalar2=None, op0=ALU.mult),
              reads=[bps[2]], writes=[bmean])
        sc.op("vector", lambda e: e.tensor_tensor(out=msq[:], in0=mean[:], in1=mean[:], op=ALU.mult), reads=[bmean], writes=[bmsq])
        sc.op("vector", lambda e: e.scalar_tensor_tensor(out=var[:], in0=ps[:, 3, :], scalar=1.0 / 256, in1=msq[:],
                                                         op0=ALU.mult, op1=ALU.subtract), reads=[bps[3], bmsq], writes=[bvar])
        sc.op("scalar", lambda e: e.activation(out=var[:], in_=var[:], func=AF.Sqrt, bias=epsc[:, 0:1], scale=1.0),
              reads=[bvar, bc], writes=[bvar])
        sc.op("vector", lambda e: e.reciprocal(out=rstd[:], in_=var[:]), reads=[bvar], writes=[brstd])
        for c in range(2):
            sc.op("vector", lambda e, c=c: e.tensor_tensor(out=tt_[c][:], in0=hc[c][:], in1=mean[:], op=ALU.subtract),
                  reads=[bhc[c], bmean], writes=[btt[c]])
            sc.op("vector", lambda e, c=c: e.tensor_tensor(out=tt_[c][:], in0=tt_[c][:], in1=rstd[:], op=ALU.mult),
                  reads=[btt[c], brstd], writes=[btt[c]])
            sc.op("scalar", lambda e, c=c: e.activation(out=yo[c][:], in_=tt_[c][:], func=AF.Silu, bias=cps[:, c, 2:3],
                                                        scale=cps[:, c, 1:2]), reads=[btt[c], bc], writes=[byo[c]])
            sc.dma("sync", lambda e, c=c, tt=tt: e.dma_start(out=yc[c * 128:(c + 1) * 128, tt * 512:(tt + 1) * 512], in_=yo[c][:]),
                   reads=[byo[c]] + ([] if fz is None else [fz.bY]), key=pfx + f"cyo{c}")
    return byo


def build_conv():
    P = SimpleProg()
    sc = Sched(P.nc, P.es)
    ob = conv_emit(P, sc)
    return P, P.finish(sc, ob)


def conv_inputs(u_b, j, conv_w, conv_b, ln_g, ln_b):
    t0 = j * TOK
    uc = np.zeros((512, TOK + CH), np.float32)
    lo = max(0, t0 - CH)
    uc[:, CH - (t0 - lo):] = u_b[lo:t0 + TOK, 0:512].T
    lay = lambda v: np.ascontiguousarray(v.reshape(2, 128).T)
    cw = np.ascontiguousarray(conv_w.T.reshape(2, 128, 31).transpose(1, 0, 2))
    cp = np.ascontiguousarray(np.stack([lay(conv_b), lay(ln_g), lay(ln_b)], axis=-1))
    return dict(uc=uc, cw=cw, cp=cp, cident=np.eye(128, dtype=np.float32))


GC = 64
NCH = S // GC
GSEG = 16
AX = mybir.AxisListType


def gdn_emit(P, sc, nu, pfx="", fz=None):
    import os
    STOP = float(os.environ.get("GDN_STOP", "99"))
    nc, es = P.nc, P.es
    if fz is None:
        raw_d = P.din(pfx + "graw", [nu, 3, 64, S + 3])
        gz_d = P.din(pfx + "gz", [nu, S, 64])
        ga_d = P.din(pfx + "ga", [nu, 64, NCH])
        gb_d = P.din(pfx + "gb", [nu, 64, NCH])
        go_d = P.dout(pfx + "go", [nu, S, 64])
    gcw_d = P.din(pfx + "gcw", [nu, 64, 12])
    gpar_d = P.din(pfx + "gpar", [nu, 64, 2])
    gng_d = P.din(pfx + "gng", [nu, 64, 64])
    gcst_d = P.din("gcst", [3, 64, 64])

    def unit_h(e, u):
        return fz.dyn(e, "gpsimd", ("gh", u))

    f = lambda n, shp: P.sb(pfx + n, shp, F32)
    cst = f("gcst_sb", [64, 3, 64])
    TriB = f("gTriB", [64, 8, 64])
    MB = f("gMB", [64, 8, 64])
    IB = f("gIB", [64, 8, 64])
    ones64 = f("gones", [64, 64])
    epsg = f("geps", [64, 1])
    bcst = Buf()
    sc.dma("sync", lambda e: e.dma_start(out=cst[:], in_=gcst_d.rearrange("a p q -> p a q")), writes=[bcst], key=pfx + "gc")
    sc.op("vector", lambda e: e.memset(ones64[:], 1.0), writes=[bcst])
    sc.op("vector", lambda e: e.memset(epsg[:], EPS), writes=[bcst])
    for j in range(8):
        sc.op("vector", lambda e, j=j: e.tensor_copy(out=TriB[:, j, :], in_=cst[:, 0, :]), reads=[bcst], writes=[bcst])
        sc.op("vector", lambda e, j=j: e.tensor_copy(out=MB[:, j, :], in_=cst[:, 1, :]), reads=[bcst], writes=[bcst])
        sc.op("vector", lambda e, j=j: e.tensor_copy(out=IB[:, j, :], in_=cst[:, 2, :]), reads=[bcst], writes=[bcst])
    Tri = cst[:, 0, :]
    I64 = cst[:, 2, :]

    def bc_n(t, n0):
        return t[:, n0:n0 + 8].unsqueeze(2).to_broadcast([64, 8, 64])


    def emit_unit(u, sc):
        u2 = u % 2
        GB, SB0 = 4 * u2, 4 * u2 + 3
        f = lambda n, shp: P.sb(pfx + f"u{u}_" + n, shp, F32)
        gcw = f("gcw_sb", [64, 12])
        dgw = P.sb(pfx + f"u{u}_" + "gdgw", [64, 12, 64], BF16)
        par = f("gpar_sb", [64, 2])
        negA = f("gnegA", [64, 1])
        ngb = f("gngb", [64, 64])
        a_t = f("ga_sb", [64, NCH])
        b_t = f("gb_sb", [64, NCH])
        g_t = f("gg", [64, NCH])
        beta = f("gbeta", [64, NCH])
        gc = f("ggc", [64, NCH])
        egc = f("gegc", [64, NCH])
        eglb = f("geglb", [64, NCH])
        edec = f("gedec", [64, NCH])
        bgk = f("gbgk", [64, NCH])
        SEGT = S // GSEG
        SEGC = NCH // GSEG
        raw = [P.sb(pfx + f"u{u}_" + f"graw{i}", [64, 515], BF16) for i in range(2)]
        xa = [f(f"gxa{i}", [64, 512]) for i in range(2)]
        xq = f("gxq", [64, 512])
        rn = f("grn", [64, 512])
        qnT = f("gqnT", [64, SEGT])
        knT = f("gknT", [64, SEGT])
        Kt = f("gKt", [64, SEGC, 64])
        Vt = f("gVt", [64, SEGC, 64])
        oseg = f("goseg", [64, SEGC, 64])
        zseg = f("gzseg", [64, SEGC, 64])
        osq = f("gosq", [64, SEGC, 64])
        oss = f("goss", [64, SEGC])
        rhsD = f("grhsD", [64, 8, 64])
        ED = f("gED", [64, 8, 64])
        EDT = f("gEDT", [64, 8, 64])
        Lp = [f(f"gL{i}", [64, 8, 64]) for i in range(2)]
        Np = [f(f"gN{i}", [64, 8, 64]) for i in range(2)]
        Pm = f("gP", [64, 8, 64])
        Lb = [P.sb(pfx + f"u{u}_" + f"gLb{i}", [64, 8, 64], BF16) for i in range(2)]
        Nb = [P.sb(pfx + f"u{u}_" + f"gNb{i}", [64, 8, 64], BF16) for i in range(2)]
        Pb = P.sb(pfx + f"u{u}_" + "gPb", [64, 8, 64], BF16)
        bLb, bNb, bPb = [Buf(), Buf()], [Buf(), Buf()], Buf()
        Kbg = f("gKbg", [64, 8, 64])
        Vb = f("gVb", [64, 8, 64])
        kdec = f("gkdec", [64, 8, 64])
        u_sb = f("gu", [64, 8, 64])
        wT = f("gwT", [64, 8, 64])
        qkT = f("gqkT", [64, 8, 64])
        St = f("gS", [64, 64])
        vn = [f(f"gvn{i}", [64, 64]) for i in range(2)]
        As = [f(f"gAs{i}", [64, 64]) for i in range(2)]
        ps = es.enter_context(nc.psum_tensor(pfx + "gps", [64, 8, 512], F32)) if fz is None else fz.ps[0:64, :, :]

        B_ = lambda: Buf()
        bpar, bg = B_(), B_()
        braw = [B_(), B_()]
        bxa = [B_(), B_()]
        bxq, brn, bqn, bkn, bKt, bVt, boseg, bz, bosq, boss = (B_() for _ in range(10))
        brhsD, bED, bEDT, bP, bKbg, bVb, bkdec, bu, bwT, bqkT, bS = (B_() for _ in range(11))
        bL = [B_(), B_()]
        bN = [B_(), B_()]
        bvn = [B_(), B_()]
        bAs = [B_(), B_()]
        bps = [B_() for _ in range(8)]
        wk = [0]

        def nps():
            i = GB + wk[0] % 3
            wk[0] += 1
            return i

        sc.dma("sync", lambda e, u=u: e.dma_start(out=gcw[:], in_=gcw_d[u]), writes=[bpar], key=pfx + "gp")
        sc.dma("sync", lambda e, u=u: e.dma_start(out=par[:], in_=gpar_d[u]), writes=[bpar], key=pfx + "gp")
        sc.dma("sync", lambda e, u=u: e.dma_start(out=ngb[:], in_=gng_d[u]), writes=[bpar], key=pfx + "gp")
        if fz is None:
            sc.dma("sync", lambda e, u=u: e.dma_start(out=a_t[:], in_=ga_d[u]), writes=[bpar], key=pfx + "gp")
            sc.dma("sync", lambda e, u=u: e.dma_start(out=b_t[:], in_=gb_d[u]), writes=[bpar], key=pfx + "gp")
        else:
            for rr in range(4):
                for (dst, ro) in ((a_t, 3200), (b_t, 3206)):
                    def absrc(e, u=u, rr=rr, ro=ro):
                        return fz.LAB[u, (0 if ro == 3200 else 1):(1 if ro == 3200 else 2), rr * TOK:(rr + 1) * TOK].rearrange(
                            "o (n s) -> s (o n)", s=64)
                    sc.dma("gpsimd", lambda e, dst=dst, rr=rr, absrc=absrc: e.dma_start(
                        out=dst[:, rr * 32:(rr + 1) * 32], in_=absrc(e), allow_slow_non_contiguous=True),
                        reads=[fz.bUr], writes=[bpar], key=pfx + "gp")
        for k in range(12):
            sc.op("gpsimd", lambda e, k=k: e.tensor_scalar(out=dgw[:, k, :], in0=I64, scalar1=gcw[:, k:k + 1], scalar2=None,
                                                           op0=ALU.mult), reads=[bcst, bpar], writes=[bpar])
        sc.op("scalar", lambda e: e.activation(out=negA[:], in_=par[:, 0:1], func=AF.Exp), reads=[bpar], writes=[bg])
        sc.op("vector", lambda e: e.tensor_scalar(out=negA[:], in0=negA[:], scalar1=-1.0, scalar2=None, op0=ALU.mult),
              reads=[bg], writes=[bg])
        sc.op("scalar", lambda e: e.activation(out=g_t[:], in_=a_t[:], func=AF.Exp, bias=par[:, 1:2], scale=1.0),
              reads=[bpar, bg], writes=[bg])
        sc.op("scalar", lambda e: e.activation(out=g_t[:], in_=g_t[:], func=AF.Ln, bias=1.0, scale=1.0), reads=[bg], writes=[bg])
        sc.op("vector", lambda e: e.tensor_scalar(out=g_t[:], in0=g_t[:], scalar1=negA[:, 0:1], scalar2=None, op0=ALU.mult),
              reads=[bg], writes=[bg])
        sc.op("scalar", lambda e: e.activation(out=beta[:], in_=b_t[:], func=AF.Sigmoid), reads=[bpar, bg], writes=[bg])
        sc.op("tensor", lambda e: e.matmul(ps[:, GB, 0:NCH], lhsT=Tri, rhs=g_t[:], start=True, stop=True),
              reads=[bcst, bg], writes=[bps[GB]])
        sc.op("tensor", lambda e: e.matmul(ps[:, GB, NCH:2 * NCH], lhsT=ones64[:], rhs=g_t[:], start=True, stop=True),
              reads=[bcst, bg], writes=[bps[GB]])
        sc.op("vector", lambda e: e.tensor_copy(out=gc[:], in_=ps[:, GB, 0:NCH]), reads=[bps[GB], bg], writes=[bg])
        sc.op("vector", lambda e: e.tensor_copy(out=eglb[:], in_=ps[:, GB, NCH:2 * NCH]), reads=[bps[GB], bg], writes=[bg])
        sc.op("vector", lambda e: e.tensor_tensor(out=edec[:], in0=eglb[:], in1=gc[:], op=ALU.subtract), reads=[bg], writes=[bg])
        sc.op("scalar", lambda e: e.activation(out=egc[:], in_=gc[:], func=AF.Exp), reads=[bg], writes=[bg])
        sc.op("scalar", lambda e: e.activation(out=eglb[:], in_=eglb[:], func=AF.Exp), reads=[bg], writes=[bg])
        sc.op("scalar", lambda e: e.activation(out=edec[:], in_=edec[:], func=AF.Exp), reads=[bg], writes=[bg])
        sc.op("vector", lambda e: e.tensor_tensor(out=bgk[:], in0=beta[:], in1=egc[:], op=ALU.mult), reads=[bg], writes=[bg])
        sc.op("vector", lambda e: e.memset(St[:], 0.0), writes=[bS])
        if STOP <= 1:
            return [bS]

        for seg in range(GSEG):
            for tt in range(SEGT // 512):
                c0 = seg * SEGT + tt * 512
                for j in range(3):
                    ri = (tt * 3 + j) % 2
                    if fz is None:
                        sc.dma("gpsimd", lambda e, u=u, j=j, ri=ri, c0=c0: e.dma_start(out=raw[ri][:], in_=raw_d[u, j, :, c0:c0 + 515]),
                               writes=[braw[ri]], key=pfx + f"graw{ri}")
                    else:
                        rr, t0 = c0 // TOK, c0 % TOK

                        def rsrc(e, rr_, ta, tb, u=u, j=j):
                            return fz.LG[u, j, :, rr_ * TOK + ta:rr_ * TOK + tb]
                        sc.dma("gpsimd", lambda e, ri=ri, rr=rr, t0=t0, rsrc=rsrc: e.dma_start(out=raw[ri][:, 3:515],
                                                                                            in_=rsrc(e, rr, t0, t0 + 512)),
                               reads=[fz.bUr], writes=[braw[ri]], key=pfx + f"graw{ri}")
                        if t0 >= 3:
                            sc.dma("gpsimd", lambda e, ri=ri, rr=rr, t0=t0, rsrc=rsrc: e.dma_start(out=raw[ri][:, 0:3],
                                                                                                in_=rsrc(e, rr, t0 - 3, t0)),
                                   reads=[fz.bUr], writes=[braw[ri]], key=pfx + f"graw{ri}")
                        elif rr > 0:
                            sc.dma("gpsimd", lambda e, ri=ri, rr=rr, rsrc=rsrc: e.dma_start(out=raw[ri][:, 0:3],
                                                                                         in_=rsrc(e, rr - 1, TOK - 3, TOK)),
                                   reads=[fz.bUr], writes=[braw[ri]], key=pfx + f"graw{ri}")
                        else:
                            sc.op("vector", lambda e, ri=ri: e.memset(raw[ri][:, 0:3], 0.0), writes=[braw[ri]])
                    p1 = nps()
                    for k in range(4):
                        sc.op("tensor", lambda e, j=j, k=k, ri=ri, p1=p1: e.matmul(ps[:, p1, :], lhsT=dgw[:, j * 4 + k, :],
                                                                                 rhs=raw[ri][:, k:k + 512], start=(k == 0), stop=(k == 3)),
                              reads=[bpar, braw[ri]], writes=[bps[p1]], inc=(k == 3))
                    xi = j % 2
                    sc.op("scalar", lambda e, xi=xi, p1=p1: e.activation(out=xa[xi][:], in_=ps[:, p1, :], func=AF.Silu),
                          reads=[bps[p1]], writes=[bxa[xi]])
                    if j < 2:
                        sc.op("scalar", lambda e, xi=xi: e.activation(out=xq[:], in_=xa[xi][:], func=AF.Square),
                              reads=[bxa[xi]], writes=[bxq])
                        p2 = nps()
                        sc.op("tensor", lambda e, p2=p2: e.matmul(ps[:, p2, :], lhsT=ones64[:], rhs=xq[:], start=True, stop=True),
                              reads=[bcst, bxq], writes=[bps[p2]])
                        sc.op("scalar", lambda e, p2=p2: e.activation(out=rn[:], in_=ps[:, p2, :], func=AF.Sqrt, bias=epsg[:, 0:1],
                                                                      scale=1.0), reads=[bps[p2], bcst], writes=[brn])
                        sc.op("vector", lambda e: e.reciprocal(out=rn[:], in_=rn[:]), reads=[brn], writes=[brn])
                        dst, bd = (qnT, bqn) if j == 0 else (knT, bkn)
                        scl = 0.125 if j == 0 else 1.0
                        sc.op("vector", lambda e, xi=xi, dst=dst, tt=tt, scl=scl: e.scalar_tensor_tensor(
                            out=dst[:, tt * 512:(tt + 1) * 512], in0=xa[xi][:], scalar=scl, in1=rn[:], op0=ALU.mult, op1=ALU.mult),
                            reads=[bxa[xi], brn], writes=[bd])
                    if j >= 1:
                        srcT = knT[:, tt * 512:(tt + 1) * 512] if j == 1 else xa[xi][:]
                        bsrc = bkn if j == 1 else bxa[xi]
                        p3 = nps()
                        for cj in range(8):
                            sc.op("tensor", lambda e, srcT=srcT, cj=cj, p3=p3: e.transpose(ps[:, p3, cj * 64:(cj + 1) * 64],
                                                                                         srcT[:, cj * 64:(cj + 1) * 64], I64),
                                  reads=[bsrc, bcst], writes=[bps[p3]], inc=(cj == 7))
                        dstT, bdt = (Kt, bKt) if j == 1 else (Vt, bVt)
                        sc.op("vector", lambda e, dstT=dstT, tt=tt, p3=p3: e.tensor_copy(
                            out=dstT[:, tt * 8:(tt + 1) * 8, :], in_=ps[:, p3, :].rearrange("p (a b) -> p a b", b=64)),
                            reads=[bps[p3]], writes=[bdt])
            if fz is None:
                sc.dma("sync", lambda e, u=u, seg=seg: e.dma_start(
                    out=zseg[:], in_=gz_d[u, seg * SEGT:(seg + 1) * SEGT, :].rearrange("(n s) d -> s n d", s=64)),
                    writes=[bz], key=pfx + "gz")
            else:
                def zsrc(e, u=u, seg=seg):
                    return fz.LZ[u, :, seg * SEGT:(seg + 1) * SEGT]
                sc.dma("gpsimd", lambda e, zsrc=zsrc: e.dma_start(out=zseg[:].rearrange("p a b -> p (a b)"), in_=zsrc(e)),
                       reads=[fz.bUr], writes=[bz], key=pfx + "gz")
            if STOP <= 2:
                return [bz, bKt, bVt, bqn]
            for gi in range(SEGC // 8):
                l0 = gi * 8
                n0 = seg * SEGC + l0
                v3 = lambda t: t[:]
                pk, pd, pdt = nps(), nps(), nps()
                for j in range(8):
                    cs = slice((l0 + j) * 64, (l0 + j + 1) * 64)
                    sc.op("tensor", lambda e, j=j, cs=cs, pk=pk: e.matmul(ps[:, pk, j * 64:(j + 1) * 64], lhsT=knT[:, cs], rhs=knT[:, cs],
                                                                         start=True, stop=True), reads=[bkn], writes=[bps[pk]], inc=(j == 7))
                sc.op("vector", lambda e, n0=n0: e.tensor_tensor(out=rhsD[:], in0=MB[:], in1=bc_n(g_t, n0), op=ALU.mult),
                      reads=[bcst, bg], writes=[brhsD])
                if STOP <= 2.1:
                    return [brhsD, bps[pk]]
                sc.op("tensor", lambda e, pd=pd: e.matmul(ps[:, pd, :], lhsT=Tri, rhs=rhsD[:].rearrange("p a b -> p (a b)"),
                                                          start=True, stop=True), reads=[bcst, brhsD], writes=[bps[pd]])
                for j in range(8):
                    sc.op("tensor", lambda e, j=j, pdt=pdt: e.matmul(ps[:, pdt, j * 64:(j + 1) * 64], lhsT=rhsD[:, j, :], rhs=Tri,
                                                                    start=True, stop=True), reads=[bcst, brhsD], writes=[bps[pdt]], inc=(j == 7))
                r3 = lambda ap: ap.rearrange("p (a b) -> p a b", b=64)
                sc.op("scalar", lambda e, pd=pd: e.activation(out=ED[:], in_=r3(ps[:, pd, :]), func=AF.Exp), reads=[bps[pd]], writes=[bED])
                sc.op("scalar", lambda e, pdt=pdt: e.activation(out=EDT[:], in_=r3(ps[:, pdt, :]), func=AF.Exp), reads=[bps[pdt]], writes=[bEDT])
                if STOP <= 2.2:
                    return [bED, bEDT]
                sc.op("vector", lambda e, pk=pk: e.tensor_tensor(out=Lp[0][:], in0=r3(ps[:, pk, :]), in1=ED[:], op=ALU.mult),
                      reads=[bps[pk], bED], writes=[bL[0]])
                sc.op("vector", lambda e, n0=n0: e.tensor_tensor(out=Lp[0][:], in0=Lp[0][:], in1=bc_n(beta, n0), op=ALU.mult),
                      reads=[bL[0], bg], writes=[bL[0]])
                sc.op("vector", lambda e: e.tensor_tensor(out=Lp[0][:], in0=Lp[0][:], in1=MB[:], op=ALU.mult),
                      reads=[bL[0], bcst], writes=[bL[0]])
                if STOP <= 2.3:
                    return [bL[0]]
                pn = nps()
                for j in range(8):
                    sc.op("tensor", lambda e, j=j, pn=pn: e.matmul(ps[:, pn, j * 64:(j + 1) * 64], lhsT=Lp[0][:, j, :], rhs=I64,
                                                                  start=True, stop=True),
                          reads=[bL[0], bcst], writes=[bps[pn]], inc=(j == 7))
                sc.op("scalar", lambda e, pn=pn: e.copy(out=Np[0][:], in_=r3(ps[:, pn, :])), reads=[bps[pn]], writes=[bN[0]])
                sc.op("vector", lambda e: e.tensor_tensor(out=Pm[:], in0=IB[:], in1=Np[0][:], op=ALU.subtract),
                      reads=[bN[0], bcst], writes=[bP])
                if STOP <= 2.4:
                    return [bP, bN[0]]
                sc.op("gpsimd", lambda e: e.tensor_copy(out=Lb[0][:], in_=Lp[0][:]), reads=[bL[0]], writes=[bLb[0]])
                sc.op("gpsimd", lambda e: e.tensor_copy(out=Nb[0][:], in_=Np[0][:]), reads=[bN[0]], writes=[bNb[0]])
                sc.op("gpsimd", lambda e: e.tensor_copy(out=Pb[:], in_=Pm[:]), reads=[bP], writes=[bPb])
                cur = 0
                for lvl in range(5):
                    nxt = 1 - cur
                    pl = nps()
                    for j in range(8):
                        sc.op("tensor", lambda e, j=j, pl=pl, cur=cur: e.matmul(ps[:, pl, j * 64:(j + 1) * 64], lhsT=Nb[cur][:, j, :],
                                                                               rhs=Lb[cur][:, j, :], start=True, stop=True),
                              reads=[bNb[cur], bLb[cur]], writes=[bps[pl]], inc=(j == 7))
                    if lvl < 4:
                        pn2 = nps()
                        for j in range(8):
                            sc.op("tensor", lambda e, j=j, pn2=pn2, cur=cur: e.matmul(ps[:, pn2, j * 64:(j + 1) * 64], lhsT=Lb[cur][:, j, :],
                                                                                     rhs=Nb[cur][:, j, :], start=True, stop=True),
                                  reads=[bNb[cur], bLb[cur]], writes=[bps[pn2]], inc=(j == 7))
                    sc.op("scalar", lambda e, pl=pl, nxt=nxt: e.copy(out=Lb[nxt][:], in_=r3(ps[:, pl, :])), reads=[bps[pl]], writes=[bLb[nxt]])
                    if lvl < 4:
                        sc.op("vector", lambda e, pn2=pn2, nxt=nxt: e.tensor_copy(out=Nb[nxt][:], in_=r3(ps[:, pn2, :])),
                              reads=[bps[pn2]], writes=[bNb[nxt]])
                    pu = nps()
                    for j in range(8):
                        sc.op("tensor", lambda e, j=j, pu=pu, nxt=nxt: e.matmul(ps[:, pu, j * 64:(j + 1) * 64], lhsT=Lb[nxt][:, j, :],
                                                                               rhs=Pb[:, j, :], start=True, stop=True),
                              reads=[bLb[nxt], bPb], writes=[bps[pu]], inc=(j == 7))
                    sc.op("vector", lambda e, pu=pu: e.tensor_tensor(out=Pm[:], in0=Pm[:], in1=r3(ps[:, pu, :]), op=ALU.add),
                          reads=[bps[pu], bP], writes=[bP])
                    if lvl < 4:
                        sc.op("gpsimd", lambda e: e.tensor_copy(out=Pb[:], in_=Pm[:]), reads=[bP], writes=[bPb])
                    cur = nxt
                if STOP <= 2.5:
                    return [bP]
                sc.op("vector", lambda e, l0=l0, n0=n0: e.tensor_tensor(out=Kbg[:], in0=Kt[:, l0:l0 + 8, :], in1=bc_n(bgk, n0), op=ALU.mult),
                      reads=[bKt, bg], writes=[bKbg])
                sc.op("vector", lambda e, l0=l0, n0=n0: e.tensor_tensor(out=Vb[:], in0=Vt[:, l0:l0 + 8, :], in1=bc_n(beta, n0), op=ALU.mult),
                      reads=[bVt, bg], writes=[bVb])
                sc.op("vector", lambda e, l0=l0, n0=n0: e.tensor_tensor(out=kdec[:], in0=Kt[:, l0:l0 + 8, :], in1=bc_n(edec, n0), op=ALU.mult),
                      reads=[bKt, bg], writes=[bkdec])
                p_u, p_w, p_q = nps(), nps(), nps()
                for j in range(8):
                    sc.op("tensor", lambda e, j=j, p_u=p_u: e.matmul(ps[:, p_u, j * 64:(j + 1) * 64], lhsT=Pm[:, j, :], rhs=Vb[:, j, :],
                                                                    start=True, stop=True), reads=[bP, bVb], writes=[bps[p_u]], inc=(j == 7))
                for j in range(8):
                    sc.op("tensor", lambda e, j=j, p_w=p_w: e.matmul(ps[:, p_w, j * 64:(j + 1) * 64], lhsT=Kbg[:, j, :], rhs=Pm[:, j, :],
                                                                    start=True, stop=True), reads=[bP, bKbg], writes=[bps[p_w]], inc=(j == 7))
                for j in range(8):
                    cs = slice((l0 + j) * 64, (l0 + j + 1) * 64)
                    sc.op("tensor", lambda e, j=j, cs=cs, p_q=p_q: e.matmul(ps[:, p_q, j * 64:(j + 1) * 64], lhsT=knT[:, cs], rhs=qnT[:, cs],
                                                                           start=True, stop=True), reads=[bkn, bqn], writes=[bps[p_q]], inc=(j == 7))
                sc.op("scalar", lambda e, p_u=p_u: e.copy(out=u_sb[:], in_=r3(ps[:, p_u, :])), reads=[bps[p_u]], writes=[bu])
                sc.op("scalar", lambda e, p_w=p_w: e.copy(out=wT[:], in_=r3(ps[:, p_w, :])), reads=[bps[p_w]], writes=[bwT])
                sc.op("vector", lambda e, p_q=p_q: e.tensor_tensor(out=qkT[:], in0=r3(ps[:, p_q, :]), in1=EDT[:], op=ALU.mult),
                      reads=[bps[p_q], bEDT], writes=[bqkT])
                sc.op("vector", lambda e: e.tensor_tensor(out=qkT[:], in0=qkT[:], in1=TriB[:], op=ALU.mult), reads=[bqkT, bcst], writes=[bqkT])
                if STOP <= 3:
                    return [bqkT, bu, bwT]
                for j in range(8):
                    n = n0 + j
                    l = l0 + j
                    cs = slice(l * 64, (l + 1) * 64)
                    i2 = j % 2
                    bx_, by_ = SB0, SB0
                    sc.op("tensor", lambda e, j=j, bx_=bx_: e.matmul(ps[:, bx_, 0:64], lhsT=wT[:, j, :], rhs=St[:], start=True, stop=True),
                          reads=[bwT, bS], writes=[bps[bx_]])
                    sc.op("tensor", lambda e, cs=cs, by_=by_: e.matmul(ps[:, by_, 192:256], lhsT=qnT[:, cs], rhs=St[:], start=True, stop=True),
                          reads=[bqn, bS], writes=[bps[by_]])
                    sc.op("vector", lambda e, j=j, bx_=bx_, i2=i2: e.tensor_tensor(out=vn[i2][:], in0=u_sb[:, j, :], in1=ps[:, bx_, 0:64],
                                                                                 op=ALU.subtract), reads=[bu, bps[bx_]], writes=[bvn[i2]])
                    sc.op("vector", lambda e, by_=by_, i2=i2, n=n: e.tensor_scalar(out=As[i2][:], in0=ps[:, by_, 192:256], scalar1=egc[:, n:n + 1],
                                                                                  scalar2=None, op0=ALU.mult), reads=[bps[by_], bg], writes=[bAs[i2]])
                    sc.op("tensor", lambda e, j=j, bx_=bx_, i2=i2: e.matmul(ps[:, bx_, 64:128], lhsT=qkT[:, j, :], rhs=vn[i2][:],
                                                                          start=True, stop=True), reads=[bqkT, bvn[i2]], writes=[bps[bx_]], inc=False)
                    sc.op("tensor", lambda e, j=j, bx_=bx_, i2=i2: e.matmul(ps[:, bx_, 128:192], lhsT=kdec[:, j, :], rhs=vn[i2][:],
                                                                          start=True, stop=True), reads=[bkdec, bvn[i2]], writes=[bps[bx_]])
                    sc.op("vector", lambda e, bx_=bx_, n=n: e.scalar_tensor_tensor(out=St[:], in0=St[:], scalar=eglb[:, n:n + 1],
                                                                                  in1=ps[:, bx_, 128:192], op0=ALU.mult, op1=ALU.add),
                          reads=[bS, bg, bps[bx_]], writes=[bS])
                    sc.op("vector", lambda e, bx_=bx_, i2=i2, l=l: e.tensor_tensor(out=oseg[:, l, :], in0=As[i2][:], in1=ps[:, bx_, 64:128],
                                                                                 op=ALU.add), reads=[bAs[i2], bps[bx_]], writes=[boseg])
                if STOP <= 4:
                    return [boseg, bS]
            sc.op("gpsimd", lambda e: e.tensor_tensor(out=osq[:], in0=oseg[:], in1=oseg[:], op=ALU.mult), reads=[boseg], writes=[bosq])
            sc.op("vector", lambda e: e.tensor_reduce(out=oss[:], in_=osq[:], axis=AX.X, op=ALU.add), reads=[bosq], writes=[boss])
            sc.op("scalar", lambda e: e.activation(out=oss[:], in_=oss[:], func=AF.Sqrt, bias=epsg[:, 0:1], scale=1.0 / 64),
                  reads=[boss, bcst], writes=[boss])
            sc.op("vector", lambda e: e.reciprocal(out=oss[:], in_=oss[:]), reads=[boss], writes=[boss])
            sc.op("vector", lambda e: e.tensor_tensor(out=osq[:], in0=oseg[:], in1=oss[:].unsqueeze(2).to_broadcast([64, SEGC, 64]),
                                                      op=ALU.mult), reads=[boseg, boss], writes=[bosq])
            sc.op("gpsimd", lambda e: e.tensor_tensor(out=osq[:], in0=osq[:], in1=ngb[:].unsqueeze(1).to_broadcast([64, SEGC, 64]),
                                                      op=ALU.mult), reads=[bosq, bpar], writes=[bosq])
            sc.op("scalar", lambda e: e.activation(out=zseg[:], in_=zseg[:], func=AF.Silu), reads=[bz], writes=[bz])
            if fz is None:
                sc.op("vector", lambda e: e.tensor_tensor(out=osq[:], in0=osq[:], in1=zseg[:], op=ALU.mult), reads=[bosq, bz], writes=[bosq])
                sc.dma("sync", lambda e, u=u, seg=seg: e.dma_start(
                    out=go_d[u, seg * SEGT:(seg + 1) * SEGT, :].rearrange("(n s) d -> s n d", s=64), in_=osq[:]),
                    reads=[bosq], key=pfx + "go")
            else:
                oT = oseg[:].rearrange("p a b -> p (a b)")
                zT = zseg[:].rearrange("p a b -> p (a b)")
                for g4 in range(SEGC // 8):
                    pt_ = nps()
                    for j in range(8):
                        sc.op("tensor", lambda e, j=j, g4=g4, pt_=pt_: e.transpose(ps[:, pt_, j * 64:(j + 1) * 64], osq[:, g4 * 8 + j, :], I64),
                              reads=[bosq, bcst], writes=[bps[pt_]], inc=(j == 7))
                    sc.op("vector", lambda e, g4=g4, pt_=pt_: e.tensor_tensor(out=oT[:, g4 * 512:(g4 + 1) * 512], in0=ps[:, pt_, :],
                                                                            in1=zT[:, g4 * 512:(g4 + 1) * 512], op=ALU.mult),
                          reads=[bps[pt_], bz, bosq], writes=[boseg])
                tok0 = seg * SEGT
                kblk, coff = tok0 // 1024, tok0 % 1024
                row0 = (kblk // 2) * 512 + 192 + (u * 2 + kblk % 2) * 64
                sc.dma("sync", lambda e, row0=row0, coff=coff: e.dma_start(out=fz.Ysend[row0:row0 + 64, coff:coff + SEGT],
                                                                          in_=oT[:, 0:SEGT]),
                       reads=[boseg, fz.bYs], key=pfx + f"go{u}")
        return [bosq, boseg]

    class _Rec:
        def __init__(self):
            self.calls = []

        def op(self, *a, **k):
            self.calls.append(("op", a, k))

        def dma(self, *a, **k):
            self.calls.append(("dma", a, k))

    outs = []
    recs = []
    for u in range(nu):
        r = _Rec()
        outs += emit_unit(u, r)
        recs.append(r.calls)
    n = max(len(c) for c in recs)
    for i in range(n):
        for c in recs:
            if i < len(c):
                kind, a, k = c[i]
                getattr(sc, kind)(*a, **k)
    return outs


def build_gdn(nu=2):
    P = SimpleProg()
    sc = Sched(P.nc, P.es)
    ob = gdn_emit(P, sc, nu)
    return P, P.finish(sc, ob)


def gdn_const_inputs():
    i = np.arange(64)
    tri = (i[:, None] <= i[None, :]).astype(np.float32)
    ms = (i[:, None] > i[None, :]).astype(np.float32)
    return dict(gcst=np.stack([tri, ms, np.eye(64, dtype=np.float32)]))


def gdn_unit_inputs(ug, h, gdn_conv_w, a_log, dt_bias, norm_g):
    GW = 384
    raw = np.zeros((3, 64, S + 3), np.float32)
    cw = np.zeros((64, 12), np.float32)
    for j in range(3):
        cols = slice(j * GW + h * 64, j * GW + (h + 1) * 64)
        raw[j, :, 3:] = ug[:, cols].T
        cw[:, j * 4:(j + 1) * 4] = gdn_conv_w[:, cols].T
    z = np.ascontiguousarray(ug[:, 3 * GW + h * 64:3 * GW + (h + 1) * 64])
    a = np.ascontiguousarray(ug[:, 4 * GW + h].reshape(NCH, 64).T)
    b = np.ascontiguousarray(ug[:, 4 * GW + 6 + h].reshape(NCH, 64).T)
    par = np.zeros((64, 2), np.float32)
    par[:, 0] = a_log[h]
    par[:, 1] = dt_bias[h]
    ng = np.ascontiguousarray(np.broadcast_to(norm_g[None, :], (64, 64))).astype(np.float32)
    return dict(graw=raw, gcw=cw, gz=z, ga=a, gb=b, gpar=par, gng=ng)


def _lay(v):
    return np.ascontiguousarray(np.asarray(v, np.float32).reshape(-1, 128).T)


_PROGS = {}


def _prog(key, builder):
    if key not in _PROGS:
        _PROGS[key] = builder()
    return _PROGS[key]


def _run(nc, in_maps):
    res = run_bass_kernel_spmd(nc, in_maps, core_ids=list(range(NCORES)))
    return res.results


def _tok_launch(key, stages, inp, xT_list, yT_list=None):
    def mk():
        p = TokProg(stages)
        return p, p.build()
    P, nc = _prog(key, mk)
    maps = []
    for c in range(NCORES):
        b = c // 4
        m = {"xT": xT_list[c], "cT": _lay(inp["c"][b])}
        for name in P.in_names:
            if name in m:
                continue
            if name.startswith("yT"):
                m[name] = yT_list[c]
            elif name == "final_g":
                m[name] = _lay(inp["final_g"])
            else:
                base, l = name[:-1], int(name[-1])
                arr = np.asarray(inp[base][l], np.float32)
                if base == "b_ada" or base.startswith("ln_"):
                    arr = _lay(arr)
                m[name] = np.ascontiguousarray(arr)
        maps.append(m)
    return _run(nc, maps)


def _mixer(inp, l, u):
    y = np.zeros((B, S, D), np.float32)
    P, nc = _prog("conv", build_conv)
    maps = []
    for c in range(NCORES):
        b, j = c // 4, c % 4
        maps.append(conv_inputs(u[b], j, np.asarray(inp["conv_w"][l]), np.asarray(inp["conv_b"][l]),
                                np.asarray(inp["conv_ln_g"][l]), np.asarray(inp["conv_ln_b"][l])))
    res = _run(nc, maps)
    for c in range(NCORES):
        b, j = c // 4, c % 4
        y[b, j * TOK:(j + 1) * TOK, 0:256] = res[c]["ycT"].T
    P, nc = _prog("moba", lambda: build_moba(3))
    tab = rope_tables()
    shared = moba_shared_inputs(tab)
    consts = [moba_const_inputs(0), moba_const_inputs(1)]
    maps = []
    for c in range(NCORES):
        b, cc = c // 4, c % 4
        units = []
        for s in range(3):
            combo = 3 * cc + s
            h, half = combo // 2, combo % 2
            q = u[b, :, 512 + h * 64:512 + (h + 1) * 64]
            k = u[b, :, 512 + 384 + h * 64:512 + 384 + (h + 1) * 64]
            v = u[b, :, 512 + 768 + h * 64:512 + 768 + (h + 1) * 64]
            d = moba_unit_inputs(q, k, v, half, tab)
            d.update(consts[half])
            units.append(d)
        m = {k_: np.ascontiguousarray(np.stack([un[k_] for un in units])) for k_ in units[0]}
        m.update(shared)
        maps.append(m)
    res = _run(nc, maps)
    for c in range(NCORES):
        b, cc = c // 4, c % 4
        for s in range(3):
            combo = 3 * cc + s
            h, half = combo // 2, combo % 2
            qpos = np.concatenate([np.arange(bl * 256, (bl + 1) * 256) for bl in HALF_BLOCKS[half]])
            y[b, qpos, 256 + h * 64:256 + (h + 1) * 64] = res[c]["moT"][s].T
    P, nc = _prog("gdn", lambda: build_gdn(2))
    gconst = gdn_const_inputs()
    allu = [(b, h) for b in range(B) for h in range(6)]
    maps = []
    assign = []
    for c in range(NCORES):
        us = [allu[i] if i < len(allu) else allu[0] for i in (2 * c, 2 * c + 1)]
        assign.append([(i < len(allu)) for i in (2 * c, 2 * c + 1)])
        units = [gdn_unit_inputs(u[b, :, 512 + 1152:], h, np.asarray(inp["gdn_conv_w"][l]), np.asarray(inp["gdn_a_log"][l]),
                                 np.asarray(inp["gdn_dt_bias"][l]), np.asarray(inp["gdn_norm_g"][l])) for (b, h) in us]
        m = {k_: np.ascontiguousarray(np.stack([un[k_] for un in units])) for k_ in units[0]}
        m.update(gconst)
        maps.append(m)
    res = _run(nc, maps)
    for c in range(NCORES):
        for s in range(2):
            i = 2 * c + s
            if i < len(allu):
                b, h = allu[i]
                y[b, :, 640 + h * 64:640 + (h + 1) * 64] = res[c]["go"][s]
    return y


YROWS = 768 + 1024
RG = [[0, 1, 2, 3], [4, 5, 6, 7]]


def moba_unit(cc, su):
    return (cc, su) if su < 2 else (4 + cc // 2, cc % 2)


def moba_owner(h, half):
    return (h, half) if h < 4 else (2 * (h - 4) + half, 2)


class Fused:
    def __init__(self):
        self.nc = bass.Bass("TRN2", target_bir_lowering=False)
        self.es = ExitStack()
        self.cur = self.es
        self.dins = {}
        self.in_names = []
        self.out_names = []
        self.phase_i = 0
        self.load_x = False
        self.store_x = False
        self._dyn = {}

    def din(self, name, shape, dt=F32):
        if name not in self.dins:
            self.in_names.append(name)
            self.dins[name] = self.nc.dram_tensor(name, list(shape), dt, kind="ExternalInput").ap()
        return self.dins[name]

    def dout(self, name, shape, dt=F32):
        if name not in self.dins:
            self.out_names.append(name)
            self.dins[name] = self.nc.dram_tensor(name, list(shape), dt, kind="ExternalOutput").ap()
        return self.dins[name]

    AW = 36800

    def sb(self, name, shape, dt):
        p = shape[0]
        n = int(np.prod(shape[1:]))
        n32 = n if dt == F32 else (n + 1) // 2
        n32 = (n32 + 7) // 8 * 8
        off = self.aoff
        self.aoff += n32
        assert self.aoff <= self.AW, (name, self.aoff)
        v = self.arena[0:p, off:off + n32]
        if dt != F32:
            v = v.bitcast(dt)
        v = v[:, 0:n]
        if len(shape) == 3:
            v = v.rearrange("p (a b) -> p a b", a=shape[1])
        return v

    def dyn(self, e, engname, key):
        c = self._dyn.setdefault(engname, {})
        if "cc" not in c:
            c["cc"] = e.snap(e.partition_id() % 4)
        if key not in c:
            cc = c["cc"]
            doff = lambda h: (h // 2) * 512 + (h % 2) * 64
            v = {"c2048": lambda: cc * 2048, "prev": lambda: (cc + 3) % 4,
                 "D0": lambda: doff(cc), "D2": lambda: (cc // 2) * 64 + 1024, "mha2": lambda: cc % 2, "mhb2": lambda: 3 - cc % 2,
                 "gh1": lambda: (cc + 4) % 6, "Dg1": lambda: doff((cc + 4) % 6)}[key]()
            c[key] = e.snap(v)
        return c[key]

    def build(self):
        nc, es = self.nc, self.es
        sc = self.sc = Sched(nc, es)
        self.x = es.enter_context(nc.sbuf_tensor("x_res", [128, KC, TOK], F32))
        self.arena = es.enter_context(nc.sbuf_tensor("arena", [128, self.AW], F32))
        self.aoff = 0
        self.bx = [[Buf(f"x{c}_{t}") for t in range(TOK // 512)] for c in range(KC)]
        self.ps = es.enter_context(nc.psum_tensor("ps_all", [128, 8, 512], F32))
        NUC = (DIN + 127) // 128
        Usend_t = nc.dram_tensor("Usend", [NUC * 128, TOK], F32)
        Urecv_t = nc.dram_tensor("Urecv", [NUC * 512 + 128, TOK], F32)
        Ysend_t = nc.dram_tensor("Ysend", [2048, 1024], F32)
        Yrecv_t = nc.dram_tensor("Yrecv", [8192, 1024], F32)
        Yfull_t = nc.dram_tensor("Yfull", [D, TOK], F32)
        self.Usend, self.Urecv, self.Ysend, self.Yrecv, self.Yfull = (t.ap() for t in (Usend_t, Urecv_t, Ysend_t, Yrecv_t, Yfull_t))
        self.LK = nc.dram_tensor("LK", [3, 64, S], F32).ap()
        self.LV = nc.dram_tensor("LV", [3, 64, S], F32).ap()
        self.LQ = nc.dram_tensor("LQ", [3, 64, MOBA_SLOTS * 256], F32).ap()
        self.LQF = nc.dram_tensor("LQF", [3, 64, S], F32).ap()
        self.LG = nc.dram_tensor("LG", [2, 3, 64, S], F32).ap()
        self.LZ = nc.dram_tensor("LZ", [2, 64, S], F32).ap()
        self.LAB = nc.dram_tensor("LAB", [2, 2, S], F32).ap()
        self.LH = nc.dram_tensor("LH", [512, CH], F32).ap()
        self.Yloc = nc.dram_tensor("Yloc", [4, 512, 1024], F32).ap()
        self.uT_dst = self.Usend
        self.yT_src = self.Yfull
        self.bU, self.bUr, self.bYs, self.bYr, self.bY = Buf("U"), Buf("Ur"), Buf("Ys"), Buf("Yr"), Buf("Yf")
        self.bL, self.bLq, self.bYl = Buf("L"), Buf("Lq"), Buf("Yl")
        self.bUr_m, self.bUr_g = Buf("Ur_m"), Buf("Ur_g")
        outb = []

        def run_phase(fn):
            self.aoff = 0
            r = fn()
            sc.barrier(exclude=("agU",))
            self.phase_i += 1
            return r

        def tok_phase(stages, load_x=False, store_x=False, ag=True):
            def fn():
                self.load_x, self.store_x = load_x, store_x
                r = TokProg(stages, fused=self).build()
                if ag:
                    order = list(range(4, 13)) + list(range(13, NUC)) + list(range(0, 4))
                    for ci in order:
                        bdst = self.bUr_m if 4 <= ci < 13 else self.bUr_g
                        sc.cc(lambda e, ci=ci: e.collective_compute(
                            "AllGather", ALU.bypass, replica_groups=RG,
                            ins=[Usend_t.ap()[ci * 128:(ci + 1) * 128, :]], outs=[Urecv_t.ap()[ci * 512:(ci + 1) * 512, :]]),
                            writes=[self.bU, bdst], key="agU")
                return r
            return run_phase(fn)

        def y_exchange():
            for ci in range(8):
                sc.cc(lambda e, ci=ci: e.collective_compute(
                    "AllGather", ALU.bypass, replica_groups=RG,
                    ins=[Ysend_t.ap()[ci * 256:(ci + 1) * 256, :]], outs=[Yrecv_t.ap()[ci * 1024:(ci + 1) * 1024, :]]),
                    writes=[self.bYs, self.bYr], key="agY")
            LB = ([0, 3, 4, 7], [1, 2, 5, 6])
            for c2 in range(2):
                sc.dma("scalar", lambda e, c2=c2: e.dma_start(
                    out=self.Yloc[:, c2 * 256:(c2 + 1) * 256, :],
                    in_=self.Yrecv[c2 * 1024:c2 * 1024 + 7168, :][bass.ds(self.dyn(e, "scalar", "c2048"), 1024), :].rearrange(
                        "(r f) t -> r f t", r=4)),
                    reads=[self.bYr], writes=[self.bYl], key="yloc")
            for h in range(6):
                for half in range(2):
                    rs, su = moba_owner(h, half)
                    for q4 in range(4):
                        lb = LB[half][q4]
                        sc.dma("sync", lambda e, h=h, lb=lb, rs=rs, su=su, q4=q4: e.dma_start(
                            out=self.Yfull[256 + h * 64:256 + (h + 1) * 64, lb * 256:(lb + 1) * 256],
                            in_=self.Yloc[rs, su * 64:(su + 1) * 64, q4 * 256:(q4 + 1) * 256]),
                            reads=[self.bYl, self.bY], key="yasm")
            for h in range(6):
                rs, g = (h, 0) if h < 4 else (h - 4, 1)
                for kk in range(2):
                    r0 = 192 + (g * 2 + kk) * 64
                    sc.dma("sync", lambda e, h=h, kk=kk, rs=rs, r0=r0: e.dma_start(
                        out=self.Yfull[640 + h * 64:640 + (h + 1) * 64, kk * 1024:(kk + 1) * 1024],
                        in_=self.Yloc[rs, r0:r0 + 64, :]),
                        reads=[self.bYl, self.bY], key="yasm")

        def localize_m():
            Ur = self.Urecv
            rk = lambda ap: ap.rearrange("d (r t) -> d r t", r=4)
            LQF = self.LQF

            def blk(e, q, dkey, B, n=64):
                R0 = (B // 128) * 512 + B % 128
                R1 = min(R0 + 2048, NUC * 512 + 128)
                return Ur[R0:R1, :][bass.ds(self.dyn(e, q, dkey), 512), :].rearrange("(r f) t -> f r t", r=4)[0:n]

            for u in range(3):
                q = "sync" if u < 2 else "scalar"
                dk = "D0" if u < 2 else "D2"
                for (dst, B) in ((self.LK, 896), (self.LV, 1280), (LQF, 512)):
                    sc.dma(q, lambda e, u=u, q=q, dk=dk, dst=dst, B=B: e.dma_start(out=rk(dst[u]), in_=blk(e, q, dk, B)),
                           reads=[self.bUr_m], writes=[self.bL], key=f"loc{q}{u}")
                for ab in range(2):
                    dstq = self.LQ[u].rearrange("d (G ab i) -> d G ab i", G=8, ab=2)[:, :, ab:ab + 1, :]
                    srcv = LQF[u].rearrange("d (G b i) -> d G b i", G=8, b=4)
                    if u < 2:
                        b = u if ab == 0 else 3 - u
                        sc.dma(q, lambda e, dstq=dstq, srcv=srcv, b=b: e.dma_start(out=dstq, in_=srcv[:, :, b:b + 1, :]),
                               reads=[self.bL], writes=[self.bLq], key=f"locq{u}")
                    else:
                        kn = "mha2" if ab == 0 else "mhb2"
                        sc.dma(q, lambda e, dstq=dstq, srcv=srcv, kn=kn: e.dma_start(
                            out=dstq, in_=srcv[:, :, bass.ds(self.dyn(e, "scalar", kn), 1), :]),
                            reads=[self.bL], writes=[self.bLq], key=f"locq{u}")

        def localize_g():
            Ur = self.Urecv
            rk = lambda ap: ap.rearrange("d (r t) -> d r t", r=4)
            LQF = self.LQF

            def blk(e, q, dkey, B, n=64):
                R0 = (B // 128) * 512 + B % 128
                R1 = min(R0 + 2048, NUC * 512 + 128)
                return Ur[R0:R1, :][bass.ds(self.dyn(e, q, dkey), 512), :].rearrange("(r f) t -> f r t", r=4)[0:n]

            for u in range(2):
                dk, gk = ("D0", "cc") if u == 0 else ("Dg1", "gh1")
                for j in range(3):
                    sc.dma("gpsimd", lambda e, u=u, j=j, dk=dk: e.dma_start(
                        out=rk(self.LG[u, j]), in_=blk(e, "gpsimd", dk, 1664 + j * 384)),
                        reads=[self.bUr_g], writes=[self.bL], key="locg")
                q2 = "gpsimd" if u == 0 else "sync"
                sc.dma(q2, lambda e, u=u, dk=dk, q2=q2: e.dma_start(out=rk(self.LZ[u]), in_=blk(e, q2, dk, 2816)),
                       reads=[self.bUr_g], writes=[self.bL], key=f"locz{u}")
                for ab, B in ((0, 3200), (1, 3206)):
                    sc.dma(q2, lambda e, u=u, ab=ab, B=B, gk=gk, q2=q2: e.dma_start(
                        out=self.LAB[u, ab:ab + 1].rearrange("o (r t) -> o r t", r=4), in_=blk(e, q2, gk, B, 1)),
                        reads=[self.bUr_g], writes=[self.bL], key=f"locz{u}")
            sc.dma("scalar", lambda e: e.dma_start(
                out=self.LH.rearrange("(c f) t -> c f t", c=4),
                in_=Ur[0:2048, TOK - CH:TOK].rearrange("(c r f) t -> r c f t", r=4, f=128)[bass.ds(self.dyn(e, "scalar", "prev"), 1)]),
                reads=[self.bUr_g], writes=[self.bL], key="loch")

        def mixer(l):
            pfx = f"L{l}_"
            run_phase(localize_m)
            run_phase(lambda: moba_emit(self, sc, 3, pfx, fz=self))
            run_phase(localize_g)
            run_phase(lambda: conv_emit(self, sc, pfx, fz=self))

            def g():
                gdn_emit(self, sc, 2, pfx, fz=self)
                y_exchange()
            run_phase(g)

        tok_phase([("ffn1", 0), ("uproj", 0)], load_x=True)
        mixer(0)
        tok_phase([("wout", 0), ("ffn2", 0), ("ffn1", 1), ("uproj", 1)])
        mixer(1)
        outb = tok_phase([("wout", 1), ("ffn2", 1), ("final",)], store_x=True, ag=False)
        with nc.Block() as block:
            sc.emit(block)
        es.close()
        return nc


_FUSED = {}


def kernel(**inp):
    if "p" not in _FUSED:
        F = Fused()
        _FUSED["p"] = (F, F.build())
    F, nc = _FUSED["p"]
    x = np.asarray(inp["x"], np.float32)
    tab = rope_tables()
    shared = moba_shared_inputs(tab)
    mconst = [moba_const_inputs(0), moba_const_inputs(1)]
    gconst = gdn_const_inputs()
    wnames = ("w_ada", "ffn1_w_gate", "ffn1_w_up", "ffn1_w_down", "w_in", "w_out", "ffn2_w_gate", "ffn2_w_up", "ffn2_w_down")
    lnames = ("b_ada", "ln_ffn1_g", "ln_mix_g", "ln_ffn2_g")
    common = {}
    for l in range(2):
        for n in wnames:
            common[f"{n}{l}"] = np.ascontiguousarray(np.asarray(inp[n][l], np.float32))
        for n in lnames:
            common[f"{n}{l}"] = _lay(inp[n][l])
        cw = np.asarray(inp["conv_w"][l], np.float32)
        lay2 = lambda v: np.ascontiguousarray(np.asarray(v, np.float32).reshape(2, 128).T)
        common[f"L{l}_cw"] = np.ascontiguousarray(cw.T.reshape(2, 128, 31).transpose(1, 0, 2))
        common[f"L{l}_cp"] = np.ascontiguousarray(np.stack([lay2(inp["conv_b"][l]), lay2(inp["conv_ln_g"][l]),
                                                            lay2(inp["conv_ln_b"][l])], axis=-1))
    common["final_g"] = _lay(inp["final_g"])
    common["cident"] = np.eye(128, dtype=np.float32)
    common.update(shared)
    common.update(gconst)
    maps = []
    for c in range(NCORES):
        b, cc = c // 4, c % 4
        m = dict(common)
        m["xT"] = np.ascontiguousarray(x[b, cc * TOK:(cc + 1) * TOK].T)
        m["cT"] = _lay(inp["c"][b])
        m["cflag"] = np.full((128, 1), 0.0 if cc == 0 else 1.0, np.float32)
        units = []
        for su in range(3):
            half = moba_unit(cc, su)[1]
            qpos = np.concatenate([np.arange(bl * 256, (bl + 1) * 256) for bl in HALF_BLOCKS[half]])
            d = dict(mconst[half])
            d["ropeq"] = np.ascontiguousarray(tab[:, :, qpos])
            units.append(d)
        for k_ in units[0]:
            m[k_] = np.ascontiguousarray(np.stack([un[k_] for un in units]))
        for l in range(2):
            heads = [cc, (cc + 4) % 6]
            gw = np.asarray(inp["gdn_conv_w"][l], np.float32)
            gcw = np.zeros((2, 64, 12), np.float32)
            gpar = np.zeros((2, 64, 2), np.float32)
            gng = np.zeros((2, 64, 64), np.float32)
            for g, h in enumerate(heads):
                for j in range(3):
                    gcw[g, :, j * 4:(j + 1) * 4] = gw[:, j * 384 + h * 64:j * 384 + (h + 1) * 64].T
                gpar[g, :, 0] = np.asarray(inp["gdn_a_log"][l], np.float32)[h]
                gpar[g, :, 1] = np.asarray(inp["gdn_dt_bias"][l], np.float32)[h]
                gng[g] = np.asarray(inp["gdn_norm_g"][l], np.float32)[None, :]
            m[f"L{l}_gcw"], m[f"L{l}_gpar"], m[f"L{l}_gng"] = gcw, gpar, gng
        maps.append({k_: m[k_] for k_ in F.in_names})
    res = run_bass_kernel_spmd(nc, maps, core_ids=list(range(NCORES))).results
    out = np.zeros((B, S, D), np.float32)
    for c in range(NCORES):
        out[c // 4, (c % 4) * TOK:(c % 4 + 1) * TOK] = res[c]["xoT"].T
    return out
```

```python
import numpy as np
from contextlib import ExitStack
import concourse.bass as bass
import concourse.mybir as mybir
from concourse.bass_utils import run_bass_kernel_spmd

F32 = mybir.dt.float32
BF16 = mybir.dt.bfloat16
AF = mybir.ActivationFunctionType
ALU = mybir.AluOpType

D = 1024
KC = 8
DFF = 2816
FC = 22
DIN = 3212
B = 2
S = 8192
NCORES = 8
TOK = 2048
EPS = 1e-6

SAME_ENG_SYNC = True


class Buf:
    __slots__ = ("name", "lw", "rd")

    def __init__(self, name=""):
        self.name = name
        self.lw = None
        self.rd = {}


class Sched:
    ENGS = ("tensor", "vector", "scalar", "gpsimd", "sync")
    EPOCH = 20000

    def __init__(self, nc, es):
        self.nc = nc
        self.es = es
        self.q = {e: [] for e in self.ENGS}
        self.cnt = {e: 0 for e in self.ENGS}
        self.seen = {e: {} for e in self.ENGS}
        self.esem = {}
        self.dsem = {}
        self.dcnt = {}
        self.cckeys = set()

    def _get_esem(self, eng, epoch):
        k = (eng, epoch)
        if k not in self.esem:
            self.esem[k] = self.es.enter_context(self.nc.semaphore(f"se_{eng}_{epoch}"))
        return self.esem[k]

    def _get_dsem(self, key):
        if key not in self.dsem:
            self.dsem[key] = self.es.enter_context(self.nc.semaphore(f"sd_{key}"))
            self.dcnt[key] = 0
        return self.dsem[key]

    def _need(self, eng, tok, waits):
        if tok is None:
            return
        kind, k, val = tok
        if kind == "e":
            if k == eng and (eng == "tensor" or not SAME_ENG_SYNC):
                return
        key = (kind, k)
        if self.seen[eng].get(key, 0) >= val:
            return
        self.seen[eng][key] = val
        waits.append(tok)

    def _deps(self, eng, reads, writes):
        waits = []
        for b in reads:
            self._need(eng, b.lw, waits)
        for b in writes:
            self._need(eng, b.lw, waits)
            for k, v in b.rd.items():
                self._need(eng, (k[0], k[1], v), waits)
        return waits

    def _mark(self, tok, reads, writes):
        key = (tok[0], tok[1])
        for b in reads:
            if b.rd.get(key, 0) < tok[2]:
                b.rd[key] = tok[2]
        for b in writes:
            b.lw = tok
            b.rd = {}

    def op(self, eng, fn, reads=(), writes=(), inc=True):
        waits = self._deps(eng, reads, writes)
        idx = self.cnt[eng] + 1
        if inc:
            self.cnt[eng] = idx
        tok = ("e", eng, idx)
        self._mark(tok, reads, writes)
        self.q[eng].append((waits, fn, tok if inc else None))

    def dma(self, qeng, fn, reads=(), writes=(), key="d"):
        waits = self._deps(qeng, reads, writes)
        self._get_dsem(key)
        self.dcnt[key] += 1
        tok = ("d", key, 16 * self.dcnt[key])
        self._mark(tok, reads, writes)
        self.q[qeng].append((waits, fn, tok))

    def cc(self, fn, reads=(), writes=(), key="cc"):
        waits = self._deps("gpsimd", reads, writes)
        self._get_dsem(key)
        self.cckeys.add(key)
        self.dcnt[key] += 1
        tok = ("c", key, self.dcnt[key])
        self._mark(tok, reads, writes)
        self.q["gpsimd"].append((waits, fn, tok))

    def barrier(self, exclude=()):
        for e in self.ENGS:
            waits = []
            for e2 in self.ENGS:
                if e2 != e and self.cnt[e2] > 0:
                    self._need(e, ("e", e2, self.cnt[e2]), waits)
            for key, n in self.dcnt.items():
                if n > 0 and key not in exclude:
                    kind = "c" if key in self.cckeys else "d"
                    self._need(e, (kind, key, n if kind == "c" else 16 * n), waits)
            self.q[e].append((waits, None, None))

    def final_wait(self, eng, toks_bufs):
        waits = self._deps(eng, (), toks_bufs)
        self.q[eng].append((waits, None, None))

    def emit(self, block):
        nc = self.nc

        def run(engname):
            def body(eng):
                for waits, fn, tok in self.q[engname]:
                    for (kind, k, val) in waits:
                        if kind == "e":
                            epoch = (val - 1) // self.EPOCH
                            eng.wait_ge(self._get_esem(k, epoch), val - epoch * self.EPOCH)
                        else:
                            eng.wait_ge(self.dsem[k], val)
                    if fn is None:
                        continue
                    ins = fn(eng)
                    if tok is not None:
                        if tok[0] == "e":
                            epoch = (tok[2] - 1) // self.EPOCH
                            ins.then_inc(self._get_esem(tok[1], epoch), 1)
                        elif tok[0] == "c":
                            ins.then_inc(self.dsem[tok[1]])
                        else:
                            ins.then_inc(self.dsem[tok[1]], 16)
                self.q[engname] = []
            return body

        for e in self.ENGS:
            for ep in range((self.cnt[e] - 1) // self.EPOCH + 1 if self.cnt[e] else 0):
                self._get_esem(e, ep)
        block.tensor(run("tensor"))
        block.vector(run("vector"))
        block.scalar(run("scalar"))
        block.gpsimd(run("gpsimd"))
        block.sync(run("sync"))


class TokProg:
    def __init__(self, stages, tok=TOK, fused=None):
        self.stages = stages
        self.tok = tok
        self.fused = fused
        if fused is None:
            self.nc = bass.Bass("TRN2", target_bir_lowering=False)
            self.es = ExitStack()
        else:
            self.nc = fused.nc
            self.es = fused.es
        self.in_names = []
        self.out_names = []

    def din(self, name, shape, dt=F32):
        if self.fused is not None:
            return self.fused.din(name, shape, dt)
        self.in_names.append(name)
        return self.nc.dram_tensor(name, list(shape), dt, kind="ExternalInput").ap()

    def dout(self, name, shape, dt=F32):
        if self.fused is not None:
            return self.fused.dout(name, shape, dt)
        self.out_names.append(name)
        return self.nc.dram_tensor(name, list(shape), dt, kind="ExternalOutput").ap()

    def sb(self, name, shape, dt):
        if self.fused is not None:
            return self.fused.sb(name, shape, dt)
        return self.es.enter_context(self.nc.sbuf_tensor(name, list(shape), dt))

    def build(self):
        nc, es = self.nc, self.es
        fz = self.fused
        T = self.tok
        NH = T // 1024
        stages = self.stages
        layers = sorted({s[1] for s in stages if len(s) > 1})
        need_v = {}
        for s in stages:
            if s[0] == "ffn1":
                need_v.setdefault(s[1], set()).update([0, 1, 2])
            elif s[0] == "uproj":
                need_v.setdefault(s[1], set()).update([3, 4])
            elif s[0] == "wout":
                need_v.setdefault(s[1], set()).update([5])
            elif s[0] == "ffn2":
                need_v.setdefault(s[1], set()).update([6, 7, 8])

        xT_d = self.din("xT", [D, T]) if (fz is None or fz.load_x) else None
        cT_d = self.din("cT", [128, KC])
        W = {}
        for l in layers:
            W[("w_ada", l)] = self.din(f"w_ada{l}", [D, 9 * D])
            W[("b_ada", l)] = self.din(f"b_ada{l}", [128, 72])
        for s in stages:
            if s[0] in ("ffn1", "ffn2"):
                l = s[1]
                n = s[0]
                W[(n + "_g", l)] = self.din(f"ln_{n}_g{l}", [128, KC])
                W[(n + "_wg", l)] = self.din(f"{n}_w_gate{l}", [D, DFF])
                W[(n + "_wu", l)] = self.din(f"{n}_w_up{l}", [D, DFF])
                W[(n + "_wd", l)] = self.din(f"{n}_w_down{l}", [DFF, D])
            elif s[0] == "uproj":
                l = s[1]
                W[("mix_g", l)] = self.din(f"ln_mix_g{l}", [128, KC])
                W[("w_in", l)] = self.din(f"w_in{l}", [D, DIN])
                W[("uT", l)] = self.dout(f"uT{l}", [DIN, T]) if fz is None else fz.uT_dst
            elif s[0] == "wout":
                l = s[1]
                W[("w_out", l)] = self.din(f"w_out{l}", [D, D])
                W[("yT", l)] = self.din(f"yT{l}", [D, T]) if fz is None else fz.yT_src
            elif s[0] == "final":
                W[("final_g",)] = self.din("final_g", [128, KC])
        xo_d = self.dout("xoT", [D, T]) if (fz is None or fz.store_x) else None

        x = self.sb("x", [128, KC, T], F32) if fz is None else fz.x
        h = self.sb("h", [128, KC, 1024], BF16)
        act = self.sb("act", [128, FC, 1024], BF16)
        wd = self.sb("wd", [128, FC, D], BF16)
        NSLOT = 4
        SLOTW = 256
        wslot = [self.sb(f"ws{i}", [128, KC, SLOTW], BF16) for i in range(NSLOT)]
        tmpA = [self.sb(f"tmpA{i}", [128, 512], F32) for i in range(2)]
        tmpB = [self.sb(f"tmpB{i}", [128, 512], F32) for i in range(2)]
        sqb = [self.sb(f"sq{i}", [128, 512], BF16) for i in range(2)]
        rstd = self.sb("rstd", [128, 512], F32)
        ones = self.sb("ones", [128, 128], BF16)
        cT = self.sb("cT_sb", [128, KC], F32)
        cact = self.sb("cact", [128, KC], BF16)
        bada = {l: self.sb(f"bada{l}", [128, 72], F32) for l in layers}
        mod = {l: self.sb(f"mod{l}", [128, 72], F32) for l in layers}
        gains = {}
        for k in W:
            if k[0] in ("ffn1_g", "ffn2_g", "mix_g", "final_g"):
                gains[k] = self.sb("g_" + "_".join(map(str, k)), [128, KC], F32)
        coefA = {}
        coefG = {}
        ps = es.enter_context(nc.psum_tensor("ps", [128, 8, 512], F32)) if fz is None else fz.ps

        sc = Sched(nc, es) if fz is None else fz.sc
        bx = [[Buf(f"x{c}_{t}") for t in range(T // 512)] for c in range(KC)] if fz is None else fz.bx
        bU = [] if fz is None else [fz.bU]
        bY = [] if fz is None else [fz.bY]
        bh = [Buf(f"h{t}") for t in range(2)]
        bact = [[Buf(f"act{f}_{t}") for t in range(2)] for f in range(FC)]
        WD_PIECES = ((0, 6), (6, 12), (12, 17), (17, 22))
        bwd = [Buf(f"wd{i}") for i in range(4)]
        wd_piece = {}
        for i, (f0, f1) in enumerate(WD_PIECES):
            for f in range(f0, f1):
                wd_piece[f] = i
        bws = [Buf(f"ws{i}") for i in range(NSLOT)]
        btA = [Buf() for _ in range(2)]
        btB = [Buf() for _ in range(2)]
        bsq = [Buf() for _ in range(2)]
        brstd = Buf()
        bones = Buf()
        bps = [Buf(f"ps{i}") for i in range(8)]
        bmisc = Buf("misc")
        bmod = Buf("mod")

        if xT_d is not None:
            xT_v = xT_d.rearrange("(c p) t -> p c t", p=128)
            for c in range(KC):
                sc.dma("sync", lambda e, c=c: e.dma_start(out=x[:, c, :], in_=xT_v[:, c, :]),
                       writes=bx[c], key=f"x{c}")
        sc.dma("sync", lambda e: e.dma_start(out=cT[:], in_=cT_d[:, :]), writes=[bmisc], key="misc")
        for l in layers:
            sc.dma("sync", lambda e, l=l: e.dma_start(out=bada[l][:], in_=W[("b_ada", l)][:, :]),
                   writes=[bmisc], key="misc")
        for k, t in gains.items():
            sc.dma("sync", lambda e, k=k, t=t: e.dma_start(out=t[:], in_=W[k][:, :]), writes=[bmisc], key="misc")
        sc.op("vector", lambda e: e.memset(ones[:], 1.0), writes=[bones])
        sc.op("scalar", lambda e: e.activation(out=cact[:], in_=cT[:], func=AF.Silu), reads=[bmisc], writes=[bmod])

        wslot_i = [0]

        def next_slot():
            i = wslot_i[0] % NSLOT
            wslot_i[0] += 1
            return i

        def load_cols(Wd, c0, ncols, nk=KC):
            i = next_slot()
            src = Wd.rearrange("(k p) n -> p k n", p=128)
            sc.dma("gpsimd", lambda e, i=i: e.dma_start(out=wslot[i][:, 0:nk, 0:ncols], in_=src[:, :, c0:c0 + ncols]),
                   writes=[bws[i]], key=f"ws{i}")
            return i

        mod_ps = ps[:, 7, 0:72]
        for l in layers:
            for v in sorted(need_v[l]):
                for hh in range(4):
                    si = load_cols(W[("w_ada", l)], v * 1024 + hh * 256, 256)
                    for jj in range(2):
                        j = hh * 2 + jj
                        col = v * 8 + j
                        for kc in range(KC):
                            sc.op("tensor",
                                  lambda e, si=si, jj=jj, kc=kc, col=col: e.matmul(
                                      ps[:, 7, col:col + 1], lhsT=wslot[si][:, kc, jj * 128:(jj + 1) * 128],
                                      rhs=cact[:, kc:kc + 1], start=(kc == 0), stop=(kc == KC - 1)),
                                  reads=[bws[si], bmod], writes=[bps[7]], inc=(kc == KC - 1))
            sc.op("vector", lambda e, l=l: e.tensor_tensor(out=mod[l][:], in0=mod_ps, in1=bada[l][:], op=ALU.add),
                  reads=[bps[7], bmisc], writes=[bmod])
            for (gk, vs, vg, half) in ((("ffn1_g", l), 1, 2, 0.5), (("mix_g", l), 4, None, None),
                                       (("ffn2_g", l), 7, 8, 0.5)):
                if gk in gains:
                    a = self.sb("cA_" + "_".join(map(str, gk)), [128, KC], F32)
                    coefA[gk] = a
                    sc.op("vector", lambda e, a=a, gk=gk, vs=vs, l=l: e.scalar_tensor_tensor(
                        out=a[:], in0=mod[l][:, vs * 8:vs * 8 + 8], scalar=1.0, in1=gains[gk][:],
                        op0=ALU.add, op1=ALU.mult), reads=[bmod, bmisc], writes=[bmod])
                    if vg is not None:
                        g = self.sb("cG_" + "_".join(map(str, gk)), [128, KC], F32)
                        coefG[gk] = g
                        sc.op("vector", lambda e, g=g, vg=vg, l=l: e.tensor_scalar(
                            out=g[:], in0=mod[l][:, vg * 8:vg * 8 + 8], scalar1=0.5, scalar2=None, op0=ALU.mult),
                            reads=[bmod], writes=[bmod])

        psi = [0]

        def next_ps(pool):
            i = pool[psi[0] % len(pool)]
            psi[0] += 1
            return i

        def norm_mod(half, A_ap, sh_ap):
            for tt in range(2):
                t0 = half * 1024 + tt * 512
                ti = t0 // 512
                pb = 6
                for c in range(KC):
                    s = c % 2
                    sc.op("scalar", lambda e, c=c, s=s, t0=t0: e.activation(out=sqb[s][:], in_=x[:, c, t0:t0 + 512],
                                                                            func=AF.Square),
                          reads=[bx[c][ti]], writes=[bsq[s]])
                    sc.op("tensor", lambda e, c=c, s=s: e.matmul(ps[:, pb, :], lhsT=ones[:], rhs=sqb[s][:],
                                                                 start=(c == 0), stop=(c == KC - 1)),
                          reads=[bones, bsq[s]], writes=[bps[pb]])
                sc.op("scalar", lambda e: e.activation(out=tmpA[0][:], in_=ps[:, pb, :], func=AF.Sqrt,
                                                       bias=eps_t[:, 0:1], scale=1.0 / D),
                      reads=[bps[pb], bmisc], writes=[btA[0]])
                sc.op("vector", lambda e: e.reciprocal(out=rstd[:], in_=tmpA[0][:]), reads=[btA[0]], writes=[brstd])
                for c in range(KC):
                    s = c % 2
                    sc.op("vector", lambda e, c=c, s=s, t0=t0: e.scalar_tensor_tensor(
                        out=tmpB[s][:], in0=x[:, c, t0:t0 + 512], scalar=A_ap[:, c:c + 1], in1=rstd[:],
                        op0=ALU.mult, op1=ALU.mult), reads=[bx[c][ti], brstd, bmod], writes=[btB[s]])
                    if sh_ap is not None:
                        sc.op("scalar", lambda e, c=c, s=s, tt=tt: e.activation(
                            out=h[:, c, tt * 512:(tt + 1) * 512], in_=tmpB[s][:], func=AF.Identity,
                            bias=sh_ap[:, c:c + 1], scale=1.0), reads=[btB[s], bmod], writes=[bh[tt]])

        def ffn(half, n, l):
            A = coefA[(n + "_g", l)]
            G = coefG[(n + "_g", l)]
            vsh = 0 if n == "ffn1" else 6
            sh = mod[l][:, vsh * 8:vsh * 8 + 8]
            norm_mod(half, A, sh)
            wdv = W[(n + "_wd", l)].rearrange("(f p) n -> p f n", p=128)
            for i, (f0, f1) in enumerate(WD_PIECES):
                sc.dma("gpsimd", lambda e, f0=f0, f1=f1: e.dma_start(out=wd[:, f0:f1, :], in_=wdv[:, f0:f1, :]),
                       writes=[bwd[i]], key=f"wd{i}")
            groups = [(g * 2, 2) for g in range(11)]
            loaded = {}

            def load_group(gi):
                f0, nf = groups[gi]
                loaded[gi] = (load_cols(W[(n + "_wg", l)], f0 * 128, nf * 128),
                              load_cols(W[(n + "_wu", l)], f0 * 128, nf * 128))
            load_group(0)
            for gi, (f0, nf) in enumerate(groups):
                if gi + 1 < len(groups):
                    load_group(gi + 1)
                sg, su = loaded[gi]
                for fi in range(nf):
                    f = f0 + fi
                    for tt in range(2):
                        pg = next_ps([0, 1])
                        pu = pg + 2
                        for kc in range(KC):
                            sc.op("tensor", lambda e, sg=sg, fi=fi, kc=kc, tt=tt, pg=pg: e.matmul(
                                ps[:, pg, :], lhsT=wslot[sg][:, kc, fi * 128:(fi + 1) * 128],
                                rhs=h[:, kc, tt * 512:(tt + 1) * 512], start=(kc == 0), stop=(kc == KC - 1)),
                                reads=[bws[sg], bh[tt]], writes=[bps[pg]], inc=(kc == KC - 1))
                        for kc in range(KC):
                            sc.op("tensor", lambda e, su=su, fi=fi, kc=kc, tt=tt, pu=pu: e.matmul(
                                ps[:, pu, :], lhsT=wslot[su][:, kc, fi * 128:(fi + 1) * 128],
                                rhs=h[:, kc, tt * 512:(tt + 1) * 512], start=(kc == 0), stop=(kc == KC - 1)),
                                reads=[bws[su], bh[tt]], writes=[bps[pu]], inc=(kc == KC - 1))
                        s = pg
                        sc.op("scalar", lambda e, s=s, pg=pg: e.activation(out=tmpA[s][:], in_=ps[:, pg, :],
                                                                           func=AF.Silu),
                              reads=[bps[pg]], writes=[btA[s]])
                        sc.op("vector", lambda e, s=s, pu=pu, f=f, tt=tt: e.tensor_tensor(
                            out=act[:, f, tt * 512:(tt + 1) * 512], in0=tmpA[s][:], in1=ps[:, pu, :], op=ALU.mult),
                            reads=[btA[s], bps[pu]], writes=[bact[f][tt]])
            for tt in range(2):
                t0 = half * 1024 + tt * 512
                ti = t0 // 512
                for d in range(KC):
                    pd = next_ps([4, 5])
                    for f in range(FC):
                        sc.op("tensor", lambda e, f=f, d=d, tt=tt, pd=pd: e.matmul(
                            ps[:, pd, :], lhsT=wd[:, f, d * 128:(d + 1) * 128], rhs=act[:, f, tt * 512:(tt + 1) * 512],
                            start=(f == 0), stop=(f == FC - 1)),
                            reads=[bwd[wd_piece[f]], bact[f][tt]], writes=[bps[pd]], inc=(f == FC - 1))
                    sc.op("vector", lambda e, d=d, t0=t0, pd=pd: e.scalar_tensor_tensor(
                        out=x[:, d, t0:t0 + 512], in0=ps[:, pd, :], scalar=G[:, d:d + 1], in1=x[:, d, t0:t0 + 512],
                        op0=ALU.mult, op1=ALU.add), reads=[bps[pd], bx[d][ti], bmod], writes=[bx[d][ti]])

        ostage = [self.sb(f"ost{i}", [128, 512], F32) for i in range(2)]
        bost = [Buf() for _ in range(2)]
        osi = [0]

        def uproj(half, l):
            A = coefA[("mix_g", l)]
            sh = mod[l][:, 3 * 8:3 * 8 + 8]
            norm_mod(half, A, sh)
            uT = W[("uT", l)]
            ngr = (DIN + 255) // 256
            loaded = {}

            def load_group(gi):
                c0 = gi * 256
                loaded[gi] = load_cols(W[("w_in", l)], c0, min(256, DIN - c0))
            load_group(0)
            for gi in range(ngr):
                if gi + 1 < ngr:
                    load_group(gi + 1)
                si = loaded[gi]
                c0 = gi * 256
                ncol = min(256, DIN - c0)
                for fi in range((ncol + 127) // 128):
                    m = min(128, ncol - fi * 128)
                    for tt in range(2):
                        t0 = half * 1024 + tt * 512
                        pg = next_ps([0, 1, 2, 3])
                        for kc in range(KC):
                            sc.op("tensor", lambda e, si=si, fi=fi, kc=kc, tt=tt, pg=pg, m=m: e.matmul(
                                ps[0:m, pg, :], lhsT=wslot[si][:, kc, fi * 128:fi * 128 + m],
                                rhs=h[:, kc, tt * 512:(tt + 1) * 512], start=(kc == 0), stop=(kc == KC - 1)),
                                reads=[bws[si], bh[tt]], writes=[bps[pg]], inc=(kc == KC - 1))
                        o = osi[0] % 2
                        osi[0] += 1
                        eng = "scalar" if o == 0 else "vector"
                        if eng == "scalar":
                            sc.op("scalar", lambda e, o=o, pg=pg, m=m: e.copy(out=ostage[o][0:m, :], in_=ps[0:m, pg, :]),
                                  reads=[bps[pg]], writes=[bost[o]])
                        else:
                            sc.op("vector", lambda e, o=o, pg=pg, m=m: e.tensor_copy(out=ostage[o][0:m, :],
                                                                                     in_=ps[0:m, pg, :]),
                                  reads=[bps[pg]], writes=[bost[o]])
                        r0 = c0 + fi * 128
                        sc.dma("sync", lambda e, o=o, m=m, r0=r0, t0=t0: e.dma_start(
                            out=uT[r0:r0 + m, t0:t0 + 512], in_=ostage[o][0:m, :]), reads=[bost[o]] + bU, key=f"ost{o}")

        ystage = [act[:, i * 8:(i + 1) * 8, 0:512] for i in range(2)]
        byst = [[bact[f][0] for f in range(i * 8, (i + 1) * 8)] for i in range(2)]

        def wout(half, l):
            yT = W[("yT", l)].rearrange("(c p) t -> p c t", p=128)
            wsl = [load_cols(W[("w_out", l)], q * 256, 256) for q in range(4)]
            G = mod[l][:, 5 * 8:5 * 8 + 8]
            for tt in range(2):
                t0 = half * 1024 + tt * 512
                ti = t0 // 512
                sc.dma("gpsimd", lambda e, tt=tt, t0=t0: e.dma_start(out=ystage[tt], in_=yT[:, :, t0:t0 + 512]),
                       reads=bY, writes=byst[tt], key=f"yst{tt}")
                for d in range(KC):
                    si = wsl[d // 2]
                    dj = d % 2
                    pd = next_ps([4, 5])
                    for kc in range(KC):
                        sc.op("tensor", lambda e, si=si, dj=dj, kc=kc, tt=tt, pd=pd: e.matmul(
                            ps[:, pd, :], lhsT=wslot[si][:, kc, dj * 128:(dj + 1) * 128], rhs=ystage[tt][:, kc, :],
                            start=(kc == 0), stop=(kc == KC - 1)),
                            reads=[bws[si]] + byst[tt], writes=[bps[pd]], inc=(kc == KC - 1))
                    sc.op("vector", lambda e, d=d, t0=t0, pd=pd: e.scalar_tensor_tensor(
                        out=x[:, d, t0:t0 + 512], in0=ps[:, pd, :], scalar=G[:, d:d + 1], in1=x[:, d, t0:t0 + 512],
                        op0=ALU.mult, op1=ALU.add), reads=[bps[pd], bx[d][ti], bmod], writes=[bx[d][ti]])

        def final(half):
            g = gains[("final_g",)]
            for tt in range(2):
                t0 = half * 1024 + tt * 512
                ti = t0 // 512
                pb = 6
                for c in range(KC):
                    s = c % 2
                    sc.op("scalar", lambda e, c=c, s=s, t0=t0: e.activation(out=sqb[s][:], in_=x[:, c, t0:t0 + 512],
                                                                            func=AF.Square),
                          reads=[bx[c][ti]], writes=[bsq[s]])
                    sc.op("tensor", lambda e, c=c, s=s: e.matmul(ps[:, pb, :], lhsT=ones[:], rhs=sqb[s][:],
                                                                 start=(c == 0), stop=(c == KC - 1)),
                          reads=[bones, bsq[s]], writes=[bps[pb]])
                sc.op("scalar", lambda e: e.activation(out=tmpA[0][:], in_=ps[:, pb, :], func=AF.Sqrt,
                                                       bias=eps_t[:, 0:1], scale=1.0 / D),
                      reads=[bps[pb], bmisc], writes=[btA[0]])
                sc.op("vector", lambda e: e.reciprocal(out=rstd[:], in_=tmpA[0][:]), reads=[btA[0]], writes=[brstd])
                for c in range(KC):
                    sc.op("vector", lambda e, c=c, t0=t0: e.scalar_tensor_tensor(
                        out=x[:, c, t0:t0 + 512], in0=x[:, c, t0:t0 + 512], scalar=g[:, c:c + 1], in1=rstd[:],
                        op0=ALU.mult, op1=ALU.mult), reads=[bx[c][ti], brstd, bmisc], writes=[bx[c][ti]])

        eps_t = self.sb("eps_t", [128, 1], F32)
        sc.op("vector", lambda e: e.memset(eps_t[:], EPS), writes=[bmisc])

        for half in range(NH):
            for s in stages:
                if s[0] in ("ffn1", "ffn2"):
                    ffn(half, s[0], s[1])
                elif s[0] == "uproj":
                    uproj(half, s[1])
                elif s[0] == "wout":
                    wout(half, s[1])
                elif s[0] == "final":
                    final(half)

        allb = []
        if xo_d is not None:
            xo_v = xo_d.rearrange("(c p) t -> p c t", p=128)
            for c in range(KC):
                sc.dma("sync", lambda e, c=c: e.dma_start(out=xo_v[:, c, :], in_=x[:, c, :]), reads=bx[c], key="xo")
                allb += bx[c]
        if fz is not None:
            return allb
        sc.final_wait("sync", allb + bost)

        with nc.Block() as block:
            sc.emit(block)
        es.close()
        return nc


NBLK = 32
MOBA_SLOTS = 16
HALF_BLOCKS = ([b for b in range(NBLK) if b % 4 in (0, 3)], [b for b in range(NBLK) if b % 4 in (1, 2)])
NEG = -30000.0


def moba_emit(P, sc, nu, pfx="", fz=None):
    nc, es = P.nc, P.es
    NQ = MOBA_SLOTS * 256
    if fz is None:
        mq = P.din(pfx + "mq", [nu, 64, NQ])
        mqs = P.din(pfx + "mqs", [nu, 16, NQ])
        mk = P.din(pfx + "mk", [nu, 64, S])
        mks = P.din(pfx + "mks", [nu, 16, S])
        mv = P.din(pfx + "mv", [nu, S, 64])
        yo = P.dout(pfx + "moT", [nu, 64, NQ])
    cq = P.din("ropeq", [nu, 2, 16, NQ])
    ck = P.din("ropek", [2, 16, S])
    pm_d = P.din("pm", [nu, 128, MOBA_SLOTS * NBLK])
    oh_d = P.din("oh", [nu, 128, MOBA_SLOTS * NBLK])
    cm_d = P.din("cm", [nu, 2, 4, 128, 256])
    boh_d = P.din("boh", [32, S])
    id_d = P.din("ident", [128, 128])

    qaug = P.sb(pfx + "qaug", [128, NQ], BF16)
    kaug = P.sb(pfx + "kaug", [128, S], BF16)
    vaug = P.sb(pfx + "vaug", [128, 64, 128], BF16)
    qf = P.sb(pfx + "qf", [64, NQ], F32)
    xt = [P.sb(pfx + f"xt{i}", [64, 1024], F32) for i in range(2)]
    xs = [P.sb(pfx + f"xs{i}", [16, 1024], F32) for i in range(2)]
    ct = [P.sb(pfx + f"ct{i}", [16, 2, 1024], F32) for i in range(2)]
    t16 = P.sb(pfx + "t16", [16, 1024], F32)
    sqf = P.sb(pfx + "sqf", [64, 1024], F32)
    kmean = P.sb(pfx + "kmean", [64, NBLK], F32)
    mx = P.sb(pfx + "mx", [128, 4], F32)
    nbias = P.sb(pfx + "nbias", [128, 1], F32)
    onesf = P.sb(pfx + "onesf", [64, 128], F32)
    ident = P.sb(pfx + "ident_sb", [128, 128], F32)
    pm = P.sb(pfx + "pm_sb", [128, MOBA_SLOTS * NBLK], F32)
    oh = P.sb(pfx + "oh_sb", [128, MOBA_SLOTS * NBLK], F32)
    cm = P.sb(pfx + "cm_sb", [128, 8, 256], F32)
    gs = P.sb(pfx + "gs", [128, NBLK], F32)
    g8 = P.sb(pfx + "g8", [128, 8], F32)
    m1 = P.sb(pfx + "m1", [128, NBLK], F32)
    m2 = P.sb(pfx + "m2", [128, NBLK], F32)
    stm = [P.sb(pfx + f"stm{i}", [128, 256], F32) for i in range(2)]
    pt = [P.sb(pfx + f"pt{i}", [128, 256], BF16) for i in range(4)]
    rec = P.sb(pfx + "rec", [64, 256], F32)
    yst = [P.sb(pfx + f"yst{i}", [64, 256], F32) for i in range(2)]
    ps = es.enter_context(nc.psum_tensor(pfx + "mps", [128, 8, 512], F32)) if fz is None else fz.ps
    if fz is not None:
        vt = P.sb(pfx + "vt", [64, 1024], F32)
        bvt = Buf()

    def rows6(e, u, r0, nr, cols):
        return fz.Urecv[r0:r0 + 5 * 64 + nr, cols][bass.ds(fz.dyn(e, "sync", ("mhr", u)), nr), :]

    bq, bk, bv, bqf = Buf(), Buf(), Buf(), Buf()
    bxt = [Buf(), Buf()]
    bxs = [Buf(), Buf()]
    bct = [Buf(), Buf()]
    bt16, bsqf, bkm, bmx, bnb, bconst, bmask = Buf(), Buf(), Buf(), Buf(), Buf(), Buf(), Buf()
    bgs, bg8, bm1, bm2 = Buf(), Buf(), Buf(), Buf()
    bstm = [Buf(), Buf()]
    bpt = [Buf(), Buf(), Buf(), Buf()]
    brec = Buf()
    byst = [Buf(), Buf()]
    bps = [Buf() for _ in range(8)]

    sc.dma("sync", lambda e: e.dma_start(out=ident[:], in_=id_d[:, :]), writes=[bconst], key=pfx + "mconst")
    sc.op("vector", lambda e: e.memset(onesf[:], 1.0), writes=[bconst])
    sc.op("vector", lambda e: e.memset(kaug[32:64, :], 0.0), writes=[bk])
    sc.op("vector", lambda e: e.memset(kaug[32:33, :], 1.0), writes=[bk])
    sc.dma("gpsimd", lambda e: e.dma_start(out=kaug[0:32, :], in_=boh_d[:, :]), writes=[bk], key=pfx + "mk0")
    sc.op("vector", lambda e: e.memset(qaug[32:64, :], 0.0), writes=[bq])
    sc.op("vector", lambda e: e.memset(vaug[:, :, 64:128], 1.0), writes=[bv])

    cnt = [0]
    for u in range(nu):
        sc.dma("sync", lambda e, u=u: e.dma_start(out=pm[:], in_=pm_d[u]), writes=[bmask], key=pfx + "mmask")
        sc.dma("sync", lambda e, u=u: e.dma_start(out=oh[:], in_=oh_d[u]), writes=[bmask], key=pfx + "mmask")
        sc.dma("sync", lambda e, u=u: e.dma_start(out=cm[:], in_=cm_d[u].rearrange("a k p q -> p (a k) q")),
               writes=[bmask], key=pfx + "mmask")
        if fz is None:
            for k0 in range(0, 64, 16):
                sc.dma("gpsimd", lambda e, u=u, k0=k0: e.dma_start(
                    out=vaug[:, k0:k0 + 16, 0:64], in_=mv[u].rearrange("(k p) d -> p k d", p=128)[:, k0:k0 + 16, :]),
                    writes=[bv], key=pfx + "mv")
        else:
            for c0 in range(0, S, 1024):
                rr, t0 = c0 // TOK, c0 % TOK

                def vsrc(e, u=u, c0=c0):
                    return fz.LV[u, :, c0:c0 + 1024]
                sc.dma("sync", lambda e, vsrc=vsrc: e.dma_start(out=vt[:], in_=vsrc(e)), reads=[fz.bUr], writes=[bvt],
                       key=pfx + "mvt")
                for cj in range(8):
                    sc.op("tensor", lambda e, cj=cj: e.transpose(ps[:, 6, cj * 64:(cj + 1) * 64], vt[:, cj * 128:(cj + 1) * 128],
                                                                 ident[0:64, 0:64]),
                          reads=[bvt, bconst], writes=[bps[6]], inc=(cj == 7))
                k0 = c0 // 128
                sc.op("vector", lambda e, k0=k0: e.tensor_copy(out=vaug[:, k0:k0 + 8, 0:64],
                                                               in_=ps[:, 6, :].rearrange("p (a b) -> p a b", b=64)),
                      reads=[bps[6]], writes=[bv])
        sc.op("vector", lambda e: e.memset(mx[:], 0.0), writes=[bmx])
        srcs_ = ((mk, mks, None, S), (mq, mqs, cq, NQ)) if fz is None else ((None, None, None, S), (None, None, cq, NQ))
        for which, (src, srcs, tab, ncols) in enumerate(srcs_):
            for c0 in range(0, ncols, 1024):
                i = cnt[0] % 2
                cnt[0] += 1
                if fz is None:
                    sc.dma("sync", lambda e, i=i, c0=c0, src=src, u=u: e.dma_start(out=xt[i][:], in_=src[u, :, c0:c0 + 1024]),
                           writes=[bxt[i]], key=pfx + f"mxt{i}")
                    sc.dma("sync", lambda e, i=i, c0=c0, srcs=srcs, u=u: e.dma_start(out=xs[i][:], in_=srcs[u, :, c0:c0 + 1024]),
                           writes=[bxs[i]], key=pfx + f"mxs{i}")
                elif which == 0:
                    rr, t0 = c0 // TOK, c0 % TOK

                    def ksrc(e, ro, nr, u=u, c0=c0):
                        return fz.LK[u, ro:ro + nr, c0:c0 + 1024]
                    sc.dma("sync", lambda e, i=i, ksrc=ksrc: e.dma_start(out=xt[i][:], in_=ksrc(e, 0, 64)),
                           reads=[fz.bUr], writes=[bxt[i]], key=pfx + f"mxt{i}")
                    sc.dma("sync", lambda e, i=i, ksrc=ksrc: e.dma_start(out=xs[i][0:8, :], in_=ksrc(e, 8, 8)),
                           reads=[fz.bUr], writes=[bxs[i]], key=pfx + f"mxs{i}")
                    sc.dma("sync", lambda e, i=i, ksrc=ksrc: e.dma_start(out=xs[i][8:16, :], in_=ksrc(e, 0, 8)),
                           reads=[fz.bUr], writes=[bxs[i]], key=pfx + f"mxs{i}")
                else:
                    def qsrc(e, ro, nr, u=u, c0=c0):
                        return fz.LQ[u, ro:ro + nr, c0:c0 + 1024]
                    sc.dma("sync", lambda e, i=i, qsrc=qsrc: e.dma_start(out=xt[i][:], in_=qsrc(e, 0, 64)),
                           reads=[fz.bUr], writes=[bxt[i]], key=pfx + f"mxt{i}")
                    sc.dma("sync", lambda e, i=i, qsrc=qsrc: e.dma_start(out=xs[i][0:8, :], in_=qsrc(e, 8, 8)),
                           reads=[fz.bUr], writes=[bxs[i]], key=pfx + f"mxs{i}")
                    sc.dma("sync", lambda e, i=i, qsrc=qsrc: e.dma_start(out=xs[i][8:16, :], in_=qsrc(e, 0, 8)),
                           reads=[fz.bUr], writes=[bxs[i]], key=pfx + f"mxs{i}")
                if which == 0:
                    sc.dma("sync", lambda e, i=i, c0=c0: e.dma_start(
                        out=ct[i][:], in_=ck[:, :, c0:c0 + 1024].rearrange("a p t -> p a t")),
                        writes=[bct[i]], key=pfx + f"mct{i}")
                else:
                    sc.dma("sync", lambda e, i=i, c0=c0, u=u: e.dma_start(
                        out=ct[i][:], in_=cq[u, :, :, c0:c0 + 1024].rearrange("a p t -> p a t")),
                        writes=[bct[i]], key=pfx + f"mct{i}")
                sc.op("vector", lambda e, i=i: e.tensor_tensor(out=t16[:], in0=xs[i][:], in1=ct[i][:, 1, :], op=ALU.mult),
                      reads=[bxs[i], bct[i]], writes=[bt16])
                sc.op("vector", lambda e, i=i: e.tensor_tensor(out=xt[i][0:16, :], in0=xt[i][0:16, :], in1=ct[i][:, 0, :],
                                                               op=ALU.mult), reads=[bxt[i], bct[i]], writes=[bxt[i]])
                sc.op("vector", lambda e, i=i: e.tensor_tensor(out=xt[i][0:16, :], in0=xt[i][0:16, :], in1=t16[:],
                                                               op=ALU.add), reads=[bxt[i], bt16], writes=[bxt[i]])
                sc.op("scalar", lambda e, i=i: e.activation(out=sqf[:], in_=xt[i][:], func=AF.Square),
                      reads=[bxt[i]], writes=[bsqf])
                for hh in range(2):
                    sc.op("tensor", lambda e, hh=hh: e.matmul(ps[:, 6, :], lhsT=onesf[:], rhs=sqf[:, hh * 512:(hh + 1) * 512],
                                                              start=True, stop=True), reads=[bconst, bsqf], writes=[bps[6]])
                    sc.op("vector", lambda e, which=which: e.tensor_reduce(out=mx[:, 2:3], in_=ps[:, 6, :], axis=mybir.AxisListType.X,
                                                                           op=ALU.max), reads=[bps[6]], writes=[bmx])
                    sc.op("vector", lambda e, which=which: e.tensor_tensor(out=mx[:, which:which + 1], in0=mx[:, which:which + 1],
                                                                           in1=mx[:, 2:3], op=ALU.max), reads=[bmx], writes=[bmx])
                if which == 0:
                    nb0 = c0 // 256
                    sc.op("vector", lambda e, i=i, nb0=nb0: e.tensor_reduce(
                        out=kmean[:, nb0:nb0 + 4], in_=xt[i][:].rearrange("p (n t) -> p n t", t=256),
                        axis=mybir.AxisListType.X, op=ALU.add), reads=[bxt[i]], writes=[bkm])
                    sc.op("scalar", lambda e, i=i, c0=c0: e.copy(out=kaug[64:128, c0:c0 + 1024], in_=xt[i][:]),
                          reads=[bxt[i]], writes=[bk])
                else:
                    sc.op("scalar", lambda e, i=i, c0=c0: e.mul(out=qaug[64:128, c0:c0 + 1024], in_=xt[i][:], mul=0.125),
                          reads=[bxt[i]], writes=[bq])
                    sc.op("vector", lambda e, i=i, c0=c0: e.tensor_copy(out=qf[:, c0:c0 + 1024], in_=xt[i][:]),
                          reads=[bxt[i]], writes=[bqf])
        sc.op("vector", lambda e: e.tensor_tensor(out=mx[:, 3:4], in0=mx[:, 0:1], in1=mx[:, 1:2], op=ALU.mult),
              reads=[bmx], writes=[bmx])
        sc.op("scalar", lambda e: e.activation(out=mx[:, 3:4], in_=mx[:, 3:4], func=AF.Sqrt), reads=[bmx], writes=[bmx])
        sc.op("vector", lambda e: e.tensor_scalar(out=nbias[:], in0=mx[:, 3:4], scalar1=-0.125, scalar2=None, op0=ALU.mult),
              reads=[bmx], writes=[bnb])
        for t in range(NQ // 128):
            r = t // 2
            sc.op("tensor", lambda e, t=t: e.matmul(ps[:, 7, 0:NBLK], lhsT=qf[:, t * 128:(t + 1) * 128], rhs=kmean[:],
                                                    start=True, stop=True), reads=[bqf, bkm], writes=[bps[7]])
            sc.op("vector", lambda e, r=r: e.tensor_tensor(out=gs[:], in0=ps[:, 7, 0:NBLK], in1=pm[:, r * NBLK:(r + 1) * NBLK],
                                                           op=ALU.add), reads=[bps[7], bmask], writes=[bgs])
            sc.op("vector", lambda e: e.max(out=g8[:], in_=gs[:]), reads=[bgs], writes=[bg8])
            sc.op("vector", lambda e: e.tensor_scalar(out=m1[:], in0=gs[:], scalar1=g8[:, 2:3], scalar2=None, op0=ALU.is_ge),
                  reads=[bgs, bg8], writes=[bm1])
            sc.op("vector", lambda e: e.tensor_scalar(out=m2[:], in0=gs[:], scalar1=-1e29, scalar2=None, op0=ALU.is_gt),
                  reads=[bgs], writes=[bm2])
            sc.op("vector", lambda e: e.tensor_tensor(out=m1[:], in0=m1[:], in1=m2[:], op=ALU.mult),
                  reads=[bm1, bm2], writes=[bm1])
            sc.op("vector", lambda e, r=r: e.tensor_tensor(out=m1[:], in0=m1[:], in1=oh[:, r * NBLK:(r + 1) * NBLK], op=ALU.add),
                  reads=[bm1, bmask], writes=[bm1])
            sc.op("vector", lambda e: e.tensor_scalar(out=m2[:], in0=m1[:], scalar1=-1.0, scalar2=-NEG, op0=ALU.add, op1=ALU.mult),
                  reads=[bm1], writes=[bm2])
            sc.op("tensor", lambda e: e.transpose(ps[0:NBLK, 7, 128:256], m2[:], ident[:]),
                  reads=[bm2, bconst], writes=[bps[7]])
            sc.op("vector", lambda e, t=t: e.tensor_copy(out=qaug[0:32, t * 128:(t + 1) * 128], in_=ps[0:NBLK, 7, 128:256]),
                  reads=[bps[7]], writes=[bq])
        tasks = [(r, kt) for r in range(MOBA_SLOTS) for kt in range(4 * r + 4)]
        NB, DEPTH = 4, 3

        def qk(i):
            r, kt = tasks[i]
            KT = 4 * r + 4
            p = i % NB
            sc.op("tensor", lambda e, kt=kt, r=r, p=p: e.matmul(ps[:, p, 0:256], lhsT=kaug[:, kt * 128:(kt + 1) * 128],
                                                               rhs=qaug[:, r * 256:(r + 1) * 256], start=True, stop=True),
                  reads=[bk, bq], writes=[bps[p]])
            if kt >= KT - 4:
                j = kt - (KT - 4) + 4 * (r % 2)
                s = j % 2
                sc.op("vector", lambda e, p=p, j=j, s=s: e.tensor_tensor(out=stm[s][:], in0=ps[:, p, 0:256], in1=cm[:, j, :],
                                                                        op=ALU.add), reads=[bps[p], bmask], writes=[bstm[s]])
                sc.op("scalar", lambda e, p=p, s=s: e.activation(out=pt[p][:], in_=stm[s][:], func=AF.Exp, bias=nbias[:, 0:1],
                                                                 scale=1.0), reads=[bstm[s], bnb], writes=[bpt[p]])
            else:
                sc.op("scalar", lambda e, p=p: e.activation(out=pt[p][:], in_=ps[:, p, 0:256], func=AF.Exp, bias=nbias[:, 0:1],
                                                            scale=1.0), reads=[bps[p], bnb], writes=[bpt[p]])

        def pv(i):
            r, kt = tasks[i]
            KT = 4 * r + 4
            p = i % NB
            po = 4 + (r % 2)
            sc.op("tensor", lambda e, kt=kt, p=p, po=po, KT=KT: e.matmul(ps[:, po, 0:256], lhsT=vaug[:, kt, :], rhs=pt[p][:],
                                                                        start=(kt == 0), stop=(kt == KT - 1)),
                  reads=[bv, bpt[p]], writes=[bps[po]])
            if kt < KT - 1:
                return
            sc.op("vector", lambda e, po=po: e.reciprocal(out=rec[:], in_=ps[64:128, po, 0:256]), reads=[bps[po]], writes=[brec])
            ys = r % 2
            sc.op("vector", lambda e, po=po, ys=ys: e.tensor_tensor(out=yst[ys][:], in0=ps[0:64, po, 0:256], in1=rec[:], op=ALU.mult),
                  reads=[bps[po], brec], writes=[byst[ys]])
            if fz is None:
                sc.dma("sync", lambda e, u=u, r=r, ys=ys: e.dma_start(out=yo[u, :, r * 256:(r + 1) * 256], in_=yst[ys][:]),
                       reads=[byst[ys]], key=pfx + f"myo{ys}")
            else:
                row0 = (r // 4) * 512 + u * 64
                sc.dma("sync", lambda e, row0=row0, r=r, ys=ys: e.dma_start(
                    out=fz.Ysend[row0:row0 + 64, (r % 4) * 256:(r % 4 + 1) * 256], in_=yst[ys][:]),
                    reads=[byst[ys], fz.bYs], key=pfx + f"myo{ys}")

        for i in range(min(DEPTH, len(tasks))):
            qk(i)
        for i in range(len(tasks)):
            if i + DEPTH < len(tasks):
                qk(i + DEPTH)
            pv(i)
    return byst


class SimpleProg:
    def __init__(self):
        self.fused = None
        self.nc = bass.Bass("TRN2", target_bir_lowering=False)
        self.es = ExitStack()
        self.in_names = []
        self.out_names = []

    din = TokProg.din
    dout = TokProg.dout
    sb = TokProg.sb

    def finish(self, sc, outbufs):
        sc.final_wait("sync", outbufs)
        with self.nc.Block() as block:
            sc.emit(block)
        self.es.close()
        return self.nc


def build_moba(nu=3):
    P = SimpleProg()
    sc = Sched(P.nc, P.es)
    ob = moba_emit(P, sc, nu)
    return P, P.finish(sc, ob)


def rope_tables():
    inv = np.exp(np.float32(-np.log(500000.0)) * np.arange(0, 16, 2, dtype=np.float32) / np.float32(16)).astype(np.float32)
    ang = (np.arange(S, dtype=np.float32)[:, None] * inv[None, :]).astype(np.float32)
    cos = np.cos(ang).astype(np.float32).T
    sin = np.sin(ang).astype(np.float32).T
    tab = np.zeros((2, 16, S), np.float32)
    tab[0, 0:8] = cos
    tab[0, 8:16] = cos
    tab[1, 0:8] = -sin
    tab[1, 8:16] = sin
    return tab


def moba_unit_inputs(uq, uk, uv, half, tab):
    blocks = HALF_BLOCKS[half]
    qpos = np.concatenate([np.arange(b * 256, (b + 1) * 256) for b in blocks])
    qT = np.ascontiguousarray(uq[qpos].T)
    kT = np.ascontiguousarray(uk.T)
    sw = np.r_[8:16, 0:8]
    return dict(mq=qT, mqs=np.ascontiguousarray(qT[sw]), mk=kT, mks=np.ascontiguousarray(kT[sw]), mv=np.ascontiguousarray(uv),
                ropeq=np.ascontiguousarray(tab[:, :, qpos]))


def moba_const_inputs(half):
    blocks = HALF_BLOCKS[half]
    pm = np.zeros((MOBA_SLOTS, NBLK), np.float32)
    oh = np.zeros((MOBA_SLOTS, NBLK), np.float32)
    for r, b in enumerate(blocks):
        pm[r, b:] = -1e30
        oh[r, b] = 1.0
    kk = np.arange(128)[:, None]
    qq = np.arange(256)[None, :]
    M0 = np.where(kk <= qq, 0.0, NEG).astype(np.float32)
    M1 = np.where(kk + 128 <= qq, 0.0, NEG).astype(np.float32)
    Z = np.zeros((128, 256), np.float32)
    cm = np.zeros((2, 4, 128, 256), np.float32)
    for par in range(2):
        r = par
        b = blocks[r]
        if b == 2 * r + 1:
            cm[par] = np.stack([Z, Z, M0, M1])
        else:
            cm[par] = np.stack([M0, M1, Z, Z])
    pmb = np.ascontiguousarray(np.broadcast_to(pm.reshape(1, -1), (128, MOBA_SLOTS * NBLK)))
    ohb = np.ascontiguousarray(np.broadcast_to(oh.reshape(1, -1), (128, MOBA_SLOTS * NBLK)))
    return dict(pm=pmb, oh=ohb, cm=cm)


def moba_shared_inputs(tab):
    boh = np.zeros((32, S), np.float32)
    for n in range(32):
        boh[n, n * 256:(n + 1) * 256] = 1.0
    return dict(ropek=tab, boh=boh, ident=np.eye(128, dtype=np.float32))


CH = 32


def conv_emit(P, sc, pfx="", fz=None):
    nc, es = P.nc, P.es
    T = TOK
    if fz is None:
        uc = P.din(pfx + "uc", [512, T + CH])
        yc = P.dout(pfx + "ycT", [256, T])
    else:
        yc = fz.Yfull
        flag_d = P.din("cflag", [128, 1])
        flag = P.sb(pfx + "cflag_sb", [128, 1], F32)
    cw = P.din(pfx + "cw", [128, 2, 31])
    cp = P.din(pfx + "cp", [128, 2, 3])
    idb = P.din("cident", [128, 128])
    a_t = [P.sb(pfx + f"ca{c}", [128, T + CH], F32) for c in range(2)]
    g_t = [P.sb(pfx + f"cg{c}", [128, T + CH], F32) for c in range(2)]
    hg = [P.sb(pfx + f"chg{c}", [128, T + CH], BF16) for c in range(2)]
    dg = [P.sb(pfx + f"cdg{c}", [128, 31, 128], BF16) for c in range(2)]
    cws = P.sb(pfx + "cws", [128, 2, 31], F32)
    cps = P.sb(pfx + "cps", [128, 2, 3], F32)
    idt = P.sb(pfx + "cidt", [128, 128], F32)
    onesf = P.sb(pfx + "cones", [128, 128], F32)
    epsc = P.sb(pfx + "ceps", [128, 1], F32)
    hc = [P.sb(pfx + f"chc{c}", [128, 512], F32) for c in range(2)]
    sq = [P.sb(pfx + f"csq{c}", [128, 512], F32) for c in range(2)]
    mean = P.sb(pfx + "cmean", [128, 512], F32)
    msq = P.sb(pfx + "cmsq", [128, 512], F32)
    var = P.sb(pfx + "cvar", [128, 512], F32)
    rstd = P.sb(pfx + "crstd", [128, 512], F32)
    tt_ = [P.sb(pfx + f"ctt{c}", [128, 512], F32) for c in range(2)]
    yo = [P.sb(pfx + f"cyo{c}", [128, 512], F32) for c in range(2)]
    ps = es.enter_context(nc.psum_tensor(pfx + "cps_", [128, 4, 512], F32)) if fz is None else fz.ps
    ba = [Buf(), Buf()]
    bg = [Buf(), Buf()]
    bhg = [Buf(), Buf()]
    bdg = [Buf(), Buf()]
    bc, bhc, bsq = Buf(), [Buf(), Buf()], [Buf(), Buf()]
    bmean, bmsq, bvar, brstd = Buf(), Buf(), Buf(), Buf()
    btt = [Buf(), Buf()]
    byo = [Buf(), Buf()]
    bps = [Buf() for _ in range(4)]

    sc.dma("sync", lambda e: e.dma_start(out=cws[:], in_=cw[:, :, :]), writes=[bc], key=pfx + "cc")
    sc.dma("sync", lambda e: e.dma_start(out=cps[:], in_=cp[:, :, :]), writes=[bc], key=pfx + "cc")
    sc.dma("sync", lambda e: e.dma_start(out=idt[:], in_=idb[:, :]), writes=[bc], key=pfx + "cc")
    sc.op("vector", lambda e: e.memset(onesf[:], 1.0), writes=[bc])
    sc.op("vector", lambda e: e.memset(epsc[:], EPS), writes=[bc])
    if fz is not None:
        sc.dma("sync", lambda e: e.dma_start(out=flag[:], in_=flag_d[:, :]), writes=[bc], key=pfx + "cc")

    def prev_rows(e, row0):
        return fz.LH[row0:row0 + 128, :]

    for c in range(2):
        if fz is None:
            sc.dma("sync", lambda e, c=c: e.dma_start(out=a_t[c][:], in_=uc[c * 128:(c + 1) * 128, :]), writes=[ba[c]],
                   key=pfx + f"ca{c}")
            sc.dma("sync", lambda e, c=c: e.dma_start(out=g_t[c][:], in_=uc[256 + c * 128:256 + (c + 1) * 128, :]),
                   writes=[bg[c]], key=pfx + f"cg{c}")
        else:
            sc.dma("sync", lambda e, c=c: e.dma_start(out=a_t[c][:, CH:], in_=fz.Usend[c * 128:(c + 1) * 128, :]),
                   reads=[fz.bU], writes=[ba[c]], key=pfx + f"ca{c}")
            sc.dma("sync", lambda e, c=c: e.dma_start(out=a_t[c][:, 0:CH], in_=prev_rows(e, c * 128)),
                   reads=[fz.bUr], writes=[ba[c]], key=pfx + f"ca{c}")
            sc.dma("sync", lambda e, c=c: e.dma_start(out=g_t[c][:, CH:], in_=fz.Usend[256 + c * 128:256 + (c + 1) * 128, :]),
                   reads=[fz.bU], writes=[bg[c]], key=pfx + f"cg{c}")
            sc.dma("sync", lambda e, c=c: e.dma_start(out=g_t[c][:, 0:CH], in_=prev_rows(e, 256 + c * 128)),
                   reads=[fz.bUr], writes=[bg[c]], key=pfx + f"cg{c}")
        sc.op("scalar", lambda e, c=c: e.activation(out=g_t[c][:], in_=g_t[c][:], func=AF.Sigmoid),
              reads=[bg[c]], writes=[bg[c]])
        sc.op("vector", lambda e, c=c: e.tensor_tensor(out=hg[c][:], in0=a_t[c][:], in1=g_t[c][:], op=ALU.mult),
              reads=[ba[c], bg[c]], writes=[bhg[c]])
        if fz is not None:
            sc.op("vector", lambda e, c=c: e.tensor_scalar(out=hg[c][:, 0:CH], in0=hg[c][:, 0:CH], scalar1=flag[:, 0:1],
                                                           scalar2=None, op0=ALU.mult), reads=[bhg[c], bc], writes=[bhg[c]])
        for k in range(31):
            sc.op("gpsimd", lambda e, c=c, k=k: e.tensor_scalar(out=dg[c][:, k, :], in0=idt[:], scalar1=cws[:, c, k:k + 1],
                                                                scalar2=None, op0=ALU.mult), reads=[bc], writes=[bdg[c]])
    for tt in range(T // 512):
        for c in range(2):
            for k in range(31):
                o = tt * 512 + 2 + k
                sc.op("tensor", lambda e, c=c, k=k, o=o: e.matmul(ps[:, c, :], lhsT=dg[c][:, k, :], rhs=hg[c][:, o:o + 512],
                                                                 start=(k == 0), stop=(k == 30)),
                      reads=[bdg[c], bhg[c]], writes=[bps[c]], inc=(k == 30))
            sc.op("scalar", lambda e, c=c: e.activation(out=hc[c][:], in_=ps[:, c, :], func=AF.Identity, bias=cps[:, c, 0:1],
                                                        scale=1.0), reads=[bps[c], bc], writes=[bhc[c]])
            sc.op("scalar", lambda e, c=c: e.activation(out=sq[c][:], in_=hc[c][:], func=AF.Square), reads=[bhc[c]],
                  writes=[bsq[c]])
        for c in range(2):
            sc.op("tensor", lambda e, c=c: e.matmul(ps[:, 2, :], lhsT=onesf[:], rhs=hc[c][:], start=(c == 0), stop=(c == 1)),
                  reads=[bc, bhc[c]], writes=[bps[2]])
        for c in range(2):
            sc.op("tensor", lambda e, c=c: e.matmul(ps[:, 3, :], lhsT=onesf[:], rhs=sq[c][:], start=(c == 0), stop=(c == 1)),
                  reads=[bc, bsq[c]], writes=[bps[3]])
        sc.op("vector", lambda e: e.tensor_scalar(out=mean[:], in0=ps[:, 2, :], scalar1=1.0 / 256, scalar2=None, op0=ALU.mult),
              reads=[bps[2]], writes=[bmean])
        sc.op("vector", lambda e: e.tensor_tensor(out=msq[:], in0=mean[:], in1=mean[:], op=ALU.mult), reads=[bmean], writes=[bmsq])
        sc.op("vector", lambda e: e.scalar_tensor_tensor(out=var[:], in0=ps[:, 3, :], scalar=1.0 / 256, in1=msq[:],
                                                         op0=ALU.mult, op1=ALU.subtract), reads=[bps[3], bmsq], writes=[bvar])
        sc.op("scalar", lambda e: e.activation(out=var[:], in_=var[:], func=AF.Sqrt, bias=epsc[:, 0:1], scale=1.0),
              reads=[bvar, bc], writes=[bvar])
        sc.op("vector", lambda e: e.reciprocal(out=rstd[:], in_=var[:]), reads=[bvar], writes=[brstd])
        for c in range(2):
            sc.op("vector", lambda e, c=c: e.tensor_tensor(out=tt_[c][:], in0=hc[c][:], in1=mean[:], op=ALU.subtract),
                  reads=[bhc[c], bmean], writes=[btt[c]])
            sc.op("vector", lambda e, c=c: e.tensor_tensor(out=tt_[c][:], in0=tt_[c][:], in1=rstd[:], op=ALU.mult),
                  reads=[btt[c], brstd], writes=[btt[c]])
            sc.op("scalar", lambda e, c=c: e.activation(out=yo[c][:], in_=tt_[c][:], func=AF.Silu, bias=cps[:, c, 2:3],
                                                        scale=cps[:, c, 1:2]), reads=[btt[c], bc], writes=[byo[c]])
            sc.dma("sync", lambda e, c=c, tt=tt: e.dma_start(out=yc[c * 128:(c + 1) * 128, tt * 512:(tt + 1) * 512], in_=yo[c][:]),
                   reads=[byo[c]] + ([] if fz is None else [fz.bY]), key=pfx + f"cyo{c}")
    return byo


def build_conv():
    P = SimpleProg()
    sc = Sched(P.nc, P.es)
    ob = conv_emit(P, sc)
    return P, P.finish(sc, ob)


def conv_inputs(u_b, j, conv_w, conv_b, ln_g, ln_b):
    t0 = j * TOK
    uc = np.zeros((512, TOK + CH), np.float32)
    lo = max(0, t0 - CH)
    uc[:, CH - (t0 - lo):] = u_b[lo:t0 + TOK, 0:512].T
    lay = lambda v: np.ascontiguousarray(v.reshape(2, 128).T)
    cw = np.ascontiguousarray(conv_w.T.reshape(2, 128, 31).transpose(1, 0, 2))
    cp = np.ascontiguousarray(np.stack([lay(conv_b), lay(ln_g), lay(ln_b)], axis=-1))
    return dict(uc=uc, cw=cw, cp=cp, cident=np.eye(128, dtype=np.float32))


GC = 64
NCH = S // GC
GSEG = 16
AX = mybir.AxisListType


def gdn_emit(P, sc, nu, pfx="", fz=None):
    import os
    STOP = float(os.environ.get("GDN_STOP", "99"))
    nc, es = P.nc, P.es
    if fz is None:
        raw_d = P.din(pfx + "graw", [nu, 3, 64, S + 3])
        gz_d = P.din(pfx + "gz", [nu, S, 64])
        ga_d = P.din(pfx + "ga", [nu, 64, NCH])
        gb_d = P.din(pfx + "gb", [nu, 64, NCH])
        go_d = P.dout(pfx + "go", [nu, S, 64])
    gcw_d = P.din(pfx + "gcw", [nu, 64, 12])
    gpar_d = P.din(pfx + "gpar", [nu, 64, 2])
    gng_d = P.din(pfx + "gng", [nu, 64, 64])
    gcst_d = P.din("gcst", [3, 64, 64])

    def unit_h(e, u):
        return fz.dyn(e, "gpsimd", ("gh", u))

    f = lambda n, shp: P.sb(pfx + n, shp, F32)
    cst = f("gcst_sb", [64, 3, 64])
    TriB = f("gTriB", [64, 8, 64])
    MB = f("gMB", [64, 8, 64])
    IB = f("gIB", [64, 8, 64])
    ones64 = f("gones", [64, 64])
    epsg = f("geps", [64, 1])
    bcst = Buf()
    sc.dma("sync", lambda e: e.dma_start(out=cst[:], in_=gcst_d.rearrange("a p q -> p a q")), writes=[bcst], key=pfx + "gc")
    sc.op("vector", lambda e: e.memset(ones64[:], 1.0), writes=[bcst])
    sc.op("vector", lambda e: e.memset(epsg[:], EPS), writes=[bcst])
    for j in range(8):
        sc.op("vector", lambda e, j=j: e.tensor_copy(out=TriB[:, j, :], in_=cst[:, 0, :]), reads=[bcst], writes=[bcst])
        sc.op("vector", lambda e, j=j: e.tensor_copy(out=MB[:, j, :], in_=cst[:, 1, :]), reads=[bcst], writes=[bcst])
        sc.op("vector", lambda e, j=j: e.tensor_copy(out=IB[:, j, :], in_=cst[:, 2, :]), reads=[bcst], writes=[bcst])
    Tri = cst[:, 0, :]
    I64 = cst[:, 2, :]

    def bc_n(t, n0):
        return t[:, n0:n0 + 8].unsqueeze(2).to_broadcast([64, 8, 64])


    def emit_unit(u, sc):
        u2 = u % 2
        GB, SB0 = 4 * u2, 4 * u2 + 3
        f = lambda n, shp: P.sb(pfx + f"u{u}_" + n, shp, F32)
        gcw = f("gcw_sb", [64, 12])
        dgw = P.sb(pfx + f"u{u}_" + "gdgw", [64, 12, 64], BF16)
        par = f("gpar_sb", [64, 2])
        negA = f("gnegA", [64, 1])
        ngb = f("gngb", [64, 64])
        a_t = f("ga_sb", [64, NCH])
        b_t = f("gb_sb", [64, NCH])
        g_t = f("gg", [64, NCH])
        beta = f("gbeta", [64, NCH])
        gc = f("ggc", [64, NCH])
        egc = f("gegc", [64, NCH])
        eglb = f("geglb", [64, NCH])
        edec = f("gedec", [64, NCH])
        bgk = f("gbgk", [64, NCH])
        SEGT = S // GSEG
        SEGC = NCH // GSEG
        raw = [P.sb(pfx + f"u{u}_" + f"graw{i}", [64, 515], BF16) for i in range(2)]
        xa = [f(f"gxa{i}", [64, 512]) for i in range(2)]
        xq = f("gxq", [64, 512])
        rn = f("grn", [64, 512])
        qnT = f("gqnT", [64, SEGT])
        knT = f("gknT", [64, SEGT])
        Kt = f("gKt", [64, SEGC, 64])
        Vt = f("gVt", [64, SEGC, 64])
        oseg = f("goseg", [64, SEGC, 64])
        zseg = f("gzseg", [64, SEGC, 64])
        osq = f("gosq", [64, SEGC, 64])
        oss = f("goss", [64, SEGC])
        rhsD = f("grhsD", [64, 8, 64])
        ED = f("gED", [64, 8, 64])
        EDT = f("gEDT", [64, 8, 64])
        Lp = [f(f"gL{i}", [64, 8, 64]) for i in range(2)]
        Np = [f(f"gN{i}", [64, 8, 64]) for i in range(2)]
        Pm = f("gP", [64, 8, 64])
        Lb = [P.sb(pfx + f"u{u}_" + f"gLb{i}", [64, 8, 64], BF16) for i in range(2)]
        Nb = [P.sb(pfx + f"u{u}_" + f"gNb{i}", [64, 8, 64], BF16) for i in range(2)]
        Pb = P.sb(pfx + f"u{u}_" + "gPb", [64, 8, 64], BF16)
        bLb, bNb, bPb = [Buf(), Buf()], [Buf(), Buf()], Buf()
        Kbg = f("gKbg", [64, 8, 64])
        Vb = f("gVb", [64, 8, 64])
        kdec = f("gkdec", [64, 8, 64])
        u_sb = f("gu", [64, 8, 64])
        wT = f("gwT", [64, 8, 64])
        qkT = f("gqkT", [64, 8, 64])
        St = f("gS", [64, 64])
        vn = [f(f"gvn{i}", [64, 64]) for i in range(2)]
        As = [f(f"gAs{i}", [64, 64]) for i in range(2)]
        ps = es.enter_context(nc.psum_tensor(pfx + "gps", [64, 8, 512], F32)) if fz is None else fz.ps[0:64, :, :]

        B_ = lambda: Buf()
        bpar, bg = B_(), B_()
        braw = [B_(), B_()]
        bxa = [B_(), B_()]
        bxq, brn, bqn, bkn, bKt, bVt, boseg, bz, bosq, boss = (B_() for _ in range(10))
        brhsD, bED, bEDT, bP, bKbg, bVb, bkdec, bu, bwT, bqkT, bS = (B_() for _ in range(11))
        bL = [B_(), B_()]
        bN = [B_(), B_()]
        bvn = [B_(), B_()]
        bAs = [B_(), B_()]
        bps = [B_() for _ in range(8)]
        wk = [0]

        def nps():
            i = GB + wk[0] % 3
            wk[0] += 1
            return i

        sc.dma("sync", lambda e, u=u: e.dma_start(out=gcw[:], in_=gcw_d[u]), writes=[bpar], key=pfx + "gp")
        sc.dma("sync", lambda e, u=u: e.dma_start(out=par[:], in_=gpar_d[u]), writes=[bpar], key=pfx + "gp")
        sc.dma("sync", lambda e, u=u: e.dma_start(out=ngb[:], in_=gng_d[u]), writes=[bpar], key=pfx + "gp")
        if fz is None:
            sc.dma("sync", lambda e, u=u: e.dma_start(out=a_t[:], in_=ga_d[u]), writes=[bpar], key=pfx + "gp")
            sc.dma("sync", lambda e, u=u: e.dma_start(out=b_t[:], in_=gb_d[u]), writes=[bpar], key=pfx + "gp")
        else:
            for rr in range(4):
                for (dst, ro) in ((a_t, 3200), (b_t, 3206)):
                    def absrc(e, u=u, rr=rr, ro=ro):
                        return fz.LAB[u, (0 if ro == 3200 else 1):(1 if ro == 3200 else 2), rr * TOK:(rr + 1) * TOK].rearrange(
                            "o (n s) -> s (o n)", s=64)
                    sc.dma("gpsimd", lambda e, dst=dst, rr=rr, absrc=absrc: e.dma_start(
                        out=dst[:, rr * 32:(rr + 1) * 32], in_=absrc(e), allow_slow_non_contiguous=True),
                        reads=[fz.bUr], writes=[bpar], key=pfx + "gp")
        for k in range(12):
            sc.op("gpsimd", lambda e, k=k: e.tensor_scalar(out=dgw[:, k, :], in0=I64, scalar1=gcw[:, k:k + 1], scalar2=None,
                                                           op0=ALU.mult), reads=[bcst, bpar], writes=[bpar])
        sc.op("scalar", lambda e: e.activation(out=negA[:], in_=par[:, 0:1], func=AF.Exp), reads=[bpar], writes=[bg])
        sc.op("vector", lambda e: e.tensor_scalar(out=negA[:], in0=negA[:], scalar1=-1.0, scalar2=None, op0=ALU.mult),
              reads=[bg], writes=[bg])
        sc.op("scalar", lambda e: e.activation(out=g_t[:], in_=a_t[:], func=AF.Exp, bias=par[:, 1:2], scale=1.0),
              reads=[bpar, bg], writes=[bg])
        sc.op("scalar", lambda e: e.activation(out=g_t[:], in_=g_t[:], func=AF.Ln, bias=1.0, scale=1.0), reads=[bg], writes=[bg])
        sc.op("vector", lambda e: e.tensor_scalar(out=g_t[:], in0=g_t[:], scalar1=negA[:, 0:1], scalar2=None, op0=ALU.mult),
              reads=[bg], writes=[bg])
        sc.op("scalar", lambda e: e.activation(out=beta[:], in_=b_t[:], func=AF.Sigmoid), reads=[bpar, bg], writes=[bg])
        sc.op("tensor", lambda e: e.matmul(ps[:, GB, 0:NCH], lhsT=Tri, rhs=g_t[:], start=True, stop=True),
              reads=[bcst, bg], writes=[bps[GB]])
        sc.op("tensor", lambda e: e.matmul(ps[:, GB, NCH:2 * NCH], lhsT=ones64[:], rhs=g_t[:], start=True, stop=True),
              reads=[bcst, bg], writes=[bps[GB]])
        sc.op("vector", lambda e: e.tensor_copy(out=gc[:], in_=ps[:, GB, 0:NCH]), reads=[bps[GB], bg], writes=[bg])
        sc.op("vector", lambda e: e.tensor_copy(out=eglb[:], in_=ps[:, GB, NCH:2 * NCH]), reads=[bps[GB], bg], writes=[bg])
        sc.op("vector", lambda e: e.tensor_tensor(out=edec[:], in0=eglb[:], in1=gc[:], op=ALU.subtract), reads=[bg], writes=[bg])
        sc.op("scalar", lambda e: e.activation(out=egc[:], in_=gc[:], func=AF.Exp), reads=[bg], writes=[bg])
        sc.op("scalar", lambda e: e.activation(out=eglb[:], in_=eglb[:], func=AF.Exp), reads=[bg], writes=[bg])
        sc.op("scalar", lambda e: e.activation(out=edec[:], in_=edec[:], func=AF.Exp), reads=[bg], writes=[bg])
        sc.op("vector", lambda e: e.tensor_tensor(out=bgk[:], in0=beta[:], in1=egc[:], op=ALU.mult), reads=[bg], writes=[bg])
        sc.op("vector", lambda e: e.memset(St[:], 0.0), writes=[bS])
        if STOP <= 1:
            return [bS]

        for seg in range(GSEG):
            for tt in range(SEGT // 512):
                c0 = seg * SEGT + tt * 512
                for j in range(3):
                    ri = (tt * 3 + j) % 2
                    if fz is None:
                        sc.dma("gpsimd", lambda e, u=u, j=j, ri=ri, c0=c0: e.dma_start(out=raw[ri][:], in_=raw_d[u, j, :, c0:c0 + 515]),
                               writes=[braw[ri]], key=pfx + f"graw{ri}")
                    else:
                        rr, t0 = c0 // TOK, c0 % TOK

                        def rsrc(e, rr_, ta, tb, u=u, j=j):
                            return fz.LG[u, j, :, rr_ * TOK + ta:rr_ * TOK + tb]
                        sc.dma("gpsimd", lambda e, ri=ri, rr=rr, t0=t0, rsrc=rsrc: e.dma_start(out=raw[ri][:, 3:515],
                                                                                            in_=rsrc(e, rr, t0, t0 + 512)),
                               reads=[fz.bUr], writes=[braw[ri]], key=pfx + f"graw{ri}")
                        if t0 >= 3:
                            sc.dma("gpsimd", lambda e, ri=ri, rr=rr, t0=t0, rsrc=rsrc: e.dma_start(out=raw[ri][:, 0:3],
                                                                                                in_=rsrc(e, rr, t0 - 3, t0)),
                                   reads=[fz.bUr], writes=[braw[ri]], key=pfx + f"graw{ri}")
                        elif rr > 0:
                            sc.dma("gpsimd", lambda e, ri=ri, rr=rr, rsrc=rsrc: e.dma_start(out=raw[ri][:, 0:3],
                                                                                         in_=rsrc(e, rr - 1, TOK - 3, TOK)),
                                   reads=[fz.bUr], writes=[braw[ri]], key=pfx + f"graw{ri}")
                        else:
                            sc.op("vector", lambda e, ri=ri: e.memset(raw[ri][:, 0:3], 0.0), writes=[braw[ri]])
                    p1 = nps()
                    for k in range(4):
                        sc.op("tensor", lambda e, j=j, k=k, ri=ri, p1=p1: e.matmul(ps[:, p1, :], lhsT=dgw[:, j * 4 + k, :],
                                                                                 rhs=raw[ri][:, k:k + 512], start=(k == 0), stop=(k == 3)),
                              reads=[bpar, braw[ri]], writes=[bps[p1]], inc=(k == 3))
                    xi = j % 2
                    sc.op("scalar", lambda e, xi=xi, p1=p1: e.activation(out=xa[xi][:], in_=ps[:, p1, :], func=AF.Silu),
                          reads=[bps[p1]], writes=[bxa[xi]])
                    if j < 2:
                        sc.op("scalar", lambda e, xi=xi: e.activation(out=xq[:], in_=xa[xi][:], func=AF.Square),
                              reads=[bxa[xi]], writes=[bxq])
                        p2 = nps()
                        sc.op("tensor", lambda e, p2=p2: e.matmul(ps[:, p2, :], lhsT=ones64[:], rhs=xq[:], start=True, stop=True),
                              reads=[bcst, bxq], writes=[bps[p2]])
                        sc.op("scalar", lambda e, p2=p2: e.activation(out=rn[:], in_=ps[:, p2, :], func=AF.Sqrt, bias=epsg[:, 0:1],
                                                                      scale=1.0), reads=[bps[p2], bcst], writes=[brn])
                        sc.op("vector", lambda e: e.reciprocal(out=rn[:], in_=rn[:]), reads=[brn], writes=[brn])
                        dst, bd = (qnT, bqn) if j == 0 else (knT, bkn)
                        scl = 0.125 if j == 0 else 1.0
                        sc.op("vector", lambda e, xi=xi, dst=dst, tt=tt, scl=scl: e.scalar_tensor_tensor(
                            out=dst[:, tt * 512:(tt + 1) * 512], in0=xa[xi][:], scalar=scl, in1=rn[:], op0=ALU.mult, op1=ALU.mult),
                            reads=[bxa[xi], brn], writes=[bd])
                    if j >= 1:
                        srcT = knT[:, tt * 512:(tt + 1) * 512] if j == 1 else xa[xi][:]
                        bsrc = bkn if j == 1 else bxa[xi]
                        p3 = nps()
                        for cj in range(8):
                            sc.op("tensor", lambda e, srcT=srcT, cj=cj, p3=p3: e.transpose(ps[:, p3, cj * 64:(cj + 1) * 64],
                                                                                         srcT[:, cj * 64:(cj + 1) * 64], I64),
                                  reads=[bsrc, bcst], writes=[bps[p3]], inc=(cj == 7))
                        dstT, bdt = (Kt, bKt) if j == 1 else (Vt, bVt)
                        sc.op("vector", lambda e, dstT=dstT, tt=tt, p3=p3: e.tensor_copy(
                            out=dstT[:, tt * 8:(tt + 1) * 8, :], in_=ps[:, p3, :].rearrange("p (a b) -> p a b", b=64)),
                            reads=[bps[p3]], writes=[bdt])
            if fz is None:
                sc.dma("sync", lambda e, u=u, seg=seg: e.dma_start(
                    out=zseg[:], in_=gz_d[u, seg * SEGT:(seg + 1) * SEGT, :].rearrange("(n s) d -> s n d", s=64)),
                    writes=[bz], key=pfx + "gz")
            else:
                def zsrc(e, u=u, seg=seg):
                    return fz.LZ[u, :, seg * SEGT:(seg + 1) * SEGT]
                sc.dma("gpsimd", lambda e, zsrc=zsrc: e.dma_start(out=zseg[:].rearrange("p a b -> p (a b)"), in_=zsrc(e)),
                       reads=[fz.bUr], writes=[bz], key=pfx + "gz")
            if STOP <= 2:
                return [bz, bKt, bVt, bqn]
            for gi in range(SEGC // 8):
                l0 = gi * 8
                n0 = seg * SEGC + l0
                v3 = lambda t: t[:]
                pk, pd, pdt = nps(), nps(), nps()
                for j in range(8):
                    cs = slice((l0 + j) * 64, (l0 + j + 1) * 64)
                    sc.op("tensor", lambda e, j=j, cs=cs, pk=pk: e.matmul(ps[:, pk, j * 64:(j + 1) * 64], lhsT=knT[:, cs], rhs=knT[:, cs],
                                                                         start=True, stop=True), reads=[bkn], writes=[bps[pk]], inc=(j == 7))
                sc.op("vector", lambda e, n0=n0: e.tensor_tensor(out=rhsD[:], in0=MB[:], in1=bc_n(g_t, n0), op=ALU.mult),
                      reads=[bcst, bg], writes=[brhsD])
                if STOP <= 2.1:
                    return [brhsD, bps[pk]]
                sc.op("tensor", lambda e, pd=pd: e.matmul(ps[:, pd, :], lhsT=Tri, rhs=rhsD[:].rearrange("p a b -> p (a b)"),
                                                          start=True, stop=True), reads=[bcst, brhsD], writes=[bps[pd]])
                for j in range(8):
                    sc.op("tensor", lambda e, j=j, pdt=pdt: e.matmul(ps[:, pdt, j * 64:(j + 1) * 64], lhsT=rhsD[:, j, :], rhs=Tri,
                                                                    start=True, stop=True), reads=[bcst, brhsD], writes=[bps[pdt]], inc=(j == 7))
                r3 = lambda ap: ap.rearrange("p (a b) -> p a b", b=64)
                sc.op("scalar", lambda e, pd=pd: e.activation(out=ED[:], in_=r3(ps[:, pd, :]), func=AF.Exp), reads=[bps[pd]], writes=[bED])
                sc.op("scalar", lambda e, pdt=pdt: e.activation(out=EDT[:], in_=r3(ps[:, pdt, :]), func=AF.Exp), reads=[bps[pdt]], writes=[bEDT])
                if STOP <= 2.2:
                    return [bED, bEDT]
                sc.op("vector", lambda e, pk=pk: e.tensor_tensor(out=Lp[0][:], in0=r3(ps[:, pk, :]), in1=ED[:], op=ALU.mult),
                      reads=[bps[pk], bED], writes=[bL[0]])
                sc.op("vector", lambda e, n0=n0: e.tensor_tensor(out=Lp[0][:], in0=Lp[0][:], in1=bc_n(beta, n0), op=ALU.mult),
                      reads=[bL[0], bg], writes=[bL[0]])
                sc.op("vector", lambda e: e.tensor_tensor(out=Lp[0][:], in0=Lp[0][:], in1=MB[:], op=ALU.mult),
                      reads=[bL[0], bcst], writes=[bL[0]])
                if STOP <= 2.3:
                    return [bL[0]]
                pn = nps()
                for j in range(8):
                    sc.op("tensor", lambda e, j=j, pn=pn: e.matmul(ps[:, pn, j * 64:(j + 1) * 64], lhsT=Lp[0][:, j, :], rhs=I64,
                                                                  start=True, stop=True),
                          reads=[bL[0], bcst], writes=[bps[pn]], inc=(j == 7))
                sc.op("scalar", lambda e, pn=pn: e.copy(out=Np[0][:], in_=r3(ps[:, pn, :])), reads=[bps[pn]], writes=[bN[0]])
                sc.op("vector", lambda e: e.tensor_tensor(out=Pm[:], in0=IB[:], in1=Np[0][:], op=ALU.subtract),
                      reads=[bN[0], bcst], writes=[bP])
                if STOP <= 2.4:
                    return [bP, bN[0]]
                sc.op("gpsimd", lambda e: e.tensor_copy(out=Lb[0][:], in_=Lp[0][:]), reads=[bL[0]], writes=[bLb[0]])
                sc.op("gpsimd", lambda e: e.tensor_copy(out=Nb[0][:], in_=Np[0][:]), reads=[bN[0]], writes=[bNb[0]])
                sc.op("gpsimd", lambda e: e.tensor_copy(out=Pb[:], in_=Pm[:]), reads=[bP], writes=[bPb])
                cur = 0
                for lvl in range(5):
                    nxt = 1 - cur
                    pl = nps()
                    for j in range(8):
                        sc.op("tensor", lambda e, j=j, pl=pl, cur=cur: e.matmul(ps[:, pl, j * 64:(j + 1) * 64], lhsT=Nb[cur][:, j, :],
                                                                               rhs=Lb[cur][:, j, :], start=True, stop=True),
                              reads=[bNb[cur], bLb[cur]], writes=[bps[pl]], inc=(j == 7))
                    if lvl < 4:
                        pn2 = nps()
                        for j in range(8):
                            sc.op("tensor", lambda e, j=j, pn2=pn2, cur=cur: e.matmul(ps[:, pn2, j * 64:(j + 1) * 64], lhsT=Lb[cur][:, j, :],
                                                                                     rhs=Nb[cur][:, j, :], start=True, stop=True),
                                  reads=[bNb[cur], bLb[cur]], writes=[bps[pn2]], inc=(j == 7))
                    sc.op("scalar", lambda e, pl=pl, nxt=nxt: e.copy(out=Lb[nxt][:], in_=r3(ps[:, pl, :])), reads=[bps[pl]], writes=[bLb[nxt]])
                    if lvl < 4:
                        sc.op("vector", lambda e, pn2=pn2, nxt=nxt: e.tensor_copy(out=Nb[nxt][:], in_=r3(ps[:, pn2, :])),
                              reads=[bps[pn2]], writes=[bNb[nxt]])
                    pu = nps()
                    for j in range(8):
                        sc.op("tensor", lambda e, j=j, pu=pu, nxt=nxt: e.matmul(ps[:, pu, j * 64:(j + 1) * 64], lhsT=Lb[nxt][:, j, :],
                                                                               rhs=Pb[:, j, :], start=True, stop=True),
                              reads=[bLb[nxt], bPb], writes=[bps[pu]], inc=(j == 7))
                    sc.op("vector", lambda e, pu=pu: e.tensor_tensor(out=Pm[:], in0=Pm[:], in1=r3(ps[:, pu, :]), op=ALU.add),
                          reads=[bps[pu], bP], writes=[bP])
                    if lvl < 4:
                        sc.op("gpsimd", lambda e: e.tensor_copy(out=Pb[:], in_=Pm[:]), reads=[bP], writes=[bPb])
                    cur = nxt
                if STOP <= 2.5:
                    return [bP]
                sc.op("vector", lambda e, l0=l0, n0=n0: e.tensor_tensor(out=Kbg[:], in0=Kt[:, l0:l0 + 8, :], in1=bc_n(bgk, n0), op=ALU.mult),
                      reads=[bKt, bg], writes=[bKbg])
                sc.op("vector", lambda e, l0=l0, n0=n0: e.tensor_tensor(out=Vb[:], in0=Vt[:, l0:l0 + 8, :], in1=bc_n(beta, n0), op=ALU.mult),
                      reads=[bVt, bg], writes=[bVb])
                sc.op("vector", lambda e, l0=l0, n0=n0: e.tensor_tensor(out=kdec[:], in0=Kt[:, l0:l0 + 8, :], in1=bc_n(edec, n0), op=ALU.mult),
                      reads=[bKt, bg], writes=[bkdec])
                p_u, p_w, p_q = nps(), nps(), nps()
                for j in range(8):
                    sc.op("tensor", lambda e, j=j, p_u=p_u: e.matmul(ps[:, p_u, j * 64:(j + 1) * 64], lhsT=Pm[:, j, :], rhs=Vb[:, j, :],
                                                                    start=True, stop=True), reads=[bP, bVb], writes=[bps[p_u]], inc=(j == 7))
                for j in range(8):
                    sc.op("tensor", lambda e, j=j, p_w=p_w: e.matmul(ps[:, p_w, j * 64:(j + 1) * 64], lhsT=Kbg[:, j, :], rhs=Pm[:, j, :],
                                                                    start=True, stop=True), reads=[bP, bKbg], writes=[bps[p_w]], inc=(j == 7))
                for j in range(8):
                    cs = slice((l0 + j) * 64, (l0 + j + 1) * 64)
                    sc.op("tensor", lambda e, j=j, cs=cs, p_q=p_q: e.matmul(ps[:, p_q, j * 64:(j + 1) * 64], lhsT=knT[:, cs], rhs=qnT[:, cs],
                                                                           start=True, stop=True), reads=[bkn, bqn], writes=[bps[p_q]], inc=(j == 7))
                sc.op("scalar", lambda e, p_u=p_u: e.copy(out=u_sb[:], in_=r3(ps[:, p_u, :])), reads=[bps[p_u]], writes=[bu])
                sc.op("scalar", lambda e, p_w=p_w: e.copy(out=wT[:], in_=r3(ps[:, p_w, :])), reads=[bps[p_w]], writes=[bwT])
                sc.op("vector", lambda e, p_q=p_q: e.tensor_tensor(out=qkT[:], in0=r3(ps[:, p_q, :]), in1=EDT[:], op=ALU.mult),
                      reads=[bps[p_q], bEDT], writes=[bqkT])
                sc.op("vector", lambda e: e.tensor_tensor(out=qkT[:], in0=qkT[:], in1=TriB[:], op=ALU.mult), reads=[bqkT, bcst], writes=[bqkT])
                if STOP <= 3:
                    return [bqkT, bu, bwT]
                for j in range(8):
                    n = n0 + j
                    l = l0 + j
                    cs = slice(l * 64, (l + 1) * 64)
                    i2 = j % 2
                    bx_, by_ = SB0, SB0
                    sc.op("tensor", lambda e, j=j, bx_=bx_: e.matmul(ps[:, bx_, 0:64], lhsT=wT[:, j, :], rhs=St[:], start=True, stop=True),
                          reads=[bwT, bS], writes=[bps[bx_]])
                    sc.op("tensor", lambda e, cs=cs, by_=by_: e.matmul(ps[:, by_, 192:256], lhsT=qnT[:, cs], rhs=St[:], start=True, stop=True),
                          reads=[bqn, bS], writes=[bps[by_]])
                    sc.op("vector", lambda e, j=j, bx_=bx_, i2=i2: e.tensor_tensor(out=vn[i2][:], in0=u_sb[:, j, :], in1=ps[:, bx_, 0:64],
                                                                                 op=ALU.subtract), reads=[bu, bps[bx_]], writes=[bvn[i2]])
                    sc.op("vector", lambda e, by_=by_, i2=i2, n=n: e.tensor_scalar(out=As[i2][:], in0=ps[:, by_, 192:256], scalar1=egc[:, n:n + 1],
                                                                                  scalar2=None, op0=ALU.mult), reads=[bps[by_], bg], writes=[bAs[i2]])
                    sc.op("tensor", lambda e, j=j, bx_=bx_, i2=i2: e.matmul(ps[:, bx_, 64:128], lhsT=qkT[:, j, :], rhs=vn[i2][:],
                                                                          start=True, stop=True), reads=[bqkT, bvn[i2]], writes=[bps[bx_]], inc=False)
                    sc.op("tensor", lambda e, j=j, bx_=bx_, i2=i2: e.matmul(ps[:, bx_, 128:192], lhsT=kdec[:, j, :], rhs=vn[i2][:],
                                                                          start=True, stop=True), reads=[bkdec, bvn[i2]], writes=[bps[bx_]])
                    sc.op("vector", lambda e, bx_=bx_, n=n: e.scalar_tensor_tensor(out=St[:], in0=St[:], scalar=eglb[:, n:n + 1],
                                                                                  in1=ps[:, bx_, 128:192], op0=ALU.mult, op1=ALU.add),
                          reads=[bS, bg, bps[bx_]], writes=[bS])
                    sc.op("vector", lambda e, bx_=bx_, i2=i2, l=l: e.tensor_tensor(out=oseg[:, l, :], in0=As[i2][:], in1=ps[:, bx_, 64:128],
                                                                                 op=ALU.add), reads=[bAs[i2], bps[bx_]], writes=[boseg])
                if STOP <= 4:
                    return [boseg, bS]
            sc.op("gpsimd", lambda e: e.tensor_tensor(out=osq[:], in0=oseg[:], in1=oseg[:], op=ALU.mult), reads=[boseg], writes=[bosq])
            sc.op("vector", lambda e: e.tensor_reduce(out=oss[:], in_=osq[:], axis=AX.X, op=ALU.add), reads=[bosq], writes=[boss])
            sc.op("scalar", lambda e: e.activation(out=oss[:], in_=oss[:], func=AF.Sqrt, bias=epsg[:, 0:1], scale=1.0 / 64),
                  reads=[boss, bcst], writes=[boss])
            sc.op("vector", lambda e: e.reciprocal(out=oss[:], in_=oss[:]), reads=[boss], writes=[boss])
            sc.op("vector", lambda e: e.tensor_tensor(out=osq[:], in0=oseg[:], in1=oss[:].unsqueeze(2).to_broadcast([64, SEGC, 64]),
                                                      op=ALU.mult), reads=[boseg, boss], writes=[bosq])
            sc.op("gpsimd", lambda e: e.tensor_tensor(out=osq[:], in0=osq[:], in1=ngb[:].unsqueeze(1).to_broadcast([64, SEGC, 64]),
                                                      op=ALU.mult), reads=[bosq, bpar], writes=[bosq])
            sc.op("scalar", lambda e: e.activation(out=zseg[:], in_=zseg[:], func=AF.Silu), reads=[bz], writes=[bz])
            if fz is None:
                sc.op("vector", lambda e: e.tensor_tensor(out=osq[:], in0=osq[:], in1=zseg[:], op=ALU.mult), reads=[bosq, bz], writes=[bosq])
                sc.dma("sync", lambda e, u=u, seg=seg: e.dma_start(
                    out=go_d[u, seg * SEGT:(seg + 1) * SEGT, :].rearrange("(n s) d -> s n d", s=64), in_=osq[:]),
                    reads=[bosq], key=pfx + "go")
            else:
                oT = oseg[:].rearrange("p a b -> p (a b)")
                zT = zseg[:].rearrange("p a b -> p (a b)")
                for g4 in range(SEGC // 8):
                    pt_ = nps()
                    for j in range(8):
                        sc.op("tensor", lambda e, j=j, g4=g4, pt_=pt_: e.transpose(ps[:, pt_, j * 64:(j + 1) * 64], osq[:, g4 * 8 + j, :], I64),
                              reads=[bosq, bcst], writes=[bps[pt_]], inc=(j == 7))
                    sc.op("vector", lambda e, g4=g4, pt_=pt_: e.tensor_tensor(out=oT[:, g4 * 512:(g4 + 1) * 512], in0=ps[:, pt_, :],
                                                                            in1=zT[:, g4 * 512:(g4 + 1) * 512], op=ALU.mult),
                          reads=[bps[pt_], bz, bosq], writes=[boseg])
                tok0 = seg * SEGT
                kblk, coff = tok0 // 1024, tok0 % 1024
                row0 = (kblk // 2) * 512 + 192 + (u * 2 + kblk % 2) * 64
                sc.dma("sync", lambda e, row0=row0, coff=coff: e.dma_start(out=fz.Ysend[row0:row0 + 64, coff:coff + SEGT],
                                                                          in_=oT[:, 0:SEGT]),
                       reads=[boseg, fz.bYs], key=pfx + f"go{u}")
        return [bosq, boseg]

    class _Rec:
        def __init__(self):
            self.calls = []

        def op(self, *a, **k):
            self.calls.append(("op", a, k))

        def dma(self, *a, **k):
            self.calls.append(("dma", a, k))

    outs = []
    recs = []
    for u in range(nu):
        r = _Rec()
        outs += emit_unit(u, r)
        recs.append(r.calls)
    n = max(len(c) for c in recs)
    for i in range(n):
        for c in recs:
            if i < len(c):
                kind, a, k = c[i]
                getattr(sc, kind)(*a, **k)
    return outs


def build_gdn(nu=2):
    P = SimpleProg()
    sc = Sched(P.nc, P.es)
    ob = gdn_emit(P, sc, nu)
    return P, P.finish(sc, ob)


def gdn_const_inputs():
    i = np.arange(64)
    tri = (i[:, None] <= i[None, :]).astype(np.float32)
    ms = (i[:, None] > i[None, :]).astype(np.float32)
    return dict(gcst=np.stack([tri, ms, np.eye(64, dtype=np.float32)]))


def gdn_unit_inputs(ug, h, gdn_conv_w, a_log, dt_bias, norm_g):
    GW = 384
    raw = np.zeros((3, 64, S + 3), np.float32)
    cw = np.zeros((64, 12), np.float32)
    for j in range(3):
        cols = slice(j * GW + h * 64, j * GW + (h + 1) * 64)
        raw[j, :, 3:] = ug[:, cols].T
        cw[:, j * 4:(j + 1) * 4] = gdn_conv_w[:, cols].T
    z = np.ascontiguousarray(ug[:, 3 * GW + h * 64:3 * GW + (h + 1) * 64])
    a = np.ascontiguousarray(ug[:, 4 * GW + h].reshape(NCH, 64).T)
    b = np.ascontiguousarray(ug[:, 4 * GW + 6 + h].reshape(NCH, 64).T)
    par = np.zeros((64, 2), np.float32)
    par[:, 0] = a_log[h]
    par[:, 1] = dt_bias[h]
    ng = np.ascontiguousarray(np.broadcast_to(norm_g[None, :], (64, 64))).astype(np.float32)
    return dict(graw=raw, gcw=cw, gz=z, ga=a, gb=b, gpar=par, gng=ng)


def _lay(v):
    return np.ascontiguousarray(np.asarray(v, np.float32).reshape(-1, 128).T)


_PROGS = {}


def _prog(key, builder):
    if key not in _PROGS:
        _PROGS[key] = builder()
    return _PROGS[key]


def _run(nc, in_maps):
    res = run_bass_kernel_spmd(nc, in_maps, core_ids=list(range(NCORES)))
    return res.results


def _tok_launch(key, stages, inp, xT_list, yT_list=None):
    def mk():
        p = TokProg(stages)
        return p, p.build()
    P, nc = _prog(key, mk)
    maps = []
    for c in range(NCORES):
        b = c // 4
        m = {"xT": xT_list[c], "cT": _lay(inp["c"][b])}
        for name in P.in_names:
            if name in m:
                continue
            if name.startswith("yT"):
                m[name] = yT_list[c]
            elif name == "final_g":
                m[name] = _lay(inp["final_g"])
            else:
                base, l = name[:-1], int(name[-1])
                arr = np.asarray(inp[base][l], np.float32)
                if base == "b_ada" or base.startswith("ln_"):
                    arr = _lay(arr)
                m[name] = np.ascontiguousarray(arr)
        maps.append(m)
    return _run(nc, maps)


def _mixer(inp, l, u):
    y = np.zeros((B, S, D), np.float32)
    P, nc = _prog("conv", build_conv)
    maps = []
    for c in range(NCORES):
        b, j = c // 4, c % 4
        maps.append(conv_inputs(u[b], j, np.asarray(inp["conv_w"][l]), np.asarray(inp["conv_b"][l]),
                                np.asarray(inp["conv_ln_g"][l]), np.asarray(inp["conv_ln_b"][l])))
    res = _run(nc, maps)
    for c in range(NCORES):
        b, j = c // 4, c % 4
        y[b, j * TOK:(j + 1) * TOK, 0:256] = res[c]["ycT"].T
    P, nc = _prog("moba", lambda: build_moba(3))
    tab = rope_tables()
    shared = moba_shared_inputs(tab)
    consts = [moba_const_inputs(0), moba_const_inputs(1)]
    maps = []
    for c in range(NCORES):
        b, cc = c // 4, c % 4
        units = []
        for s in range(3):
            combo = 3 * cc + s
            h, half = combo // 2, combo % 2
            q = u[b, :, 512 + h * 64:512 + (h + 1) * 64]
            k = u[b, :, 512 + 384 + h * 64:512 + 384 + (h + 1) * 64]
            v = u[b, :, 512 + 768 + h * 64:512 + 768 + (h + 1) * 64]
            d = moba_unit_inputs(q, k, v, half, tab)
            d.update(consts[half])
            units.append(d)
        m = {k_: np.ascontiguousarray(np.stack([un[k_] for un in units])) for k_ in units[0]}
        m.update(shared)
        maps.append(m)
    res = _run(nc, maps)
    for c in range(NCORES):
        b, cc = c // 4, c % 4
        for s in range(3):
            combo = 3 * cc + s
            h, half = combo // 2, combo % 2
            qpos = np.concatenate([np.arange(bl * 256, (bl + 1) * 256) for bl in HALF_BLOCKS[half]])
            y[b, qpos, 256 + h * 64:256 + (h + 1) * 64] = res[c]["moT"][s].T
    P, nc = _prog("gdn", lambda: build_gdn(2))
    gconst = gdn_const_inputs()
    allu = [(b, h) for b in range(B) for h in range(6)]
    maps = []
    assign = []
    for c in range(NCORES):
        us = [allu[i] if i < len(allu) else allu[0] for i in (2 * c, 2 * c + 1)]
        assign.append([(i < len(allu)) for i in (2 * c, 2 * c + 1)])
        units = [gdn_unit_inputs(u[b, :, 512 + 1152:], h, np.asarray(inp["gdn_conv_w"][l]), np.asarray(inp["gdn_a_log"][l]),
                                 np.asarray(inp["gdn_dt_bias"][l]), np.asarray(inp["gdn_norm_g"][l])) for (b, h) in us]
        m = {k_: np.ascontiguousarray(np.stack([un[k_] for un in units])) for k_ in units[0]}
        m.update(gconst)
        maps.append(m)
    res = _run(nc, maps)
    for c in range(NCORES):
        for s in range(2):
            i = 2 * c + s
            if i < len(allu):
                b, h = allu[i]
                y[b, :, 640 + h * 64:640 + (h + 1) * 64] = res[c]["go"][s]
    return y


YROWS = 768 + 1024
RG = [[0, 1, 2, 3], [4, 5, 6, 7]]


def moba_unit(cc, su):
    return (cc, su) if su < 2 else (4 + cc // 2, cc % 2)


def moba_owner(h, half):
    return (h, half) if h < 4 else (2 * (h - 4) + half, 2)


class Fused:
    def __init__(self):
        self.nc = bass.Bass("TRN2", target_bir_lowering=False)
        self.es = ExitStack()
        self.cur = self.es
        self.dins = {}
        self.in_names = []
        self.out_names = []
        self.phase_i = 0
        self.load_x = False
        self.store_x = False
        self._dyn = {}

    def din(self, name, shape, dt=F32):
        if name not in self.dins:
            self.in_names.append(name)
            self.dins[name] = self.nc.dram_tensor(name, list(shape), dt, kind="ExternalInput").ap()
        return self.dins[name]

    def dout(self, name, shape, dt=F32):
        if name not in self.dins:
            self.out_names.append(name)
            self.dins[name] = self.nc.dram_tensor(name, list(shape), dt, kind="ExternalOutput").ap()
        return self.dins[name]

    AW = 36800

    def sb(self, name, shape, dt):
        p = shape[0]
        n = int(np.prod(shape[1:]))
        n32 = n if dt == F32 else (n + 1) // 2
        n32 = (n32 + 7) // 8 * 8
        off = self.aoff
        self.aoff += n32
        assert self.aoff <= self.AW, (name, self.aoff)
        v = self.arena[0:p, off:off + n32]
        if dt != F32:
            v = v.bitcast(dt)
        v = v[:, 0:n]
        if len(shape) == 3:
            v = v.rearrange("p (a b) -> p a b", a=shape[1])
        return v

    def dyn(self, e, engname, key):
        c = self._dyn.setdefault(engname, {})
        if "cc" not in c:
            c["cc"] = e.snap(e.partition_id() % 4)
        if key not in c:
            cc = c["cc"]
            doff = lambda h: (h // 2) * 512 + (h % 2) * 64
            v = {"c2048": lambda: cc * 2048, "prev": lambda: (cc + 3) % 4,
                 "D0": lambda: doff(cc), "D2": lambda: (cc // 2) * 64 + 1024, "mha2": lambda: cc % 2, "mhb2": lambda: 3 - cc % 2,
                 "gh1": lambda: (cc + 4) % 6, "Dg1": lambda: doff((cc + 4) % 6)}[key]()
            c[key] = e.snap(v)
        return c[key]

    def build(self):
        nc, es = self.nc, self.es
        sc = self.sc = Sched(nc, es)
        self.x = es.enter_context(nc.sbuf_tensor("x_res", [128, KC, TOK], F32))
        self.arena = es.enter_context(nc.sbuf_tensor("arena", [128, self.AW], F32))
        self.aoff = 0
        self.bx = [[Buf(f"x{c}_{t}") for t in range(TOK // 512)] for c in range(KC)]
        self.ps = es.enter_context(nc.psum_tensor("ps_all", [128, 8, 512], F32))
        NUC = (DIN + 127) // 128
        Usend_t = nc.dram_tensor("Usend", [NUC * 128, TOK], F32)
        Urecv_t = nc.dram_tensor("Urecv", [NUC * 512 + 128, TOK], F32)
        Ysend_t = nc.dram_tensor("Ysend", [2048, 1024], F32)
        Yrecv_t = nc.dram_tensor("Yrecv", [8192, 1024], F32)
        Yfull_t = nc.dram_tensor("Yfull", [D, TOK], F32)
        self.Usend, self.Urecv, self.Ysend, self.Yrecv, self.Yfull = (t.ap() for t in (Usend_t, Urecv_t, Ysend_t, Yrecv_t, Yfull_t))
        self.LK = nc.dram_tensor("LK", [3, 64, S], F32).ap()
        self.LV = nc.dram_tensor("LV", [3, 64, S], F32).ap()
        self.LQ = nc.dram_tensor("LQ", [3, 64, MOBA_SLOTS * 256], F32).ap()
        self.LQF = nc.dram_tensor("LQF", [3, 64, S], F32).ap()
        self.LG = nc.dram_tensor("LG", [2, 3, 64, S], F32).ap()
        self.LZ = nc.dram_tensor("LZ", [2, 64, S], F32).ap()
        self.LAB = nc.dram_tensor("LAB", [2, 2, S], F32).ap()
        self.LH = nc.dram_tensor("LH", [512, CH], F32).ap()
        self.Yloc = nc.dram_tensor("Yloc", [4, 512, 1024], F32).ap()
        self.uT_dst = self.Usend
        self.yT_src = self.Yfull
        self.bU, self.bUr, self.bYs, self.bYr, self.bY = Buf("U"), Buf("Ur"), Buf("Ys"), Buf("Yr"), Buf("Yf")
        self.bL, self.bLq, self.bYl = Buf("L"), Buf("Lq"), Buf("Yl")
        self.bUr_m, self.bUr_g = Buf("Ur_m"), Buf("Ur_g")
        outb = []

        def run_phase(fn):
            self.aoff = 0
            r = fn()
            sc.barrier(exclude=("agUm", "agUg"))
            self.phase_i += 1
            return r

        def tok_phase(stages, load_x=False, store_x=False, ag=True):
            def fn():
                self.load_x, self.store_x = load_x, store_x
                r = TokProg(stages, fused=self).build()
                if ag:
                    order = list(range(4, 13)) + list(range(13, NUC)) + list(range(0, 4))
                    for ci in order:
                        bdst = self.bUr_m if 4 <= ci < 13 else self.bUr_g
                        sc.cc(lambda e, ci=ci: e.collective_compute(
                            "AllGather", ALU.bypass, replica_groups=RG,
                            ins=[Usend_t.ap()[ci * 128:(ci + 1) * 128, :]], outs=[Urecv_t.ap()[ci * 512:(ci + 1) * 512, :]]),
                            writes=[self.bU, bdst], key=("agUm" if 4 <= ci < 13 else "agUg"))
                return r
            return run_phase(fn)

        def y_exchange():
            for ci in range(8):
                sc.cc(lambda e, ci=ci: e.collective_compute(
                    "AllGather", ALU.bypass, replica_groups=RG,
                    ins=[Ysend_t.ap()[ci * 256:(ci + 1) * 256, :]], outs=[Yrecv_t.ap()[ci * 1024:(ci + 1) * 1024, :]]),
                    writes=[self.bYs, self.bYr], key="agY")
            LB = ([0, 3, 4, 7], [1, 2, 5, 6])
            for c2 in range(2):
                sc.dma("scalar", lambda e, c2=c2: e.dma_start(
                    out=self.Yloc[:, c2 * 256:(c2 + 1) * 256, :],
                    in_=self.Yrecv[c2 * 1024:c2 * 1024 + 7168, :][bass.ds(self.dyn(e, "scalar", "c2048"), 1024), :].rearrange(
                        "(r f) t -> r f t", r=4)),
                    reads=[self.bYr], writes=[self.bYl], key="yloc")
            for h in range(6):
                for half in range(2):
                    rs, su = moba_owner(h, half)
                    for q4 in range(4):
                        lb = LB[half][q4]
                        sc.dma("sync", lambda e, h=h, lb=lb, rs=rs, su=su, q4=q4: e.dma_start(
                            out=self.Yfull[256 + h * 64:256 + (h + 1) * 64, lb * 256:(lb + 1) * 256],
                            in_=self.Yloc[rs, su * 64:(su + 1) * 64, q4 * 256:(q4 + 1) * 256]),
                            reads=[self.bYl, self.bY], key="yasm")
            for h in range(6):
                rs, g = (h, 0) if h < 4 else (h - 4, 1)
                for kk in range(2):
                    r0 = 192 + (g * 2 + kk) * 64
                    sc.dma("sync", lambda e, h=h, kk=kk, rs=rs, r0=r0: e.dma_start(
                        out=self.Yfull[640 + h * 64:640 + (h + 1) * 64, kk * 1024:(kk + 1) * 1024],
                        in_=self.Yloc[rs, r0:r0 + 64, :]),
                        reads=[self.bYl, self.bY], key="yasm")

        def localize_m():
            Ur = self.Urecv
            rk = lambda ap: ap.rearrange("d (r t) -> d r t", r=4)
            LQF = self.LQF

            def blk(e, q, dkey, B, n=64):
                R0 = (B // 128) * 512 + B % 128
                R1 = min(R0 + 2048, NUC * 512 + 128)
                return Ur[R0:R1, :][bass.ds(self.dyn(e, q, dkey), 512), :].rearrange("(r f) t -> f r t", r=4)[0:n]

            for u in range(3):
                q = "sync" if u < 2 else "scalar"
                dk = "D0" if u < 2 else "D2"
                for (dst, B) in ((self.LK, 896), (self.LV, 1280), (LQF, 512)):
                    sc.dma(q, lambda e, u=u, q=q, dk=dk, dst=dst, B=B: e.dma_start(out=rk(dst[u]), in_=blk(e, q, dk, B)),
                           reads=[self.bUr_m], writes=[self.bL], key=f"loc{q}{u}")
                for ab in range(2):
                    dstq = self.LQ[u].rearrange("d (G ab i) -> d G ab i", G=8, ab=2)[:, :, ab:ab + 1, :]
                    srcv = LQF[u].rearrange("d (G b i) -> d G b i", G=8, b=4)
                    if u < 2:
                        b = u if ab == 0 else 3 - u
                        sc.dma(q, lambda e, dstq=dstq, srcv=srcv, b=b: e.dma_start(out=dstq, in_=srcv[:, :, b:b + 1, :]),
                               reads=[self.bL], writes=[self.bLq], key=f"locq{u}")
                    else:
                        kn = "mha2" if ab == 0 else "mhb2"
                        sc.dma(q, lambda e, dstq=dstq, srcv=srcv, kn=kn: e.dma_start(
                            out=dstq, in_=srcv[:, :, bass.ds(self.dyn(e, "scalar", kn), 1), :]),
                            reads=[self.bL], writes=[self.bLq], key=f"locq{u}")

        def localize_g():
            Ur = self.Urecv
            rk = lambda ap: ap.rearrange("d (r t) -> d r t", r=4)
            LQF = self.LQF

            def blk(e, q, dkey, B, n=64):
                R0 = (B // 128) * 512 + B % 128
                R1 = min(R0 + 2048, NUC * 512 + 128)
                return Ur[R0:R1, :][bass.ds(self.dyn(e, q, dkey), 512), :].rearrange("(r f) t -> f r t", r=4)[0:n]

            for u in range(2):
                dk, gk = ("D0", "cc") if u == 0 else ("Dg1", "gh1")
                for j in range(3):
                    sc.dma("gpsimd", lambda e, u=u, j=j, dk=dk: e.dma_start(
                        out=rk(self.LG[u, j]), in_=blk(e, "gpsimd", dk, 1664 + j * 384)),
                        reads=[self.bUr_g], writes=[self.bL], key="locg")
                q2 = "gpsimd" if u == 0 else "sync"
                sc.dma(q2, lambda e, u=u, dk=dk, q2=q2: e.dma_start(out=rk(self.LZ[u]), in_=blk(e, q2, dk, 2816)),
                       reads=[self.bUr_g], writes=[self.bL], key=f"locz{u}")
                for ab, B in ((0, 3200), (1, 3206)):
                    sc.dma(q2, lambda e, u=u, ab=ab, B=B, gk=gk, q2=q2: e.dma_start(
                        out=self.LAB[u, ab:ab + 1].rearrange("o (r t) -> o r t", r=4), in_=blk(e, q2, gk, B, 1)),
                        reads=[self.bUr_g], writes=[self.bL], key=f"locz{u}")
            sc.dma("scalar", lambda e: e.dma_start(
                out=self.LH.rearrange("(c f) t -> c f t", c=4),
                in_=Ur[0:2048, TOK - CH:TOK].rearrange("(c r f) t -> r c f t", r=4, f=128)[bass.ds(self.dyn(e, "scalar", "prev"), 1)]),
                reads=[self.bUr_g], writes=[self.bL], key="loch")

        def mixer(l):
            pfx = f"L{l}_"
            run_phase(localize_m)
            run_phase(lambda: moba_emit(self, sc, 3, pfx, fz=self))
            run_phase(localize_g)
            run_phase(lambda: conv_emit(self, sc, pfx, fz=self))

            def g():
                gdn_emit(self, sc, 2, pfx, fz=self)
                y_exchange()
            run_phase(g)

        tok_phase([("ffn1", 0), ("uproj", 0)], load_x=True)
        mixer(0)
        tok_phase([("wout", 0), ("ffn2", 0), ("ffn1", 1), ("uproj", 1)])
        mixer(1)
        outb = tok_phase([("wout", 1), ("ffn2", 1), ("final",)], store_x=True, ag=False)
        with nc.Block() as block:
            sc.emit(block)
        es.close()
        return nc


_FUSED = {}


def kernel(**inp):
    if "p" not in _FUSED:
        F = Fused()
        _FUSED["p"] = (F, F.build())
    F, nc = _FUSED["p"]
    x = np.asarray(inp["x"], np.float32)
    tab = rope_tables()
    shared = moba_shared_inputs(tab)
    mconst = [moba_const_inputs(0), moba_const_inputs(1)]
    gconst = gdn_const_inputs()
    wnames = ("w_ada", "ffn1_w_gate", "ffn1_w_up", "ffn1_w_down", "w_in", "w_out", "ffn2_w_gate", "ffn2_w_up", "ffn2_w_down")
    lnames = ("b_ada", "ln_ffn1_g", "ln_mix_g", "ln_ffn2_g")
    common = {}
    for l in range(2):
        for n in wnames:
            common[f"{n}{l}"] = np.ascontiguousarray(np.asarray(inp[n][l], np.float32))
        for n in lnames:
            common[f"{n}{l}"] = _lay(inp[n][l])
        cw = np.asarray(inp["conv_w"][l], np.float32)
        lay2 = lambda v: np.ascontiguousarray(np.asarray(v, np.float32).reshape(2, 128).T)
        common[f"L{l}_cw"] = np.ascontiguousarray(cw.T.reshape(2, 128, 31).transpose(1, 0, 2))
        common[f"L{l}_cp"] = np.ascontiguousarray(np.stack([lay2(inp["conv_b"][l]), lay2(inp["conv_ln_g"][l]),
                                                            lay2(inp["conv_ln_b"][l])], axis=-1))
    common["final_g"] = _lay(inp["final_g"])
    common["cident"] = np.eye(128, dtype=np.float32)
    common.update(shared)
    common.update(gconst)
    maps = []
    for c in range(NCORES):
        b, cc = c // 4, c % 4
        m = dict(common)
        m["xT"] = np.ascontiguousarray(x[b, cc * TOK:(cc + 1) * TOK].T)
        m["cT"] = _lay(inp["c"][b])
        m["cflag"] = np.full((128, 1), 0.0 if cc == 0 else 1.0, np.float32)
        units = []
        for su in range(3):
            half = moba_unit(cc, su)[1]
            qpos = np.concatenate([np.arange(bl * 256, (bl + 1) * 256) for bl in HALF_BLOCKS[half]])
            d = dict(mconst[half])
            d["ropeq"] = np.ascontiguousarray(tab[:, :, qpos])
            units.append(d)
        for k_ in units[0]:
            m[k_] = np.ascontiguousarray(np.stack([un[k_] for un in units]))
        for l in range(2):
            heads = [cc, (cc + 4) % 6]
            gw = np.asarray(inp["gdn_conv_w"][l], np.float32)
            gcw = np.zeros((2, 64, 12), np.float32)
            gpar = np.zeros((2, 64, 2), np.float32)
            gng = np.zeros((2, 64, 64), np.float32)
            for g, h in enumerate(heads):
                for j in range(3):
                    gcw[g, :, j * 4:(j + 1) * 4] = gw[:, j * 384 + h * 64:j * 384 + (h + 1) * 64].T
                gpar[g, :, 0] = np.asarray(inp["gdn_a_log"][l], np.float32)[h]
                gpar[g, :, 1] = np.asarray(inp["gdn_dt_bias"][l], np.float32)[h]
                gng[g] = np.asarray(inp["gdn_norm_g"][l], np.float32)[None, :]
            m[f"L{l}_gcw"], m[f"L{l}_gpar"], m[f"L{l}_gng"] = gcw, gpar, gng
        maps.append({k_: m[k_] for k_ in F.in_names})
    res = run_bass_kernel_spmd(nc, maps, core_ids=list(range(NCORES))).results
    out = np.zeros((B, S, D), np.float32)
    for c in range(NCORES):
        out[c // 4, (c % 4) * TOK:(c % 4 + 1) * TOK] = res[c]["xoT"].T
    return out
```

```python
import numpy as np
from contextlib import ExitStack
import concourse.bass as bass
import concourse.mybir as mybir
from concourse.bass_utils import run_bass_kernel_spmd

F32 = mybir.dt.float32
BF16 = mybir.dt.bfloat16
AF = mybir.ActivationFunctionType
ALU = mybir.AluOpType

D = 1024
KC = 8
DFF = 2816
FC = 22
DIN = 3212
B = 2
S = 8192
NCORES = 8
TOK = 2048
EPS = 1e-6

SAME_ENG_SYNC = True


class Buf:
    __slots__ = ("name", "lw", "rd")

    def __init__(self, name=""):
        self.name = name
        self.lw = None
        self.rd = {}


class Sched:
    ENGS = ("tensor", "vector", "scalar", "gpsimd", "sync")
    EPOCH = 20000

    def __init__(self, nc, es):
        self.nc = nc
        self.es = es
        self.q = {e: [] for e in self.ENGS}
        self.cnt = {e: 0 for e in self.ENGS}
        self.seen = {e: {} for e in self.ENGS}
        self.esem = {}
        self.dsem = {}
        self.dcnt = {}
        self.cckeys = set()

    def _get_esem(self, eng, epoch):
        k = (eng, epoch)
        if k not in self.esem:
            self.esem[k] = self.es.enter_context(self.nc.semaphore(f"se_{eng}_{epoch}"))
        return self.esem[k]

    def _get_dsem(self, key):
        if key not in self.dsem:
            self.dsem[key] = self.es.enter_context(self.nc.semaphore(f"sd_{key}"))
            self.dcnt[key] = 0
        return self.dsem[key]

    def _need(self, eng, tok, waits):
        if tok is None:
            return
        kind, k, val = tok
        if kind == "e":
            if k == eng and (eng == "tensor" or not SAME_ENG_SYNC):
                return
        key = (kind, k)
        if self.seen[eng].get(key, 0) >= val:
            return
        self.seen[eng][key] = val
        waits.append(tok)

    def _deps(self, eng, reads, writes):
        waits = []
        for b in reads:
            self._need(eng, b.lw, waits)
        for b in writes:
            self._need(eng, b.lw, waits)
            for k, v in b.rd.items():
                self._need(eng, (k[0], k[1], v), waits)
        return waits

    def _mark(self, tok, reads, writes):
        key = (tok[0], tok[1])
        for b in reads:
            if b.rd.get(key, 0) < tok[2]:
                b.rd[key] = tok[2]
        for b in writes:
            b.lw = tok
            b.rd = {}

    def op(self, eng, fn, reads=(), writes=(), inc=True):
        waits = self._deps(eng, reads, writes)
        idx = self.cnt[eng] + 1
        if inc:
            self.cnt[eng] = idx
        tok = ("e", eng, idx)
        self._mark(tok, reads, writes)
        self.q[eng].append((waits, fn, tok if inc else None))

    def dma(self, qeng, fn, reads=(), writes=(), key="d"):
        waits = self._deps(qeng, reads, writes)
        self._get_dsem(key)
        self.dcnt[key] += 1
        tok = ("d", key, 16 * self.dcnt[key])
        self._mark(tok, reads, writes)
        self.q[qeng].append((waits, fn, tok))

    def cc(self, fn, reads=(), writes=(), key="cc"):
        waits = self._deps("gpsimd", reads, writes)
        self._get_dsem(key)
        self.cckeys.add(key)
        self.dcnt[key] += 1
        tok = ("c", key, self.dcnt[key])
        self._mark(tok, reads, writes)
        self.q["gpsimd"].append((waits, fn, tok))

    def barrier(self, exclude=()):
        for e in self.ENGS:
            waits = []
            for e2 in self.ENGS:
                if e2 != e and self.cnt[e2] > 0:
                    self._need(e, ("e", e2, self.cnt[e2]), waits)
            for key, n in self.dcnt.items():
                if n > 0 and key not in exclude:
                    kind = "c" if key in self.cckeys else "d"
                    self._need(e, (kind, key, n if kind == "c" else 16 * n), waits)
            self.q[e].append((waits, None, None))

    def final_wait(self, eng, toks_bufs):
        waits = self._deps(eng, (), toks_bufs)
        self.q[eng].append((waits, None, None))

    def emit(self, block):
        nc = self.nc

        def run(engname):
            def body(eng):
                for waits, fn, tok in self.q[engname]:
                    for (kind, k, val) in waits:
                        if kind == "e":
                            epoch = (val - 1) // self.EPOCH
                            eng.wait_ge(self._get_esem(k, epoch), val - epoch * self.EPOCH)
                        else:
                            eng.wait_ge(self.dsem[k], val)
                    if fn is None:
                        continue
                    ins = fn(eng)
                    if tok is not None:
                        if tok[0] == "e":
                            epoch = (tok[2] - 1) // self.EPOCH
                            ins.then_inc(self._get_esem(tok[1], epoch), 1)
                        elif tok[0] == "c":
                            ins.then_inc(self.dsem[tok[1]])
                        else:
                            ins.then_inc(self.dsem[tok[1]], 16)
                self.q[engname] = []
            return body

        for e in self.ENGS:
            for ep in range((self.cnt[e] - 1) // self.EPOCH + 1 if self.cnt[e] else 0):
                self._get_esem(e, ep)
        block.tensor(run("tensor"))
        block.vector(run("vector"))
        block.scalar(run("scalar"))
        block.gpsimd(run("gpsimd"))
        block.sync(run("sync"))


class TokProg:
    def __init__(self, stages, tok=TOK, fused=None):
        self.stages = stages
        self.tok = tok
        self.fused = fused
        if fused is None:
            self.nc = bass.Bass("TRN2", target_bir_lowering=False)
            self.es = ExitStack()
        else:
            self.nc = fused.nc
            self.es = fused.es
        self.in_names = []
        self.out_names = []

    def din(self, name, shape, dt=F32):
        if self.fused is not None:
            return self.fused.din(name, shape, dt)
        self.in_names.append(name)
        return self.nc.dram_tensor(name, list(shape), dt, kind="ExternalInput").ap()

    def dout(self, name, shape, dt=F32):
        if self.fused is not None:
            return self.fused.dout(name, shape, dt)
        self.out_names.append(name)
        return self.nc.dram_tensor(name, list(shape), dt, kind="ExternalOutput").ap()

    def sb(self, name, shape, dt):
        if self.fused is not None:
            return self.fused.sb(name, shape, dt)
        return self.es.enter_context(self.nc.sbuf_tensor(name, list(shape), dt))

    def build(self):
        nc, es = self.nc, self.es
        fz = self.fused
        T = self.tok
        NH = T // 1024
        stages = self.stages
        layers = sorted({s[1] for s in stages if len(s) > 1})
        need_v = {}
        for s in stages:
            if s[0] == "ffn1":
                need_v.setdefault(s[1], set()).update([0, 1, 2])
            elif s[0] == "uproj":
                need_v.setdefault(s[1], set()).update([3, 4])
            elif s[0] == "wout":
                need_v.setdefault(s[1], set()).update([5])
            elif s[0] == "ffn2":
                need_v.setdefault(s[1], set()).update([6, 7, 8])

        xT_d = self.din("xT", [D, T]) if (fz is None or fz.load_x) else None
        cT_d = self.din("cT", [128, KC])
        W = {}
        for l in layers:
            W[("w_ada", l)] = self.din(f"w_ada{l}", [D, 9 * D])
            W[("b_ada", l)] = self.din(f"b_ada{l}", [128, 72])
        for s in stages:
            if s[0] in ("ffn1", "ffn2"):
                l = s[1]
                n = s[0]
                W[(n + "_g", l)] = self.din(f"ln_{n}_g{l}", [128, KC])
                W[(n + "_wg", l)] = self.din(f"{n}_w_gate{l}", [D, DFF])
                W[(n + "_wu", l)] = self.din(f"{n}_w_up{l}", [D, DFF])
                W[(n + "_wd", l)] = self.din(f"{n}_w_down{l}", [DFF, D])
            elif s[0] == "uproj":
                l = s[1]
                W[("mix_g", l)] = self.din(f"ln_mix_g{l}", [128, KC])
                W[("w_in", l)] = self.din(f"w_in{l}", [D, DIN])
                W[("uT", l)] = self.dout(f"uT{l}", [DIN, T]) if fz is None else fz.uT_dst
            elif s[0] == "wout":
                l = s[1]
                W[("w_out", l)] = self.din(f"w_out{l}", [D, D])
                W[("yT", l)] = self.din(f"yT{l}", [D, T]) if fz is None else fz.yT_src
            elif s[0] == "final":
                W[("final_g",)] = self.din("final_g", [128, KC])
        xo_d = self.dout("xoT", [D, T]) if (fz is None or fz.store_x) else None

        x = self.sb("x", [128, KC, T], F32) if fz is None else fz.x
        h = self.sb("h", [128, KC, 1024], BF16)
        act = self.sb("act", [128, FC, 1024], BF16)
        wd = self.sb("wd", [128, FC, D], BF16)
        NSLOT = 4
        SLOTW = 256
        wslot = [self.sb(f"ws{i}", [128, KC, SLOTW], BF16) for i in range(NSLOT)]
        tmpA = [self.sb(f"tmpA{i}", [128, 512], F32) for i in range(2)]
        tmpB = [self.sb(f"tmpB{i}", [128, 512], F32) for i in range(2)]
        sqb = [self.sb(f"sq{i}", [128, 512], BF16) for i in range(2)]
        rstd = self.sb("rstd", [128, 512], F32)
        ones = self.sb("ones", [128, 128], BF16)
        cT = self.sb("cT_sb", [128, KC], F32)
        cact = self.sb("cact", [128, KC], BF16)
        bada = {l: self.sb(f"bada{l}", [128, 72], F32) for l in layers}
        mod = {l: self.sb(f"mod{l}", [128, 72], F32) for l in layers}
        gains = {}
        for k in W:
            if k[0] in ("ffn1_g", "ffn2_g", "mix_g", "final_g"):
                gains[k] = self.sb("g_" + "_".join(map(str, k)), [128, KC], F32)
        coefA = {}
        coefG = {}
        ps = es.enter_context(nc.psum_tensor("ps", [128, 8, 512], F32)) if fz is None else fz.ps

        sc = Sched(nc, es) if fz is None else fz.sc
        bx = [[Buf(f"x{c}_{t}") for t in range(T // 512)] for c in range(KC)] if fz is None else fz.bx
        bU = [] if fz is None else [fz.bU]
        bY = [] if fz is None else [fz.bY]
        bh = [Buf(f"h{t}") for t in range(2)]
        bact = [[Buf(f"act{f}_{t}") for t in range(2)] for f in range(FC)]
        WD_PIECES = ((0, 6), (6, 12), (12, 17), (17, 22))
        bwd = [Buf(f"wd{i}") for i in range(4)]
        wd_piece = {}
        for i, (f0, f1) in enumerate(WD_PIECES):
            for f in range(f0, f1):
                wd_piece[f] = i
        bws = [Buf(f"ws{i}") for i in range(NSLOT)]
        btA = [Buf() for _ in range(2)]
        btB = [Buf() for _ in range(2)]
        bsq = [Buf() for _ in range(2)]
        brstd = Buf()
        bones = Buf()
        bps = [Buf(f"ps{i}") for i in range(8)]
        bmisc = Buf("misc")
        bmod = Buf("mod")

        if xT_d is not None:
            xT_v = xT_d.rearrange("(c p) t -> p c t", p=128)
            for c in range(KC):
                sc.dma("sync", lambda e, c=c: e.dma_start(out=x[:, c, :], in_=xT_v[:, c, :]),
                       writes=bx[c], key=f"x{c}")
        sc.dma("sync", lambda e: e.dma_start(out=cT[:], in_=cT_d[:, :]), writes=[bmisc], key="misc")
        for l in layers:
            sc.dma("sync", lambda e, l=l: e.dma_start(out=bada[l][:], in_=W[("b_ada", l)][:, :]),
                   writes=[bmisc], key="misc")
        for k, t in gains.items():
            sc.dma("sync", lambda e, k=k, t=t: e.dma_start(out=t[:], in_=W[k][:, :]), writes=[bmisc], key="misc")
        sc.op("vector", lambda e: e.memset(ones[:], 1.0), writes=[bones])
        sc.op("scalar", lambda e: e.activation(out=cact[:], in_=cT[:], func=AF.Silu), reads=[bmisc], writes=[bmod])

        wslot_i = [0]

        def next_slot():
            i = wslot_i[0] % NSLOT
            wslot_i[0] += 1
            return i

        def load_cols(Wd, c0, ncols, nk=KC):
            i = next_slot()
            src = Wd.rearrange("(k p) n -> p k n", p=128)
            sc.dma("gpsimd", lambda e, i=i: e.dma_start(out=wslot[i][:, 0:nk, 0:ncols], in_=src[:, :, c0:c0 + ncols]),
                   writes=[bws[i]], key=f"ws{i}")
            return i

        mod_ps = ps[:, 7, 0:72]
        for l in layers:
            for v in sorted(need_v[l]):
                for hh in range(4):
                    si = load_cols(W[("w_ada", l)], v * 1024 + hh * 256, 256)
                    for jj in range(2):
                        j = hh * 2 + jj
                        col = v * 8 + j
                        for kc in range(KC):
                            sc.op("tensor",
                                  lambda e, si=si, jj=jj, kc=kc, col=col: e.matmul(
                                      ps[:, 7, col:col + 1], lhsT=wslot[si][:, kc, jj * 128:(jj + 1) * 128],
                                      rhs=cact[:, kc:kc + 1], start=(kc == 0), stop=(kc == KC - 1)),
                                  reads=[bws[si], bmod], writes=[bps[7]], inc=(kc == KC - 1))
            sc.op("vector", lambda e, l=l: e.tensor_tensor(out=mod[l][:], in0=mod_ps, in1=bada[l][:], op=ALU.add),
                  reads=[bps[7], bmisc], writes=[bmod])
            for (gk, vs, vg, half) in ((("ffn1_g", l), 1, 2, 0.5), (("mix_g", l), 4, None, None),
                                       (("ffn2_g", l), 7, 8, 0.5)):
                if gk in gains:
                    a = self.sb("cA_" + "_".join(map(str, gk)), [128, KC], F32)
                    coefA[gk] = a
                    sc.op("vector", lambda e, a=a, gk=gk, vs=vs, l=l: e.scalar_tensor_tensor(
                        out=a[:], in0=mod[l][:, vs * 8:vs * 8 + 8], scalar=1.0, in1=gains[gk][:],
                        op0=ALU.add, op1=ALU.mult), reads=[bmod, bmisc], writes=[bmod])
                    if vg is not None:
                        g = self.sb("cG_" + "_".join(map(str, gk)), [128, KC], F32)
                        coefG[gk] = g
                        sc.op("vector", lambda e, g=g, vg=vg, l=l: e.tensor_scalar(
                            out=g[:], in0=mod[l][:, vg * 8:vg * 8 + 8], scalar1=0.5, scalar2=None, op0=ALU.mult),
                            reads=[bmod], writes=[bmod])

        psi = [0]

        def next_ps(pool):
            i = pool[psi[0] % len(pool)]
            psi[0] += 1
            return i

        def norm_mod(half, A_ap, sh_ap):
            for tt in range(2):
                t0 = half * 1024 + tt * 512
                ti = t0 // 512
                pb = 6
                for c in range(KC):
                    s = c % 2
                    sc.op("scalar", lambda e, c=c, s=s, t0=t0: e.activation(out=sqb[s][:], in_=x[:, c, t0:t0 + 512],
                                                                            func=AF.Square),
                          reads=[bx[c][ti]], writes=[bsq[s]])
                    sc.op("tensor", lambda e, c=c, s=s: e.matmul(ps[:, pb, :], lhsT=ones[:], rhs=sqb[s][:],
                                                                 start=(c == 0), stop=(c == KC - 1)),
                          reads=[bones, bsq[s]], writes=[bps[pb]])
                sc.op("scalar", lambda e: e.activation(out=tmpA[0][:], in_=ps[:, pb, :], func=AF.Sqrt,
                                                       bias=eps_t[:, 0:1], scale=1.0 / D),
                      reads=[bps[pb], bmisc], writes=[btA[0]])
                sc.op("vector", lambda e: e.reciprocal(out=rstd[:], in_=tmpA[0][:]), reads=[btA[0]], writes=[brstd])
                for c in range(KC):
                    s = c % 2
                    sc.op("vector", lambda e, c=c, s=s, t0=t0: e.scalar_tensor_tensor(
                        out=tmpB[s][:], in0=x[:, c, t0:t0 + 512], scalar=A_ap[:, c:c + 1], in1=rstd[:],
                        op0=ALU.mult, op1=ALU.mult), reads=[bx[c][ti], brstd, bmod], writes=[btB[s]])
                    if sh_ap is not None:
                        sc.op("scalar", lambda e, c=c, s=s, tt=tt: e.activation(
                            out=h[:, c, tt * 512:(tt + 1) * 512], in_=tmpB[s][:], func=AF.Identity,
                            bias=sh_ap[:, c:c + 1], scale=1.0), reads=[btB[s], bmod], writes=[bh[tt]])

        def ffn(half, n, l):
            A = coefA[(n + "_g", l)]
            G = coefG[(n + "_g", l)]
            vsh = 0 if n == "ffn1" else 6
            sh = mod[l][:, vsh * 8:vsh * 8 + 8]
            norm_mod(half, A, sh)
            wdv = W[(n + "_wd", l)].rearrange("(f p) n -> p f n", p=128)
            for i, (f0, f1) in enumerate(WD_PIECES):
                sc.dma("gpsimd", lambda e, f0=f0, f1=f1: e.dma_start(out=wd[:, f0:f1, :], in_=wdv[:, f0:f1, :]),
                       writes=[bwd[i]], key=f"wd{i}")
            groups = [(g * 2, 2) for g in range(11)]
            loaded = {}

            def load_group(gi):
                f0, nf = groups[gi]
                loaded[gi] = (load_cols(W[(n + "_wg", l)], f0 * 128, nf * 128),
                              load_cols(W[(n + "_wu", l)], f0 * 128, nf * 128))
            load_group(0)
            for gi, (f0, nf) in enumerate(groups):
                if gi + 1 < len(groups):
                    load_group(gi + 1)
                sg, su = loaded[gi]
                for fi in range(nf):
                    f = f0 + fi
                    for tt in range(2):
                        pg = next_ps([0, 1])
                        pu = pg + 2
                        for kc in range(KC):
                            sc.op("tensor", lambda e, sg=sg, fi=fi, kc=kc, tt=tt, pg=pg: e.matmul(
                                ps[:, pg, :], lhsT=wslot[sg][:, kc, fi * 128:(fi + 1) * 128],
                                rhs=h[:, kc, tt * 512:(tt + 1) * 512], start=(kc == 0), stop=(kc == KC - 1)),
                                reads=[bws[sg], bh[tt]], writes=[bps[pg]], inc=(kc == KC - 1))
                        for kc in range(KC):
                            sc.op("tensor", lambda e, su=su, fi=fi, kc=kc, tt=tt, pu=pu: e.matmul(
                                ps[:, pu, :], lhsT=wslot[su][:, kc, fi * 128:(fi + 1) * 128],
                                rhs=h[:, kc, tt * 512:(tt + 1) * 512], start=(kc == 0), stop=(kc == KC - 1)),
                                reads=[bws[su], bh[tt]], writes=[bps[pu]], inc=(kc == KC - 1))
                        s = pg
                        sc.op("scalar", lambda e, s=s, pg=pg: e.activation(out=tmpA[s][:], in_=ps[:, pg, :],
                                                                           func=AF.Silu),
                              reads=[bps[pg]], writes=[btA[s]])
                        sc.op("vector", lambda e, s=s, pu=pu, f=f, tt=tt: e.tensor_tensor(
                            out=act[:, f, tt * 512:(tt + 1) * 512], in0=tmpA[s][:], in1=ps[:, pu, :], op=ALU.mult),
                            reads=[btA[s], bps[pu]], writes=[bact[f][tt]])
            for tt in range(2):
                t0 = half * 1024 + tt * 512
                ti = t0 // 512
                for d in range(KC):
                    pd = next_ps([4, 5])
                    for f in range(FC):
                        sc.op("tensor", lambda e, f=f, d=d, tt=tt, pd=pd: e.matmul(
                            ps[:, pd, :], lhsT=wd[:, f, d * 128:(d + 1) * 128], rhs=act[:, f, tt * 512:(tt + 1) * 512],
                            start=(f == 0), stop=(f == FC - 1)),
                            reads=[bwd[wd_piece[f]], bact[f][tt]], writes=[bps[pd]], inc=(f == FC - 1))
                    sc.op("vector", lambda e, d=d, t0=t0, pd=pd: e.scalar_tensor_tensor(
                        out=x[:, d, t0:t0 + 512], in0=ps[:, pd, :], scalar=G[:, d:d + 1], in1=x[:, d, t0:t0 + 512],
                        op0=ALU.mult, op1=ALU.add), reads=[bps[pd], bx[d][ti], bmod], writes=[bx[d][ti]])

        ostage = [self.sb(f"ost{i}", [128, 512], F32) for i in range(2)]
        bost = [Buf() for _ in range(2)]
        osi = [0]

        def uproj(half, l):
            A = coefA[("mix_g", l)]
            sh = mod[l][:, 3 * 8:3 * 8 + 8]
            norm_mod(half, A, sh)
            uT = W[("uT", l)]
            ngr = (DIN + 255) // 256
            loaded = {}

            def load_group(gi):
                c0 = gi * 256
                loaded[gi] = load_cols(W[("w_in", l)], c0, min(256, DIN - c0))
            load_group(0)
            for gi in range(ngr):
                if gi + 1 < ngr:
                    load_group(gi + 1)
                si = loaded[gi]
                c0 = gi * 256
                ncol = min(256, DIN - c0)
                for fi in range((ncol + 127) // 128):
                    m = min(128, ncol - fi * 128)
                    for tt in range(2):
                        t0 = half * 1024 + tt * 512
                        pg = next_ps([0, 1, 2, 3])
                        for kc in range(KC):
                            sc.op("tensor", lambda e, si=si, fi=fi, kc=kc, tt=tt, pg=pg, m=m: e.matmul(
                                ps[0:m, pg, :], lhsT=wslot[si][:, kc, fi * 128:fi * 128 + m],
                                rhs=h[:, kc, tt * 512:(tt + 1) * 512], start=(kc == 0), stop=(kc == KC - 1)),
                                reads=[bws[si], bh[tt]], writes=[bps[pg]], inc=(kc == KC - 1))
                        o = osi[0] % 2
                        osi[0] += 1
                        eng = "scalar" if o == 0 else "vector"
                        if eng == "scalar":
                            sc.op("scalar", lambda e, o=o, pg=pg, m=m: e.copy(out=ostage[o][0:m, :], in_=ps[0:m, pg, :]),
                                  reads=[bps[pg]], writes=[bost[o]])
                        else:
                            sc.op("vector", lambda e, o=o, pg=pg, m=m: e.tensor_copy(out=ostage[o][0:m, :],
                                                                                     in_=ps[0:m, pg, :]),
                                  reads=[bps[pg]], writes=[bost[o]])
                        r0 = c0 + fi * 128
                        sc.dma("sync", lambda e, o=o, m=m, r0=r0, t0=t0: e.dma_start(
                            out=uT[r0:r0 + m, t0:t0 + 512], in_=ostage[o][0:m, :]), reads=[bost[o]] + bU, key=f"ost{o}")

        ystage = [act[:, i * 8:(i + 1) * 8, 0:512] for i in range(2)]
        byst = [[bact[f][0] for f in range(i * 8, (i + 1) * 8)] for i in range(2)]

        def wout(half, l):
            yT = W[("yT", l)].rearrange("(c p) t -> p c t", p=128)
            wsl = [load_cols(W[("w_out", l)], q * 256, 256) for q in range(4)]
            G = mod[l][:, 5 * 8:5 * 8 + 8]
            for tt in range(2):
                t0 = half * 1024 + tt * 512
                ti = t0 // 512
                sc.dma("gpsimd", lambda e, tt=tt, t0=t0: e.dma_start(out=ystage[tt], in_=yT[:, :, t0:t0 + 512]),
                       reads=bY, writes=byst[tt], key=f"yst{tt}")
                for d in range(KC):
                    si = wsl[d // 2]
                    dj = d % 2
                    pd = next_ps([4, 5])
                    for kc in range(KC):
                        sc.op("tensor", lambda e, si=si, dj=dj, kc=kc, tt=tt, pd=pd: e.matmul(
                            ps[:, pd, :], lhsT=wslot[si][:, kc, dj * 128:(dj + 1) * 128], rhs=ystage[tt][:, kc, :],
                            start=(kc == 0), stop=(kc == KC - 1)),
                            reads=[bws[si]] + byst[tt], writes=[bps[pd]], inc=(kc == KC - 1))
                    sc.op("vector", lambda e, d=d, t0=t0, pd=pd: e.scalar_tensor_tensor(
                        out=x[:, d, t0:t0 + 512], in0=ps[:, pd, :], scalar=G[:, d:d + 1], in1=x[:, d, t0:t0 + 512],
                        op0=ALU.mult, op1=ALU.add), reads=[bps[pd], bx[d][ti], bmod], writes=[bx[d][ti]])

        def final(half):
            g = gains[("final_g",)]
            for tt in range(2):
                t0 = half * 1024 + tt * 512
                ti = t0 // 512
                pb = 6
                for c in range(KC):
                    s = c % 2
                    sc.op("scalar", lambda e, c=c, s=s, t0=t0: e.activation(out=sqb[s][:], in_=x[:, c, t0:t0 + 512],
                                                                            func=AF.Square),
                          reads=[bx[c][ti]], writes=[bsq[s]])
                    sc.op("tensor", lambda e, c=c, s=s: e.matmul(ps[:, pb, :], lhsT=ones[:], rhs=sqb[s][:],
                                                                 start=(c == 0), stop=(c == KC - 1)),
                          reads=[bones, bsq[s]], writes=[bps[pb]])
                sc.op("scalar", lambda e: e.activation(out=tmpA[0][:], in_=ps[:, pb, :], func=AF.Sqrt,
                                                       bias=eps_t[:, 0:1], scale=1.0 / D),
                      reads=[bps[pb], bmisc], writes=[btA[0]])
                sc.op("vector", lambda e: e.reciprocal(out=rstd[:], in_=tmpA[0][:]), reads=[btA[0]], writes=[brstd])
                for c in range(KC):
                    sc.op("vector", lambda e, c=c, t0=t0: e.scalar_tensor_tensor(
                        out=x[:, c, t0:t0 + 512], in0=x[:, c, t0:t0 + 512], scalar=g[:, c:c + 1], in1=rstd[:],
                        op0=ALU.mult, op1=ALU.mult), reads=[bx[c][ti], brstd, bmisc], writes=[bx[c][ti]])

        eps_t = self.sb("eps_t", [128, 1], F32)
        sc.op("vector", lambda e: e.memset(eps_t[:], EPS), writes=[bmisc])

        for half in range(NH):
            for s in stages:
                if s[0] in ("ffn1", "ffn2"):
                    ffn(half, s[0], s[1])
                elif s[0] == "uproj":
                    uproj(half, s[1])
                elif s[0] == "wout":
                    wout(half, s[1])
                elif s[0] == "final":
                    final(half)

        allb = []
        if xo_d is not None:
            xo_v = xo_d.rearrange("(c p) t -> p c t", p=128)
            for c in range(KC):
                sc.dma("sync", lambda e, c=c: e.dma_start(out=xo_v[:, c, :], in_=x[:, c, :]), reads=bx[c], key="xo")
                allb += bx[c]
        if fz is not None:
            return allb
        sc.final_wait("sync", allb + bost)

        with nc.Block() as block:
            sc.emit(block)
        es.close()
        return nc


NBLK = 32
MOBA_SLOTS = 16
HALF_BLOCKS = ([b for b in range(NBLK) if b % 4 in (0, 3)], [b for b in range(NBLK) if b % 4 in (1, 2)])
NEG = -30000.0


def moba_emit(P, sc, nu, pfx="", fz=None):
    nc, es = P.nc, P.es
    NQ = MOBA_SLOTS * 256
    if fz is None:
        mq = P.din(pfx + "mq", [nu, 64, NQ])
        mqs = P.din(pfx + "mqs", [nu, 16, NQ])
        mk = P.din(pfx + "mk", [nu, 64, S])
        mks = P.din(pfx + "mks", [nu, 16, S])
        mv = P.din(pfx + "mv", [nu, S, 64])
        yo = P.dout(pfx + "moT", [nu, 64, NQ])
    cq = P.din("ropeq", [nu, 2, 16, NQ])
    ck = P.din("ropek", [2, 16, S])
    pm_d = P.din("pm", [nu, 128, MOBA_SLOTS * NBLK])
    oh_d = P.din("oh", [nu, 128, MOBA_SLOTS * NBLK])
    cm_d = P.din("cm", [nu, 2, 4, 128, 256])
    boh_d = P.din("boh", [32, S])
    id_d = P.din("ident", [128, 128])

    qaug = P.sb(pfx + "qaug", [128, NQ], BF16)
    kaug = P.sb(pfx + "kaug", [128, S], BF16)
    vaug = P.sb(pfx + "vaug", [128, 64, 128], BF16)
    qf = P.sb(pfx + "qf", [64, NQ], F32)
    xt = [P.sb(pfx + f"xt{i}", [64, 1024], F32) for i in range(2)]
    xs = [P.sb(pfx + f"xs{i}", [16, 1024], F32) for i in range(2)]
    ct = [P.sb(pfx + f"ct{i}", [16, 2, 1024], F32) for i in range(2)]
    t16 = P.sb(pfx + "t16", [16, 1024], F32)
    sqf = P.sb(pfx + "sqf", [64, 1024], F32)
    kmean = P.sb(pfx + "kmean", [64, NBLK], F32)
    mx = P.sb(pfx + "mx", [128, 4], F32)
    nbias = P.sb(pfx + "nbias", [128, 1], F32)
    onesf = P.sb(pfx + "onesf", [64, 128], F32)
    ident = P.sb(pfx + "ident_sb", [128, 128], F32)
    pm = P.sb(pfx + "pm_sb", [128, MOBA_SLOTS * NBLK], F32)
    oh = P.sb(pfx + "oh_sb", [128, MOBA_SLOTS * NBLK], F32)
    cm = P.sb(pfx + "cm_sb", [128, 8, 256], F32)
    gs = P.sb(pfx + "gs", [128, NBLK], F32)
    g8 = P.sb(pfx + "g8", [128, 8], F32)
    m1 = P.sb(pfx + "m1", [128, NBLK], F32)
    m2 = P.sb(pfx + "m2", [128, NBLK], F32)
    stm = [P.sb(pfx + f"stm{i}", [128, 256], F32) for i in range(2)]
    pt = [P.sb(pfx + f"pt{i}", [128, 256], BF16) for i in range(4)]
    rec = P.sb(pfx + "rec", [64, 256], F32)
    yst = [P.sb(pfx + f"yst{i}", [64, 256], F32) for i in range(2)]
    ps = es.enter_context(nc.psum_tensor(pfx + "mps", [128, 8, 512], F32)) if fz is None else fz.ps
    if fz is not None:
        vt = P.sb(pfx + "vt", [64, 1024], F32)
        bvt = Buf()

    def rows6(e, u, r0, nr, cols):
        return fz.Urecv[r0:r0 + 5 * 64 + nr, cols][bass.ds(fz.dyn(e, "sync", ("mhr", u)), nr), :]

    bq, bk, bv, bqf = Buf(), Buf(), Buf(), Buf()
    bxt = [Buf(), Buf()]
    bxs = [Buf(), Buf()]
    bct = [Buf(), Buf()]
    bt16, bsqf, bkm, bmx, bnb, bconst, bmask = Buf(), Buf(), Buf(), Buf(), Buf(), Buf(), Buf()
    bgs, bg8, bm1, bm2 = Buf(), Buf(), Buf(), Buf()
    bstm = [Buf(), Buf()]
    bpt = [Buf(), Buf(), Buf(), Buf()]
    brec = Buf()
    byst = [Buf(), Buf()]
    bps = [Buf() for _ in range(8)]

    sc.dma("sync", lambda e: e.dma_start(out=ident[:], in_=id_d[:, :]), writes=[bconst], key=pfx + "mconst")
    sc.op("vector", lambda e: e.memset(onesf[:], 1.0), writes=[bconst])
    sc.op("vector", lambda e: e.memset(kaug[32:64, :], 0.0), writes=[bk])
    sc.op("vector", lambda e: e.memset(kaug[32:33, :], 1.0), writes=[bk])
    sc.dma("gpsimd", lambda e: e.dma_start(out=kaug[0:32, :], in_=boh_d[:, :]), writes=[bk], key=pfx + "mk0")
    sc.op("vector", lambda e: e.memset(qaug[32:64, :], 0.0), writes=[bq])
    sc.op("vector", lambda e: e.memset(vaug[:, :, 64:128], 1.0), writes=[bv])

    cnt = [0]
    for u in range(nu):
        sc.dma("sync", lambda e, u=u: e.dma_start(out=pm[:], in_=pm_d[u]), writes=[bmask], key=pfx + "mmask")
        sc.dma("sync", lambda e, u=u: e.dma_start(out=oh[:], in_=oh_d[u]), writes=[bmask], key=pfx + "mmask")
        sc.dma("sync", lambda e, u=u: e.dma_start(out=cm[:], in_=cm_d[u].rearrange("a k p q -> p (a k) q")),
               writes=[bmask], key=pfx + "mmask")
        if fz is None:
            for k0 in range(0, 64, 16):
                sc.dma("gpsimd", lambda e, u=u, k0=k0: e.dma_start(
                    out=vaug[:, k0:k0 + 16, 0:64], in_=mv[u].rearrange("(k p) d -> p k d", p=128)[:, k0:k0 + 16, :]),
                    writes=[bv], key=pfx + "mv")
        else:
            for c0 in range(0, S, 1024):
                rr, t0 = c0 // TOK, c0 % TOK

                def vsrc(e, u=u, c0=c0):
                    return fz.LV[u, :, c0:c0 + 1024]
                sc.dma("sync", lambda e, vsrc=vsrc: e.dma_start(out=vt[:], in_=vsrc(e)), reads=[fz.bUr], writes=[bvt],
                       key=pfx + "mvt")
                for cj in range(8):
                    sc.op("tensor", lambda e, cj=cj: e.transpose(ps[:, 6, cj * 64:(cj + 1) * 64], vt[:, cj * 128:(cj + 1) * 128],
                                                                 ident[0:64, 0:64]),
                          reads=[bvt, bconst], writes=[bps[6]], inc=(cj == 7))
                k0 = c0 // 128
                sc.op("vector", lambda e, k0=k0: e.tensor_copy(out=vaug[:, k0:k0 + 8, 0:64],
                                                               in_=ps[:, 6, :].rearrange("p (a b) -> p a b", b=64)),
                      reads=[bps[6]], writes=[bv])
        sc.op("vector", lambda e: e.memset(mx[:], 0.0), writes=[bmx])
        srcs_ = ((mk, mks, None, S), (mq, mqs, cq, NQ)) if fz is None else ((None, None, None, S), (None, None, cq, NQ))
        for which, (src, srcs, tab, ncols) in enumerate(srcs_):
            for c0 in range(0, ncols, 1024):
                i = cnt[0] % 2
                cnt[0] += 1
                if fz is None:
                    sc.dma("sync", lambda e, i=i, c0=c0, src=src, u=u: e.dma_start(out=xt[i][:], in_=src[u, :, c0:c0 + 1024]),
                           writes=[bxt[i]], key=pfx + f"mxt{i}")
                    sc.dma("sync", lambda e, i=i, c0=c0, srcs=srcs, u=u: e.dma_start(out=xs[i][:], in_=srcs[u, :, c0:c0 + 1024]),
                           writes=[bxs[i]], key=pfx + f"mxs{i}")
                elif which == 0:
                    rr, t0 = c0 // TOK, c0 % TOK

                    def ksrc(e, ro, nr, u=u, c0=c0):
                        return fz.LK[u, ro:ro + nr, c0:c0 + 1024]
                    sc.dma("sync", lambda e, i=i, ksrc=ksrc: e.dma_start(out=xt[i][:], in_=ksrc(e, 0, 64)),
                           reads=[fz.bUr], writes=[bxt[i]], key=pfx + f"mxt{i}")
                    sc.dma("sync", lambda e, i=i, ksrc=ksrc: e.dma_start(out=xs[i][0:8, :], in_=ksrc(e, 8, 8)),
                           reads=[fz.bUr], writes=[bxs[i]], key=pfx + f"mxs{i}")
                    sc.dma("sync", lambda e, i=i, ksrc=ksrc: e.dma_start(out=xs[i][8:16, :], in_=ksrc(e, 0, 8)),
                           reads=[fz.bUr], writes=[bxs[i]], key=pfx + f"mxs{i}")
                else:
                    def qsrc(e, ro, nr, u=u, c0=c0):
                        return fz.LQ[u, ro:ro + nr, c0:c0 + 1024]
                    sc.dma("sync", lambda e, i=i, qsrc=qsrc: e.dma_start(out=xt[i][:], in_=qsrc(e, 0, 64)),
                           reads=[fz.bUr], writes=[bxt[i]], key=pfx + f"mxt{i}")
                    sc.dma("sync", lambda e, i=i, qsrc=qsrc: e.dma_start(out=xs[i][0:8, :], in_=qsrc(e, 8, 8)),
                           reads=[fz.bUr], writes=[bxs[i]], key=pfx + f"mxs{i}")
                    sc.dma("sync", lambda e, i=i, qsrc=qsrc: e.dma_start(out=xs[i][8:16, :], in_=qsrc(e, 0, 8)),
                           reads=[fz.bUr], writes=[bxs[i]], key=pfx + f"mxs{i}")
                if which == 0:
                    sc.dma("sync", lambda e, i=i, c0=c0: e.dma_start(
                        out=ct[i][:], in_=ck[:, :, c0:c0 + 1024].rearrange("a p t -> p a t")),
                        writes=[bct[i]], key=pfx + f"mct{i}")
                else:
                    sc.dma("sync", lambda e, i=i, c0=c0, u=u: e.dma_start(
                        out=ct[i][:], in_=cq[u, :, :, c0:c0 + 1024].rearrange("a p t -> p a t")),
                        writes=[bct[i]], key=pfx + f"mct{i}")
                sc.op("vector", lambda e, i=i: e.tensor_tensor(out=t16[:], in0=xs[i][:], in1=ct[i][:, 1, :], op=ALU.mult),
                      reads=[bxs[i], bct[i]], writes=[bt16])
                sc.op("vector", lambda e, i=i: e.tensor_tensor(out=xt[i][0:16, :], in0=xt[i][0:16, :], in1=ct[i][:, 0, :],
                                                               op=ALU.mult), reads=[bxt[i], bct[i]], writes=[bxt[i]])
                sc.op("vector", lambda e, i=i: e.tensor_tensor(out=xt[i][0:16, :], in0=xt[i][0:16, :], in1=t16[:],
                                                               op=ALU.add), reads=[bxt[i], bt16], writes=[bxt[i]])
                sc.op("scalar", lambda e, i=i: e.activation(out=sqf[:], in_=xt[i][:], func=AF.Square),
                      reads=[bxt[i]], writes=[bsqf])
                for hh in range(2):
                    sc.op("tensor", lambda e, hh=hh: e.matmul(ps[:, 6, :], lhsT=onesf[:], rhs=sqf[:, hh * 512:(hh + 1) * 512],
                                                              start=True, stop=True), reads=[bconst, bsqf], writes=[bps[6]])
                    sc.op("vector", lambda e, which=which: e.tensor_reduce(out=mx[:, 2:3], in_=ps[:, 6, :], axis=mybir.AxisListType.X,
                                                                           op=ALU.max), reads=[bps[6]], writes=[bmx])
                    sc.op("vector", lambda e, which=which: e.tensor_tensor(out=mx[:, which:which + 1], in0=mx[:, which:which + 1],
                                                                           in1=mx[:, 2:3], op=ALU.max), reads=[bmx], writes=[bmx])
                if which == 0:
                    nb0 = c0 // 256
                    sc.op("vector", lambda e, i=i, nb0=nb0: e.tensor_reduce(
                        out=kmean[:, nb0:nb0 + 4], in_=xt[i][:].rearrange("p (n t) -> p n t", t=256),
                        axis=mybir.AxisListType.X, op=ALU.add), reads=[bxt[i]], writes=[bkm])
                    sc.op("scalar", lambda e, i=i, c0=c0: e.copy(out=kaug[64:128, c0:c0 + 1024], in_=xt[i][:]),
                          reads=[bxt[i]], writes=[bk])
                else:
                    sc.op("scalar", lambda e, i=i, c0=c0: e.mul(out=qaug[64:128, c0:c0 + 1024], in_=xt[i][:], mul=0.125),
                          reads=[bxt[i]], writes=[bq])
                    sc.op("vector", lambda e, i=i, c0=c0: e.tensor_copy(out=qf[:, c0:c0 + 1024], in_=xt[i][:]),
                          reads=[bxt[i]], writes=[bqf])
        sc.op("vector", lambda e: e.tensor_tensor(out=mx[:, 3:4], in0=mx[:, 0:1], in1=mx[:, 1:2], op=ALU.mult),
              reads=[bmx], writes=[bmx])
        sc.op("scalar", lambda e: e.activation(out=mx[:, 3:4], in_=mx[:, 3:4], func=AF.Sqrt), reads=[bmx], writes=[bmx])
        sc.op("vector", lambda e: e.tensor_scalar(out=nbias[:], in0=mx[:, 3:4], scalar1=-0.125, scalar2=None, op0=ALU.mult),
              reads=[bmx], writes=[bnb])
        for t in range(NQ // 128):
            r = t // 2
            sc.op("tensor", lambda e, t=t: e.matmul(ps[:, 7, 0:NBLK], lhsT=qf[:, t * 128:(t + 1) * 128], rhs=kmean[:],
                                                    start=True, stop=True), reads=[bqf, bkm], writes=[bps[7]])
            sc.op("vector", lambda e, r=r: e.tensor_tensor(out=gs[:], in0=ps[:, 7, 0:NBLK], in1=pm[:, r * NBLK:(r + 1) * NBLK],
                                                           op=ALU.add), reads=[bps[7], bmask], writes=[bgs])
            sc.op("vector", lambda e: e.max(out=g8[:], in_=gs[:]), reads=[bgs], writes=[bg8])
            sc.op("vector", lambda e: e.tensor_scalar(out=m1[:], in0=gs[:], scalar1=g8[:, 2:3], scalar2=None, op0=ALU.is_ge),
                  reads=[bgs, bg8], writes=[bm1])
            sc.op("vector", lambda e: e.tensor_scalar(out=m2[:], in0=gs[:], scalar1=-1e29, scalar2=None, op0=ALU.is_gt),
                  reads=[bgs], writes=[bm2])
            sc.op("vector", lambda e: e.tensor_tensor(out=m1[:], in0=m1[:], in1=m2[:], op=ALU.mult),
                  reads=[bm1, bm2], writes=[bm1])
            sc.op("vector", lambda e, r=r: e.tensor_tensor(out=m1[:], in0=m1[:], in1=oh[:, r * NBLK:(r + 1) * NBLK], op=ALU.add),
                  reads=[bm1, bmask], writes=[bm1])
            sc.op("vector", lambda e: e.tensor_scalar(out=m2[:], in0=m1[:], scalar1=-1.0, scalar2=-NEG, op0=ALU.add, op1=ALU.mult),
                  reads=[bm1], writes=[bm2])
            sc.op("tensor", lambda e: e.transpose(ps[0:NBLK, 7, 128:256], m2[:], ident[:]),
                  reads=[bm2, bconst], writes=[bps[7]])
            sc.op("vector", lambda e, t=t: e.tensor_copy(out=qaug[0:32, t * 128:(t + 1) * 128], in_=ps[0:NBLK, 7, 128:256]),
                  reads=[bps[7]], writes=[bq])
        tasks = [(r, kt) for r in range(MOBA_SLOTS) for kt in range(4 * r + 4)]
        NB, DEPTH = 4, 3

        def qk(i):
            r, kt = tasks[i]
            KT = 4 * r + 4
            p = i % NB
            sc.op("tensor", lambda e, kt=kt, r=r, p=p: e.matmul(ps[:, p, 0:256], lhsT=kaug[:, kt * 128:(kt + 1) * 128],
                                                               rhs=qaug[:, r * 256:(r + 1) * 256], start=True, stop=True),
                  reads=[bk, bq], writes=[bps[p]])
            if kt >= KT - 4:
                j = kt - (KT - 4) + 4 * (r % 2)
                s = j % 2
                sc.op("vector", lambda e, p=p, j=j, s=s: e.tensor_tensor(out=stm[s][:], in0=ps[:, p, 0:256], in1=cm[:, j, :],
                                                                        op=ALU.add), reads=[bps[p], bmask], writes=[bstm[s]])
                sc.op("scalar", lambda e, p=p, s=s: e.activation(out=pt[p][:], in_=stm[s][:], func=AF.Exp, bias=nbias[:, 0:1],
                                                                 scale=1.0), reads=[bstm[s], bnb], writes=[bpt[p]])
            else:
                sc.op("scalar", lambda e, p=p: e.activation(out=pt[p][:], in_=ps[:, p, 0:256], func=AF.Exp, bias=nbias[:, 0:1],
                                                            scale=1.0), reads=[bps[p], bnb], writes=[bpt[p]])

        def pv(i):
            r, kt = tasks[i]
            KT = 4 * r + 4
            p = i % NB
            po = 4 + (r % 2)
            sc.op("tensor", lambda e, kt=kt, p=p, po=po, KT=KT: e.matmul(ps[:, po, 0:256], lhsT=vaug[:, kt, :], rhs=pt[p][:],
                                                                        start=(kt == 0), stop=(kt == KT - 1)),
                  reads=[bv, bpt[p]], writes=[bps[po]])
            if kt < KT - 1:
                return
            sc.op("vector", lambda e, po=po: e.reciprocal(out=rec[:], in_=ps[64:128, po, 0:256]), reads=[bps[po]], writes=[brec])
            ys = r % 2
            sc.op("vector", lambda e, po=po, ys=ys: e.tensor_tensor(out=yst[ys][:], in0=ps[0:64, po, 0:256], in1=rec[:], op=ALU.mult),
                  reads=[bps[po], brec], writes=[byst[ys]])
            if fz is None:
                sc.dma("sync", lambda e, u=u, r=r, ys=ys: e.dma_start(out=yo[u, :, r * 256:(r + 1) * 256], in_=yst[ys][:]),
                       reads=[byst[ys]], key=pfx + f"myo{ys}")
            else:
                row0 = (r // 4) * 512 + u * 64
                sc.dma("sync", lambda e, row0=row0, r=r, ys=ys: e.dma_start(
                    out=fz.Ysend[row0:row0 + 64, (r % 4) * 256:(r % 4 + 1) * 256], in_=yst[ys][:]),
                    reads=[byst[ys], fz.bYs], key=pfx + f"myo{ys}")

        for i in range(min(DEPTH, len(tasks))):
            qk(i)
        for i in range(len(tasks)):
            if i + DEPTH < len(tasks):
                qk(i + DEPTH)
            pv(i)
    return byst


class SimpleProg:
    def __init__(self):
        self.fused = None
        self.nc = bass.Bass("TRN2", target_bir_lowering=False)
        self.es = ExitStack()
        self.in_names = []
        self.out_names = []

    din = TokProg.din
    dout = TokProg.dout
    sb = TokProg.sb

    def finish(self, sc, outbufs):
        sc.final_wait("sync", outbufs)
        with self.nc.Block() as block:
            sc.emit(block)
        self.es.close()
        return self.nc


def build_moba(nu=3):
    P = SimpleProg()
    sc = Sched(P.nc, P.es)
    ob = moba_emit(P, sc, nu)
    return P, P.finish(sc, ob)


def rope_tables():
    inv = np.exp(np.float32(-np.log(500000.0)) * np.arange(0, 16, 2, dtype=np.float32) / np.float32(16)).astype(np.float32)
    ang = (np.arange(S, dtype=np.float32)[:, None] * inv[None, :]).astype(np.float32)
    cos = np.cos(ang).astype(np.float32).T
    sin = np.sin(ang).astype(np.float32).T
    tab = np.zeros((2, 16, S), np.float32)
    tab[0, 0:8] = cos
    tab[0, 8:16] = cos
    tab[1, 0:8] = -sin
    tab[1, 8:16] = sin
    return tab


def moba_unit_inputs(uq, uk, uv, half, tab):
    blocks = HALF_BLOCKS[half]
    qpos = np.concatenate([np.arange(b * 256, (b + 1) * 256) for b in blocks])
    qT = np.ascontiguousarray(uq[qpos].T)
    kT = np.ascontiguousarray(uk.T)
    sw = np.r_[8:16, 0:8]
    return dict(mq=qT, mqs=np.ascontiguousarray(qT[sw]), mk=kT, mks=np.ascontiguousarray(kT[sw]), mv=np.ascontiguousarray(uv),
                ropeq=np.ascontiguousarray(tab[:, :, qpos]))


def moba_const_inputs(half):
    blocks = HALF_BLOCKS[half]
    pm = np.zeros((MOBA_SLOTS, NBLK), np.float32)
    oh = np.zeros((MOBA_SLOTS, NBLK), np.float32)
    for r, b in enumerate(blocks):
        pm[r, b:] = -1e30
        oh[r, b] = 1.0
    kk = np.arange(128)[:, None]
    qq = np.arange(256)[None, :]
    M0 = np.where(kk <= qq, 0.0, NEG).astype(np.float32)
    M1 = np.where(kk + 128 <= qq, 0.0, NEG).astype(np.float32)
    Z = np.zeros((128, 256), np.float32)
    cm = np.zeros((2, 4, 128, 256), np.float32)
    for par in range(2):
        r = par
        b = blocks[r]
        if b == 2 * r + 1:
            cm[par] = np.stack([Z, Z, M0, M1])
        else:
            cm[par] = np.stack([M0, M1, Z, Z])
    pmb = np.ascontiguousarray(np.broadcast_to(pm.reshape(1, -1), (128, MOBA_SLOTS * NBLK)))
    ohb = np.ascontiguousarray(np.broadcast_to(oh.reshape(1, -1), (128, MOBA_SLOTS * NBLK)))
    return dict(pm=pmb, oh=ohb, cm=cm)


def moba_shared_inputs(tab):
    boh = np.zeros((32, S), np.float32)
    for n in range(32):
        boh[n, n * 256:(n + 1) * 256] = 1.0
    return dict(ropek=tab, boh=boh, ident=np.eye(128, dtype=np.float32))


CH = 32


def conv_emit(P, sc, pfx="", fz=None):
    nc, es = P.nc, P.es
    T = TOK
    if fz is None:
        uc = P.din(pfx + "uc", [512, T + CH])
        yc = P.dout(pfx + "ycT", [256, T])
    else:
        yc = fz.Yfull
        flag_d = P.din("cflag", [128, 1])
        flag = P.sb(pfx + "cflag_sb", [128, 1], F32)
    cw = P.din(pfx + "cw", [128, 2, 31])
    cp = P.din(pfx + "cp", [128, 2, 3])
    idb = P.din("cident", [128, 128])
    a_t = [P.sb(pfx + f"ca{c}", [128, T + CH], F32) for c in range(2)]
    g_t = [P.sb(pfx + f"cg{c}", [128, T + CH], F32) for c in range(2)]
    hg = [P.sb(pfx + f"chg{c}", [128, T + CH], BF16) for c in range(2)]
    dg = [P.sb(pfx + f"cdg{c}", [128, 31, 128], BF16) for c in range(2)]
    cws = P.sb(pfx + "cws", [128, 2, 31], F32)
    cps = P.sb(pfx + "cps", [128, 2, 3], F32)
    idt = P.sb(pfx + "cidt", [128, 128], F32)
    onesf = P.sb(pfx + "cones", [128, 128], F32)
    epsc = P.sb(pfx + "ceps", [128, 1], F32)
    hc = [P.sb(pfx + f"chc{c}", [128, 512], F32) for c in range(2)]
    sq = [P.sb(pfx + f"csq{c}", [128, 512], F32) for c in range(2)]
    mean = P.sb(pfx + "cmean", [128, 512], F32)
    msq = P.sb(pfx + "cmsq", [128, 512], F32)
    var = P.sb(pfx + "cvar", [128, 512], F32)
    rstd = P.sb(pfx + "crstd", [128, 512], F32)
    tt_ = [P.sb(pfx + f"ctt{c}", [128, 512], F32) for c in range(2)]
    yo = [P.sb(pfx + f"cyo{c}", [128, 512], F32) for c in range(2)]
    ps = es.enter_context(nc.psum_tensor(pfx + "cps_", [128, 4, 512], F32)) if fz is None else fz.ps
    ba = [Buf(), Buf()]
    bg = [Buf(), Buf()]
    bhg = [Buf(), Buf()]
    bdg = [Buf(), Buf()]
    bc, bhc, bsq = Buf(), [Buf(), Buf()], [Buf(), Buf()]
    bmean, bmsq, bvar, brstd = Buf(), Buf(), Buf(), Buf()
    btt = [Buf(), Buf()]
    byo = [Buf(), Buf()]
    bps = [Buf() for _ in range(4)]

    sc.dma("sync", lambda e: e.dma_start(out=cws[:], in_=cw[:, :, :]), writes=[bc], key=pfx + "cc")
    sc.dma("sync", lambda e: e.dma_start(out=cps[:], in_=cp[:, :, :]), writes=[bc], key=pfx + "cc")
    sc.dma("sync", lambda e: e.dma_start(out=idt[:], in_=idb[:, :]), writes=[bc], key=pfx + "cc")
    sc.op("vector", lambda e: e.memset(onesf[:], 1.0), writes=[bc])
    sc.op("vector", lambda e: e.memset(epsc[:], EPS), writes=[bc])
    if fz is not None:
        sc.dma("sync", lambda e: e.dma_start(out=flag[:], in_=flag_d[:, :]), writes=[bc], key=pfx + "cc")

    def prev_rows(e, row0):
        return fz.LH[row0:row0 + 128, :]

    for c in range(2):
        if fz is None:
            sc.dma("sync", lambda e, c=c: e.dma_start(out=a_t[c][:], in_=uc[c * 128:(c + 1) * 128, :]), writes=[ba[c]],
                   key=pfx + f"ca{c}")
            sc.dma("sync", lambda e, c=c: e.dma_start(out=g_t[c][:], in_=uc[256 + c * 128:256 + (c + 1) * 128, :]),
                   writes=[bg[c]], key=pfx + f"cg{c}")
        else:
            sc.dma("sync", lambda e, c=c: e.dma_start(out=a_t[c][:, CH:], in_=fz.Usend[c * 128:(c + 1) * 128, :]),
                   reads=[fz.bU], writes=[ba[c]], key=pfx + f"ca{c}")
            sc.dma("sync", lambda e, c=c: e.dma_start(out=a_t[c][:, 0:CH], in_=prev_rows(e, c * 128)),
                   reads=[fz.bUr], writes=[ba[c]], key=pfx + f"ca{c}")
            sc.dma("sync", lambda e, c=c: e.dma_start(out=g_t[c][:, CH:], in_=fz.Usend[256 + c * 128:256 + (c + 1) * 128, :]),
                   reads=[fz.bU], writes=[bg[c]], key=pfx + f"cg{c}")
            sc.dma("sync", lambda e, c=c: e.dma_start(out=g_t[c][:, 0:CH], in_=prev_rows(e, 256 + c * 128)),
                   reads=[fz.bUr], writes=[bg[c]], key=pfx + f"cg{c}")
        sc.op("scalar", lambda e, c=c: e.activation(out=g_t[c][:], in_=g_t[c][:], func=AF.Sigmoid),
              reads=[bg[c]], writes=[bg[c]])
        sc.op("vector", lambda e, c=c: e.tensor_tensor(out=hg[c][:], in0=a_t[c][:], in1=g_t[c][:], op=ALU.mult),
              reads=[ba[c], bg[c]], writes=[bhg[c]])
        if fz is not None:
            sc.op("vector", lambda e, c=c: e.tensor_scalar(out=hg[c][:, 0:CH], in0=hg[c][:, 0:CH], scalar1=flag[:, 0:1],
                                                           scalar2=None, op0=ALU.mult), reads=[bhg[c], bc], writes=[bhg[c]])
        for k in range(31):
            sc.op("gpsimd", lambda e, c=c, k=k: e.tensor_scalar(out=dg[c][:, k, :], in0=idt[:], scalar1=cws[:, c, k:k + 1],
                                                                scalar2=None, op0=ALU.mult), reads=[bc], writes=[bdg[c]])
    for tt in range(T // 512):
        for c in range(2):
            for k in range(31):
                o = tt * 512 + 2 + k
                sc.op("tensor", lambda e, c=c, k=k, o=o: e.matmul(ps[:, c, :], lhsT=dg[c][:, k, :], rhs=hg[c][:, o:o + 512],
                                                                 start=(k == 0), stop=(k == 30)),
                      reads=[bdg[c], bhg[c]], writes=[bps[c]], inc=(k == 30))
            sc.op("scalar", lambda e, c=c: e.activation(out=hc[c][:], in_=ps[:, c, :], func=AF.Identity, bias=cps[:, c, 0:1],
                                                        scale=1.0), reads=[bps[c], bc], writes=[bhc[c]])
            sc.op("scalar", lambda e, c=c: e.activation(out=sq[c][:], in_=hc[c][:], func=AF.Square), reads=[bhc[c]],
                  writes=[bsq[c]])
        for c in range(2):
            sc.op("tensor", lambda e, c=c: e.matmul(ps[:, 2, :], lhsT=onesf[:], rhs=hc[c][:], start=(c == 0), stop=(c == 1)),
                  reads=[bc, bhc[c]], writes=[bps[2]])
        for c in range(2):
            sc.op("tensor", lambda e, c=c: e.matmul(ps[:, 3, :], lhsT=onesf[:], rhs=sq[c][:], start=(c == 0), stop=(c == 1)),
                  reads=[bc, bsq[c]], writes=[bps[3]])
        sc.op("vector", lambda e: e.tensor_scalar(out=mean[:], in0=ps[:, 2, :], scalar1=1.0 / 256, scalar2=None, op0=ALU.mult),
              reads=[bps[2]], writes=[bmean])
        sc.op("vector", lambda e: e.tensor_tensor(out=msq[:], in0=mean[:], in1=mean[:], op=ALU.mult), reads=[bmean], writes=[bmsq])
        sc.op("vector", lambda e: e.scalar_tensor_tensor(out=var[:], in0=ps[:, 3, :], scalar=1.0 / 256, in1=msq[:],
                                                         op0=ALU.mult, op1=ALU.subtract), reads=[bps[3], bmsq], writes=[bvar])
        sc.op("scalar", lambda e: e.activation(out=var[:], in_=var[:], func=AF.Sqrt, bias=epsc[:, 0:1], scale=1.0),
              reads=[bvar, bc], writes=[bvar])
        sc.op("vector", lambda e: e.reciprocal(out=rstd[:], in_=var[:]), reads=[bvar], writes=[brstd])
        for c in range(2):
            sc.op("vector", lambda e, c=c: e.tensor_tensor(out=tt_[c][:], in0=hc[c][:], in1=mean[:], op=ALU.subtract),
                  reads=[bhc[c], bmean], writes=[btt[c]])
            sc.op("vector", lambda e, c=c: e.tensor_tensor(out=tt_[c][:], in0=tt_[c][:], in1=rstd[:], op=ALU.mult),
                  reads=[btt[c], brstd], writes=[btt[c]])
            sc.op("scalar", lambda e, c=c: e.activation(out=yo[c][:], in_=tt_[c][:], func=AF.Silu, bias=cps[:, c, 2:3],
                                                        scale=cps[:, c, 1:2]), reads=[btt[c], bc], writes=[byo[c]])
            sc.dma("sync", lambda e, c=c, tt=tt: e.dma_start(out=yc[c * 128:(c + 1) * 128, tt * 512:(tt + 1) * 512], in_=yo[c][:]),
                   reads=[byo[c]] + ([] if fz is None else [fz.bY]), key=pfx + f"cyo{c}")
    return byo


def build_conv():
    P = SimpleProg()
    sc = Sched(P.nc, P.es)
    ob = conv_emit(P, sc)
    return P, P.finish(sc, ob)


def conv_inputs(u_b, j, conv_w, conv_b, ln_g, ln_b):
    t0 = j * TOK
    uc = np.zeros((512, TOK + CH), np.float32)
    lo = max(0, t0 - CH)
    uc[:, CH - (t0 - lo):] = u_b[lo:t0 + TOK, 0:512].T
    lay = lambda v: np.ascontiguousarray(v.reshape(2, 128).T)
    cw = np.ascontiguousarray(conv_w.T.reshape(2, 128, 31).transpose(1, 0, 2))
    cp = np.ascontiguousarray(np.stack([lay(conv_b), lay(ln_g), lay(ln_b)], axis=-1))
    return dict(uc=uc, cw=cw, cp=cp, cident=np.eye(128, dtype=np.float32))


GC = 64
NCH = S // GC
GSEG = 16
AX = mybir.AxisListType


def gdn_emit(P, sc, nu, pfx="", fz=None):
    import os
    STOP = float(os.environ.get("GDN_STOP", "99"))
    nc, es = P.nc, P.es
    if fz is None:
        raw_d = P.din(pfx + "graw", [nu, 3, 64, S + 3])
        gz_d = P.din(pfx + "gz", [nu, S, 64])
        ga_d = P.din(pfx + "ga", [nu, 64, NCH])
        gb_d = P.din(pfx + "gb", [nu, 64, NCH])
        go_d = P.dout(pfx + "go", [nu, S, 64])
    gcw_d = P.din(pfx + "gcw", [nu, 64, 12])
    gpar_d = P.din(pfx + "gpar", [nu, 64, 2])
    gng_d = P.din(pfx + "gng", [nu, 64, 64])
    gcst_d = P.din("gcst", [3, 64, 64])

    def unit_h(e, u):
        return fz.dyn(e, "gpsimd", ("gh", u))

    f = lambda n, shp: P.sb(pfx + n, shp, F32)
    cst = f("gcst_sb", [64, 3, 64])
    TriB = f("gTriB", [64, 8, 64])
    MB = f("gMB", [64, 8, 64])
    IB = f("gIB", [64, 8, 64])
    ones64 = f("gones", [64, 64])
    epsg = f("geps", [64, 1])
    bcst = Buf()
    sc.dma("sync", lambda e: e.dma_start(out=cst[:], in_=gcst_d.rearrange("a p q -> p a q")), writes=[bcst], key=pfx + "gc")
    sc.op("vector", lambda e: e.memset(ones64[:], 1.0), writes=[bcst])
    sc.op("vector", lambda e: e.memset(epsg[:], EPS), writes=[bcst])
    for j in range(8):
        sc.op("vector", lambda e, j=j: e.tensor_copy(out=TriB[:, j, :], in_=cst[:, 0, :]), reads=[bcst], writes=[bcst])
        sc.op("vector", lambda e, j=j: e.tensor_copy(out=MB[:, j, :], in_=cst[:, 1, :]), reads=[bcst], writes=[bcst])
        sc.op("vector", lambda e, j=j: e.tensor_copy(out=IB[:, j, :], in_=cst[:, 2, :]), reads=[bcst], writes=[bcst])
    Tri = cst[:, 0, :]
    I64 = cst[:, 2, :]

    def bc_n(t, n0):
        return t[:, n0:n0 + 8].unsqueeze(2).to_broadcast([64, 8, 64])


    def emit_unit(u, sc):
        u2 = u % 2
        GB, SB0 = 4 * u2, 4 * u2 + 3
        f = lambda n, shp: P.sb(pfx + f"u{u}_" + n, shp, F32)
        gcw = f("gcw_sb", [64, 12])
        dgw = P.sb(pfx + f"u{u}_" + "gdgw", [64, 12, 64], BF16)
        par = f("gpar_sb", [64, 2])
        negA = f("gnegA", [64, 1])
        ngb = f("gngb", [64, 64])
        a_t = f("ga_sb", [64, NCH])
        b_t = f("gb_sb", [64, NCH])
        g_t = f("gg", [64, NCH])
        beta = f("gbeta", [64, NCH])
        gc = f("ggc", [64, NCH])
        egc = f("gegc", [64, NCH])
        eglb = f("geglb", [64, NCH])
        edec = f("gedec", [64, NCH])
        bgk = f("gbgk", [64, NCH])
        SEGT = S // GSEG
        SEGC = NCH // GSEG
        raw = [P.sb(pfx + f"u{u}_" + f"graw{i}", [64, 515], BF16) for i in range(2)]
        xa = [f(f"gxa{i}", [64, 512]) for i in range(2)]
        xq = f("gxq", [64, 512])
        rn = f("grn", [64, 512])
        qnT = f("gqnT", [64, SEGT])
        knT = f("gknT", [64, SEGT])
        Kt = f("gKt", [64, SEGC, 64])
        Vt = f("gVt", [64, SEGC, 64])
        oseg = f("goseg", [64, SEGC, 64])
        zseg = f("gzseg", [64, SEGC, 64])
        osq = f("gosq", [64, SEGC, 64])
        oss = f("goss", [64, SEGC])
        rhsD = f("grhsD", [64, 8, 64])
        ED = f("gED", [64, 8, 64])
        EDT = f("gEDT", [64, 8, 64])
        Lp = [f(f"gL{i}", [64, 8, 64]) for i in range(2)]
        Np = [f(f"gN{i}", [64, 8, 64]) for i in range(2)]
        Pm = f("gP", [64, 8, 64])
        Lb = [P.sb(pfx + f"u{u}_" + f"gLb{i}", [64, 8, 64], BF16) for i in range(2)]
        Nb = [P.sb(pfx + f"u{u}_" + f"gNb{i}", [64, 8, 64], BF16) for i in range(2)]
        Pb = P.sb(pfx + f"u{u}_" + "gPb", [64, 8, 64], BF16)
        bLb, bNb, bPb = [Buf(), Buf()], [Buf(), Buf()], Buf()
        Kbg = f("gKbg", [64, 8, 64])
        Vb = f("gVb", [64, 8, 64])
        kdec = f("gkdec", [64, 8, 64])
        u_sb = f("gu", [64, 8, 64])
        wT = f("gwT", [64, 8, 64])
        qkT = f("gqkT", [64, 8, 64])
        St = f("gS", [64, 64])
        vn = [f(f"gvn{i}", [64, 64]) for i in range(2)]
        As = [f(f"gAs{i}", [64, 64]) for i in range(2)]
        ps = es.enter_context(nc.psum_tensor(pfx + "gps", [64, 8, 512], F32)) if fz is None else fz.ps[0:64, :, :]

        B_ = lambda: Buf()
        bpar, bg = B_(), B_()
        braw = [B_(), B_()]
        bxa = [B_(), B_()]
        bxq, brn, bqn, bkn, bKt, bVt, boseg, bz, bosq, boss = (B_() for _ in range(10))
        brhsD, bED, bEDT, bP, bKbg, bVb, bkdec, bu, bwT, bqkT, bS = (B_() for _ in range(11))
        bL = [B_(), B_()]
        bN = [B_(), B_()]
        bvn = [B_(), B_()]
        bAs = [B_(), B_()]
        bps = [B_() for _ in range(8)]
        wk = [0]

        def nps():
            i = GB + wk[0] % 3
            wk[0] += 1
            return i

        sc.dma("sync", lambda e, u=u: e.dma_start(out=gcw[:], in_=gcw_d[u]), writes=[bpar], key=pfx + "gp")
        sc.dma("sync", lambda e, u=u: e.dma_start(out=par[:], in_=gpar_d[u]), writes=[bpar], key=pfx + "gp")
        sc.dma("sync", lambda e, u=u: e.dma_start(out=ngb[:], in_=gng_d[u]), writes=[bpar], key=pfx + "gp")
        if fz is None:
            sc.dma("sync", lambda e, u=u: e.dma_start(out=a_t[:], in_=ga_d[u]), writes=[bpar], key=pfx + "gp")
            sc.dma("sync", lambda e, u=u: e.dma_start(out=b_t[:], in_=gb_d[u]), writes=[bpar], key=pfx + "gp")
        else:
            for rr in range(4):
                for (dst, ro) in ((a_t, 3200), (b_t, 3206)):
                    def absrc(e, u=u, rr=rr, ro=ro):
                        return fz.LAB[u, (0 if ro == 3200 else 1):(1 if ro == 3200 else 2), rr * TOK:(rr + 1) * TOK].rearrange(
                            "o (n s) -> s (o n)", s=64)
                    sc.dma("gpsimd", lambda e, dst=dst, rr=rr, absrc=absrc: e.dma_start(
                        out=dst[:, rr * 32:(rr + 1) * 32], in_=absrc(e), allow_slow_non_contiguous=True),
                        reads=[fz.bUr], writes=[bpar], key=pfx + "gp")
        for k in range(12):
            sc.op("gpsimd", lambda e, k=k: e.tensor_scalar(out=dgw[:, k, :], in0=I64, scalar1=gcw[:, k:k + 1], scalar2=None,
                                                           op0=ALU.mult), reads=[bcst, bpar], writes=[bpar])
        sc.op("scalar", lambda e: e.activation(out=negA[:], in_=par[:, 0:1], func=AF.Exp), reads=[bpar], writes=[bg])
        sc.op("vector", lambda e: e.tensor_scalar(out=negA[:], in0=negA[:], scalar1=-1.0, scalar2=None, op0=ALU.mult),
              reads=[bg], writes=[bg])
        sc.op("scalar", lambda e: e.activation(out=g_t[:], in_=a_t[:], func=AF.Exp, bias=par[:, 1:2], scale=1.0),
              reads=[bpar, bg], writes=[bg])
        sc.op("scalar", lambda e: e.activation(out=g_t[:], in_=g_t[:], func=AF.Ln, bias=1.0, scale=1.0), reads=[bg], writes=[bg])
        sc.op("vector", lambda e: e.tensor_scalar(out=g_t[:], in0=g_t[:], scalar1=negA[:, 0:1], scalar2=None, op0=ALU.mult),
              reads=[bg], writes=[bg])
        sc.op("scalar", lambda e: e.activation(out=beta[:], in_=b_t[:], func=AF.Sigmoid), reads=[bpar, bg], writes=[bg])
        sc.op("tensor", lambda e: e.matmul(ps[:, GB, 0:NCH], lhsT=Tri, rhs=g_t[:], start=True, stop=True),
              reads=[bcst, bg], writes=[bps[GB]])
        sc.op("tensor", lambda e: e.matmul(ps[:, GB, NCH:2 * NCH], lhsT=ones64[:], rhs=g_t[:], start=True, stop=True),
              reads=[bcst, bg], writes=[bps[GB]])
        sc.op("vector", lambda e: e.tensor_copy(out=gc[:], in_=ps[:, GB, 0:NCH]), reads=[bps[GB], bg], writes=[bg])
        sc.op("vector", lambda e: e.tensor_copy(out=eglb[:], in_=ps[:, GB, NCH:2 * NCH]), reads=[bps[GB], bg], writes=[bg])
        sc.op("vector", lambda e: e.tensor_tensor(out=edec[:], in0=eglb[:], in1=gc[:], op=ALU.subtract), reads=[bg], writes=[bg])
        sc.op("scalar", lambda e: e.activation(out=egc[:], in_=gc[:], func=AF.Exp), reads=[bg], writes=[bg])
        sc.op("scalar", lambda e: e.activation(out=eglb[:], in_=eglb[:], func=AF.Exp), reads=[bg], writes=[bg])
        sc.op("scalar", lambda e: e.activation(out=edec[:], in_=edec[:], func=AF.Exp), reads=[bg], writes=[bg])
        sc.op("vector", lambda e: e.tensor_tensor(out=bgk[:], in0=beta[:], in1=egc[:], op=ALU.mult), reads=[bg], writes=[bg])
        sc.op("vector", lambda e: e.memset(St[:], 0.0), writes=[bS])
        if STOP <= 1:
            return [bS]

        for seg in range(GSEG):
            for tt in range(SEGT // 512):
                c0 = seg * SEGT + tt * 512
                for j in range(3):
                    ri = (tt * 3 + j) % 2
                    if fz is None:
                        sc.dma("gpsimd", lambda e, u=u, j=j, ri=ri, c0=c0: e.dma_start(out=raw[ri][:], in_=raw_d[u, j, :, c0:c0 + 515]),
                               writes=[braw[ri]], key=pfx + f"graw{ri}")
                    else:
                        rr, t0 = c0 // TOK, c0 % TOK

                        def rsrc(e, rr_, ta, tb, u=u, j=j):
                            return fz.LG[u, j, :, rr_ * TOK + ta:rr_ * TOK + tb]
                        sc.dma("gpsimd", lambda e, ri=ri, rr=rr, t0=t0, rsrc=rsrc: e.dma_start(out=raw[ri][:, 3:515],
                                                                                            in_=rsrc(e, rr, t0, t0 + 512)),
                               reads=[fz.bUr], writes=[braw[ri]], key=pfx + f"graw{ri}")
                        if t0 >= 3:
                            sc.dma("gpsimd", lambda e, ri=ri, rr=rr, t0=t0, rsrc=rsrc: e.dma_start(out=raw[ri][:, 0:3],
                                                                                                in_=rsrc(e, rr, t0 - 3, t0)),
                                   reads=[fz.bUr], writes=[braw[ri]], key=pfx + f"graw{ri}")
                        elif rr > 0:
                            sc.dma("gpsimd", lambda e, ri=ri, rr=rr, rsrc=rsrc: e.dma_start(out=raw[ri][:, 0:3],
                                                                                         in_=rsrc(e, rr - 1, TOK - 3, TOK)),
                                   reads=[fz.bUr], writes=[braw[ri]], key=pfx + f"graw{ri}")
                        else:
                            sc.op("vector", lambda e, ri=ri: e.memset(raw[ri][:, 0:3], 0.0), writes=[braw[ri]])
                    p1 = nps()
                    for k in range(4):
                        sc.op("tensor", lambda e, j=j, k=k, ri=ri, p1=p1: e.matmul(ps[:, p1, :], lhsT=dgw[:, j * 4 + k, :],
                                                                                 rhs=raw[ri][:, k:k + 512], start=(k == 0), stop=(k == 3)),
                              reads=[bpar, braw[ri]], writes=[bps[p1]], inc=(k == 3))
                    xi = j % 2
                    sc.op("scalar", lambda e, xi=xi, p1=p1: e.activation(out=xa[xi][:], in_=ps[:, p1, :], func=AF.Silu),
                          reads=[bps[p1]], writes=[bxa[xi]])
                    if j < 2:
                        sc.op("scalar", lambda e, xi=xi: e.activation(out=xq[:], in_=xa[xi][:], func=AF.Square),
                              reads=[bxa[xi]], writes=[bxq])
                        p2 = nps()
                        sc.op("tensor", lambda e, p2=p2: e.matmul(ps[:, p2, :], lhsT=ones64[:], rhs=xq[:], start=True, stop=True),
                              reads=[bcst, bxq], writes=[bps[p2]])
                        sc.op("scalar", lambda e, p2=p2: e.activation(out=rn[:], in_=ps[:, p2, :], func=AF.Sqrt, bias=epsg[:, 0:1],
                                                                      scale=1.0), reads=[bps[p2], bcst], writes=[brn])
                        sc.op("vector", lambda e: e.reciprocal(out=rn[:], in_=rn[:]), reads=[brn], writes=[brn])
                        dst, bd = (qnT, bqn) if j == 0 else (knT, bkn)
                        scl = 0.125 if j == 0 else 1.0
                        sc.op("vector", lambda e, xi=xi, dst=dst, tt=tt, scl=scl: e.scalar_tensor_tensor(
                            out=dst[:, tt * 512:(tt + 1) * 512], in0=xa[xi][:], scalar=scl, in1=rn[:], op0=ALU.mult, op1=ALU.mult),
                            reads=[bxa[xi], brn], writes=[bd])
                    if j >= 1:
                        srcT = knT[:, tt * 512:(tt + 1) * 512] if j == 1 else xa[xi][:]
                        bsrc = bkn if j == 1 else bxa[xi]
                        p3 = nps()
                        for cj in range(8):
                            sc.op("tensor", lambda e, srcT=srcT, cj=cj, p3=p3: e.transpose(ps[:, p3, cj * 64:(cj + 1) * 64],
                                                                                         srcT[:, cj * 64:(cj + 1) * 64], I64),
                                  reads=[bsrc, bcst], writes=[bps[p3]], inc=(cj == 7))
                        dstT, bdt = (Kt, bKt) if j == 1 else (Vt, bVt)
                        sc.op("vector", lambda e, dstT=dstT, tt=tt, p3=p3: e.tensor_copy(
                            out=dstT[:, tt * 8:(tt + 1) * 8, :], in_=ps[:, p3, :].rearrange("p (a b) -> p a b", b=64)),
                            reads=[bps[p3]], writes=[bdt])
            if fz is None:
                sc.dma("sync", lambda e, u=u, seg=seg: e.dma_start(
                    out=zseg[:], in_=gz_d[u, seg * SEGT:(seg + 1) * SEGT, :].rearrange("(n s) d -> s n d", s=64)),
                    writes=[bz], key=pfx + "gz")
            else:
                def zsrc(e, u=u, seg=seg):
                    return fz.LZ[u, :, seg * SEGT:(seg + 1) * SEGT]
                sc.dma("gpsimd", lambda e, zsrc=zsrc: e.dma_start(out=zseg[:].rearrange("p a b -> p (a b)"), in_=zsrc(e)),
                       reads=[fz.bUr], writes=[bz], key=pfx + "gz")
            if STOP <= 2:
                return [bz, bKt, bVt, bqn]
            for gi in range(SEGC // 8):
                l0 = gi * 8
                n0 = seg * SEGC + l0
                v3 = lambda t: t[:]
                pk, pd, pdt = nps(), nps(), nps()
                for j in range(8):
                    cs = slice((l0 + j) * 64, (l0 + j + 1) * 64)
                    sc.op("tensor", lambda e, j=j, cs=cs, pk=pk: e.matmul(ps[:, pk, j * 64:(j + 1) * 64], lhsT=knT[:, cs], rhs=knT[:, cs],
                                                                         start=True, stop=True), reads=[bkn], writes=[bps[pk]], inc=(j == 7))
                sc.op("vector", lambda e, n0=n0: e.tensor_tensor(out=rhsD[:], in0=MB[:], in1=bc_n(g_t, n0), op=ALU.mult),
                      reads=[bcst, bg], writes=[brhsD])
                if STOP <= 2.1:
                    return [brhsD, bps[pk]]
                sc.op("tensor", lambda e, pd=pd: e.matmul(ps[:, pd, :], lhsT=Tri, rhs=rhsD[:].rearrange("p a b -> p (a b)"),
                                                          start=True, stop=True), reads=[bcst, brhsD], writes=[bps[pd]])
                for j in range(8):
                    sc.op("tensor", lambda e, j=j, pdt=pdt: e.matmul(ps[:, pdt, j * 64:(j + 1) * 64], lhsT=rhsD[:, j, :], rhs=Tri,
                                                                    start=True, stop=True), reads=[bcst, brhsD], writes=[bps[pdt]], inc=(j == 7))
                r3 = lambda ap: ap.rearrange("p (a b) -> p a b", b=64)
                sc.op("scalar", lambda e, pd=pd: e.activation(out=ED[:], in_=r3(ps[:, pd, :]), func=AF.Exp), reads=[bps[pd]], writes=[bED])
                sc.op("scalar", lambda e, pdt=pdt: e.activation(out=EDT[:], in_=r3(ps[:, pdt, :]), func=AF.Exp), reads=[bps[pdt]], writes=[bEDT])
                if STOP <= 2.2:
                    return [bED, bEDT]
                sc.op("vector", lambda e, pk=pk: e.tensor_tensor(out=Lp[0][:], in0=r3(ps[:, pk, :]), in1=ED[:], op=ALU.mult),
                      reads=[bps[pk], bED], writes=[bL[0]])
                sc.op("vector", lambda e, n0=n0: e.tensor_tensor(out=Lp[0][:], in0=Lp[0][:], in1=bc_n(beta, n0), op=ALU.mult),
                      reads=[bL[0], bg], writes=[bL[0]])
                sc.op("vector", lambda e: e.tensor_tensor(out=Lp[0][:], in0=Lp[0][:], in1=MB[:], op=ALU.mult),
                      reads=[bL[0], bcst], writes=[bL[0]])
                if STOP <= 2.3:
                    return [bL[0]]
                pn = nps()
                for j in range(8):
                    sc.op("tensor", lambda e, j=j, pn=pn: e.matmul(ps[:, pn, j * 64:(j + 1) * 64], lhsT=Lp[0][:, j, :], rhs=I64,
                                                                  start=True, stop=True),
                          reads=[bL[0], bcst], writes=[bps[pn]], inc=(j == 7))
                sc.op("scalar", lambda e, pn=pn: e.copy(out=Np[0][:], in_=r3(ps[:, pn, :])), reads=[bps[pn]], writes=[bN[0]])
                sc.op("vector", lambda e: e.tensor_tensor(out=Pm[:], in0=IB[:], in1=Np[0][:], op=ALU.subtract),
                      reads=[bN[0], bcst], writes=[bP])
                if STOP <= 2.4:
                    return [bP, bN[0]]
                sc.op("gpsimd", lambda e: e.tensor_copy(out=Lb[0][:], in_=Lp[0][:]), reads=[bL[0]], writes=[bLb[0]])
                sc.op("gpsimd", lambda e: e.tensor_copy(out=Nb[0][:], in_=Np[0][:]), reads=[bN[0]], writes=[bNb[0]])
                sc.op("gpsimd", lambda e: e.tensor_copy(out=Pb[:], in_=Pm[:]), reads=[bP], writes=[bPb])
                cur = 0
                for lvl in range(5):
                    nxt = 1 - cur
                    pl = nps()
                    for j in range(8):
                        sc.op("tensor", lambda e, j=j, pl=pl, cur=cur: e.matmul(ps[:, pl, j * 64:(j + 1) * 64], lhsT=Nb[cur][:, j, :],
                                                                               rhs=Lb[cur][:, j, :], start=True, stop=True),
                              reads=[bNb[cur], bLb[cur]], writes=[bps[pl]], inc=(j == 7))
                    if lvl < 4:
                        pn2 = nps()
                        for j in range(8):
                            sc.op("tensor", lambda e, j=j, pn2=pn2, cur=cur: e.matmul(ps[:, pn2, j * 64:(j + 1) * 64], lhsT=Lb[cur][:, j, :],
                                                                                     rhs=Nb[cur][:, j, :], start=True, stop=True),
                                  reads=[bNb[cur], bLb[cur]], writes=[bps[pn2]], inc=(j == 7))
                    sc.op("scalar", lambda e, pl=pl, nxt=nxt: e.copy(out=Lb[nxt][:], in_=r3(ps[:, pl, :])), reads=[bps[pl]], writes=[bLb[nxt]])
                    if lvl < 4:
                        sc.op("vector", lambda e, pn2=pn2, nxt=nxt: e.tensor_copy(out=Nb[nxt][:], in_=r3(ps[:, pn2, :])),
                              reads=[bps[pn2]], writes=[bNb[nxt]])
                    pu = nps()
                    for j in range(8):
                        sc.op("tensor", lambda e, j=j, pu=pu, nxt=nxt: e.matmul(ps[:, pu, j * 64:(j + 1) * 64], lhsT=Lb[nxt][:, j, :],
                                                                               rhs=Pb[:, j, :], start=True, stop=True),
                              reads=[bLb[nxt], bPb], writes=[bps[pu]], inc=(j == 7))
                    sc.op("vector", lambda e, pu=pu: e.tensor_tensor(out=Pm[:], in0=Pm[:], in1=r3(ps[:, pu, :]), op=ALU.add),
                          reads=[bps[pu], bP], writes=[bP])
                    if lvl < 4:
                        sc.op("gpsimd", lambda e: e.tensor_copy(out=Pb[:], in_=Pm[:]), reads=[bP], writes=[bPb])
                    cur = nxt
                if STOP <= 2.5:
                    return [bP]
                sc.op("vector", lambda e, l0=l0, n0=n0: e.tensor_tensor(out=Kbg[:], in0=Kt[:, l0:l0 + 8, :], in1=bc_n(bgk, n0), op=ALU.mult),
                      reads=[bKt, bg], writes=[bKbg])
                sc.op("vector", lambda e, l0=l0, n0=n0: e.tensor_tensor(out=Vb[:], in0=Vt[:, l0:l0 + 8, :], in1=bc_n(beta, n0), op=ALU.mult),
                      reads=[bVt, bg], writes=[bVb])
                sc.op("vector", lambda e, l0=l0, n0=n0: e.tensor_tensor(out=kdec[:], in0=Kt[:, l0:l0 + 8, :], in1=bc_n(edec, n0), op=ALU.mult),
                      reads=[bKt, bg], writes=[bkdec])
                p_u, p_w, p_q = nps(), nps(), nps()
                for j in range(8):
                    sc.op("tensor", lambda e, j=j, p_u=p_u: e.matmul(ps[:, p_u, j * 64:(j + 1) * 64], lhsT=Pm[:, j, :], rhs=Vb[:, j, :],
                                                                    start=True, stop=True), reads=[bP, bVb], writes=[bps[p_u]], inc=(j == 7))
                for j in range(8):
                    sc.op("tensor", lambda e, j=j, p_w=p_w: e.matmul(ps[:, p_w, j * 64:(j + 1) * 64], lhsT=Kbg[:, j, :], rhs=Pm[:, j, :],
                                                                    start=True, stop=True), reads=[bP, bKbg], writes=[bps[p_w]], inc=(j == 7))
                for j in range(8):
                    cs = slice((l0 + j) * 64, (l0 + j + 1) * 64)
                    sc.op("tensor", lambda e, j=j, cs=cs, p_q=p_q: e.matmul(ps[:, p_q, j * 64:(j + 1) * 64], lhsT=knT[:, cs], rhs=qnT[:, cs],
                                                                           start=True, stop=True), reads=[bkn, bqn], writes=[bps[p_q]], inc=(j == 7))
                sc.op("scalar", lambda e, p_u=p_u: e.copy(out=u_sb[:], in_=r3(ps[:, p_u, :])), reads=[bps[p_u]], writes=[bu])
                sc.op("scalar", lambda e, p_w=p_w: e.copy(out=wT[:], in_=r3(ps[:, p_w, :])), reads=[bps[p_w]], writes=[bwT])
                sc.op("vector", lambda e, p_q=p_q: e.tensor_tensor(out=qkT[:], in0=r3(ps[:, p_q, :]), in1=EDT[:], op=ALU.mult),
                      reads=[bps[p_q], bEDT], writes=[bqkT])
                sc.op("vector", lambda e: e.tensor_tensor(out=qkT[:], in0=qkT[:], in1=TriB[:], op=ALU.mult), reads=[bqkT, bcst], writes=[bqkT])
                if STOP <= 3:
                    return [bqkT, bu, bwT]
                for j in range(8):
                    n = n0 + j
                    l = l0 + j
                    cs = slice(l * 64, (l + 1) * 64)
                    i2 = j % 2
                    bx_, by_ = SB0, SB0
                    sc.op("tensor", lambda e, j=j, bx_=bx_: e.matmul(ps[:, bx_, 0:64], lhsT=wT[:, j, :], rhs=St[:], start=True, stop=True),
                          reads=[bwT, bS], writes=[bps[bx_]])
                    sc.op("tensor", lambda e, cs=cs, by_=by_: e.matmul(ps[:, by_, 192:256], lhsT=qnT[:, cs], rhs=St[:], start=True, stop=True),
                          reads=[bqn, bS], writes=[bps[by_]])
                    sc.op("vector", lambda e, j=j, bx_=bx_, i2=i2: e.tensor_tensor(out=vn[i2][:], in0=u_sb[:, j, :], in1=ps[:, bx_, 0:64],
                                                                                 op=ALU.subtract), reads=[bu, bps[bx_]], writes=[bvn[i2]])
                    sc.op("vector", lambda e, by_=by_, i2=i2, n=n: e.tensor_scalar(out=As[i2][:], in0=ps[:, by_, 192:256], scalar1=egc[:, n:n + 1],
                                                                                  scalar2=None, op0=ALU.mult), reads=[bps[by_], bg], writes=[bAs[i2]])
                    sc.op("tensor", lambda e, j=j, bx_=bx_, i2=i2: e.matmul(ps[:, bx_, 64:128], lhsT=qkT[:, j, :], rhs=vn[i2][:],
                                                                          start=True, stop=True), reads=[bqkT, bvn[i2]], writes=[bps[bx_]], inc=False)
                    sc.op("tensor", lambda e, j=j, bx_=bx_, i2=i2: e.matmul(ps[:, bx_, 128:192], lhsT=kdec[:, j, :], rhs=vn[i2][:],
                                                                          start=True, stop=True), reads=[bkdec, bvn[i2]], writes=[bps[bx_]])
                    sc.op("vector", lambda e, bx_=bx_, n=n: e.scalar_tensor_tensor(out=St[:], in0=St[:], scalar=eglb[:, n:n + 1],
                                                                                  in1=ps[:, bx_, 128:192], op0=ALU.mult, op1=ALU.add),
                          reads=[bS, bg, bps[bx_]], writes=[bS])
                    sc.op("vector", lambda e, bx_=bx_, i2=i2, l=l: e.tensor_tensor(out=oseg[:, l, :], in0=As[i2][:], in1=ps[:, bx_, 64:128],
                                                                                 op=ALU.add), reads=[bAs[i2], bps[bx_]], writes=[boseg])
                if STOP <= 4:
                    return [boseg, bS]
            sc.op("gpsimd", lambda e: e.tensor_tensor(out=osq[:], in0=oseg[:], in1=oseg[:], op=ALU.mult), reads=[boseg], writes=[bosq])
            sc.op("vector", lambda e: e.tensor_reduce(out=oss[:], in_=osq[:], axis=AX.X, op=ALU.add), reads=[bosq], writes=[boss])
            sc.op("scalar", lambda e: e.activation(out=oss[:], in_=oss[:], func=AF.Sqrt, bias=epsg[:, 0:1], scale=1.0 / 64),
                  reads=[boss, bcst], writes=[boss])
            sc.op("vector", lambda e: e.reciprocal(out=oss[:], in_=oss[:]), reads=[boss], writes=[boss])
            sc.op("vector", lambda e: e.tensor_tensor(out=osq[:], in0=oseg[:], in1=oss[:].unsqueeze(2).to_broadcast([64, SEGC, 64]),
                                                      op=ALU.mult), reads=[boseg, boss], writes=[bosq])
            sc.op("gpsimd", lambda e: e.tensor_tensor(out=osq[:], in0=osq[:], in1=ngb[:].unsqueeze(1).to_broadcast([64, SEGC, 64]),
                                                      op=ALU.mult), reads=[bosq, bpar], writes=[bosq])
            sc.op("scalar", lambda e: e.activation(out=zseg[:], in_=zseg[:], func=AF.Silu), reads=[bz], writes=[bz])
            if fz is None:
                sc.op("vector", lambda e: e.tensor_tensor(out=osq[:], in0=osq[:], in1=zseg[:], op=ALU.mult), reads=[bosq, bz], writes=[bosq])
                sc.dma("sync", lambda e, u=u, seg=seg: e.dma_start(
                    out=go_d[u, seg * SEGT:(seg + 1) * SEGT, :].rearrange("(n s) d -> s n d", s=64), in_=osq[:]),
                    reads=[bosq], key=pfx + "go")
            else:
                oT = oseg[:].rearrange("p a b -> p (a b)")
                zT = zseg[:].rearrange("p a b -> p (a b)")
                for g4 in range(SEGC // 8):
                    pt_ = nps()
                    for j in range(8):
                        sc.op("tensor", lambda e, j=j, g4=g4, pt_=pt_: e.transpose(ps[:, pt_, j * 64:(j + 1) * 64], osq[:, g4 * 8 + j, :], I64),
                              reads=[bosq, bcst], writes=[bps[pt_]], inc=(j == 7))
                    sc.op("vector", lambda e, g4=g4, pt_=pt_: e.tensor_tensor(out=oT[:, g4 * 512:(g4 + 1) * 512], in0=ps[:, pt_, :],
                                                                            in1=zT[:, g4 * 512:(g4 + 1) * 512], op=ALU.mult),
                          reads=[bps[pt_], bz, bosq], writes=[boseg])
                tok0 = seg * SEGT
                kblk, coff = tok0 // 1024, tok0 % 1024
                row0 = (kblk // 2) * 512 + 192 + (u * 2 + kblk % 2) * 64
                sc.dma("sync", lambda e, row0=row0, coff=coff: e.dma_start(out=fz.Ysend[row0:row0 + 64, coff:coff + SEGT],
                                                                          in_=oT[:, 0:SEGT]),
                       reads=[boseg, fz.bYs], key=pfx + f"go{u}")
        return [bosq, boseg]

    class _Rec:
        def __init__(self):
            self.calls = []

        def op(self, *a, **k):
            self.calls.append(("op", a, k))

        def dma(self, *a, **k):
            self.calls.append(("dma", a, k))

    outs = []
    recs = []
    for u in range(nu):
        r = _Rec()
        outs += emit_unit(u, r)
        recs.append(r.calls)
    n = max(len(c) for c in recs)
    for i in range(n):
        for c in recs:
            if i < len(c):
                kind, a, k = c[i]
                getattr(sc, kind)(*a, **k)
    return outs


def build_gdn(nu=2):
    P = SimpleProg()
    sc = Sched(P.nc, P.es)
    ob = gdn_emit(P, sc, nu)
    return P, P.finish(sc, ob)


def gdn_const_inputs():
    i = np.arange(64)
    tri = (i[:, None] <= i[None, :]).astype(np.float32)
    ms = (i[:, None] > i[None, :]).astype(np.float32)
    return dict(gcst=np.stack([tri, ms, np.eye(64, dtype=np.float32)]))


def gdn_unit_inputs(ug, h, gdn_conv_w, a_log, dt_bias, norm_g):
    GW = 384
    raw = np.zeros((3, 64, S + 3), np.float32)
    cw = np.zeros((64, 12), np.float32)
    for j in range(3):
        cols = slice(j * GW + h * 64, j * GW + (h + 1) * 64)
        raw[j, :, 3:] = ug[:, cols].T
        cw[:, j * 4:(j + 1) * 4] = gdn_conv_w[:, cols].T
    z = np.ascontiguousarray(ug[:, 3 * GW + h * 64:3 * GW + (h + 1) * 64])
    a = np.ascontiguousarray(ug[:, 4 * GW + h].reshape(NCH, 64).T)
    b = np.ascontiguousarray(ug[:, 4 * GW + 6 + h].reshape(NCH, 64).T)
    par = np.zeros((64, 2), np.float32)
    par[:, 0] = a_log[h]
    par[:, 1] = dt_bias[h]
    ng = np.ascontiguousarray(np.broadcast_to(norm_g[None, :], (64, 64))).astype(np.float32)
    return dict(graw=raw, gcw=cw, gz=z, ga=a, gb=b, gpar=par, gng=ng)


def _lay(v):
    return np.ascontiguousarray(np.asarray(v, np.float32).reshape(-1, 128).T)


_PROGS = {}


def _prog(key, builder):
    if key not in _PROGS:
        _PROGS[key] = builder()
    return _PROGS[key]


def _run(nc, in_maps):
    res = run_bass_kernel_spmd(nc, in_maps, core_ids=list(range(NCORES)))
    return res.results


def _tok_launch(key, stages, inp, xT_list, yT_list=None):
    def mk():
        p = TokProg(stages)
        return p, p.build()
    P, nc = _prog(key, mk)
    maps = []
    for c in range(NCORES):
        b = c // 4
        m = {"xT": xT_list[c], "cT": _lay(inp["c"][b])}
        for name in P.in_names:
            if name in m:
                continue
            if name.startswith("yT"):
                m[name] = yT_list[c]
            elif name == "final_g":
                m[name] = _lay(inp["final_g"])
            else:
                base, l = name[:-1], int(name[-1])
                arr = np.asarray(inp[base][l], np.float32)
                if base == "b_ada" or base.startswith("ln_"):
                    arr = _lay(arr)
                m[name] = np.ascontiguousarray(arr)
        maps.append(m)
    return _run(nc, maps)


def _mixer(inp, l, u):
    y = np.zeros((B, S, D), np.float32)
    P, nc = _prog("conv", build_conv)
    maps = []
    for c in range(NCORES):
        b, j = c // 4, c % 4
        maps.append(conv_inputs(u[b], j, np.asarray(inp["conv_w"][l]), np.asarray(inp["conv_b"][l]),
                                np.asarray(inp["conv_ln_g"][l]), np.asarray(inp["conv_ln_b"][l])))
    res = _run(nc, maps)
    for c in range(NCORES):
        b, j = c // 4, c % 4
        y[b, j * TOK:(j + 1) * TOK, 0:256] = res[c]["ycT"].T
    P, nc = _prog("moba", lambda: build_moba(3))
    tab = rope_tables()
    shared = moba_shared_inputs(tab)
    consts = [moba_const_inputs(0), moba_const_inputs(1)]
    maps = []
    for c in range(NCORES):
        b, cc = c // 4, c % 4
        units = []
        for s in range(3):
            combo = 3 * cc + s
            h, half = combo // 2, combo % 2
            q = u[b, :, 512 + h * 64:512 + (h + 1) * 64]
            k = u[b, :, 512 + 384 + h * 64:512 + 384 + (h + 1) * 64]
            v = u[b, :, 512 + 768 + h * 64:512 + 768 + (h + 1) * 64]
            d = moba_unit_inputs(q, k, v, half, tab)
            d.update(consts[half])
            units.append(d)
        m = {k_: np.ascontiguousarray(np.stack([un[k_] for un in units])) for k_ in units[0]}
        m.update(shared)
        maps.append(m)
    res = _run(nc, maps)
    for c in range(NCORES):
        b, cc = c // 4, c % 4
        for s in range(3):
            combo = 3 * cc + s
            h, half = combo // 2, combo % 2
            qpos = np.concatenate([np.arange(bl * 256, (bl + 1) * 256) for bl in HALF_BLOCKS[half]])
            y[b, qpos, 256 + h * 64:256 + (h + 1) * 64] = res[c]["moT"][s].T
    P, nc = _prog("gdn", lambda: build_gdn(2))
    gconst = gdn_const_inputs()
    allu = [(b, h) for b in range(B) for h in range(6)]
    maps = []
    assign = []
    for c in range(NCORES):
        us = [allu[i] if i < len(allu) else allu[0] for i in (2 * c, 2 * c + 1)]
        assign.append([(i < len(allu)) for i in (2 * c, 2 * c + 1)])
        units = [gdn_unit_inputs(u[b, :, 512 + 1152:], h, np.asarray(inp["gdn_conv_w"][l]), np.asarray(inp["gdn_a_log"][l]),
                                 np.asarray(inp["gdn_dt_bias"][l]), np.asarray(inp["gdn_norm_g"][l])) for (b, h) in us]
        m = {k_: np.ascontiguousarray(np.stack([un[k_] for un in units])) for k_ in units[0]}
        m.update(gconst)
        maps.append(m)
    res = _run(nc, maps)
    for c in range(NCORES):
        for s in range(2):
            i = 2 * c + s
            if i < len(allu):
                b, h = allu[i]
                y[b, :, 640 + h * 64:640 + (h + 1) * 64] = res[c]["go"][s]
    return y


YROWS = 768 + 1024
RG = [[0, 1, 2, 3], [4, 5, 6, 7]]


def moba_unit(cc, su):
    return (cc, su) if su < 2 else (4 + cc // 2, cc % 2)


def moba_owner(h, half):
    return (h, half) if h < 4 else (2 * (h - 4) + half, 2)


class Fused:
    def __init__(self):
        self.nc = bass.Bass("TRN2", target_bir_lowering=False)
        self.es = ExitStack()
        self.cur = self.es
        self.dins = {}
        self.in_names = []
        self.out_names = []
        self.phase_i = 0
        self.load_x = False
        self.store_x = False
        self._dyn = {}

    def din(self, name, shape, dt=F32):
        if name not in self.dins:
            self.in_names.append(name)
            self.dins[name] = self.nc.dram_tensor(name, list(shape), dt, kind="ExternalInput").ap()
        return self.dins[name]

    def dout(self, name, shape, dt=F32):
        if name not in self.dins:
            self.out_names.append(name)
            self.dins[name] = self.nc.dram_tensor(name, list(shape), dt, kind="ExternalOutput").ap()
        return self.dins[name]

    AW = 36800

    def sb(self, name, shape, dt):
        p = shape[0]
        n = int(np.prod(shape[1:]))
        n32 = n if dt == F32 else (n + 1) // 2
        n32 = (n32 + 7) // 8 * 8
        off = self.aoff
        self.aoff += n32
        assert self.aoff <= self.AW, (name, self.aoff)
        v = self.arena[0:p, off:off + n32]
        if dt != F32:
            v = v.bitcast(dt)
        v = v[:, 0:n]
        if len(shape) == 3:
            v = v.rearrange("p (a b) -> p a b", a=shape[1])
        return v

    def dyn(self, e, engname, key):
        c = self._dyn.setdefault(engname, {})
        if "cc" not in c:
            c["cc"] = e.snap(e.partition_id() % 4)
        if key not in c:
            cc = c["cc"]
            doff = lambda h: (h // 2) * 512 + (h % 2) * 64
            v = {"c2048": lambda: cc * 2048, "prev": lambda: (cc + 3) % 4,
                 "D0": lambda: doff(cc), "D2": lambda: (cc // 2) * 64 + 1024, "mha2": lambda: cc % 2, "mhb2": lambda: 3 - cc % 2,
                 "gh1": lambda: (cc + 4) % 6, "Dg1": lambda: doff((cc + 4) % 6)}[key]()
            c[key] = e.snap(v)
        return c[key]

    def build(self):
        nc, es = self.nc, self.es
        sc = self.sc = Sched(nc, es)
        self.x = es.enter_context(nc.sbuf_tensor("x_res", [128, KC, TOK], F32))
        self.arena = es.enter_context(nc.sbuf_tensor("arena", [128, self.AW], F32))
        self.aoff = 0
        self.bx = [[Buf(f"x{c}_{t}") for t in range(TOK // 512)] for c in range(KC)]
        self.ps = es.enter_context(nc.psum_tensor("ps_all", [128, 8, 512], F32))
        NUC = (DIN + 127) // 128
        Usend_t = nc.dram_tensor("Usend", [NUC * 128, TOK], F32)
        Urecv_t = nc.dram_tensor("Urecv", [NUC * 512 + 128, TOK], F32)
        Ysend_t = nc.dram_tensor("Ysend", [2048, 1024], F32)
        Yrecv_t = nc.dram_tensor("Yrecv", [8192, 1024], F32)
        Yfull_t = nc.dram_tensor("Yfull", [D, TOK], F32)
        self.Usend, self.Urecv, self.Ysend, self.Yrecv, self.Yfull = (t.ap() for t in (Usend_t, Urecv_t, Ysend_t, Yrecv_t, Yfull_t))
        self.LK = nc.dram_tensor("LK", [3, 64, S], F32).ap()
        self.LV = nc.dram_tensor("LV", [3, 64, S], F32).ap()
        self.LQ = nc.dram_tensor("LQ", [3, 64, MOBA_SLOTS * 256], F32).ap()
        self.LQF = nc.dram_tensor("LQF", [3, 64, S], F32).ap()
        self.LG = nc.dram_tensor("LG", [2, 3, 64, S], F32).ap()
        self.LZ = nc.dram_tensor("LZ", [2, 64, S], F32).ap()
        self.LAB = nc.dram_tensor("LAB", [2, 2, S], F32).ap()
        self.LH = nc.dram_tensor("LH", [512, CH], F32).ap()
        self.Yloc = nc.dram_tensor("Yloc", [4, 512, 1024], F32).ap()
        self.uT_dst = self.Usend
        self.yT_src = self.Yfull
        self.bU, self.bUr, self.bYs, self.bYr, self.bY = Buf("U"), Buf("Ur"), Buf("Ys"), Buf("Yr"), Buf("Yf")
        self.bL, self.bLq, self.bYl = Buf("L"), Buf("Lq"), Buf("Yl")
        self.bUr_m, self.bUr_g = Buf("Ur_m"), Buf("Ur_g")
        outb = []

        def run_phase(fn):
            self.aoff = 0
            r = fn()
            sc.barrier(exclude=("agUm", "agUg"))
            self.phase_i += 1
            return r

        def tok_phase(stages, load_x=False, store_x=False, ag=True):
            def fn():
                self.load_x, self.store_x = load_x, store_x
                r = TokProg(stages, fused=self).build()
                if ag:
                    order = list(range(4, 13)) + list(range(13, NUC)) + list(range(0, 4))
                    waits = sc._deps("gpsimd", (), [self.bU, self.bUr_m, self.bUr_g])
                    sc.q["gpsimd"].append((waits, None, None))
                    for ci in order:
                        key = "agUm" if 4 <= ci < 13 else "agUg"
                        sc._get_dsem(key)
                        sc.cckeys.add(key)
                        sc.dcnt[key] += 1
                        sc.q["gpsimd"].append(([], (lambda e, ci=ci: e.collective_compute(
                            "AllGather", ALU.bypass, replica_groups=RG,
                            ins=[Usend_t.ap()[ci * 128:(ci + 1) * 128, :]], outs=[Urecv_t.ap()[ci * 512:(ci + 1) * 512, :]])),
                            ("c", key, sc.dcnt[key])))
                    for b, key in ((self.bUr_m, "agUm"), (self.bUr_g, "agUg"), (self.bU, "agUg")):
                        b.lw = ("c", key, sc.dcnt[key])
                        b.rd = {}
                return r
            return run_phase(fn)

        def y_exchange():
            for ci in range(8):
                sc.cc(lambda e, ci=ci: e.collective_compute(
                    "AllGather", ALU.bypass, replica_groups=RG,
                    ins=[Ysend_t.ap()[ci * 256:(ci + 1) * 256, :]], outs=[Yrecv_t.ap()[ci * 1024:(ci + 1) * 1024, :]]),
                    writes=[self.bYs, self.bYr], key="agY")
            LB = ([0, 3, 4, 7], [1, 2, 5, 6])
            for c2 in range(2):
                sc.dma("scalar", lambda e, c2=c2: e.dma_start(
                    out=self.Yloc[:, c2 * 256:(c2 + 1) * 256, :],
                    in_=self.Yrecv[c2 * 1024:c2 * 1024 + 7168, :][bass.ds(self.dyn(e, "scalar", "c2048"), 1024), :].rearrange(
                        "(r f) t -> r f t", r=4)),
                    reads=[self.bYr], writes=[self.bYl], key="yloc")
            for h in range(6):
                for half in range(2):
                    rs, su = moba_owner(h, half)
                    for q4 in range(4):
                        lb = LB[half][q4]
                        sc.dma("sync", lambda e, h=h, lb=lb, rs=rs, su=su, q4=q4: e.dma_start(
                            out=self.Yfull[256 + h * 64:256 + (h + 1) * 64, lb * 256:(lb + 1) * 256],
                            in_=self.Yloc[rs, su * 64:(su + 1) * 64, q4 * 256:(q4 + 1) * 256]),
                            reads=[self.bYl, self.bY], key="yasm")
            for h in range(6):
                rs, g = (h, 0) if h < 4 else (h - 4, 1)
                for kk in range(2):
                    r0 = 192 + (g * 2 + kk) * 64
                    sc.dma("sync", lambda e, h=h, kk=kk, rs=rs, r0=r0: e.dma_start(
                        out=self.Yfull[640 + h * 64:640 + (h + 1) * 64, kk * 1024:(kk + 1) * 1024],
                        in_=self.Yloc[rs, r0:r0 + 64, :]),
                        reads=[self.bYl, self.bY], key="yasm")

        def localize_m():
            Ur = self.Urecv
            rk = lambda ap: ap.rearrange("d (r t) -> d r t", r=4)
            LQF = self.LQF

            def blk(e, q, dkey, B, n=64):
                R0 = (B // 128) * 512 + B % 128
                R1 = min(R0 + 2048, NUC * 512 + 128)
                return Ur[R0:R1, :][bass.ds(self.dyn(e, q, dkey), 512), :].rearrange("(r f) t -> f r t", r=4)[0:n]

            for u in range(3):
                q = "sync" if u < 2 else "scalar"
                dk = "D0" if u < 2 else "D2"
                for (dst, B) in ((self.LK, 896), (self.LV, 1280), (LQF, 512)):
                    sc.dma(q, lambda e, u=u, q=q, dk=dk, dst=dst, B=B: e.dma_start(out=rk(dst[u]), in_=blk(e, q, dk, B)),
                           reads=[self.bUr_m], writes=[self.bL], key=f"loc{q}{u}")
                for ab in range(2):
                    dstq = self.LQ[u].rearrange("d (G ab i) -> d G ab i", G=8, ab=2)[:, :, ab:ab + 1, :]
                    srcv = LQF[u].rearrange("d (G b i) -> d G b i", G=8, b=4)
                    if u < 2:
                        b = u if ab == 0 else 3 - u
                        sc.dma(q, lambda e, dstq=dstq, srcv=srcv, b=b: e.dma_start(out=dstq, in_=srcv[:, :, b:b + 1, :]),
                               reads=[self.bL], writes=[self.bLq], key=f"locq{u}")
                    else:
                        kn = "mha2" if ab == 0 else "mhb2"
                        sc.dma(q, lambda e, dstq=dstq, srcv=srcv, kn=kn: e.dma_start(
                            out=dstq, in_=srcv[:, :, bass.ds(self.dyn(e, "scalar", kn), 1), :]),
                            reads=[self.bL], writes=[self.bLq], key=f"locq{u}")

        def localize_g():
            Ur = self.Urecv
            rk = lambda ap: ap.rearrange("d (r t) -> d r t", r=4)
            LQF = self.LQF

            def blk(e, q, dkey, B, n=64):
                R0 = (B // 128) * 512 + B % 128
                R1 = min(R0 + 2048, NUC * 512 + 128)
                return Ur[R0:R1, :][bass.ds(self.dyn(e, q, dkey), 512), :].rearrange("(r f) t -> f r t", r=4)[0:n]

            for u in range(2):
                dk, gk = ("D0", "cc") if u == 0 else ("Dg1", "gh1")
                for j in range(3):
                    sc.dma("gpsimd", lambda e, u=u, j=j, dk=dk: e.dma_start(
                        out=rk(self.LG[u, j]), in_=blk(e, "gpsimd", dk, 1664 + j * 384)),
                        reads=[self.bUr_g], writes=[self.bL], key="locg")
                q2 = "gpsimd" if u == 0 else "sync"
                sc.dma(q2, lambda e, u=u, dk=dk, q2=q2: e.dma_start(out=rk(self.LZ[u]), in_=blk(e, q2, dk, 2816)),
                       reads=[self.bUr_g], writes=[self.bL], key=f"locz{u}")
                for ab, B in ((0, 3200), (1, 3206)):
                    sc.dma(q2, lambda e, u=u, ab=ab, B=B, gk=gk, q2=q2: e.dma_start(
                        out=self.LAB[u, ab:ab + 1].rearrange("o (r t) -> o r t", r=4), in_=blk(e, q2, gk, B, 1)),
                        reads=[self.bUr_g], writes=[self.bL], key=f"locz{u}")
            sc.dma("scalar", lambda e: e.dma_start(
                out=self.LH.rearrange("(c f) t -> c f t", c=4),
                in_=Ur[0:2048, TOK - CH:TOK].rearrange("(c r f) t -> r c f t", r=4, f=128)[bass.ds(self.dyn(e, "scalar", "prev"), 1)]),
                reads=[self.bUr_g], writes=[self.bL], key="loch")

        def mixer(l):
            pfx = f"L{l}_"
            run_phase(localize_m)
            run_phase(lambda: moba_emit(self, sc, 3, pfx, fz=self))
            run_phase(localize_g)
            run_phase(lambda: conv_emit(self, sc, pfx, fz=self))

            def g():
                gdn_emit(self, sc, 2, pfx, fz=self)
                y_exchange()
            run_phase(g)

        tok_phase([("ffn1", 0), ("uproj", 0)], load_x=True)
        mixer(0)
        tok_phase([("wout", 0), ("ffn2", 0), ("ffn1", 1), ("uproj", 1)])
        mixer(1)
        outb = tok_phase([("wout", 1), ("ffn2", 1), ("final",)], store_x=True, ag=False)
        with nc.Block() as block:
            sc.emit(block)
        es.close()
        return nc


_FUSED = {}


def kernel(**inp):
    if "p" not in _FUSED:
        F = Fused()
        _FUSED["p"] = (F, F.build())
    F, nc = _FUSED["p"]
    x = np.asarray(inp["x"], np.float32)
    tab = rope_tables()
    shared = moba_shared_inputs(tab)
    mconst = [moba_const_inputs(0), moba_const_inputs(1)]
    gconst = gdn_const_inputs()
    wnames = ("w_ada", "ffn1_w_gate", "ffn1_w_up", "ffn1_w_down", "w_in", "w_out", "ffn2_w_gate", "ffn2_w_up", "ffn2_w_down")
    lnames = ("b_ada", "ln_ffn1_g", "ln_mix_g", "ln_ffn2_g")
    common = {}
    for l in range(2):
        for n in wnames:
            common[f"{n}{l}"] = np.ascontiguousarray(np.asarray(inp[n][l], np.float32))
        for n in lnames:
            common[f"{n}{l}"] = _lay(inp[n][l])
        cw = np.asarray(inp["conv_w"][l], np.float32)
        lay2 = lambda v: np.ascontiguousarray(np.asarray(v, np.float32).reshape(2, 128).T)
        common[f"L{l}_cw"] = np.ascontiguousarray(cw.T.reshape(2, 128, 31).transpose(1, 0, 2))
        common[f"L{l}_cp"] = np.ascontiguousarray(np.stack([lay2(inp["conv_b"][l]), lay2(inp["conv_ln_g"][l]),
                                                            lay2(inp["conv_ln_b"][l])], axis=-1))
    common["final_g"] = _lay(inp["final_g"])
    common["cident"] = np.eye(128, dtype=np.float32)
    common.update(shared)
    common.update(gconst)
    maps = []
    for c in range(NCORES):
        b, cc = c // 4, c % 4
        m = dict(common)
        m["xT"] = np.ascontiguousarray(x[b, cc * TOK:(cc + 1) * TOK].T)
        m["cT"] = _lay(inp["c"][b])
        m["cflag"] = np.full((128, 1), 0.0 if cc == 0 else 1.0, np.float32)
        units = []
        for su in range(3):
            half = moba_unit(cc, su)[1]
            qpos = np.concatenate([np.arange(bl * 256, (bl + 1) * 256) for bl in HALF_BLOCKS[half]])
            d = dict(mconst[half])
            d["ropeq"] = np.ascontiguousarray(tab[:, :, qpos])
            units.append(d)
        for k_ in units[0]:
            m[k_] = np.ascontiguousarray(np.stack([un[k_] for un in units]))
        for l in range(2):
            heads = [cc, (cc + 4) % 6]
            gw = np.asarray(inp["gdn_conv_w"][l], np.float32)
            gcw = np.zeros((2, 64, 12), np.float32)
            gpar = np.zeros((2, 64, 2), np.float32)
            gng = np.zeros((2, 64, 64), np.float32)
            for g, h in enumerate(heads):
                for j in range(3):
                    gcw[g, :, j * 4:(j + 1) * 4] = gw[:, j * 384 + h * 64:j * 384 + (h + 1) * 64].T
                gpar[g, :, 0] = np.asarray(inp["gdn_a_log"][l], np.float32)[h]
                gpar[g, :, 1] = np.asarray(inp["gdn_dt_bias"][l], np.float32)[h]
                gng[g] = np.asarray(inp["gdn_norm_g"][l], np.float32)[None, :]
            m[f"L{l}_gcw"], m[f"L{l}_gpar"], m[f"L{l}_gng"] = gcw, gpar, gng
        maps.append({k_: m[k_] for k_ in F.in_names})
    res = run_bass_kernel_spmd(nc, maps, core_ids=list(range(NCORES))).results
    out = np.zeros((B, S, D), np.float32)
    for c in range(NCORES):
        out[c // 4, (c % 4) * TOK:(c % 4 + 1) * TOK] = res[c]["xoT"].T
    return out
```

```python
import numpy as np
from contextlib import ExitStack
import concourse.bass as bass
import concourse.mybir as mybir
from concourse.bass_utils import run_bass_kernel_spmd

F32 = mybir.dt.float32
BF16 = mybir.dt.bfloat16
AF = mybir.ActivationFunctionType
ALU = mybir.AluOpType

D = 1024
KC = 8
DFF = 2816
FC = 22
DIN = 3212
B = 2
S = 8192
NCORES = 8
TOK = 2048
EPS = 1e-6

SAME_ENG_SYNC = True


class Buf:
    __slots__ = ("name", "lw", "rd")

    def __init__(self, name=""):
        self.name = name
        self.lw = None
        self.rd = {}


class Sched:
    ENGS = ("tensor", "vector", "scalar", "gpsimd", "sync")
    EPOCH = 20000

    def __init__(self, nc, es):
        self.nc = nc
        self.es = es
        self.q = {e: [] for e in self.ENGS}
        self.cnt = {e: 0 for e in self.ENGS}
        self.seen = {e: {} for e in self.ENGS}
        self.esem = {}
        self.dsem = {}
        self.dcnt = {}
        self.cckeys = set()

    def _get_esem(self, eng, epoch):
        k = (eng, epoch)
        if k not in self.esem:
            self.esem[k] = self.es.enter_context(self.nc.semaphore(f"se_{eng}_{epoch}"))
        return self.esem[k]

    def _get_dsem(self, key):
        if key not in self.dsem:
            self.dsem[key] = self.es.enter_context(self.nc.semaphore(f"sd_{key}"))
            self.dcnt[key] = 0
        return self.dsem[key]

    def _need(self, eng, tok, waits):
        if tok is None:
            return
        kind, k, val = tok
        if kind == "e":
            if k == eng and (eng == "tensor" or not SAME_ENG_SYNC):
                return
        key = (kind, k)
        if self.seen[eng].get(key, 0) >= val:
            return
        self.seen[eng][key] = val
        waits.append(tok)

    def _deps(self, eng, reads, writes):
        waits = []
        for b in reads:
            self._need(eng, b.lw, waits)
        for b in writes:
            self._need(eng, b.lw, waits)
            for k, v in b.rd.items():
                self._need(eng, (k[0], k[1], v), waits)
        return waits

    def _mark(self, tok, reads, writes):
        key = (tok[0], tok[1])
        for b in reads:
            if b.rd.get(key, 0) < tok[2]:
                b.rd[key] = tok[2]
        for b in writes:
            b.lw = tok
            b.rd = {}

    def op(self, eng, fn, reads=(), writes=(), inc=True):
        waits = self._deps(eng, reads, writes)
        idx = self.cnt[eng] + 1
        if inc:
            self.cnt[eng] = idx
        tok = ("e", eng, idx)
        self._mark(tok, reads, writes)
        self.q[eng].append((waits, fn, tok if inc else None))

    def dma(self, qeng, fn, reads=(), writes=(), key="d"):
        waits = self._deps(qeng, reads, writes)
        self._get_dsem(key)
        self.dcnt[key] += 1
        tok = ("d", key, 16 * self.dcnt[key])
        self._mark(tok, reads, writes)
        self.q[qeng].append((waits, fn, tok))

    def cc(self, fn, reads=(), writes=(), key="cc"):
        waits = self._deps("gpsimd", reads, writes)
        self._get_dsem(key)
        self.cckeys.add(key)
        self.dcnt[key] += 1
        tok = ("c", key, self.dcnt[key])
        self._mark(tok, reads, writes)
        self.q["gpsimd"].append((waits, fn, tok))

    def barrier(self, exclude=()):
        for e in self.ENGS:
            waits = []
            for e2 in self.ENGS:
                if e2 != e and self.cnt[e2] > 0:
                    self._need(e, ("e", e2, self.cnt[e2]), waits)
            for key, n in self.dcnt.items():
                if n > 0 and key not in exclude:
                    kind = "c" if key in self.cckeys else "d"
                    self._need(e, (kind, key, n if kind == "c" else 16 * n), waits)
            self.q[e].append((waits, None, None))

    def final_wait(self, eng, toks_bufs):
        waits = self._deps(eng, (), toks_bufs)
        self.q[eng].append((waits, None, None))

    def emit(self, block):
        nc = self.nc

        def run(engname):
            def body(eng):
                for waits, fn, tok in self.q[engname]:
                    for (kind, k, val) in waits:
                        if kind == "e":
                            epoch = (val - 1) // self.EPOCH
                            eng.wait_ge(self._get_esem(k, epoch), val - epoch * self.EPOCH)
                        else:
                            eng.wait_ge(self.dsem[k], val)
                    if fn is None:
                        continue
                    ins = fn(eng)
                    if tok is not None:
                        if tok[0] == "e":
                            epoch = (tok[2] - 1) // self.EPOCH
                            ins.then_inc(self._get_esem(tok[1], epoch), 1)
                        elif tok[0] == "c":
                            ins.then_inc(self.dsem[tok[1]])
                        else:
                            ins.then_inc(self.dsem[tok[1]], 16)
                self.q[engname] = []
            return body

        for e in self.ENGS:
            for ep in range((self.cnt[e] - 1) // self.EPOCH + 1 if self.cnt[e] else 0):
                self._get_esem(e, ep)
        block.tensor(run("tensor"))
        block.vector(run("vector"))
        block.scalar(run("scalar"))
        block.gpsimd(run("gpsimd"))
        block.sync(run("sync"))


class TokProg:
    def __init__(self, stages, tok=TOK, fused=None):
        self.stages = stages
        self.tok = tok
        self.fused = fused
        if fused is None:
            self.nc = bass.Bass("TRN2", target_bir_lowering=False)
            self.es = ExitStack()
        else:
            self.nc = fused.nc
            self.es = fused.es
        self.in_names = []
        self.out_names = []

    def din(self, name, shape, dt=F32):
        if self.fused is not None:
            return self.fused.din(name, shape, dt)
        self.in_names.append(name)
        return self.nc.dram_tensor(name, list(shape), dt, kind="ExternalInput").ap()

    def dout(self, name, shape, dt=F32):
        if self.fused is not None:
            return self.fused.dout(name, shape, dt)
        self.out_names.append(name)
        return self.nc.dram_tensor(name, list(shape), dt, kind="ExternalOutput").ap()

    def sb(self, name, shape, dt):
        if self.fused is not None:
            return self.fused.sb(name, shape, dt)
        return self.es.enter_context(self.nc.sbuf_tensor(name, list(shape), dt))

    def build(self):
        nc, es = self.nc, self.es
        fz = self.fused
        T = self.tok
        NH = T // 1024
        stages = self.stages
        layers = sorted({s[1] for s in stages if len(s) > 1})
        need_v = {}
        for s in stages:
            if s[0] == "ffn1":
                need_v.setdefault(s[1], set()).update([0, 1, 2])
            elif s[0] == "uproj":
                need_v.setdefault(s[1], set()).update([3, 4])
            elif s[0] == "wout":
                need_v.setdefault(s[1], set()).update([5])
            elif s[0] == "ffn2":
                need_v.setdefault(s[1], set()).update([6, 7, 8])

        xT_d = self.din("xT", [D, T]) if (fz is None or fz.load_x) else None
        cT_d = self.din("cT", [128, KC])
        W = {}
        for l in layers:
            W[("w_ada", l)] = self.din(f"w_ada{l}", [D, 9 * D])
            W[("b_ada", l)] = self.din(f"b_ada{l}", [128, 72])
        for s in stages:
            if s[0] in ("ffn1", "ffn2"):
                l = s[1]
                n = s[0]
                W[(n + "_g", l)] = self.din(f"ln_{n}_g{l}", [128, KC])
                W[(n + "_wg", l)] = self.din(f"{n}_w_gate{l}", [D, DFF])
                W[(n + "_wu", l)] = self.din(f"{n}_w_up{l}", [D, DFF])
                W[(n + "_wd", l)] = self.din(f"{n}_w_down{l}", [DFF, D])
            elif s[0] == "uproj":
                l = s[1]
                W[("mix_g", l)] = self.din(f"ln_mix_g{l}", [128, KC])
                W[("w_in", l)] = self.din(f"w_in{l}", [D, DIN])
                W[("uT", l)] = self.dout(f"uT{l}", [DIN, T]) if fz is None else fz.uT_dst
            elif s[0] == "wout":
                l = s[1]
                W[("w_out", l)] = self.din(f"w_out{l}", [D, D])
                W[("yT", l)] = self.din(f"yT{l}", [D, T]) if fz is None else fz.yT_src
            elif s[0] == "final":
                W[("final_g",)] = self.din("final_g", [128, KC])
        xo_d = self.dout("xoT", [D, T]) if (fz is None or fz.store_x) else None

        x = self.sb("x", [128, KC, T], F32) if fz is None else fz.x
        h = self.sb("h", [128, KC, 1024], BF16)
        act = self.sb("act", [128, FC, 1024], BF16)
        wd = self.sb("wd", [128, FC, D], BF16)
        NSLOT = 4
        SLOTW = 256
        wslot = [self.sb(f"ws{i}", [128, KC, SLOTW], BF16) for i in range(NSLOT)]
        tmpA = [self.sb(f"tmpA{i}", [128, 512], F32) for i in range(2)]
        tmpB = [self.sb(f"tmpB{i}", [128, 512], F32) for i in range(2)]
        sqb = [self.sb(f"sq{i}", [128, 512], BF16) for i in range(2)]
        rstd = self.sb("rstd", [128, 512], F32)
        ones = self.sb("ones", [128, 128], BF16)
        cT = self.sb("cT_sb", [128, KC], F32)
        cact = self.sb("cact", [128, KC], BF16)
        bada = {l: self.sb(f"bada{l}", [128, 72], F32) for l in layers}
        mod = {l: self.sb(f"mod{l}", [128, 72], F32) for l in layers}
        gains = {}
        for k in W:
            if k[0] in ("ffn1_g", "ffn2_g", "mix_g", "final_g"):
                gains[k] = self.sb("g_" + "_".join(map(str, k)), [128, KC], F32)
        coefA = {}
        coefG = {}
        ps = es.enter_context(nc.psum_tensor("ps", [128, 8, 512], F32)) if fz is None else fz.ps

        sc = Sched(nc, es) if fz is None else fz.sc
        bx = [[Buf(f"x{c}_{t}") for t in range(T // 512)] for c in range(KC)] if fz is None else fz.bx
        bU = [] if fz is None else [fz.bU]
        bY = [] if fz is None else [fz.bY]
        bh = [Buf(f"h{t}") for t in range(2)]
        bact = [[Buf(f"act{f}_{t}") for t in range(2)] for f in range(FC)]
        WD_PIECES = ((0, 6), (6, 12), (12, 17), (17, 22))
        bwd = [Buf(f"wd{i}") for i in range(4)]
        wd_piece = {}
        for i, (f0, f1) in enumerate(WD_PIECES):
            for f in range(f0, f1):
                wd_piece[f] = i
        bws = [Buf(f"ws{i}") for i in range(NSLOT)]
        btA = [Buf() for _ in range(2)]
        btB = [Buf() for _ in range(2)]
        bsq = [Buf() for _ in range(2)]
        brstd = Buf()
        bones = Buf()
        bps = [Buf(f"ps{i}") for i in range(8)]
        bmisc = Buf("misc")
        bmod = Buf("mod")

        if xT_d is not None:
            xT_v = xT_d.rearrange("(c p) t -> p c t", p=128)
            for c in range(KC):
                sc.dma("sync", lambda e, c=c: e.dma_start(out=x[:, c, :], in_=xT_v[:, c, :]),
                       writes=bx[c], key=f"x{c}")
        sc.dma("sync", lambda e: e.dma_start(out=cT[:], in_=cT_d[:, :]), writes=[bmisc], key="misc")
        for l in layers:
            sc.dma("sync", lambda e, l=l: e.dma_start(out=bada[l][:], in_=W[("b_ada", l)][:, :]),
                   writes=[bmisc], key="misc")
        for k, t in gains.items():
            sc.dma("sync", lambda e, k=k, t=t: e.dma_start(out=t[:], in_=W[k][:, :]), writes=[bmisc], key="misc")
        sc.op("vector", lambda e: e.memset(ones[:], 1.0), writes=[bones])
        sc.op("scalar", lambda e: e.activation(out=cact[:], in_=cT[:], func=AF.Silu), reads=[bmisc], writes=[bmod])

        wslot_i = [0]

        def next_slot():
            i = wslot_i[0] % NSLOT
            wslot_i[0] += 1
            return i

        def load_cols(Wd, c0, ncols, nk=KC):
            i = next_slot()
            src = Wd.rearrange("(k p) n -> p k n", p=128)
            sc.dma("gpsimd", lambda e, i=i: e.dma_start(out=wslot[i][:, 0:nk, 0:ncols], in_=src[:, :, c0:c0 + ncols]),
                   writes=[bws[i]], key=f"ws{i}")
            return i

        mod_ps = ps[:, 7, 0:72]
        for l in layers:
            for v in sorted(need_v[l]):
                for hh in range(4):
                    si = load_cols(W[("w_ada", l)], v * 1024 + hh * 256, 256)
                    for jj in range(2):
                        j = hh * 2 + jj
                        col = v * 8 + j
                        for kc in range(KC):
                            sc.op("tensor",
                                  lambda e, si=si, jj=jj, kc=kc, col=col: e.matmul(
                                      ps[:, 7, col:col + 1], lhsT=wslot[si][:, kc, jj * 128:(jj + 1) * 128],
                                      rhs=cact[:, kc:kc + 1], start=(kc == 0), stop=(kc == KC - 1)),
                                  reads=[bws[si], bmod], writes=[bps[7]], inc=(kc == KC - 1))
            for v in sorted(need_v[l]):
                sc.op("vector", lambda e, l=l, v=v: e.tensor_tensor(out=mod[l][:, v * 8:(v + 1) * 8], in0=ps[:, 7, v * 8:(v + 1) * 8],
                                                                    in1=bada[l][:, v * 8:(v + 1) * 8], op=ALU.add),
                      reads=[bps[7], bmisc], writes=[bmod])
            for (gk, vs, vg, half) in ((("ffn1_g", l), 1, 2, 0.5), (("mix_g", l), 4, None, None),
                                       (("ffn2_g", l), 7, 8, 0.5)):
                if gk in gains:
                    a = self.sb("cA_" + "_".join(map(str, gk)), [128, KC], F32)
                    coefA[gk] = a
                    sc.op("vector", lambda e, a=a, gk=gk, vs=vs, l=l: e.scalar_tensor_tensor(
                        out=a[:], in0=mod[l][:, vs * 8:vs * 8 + 8], scalar=1.0, in1=gains[gk][:],
                        op0=ALU.add, op1=ALU.mult), reads=[bmod, bmisc], writes=[bmod])
                    if vg is not None:
                        g = self.sb("cG_" + "_".join(map(str, gk)), [128, KC], F32)
                        coefG[gk] = g
                        sc.op("vector", lambda e, g=g, vg=vg, l=l: e.tensor_scalar(
                            out=g[:], in0=mod[l][:, vg * 8:vg * 8 + 8], scalar1=0.5, scalar2=None, op0=ALU.mult),
                            reads=[bmod], writes=[bmod])

        psi = [0]

        def next_ps(pool):
            i = pool[psi[0] % len(pool)]
            psi[0] += 1
            return i

        def norm_mod(half, A_ap, sh_ap):
            for tt in range(2):
                t0 = half * 1024 + tt * 512
                ti = t0 // 512
                pb = 6
                for c in range(KC):
                    s = c % 2
                    sc.op("scalar", lambda e, c=c, s=s, t0=t0: e.activation(out=sqb[s][:], in_=x[:, c, t0:t0 + 512],
                                                                            func=AF.Square),
                          reads=[bx[c][ti]], writes=[bsq[s]])
                    sc.op("tensor", lambda e, c=c, s=s: e.matmul(ps[:, pb, :], lhsT=ones[:], rhs=sqb[s][:],
                                                                 start=(c == 0), stop=(c == KC - 1)),
                          reads=[bones, bsq[s]], writes=[bps[pb]])
                sc.op("scalar", lambda e: e.activation(out=tmpA[0][:], in_=ps[:, pb, :], func=AF.Sqrt,
                                                       bias=eps_t[:, 0:1], scale=1.0 / D),
                      reads=[bps[pb], bmisc], writes=[btA[0]])
                sc.op("vector", lambda e: e.reciprocal(out=rstd[:], in_=tmpA[0][:]), reads=[btA[0]], writes=[brstd])
                for c in range(KC):
                    s = c % 2
                    sc.op("vector", lambda e, c=c, s=s, t0=t0: e.scalar_tensor_tensor(
                        out=tmpB[s][:], in0=x[:, c, t0:t0 + 512], scalar=A_ap[:, c:c + 1], in1=rstd[:],
                        op0=ALU.mult, op1=ALU.mult), reads=[bx[c][ti], brstd, bmod], writes=[btB[s]])
                    if sh_ap is not None:
                        sc.op("scalar", lambda e, c=c, s=s, tt=tt: e.activation(
                            out=h[:, c, tt * 512:(tt + 1) * 512], in_=tmpB[s][:], func=AF.Identity,
                            bias=sh_ap[:, c:c + 1], scale=1.0), reads=[btB[s], bmod], writes=[bh[tt]])

        def ffn(half, n, l):
            A = coefA[(n + "_g", l)]
            G = coefG[(n + "_g", l)]
            vsh = 0 if n == "ffn1" else 6
            sh = mod[l][:, vsh * 8:vsh * 8 + 8]
            norm_mod(half, A, sh)
            wdv = W[(n + "_wd", l)].rearrange("(f p) n -> p f n", p=128)
            for i, (f0, f1) in enumerate(WD_PIECES):
                sc.dma("gpsimd", lambda e, f0=f0, f1=f1: e.dma_start(out=wd[:, f0:f1, :], in_=wdv[:, f0:f1, :]),
                       writes=[bwd[i]], key=f"wd{i}")
            groups = [(g * 2, 2) for g in range(11)]
            loaded = {}

            def load_group(gi):
                f0, nf = groups[gi]
                loaded[gi] = (load_cols(W[(n + "_wg", l)], f0 * 128, nf * 128),
                              load_cols(W[(n + "_wu", l)], f0 * 128, nf * 128))
            load_group(0)
            for gi, (f0, nf) in enumerate(groups):
                if gi + 1 < len(groups):
                    load_group(gi + 1)
                sg, su = loaded[gi]
                for fi in range(nf):
                    f = f0 + fi
                    for tt in range(2):
                        pg = next_ps([0, 1])
                        pu = pg + 2
                        for kc in range(KC):
                            sc.op("tensor", lambda e, sg=sg, fi=fi, kc=kc, tt=tt, pg=pg: e.matmul(
                                ps[:, pg, :], lhsT=wslot[sg][:, kc, fi * 128:(fi + 1) * 128],
                                rhs=h[:, kc, tt * 512:(tt + 1) * 512], start=(kc == 0), stop=(kc == KC - 1)),
                                reads=[bws[sg], bh[tt]], writes=[bps[pg]], inc=(kc == KC - 1))
                        for kc in range(KC):
                            sc.op("tensor", lambda e, su=su, fi=fi, kc=kc, tt=tt, pu=pu: e.matmul(
                                ps[:, pu, :], lhsT=wslot[su][:, kc, fi * 128:(fi + 1) * 128],
                                rhs=h[:, kc, tt * 512:(tt + 1) * 512], start=(kc == 0), stop=(kc == KC - 1)),
                                reads=[bws[su], bh[tt]], writes=[bps[pu]], inc=(kc == KC - 1))
                        s = pg
                        sc.op("scalar", lambda e, s=s, pg=pg: e.activation(out=tmpA[s][:], in_=ps[:, pg, :],
                                                                           func=AF.Silu),
                              reads=[bps[pg]], writes=[btA[s]])
                        sc.op("vector", lambda e, s=s, pu=pu, f=f, tt=tt: e.tensor_tensor(
                            out=act[:, f, tt * 512:(tt + 1) * 512], in0=tmpA[s][:], in1=ps[:, pu, :], op=ALU.mult),
                            reads=[btA[s], bps[pu]], writes=[bact[f][tt]])
            for tt in range(2):
                t0 = half * 1024 + tt * 512
                ti = t0 // 512
                for d in range(KC):
                    pd = next_ps([4, 5])
                    for f in range(FC):
                        sc.op("tensor", lambda e, f=f, d=d, tt=tt, pd=pd: e.matmul(
                            ps[:, pd, :], lhsT=wd[:, f, d * 128:(d + 1) * 128], rhs=act[:, f, tt * 512:(tt + 1) * 512],
                            start=(f == 0), stop=(f == FC - 1)),
                            reads=[bwd[wd_piece[f]], bact[f][tt]], writes=[bps[pd]], inc=(f == FC - 1))
                    sc.op("vector", lambda e, d=d, t0=t0, pd=pd: e.scalar_tensor_tensor(
                        out=x[:, d, t0:t0 + 512], in0=ps[:, pd, :], scalar=G[:, d:d + 1], in1=x[:, d, t0:t0 + 512],
                        op0=ALU.mult, op1=ALU.add), reads=[bps[pd], bx[d][ti], bmod], writes=[bx[d][ti]])

        ostage = [self.sb(f"ost{i}", [128, 512], F32) for i in range(2)]
        bost = [Buf() for _ in range(2)]
        osi = [0]

        def uproj(half, l):
            A = coefA[("mix_g", l)]
            sh = mod[l][:, 3 * 8:3 * 8 + 8]
            norm_mod(half, A, sh)
            uT = W[("uT", l)]
            ngr = (DIN + 255) // 256
            loaded = {}

            def load_group(gi):
                c0 = gi * 256
                loaded[gi] = load_cols(W[("w_in", l)], c0, min(256, DIN - c0))
            load_group(0)
            for gi in range(ngr):
                if gi + 1 < ngr:
                    load_group(gi + 1)
                si = loaded[gi]
                c0 = gi * 256
                ncol = min(256, DIN - c0)
                for fi in range((ncol + 127) // 128):
                    m = min(128, ncol - fi * 128)
                    for tt in range(2):
                        t0 = half * 1024 + tt * 512
                        pg = next_ps([0, 1, 2, 3])
                        for kc in range(KC):
                            sc.op("tensor", lambda e, si=si, fi=fi, kc=kc, tt=tt, pg=pg, m=m: e.matmul(
                                ps[0:m, pg, :], lhsT=wslot[si][:, kc, fi * 128:fi * 128 + m],
                                rhs=h[:, kc, tt * 512:(tt + 1) * 512], start=(kc == 0), stop=(kc == KC - 1)),
                                reads=[bws[si], bh[tt]], writes=[bps[pg]], inc=(kc == KC - 1))
                        o = osi[0] % 2
                        osi[0] += 1
                        eng = "scalar" if o == 0 else "vector"
                        if eng == "scalar":
                            sc.op("scalar", lambda e, o=o, pg=pg, m=m: e.copy(out=ostage[o][0:m, :], in_=ps[0:m, pg, :]),
                                  reads=[bps[pg]], writes=[bost[o]])
                        else:
                            sc.op("vector", lambda e, o=o, pg=pg, m=m: e.tensor_copy(out=ostage[o][0:m, :],
                                                                                     in_=ps[0:m, pg, :]),
                                  reads=[bps[pg]], writes=[bost[o]])
                        r0 = c0 + fi * 128
                        sc.dma("sync", lambda e, o=o, m=m, r0=r0, t0=t0: e.dma_start(
                            out=uT[r0:r0 + m, t0:t0 + 512], in_=ostage[o][0:m, :]), reads=[bost[o]] + bU, key=f"ost{o}")

        ystage = [act[:, i * 8:(i + 1) * 8, 0:512] for i in range(2)]
        byst = [[bact[f][0] for f in range(i * 8, (i + 1) * 8)] for i in range(2)]

        def wout(half, l):
            yT = W[("yT", l)].rearrange("(c p) t -> p c t", p=128)
            wsl = [load_cols(W[("w_out", l)], q * 256, 256) for q in range(4)]
            G = mod[l][:, 5 * 8:5 * 8 + 8]
            for tt in range(2):
                t0 = half * 1024 + tt * 512
                ti = t0 // 512
                sc.dma("gpsimd", lambda e, tt=tt, t0=t0: e.dma_start(out=ystage[tt], in_=yT[:, :, t0:t0 + 512]),
                       reads=bY, writes=byst[tt], key=f"yst{tt}")
                for d in range(KC):
                    si = wsl[d // 2]
                    dj = d % 2
                    pd = next_ps([4, 5])
                    for kc in range(KC):
                        sc.op("tensor", lambda e, si=si, dj=dj, kc=kc, tt=tt, pd=pd: e.matmul(
                            ps[:, pd, :], lhsT=wslot[si][:, kc, dj * 128:(dj + 1) * 128], rhs=ystage[tt][:, kc, :],
                            start=(kc == 0), stop=(kc == KC - 1)),
                            reads=[bws[si]] + byst[tt], writes=[bps[pd]], inc=(kc == KC - 1))
                    sc.op("vector", lambda e, d=d, t0=t0, pd=pd: e.scalar_tensor_tensor(
                        out=x[:, d, t0:t0 + 512], in0=ps[:, pd, :], scalar=G[:, d:d + 1], in1=x[:, d, t0:t0 + 512],
                        op0=ALU.mult, op1=ALU.add), reads=[bps[pd], bx[d][ti], bmod], writes=[bx[d][ti]])

        def final(half):
            g = gains[("final_g",)]
            for tt in range(2):
                t0 = half * 1024 + tt * 512
                ti = t0 // 512
                pb = 6
                for c in range(KC):
                    s = c % 2
                    sc.op("scalar", lambda e, c=c, s=s, t0=t0: e.activation(out=sqb[s][:], in_=x[:, c, t0:t0 + 512],
                                                                            func=AF.Square),
                          reads=[bx[c][ti]], writes=[bsq[s]])
                    sc.op("tensor", lambda e, c=c, s=s: e.matmul(ps[:, pb, :], lhsT=ones[:], rhs=sqb[s][:],
                                                                 start=(c == 0), stop=(c == KC - 1)),
                          reads=[bones, bsq[s]], writes=[bps[pb]])
                sc.op("scalar", lambda e: e.activation(out=tmpA[0][:], in_=ps[:, pb, :], func=AF.Sqrt,
                                                       bias=eps_t[:, 0:1], scale=1.0 / D),
                      reads=[bps[pb], bmisc], writes=[btA[0]])
                sc.op("vector", lambda e: e.reciprocal(out=rstd[:], in_=tmpA[0][:]), reads=[btA[0]], writes=[brstd])
                for c in range(KC):
                    sc.op("vector", lambda e, c=c, t0=t0: e.scalar_tensor_tensor(
                        out=x[:, c, t0:t0 + 512], in0=x[:, c, t0:t0 + 512], scalar=g[:, c:c + 1], in1=rstd[:],
                        op0=ALU.mult, op1=ALU.mult), reads=[bx[c][ti], brstd, bmisc], writes=[bx[c][ti]])

        eps_t = self.sb("eps_t", [128, 1], F32)
        sc.op("vector", lambda e: e.memset(eps_t[:], EPS), writes=[bmisc])

        for half in range(NH):
            for s in stages:
                if s[0] in ("ffn1", "ffn2"):
                    ffn(half, s[0], s[1])
                elif s[0] == "uproj":
                    uproj(half, s[1])
                elif s[0] == "wout":
                    wout(half, s[1])
                elif s[0] == "final":
                    final(half)

        allb = []
        if xo_d is not None:
            xo_v = xo_d.rearrange("(c p) t -> p c t", p=128)
            for c in range(KC):
                sc.dma("sync", lambda e, c=c: e.dma_start(out=xo_v[:, c, :], in_=x[:, c, :]), reads=bx[c], key="xo")
                allb += bx[c]
        if fz is not None:
            return allb
        sc.final_wait("sync", allb + bost)

        with nc.Block() as block:
            sc.emit(block)
        es.close()
        return nc


NBLK = 32
MOBA_SLOTS = 16
HALF_BLOCKS = ([b for b in range(NBLK) if b % 4 in (0, 3)], [b for b in range(NBLK) if b % 4 in (1, 2)])
NEG = -30000.0


def moba_emit(P, sc, nu, pfx="", fz=None):
    nc, es = P.nc, P.es
    NQ = MOBA_SLOTS * 256
    if fz is None:
        mq = P.din(pfx + "mq", [nu, 64, NQ])
        mqs = P.din(pfx + "mqs", [nu, 16, NQ])
        mk = P.din(pfx + "mk", [nu, 64, S])
        mks = P.din(pfx + "mks", [nu, 16, S])
        mv = P.din(pfx + "mv", [nu, S, 64])
        yo = P.dout(pfx + "moT", [nu, 64, NQ])
    cq = P.din("ropeq", [nu, 2, 16, NQ])
    ck = P.din("ropek", [2, 16, S])
    pm_d = P.din("pm", [nu, 128, MOBA_SLOTS * NBLK])
    oh_d = P.din("oh", [nu, 128, MOBA_SLOTS * NBLK])
    cm_d = P.din("cm", [nu, 2, 4, 128, 256])
    boh_d = P.din("boh", [32, S])
    id_d = P.din("ident", [128, 128])

    qaug = P.sb(pfx + "qaug", [128, NQ], BF16)
    kaug = P.sb(pfx + "kaug", [128, S], BF16)
    vaug = P.sb(pfx + "vaug", [128, 64, 128], BF16)
    qf = P.sb(pfx + "qf", [64, NQ], F32)
    xt = [P.sb(pfx + f"xt{i}", [64, 1024], F32) for i in range(2)]
    xs = [P.sb(pfx + f"xs{i}", [16, 1024], F32) for i in range(2)]
    ct = [P.sb(pfx + f"ct{i}", [16, 2, 1024], F32) for i in range(2)]
    t16 = P.sb(pfx + "t16", [16, 1024], F32)
    sqf = P.sb(pfx + "sqf", [64, 1024], F32)
    kmean = P.sb(pfx + "kmean", [64, NBLK], F32)
    mx = P.sb(pfx + "mx", [128, 4], F32)
    nbias = P.sb(pfx + "nbias", [128, 1], F32)
    onesf = P.sb(pfx + "onesf", [64, 128], F32)
    ident = P.sb(pfx + "ident_sb", [128, 128], F32)
    pm = P.sb(pfx + "pm_sb", [128, MOBA_SLOTS * NBLK], F32)
    oh = P.sb(pfx + "oh_sb", [128, MOBA_SLOTS * NBLK], F32)
    cm = P.sb(pfx + "cm_sb", [128, 8, 256], F32)
    gs = P.sb(pfx + "gs", [128, NBLK], F32)
    g8 = P.sb(pfx + "g8", [128, 8], F32)
    m1 = P.sb(pfx + "m1", [128, NBLK], F32)
    m2 = P.sb(pfx + "m2", [128, NBLK], F32)
    stm = [P.sb(pfx + f"stm{i}", [128, 256], F32) for i in range(2)]
    pt = [P.sb(pfx + f"pt{i}", [128, 256], BF16) for i in range(4)]
    rec = P.sb(pfx + "rec", [64, 256], F32)
    yst = [P.sb(pfx + f"yst{i}", [64, 256], F32) for i in range(2)]
    ps = es.enter_context(nc.psum_tensor(pfx + "mps", [128, 8, 512], F32)) if fz is None else fz.ps
    if fz is not None:
        vt = P.sb(pfx + "vt", [64, 1024], F32)
        bvt = Buf()

    def rows6(e, u, r0, nr, cols):
        return fz.Urecv[r0:r0 + 5 * 64 + nr, cols][bass.ds(fz.dyn(e, "sync", ("mhr", u)), nr), :]

    bq, bk, bv, bqf = Buf(), Buf(), Buf(), Buf()
    bxt = [Buf(), Buf()]
    bxs = [Buf(), Buf()]
    bct = [Buf(), Buf()]
    bt16, bsqf, bkm, bmx, bnb, bconst, bmask = Buf(), Buf(), Buf(), Buf(), Buf(), Buf(), Buf()
    bgs, bg8, bm1, bm2 = Buf(), Buf(), Buf(), Buf()
    bstm = [Buf(), Buf()]
    bpt = [Buf(), Buf(), Buf(), Buf()]
    brec = Buf()
    byst = [Buf(), Buf()]
    bps = [Buf() for _ in range(8)]

    sc.dma("sync", lambda e: e.dma_start(out=ident[:], in_=id_d[:, :]), writes=[bconst], key=pfx + "mconst")
    sc.op("vector", lambda e: e.memset(onesf[:], 1.0), writes=[bconst])
    sc.op("vector", lambda e: e.memset(kaug[32:64, :], 0.0), writes=[bk])
    sc.op("vector", lambda e: e.memset(kaug[32:33, :], 1.0), writes=[bk])
    sc.dma("gpsimd", lambda e: e.dma_start(out=kaug[0:32, :], in_=boh_d[:, :]), writes=[bk], key=pfx + "mk0")
    sc.op("vector", lambda e: e.memset(qaug[32:64, :], 0.0), writes=[bq])
    sc.op("vector", lambda e: e.memset(vaug[:, :, 64:128], 1.0), writes=[bv])

    cnt = [0]
    for u in range(nu):
        sc.dma("sync", lambda e, u=u: e.dma_start(out=pm[:], in_=pm_d[u]), writes=[bmask], key=pfx + "mmask")
        sc.dma("sync", lambda e, u=u: e.dma_start(out=oh[:], in_=oh_d[u]), writes=[bmask], key=pfx + "mmask")
        sc.dma("sync", lambda e, u=u: e.dma_start(out=cm[:], in_=cm_d[u].rearrange("a k p q -> p (a k) q")),
               writes=[bmask], key=pfx + "mmask")
        if fz is None:
            for k0 in range(0, 64, 16):
                sc.dma("gpsimd", lambda e, u=u, k0=k0: e.dma_start(
                    out=vaug[:, k0:k0 + 16, 0:64], in_=mv[u].rearrange("(k p) d -> p k d", p=128)[:, k0:k0 + 16, :]),
                    writes=[bv], key=pfx + "mv")
        else:
            for c0 in range(0, S, 1024):
                rr, t0 = c0 // TOK, c0 % TOK

                def vsrc(e, u=u, c0=c0):
                    return fz.LV[u, :, c0:c0 + 1024]
                sc.dma("sync", lambda e, vsrc=vsrc: e.dma_start(out=vt[:], in_=vsrc(e)), reads=[fz.bUr], writes=[bvt],
                       key=pfx + "mvt")
                for cj in range(8):
                    sc.op("tensor", lambda e, cj=cj: e.transpose(ps[:, 6, cj * 64:(cj + 1) * 64], vt[:, cj * 128:(cj + 1) * 128],
                                                                 ident[0:64, 0:64]),
                          reads=[bvt, bconst], writes=[bps[6]], inc=(cj == 7))
                k0 = c0 // 128
                sc.op("vector", lambda e, k0=k0: e.tensor_copy(out=vaug[:, k0:k0 + 8, 0:64],
                                                               in_=ps[:, 6, :].rearrange("p (a b) -> p a b", b=64)),
                      reads=[bps[6]], writes=[bv])
        sc.op("vector", lambda e: e.memset(mx[:], 0.0), writes=[bmx])
        srcs_ = ((mk, mks, None, S), (mq, mqs, cq, NQ)) if fz is None else ((None, None, None, S), (None, None, cq, NQ))
        for which, (src, srcs, tab, ncols) in enumerate(srcs_):
            for c0 in range(0, ncols, 1024):
                i = cnt[0] % 2
                cnt[0] += 1
                if fz is None:
                    sc.dma("sync", lambda e, i=i, c0=c0, src=src, u=u: e.dma_start(out=xt[i][:], in_=src[u, :, c0:c0 + 1024]),
                           writes=[bxt[i]], key=pfx + f"mxt{i}")
                    sc.dma("sync", lambda e, i=i, c0=c0, srcs=srcs, u=u: e.dma_start(out=xs[i][:], in_=srcs[u, :, c0:c0 + 1024]),
                           writes=[bxs[i]], key=pfx + f"mxs{i}")
                elif which == 0:
                    rr, t0 = c0 // TOK, c0 % TOK

                    def ksrc(e, ro, nr, u=u, c0=c0):
                        return fz.LK[u, ro:ro + nr, c0:c0 + 1024]
                    sc.dma("sync", lambda e, i=i, ksrc=ksrc: e.dma_start(out=xt[i][:], in_=ksrc(e, 0, 64)),
                           reads=[fz.bUr], writes=[bxt[i]], key=pfx + f"mxt{i}")
                    sc.dma("sync", lambda e, i=i, ksrc=ksrc: e.dma_start(out=xs[i][0:8, :], in_=ksrc(e, 8, 8)),
                           reads=[fz.bUr], writes=[bxs[i]], key=pfx + f"mxs{i}")
                    sc.dma("sync", lambda e, i=i, ksrc=ksrc: e.dma_start(out=xs[i][8:16, :], in_=ksrc(e, 0, 8)),
                           reads=[fz.bUr], writes=[bxs[i]], key=pfx + f"mxs{i}")
                else:
                    def qsrc(e, ro, nr, u=u, c0=c0):
                        return fz.LQ[u, ro:ro + nr, c0:c0 + 1024]
                    sc.dma("sync", lambda e, i=i, qsrc=qsrc: e.dma_start(out=xt[i][:], in_=qsrc(e, 0, 64)),
                           reads=[fz.bUr], writes=[bxt[i]], key=pfx + f"mxt{i}")
                    sc.dma("sync", lambda e, i=i, qsrc=qsrc: e.dma_start(out=xs[i][0:8, :], in_=qsrc(e, 8, 8)),
                           reads=[fz.bUr], writes=[bxs[i]], key=pfx + f"mxs{i}")
                    sc.dma("sync", lambda e, i=i, qsrc=qsrc: e.dma_start(out=xs[i][8:16, :], in_=qsrc(e, 0, 8)),
                           reads=[fz.bUr], writes=[bxs[i]], key=pfx + f"mxs{i}")
                if which == 0:
                    sc.dma("sync", lambda e, i=i, c0=c0: e.dma_start(
                        out=ct[i][:], in_=ck[:, :, c0:c0 + 1024].rearrange("a p t -> p a t")),
                        writes=[bct[i]], key=pfx + f"mct{i}")
                else:
                    sc.dma("sync", lambda e, i=i, c0=c0, u=u: e.dma_start(
                        out=ct[i][:], in_=cq[u, :, :, c0:c0 + 1024].rearrange("a p t -> p a t")),
                        writes=[bct[i]], key=pfx + f"mct{i}")
                sc.op("vector", lambda e, i=i: e.tensor_tensor(out=t16[:], in0=xs[i][:], in1=ct[i][:, 1, :], op=ALU.mult),
                      reads=[bxs[i], bct[i]], writes=[bt16])
                sc.op("vector", lambda e, i=i: e.tensor_tensor(out=xt[i][0:16, :], in0=xt[i][0:16, :], in1=ct[i][:, 0, :],
                                                               op=ALU.mult), reads=[bxt[i], bct[i]], writes=[bxt[i]])
                sc.op("vector", lambda e, i=i: e.tensor_tensor(out=xt[i][0:16, :], in0=xt[i][0:16, :], in1=t16[:],
                                                               op=ALU.add), reads=[bxt[i], bt16], writes=[bxt[i]])
                sc.op("scalar", lambda e, i=i: e.activation(out=sqf[:], in_=xt[i][:], func=AF.Square),
                      reads=[bxt[i]], writes=[bsqf])
                for hh in range(2):
                    sc.op("tensor", lambda e, hh=hh: e.matmul(ps[:, 6, :], lhsT=onesf[:], rhs=sqf[:, hh * 512:(hh + 1) * 512],
                                                              start=True, stop=True), reads=[bconst, bsqf], writes=[bps[6]])
                    sc.op("vector", lambda e, which=which: e.tensor_reduce(out=mx[:, 2:3], in_=ps[:, 6, :], axis=mybir.AxisListType.X,
                                                                           op=ALU.max), reads=[bps[6]], writes=[bmx])
                    sc.op("vector", lambda e, which=which: e.tensor_tensor(out=mx[:, which:which + 1], in0=mx[:, which:which + 1],
                                                                           in1=mx[:, 2:3], op=ALU.max), reads=[bmx], writes=[bmx])
                if which == 0:
                    nb0 = c0 // 256
                    sc.op("vector", lambda e, i=i, nb0=nb0: e.tensor_reduce(
                        out=kmean[:, nb0:nb0 + 4], in_=xt[i][:].rearrange("p (n t) -> p n t", t=256),
                        axis=mybir.AxisListType.X, op=ALU.add), reads=[bxt[i]], writes=[bkm])
                    sc.op("scalar", lambda e, i=i, c0=c0: e.copy(out=kaug[64:128, c0:c0 + 1024], in_=xt[i][:]),
                          reads=[bxt[i]], writes=[bk])
                else:
                    sc.op("scalar", lambda e, i=i, c0=c0: e.mul(out=qaug[64:128, c0:c0 + 1024], in_=xt[i][:], mul=0.125),
                          reads=[bxt[i]], writes=[bq])
                    sc.op("vector", lambda e, i=i, c0=c0: e.tensor_copy(out=qf[:, c0:c0 + 1024], in_=xt[i][:]),
                          reads=[bxt[i]], writes=[bqf])
        sc.op("vector", lambda e: e.tensor_tensor(out=mx[:, 3:4], in0=mx[:, 0:1], in1=mx[:, 1:2], op=ALU.mult),
              reads=[bmx], writes=[bmx])
        sc.op("scalar", lambda e: e.activation(out=mx[:, 3:4], in_=mx[:, 3:4], func=AF.Sqrt), reads=[bmx], writes=[bmx])
        sc.op("vector", lambda e: e.tensor_scalar(out=nbias[:], in0=mx[:, 3:4], scalar1=-0.125, scalar2=None, op0=ALU.mult),
              reads=[bmx], writes=[bnb])
        for t in range(NQ // 128):
            r = t // 2
            sc.op("tensor", lambda e, t=t: e.matmul(ps[:, 7, 0:NBLK], lhsT=qf[:, t * 128:(t + 1) * 128], rhs=kmean[:],
                                                    start=True, stop=True), reads=[bqf, bkm], writes=[bps[7]])
            sc.op("vector", lambda e, r=r: e.tensor_tensor(out=gs[:], in0=ps[:, 7, 0:NBLK], in1=pm[:, r * NBLK:(r + 1) * NBLK],
                                                           op=ALU.add), reads=[bps[7], bmask], writes=[bgs])
            sc.op("vector", lambda e: e.max(out=g8[:], in_=gs[:]), reads=[bgs], writes=[bg8])
            sc.op("vector", lambda e: e.tensor_scalar(out=m1[:], in0=gs[:], scalar1=g8[:, 2:3], scalar2=None, op0=ALU.is_ge),
                  reads=[bgs, bg8], writes=[bm1])
            sc.op("vector", lambda e: e.tensor_scalar(out=m2[:], in0=gs[:], scalar1=-1e29, scalar2=None, op0=ALU.is_gt),
                  reads=[bgs], writes=[bm2])
            sc.op("vector", lambda e: e.tensor_tensor(out=m1[:], in0=m1[:], in1=m2[:], op=ALU.mult),
                  reads=[bm1, bm2], writes=[bm1])
            sc.op("vector", lambda e, r=r: e.tensor_tensor(out=m1[:], in0=m1[:], in1=oh[:, r * NBLK:(r + 1) * NBLK], op=ALU.add),
                  reads=[bm1, bmask], writes=[bm1])
            sc.op("vector", lambda e: e.tensor_scalar(out=m2[:], in0=m1[:], scalar1=-1.0, scalar2=-NEG, op0=ALU.add, op1=ALU.mult),
                  reads=[bm1], writes=[bm2])
            sc.op("tensor", lambda e: e.transpose(ps[0:NBLK, 7, 128:256], m2[:], ident[:]),
                  reads=[bm2, bconst], writes=[bps[7]])
            sc.op("vector", lambda e, t=t: e.tensor_copy(out=qaug[0:32, t * 128:(t + 1) * 128], in_=ps[0:NBLK, 7, 128:256]),
                  reads=[bps[7]], writes=[bq])
        tasks = [(r, kt) for r in range(MOBA_SLOTS) for kt in range(4 * r + 4)]
        NB, DEPTH = 4, 3

        def qk(i):
            r, kt = tasks[i]
            KT = 4 * r + 4
            p = i % NB
            sc.op("tensor", lambda e, kt=kt, r=r, p=p: e.matmul(ps[:, p, 0:256], lhsT=kaug[:, kt * 128:(kt + 1) * 128],
                                                               rhs=qaug[:, r * 256:(r + 1) * 256], start=True, stop=True),
                  reads=[bk, bq], writes=[bps[p]])
            if kt >= KT - 4:
                j = kt - (KT - 4) + 4 * (r % 2)
                s = j % 2
                sc.op("vector", lambda e, p=p, j=j, s=s: e.tensor_tensor(out=stm[s][:], in0=ps[:, p, 0:256], in1=cm[:, j, :],
                                                                        op=ALU.add), reads=[bps[p], bmask], writes=[bstm[s]])
                sc.op("scalar", lambda e, p=p, s=s: e.activation(out=pt[p][:], in_=stm[s][:], func=AF.Exp, bias=nbias[:, 0:1],
                                                                 scale=1.0), reads=[bstm[s], bnb], writes=[bpt[p]])
            else:
                sc.op("scalar", lambda e, p=p: e.activation(out=pt[p][:], in_=ps[:, p, 0:256], func=AF.Exp, bias=nbias[:, 0:1],
                                                            scale=1.0), reads=[bps[p], bnb], writes=[bpt[p]])

        def pv(i):
            r, kt = tasks[i]
            KT = 4 * r + 4
            p = i % NB
            po = 4 + (r % 2)
            sc.op("tensor", lambda e, kt=kt, p=p, po=po, KT=KT: e.matmul(ps[:, po, 0:256], lhsT=vaug[:, kt, :], rhs=pt[p][:],
                                                                        start=(kt == 0), stop=(kt == KT - 1)),
                  reads=[bv, bpt[p]], writes=[bps[po]])
            if kt < KT - 1:
                return
            sc.op("vector", lambda e, po=po: e.reciprocal(out=rec[:], in_=ps[64:128, po, 0:256]), reads=[bps[po]], writes=[brec])
            ys = r % 2
            sc.op("vector", lambda e, po=po, ys=ys: e.tensor_tensor(out=yst[ys][:], in0=ps[0:64, po, 0:256], in1=rec[:], op=ALU.mult),
                  reads=[bps[po], brec], writes=[byst[ys]])
            if fz is None:
                sc.dma("sync", lambda e, u=u, r=r, ys=ys: e.dma_start(out=yo[u, :, r * 256:(r + 1) * 256], in_=yst[ys][:]),
                       reads=[byst[ys]], key=pfx + f"myo{ys}")
            else:
                row0 = (r // 4) * 512 + u * 64
                sc.dma("sync", lambda e, row0=row0, r=r, ys=ys: e.dma_start(
                    out=fz.Ysend[row0:row0 + 64, (r % 4) * 256:(r % 4 + 1) * 256], in_=yst[ys][:]),
                    reads=[byst[ys], fz.bYs], key=pfx + f"myo{ys}")

        for i in range(min(DEPTH, len(tasks))):
            qk(i)
        for i in range(len(tasks)):
            if i + DEPTH < len(tasks):
                qk(i + DEPTH)
            pv(i)
    return byst


class SimpleProg:
    def __init__(self):
        self.fused = None
        self.nc = bass.Bass("TRN2", target_bir_lowering=False)
        self.es = ExitStack()
        self.in_names = []
        self.out_names = []

    din = TokProg.din
    dout = TokProg.dout
    sb = TokProg.sb

    def finish(self, sc, outbufs):
        sc.final_wait("sync", outbufs)
        with self.nc.Block() as block:
            sc.emit(block)
        self.es.close()
        return self.nc


def build_moba(nu=3):
    P = SimpleProg()
    sc = Sched(P.nc, P.es)
    ob = moba_emit(P, sc, nu)
    return P, P.finish(sc, ob)


def rope_tables():
    inv = np.exp(np.float32(-np.log(500000.0)) * np.arange(0, 16, 2, dtype=np.float32) / np.float32(16)).astype(np.float32)
    ang = (np.arange(S, dtype=np.float32)[:, None] * inv[None, :]).astype(np.float32)
    cos = np.cos(ang).astype(np.float32).T
    sin = np.sin(ang).astype(np.float32).T
    tab = np.zeros((2, 16, S), np.float32)
    tab[0, 0:8] = cos
    tab[0, 8:16] = cos
    tab[1, 0:8] = -sin
    tab[1, 8:16] = sin
    return tab


def moba_unit_inputs(uq, uk, uv, half, tab):
    blocks = HALF_BLOCKS[half]
    qpos = np.concatenate([np.arange(b * 256, (b + 1) * 256) for b in blocks])
    qT = np.ascontiguousarray(uq[qpos].T)
    kT = np.ascontiguousarray(uk.T)
    sw = np.r_[8:16, 0:8]
    return dict(mq=qT, mqs=np.ascontiguousarray(qT[sw]), mk=kT, mks=np.ascontiguousarray(kT[sw]), mv=np.ascontiguousarray(uv),
                ropeq=np.ascontiguousarray(tab[:, :, qpos]))


def moba_const_inputs(half):
    blocks = HALF_BLOCKS[half]
    pm = np.zeros((MOBA_SLOTS, NBLK), np.float32)
    oh = np.zeros((MOBA_SLOTS, NBLK), np.float32)
    for r, b in enumerate(blocks):
        pm[r, b:] = -1e30
        oh[r, b] = 1.0
    kk = np.arange(128)[:, None]
    qq = np.arange(256)[None, :]
    M0 = np.where(kk <= qq, 0.0, NEG).astype(np.float32)
    M1 = np.where(kk + 128 <= qq, 0.0, NEG).astype(np.float32)
    Z = np.zeros((128, 256), np.float32)
    cm = np.zeros((2, 4, 128, 256), np.float32)
    for par in range(2):
        r = par
        b = blocks[r]
        if b == 2 * r + 1:
            cm[par] = np.stack([Z, Z, M0, M1])
        else:
            cm[par] = np.stack([M0, M1, Z, Z])
    pmb = np.ascontiguousarray(np.broadcast_to(pm.reshape(1, -1), (128, MOBA_SLOTS * NBLK)))
    ohb = np.ascontiguousarray(np.broadcast_to(oh.reshape(1, -1), (128, MOBA_SLOTS * NBLK)))
    return dict(pm=pmb, oh=ohb, cm=cm)


def moba_shared_inputs(tab):
    boh = np.zeros((32, S), np.float32)
    for n in range(32):
        boh[n, n * 256:(n + 1) * 256] = 1.0
    return dict(ropek=tab, boh=boh, ident=np.eye(128, dtype=np.float32))


CH = 32


def conv_emit(P, sc, pfx="", fz=None):
    nc, es = P.nc, P.es
    T = TOK
    if fz is None:
        uc = P.din(pfx + "uc", [512, T + CH])
        yc = P.dout(pfx + "ycT", [256, T])
    else:
        yc = fz.Yfull
        flag_d = P.din("cflag", [128, 1])
        flag = P.sb(pfx + "cflag_sb", [128, 1], F32)
    cw = P.din(pfx + "cw", [128, 2, 31])
    cp = P.din(pfx + "cp", [128, 2, 3])
    idb = P.din("cident", [128, 128])
    a_t = [P.sb(pfx + f"ca{c}", [128, T + CH], F32) for c in range(2)]
    g_t = [P.sb(pfx + f"cg{c}", [128, T + CH], F32) for c in range(2)]
    hg = [P.sb(pfx + f"chg{c}", [128, T + CH], BF16) for c in range(2)]
    dg = [P.sb(pfx + f"cdg{c}", [128, 31, 128], BF16) for c in range(2)]
    cws = P.sb(pfx + "cws", [128, 2, 31], F32)
    cps = P.sb(pfx + "cps", [128, 2, 3], F32)
    idt = P.sb(pfx + "cidt", [128, 128], F32)
    onesf = P.sb(pfx + "cones", [128, 128], F32)
    epsc = P.sb(pfx + "ceps", [128, 1], F32)
    hc = [P.sb(pfx + f"chc{c}", [128, 512], F32) for c in range(2)]
    sq = [P.sb(pfx + f"csq{c}", [128, 512], F32) for c in range(2)]
    mean = P.sb(pfx + "cmean", [128, 512], F32)
    msq = P.sb(pfx + "cmsq", [128, 512], F32)
    var = P.sb(pfx + "cvar", [128, 512], F32)
    rstd = P.sb(pfx + "crstd", [128, 512], F32)
    tt_ = [P.sb(pfx + f"ctt{c}", [128, 512], F32) for c in range(2)]
    yo = [P.sb(pfx + f"cyo{c}", [128, 512], F32) for c in range(2)]
    ps = es.enter_context(nc.psum_tensor(pfx + "cps_", [128, 4, 512], F32)) if fz is None else fz.ps
    ba = [Buf(), Buf()]
    bg = [Buf(), Buf()]
    bhg = [Buf(), Buf()]
    bdg = [Buf(), Buf()]
    bc, bhc, bsq = Buf(), [Buf(), Buf()], [Buf(), Buf()]
    bmean, bmsq, bvar, brstd = Buf(), Buf(), Buf(), Buf()
    btt = [Buf(), Buf()]
    byo = [Buf(), Buf()]
    bps = [Buf() for _ in range(4)]

    sc.dma("sync", lambda e: e.dma_start(out=cws[:], in_=cw[:, :, :]), writes=[bc], key=pfx + "cc")
    sc.dma("sync", lambda e: e.dma_start(out=cps[:], in_=cp[:, :, :]), writes=[bc], key=pfx + "cc")
    sc.dma("sync", lambda e: e.dma_start(out=idt[:], in_=idb[:, :]), writes=[bc], key=pfx + "cc")
    sc.op("vector", lambda e: e.memset(onesf[:], 1.0), writes=[bc])
    sc.op("vector", lambda e: e.memset(epsc[:], EPS), writes=[bc])
    if fz is not None:
        sc.dma("sync", lambda e: e.dma_start(out=flag[:], in_=flag_d[:, :]), writes=[bc], key=pfx + "cc")

    def prev_rows(e, row0):
        return fz.LH[row0:row0 + 128, :]

    for c in range(2):
        if fz is None:
            sc.dma("sync", lambda e, c=c: e.dma_start(out=a_t[c][:], in_=uc[c * 128:(c + 1) * 128, :]), writes=[ba[c]],
                   key=pfx + f"ca{c}")
            sc.dma("sync", lambda e, c=c: e.dma_start(out=g_t[c][:], in_=uc[256 + c * 128:256 + (c + 1) * 128, :]),
                   writes=[bg[c]], key=pfx + f"cg{c}")
        else:
            sc.dma("sync", lambda e, c=c: e.dma_start(out=a_t[c][:, CH:], in_=fz.Usend[c * 128:(c + 1) * 128, :]),
                   reads=[fz.bU], writes=[ba[c]], key=pfx + f"ca{c}")
            sc.dma("sync", lambda e, c=c: e.dma_start(out=a_t[c][:, 0:CH], in_=prev_rows(e, c * 128)),
                   reads=[fz.bUr], writes=[ba[c]], key=pfx + f"ca{c}")
            sc.dma("sync", lambda e, c=c: e.dma_start(out=g_t[c][:, CH:], in_=fz.Usend[256 + c * 128:256 + (c + 1) * 128, :]),
                   reads=[fz.bU], writes=[bg[c]], key=pfx + f"cg{c}")
            sc.dma("sync", lambda e, c=c: e.dma_start(out=g_t[c][:, 0:CH], in_=prev_rows(e, 256 + c * 128)),
                   reads=[fz.bUr], writes=[bg[c]], key=pfx + f"cg{c}")
        sc.op("scalar", lambda e, c=c: e.activation(out=g_t[c][:], in_=g_t[c][:], func=AF.Sigmoid),
              reads=[bg[c]], writes=[bg[c]])
        sc.op("vector", lambda e, c=c: e.tensor_tensor(out=hg[c][:], in0=a_t[c][:], in1=g_t[c][:], op=ALU.mult),
              reads=[ba[c], bg[c]], writes=[bhg[c]])
        if fz is not None:
            sc.op("vector", lambda e, c=c: e.tensor_scalar(out=hg[c][:, 0:CH], in0=hg[c][:, 0:CH], scalar1=flag[:, 0:1],
                                                           scalar2=None, op0=ALU.mult), reads=[bhg[c], bc], writes=[bhg[c]])
        for k in range(31):
            sc.op("gpsimd", lambda e, c=c, k=k: e.tensor_scalar(out=dg[c][:, k, :], in0=idt[:], scalar1=cws[:, c, k:k + 1],
                                                                scalar2=None, op0=ALU.mult), reads=[bc], writes=[bdg[c]])
    for tt in range(T // 512):
        for c in range(2):
            for k in range(31):
                o = tt * 512 + 2 + k
                sc.op("tensor", lambda e, c=c, k=k, o=o: e.matmul(ps[:, c, :], lhsT=dg[c][:, k, :], rhs=hg[c][:, o:o + 512],
                                                                 start=(k == 0), stop=(k == 30)),
                      reads=[bdg[c], bhg[c]], writes=[bps[c]], inc=(k == 30))
            sc.op("scalar", lambda e, c=c: e.activation(out=hc[c][:], in_=ps[:, c, :], func=AF.Identity, bias=cps[:, c, 0:1],
                                                        scale=1.0), reads=[bps[c], bc], writes=[bhc[c]])
            sc.op("scalar", lambda e, c=c: e.activation(out=sq[c][:], in_=hc[c][:], func=AF.Square), reads=[bhc[c]],
                  writes=[bsq[c]])
        for c in range(2):
            sc.op("tensor", lambda e, c=c: e.matmul(ps[:, 2, :], lhsT=onesf[:], rhs=hc[c][:], start=(c == 0), stop=(c == 1)),
                  reads=[bc, bhc[c]], writes=[bps[2]])
        for c in range(2):
            sc.op("tensor", lambda e, c=c: e.matmul(ps[:, 3, :], lhsT=onesf[:], rhs=sq[c][:], start=(c == 0), stop=(c == 1)),
                  reads=[bc, bsq[c]], writes=[bps[3]])
        sc.op("vector", lambda e: e.tensor_scalar(out=mean[:], in0=ps[:, 2, :], scalar1=1.0 / 256, scalar2=None, op0=ALU.mult),
              reads=[bps[2]], writes=[bmean])
        sc.op("vector", lambda e: e.tensor_tensor(out=msq[:], in0=mean[:], in1=mean[:], op=ALU.mult), reads=[bmean], writes=[bmsq])
        sc.op("vector", lambda e: e.scalar_tensor_tensor(out=var[:], in0=ps[:, 3, :], scalar=1.0 / 256, in1=msq[:],
                                                         op0=ALU.mult, op1=ALU.subtract), reads=[bps[3], bmsq], writes=[bvar])
        sc.op("scalar", lambda e: e.activation(out=var[:], in_=var[:], func=AF.Sqrt, bias=epsc[:, 0:1], scale=1.0),
              reads=[bvar, bc], writes=[bvar])
        sc.op("vector", lambda e: e.reciprocal(out=rstd[:], in_=var[:]), reads=[bvar], writes=[brstd])
        for c in range(2):
            sc.op("vector", lambda e, c=c: e.tensor_tensor(out=tt_[c][:], in0=hc[c][:], in1=mean[:], op=ALU.subtract),
                  reads=[bhc[c], bmean], writes=[btt[c]])
            sc.op("vector", lambda e, c=c: e.tensor_tensor(out=tt_[c][:], in0=tt_[c][:], in1=rstd[:], op=ALU.mult),
                  reads=[btt[c], brstd], writes=[btt[c]])
            sc.op("scalar", lambda e, c=c: e.activation(out=yo[c][:], in_=tt_[c][:], func=AF.Silu, bias=cps[:, c, 2:3],
                                                        scale=cps[:, c, 1:2]), reads=[btt[c], bc], writes=[byo[c]])
            sc.dma("sync", lambda e, c=c, tt=tt: e.dma_start(out=yc[c * 128:(c + 1) * 128, tt * 512:(tt + 1) * 512], in_=yo[c][:]),
                   reads=[byo[c]] + ([] if fz is None else [fz.bY]), key=pfx + f"cyo{c}")
    return byo


def build_conv():
    P = SimpleProg()
    sc = Sched(P.nc, P.es)
    ob = conv_emit(P, sc)
    return P, P.finish(sc, ob)


def conv_inputs(u_b, j, conv_w, conv_b, ln_g, ln_b):
    t0 = j * TOK
    uc = np.zeros((512, TOK + CH), np.float32)
    lo = max(0, t0 - CH)
    uc[:, CH - (t0 - lo):] = u_b[lo:t0 + TOK, 0:512].T
    lay = lambda v: np.ascontiguousarray(v.reshape(2, 128).T)
    cw = np.ascontiguousarray(conv_w.T.reshape(2, 128, 31).transpose(1, 0, 2))
    cp = np.ascontiguousarray(np.stack([lay(conv_b), lay(ln_g), lay(ln_b)], axis=-1))
    return dict(uc=uc, cw=cw, cp=cp, cident=np.eye(128, dtype=np.float32))


GC = 64
NCH = S // GC
GSEG = 16
AX = mybir.AxisListType


def gdn_emit(P, sc, nu, pfx="", fz=None):
    import os
    STOP = float(os.environ.get("GDN_STOP", "99"))
    nc, es = P.nc, P.es
    if fz is None:
        raw_d = P.din(pfx + "graw", [nu, 3, 64, S + 3])
        gz_d = P.din(pfx + "gz", [nu, S, 64])
        ga_d = P.din(pfx + "ga", [nu, 64, NCH])
        gb_d = P.din(pfx + "gb", [nu, 64, NCH])
        go_d = P.dout(pfx + "go", [nu, S, 64])
    gcw_d = P.din(pfx + "gcw", [nu, 64, 12])
    gpar_d = P.din(pfx + "gpar", [nu, 64, 2])
    gng_d = P.din(pfx + "gng", [nu, 64, 64])
    gcst_d = P.din("gcst", [3, 64, 64])

    def unit_h(e, u):
        return fz.dyn(e, "gpsimd", ("gh", u))

    f = lambda n, shp: P.sb(pfx + n, shp, F32)
    cst = f("gcst_sb", [64, 3, 64])
    TriB = f("gTriB", [64, 8, 64])
    MB = f("gMB", [64, 8, 64])
    IB = f("gIB", [64, 8, 64])
    ones64 = f("gones", [64, 64])
    epsg = f("geps", [64, 1])
    bcst = Buf()
    sc.dma("sync", lambda e: e.dma_start(out=cst[:], in_=gcst_d.rearrange("a p q -> p a q")), writes=[bcst], key=pfx + "gc")
    sc.op("vector", lambda e: e.memset(ones64[:], 1.0), writes=[bcst])
    sc.op("vector", lambda e: e.memset(epsg[:], EPS), writes=[bcst])
    for j in range(8):
        sc.op("vector", lambda e, j=j: e.tensor_copy(out=TriB[:, j, :], in_=cst[:, 0, :]), reads=[bcst], writes=[bcst])
        sc.op("vector", lambda e, j=j: e.tensor_copy(out=MB[:, j, :], in_=cst[:, 1, :]), reads=[bcst], writes=[bcst])
        sc.op("vector", lambda e, j=j: e.tensor_copy(out=IB[:, j, :], in_=cst[:, 2, :]), reads=[bcst], writes=[bcst])
    Tri = cst[:, 0, :]
    I64 = cst[:, 2, :]

    def bc_n(t, n0):
        return t[:, n0:n0 + 8].unsqueeze(2).to_broadcast([64, 8, 64])


    def emit_unit(u, sc):
        u2 = u % 2
        GB, SB0 = 4 * u2, 4 * u2 + 3
        f = lambda n, shp: P.sb(pfx + f"u{u}_" + n, shp, F32)
        gcw = f("gcw_sb", [64, 12])
        dgw = P.sb(pfx + f"u{u}_" + "gdgw", [64, 12, 64], BF16)
        par = f("gpar_sb", [64, 2])
        negA = f("gnegA", [64, 1])
        ngb = f("gngb", [64, 64])
        a_t = f("ga_sb", [64, NCH])
        b_t = f("gb_sb", [64, NCH])
        g_t = f("gg", [64, NCH])
        beta = f("gbeta", [64, NCH])
        gc = f("ggc", [64, NCH])
        egc = f("gegc", [64, NCH])
        eglb = f("geglb", [64, NCH])
        edec = f("gedec", [64, NCH])
        bgk = f("gbgk", [64, NCH])
        SEGT = S // GSEG
        SEGC = NCH // GSEG
        raw = [P.sb(pfx + f"u{u}_" + f"graw{i}", [64, 515], BF16) for i in range(2)]
        xa = [f(f"gxa{i}", [64, 512]) for i in range(2)]
        xq = f("gxq", [64, 512])
        rn = f("grn", [64, 512])
        qnT = f("gqnT", [64, SEGT])
        knT = f("gknT", [64, SEGT])
        Kt = f("gKt", [64, SEGC, 64])
        Vt = f("gVt", [64, SEGC, 64])
        oseg = f("goseg", [64, SEGC, 64])
        zseg = f("gzseg", [64, SEGC, 64])
        osq = f("gosq", [64, SEGC, 64])
        oss = f("goss", [64, SEGC])
        rhsD = f("grhsD", [64, 8, 64])
        ED = f("gED", [64, 8, 64])
        EDT = f("gEDT", [64, 8, 64])
        Lp = [f(f"gL{i}", [64, 8, 64]) for i in range(2)]
        Np = [f(f"gN{i}", [64, 8, 64]) for i in range(2)]
        Pm = f("gP", [64, 8, 64])
        Lb = [P.sb(pfx + f"u{u}_" + f"gLb{i}", [64, 8, 64], BF16) for i in range(2)]
        Nb = [P.sb(pfx + f"u{u}_" + f"gNb{i}", [64, 8, 64], BF16) for i in range(2)]
        Pb = P.sb(pfx + f"u{u}_" + "gPb", [64, 8, 64], BF16)
        bLb, bNb, bPb = [Buf(), Buf()], [Buf(), Buf()], Buf()
        Kbg = f("gKbg", [64, 8, 64])
        Vb = f("gVb", [64, 8, 64])
        kdec = f("gkdec", [64, 8, 64])
        u_sb = f("gu", [64, 8, 64])
        wT = f("gwT", [64, 8, 64])
        qkT = f("gqkT", [64, 8, 64])
        St = f("gS", [64, 64])
        vn = [f(f"gvn{i}", [64, 64]) for i in range(2)]
        As = [f(f"gAs{i}", [64, 64]) for i in range(2)]
        ps = es.enter_context(nc.psum_tensor(pfx + "gps", [64, 8, 512], F32)) if fz is None else fz.ps[0:64, :, :]

        B_ = lambda: Buf()
        bpar, bg = B_(), B_()
        braw = [B_(), B_()]
        bxa = [B_(), B_()]
        bxq, brn, bqn, bkn, bKt, bVt, boseg, bz, bosq, boss = (B_() for _ in range(10))
        brhsD, bED, bEDT, bP, bKbg, bVb, bkdec, bu, bwT, bqkT, bS = (B_() for _ in range(11))
        bL = [B_(), B_()]
        bN = [B_(), B_()]
        bvn = [B_(), B_()]
        bAs = [B_(), B_()]
        bps = [B_() for _ in range(8)]
        wk = [0]

        def nps():
            i = GB + wk[0] % 3
            wk[0] += 1
            return i

        sc.dma("sync", lambda e, u=u: e.dma_start(out=gcw[:], in_=gcw_d[u]), writes=[bpar], key=f"gu{u}" + "gp")
        sc.dma("sync", lambda e, u=u: e.dma_start(out=par[:], in_=gpar_d[u]), writes=[bpar], key=f"gu{u}" + "gp")
        sc.dma("sync", lambda e, u=u: e.dma_start(out=ngb[:], in_=gng_d[u]), writes=[bpar], key=f"gu{u}" + "gp")
        if fz is None:
            sc.dma("sync", lambda e, u=u: e.dma_start(out=a_t[:], in_=ga_d[u]), writes=[bpar], key=f"gu{u}" + "gp")
            sc.dma("sync", lambda e, u=u: e.dma_start(out=b_t[:], in_=gb_d[u]), writes=[bpar], key=f"gu{u}" + "gp")
        else:
            for rr in range(4):
                for (dst, ro) in ((a_t, 3200), (b_t, 3206)):
                    def absrc(e, u=u, rr=rr, ro=ro):
                        return fz.LAB[u, (0 if ro == 3200 else 1):(1 if ro == 3200 else 2), rr * TOK:(rr + 1) * TOK].rearrange(
                            "o (n s) -> s (o n)", s=64)
                    sc.dma("gpsimd", lambda e, dst=dst, rr=rr, absrc=absrc: e.dma_start(
                        out=dst[:, rr * 32:(rr + 1) * 32], in_=absrc(e), allow_slow_non_contiguous=True),
                        reads=[fz.bUr], writes=[bpar], key=f"gu{u}" + "gp")
        for k in range(12):
            sc.op("gpsimd", lambda e, k=k: e.tensor_scalar(out=dgw[:, k, :], in0=I64, scalar1=gcw[:, k:k + 1], scalar2=None,
                                                           op0=ALU.mult), reads=[bcst, bpar], writes=[bpar])
        sc.op("scalar", lambda e: e.activation(out=negA[:], in_=par[:, 0:1], func=AF.Exp), reads=[bpar], writes=[bg])
        sc.op("vector", lambda e: e.tensor_scalar(out=negA[:], in0=negA[:], scalar1=-1.0, scalar2=None, op0=ALU.mult),
              reads=[bg], writes=[bg])
        sc.op("scalar", lambda e: e.activation(out=g_t[:], in_=a_t[:], func=AF.Exp, bias=par[:, 1:2], scale=1.0),
              reads=[bpar, bg], writes=[bg])
        sc.op("scalar", lambda e: e.activation(out=g_t[:], in_=g_t[:], func=AF.Ln, bias=1.0, scale=1.0), reads=[bg], writes=[bg])
        sc.op("vector", lambda e: e.tensor_scalar(out=g_t[:], in0=g_t[:], scalar1=negA[:, 0:1], scalar2=None, op0=ALU.mult),
              reads=[bg], writes=[bg])
        sc.op("scalar", lambda e: e.activation(out=beta[:], in_=b_t[:], func=AF.Sigmoid), reads=[bpar, bg], writes=[bg])
        sc.op("tensor", lambda e: e.matmul(ps[:, GB, 0:NCH], lhsT=Tri, rhs=g_t[:], start=True, stop=True),
              reads=[bcst, bg], writes=[bps[GB]])
        sc.op("tensor", lambda e: e.matmul(ps[:, GB, NCH:2 * NCH], lhsT=ones64[:], rhs=g_t[:], start=True, stop=True),
              reads=[bcst, bg], writes=[bps[GB]])
        sc.op("vector", lambda e: e.tensor_copy(out=gc[:], in_=ps[:, GB, 0:NCH]), reads=[bps[GB], bg], writes=[bg])
        sc.op("vector", lambda e: e.tensor_copy(out=eglb[:], in_=ps[:, GB, NCH:2 * NCH]), reads=[bps[GB], bg], writes=[bg])
        sc.op("vector", lambda e: e.tensor_tensor(out=edec[:], in0=eglb[:], in1=gc[:], op=ALU.subtract), reads=[bg], writes=[bg])
        sc.op("scalar", lambda e: e.activation(out=egc[:], in_=gc[:], func=AF.Exp), reads=[bg], writes=[bg])
        sc.op("scalar", lambda e: e.activation(out=eglb[:], in_=eglb[:], func=AF.Exp), reads=[bg], writes=[bg])
        sc.op("scalar", lambda e: e.activation(out=edec[:], in_=edec[:], func=AF.Exp), reads=[bg], writes=[bg])
        sc.op("vector", lambda e: e.tensor_tensor(out=bgk[:], in0=beta[:], in1=egc[:], op=ALU.mult), reads=[bg], writes=[bg])
        sc.op("vector", lambda e: e.memset(St[:], 0.0), writes=[bS])
        if STOP <= 1:
            return [bS]

        for seg in range(GSEG):
            for tt in range(SEGT // 512):
                c0 = seg * SEGT + tt * 512
                for j in range(3):
                    ri = (tt * 3 + j) % 2
                    if fz is None:
                        sc.dma("gpsimd", lambda e, u=u, j=j, ri=ri, c0=c0: e.dma_start(out=raw[ri][:], in_=raw_d[u, j, :, c0:c0 + 515]),
                               writes=[braw[ri]], key=f"gu{u}" + f"graw{ri}")
                    else:
                        rr, t0 = c0 // TOK, c0 % TOK

                        def rsrc(e, rr_, ta, tb, u=u, j=j):
                            return fz.LG[u, j, :, rr_ * TOK + ta:rr_ * TOK + tb]
                        sc.dma("gpsimd", lambda e, ri=ri, rr=rr, t0=t0, rsrc=rsrc: e.dma_start(out=raw[ri][:, 3:515],
                                                                                            in_=rsrc(e, rr, t0, t0 + 512)),
                               reads=[fz.bUr], writes=[braw[ri]], key=f"gu{u}" + f"graw{ri}")
                        if t0 >= 3:
                            sc.dma("gpsimd", lambda e, ri=ri, rr=rr, t0=t0, rsrc=rsrc: e.dma_start(out=raw[ri][:, 0:3],
                                                                                                in_=rsrc(e, rr, t0 - 3, t0)),
                                   reads=[fz.bUr], writes=[braw[ri]], key=f"gu{u}" + f"graw{ri}")
                        elif rr > 0:
                            sc.dma("gpsimd", lambda e, ri=ri, rr=rr, rsrc=rsrc: e.dma_start(out=raw[ri][:, 0:3],
                                                                                         in_=rsrc(e, rr - 1, TOK - 3, TOK)),
                                   reads=[fz.bUr], writes=[braw[ri]], key=f"gu{u}" + f"graw{ri}")
                        else:
                            sc.op("vector", lambda e, ri=ri: e.memset(raw[ri][:, 0:3], 0.0), writes=[braw[ri]])
                    p1 = nps()
                    for k in range(4):
                        sc.op("tensor", lambda e, j=j, k=k, ri=ri, p1=p1: e.matmul(ps[:, p1, :], lhsT=dgw[:, j * 4 + k, :],
                                                                                 rhs=raw[ri][:, k:k + 512], start=(k == 0), stop=(k == 3)),
                              reads=[bpar, braw[ri]], writes=[bps[p1]], inc=(k == 3))
                    xi = j % 2
                    sc.op("scalar", lambda e, xi=xi, p1=p1: e.activation(out=xa[xi][:], in_=ps[:, p1, :], func=AF.Silu),
                          reads=[bps[p1]], writes=[bxa[xi]])
                    if j < 2:
                        sc.op("scalar", lambda e, xi=xi: e.activation(out=xq[:], in_=xa[xi][:], func=AF.Square),
                              reads=[bxa[xi]], writes=[bxq])
                        p2 = nps()
                        sc.op("tensor", lambda e, p2=p2: e.matmul(ps[:, p2, :], lhsT=ones64[:], rhs=xq[:], start=True, stop=True),
                              reads=[bcst, bxq], writes=[bps[p2]])
                        sc.op("scalar", lambda e, p2=p2: e.activation(out=rn[:], in_=ps[:, p2, :], func=AF.Sqrt, bias=epsg[:, 0:1],
                                                                      scale=1.0), reads=[bps[p2], bcst], writes=[brn])
                        sc.op("vector", lambda e: e.reciprocal(out=rn[:], in_=rn[:]), reads=[brn], writes=[brn])
                        dst, bd = (qnT, bqn) if j == 0 else (knT, bkn)
                        scl = 0.125 if j == 0 else 1.0
                        sc.op("vector", lambda e, xi=xi, dst=dst, tt=tt, scl=scl: e.scalar_tensor_tensor(
                            out=dst[:, tt * 512:(tt + 1) * 512], in0=xa[xi][:], scalar=scl, in1=rn[:], op0=ALU.mult, op1=ALU.mult),
                            reads=[bxa[xi], brn], writes=[bd])
                    if j >= 1:
                        srcT = knT[:, tt * 512:(tt + 1) * 512] if j == 1 else xa[xi][:]
                        bsrc = bkn if j == 1 else bxa[xi]
                        p3 = nps()
                        for cj in range(8):
                            sc.op("tensor", lambda e, srcT=srcT, cj=cj, p3=p3: e.transpose(ps[:, p3, cj * 64:(cj + 1) * 64],
                                                                                         srcT[:, cj * 64:(cj + 1) * 64], I64),
                                  reads=[bsrc, bcst], writes=[bps[p3]], inc=(cj == 7))
                        dstT, bdt = (Kt, bKt) if j == 1 else (Vt, bVt)
                        sc.op("vector", lambda e, dstT=dstT, tt=tt, p3=p3: e.tensor_copy(
                            out=dstT[:, tt * 8:(tt + 1) * 8, :], in_=ps[:, p3, :].rearrange("p (a b) -> p a b", b=64)),
                            reads=[bps[p3]], writes=[bdt])
            if fz is None:
                sc.dma("sync", lambda e, u=u, seg=seg: e.dma_start(
                    out=zseg[:], in_=gz_d[u, seg * SEGT:(seg + 1) * SEGT, :].rearrange("(n s) d -> s n d", s=64)),
                    writes=[bz], key=f"gu{u}" + "gz")
            else:
                def zsrc(e, u=u, seg=seg):
                    return fz.LZ[u, :, seg * SEGT:(seg + 1) * SEGT]
                sc.dma("gpsimd", lambda e, zsrc=zsrc: e.dma_start(out=zseg[:].rearrange("p a b -> p (a b)"), in_=zsrc(e)),
                       reads=[fz.bUr], writes=[bz], key=f"gu{u}" + "gz")
            if STOP <= 2:
                return [bz, bKt, bVt, bqn]
            for gi in range(SEGC // 8):
                l0 = gi * 8
                n0 = seg * SEGC + l0
                v3 = lambda t: t[:]
                pk, pd, pdt = nps(), nps(), nps()
                for j in range(8):
                    cs = slice((l0 + j) * 64, (l0 + j + 1) * 64)
                    sc.op("tensor", lambda e, j=j, cs=cs, pk=pk: e.matmul(ps[:, pk, j * 64:(j + 1) * 64], lhsT=knT[:, cs], rhs=knT[:, cs],
                                                                         start=True, stop=True), reads=[bkn], writes=[bps[pk]], inc=(j == 7))
                sc.op("vector", lambda e, n0=n0: e.tensor_tensor(out=rhsD[:], in0=MB[:], in1=bc_n(g_t, n0), op=ALU.mult),
                      reads=[bcst, bg], writes=[brhsD])
                if STOP <= 2.1:
                    return [brhsD, bps[pk]]
                sc.op("tensor", lambda e, pd=pd: e.matmul(ps[:, pd, :], lhsT=Tri, rhs=rhsD[:].rearrange("p a b -> p (a b)"),
                                                          start=True, stop=True), reads=[bcst, brhsD], writes=[bps[pd]])
                for j in range(8):
                    sc.op("tensor", lambda e, j=j, pdt=pdt: e.matmul(ps[:, pdt, j * 64:(j + 1) * 64], lhsT=rhsD[:, j, :], rhs=Tri,
                                                                    start=True, stop=True), reads=[bcst, brhsD], writes=[bps[pdt]], inc=(j == 7))
                r3 = lambda ap: ap.rearrange("p (a b) -> p a b", b=64)
                sc.op("scalar", lambda e, pd=pd: e.activation(out=ED[:], in_=r3(ps[:, pd, :]), func=AF.Exp), reads=[bps[pd]], writes=[bED])
                sc.op("scalar", lambda e, pdt=pdt: e.activation(out=EDT[:], in_=r3(ps[:, pdt, :]), func=AF.Exp), reads=[bps[pdt]], writes=[bEDT])
                if STOP <= 2.2:
                    return [bED, bEDT]
                sc.op("vector", lambda e, pk=pk: e.tensor_tensor(out=Lp[0][:], in0=r3(ps[:, pk, :]), in1=ED[:], op=ALU.mult),
                      reads=[bps[pk], bED], writes=[bL[0]])
                sc.op("vector", lambda e, n0=n0: e.tensor_tensor(out=Lp[0][:], in0=Lp[0][:], in1=bc_n(beta, n0), op=ALU.mult),
                      reads=[bL[0], bg], writes=[bL[0]])
                sc.op("vector", lambda e: e.tensor_tensor(out=Lp[0][:], in0=Lp[0][:], in1=MB[:], op=ALU.mult),
                      reads=[bL[0], bcst], writes=[bL[0]])
                if STOP <= 2.3:
                    return [bL[0]]
                pn = nps()
                for j in range(8):
                    sc.op("tensor", lambda e, j=j, pn=pn: e.matmul(ps[:, pn, j * 64:(j + 1) * 64], lhsT=Lp[0][:, j, :], rhs=I64,
                                                                  start=True, stop=True),
                          reads=[bL[0], bcst], writes=[bps[pn]], inc=(j == 7))
                sc.op("scalar", lambda e, pn=pn: e.copy(out=Np[0][:], in_=r3(ps[:, pn, :])), reads=[bps[pn]], writes=[bN[0]])
                sc.op("vector", lambda e: e.tensor_tensor(out=Pm[:], in0=IB[:], in1=Np[0][:], op=ALU.subtract),
                      reads=[bN[0], bcst], writes=[bP])
                if STOP <= 2.4:
                    return [bP, bN[0]]
                sc.op("gpsimd", lambda e: e.tensor_copy(out=Lb[0][:], in_=Lp[0][:]), reads=[bL[0]], writes=[bLb[0]])
                sc.op("gpsimd", lambda e: e.tensor_copy(out=Nb[0][:], in_=Np[0][:]), reads=[bN[0]], writes=[bNb[0]])
                sc.op("gpsimd", lambda e: e.tensor_copy(out=Pb[:], in_=Pm[:]), reads=[bP], writes=[bPb])
                cur = 0
                for lvl in range(5):
                    nxt = 1 - cur
                    pl = nps()
                    for j in range(8):
                        sc.op("tensor", lambda e, j=j, pl=pl, cur=cur: e.matmul(ps[:, pl, j * 64:(j + 1) * 64], lhsT=Nb[cur][:, j, :],
                                                                               rhs=Lb[cur][:, j, :], start=True, stop=True),
                              reads=[bNb[cur], bLb[cur]], writes=[bps[pl]], inc=(j == 7))
                    if lvl < 4:
                        pn2 = nps()
                        for j in range(8):
                            sc.op("tensor", lambda e, j=j, pn2=pn2, cur=cur: e.matmul(ps[:, pn2, j * 64:(j + 1) * 64], lhsT=Lb[cur][:, j, :],
                                                                                     rhs=Nb[cur][:, j, :], start=True, stop=True),
                                  reads=[bNb[cur], bLb[cur]], writes=[bps[pn2]], inc=(j == 7))
                    sc.op("scalar", lambda e, pl=pl, nxt=nxt: e.copy(out=Lb[nxt][:], in_=r3(ps[:, pl, :])), reads=[bps[pl]], writes=[bLb[nxt]])
                    if lvl < 4:
                        sc.op("vector", lambda e, pn2=pn2, nxt=nxt: e.tensor_copy(out=Nb[nxt][:], in_=r3(ps[:, pn2, :])),
                              reads=[bps[pn2]], writes=[bNb[nxt]])
                    pu = nps()
                    for j in range(8):
                        sc.op("tensor", lambda e, j=j, pu=pu, nxt=nxt: e.matmul(ps[:, pu, j * 64:(j + 1) * 64], lhsT=Lb[nxt][:, j, :],
                                                                               rhs=Pb[:, j, :], start=True, stop=True),
                              reads=[bLb[nxt], bPb], writes=[bps[pu]], inc=(j == 7))
                    sc.op("vector", lambda e, pu=pu: e.tensor_tensor(out=Pm[:], in0=Pm[:], in1=r3(ps[:, pu, :]), op=ALU.add),
                          reads=[bps[pu], bP], writes=[bP])
                    if lvl < 4:
                        sc.op("gpsimd", lambda e: e.tensor_copy(out=Pb[:], in_=Pm[:]), reads=[bP], writes=[bPb])
                    cur = nxt
                if STOP <= 2.5:
                    return [bP]
                sc.op("vector", lambda e, l0=l0, n0=n0: e.tensor_tensor(out=Kbg[:], in0=Kt[:, l0:l0 + 8, :], in1=bc_n(bgk, n0), op=ALU.mult),
                      reads=[bKt, bg], writes=[bKbg])
                sc.op("vector", lambda e, l0=l0, n0=n0: e.tensor_tensor(out=Vb[:], in0=Vt[:, l0:l0 + 8, :], in1=bc_n(beta, n0), op=ALU.mult),
                      reads=[bVt, bg], writes=[bVb])
                sc.op("vector", lambda e, l0=l0, n0=n0: e.tensor_tensor(out=kdec[:], in0=Kt[:, l0:l0 + 8, :], in1=bc_n(edec, n0), op=ALU.mult),
                      reads=[bKt, bg], writes=[bkdec])
                p_u, p_w, p_q = nps(), nps(), nps()
                for j in range(8):
                    sc.op("tensor", lambda e, j=j, p_u=p_u: e.matmul(ps[:, p_u, j * 64:(j + 1) * 64], lhsT=Pm[:, j, :], rhs=Vb[:, j, :],
                                                                    start=True, stop=True), reads=[bP, bVb], writes=[bps[p_u]], inc=(j == 7))
                for j in range(8):
                    sc.op("tensor", lambda e, j=j, p_w=p_w: e.matmul(ps[:, p_w, j * 64:(j + 1) * 64], lhsT=Kbg[:, j, :], rhs=Pm[:, j, :],
                                                                    start=True, stop=True), reads=[bP, bKbg], writes=[bps[p_w]], inc=(j == 7))
                for j in range(8):
                    cs = slice((l0 + j) * 64, (l0 + j + 1) * 64)
                    sc.op("tensor", lambda e, j=j, cs=cs, p_q=p_q: e.matmul(ps[:, p_q, j * 64:(j + 1) * 64], lhsT=knT[:, cs], rhs=qnT[:, cs],
                                                                           start=True, stop=True), reads=[bkn, bqn], writes=[bps[p_q]], inc=(j == 7))
                sc.op("scalar", lambda e, p_u=p_u: e.copy(out=u_sb[:], in_=r3(ps[:, p_u, :])), reads=[bps[p_u]], writes=[bu])
                sc.op("scalar", lambda e, p_w=p_w: e.copy(out=wT[:], in_=r3(ps[:, p_w, :])), reads=[bps[p_w]], writes=[bwT])
                sc.op("vector", lambda e, p_q=p_q: e.tensor_tensor(out=qkT[:], in0=r3(ps[:, p_q, :]), in1=EDT[:], op=ALU.mult),
                      reads=[bps[p_q], bEDT], writes=[bqkT])
                sc.op("vector", lambda e: e.tensor_tensor(out=qkT[:], in0=qkT[:], in1=TriB[:], op=ALU.mult), reads=[bqkT, bcst], writes=[bqkT])
                if STOP <= 3:
                    return [bqkT, bu, bwT]
                for j in range(8):
                    n = n0 + j
                    l = l0 + j
                    cs = slice(l * 64, (l + 1) * 64)
                    i2 = j % 2
                    bx_, by_ = SB0, SB0
                    sc.op("tensor", lambda e, j=j, bx_=bx_: e.matmul(ps[:, bx_, 0:64], lhsT=wT[:, j, :], rhs=St[:], start=True, stop=True),
                          reads=[bwT, bS], writes=[bps[bx_]])
                    sc.op("tensor", lambda e, cs=cs, by_=by_: e.matmul(ps[:, by_, 192:256], lhsT=qnT[:, cs], rhs=St[:], start=True, stop=True),
                          reads=[bqn, bS], writes=[bps[by_]])
                    sc.op("vector", lambda e, j=j, bx_=bx_, i2=i2: e.tensor_tensor(out=vn[i2][:], in0=u_sb[:, j, :], in1=ps[:, bx_, 0:64],
                                                                                 op=ALU.subtract), reads=[bu, bps[bx_]], writes=[bvn[i2]])
                    sc.op("vector", lambda e, by_=by_, i2=i2, n=n: e.tensor_scalar(out=As[i2][:], in0=ps[:, by_, 192:256], scalar1=egc[:, n:n + 1],
                                                                                  scalar2=None, op0=ALU.mult), reads=[bps[by_], bg], writes=[bAs[i2]])
                    sc.op("tensor", lambda e, j=j, bx_=bx_, i2=i2: e.matmul(ps[:, bx_, 64:128], lhsT=qkT[:, j, :], rhs=vn[i2][:],
                                                                          start=True, stop=True), reads=[bqkT, bvn[i2]], writes=[bps[bx_]], inc=False)
                    sc.op("tensor", lambda e, j=j, bx_=bx_, i2=i2: e.matmul(ps[:, bx_, 128:192], lhsT=kdec[:, j, :], rhs=vn[i2][:],
                                                                          start=True, stop=True), reads=[bkdec, bvn[i2]], writes=[bps[bx_]])
                    sc.op("vector", lambda e, bx_=bx_, n=n: e.scalar_tensor_tensor(out=St[:], in0=St[:], scalar=eglb[:, n:n + 1],
                                                                                  in1=ps[:, bx_, 128:192], op0=ALU.mult, op1=ALU.add),
                          reads=[bS, bg, bps[bx_]], writes=[bS])
                    sc.op("vector", lambda e, bx_=bx_, i2=i2, l=l: e.tensor_tensor(out=oseg[:, l, :], in0=As[i2][:], in1=ps[:, bx_, 64:128],
                                                                                 op=ALU.add), reads=[bAs[i2], bps[bx_]], writes=[boseg])
                if STOP <= 4:
                    return [boseg, bS]
            sc.op("gpsimd", lambda e: e.tensor_tensor(out=osq[:], in0=oseg[:], in1=oseg[:], op=ALU.mult), reads=[boseg], writes=[bosq])
            sc.op("vector", lambda e: e.tensor_reduce(out=oss[:], in_=osq[:], axis=AX.X, op=ALU.add), reads=[bosq], writes=[boss])
            sc.op("scalar", lambda e: e.activation(out=oss[:], in_=oss[:], func=AF.Sqrt, bias=epsg[:, 0:1], scale=1.0 / 64),
                  reads=[boss, bcst], writes=[boss])
            sc.op("vector", lambda e: e.reciprocal(out=oss[:], in_=oss[:]), reads=[boss], writes=[boss])
            sc.op("vector", lambda e: e.tensor_tensor(out=osq[:], in0=oseg[:], in1=oss[:].unsqueeze(2).to_broadcast([64, SEGC, 64]),
                                                      op=ALU.mult), reads=[boseg, boss], writes=[bosq])
            sc.op("gpsimd", lambda e: e.tensor_tensor(out=osq[:], in0=osq[:], in1=ngb[:].unsqueeze(1).to_broadcast([64, SEGC, 64]),
                                                      op=ALU.mult), reads=[bosq, bpar], writes=[bosq])
            sc.op("scalar", lambda e: e.activation(out=zseg[:], in_=zseg[:], func=AF.Silu), reads=[bz], writes=[bz])
            if fz is None:
                sc.op("vector", lambda e: e.tensor_tensor(out=osq[:], in0=osq[:], in1=zseg[:], op=ALU.mult), reads=[bosq, bz], writes=[bosq])
                sc.dma("sync", lambda e, u=u, seg=seg: e.dma_start(
                    out=go_d[u, seg * SEGT:(seg + 1) * SEGT, :].rearrange("(n s) d -> s n d", s=64), in_=osq[:]),
                    reads=[bosq], key=f"gu{u}" + "go")
            else:
                oT = oseg[:].rearrange("p a b -> p (a b)")
                zT = zseg[:].rearrange("p a b -> p (a b)")
                for g4 in range(SEGC // 8):
                    pt_ = nps()
                    for j in range(8):
                        sc.op("tensor", lambda e, j=j, g4=g4, pt_=pt_: e.transpose(ps[:, pt_, j * 64:(j + 1) * 64], osq[:, g4 * 8 + j, :], I64),
                              reads=[bosq, bcst], writes=[bps[pt_]], inc=(j == 7))
                    sc.op("vector", lambda e, g4=g4, pt_=pt_: e.tensor_tensor(out=oT[:, g4 * 512:(g4 + 1) * 512], in0=ps[:, pt_, :],
                                                                            in1=zT[:, g4 * 512:(g4 + 1) * 512], op=ALU.mult),
                          reads=[bps[pt_], bz, bosq], writes=[boseg])
                tok0 = seg * SEGT
                kblk, coff = tok0 // 1024, tok0 % 1024
                row0 = (kblk // 2) * 512 + 192 + (u * 2 + kblk % 2) * 64
                sc.dma("sync", lambda e, row0=row0, coff=coff: e.dma_start(out=fz.Ysend[row0:row0 + 64, coff:coff + SEGT],
                                                                          in_=oT[:, 0:SEGT]),
                       reads=[boseg, fz.bYs], key=f"gu{u}" + "go")
        return [bosq, boseg]

    class _Rec:
        def __init__(self):
            self.calls = []

        def op(self, *a, **k):
            self.calls.append(("op", a, k))

        def dma(self, *a, **k):
            self.calls.append(("dma", a, k))

    outs = []
    recs = []
    for u in range(nu):
        r = _Rec()
        outs += emit_unit(u, r)
        recs.append(r.calls)
    n = max(len(c) for c in recs)
    for i in range(n):
        for c in recs:
            if i < len(c):
                kind, a, k = c[i]
                getattr(sc, kind)(*a, **k)
    return outs


def build_gdn(nu=2):
    P = SimpleProg()
    sc = Sched(P.nc, P.es)
    ob = gdn_emit(P, sc, nu)
    return P, P.finish(sc, ob)


def gdn_const_inputs():
    i = np.arange(64)
    tri = (i[:, None] <= i[None, :]).astype(np.float32)
    ms = (i[:, None] > i[None, :]).astype(np.float32)
    return dict(gcst=np.stack([tri, ms, np.eye(64, dtype=np.float32)]))


def gdn_unit_inputs(ug, h, gdn_conv_w, a_log, dt_bias, norm_g):
    GW = 384
    raw = np.zeros((3, 64, S + 3), np.float32)
    cw = np.zeros((64, 12), np.float32)
    for j in range(3):
        cols = slice(j * GW + h * 64, j * GW + (h + 1) * 64)
        raw[j, :, 3:] = ug[:, cols].T
        cw[:, j * 4:(j + 1) * 4] = gdn_conv_w[:, cols].T
    z = np.ascontiguousarray(ug[:, 3 * GW + h * 64:3 * GW + (h + 1) * 64])
    a = np.ascontiguousarray(ug[:, 4 * GW + h].reshape(NCH, 64).T)
    b = np.ascontiguousarray(ug[:, 4 * GW + 6 + h].reshape(NCH, 64).T)
    par = np.zeros((64, 2), np.float32)
    par[:, 0] = a_log[h]
    par[:, 1] = dt_bias[h]
    ng = np.ascontiguousarray(np.broadcast_to(norm_g[None, :], (64, 64))).astype(np.float32)
    return dict(graw=raw, gcw=cw, gz=z, ga=a, gb=b, gpar=par, gng=ng)


def _lay(v):
    return np.ascontiguousarray(np.asarray(v, np.float32).reshape(-1, 128).T)


_PROGS = {}


def _prog(key, builder):
    if key not in _PROGS:
        _PROGS[key] = builder()
    return _PROGS[key]


def _run(nc, in_maps):
    res = run_bass_kernel_spmd(nc, in_maps, core_ids=list(range(NCORES)))
    return res.results


def _tok_launch(key, stages, inp, xT_list, yT_list=None):
    def mk():
        p = TokProg(stages)
        return p, p.build()
    P, nc = _prog(key, mk)
    maps = []
    for c in range(NCORES):
        b = c // 4
        m = {"xT": xT_list[c], "cT": _lay(inp["c"][b])}
        for name in P.in_names:
            if name in m:
                continue
            if name.startswith("yT"):
                m[name] = yT_list[c]
            elif name == "final_g":
                m[name] = _lay(inp["final_g"])
            else:
                base, l = name[:-1], int(name[-1])
                arr = np.asarray(inp[base][l], np.float32)
                if base == "b_ada" or base.startswith("ln_"):
                    arr = _lay(arr)
                m[name] = np.ascontiguousarray(arr)
        maps.append(m)
    return _run(nc, maps)


def _mixer(inp, l, u):
    y = np.zeros((B, S, D), np.float32)
    P, nc = _prog("conv", build_conv)
    maps = []
    for c in range(NCORES):
        b, j = c // 4, c % 4
        maps.append(conv_inputs(u[b], j, np.asarray(inp["conv_w"][l]), np.asarray(inp["conv_b"][l]),
                                np.asarray(inp["conv_ln_g"][l]), np.asarray(inp["conv_ln_b"][l])))
    res = _run(nc, maps)
    for c in range(NCORES):
        b, j = c // 4, c % 4
        y[b, j * TOK:(j + 1) * TOK, 0:256] = res[c]["ycT"].T
    P, nc = _prog("moba", lambda: build_moba(3))
    tab = rope_tables()
    shared = moba_shared_inputs(tab)
    consts = [moba_const_inputs(0), moba_const_inputs(1)]
    maps = []
    for c in range(NCORES):
        b, cc = c // 4, c % 4
        units = []
        for s in range(3):
            combo = 3 * cc + s
            h, half = combo // 2, combo % 2
            q = u[b, :, 512 + h * 64:512 + (h + 1) * 64]
            k = u[b, :, 512 + 384 + h * 64:512 + 384 + (h + 1) * 64]
            v = u[b, :, 512 + 768 + h * 64:512 + 768 + (h + 1) * 64]
            d = moba_unit_inputs(q, k, v, half, tab)
            d.update(consts[half])
            units.append(d)
        m = {k_: np.ascontiguousarray(np.stack([un[k_] for un in units])) for k_ in units[0]}
        m.update(shared)
        maps.append(m)
    res = _run(nc, maps)
    for c in range(NCORES):
        b, cc = c // 4, c % 4
        for s in range(3):
            combo = 3 * cc + s
            h, half = combo // 2, combo % 2
            qpos = np.concatenate([np.arange(bl * 256, (bl + 1) * 256) for bl in HALF_BLOCKS[half]])
            y[b, qpos, 256 + h * 64:256 + (h + 1) * 64] = res[c]["moT"][s].T
    P, nc = _prog("gdn", lambda: build_gdn(2))
    gconst = gdn_const_inputs()
    allu = [(b, h) for b in range(B) for h in range(6)]
    maps = []
    assign = []
    for c in range(NCORES):
        us = [allu[i] if i < len(allu) else allu[0] for i in (2 * c, 2 * c + 1)]
        assign.append([(i < len(allu)) for i in (2 * c, 2 * c + 1)])
        units = [gdn_unit_inputs(u[b, :, 512 + 1152:], h, np.asarray(inp["gdn_conv_w"][l]), np.asarray(inp["gdn_a_log"][l]),
                                 np.asarray(inp["gdn_dt_bias"][l]), np.asarray(inp["gdn_norm_g"][l])) for (b, h) in us]
        m = {k_: np.ascontiguousarray(np.stack([un[k_] for un in units])) for k_ in units[0]}
        m.update(gconst)
        maps.append(m)
    res = _run(nc, maps)
    for c in range(NCORES):
        for s in range(2):
            i = 2 * c + s
            if i < len(allu):
                b, h = allu[i]
                y[b, :, 640 + h * 64:640 + (h + 1) * 64] = res[c]["go"][s]
    return y


YROWS = 768 + 1024
RG = [[0, 1, 2, 3], [4, 5, 6, 7]]


def moba_unit(cc, su):
    return (cc, su) if su < 2 else (4 + cc // 2, cc % 2)


def moba_owner(h, half):
    return (h, half) if h < 4 else (2 * (h - 4) + half, 2)


class Fused:
    def __init__(self):
        self.nc = bass.Bass("TRN2", target_bir_lowering=False)
        self.es = ExitStack()
        self.cur = self.es
        self.dins = {}
        self.in_names = []
        self.out_names = []
        self.phase_i = 0
        self.load_x = False
        self.store_x = False
        self._dyn = {}

    def din(self, name, shape, dt=F32):
        if name not in self.dins:
            self.in_names.append(name)
            self.dins[name] = self.nc.dram_tensor(name, list(shape), dt, kind="ExternalInput").ap()
        return self.dins[name]

    def dout(self, name, shape, dt=F32):
        if name not in self.dins:
            self.out_names.append(name)
            self.dins[name] = self.nc.dram_tensor(name, list(shape), dt, kind="ExternalOutput").ap()
        return self.dins[name]

    AW = 36800

    def sb(self, name, shape, dt):
        p = shape[0]
        n = int(np.prod(shape[1:]))
        n32 = n if dt == F32 else (n + 1) // 2
        n32 = (n32 + 7) // 8 * 8
        off = self.aoff
        self.aoff += n32
        assert self.aoff <= self.AW, (name, self.aoff)
        v = self.arena[0:p, off:off + n32]
        if dt != F32:
            v = v.bitcast(dt)
        v = v[:, 0:n]
        if len(shape) == 3:
            v = v.rearrange("p (a b) -> p a b", a=shape[1])
        return v

    def dyn(self, e, engname, key):
        c = self._dyn.setdefault(engname, {})
        if "cc" not in c:
            c["cc"] = e.snap(e.partition_id() % 4)
        if key not in c:
            cc = c["cc"]
            doff = lambda h: (h // 2) * 512 + (h % 2) * 64
            v = {"c2048": lambda: cc * 2048, "prev": lambda: (cc + 3) % 4,
                 "D0": lambda: doff(cc), "D2": lambda: (cc // 2) * 64 + 1024, "mha2": lambda: cc % 2, "mhb2": lambda: 3 - cc % 2,
                 "gh1": lambda: (cc + 4) % 6, "Dg1": lambda: doff((cc + 4) % 6)}[key]()
            c[key] = e.snap(v)
        return c[key]

    def build(self):
        nc, es = self.nc, self.es
        sc = self.sc = Sched(nc, es)
        self.x = es.enter_context(nc.sbuf_tensor("x_res", [128, KC, TOK], F32))
        self.arena = es.enter_context(nc.sbuf_tensor("arena", [128, self.AW], F32))
        self.aoff = 0
        self.bx = [[Buf(f"x{c}_{t}") for t in range(TOK // 512)] for c in range(KC)]
        self.ps = es.enter_context(nc.psum_tensor("ps_all", [128, 8, 512], F32))
        NUC = (DIN + 127) // 128
        Usend_t = nc.dram_tensor("Usend", [NUC * 128, TOK], F32)
        Urecv_t = nc.dram_tensor("Urecv", [NUC * 512 + 128, TOK], F32)
        Ysend_t = nc.dram_tensor("Ysend", [2048, 1024], F32)
        Yrecv_t = nc.dram_tensor("Yrecv", [8192, 1024], F32)
        Yfull_t = nc.dram_tensor("Yfull", [D, TOK], F32)
        self.Usend, self.Urecv, self.Ysend, self.Yrecv, self.Yfull = (t.ap() for t in (Usend_t, Urecv_t, Ysend_t, Yrecv_t, Yfull_t))
        self.LK = nc.dram_tensor("LK", [3, 64, S], F32).ap()
        self.LV = nc.dram_tensor("LV", [3, 64, S], F32).ap()
        self.LQ = nc.dram_tensor("LQ", [3, 64, MOBA_SLOTS * 256], F32).ap()
        self.LQF = nc.dram_tensor("LQF", [3, 64, S], F32).ap()
        self.LG = nc.dram_tensor("LG", [2, 3, 64, S], F32).ap()
        self.LZ = nc.dram_tensor("LZ", [2, 64, S], F32).ap()
        self.LAB = nc.dram_tensor("LAB", [2, 2, S], F32).ap()
        self.LH = nc.dram_tensor("LH", [512, CH], F32).ap()
        self.Yloc = nc.dram_tensor("Yloc", [4, 512, 1024], F32).ap()
        self.uT_dst = self.Usend
        self.yT_src = self.Yfull
        self.bU, self.bUr, self.bYs, self.bYr, self.bY = Buf("U"), Buf("Ur"), Buf("Ys"), Buf("Yr"), Buf("Yf")
        self.bL, self.bLq, self.bYl = Buf("L"), Buf("Lq"), Buf("Yl")
        self.bUr_m, self.bUr_g = Buf("Ur_m"), Buf("Ur_g")
        outb = []

        def run_phase(fn):
            self.aoff = 0
            r = fn()
            sc.barrier(exclude=("agUm", "agUg"))
            self.phase_i += 1
            return r

        def tok_phase(stages, load_x=False, store_x=False, ag=True):
            def fn():
                self.load_x, self.store_x = load_x, store_x
                r = TokProg(stages, fused=self).build()
                if ag:
                    order = list(range(4, 13)) + list(range(13, NUC)) + list(range(0, 4))
                    waits = sc._deps("gpsimd", (), [self.bU, self.bUr_m, self.bUr_g])
                    sc.q["gpsimd"].append((waits, None, None))
                    for ci in order:
                        key = "agUm" if 4 <= ci < 13 else "agUg"
                        sc._get_dsem(key)
                        sc.cckeys.add(key)
                        sc.dcnt[key] += 1
                        sc.q["gpsimd"].append(([], (lambda e, ci=ci: e.collective_compute(
                            "AllGather", ALU.bypass, replica_groups=RG,
                            ins=[Usend_t.ap()[ci * 128:(ci + 1) * 128, :]], outs=[Urecv_t.ap()[ci * 512:(ci + 1) * 512, :]])),
                            ("c", key, sc.dcnt[key])))
                    for b, key in ((self.bUr_m, "agUm"), (self.bUr_g, "agUg"), (self.bU, "agUg")):
                        b.lw = ("c", key, sc.dcnt[key])
                        b.rd = {}
                return r
            return run_phase(fn)

        def y_exchange():
            for ci in range(8):
                sc.cc(lambda e, ci=ci: e.collective_compute(
                    "AllGather", ALU.bypass, replica_groups=RG,
                    ins=[Ysend_t.ap()[ci * 256:(ci + 1) * 256, :]], outs=[Yrecv_t.ap()[ci * 1024:(ci + 1) * 1024, :]]),
                    writes=[self.bYs, self.bYr], key="agY")
            LB = ([0, 3, 4, 7], [1, 2, 5, 6])
            for c2 in range(2):
                sc.dma("scalar", lambda e, c2=c2: e.dma_start(
                    out=self.Yloc[:, c2 * 256:(c2 + 1) * 256, :],
                    in_=self.Yrecv[c2 * 1024:c2 * 1024 + 7168, :][bass.ds(self.dyn(e, "scalar", "c2048"), 1024), :].rearrange(
                        "(r f) t -> r f t", r=4)),
                    reads=[self.bYr], writes=[self.bYl], key="yloc")
            for h in range(6):
                for half in range(2):
                    rs, su = moba_owner(h, half)
                    for q4 in range(4):
                        lb = LB[half][q4]
                        sc.dma("sync", lambda e, h=h, lb=lb, rs=rs, su=su, q4=q4: e.dma_start(
                            out=self.Yfull[256 + h * 64:256 + (h + 1) * 64, lb * 256:(lb + 1) * 256],
                            in_=self.Yloc[rs, su * 64:(su + 1) * 64, q4 * 256:(q4 + 1) * 256]),
                            reads=[self.bYl, self.bY], key="yasm")
            for h in range(6):
                rs, g = (h, 0) if h < 4 else (h - 4, 1)
                for kk in range(2):
                    r0 = 192 + (g * 2 + kk) * 64
                    sc.dma("sync", lambda e, h=h, kk=kk, rs=rs, r0=r0: e.dma_start(
                        out=self.Yfull[640 + h * 64:640 + (h + 1) * 64, kk * 1024:(kk + 1) * 1024],
                        in_=self.Yloc[rs, r0:r0 + 64, :]),
                        reads=[self.bYl, self.bY], key="yasm")

        def localize_m():
            Ur = self.Urecv
            rk = lambda ap: ap.rearrange("d (r t) -> d r t", r=4)
            LQF = self.LQF

            def blk(e, q, dkey, B, n=64):
                R0 = (B // 128) * 512 + B % 128
                R1 = min(R0 + 2048, NUC * 512 + 128)
                return Ur[R0:R1, :][bass.ds(self.dyn(e, q, dkey), 512), :].rearrange("(r f) t -> f r t", r=4)[0:n]

            for u in range(3):
                q = "sync" if u < 2 else "scalar"
                dk = "D0" if u < 2 else "D2"
                for (dst, B) in ((self.LK, 896), (self.LV, 1280), (LQF, 512)):
                    sc.dma(q, lambda e, u=u, q=q, dk=dk, dst=dst, B=B: e.dma_start(out=rk(dst[u]), in_=blk(e, q, dk, B)),
                           reads=[self.bUr_m], writes=[self.bL], key=f"loc{q}{u}")
                for ab in range(2):
                    dstq = self.LQ[u].rearrange("d (G ab i) -> d G ab i", G=8, ab=2)[:, :, ab:ab + 1, :]
                    srcv = LQF[u].rearrange("d (G b i) -> d G b i", G=8, b=4)
                    if u < 2:
                        b = u if ab == 0 else 3 - u
                        sc.dma(q, lambda e, dstq=dstq, srcv=srcv, b=b: e.dma_start(out=dstq, in_=srcv[:, :, b:b + 1, :]),
                               reads=[self.bL], writes=[self.bLq], key=f"locq{u}")
                    else:
                        kn = "mha2" if ab == 0 else "mhb2"
                        sc.dma(q, lambda e, dstq=dstq, srcv=srcv, kn=kn: e.dma_start(
                            out=dstq, in_=srcv[:, :, bass.ds(self.dyn(e, "scalar", kn), 1), :]),
                            reads=[self.bL], writes=[self.bLq], key=f"locq{u}")

        def localize_g():
            Ur = self.Urecv
            rk = lambda ap: ap.rearrange("d (r t) -> d r t", r=4)
            LQF = self.LQF

            def blk(e, q, dkey, B, n=64):
                R0 = (B // 128) * 512 + B % 128
                R1 = min(R0 + 2048, NUC * 512 + 128)
                return Ur[R0:R1, :][bass.ds(self.dyn(e, q, dkey), 512), :].rearrange("(r f) t -> f r t", r=4)[0:n]

            for u in range(2):
                dk, gk = ("D0", "cc") if u == 0 else ("Dg1", "gh1")
                for j in range(3):
                    sc.dma("gpsimd", lambda e, u=u, j=j, dk=dk: e.dma_start(
                        out=rk(self.LG[u, j]), in_=blk(e, "gpsimd", dk, 1664 + j * 384)),
                        reads=[self.bUr_g], writes=[self.bL], key="locg")
                q2 = "gpsimd" if u == 0 else "sync"
                sc.dma(q2, lambda e, u=u, dk=dk, q2=q2: e.dma_start(out=rk(self.LZ[u]), in_=blk(e, q2, dk, 2816)),
                       reads=[self.bUr_g], writes=[self.bL], key=f"locz{u}")
                for ab, B in ((0, 3200), (1, 3206)):
                    sc.dma(q2, lambda e, u=u, ab=ab, B=B, gk=gk, q2=q2: e.dma_start(
                        out=self.LAB[u, ab:ab + 1].rearrange("o (r t) -> o r t", r=4), in_=blk(e, q2, gk, B, 1)),
                        reads=[self.bUr_g], writes=[self.bL], key=f"locz{u}")
            sc.dma("scalar", lambda e: e.dma_start(
                out=self.LH.rearrange("(c f) t -> c f t", c=4),
                in_=Ur[0:2048, TOK - CH:TOK].rearrange("(c r f) t -> r c f t", r=4, f=128)[bass.ds(self.dyn(e, "scalar", "prev"), 1)]),
                reads=[self.bUr_g], writes=[self.bL], key="loch")

        def mixer(l):
            pfx = f"L{l}_"
            run_phase(localize_m)
            run_phase(lambda: moba_emit(self, sc, 3, pfx, fz=self))
            run_phase(localize_g)
            run_phase(lambda: conv_emit(self, sc, pfx, fz=self))

            def g():
                gdn_emit(self, sc, 2, pfx, fz=self)
                y_exchange()
            run_phase(g)

        zpad = self.din("zpad", [NUC * 128 - DIN, TOK])
        sc.dma("sync", lambda e: e.dma_start(out=self.Usend[DIN:NUC * 128, :], in_=zpad[:, :]), reads=[self.bU], key="zpad")
        tok_phase([("ffn1", 0), ("uproj", 0)], load_x=True)
        mixer(0)
        tok_phase([("wout", 0), ("ffn2", 0), ("ffn1", 1), ("uproj", 1)])
        mixer(1)
        outb = tok_phase([("wout", 1), ("ffn2", 1), ("final",)], store_x=True, ag=False)
        with nc.Block() as block:
            sc.emit(block)
        es.close()
        return nc


_FUSED = {}


def kernel(**inp):
    if "p" not in _FUSED:
        F = Fused()
        _FUSED["p"] = (F, F.build())
    F, nc = _FUSED["p"]
    x = np.asarray(inp["x"], np.float32)
    tab = rope_tables()
    shared = moba_shared_inputs(tab)
    mconst = [moba_const_inputs(0), moba_const_inputs(1)]
    gconst = gdn_const_inputs()
    wnames = ("w_ada", "ffn1_w_gate", "ffn1_w_up", "ffn1_w_down", "w_in", "w_out", "ffn2_w_gate", "ffn2_w_up", "ffn2_w_down")
    lnames = ("b_ada", "ln_ffn1_g", "ln_mix_g", "ln_ffn2_g")
    common = {}
    for l in range(2):
        for n in wnames:
            common[f"{n}{l}"] = np.ascontiguousarray(np.asarray(inp[n][l], np.float32))
        for n in lnames:
            common[f"{n}{l}"] = _lay(inp[n][l])
        cw = np.asarray(inp["conv_w"][l], np.float32)
        lay2 = lambda v: np.ascontiguousarray(np.asarray(v, np.float32).reshape(2, 128).T)
        common[f"L{l}_cw"] = np.ascontiguousarray(cw.T.reshape(2, 128, 31).transpose(1, 0, 2))
        common[f"L{l}_cp"] = np.ascontiguousarray(np.stack([lay2(inp["conv_b"][l]), lay2(inp["conv_ln_g"][l]),
                                                            lay2(inp["conv_ln_b"][l])], axis=-1))
    common["final_g"] = _lay(inp["final_g"])
    common["cident"] = np.eye(128, dtype=np.float32)
    common["zpad"] = np.zeros((((DIN + 127) // 128) * 128 - DIN, TOK), np.float32)
    common.update(shared)
    common.update(gconst)
    maps = []
    for c in range(NCORES):
        b, cc = c // 4, c % 4
        m = dict(common)
        m["xT"] = np.ascontiguousarray(x[b, cc * TOK:(cc + 1) * TOK].T)
        m["cT"] = _lay(inp["c"][b])
        m["cflag"] = np.full((128, 1), 0.0 if cc == 0 else 1.0, np.float32)
        units = []
        for su in range(3):
            half = moba_unit(cc, su)[1]
            qpos = np.concatenate([np.arange(bl * 256, (bl + 1) * 256) for bl in HALF_BLOCKS[half]])
            d = dict(mconst[half])
            d["ropeq"] = np.ascontiguousarray(tab[:, :, qpos])
            units.append(d)
        for k_ in units[0]:
            m[k_] = np.ascontiguousarray(np.stack([un[k_] for un in units]))
        for l in range(2):
            heads = [cc, (cc + 4) % 6]
            gw = np.asarray(inp["gdn_conv_w"][l], np.float32)
            gcw = np.zeros((2, 64, 12), np.float32)
            gpar = np.zeros((2, 64, 2), np.float32)
            gng = np.zeros((2, 64, 64), np.float32)
            for g, h in enumerate(heads):
                for j in range(3):
                    gcw[g, :, j * 4:(j + 1) * 4] = gw[:, j * 384 + h * 64:j * 384 + (h + 1) * 64].T
                gpar[g, :, 0] = np.asarray(inp["gdn_a_log"][l], np.float32)[h]
                gpar[g, :, 1] = np.asarray(inp["gdn_dt_bias"][l], np.float32)[h]
                gng[g] = np.asarray(inp["gdn_norm_g"][l], np.float32)[None, :]
            m[f"L{l}_gcw"], m[f"L{l}_gpar"], m[f"L{l}_gng"] = gcw, gpar, gng
        maps.append({k_: m[k_] for k_ in F.in_names})
    res = run_bass_kernel_spmd(nc, maps, core_ids=list(range(NCORES))).results
    out = np.zeros((B, S, D), np.float32)
    for c in range(NCORES):
        out[c // 4, (c % 4) * TOK:(c % 4 + 1) * TOK] = res[c]["xoT"].T
    return out
```

```python
import numpy as np
from contextlib import ExitStack
import concourse.bass as bass
import concourse.mybir as mybir
from concourse.bass_utils import run_bass_kernel_spmd

F32 = mybir.dt.float32
BF16 = mybir.dt.bfloat16
AF = mybir.ActivationFunctionType
ALU = mybir.AluOpType

D = 1024
KC = 8
DFF = 2816
FC = 22
DIN = 3212
B = 2
S = 8192
NCORES = 8
TOK = 2048
EPS = 1e-6

SAME_ENG_SYNC = True


class Buf:
    __slots__ = ("name", "lw", "rd")

    def __init__(self, name=""):
        self.name = name
        self.lw = None
        self.rd = {}


class Sched:
    ENGS = ("tensor", "vector", "scalar", "gpsimd", "sync")
    EPOCH = 20000

    def __init__(self, nc, es):
        self.nc = nc
        self.es = es
        self.q = {e: [] for e in self.ENGS}
        self.cnt = {e: 0 for e in self.ENGS}
        self.seen = {e: {} for e in self.ENGS}
        self.esem = {}
        self.dsem = {}
        self.dcnt = {}
        self.cckeys = set()

    def _get_esem(self, eng, epoch):
        k = (eng, epoch)
        if k not in self.esem:
            self.esem[k] = self.es.enter_context(self.nc.semaphore(f"se_{eng}_{epoch}"))
        return self.esem[k]

    def _get_dsem(self, key):
        if key not in self.dsem:
            self.dsem[key] = self.es.enter_context(self.nc.semaphore(f"sd_{key}"))
            self.dcnt[key] = 0
        return self.dsem[key]

    def _need(self, eng, tok, waits):
        if tok is None:
            return
        kind, k, val = tok
        if kind == "e":
            if k == eng and (eng == "tensor" or not SAME_ENG_SYNC):
                return
        key = (kind, k)
        if self.seen[eng].get(key, 0) >= val:
            return
        self.seen[eng][key] = val
        waits.append(tok)

    def _deps(self, eng, reads, writes):
        waits = []
        for b in reads:
            self._need(eng, b.lw, waits)
        for b in writes:
            self._need(eng, b.lw, waits)
            for k, v in b.rd.items():
                self._need(eng, (k[0], k[1], v), waits)
        return waits

    def _mark(self, tok, reads, writes):
        key = (tok[0], tok[1])
        for b in reads:
            if b.rd.get(key, 0) < tok[2]:
                b.rd[key] = tok[2]
        for b in writes:
            b.lw = tok
            b.rd = {}

    def op(self, eng, fn, reads=(), writes=(), inc=True):
        waits = self._deps(eng, reads, writes)
        idx = self.cnt[eng] + 1
        if inc:
            self.cnt[eng] = idx
        tok = ("e", eng, idx)
        self._mark(tok, reads, writes)
        self.q[eng].append((waits, fn, tok if inc else None))

    def dma(self, qeng, fn, reads=(), writes=(), key="d"):
        waits = self._deps(qeng, reads, writes)
        self._get_dsem(key)
        self.dcnt[key] += 1
        tok = ("d", key, 16 * self.dcnt[key])
        self._mark(tok, reads, writes)
        self.q[qeng].append((waits, fn, tok))

    def cc(self, fn, reads=(), writes=(), key="cc"):
        waits = self._deps("gpsimd", reads, writes)
        self._get_dsem(key)
        self.cckeys.add(key)
        self.dcnt[key] += 1
        tok = ("c", key, self.dcnt[key])
        self._mark(tok, reads, writes)
        self.q["gpsimd"].append((waits, fn, tok))

    def barrier(self, exclude=()):
        for e in self.ENGS:
            waits = []
            for e2 in self.ENGS:
                if e2 != e and self.cnt[e2] > 0:
                    self._need(e, ("e", e2, self.cnt[e2]), waits)
            for key, n in self.dcnt.items():
                if n > 0 and key not in exclude:
                    kind = "c" if key in self.cckeys else "d"
                    self._need(e, (kind, key, n if kind == "c" else 16 * n), waits)
            self.q[e].append((waits, None, None))

    def final_wait(self, eng, toks_bufs):
        waits = self._deps(eng, (), toks_bufs)
        self.q[eng].append((waits, None, None))

    def emit(self, block):
        nc = self.nc

        def run(engname):
            def body(eng):
                for waits, fn, tok in self.q[engname]:
                    for (kind, k, val) in waits:
                        if kind == "e":
                            epoch = (val - 1) // self.EPOCH
                            eng.wait_ge(self._get_esem(k, epoch), val - epoch * self.EPOCH)
                        else:
                            eng.wait_ge(self.dsem[k], val)
                    if fn is None:
                        continue
                    ins = fn(eng)
                    if tok is not None:
                        if tok[0] == "e":
                            epoch = (tok[2] - 1) // self.EPOCH
                            ins.then_inc(self._get_esem(tok[1], epoch), 1)
                        elif tok[0] == "c":
                            ins.then_inc(self.dsem[tok[1]])
                        else:
                            ins.then_inc(self.dsem[tok[1]], 16)
                self.q[engname] = []
            return body

        for e in self.ENGS:
            for ep in range((self.cnt[e] - 1) // self.EPOCH + 1 if self.cnt[e] else 0):
                self._get_esem(e, ep)
        block.tensor(run("tensor"))
        block.vector(run("vector"))
        block.scalar(run("scalar"))
        block.gpsimd(run("gpsimd"))
        block.sync(run("sync"))


class TokProg:
    def __init__(self, stages, tok=TOK, fused=None):
        self.stages = stages
        self.tok = tok
        self.fused = fused
        if fused is None:
            self.nc = bass.Bass("TRN2", target_bir_lowering=False)
            self.es = ExitStack()
        else:
            self.nc = fused.nc
            self.es = fused.es
        self.in_names = []
        self.out_names = []

    def din(self, name, shape, dt=F32):
        if self.fused is not None:
            return self.fused.din(name, shape, dt)
        self.in_names.append(name)
        return self.nc.dram_tensor(name, list(shape), dt, kind="ExternalInput").ap()

    def dout(self, name, shape, dt=F32):
        if self.fused is not None:
            return self.fused.dout(name, shape, dt)
        self.out_names.append(name)
        return self.nc.dram_tensor(name, list(shape), dt, kind="ExternalOutput").ap()

    def sb(self, name, shape, dt):
        if self.fused is not None:
            return self.fused.sb(name, shape, dt)
        return self.es.enter_context(self.nc.sbuf_tensor(name, list(shape), dt))

    def build(self):
        nc, es = self.nc, self.es
        fz = self.fused
        T = self.tok
        NH = T // 1024
        stages = self.stages
        layers = sorted({s[1] for s in stages if len(s) > 1})
        need_v = {}
        for s in stages:
            if s[0] == "ffn1":
                need_v.setdefault(s[1], set()).update([0, 1, 2])
            elif s[0] == "uproj":
                need_v.setdefault(s[1], set()).update([3, 4])
            elif s[0] == "wout":
                need_v.setdefault(s[1], set()).update([5])
            elif s[0] == "ffn2":
                need_v.setdefault(s[1], set()).update([6, 7, 8])

        xT_d = self.din("xT", [D, T]) if (fz is None or fz.load_x) else None
        cT_d = self.din("cT", [128, KC])
        W = {}
        for l in layers:
            W[("w_ada", l)] = self.din(f"w_ada{l}", [D, 9 * D])
            W[("b_ada", l)] = self.din(f"b_ada{l}", [128, 72])
        for s in stages:
            if s[0] in ("ffn1", "ffn2"):
                l = s[1]
                n = s[0]
                W[(n + "_g", l)] = self.din(f"ln_{n}_g{l}", [128, KC])
                W[(n + "_wg", l)] = self.din(f"{n}_w_gate{l}", [D, DFF])
                W[(n + "_wu", l)] = self.din(f"{n}_w_up{l}", [D, DFF])
                W[(n + "_wd", l)] = self.din(f"{n}_w_down{l}", [DFF, D])
            elif s[0] == "uproj":
                l = s[1]
                W[("mix_g", l)] = self.din(f"ln_mix_g{l}", [128, KC])
                W[("w_in", l)] = self.din(f"w_in{l}", [D, DIN])
                W[("uT", l)] = self.dout(f"uT{l}", [DIN, T]) if fz is None else fz.uT_dst
            elif s[0] == "wout":
                l = s[1]
                W[("w_out", l)] = self.din(f"w_out{l}", [D, D])
                W[("yT", l)] = self.din(f"yT{l}", [D, T]) if fz is None else fz.yT_src
            elif s[0] == "final":
                W[("final_g",)] = self.din("final_g", [128, KC])
        xo_d = self.dout("xoT", [D, T]) if (fz is None or fz.store_x) else None

        x = self.sb("x", [128, KC, T], F32) if fz is None else fz.x
        h = self.sb("h", [128, KC, 1024], BF16)
        act = self.sb("act", [128, FC, 1024], BF16)
        wd = self.sb("wd", [128, FC, D], BF16)
        NSLOT = 4
        SLOTW = 256
        wslot = [self.sb(f"ws{i}", [128, KC, SLOTW], BF16) for i in range(NSLOT)]
        tmpA = [self.sb(f"tmpA{i}", [128, 512], F32) for i in range(2)]
        tmpB = [self.sb(f"tmpB{i}", [128, 512], F32) for i in range(2)]
        sqb = [self.sb(f"sq{i}", [128, 512], BF16) for i in range(2)]
        rstd = self.sb("rstd", [128, 512], F32)
        ones = self.sb("ones", [128, 128], BF16)
        cT = self.sb("cT_sb", [128, KC], F32)
        cact = self.sb("cact", [128, KC], BF16)
        bada = {l: self.sb(f"bada{l}", [128, 72], F32) for l in layers}
        mod = {l: self.sb(f"mod{l}", [128, 72], F32) for l in layers}
        gains = {}
        for k in W:
            if k[0] in ("ffn1_g", "ffn2_g", "mix_g", "final_g"):
                gains[k] = self.sb("g_" + "_".join(map(str, k)), [128, KC], F32)
        coefA = {}
        coefG = {}
        ps = es.enter_context(nc.psum_tensor("ps", [128, 8, 512], F32)) if fz is None else fz.ps

        sc = Sched(nc, es) if fz is None else fz.sc
        bx = [[Buf(f"x{c}_{t}") for t in range(T // 512)] for c in range(KC)] if fz is None else fz.bx
        bU = [] if fz is None else [fz.bU]
        bY = [] if fz is None else [fz.bY]
        bh = [Buf(f"h{t}") for t in range(2)]
        bact = [[Buf(f"act{f}_{t}") for t in range(2)] for f in range(FC)]
        WD_PIECES = ((0, 6), (6, 12), (12, 17), (17, 22))
        bwd = [Buf(f"wd{i}") for i in range(4)]
        wd_piece = {}
        for i, (f0, f1) in enumerate(WD_PIECES):
            for f in range(f0, f1):
                wd_piece[f] = i
        bws = [Buf(f"ws{i}") for i in range(NSLOT)]
        btA = [Buf() for _ in range(2)]
        btB = [Buf() for _ in range(2)]
        bsq = [Buf() for _ in range(2)]
        brstd = Buf()
        bones = Buf()
        bps = [Buf(f"ps{i}") for i in range(8)]
        bmisc = Buf("misc")
        bmod = Buf("mod")

        if xT_d is not None:
            xT_v = xT_d.rearrange("(c p) t -> p c t", p=128)
            for c in range(KC):
                sc.dma("sync", lambda e, c=c: e.dma_start(out=x[:, c, :], in_=xT_v[:, c, :]),
                       writes=bx[c], key=f"x{c}")
        sc.dma("sync", lambda e: e.dma_start(out=cT[:], in_=cT_d[:, :]), writes=[bmisc], key="misc")
        for l in layers:
            sc.dma("sync", lambda e, l=l: e.dma_start(out=bada[l][:], in_=W[("b_ada", l)][:, :]),
                   writes=[bmisc], key="misc")
        for k, t in gains.items():
            sc.dma("sync", lambda e, k=k, t=t: e.dma_start(out=t[:], in_=W[k][:, :]), writes=[bmisc], key="misc")
        sc.op("vector", lambda e: e.memset(ones[:], 1.0), writes=[bones])
        sc.op("scalar", lambda e: e.activation(out=cact[:], in_=cT[:], func=AF.Silu), reads=[bmisc], writes=[bmod])

        wslot_i = [0]

        def next_slot():
            i = wslot_i[0] % NSLOT
            wslot_i[0] += 1
            return i

        def load_cols(Wd, c0, ncols, nk=KC):
            i = next_slot()
            src = Wd.rearrange("(k p) n -> p k n", p=128)
            sc.dma("gpsimd", lambda e, i=i: e.dma_start(out=wslot[i][:, 0:nk, 0:ncols], in_=src[:, :, c0:c0 + ncols]),
                   writes=[bws[i]], key=f"ws{i}")
            return i

        mod_ps = ps[:, 7, 0:72]
        for l in layers:
            for v in sorted(need_v[l]):
                for hh in range(4):
                    si = load_cols(W[("w_ada", l)], v * 1024 + hh * 256, 256)
                    for jj in range(2):
                        j = hh * 2 + jj
                        col = v * 8 + j
                        for kc in range(KC):
                            sc.op("tensor",
                                  lambda e, si=si, jj=jj, kc=kc, col=col: e.matmul(
                                      ps[:, 7, col:col + 1], lhsT=wslot[si][:, kc, jj * 128:(jj + 1) * 128],
                                      rhs=cact[:, kc:kc + 1], start=(kc == 0), stop=(kc == KC - 1)),
                                  reads=[bws[si], bmod], writes=[bps[7]], inc=(kc == KC - 1))
            for v in sorted(need_v[l]):
                sc.op("vector", lambda e, l=l, v=v: e.tensor_tensor(out=mod[l][:, v * 8:(v + 1) * 8], in0=ps[:, 7, v * 8:(v + 1) * 8],
                                                                    in1=bada[l][:, v * 8:(v + 1) * 8], op=ALU.add),
                      reads=[bps[7], bmisc], writes=[bmod])
            for (gk, vs, vg, half) in ((("ffn1_g", l), 1, 2, 0.5), (("mix_g", l), 4, None, None),
                                       (("ffn2_g", l), 7, 8, 0.5)):
                if gk in gains:
                    a = self.sb("cA_" + "_".join(map(str, gk)), [128, KC], F32)
                    coefA[gk] = a
                    sc.op("vector", lambda e, a=a, gk=gk, vs=vs, l=l: e.scalar_tensor_tensor(
                        out=a[:], in0=mod[l][:, vs * 8:vs * 8 + 8], scalar=1.0, in1=gains[gk][:],
                        op0=ALU.add, op1=ALU.mult), reads=[bmod, bmisc], writes=[bmod])
                    if vg is not None:
                        g = self.sb("cG_" + "_".join(map(str, gk)), [128, KC], F32)
                        coefG[gk] = g
                        sc.op("vector", lambda e, g=g, vg=vg, l=l: e.tensor_scalar(
                            out=g[:], in0=mod[l][:, vg * 8:vg * 8 + 8], scalar1=0.5, scalar2=None, op0=ALU.mult),
                            reads=[bmod], writes=[bmod])

        psi = [0]

        def next_ps(pool):
            i = pool[psi[0] % len(pool)]
            psi[0] += 1
            return i

        def norm_mod(half, A_ap, sh_ap):
            for tt in range(2):
                t0 = half * 1024 + tt * 512
                ti = t0 // 512
                pb = 6
                for c in range(KC):
                    s = c % 2
                    sc.op("scalar", lambda e, c=c, s=s, t0=t0: e.activation(out=sqb[s][:], in_=x[:, c, t0:t0 + 512],
                                                                            func=AF.Square),
                          reads=[bx[c][ti]], writes=[bsq[s]])
                    sc.op("tensor", lambda e, c=c, s=s: e.matmul(ps[:, pb, :], lhsT=ones[:], rhs=sqb[s][:],
                                                                 start=(c == 0), stop=(c == KC - 1)),
                          reads=[bones, bsq[s]], writes=[bps[pb]])
                sc.op("scalar", lambda e: e.activation(out=tmpA[0][:], in_=ps[:, pb, :], func=AF.Sqrt,
                                                       bias=eps_t[:, 0:1], scale=1.0 / D),
                      reads=[bps[pb], bmisc], writes=[btA[0]])
                sc.op("vector", lambda e: e.reciprocal(out=rstd[:], in_=tmpA[0][:]), reads=[btA[0]], writes=[brstd])
                for c in range(KC):
                    s = c % 2
                    sc.op("vector", lambda e, c=c, s=s, t0=t0: e.scalar_tensor_tensor(
                        out=tmpB[s][:], in0=x[:, c, t0:t0 + 512], scalar=A_ap[:, c:c + 1], in1=rstd[:],
                        op0=ALU.mult, op1=ALU.mult), reads=[bx[c][ti], brstd, bmod], writes=[btB[s]])
                    if sh_ap is not None:
                        sc.op("scalar", lambda e, c=c, s=s, tt=tt: e.activation(
                            out=h[:, c, tt * 512:(tt + 1) * 512], in_=tmpB[s][:], func=AF.Identity,
                            bias=sh_ap[:, c:c + 1], scale=1.0), reads=[btB[s], bmod], writes=[bh[tt]])

        def ffn(half, n, l):
            A = coefA[(n + "_g", l)]
            G = coefG[(n + "_g", l)]
            vsh = 0 if n == "ffn1" else 6
            sh = mod[l][:, vsh * 8:vsh * 8 + 8]
            norm_mod(half, A, sh)
            wdv = W[(n + "_wd", l)].rearrange("(f p) n -> p f n", p=128)
            for i, (f0, f1) in enumerate(WD_PIECES):
                sc.dma("gpsimd", lambda e, f0=f0, f1=f1: e.dma_start(out=wd[:, f0:f1, :], in_=wdv[:, f0:f1, :]),
                       writes=[bwd[i]], key=f"wd{i}")
            groups = [(g * 2, 2) for g in range(11)]
            loaded = {}

            def load_group(gi):
                f0, nf = groups[gi]
                loaded[gi] = (load_cols(W[(n + "_wg", l)], f0 * 128, nf * 128),
                              load_cols(W[(n + "_wu", l)], f0 * 128, nf * 128))
            load_group(0)
            for gi, (f0, nf) in enumerate(groups):
                if gi + 1 < len(groups):
                    load_group(gi + 1)
                sg, su = loaded[gi]
                for fi in range(nf):
                    f = f0 + fi
                    for tt in range(2):
                        pg = next_ps([0, 1])
                        pu = pg + 2
                        for kc in range(KC):
                            sc.op("tensor", lambda e, sg=sg, fi=fi, kc=kc, tt=tt, pg=pg: e.matmul(
                                ps[:, pg, :], lhsT=wslot[sg][:, kc, fi * 128:(fi + 1) * 128],
                                rhs=h[:, kc, tt * 512:(tt + 1) * 512], start=(kc == 0), stop=(kc == KC - 1)),
                                reads=[bws[sg], bh[tt]], writes=[bps[pg]], inc=(kc == KC - 1))
                        for kc in range(KC):
                            sc.op("tensor", lambda e, su=su, fi=fi, kc=kc, tt=tt, pu=pu: e.matmul(
                                ps[:, pu, :], lhsT=wslot[su][:, kc, fi * 128:(fi + 1) * 128],
                                rhs=h[:, kc, tt * 512:(tt + 1) * 512], start=(kc == 0), stop=(kc == KC - 1)),
                                reads=[bws[su], bh[tt]], writes=[bps[pu]], inc=(kc == KC - 1))
                        s = pg
                        sc.op("scalar", lambda e, s=s, pg=pg: e.activation(out=tmpA[s][:], in_=ps[:, pg, :],
                                                                           func=AF.Silu),
                              reads=[bps[pg]], writes=[btA[s]])
                        sc.op("vector", lambda e, s=s, pu=pu, f=f, tt=tt: e.tensor_tensor(
                            out=act[:, f, tt * 512:(tt + 1) * 512], in0=tmpA[s][:], in1=ps[:, pu, :], op=ALU.mult),
                            reads=[btA[s], bps[pu]], writes=[bact[f][tt]])
            for tt in range(2):
                t0 = half * 1024 + tt * 512
                ti = t0 // 512
                for d in range(KC):
                    pd = next_ps([4, 5])
                    for f in range(FC):
                        sc.op("tensor", lambda e, f=f, d=d, tt=tt, pd=pd: e.matmul(
                            ps[:, pd, :], lhsT=wd[:, f, d * 128:(d + 1) * 128], rhs=act[:, f, tt * 512:(tt + 1) * 512],
                            start=(f == 0), stop=(f == FC - 1)),
                            reads=[bwd[wd_piece[f]], bact[f][tt]], writes=[bps[pd]], inc=(f == FC - 1))
                    sc.op("vector", lambda e, d=d, t0=t0, pd=pd: e.scalar_tensor_tensor(
                        out=x[:, d, t0:t0 + 512], in0=ps[:, pd, :], scalar=G[:, d:d + 1], in1=x[:, d, t0:t0 + 512],
                        op0=ALU.mult, op1=ALU.add), reads=[bps[pd], bx[d][ti], bmod], writes=[bx[d][ti]])

        ostage = [self.sb(f"ost{i}", [128, 512], F32) for i in range(2)]
        bost = [Buf() for _ in range(2)]
        osi = [0]

        def uproj(half, l):
            A = coefA[("mix_g", l)]
            sh = mod[l][:, 3 * 8:3 * 8 + 8]
            norm_mod(half, A, sh)
            uT = W[("uT", l)]
            ngr = (DIN + 255) // 256
            loaded = {}

            def load_group(gi):
                c0 = gi * 256
                loaded[gi] = load_cols(W[("w_in", l)], c0, min(256, DIN - c0))
            load_group(0)
            for gi in range(ngr):
                if gi + 1 < ngr:
                    load_group(gi + 1)
                si = loaded[gi]
                c0 = gi * 256
                ncol = min(256, DIN - c0)
                for fi in range((ncol + 127) // 128):
                    m = min(128, ncol - fi * 128)
                    for tt in range(2):
                        t0 = half * 1024 + tt * 512
                        pg = next_ps([0, 1, 2, 3])
                        for kc in range(KC):
                            sc.op("tensor", lambda e, si=si, fi=fi, kc=kc, tt=tt, pg=pg, m=m: e.matmul(
                                ps[0:m, pg, :], lhsT=wslot[si][:, kc, fi * 128:fi * 128 + m],
                                rhs=h[:, kc, tt * 512:(tt + 1) * 512], start=(kc == 0), stop=(kc == KC - 1)),
                                reads=[bws[si], bh[tt]], writes=[bps[pg]], inc=(kc == KC - 1))
                        o = osi[0] % 2
                        osi[0] += 1
                        eng = "scalar" if o == 0 else "vector"
                        if eng == "scalar":
                            sc.op("scalar", lambda e, o=o, pg=pg, m=m: e.copy(out=ostage[o][0:m, :], in_=ps[0:m, pg, :]),
                                  reads=[bps[pg]], writes=[bost[o]])
                        else:
                            sc.op("vector", lambda e, o=o, pg=pg, m=m: e.tensor_copy(out=ostage[o][0:m, :],
                                                                                     in_=ps[0:m, pg, :]),
                                  reads=[bps[pg]], writes=[bost[o]])
                        r0 = c0 + fi * 128
                        sc.dma("sync", lambda e, o=o, m=m, r0=r0, t0=t0: e.dma_start(
                            out=uT[r0:r0 + m, t0:t0 + 512], in_=ostage[o][0:m, :]), reads=[bost[o]] + bU, key=f"ost{o}")

        ystage = [act[:, i * 8:(i + 1) * 8, 0:512] for i in range(2)]
        byst = [[bact[f][0] for f in range(i * 8, (i + 1) * 8)] for i in range(2)]

        def wout(half, l):
            yT = W[("yT", l)].rearrange("(c p) t -> p c t", p=128)
            wsl = [load_cols(W[("w_out", l)], q * 256, 256) for q in range(4)]
            G = mod[l][:, 5 * 8:5 * 8 + 8]
            for tt in range(2):
                t0 = half * 1024 + tt * 512
                ti = t0 // 512
                sc.dma("gpsimd", lambda e, tt=tt, t0=t0: e.dma_start(out=ystage[tt], in_=yT[:, :, t0:t0 + 512]),
                       reads=bY, writes=byst[tt], key=f"yst{tt}")
                for d in range(KC):
                    si = wsl[d // 2]
                    dj = d % 2
                    pd = next_ps([4, 5])
                    for kc in range(KC):
                        sc.op("tensor", lambda e, si=si, dj=dj, kc=kc, tt=tt, pd=pd: e.matmul(
                            ps[:, pd, :], lhsT=wslot[si][:, kc, dj * 128:(dj + 1) * 128], rhs=ystage[tt][:, kc, :],
                            start=(kc == 0), stop=(kc == KC - 1)),
                            reads=[bws[si]] + byst[tt], writes=[bps[pd]], inc=(kc == KC - 1))
                    sc.op("vector", lambda e, d=d, t0=t0, pd=pd: e.scalar_tensor_tensor(
                        out=x[:, d, t0:t0 + 512], in0=ps[:, pd, :], scalar=G[:, d:d + 1], in1=x[:, d, t0:t0 + 512],
                        op0=ALU.mult, op1=ALU.add), reads=[bps[pd], bx[d][ti], bmod], writes=[bx[d][ti]])

        def final(half):
            g = gains[("final_g",)]
            for tt in range(2):
                t0 = half * 1024 + tt * 512
                ti = t0 // 512
                pb = 6
                for c in range(KC):
                    s = c % 2
                    sc.op("scalar", lambda e, c=c, s=s, t0=t0: e.activation(out=sqb[s][:], in_=x[:, c, t0:t0 + 512],
                                                                            func=AF.Square),
                          reads=[bx[c][ti]], writes=[bsq[s]])
                    sc.op("tensor", lambda e, c=c, s=s: e.matmul(ps[:, pb, :], lhsT=ones[:], rhs=sqb[s][:],
                                                                 start=(c == 0), stop=(c == KC - 1)),
                          reads=[bones, bsq[s]], writes=[bps[pb]])
                sc.op("scalar", lambda e: e.activation(out=tmpA[0][:], in_=ps[:, pb, :], func=AF.Sqrt,
                                                       bias=eps_t[:, 0:1], scale=1.0 / D),
                      reads=[bps[pb], bmisc], writes=[btA[0]])
                sc.op("vector", lambda e: e.reciprocal(out=rstd[:], in_=tmpA[0][:]), reads=[btA[0]], writes=[brstd])
                for c in range(KC):
                    sc.op("vector", lambda e, c=c, t0=t0: e.scalar_tensor_tensor(
                        out=x[:, c, t0:t0 + 512], in0=x[:, c, t0:t0 + 512], scalar=g[:, c:c + 1], in1=rstd[:],
                        op0=ALU.mult, op1=ALU.mult), reads=[bx[c][ti], brstd, bmisc], writes=[bx[c][ti]])

        eps_t = self.sb("eps_t", [128, 1], F32)
        sc.op("vector", lambda e: e.memset(eps_t[:], EPS), writes=[bmisc])

        for half in range(NH):
            for s in stages:
                if s[0] in ("ffn1", "ffn2"):
                    ffn(half, s[0], s[1])
                elif s[0] == "uproj":
                    uproj(half, s[1])
                elif s[0] == "wout":
                    wout(half, s[1])
                elif s[0] == "final":
                    final(half)

        allb = []
        if xo_d is not None:
            xo_v = xo_d.rearrange("(c p) t -> p c t", p=128)
            for c in range(KC):
                sc.dma("sync", lambda e, c=c: e.dma_start(out=xo_v[:, c, :], in_=x[:, c, :]), reads=bx[c], key="xo")
                allb += bx[c]
        if fz is not None:
            return allb
        sc.final_wait("sync", allb + bost)

        with nc.Block() as block:
            sc.emit(block)
        es.close()
        return nc


NBLK = 32
MOBA_SLOTS = 16
HALF_BLOCKS = ([b for b in range(NBLK) if b % 4 in (0, 3)], [b for b in range(NBLK) if b % 4 in (1, 2)])
NEG = -30000.0


def moba_emit(P, sc, nu, pfx="", fz=None):
    nc, es = P.nc, P.es
    NQ = MOBA_SLOTS * 256
    if fz is None:
        mq = P.din(pfx + "mq", [nu, 64, NQ])
        mqs = P.din(pfx + "mqs", [nu, 16, NQ])
        mk = P.din(pfx + "mk", [nu, 64, S])
        mks = P.din(pfx + "mks", [nu, 16, S])
        mv = P.din(pfx + "mv", [nu, S, 64])
        yo = P.dout(pfx + "moT", [nu, 64, NQ])
    cq = P.din("ropeq", [nu, 2, 16, NQ])
    ck = P.din("ropek", [2, 16, S])
    pm_d = P.din("pm", [nu, 128, MOBA_SLOTS * NBLK])
    oh_d = P.din("oh", [nu, 128, MOBA_SLOTS * NBLK])
    cm_d = P.din("cm", [nu, 2, 4, 128, 256])
    boh_d = P.din("boh", [32, S])
    id_d = P.din("ident", [128, 128])

    qaug = P.sb(pfx + "qaug", [128, NQ], BF16)
    kaug = P.sb(pfx + "kaug", [128, S], BF16)
    vaug = P.sb(pfx + "vaug", [128, 64, 128], BF16)
    qf = P.sb(pfx + "qf", [64, NQ], F32)
    xt = [P.sb(pfx + f"xt{i}", [64, 1024], F32) for i in range(2)]
    xs = [P.sb(pfx + f"xs{i}", [16, 1024], F32) for i in range(2)]
    ct = [P.sb(pfx + f"ct{i}", [16, 2, 1024], F32) for i in range(2)]
    t16 = P.sb(pfx + "t16", [16, 1024], F32)
    sqf = P.sb(pfx + "sqf", [64, 1024], F32)
    kmean = P.sb(pfx + "kmean", [64, NBLK], F32)
    mx = P.sb(pfx + "mx", [128, 4], F32)
    nbias = P.sb(pfx + "nbias", [128, 1], F32)
    onesf = P.sb(pfx + "onesf", [64, 128], F32)
    ident = P.sb(pfx + "ident_sb", [128, 128], F32)
    pm = P.sb(pfx + "pm_sb", [128, MOBA_SLOTS * NBLK], F32)
    oh = P.sb(pfx + "oh_sb", [128, MOBA_SLOTS * NBLK], F32)
    cm = P.sb(pfx + "cm_sb", [128, 8, 256], F32)
    gs = P.sb(pfx + "gs", [128, NBLK], F32)
    g8 = P.sb(pfx + "g8", [128, 8], F32)
    m1 = P.sb(pfx + "m1", [128, NBLK], F32)
    m2 = P.sb(pfx + "m2", [128, NBLK], F32)
    stm = [P.sb(pfx + f"stm{i}", [128, 256], F32) for i in range(2)]
    pt = [P.sb(pfx + f"pt{i}", [128, 512], BF16) for i in range(4)]
    rec = P.sb(pfx + "rec", [64, 256], F32)
    yst = [P.sb(pfx + f"yst{i}", [64, 256], F32) for i in range(2)]
    ps = es.enter_context(nc.psum_tensor(pfx + "mps", [128, 8, 512], F32)) if fz is None else fz.ps
    if fz is not None:
        vt = P.sb(pfx + "vt", [64, 1024], F32)
        bvt = Buf()

    def rows6(e, u, r0, nr, cols):
        return fz.Urecv[r0:r0 + 5 * 64 + nr, cols][bass.ds(fz.dyn(e, "sync", ("mhr", u)), nr), :]

    bq, bk, bv, bqf = Buf(), Buf(), Buf(), Buf()
    bxt = [Buf(), Buf()]
    bxs = [Buf(), Buf()]
    bct = [Buf(), Buf()]
    bt16, bsqf, bkm, bmx, bnb, bconst, bmask = Buf(), Buf(), Buf(), Buf(), Buf(), Buf(), Buf()
    bgs, bg8, bm1, bm2 = Buf(), Buf(), Buf(), Buf()
    bstm = [Buf(), Buf()]
    bpt = [Buf(), Buf(), Buf(), Buf()]
    brec = Buf()
    byst = [Buf(), Buf()]
    bps = [Buf() for _ in range(8)]

    sc.dma("sync", lambda e: e.dma_start(out=ident[:], in_=id_d[:, :]), writes=[bconst], key=pfx + "mconst")
    sc.op("vector", lambda e: e.memset(onesf[:], 1.0), writes=[bconst])
    sc.op("vector", lambda e: e.memset(kaug[32:64, :], 0.0), writes=[bk])
    sc.op("vector", lambda e: e.memset(kaug[32:33, :], 1.0), writes=[bk])
    sc.dma("gpsimd", lambda e: e.dma_start(out=kaug[0:32, :], in_=boh_d[:, :]), writes=[bk], key=pfx + "mk0")
    sc.op("vector", lambda e: e.memset(qaug[32:64, :], 0.0), writes=[bq])
    sc.op("vector", lambda e: e.memset(vaug[:, :, 64:128], 1.0), writes=[bv])

    cnt = [0]
    for u in range(nu):
        sc.dma("sync", lambda e, u=u: e.dma_start(out=pm[:], in_=pm_d[u]), writes=[bmask], key=pfx + "mmask")
        sc.dma("sync", lambda e, u=u: e.dma_start(out=oh[:], in_=oh_d[u]), writes=[bmask], key=pfx + "mmask")
        sc.dma("sync", lambda e, u=u: e.dma_start(out=cm[:], in_=cm_d[u].rearrange("a k p q -> p (a k) q")),
               writes=[bmask], key=pfx + "mmask")
        if fz is None:
            for k0 in range(0, 64, 16):
                sc.dma("gpsimd", lambda e, u=u, k0=k0: e.dma_start(
                    out=vaug[:, k0:k0 + 16, 0:64], in_=mv[u].rearrange("(k p) d -> p k d", p=128)[:, k0:k0 + 16, :]),
                    writes=[bv], key=pfx + "mv")
        else:
            for c0 in range(0, S, 1024):
                rr, t0 = c0 // TOK, c0 % TOK

                def vsrc(e, u=u, c0=c0):
                    return fz.LV[u, :, c0:c0 + 1024]
                sc.dma("sync", lambda e, vsrc=vsrc: e.dma_start(out=vt[:], in_=vsrc(e)), reads=[fz.bUr], writes=[bvt],
                       key=pfx + "mvt")
                for cj in range(8):
                    sc.op("tensor", lambda e, cj=cj: e.transpose(ps[:, 6, cj * 64:(cj + 1) * 64], vt[:, cj * 128:(cj + 1) * 128],
                                                                 ident[0:64, 0:64]),
                          reads=[bvt, bconst], writes=[bps[6]], inc=(cj == 7))
                k0 = c0 // 128
                sc.op("vector", lambda e, k0=k0: e.tensor_copy(out=vaug[:, k0:k0 + 8, 0:64],
                                                               in_=ps[:, 6, :].rearrange("p (a b) -> p a b", b=64)),
                      reads=[bps[6]], writes=[bv])
        sc.op("vector", lambda e: e.memset(mx[:], 0.0), writes=[bmx])
        srcs_ = ((mk, mks, None, S), (mq, mqs, cq, NQ)) if fz is None else ((None, None, None, S), (None, None, cq, NQ))
        for which, (src, srcs, tab, ncols) in enumerate(srcs_):
            for c0 in range(0, ncols, 1024):
                i = cnt[0] % 2
                cnt[0] += 1
                if fz is None:
                    sc.dma("sync", lambda e, i=i, c0=c0, src=src, u=u: e.dma_start(out=xt[i][:], in_=src[u, :, c0:c0 + 1024]),
                           writes=[bxt[i]], key=pfx + f"mxt{i}")
                    sc.dma("sync", lambda e, i=i, c0=c0, srcs=srcs, u=u: e.dma_start(out=xs[i][:], in_=srcs[u, :, c0:c0 + 1024]),
                           writes=[bxs[i]], key=pfx + f"mxs{i}")
                elif which == 0:
                    rr, t0 = c0 // TOK, c0 % TOK

                    def ksrc(e, ro, nr, u=u, c0=c0):
                        return fz.LK[u, ro:ro + nr, c0:c0 + 1024]
                    sc.dma("sync", lambda e, i=i, ksrc=ksrc: e.dma_start(out=xt[i][:], in_=ksrc(e, 0, 64)),
                           reads=[fz.bUr], writes=[bxt[i]], key=pfx + f"mxt{i}")
                    sc.dma("sync", lambda e, i=i, ksrc=ksrc: e.dma_start(out=xs[i][0:8, :], in_=ksrc(e, 8, 8)),
                           reads=[fz.bUr], writes=[bxs[i]], key=pfx + f"mxs{i}")
                    sc.dma("sync", lambda e, i=i, ksrc=ksrc: e.dma_start(out=xs[i][8:16, :], in_=ksrc(e, 0, 8)),
                           reads=[fz.bUr], writes=[bxs[i]], key=pfx + f"mxs{i}")
                else:
                    def qsrc(e, ro, nr, u=u, c0=c0):
                        return fz.LQ[u, ro:ro + nr, c0:c0 + 1024]
                    sc.dma("sync", lambda e, i=i, qsrc=qsrc: e.dma_start(out=xt[i][:], in_=qsrc(e, 0, 64)),
                           reads=[fz.bUr], writes=[bxt[i]], key=pfx + f"mxt{i}")
                    sc.dma("sync", lambda e, i=i, qsrc=qsrc: e.dma_start(out=xs[i][0:8, :], in_=qsrc(e, 8, 8)),
                           reads=[fz.bUr], writes=[bxs[i]], key=pfx + f"mxs{i}")
                    sc.dma("sync", lambda e, i=i, qsrc=qsrc: e.dma_start(out=xs[i][8:16, :], in_=qsrc(e, 0, 8)),
                           reads=[fz.bUr], writes=[bxs[i]], key=pfx + f"mxs{i}")
                if which == 0:
                    sc.dma("sync", lambda e, i=i, c0=c0: e.dma_start(
                        out=ct[i][:], in_=ck[:, :, c0:c0 + 1024].rearrange("a p t -> p a t")),
                        writes=[bct[i]], key=pfx + f"mct{i}")
                else:
                    sc.dma("sync", lambda e, i=i, c0=c0, u=u: e.dma_start(
                        out=ct[i][:], in_=cq[u, :, :, c0:c0 + 1024].rearrange("a p t -> p a t")),
                        writes=[bct[i]], key=pfx + f"mct{i}")
                sc.op("vector", lambda e, i=i: e.tensor_tensor(out=t16[:], in0=xs[i][:], in1=ct[i][:, 1, :], op=ALU.mult),
                      reads=[bxs[i], bct[i]], writes=[bt16])
                sc.op("vector", lambda e, i=i: e.tensor_tensor(out=xt[i][0:16, :], in0=xt[i][0:16, :], in1=ct[i][:, 0, :],
                                                               op=ALU.mult), reads=[bxt[i], bct[i]], writes=[bxt[i]])
                sc.op("vector", lambda e, i=i: e.tensor_tensor(out=xt[i][0:16, :], in0=xt[i][0:16, :], in1=t16[:],
                                                               op=ALU.add), reads=[bxt[i], bt16], writes=[bxt[i]])
                sc.op("scalar", lambda e, i=i: e.activation(out=sqf[:], in_=xt[i][:], func=AF.Square),
                      reads=[bxt[i]], writes=[bsqf])
                for hh in range(2):
                    sc.op("tensor", lambda e, hh=hh: e.matmul(ps[:, 6, :], lhsT=onesf[:], rhs=sqf[:, hh * 512:(hh + 1) * 512],
                                                              start=True, stop=True), reads=[bconst, bsqf], writes=[bps[6]])
                    sc.op("vector", lambda e, which=which: e.tensor_reduce(out=mx[:, 2:3], in_=ps[:, 6, :], axis=mybir.AxisListType.X,
                                                                           op=ALU.max), reads=[bps[6]], writes=[bmx])
                    sc.op("vector", lambda e, which=which: e.tensor_tensor(out=mx[:, which:which + 1], in0=mx[:, which:which + 1],
                                                                           in1=mx[:, 2:3], op=ALU.max), reads=[bmx], writes=[bmx])
                if which == 0:
                    nb0 = c0 // 256
                    sc.op("vector", lambda e, i=i, nb0=nb0: e.tensor_reduce(
                        out=kmean[:, nb0:nb0 + 4], in_=xt[i][:].rearrange("p (n t) -> p n t", t=256),
                        axis=mybir.AxisListType.X, op=ALU.add), reads=[bxt[i]], writes=[bkm])
                    sc.op("scalar", lambda e, i=i, c0=c0: e.copy(out=kaug[64:128, c0:c0 + 1024], in_=xt[i][:]),
                          reads=[bxt[i]], writes=[bk])
                else:
                    sc.op("scalar", lambda e, i=i, c0=c0: e.mul(out=qaug[64:128, c0:c0 + 1024], in_=xt[i][:], mul=0.125),
                          reads=[bxt[i]], writes=[bq])
                    sc.op("vector", lambda e, i=i, c0=c0: e.tensor_copy(out=qf[:, c0:c0 + 1024], in_=xt[i][:]),
                          reads=[bxt[i]], writes=[bqf])
        sc.op("vector", lambda e: e.tensor_tensor(out=mx[:, 3:4], in0=mx[:, 0:1], in1=mx[:, 1:2], op=ALU.mult),
              reads=[bmx], writes=[bmx])
        sc.op("scalar", lambda e: e.activation(out=mx[:, 3:4], in_=mx[:, 3:4], func=AF.Sqrt), reads=[bmx], writes=[bmx])
        sc.op("vector", lambda e: e.tensor_scalar(out=nbias[:], in0=mx[:, 3:4], scalar1=-0.125, scalar2=None, op0=ALU.mult),
              reads=[bmx], writes=[bnb])
        for t in range(NQ // 128):
            r = t // 2
            sc.op("tensor", lambda e, t=t: e.matmul(ps[:, 7, 0:NBLK], lhsT=qf[:, t * 128:(t + 1) * 128], rhs=kmean[:],
                                                    start=True, stop=True), reads=[bqf, bkm], writes=[bps[7]])
            sc.op("vector", lambda e, r=r: e.tensor_tensor(out=gs[:], in0=ps[:, 7, 0:NBLK], in1=pm[:, r * NBLK:(r + 1) * NBLK],
                                                           op=ALU.add), reads=[bps[7], bmask], writes=[bgs])
            sc.op("vector", lambda e: e.max(out=g8[:], in_=gs[:]), reads=[bgs], writes=[bg8])
            sc.op("vector", lambda e: e.tensor_scalar(out=m1[:], in0=gs[:], scalar1=g8[:, 2:3], scalar2=None, op0=ALU.is_ge),
                  reads=[bgs, bg8], writes=[bm1])
            sc.op("vector", lambda e: e.tensor_scalar(out=m2[:], in0=gs[:], scalar1=-1e29, scalar2=None, op0=ALU.is_gt),
                  reads=[bgs], writes=[bm2])
            sc.op("vector", lambda e: e.tensor_tensor(out=m1[:], in0=m1[:], in1=m2[:], op=ALU.mult),
                  reads=[bm1, bm2], writes=[bm1])
            sc.op("vector", lambda e, r=r: e.tensor_tensor(out=m1[:], in0=m1[:], in1=oh[:, r * NBLK:(r + 1) * NBLK], op=ALU.add),
                  reads=[bm1, bmask], writes=[bm1])
            sc.op("vector", lambda e: e.tensor_scalar(out=m2[:], in0=m1[:], scalar1=-1.0, scalar2=-NEG, op0=ALU.add, op1=ALU.mult),
                  reads=[bm1], writes=[bm2])
            sc.op("tensor", lambda e: e.transpose(ps[0:NBLK, 7, 128:256], m2[:], ident[:]),
                  reads=[bm2, bconst], writes=[bps[7]])
            sc.op("vector", lambda e, t=t: e.tensor_copy(out=qaug[0:32, t * 128:(t + 1) * 128], in_=ps[0:NBLK, 7, 128:256]),
                  reads=[bps[7]], writes=[bq])
        tasks = []
        for m in range(MOBA_SLOTS // 2):
            KT0, KT1 = 8 * m + 4, 8 * m + 8
            tasks += [(m, kt, kt < KT0) for kt in range(KT1)]
        NB, DEPTH = 4, 3

        def qk(i):
            m, kt, wide = tasks[i]
            r0 = 2 * m
            KT0, KT1 = 8 * m + 4, 8 * m + 8
            p = i % NB
            c0 = 0 if wide else 256
            sc.op("tensor", lambda e, kt=kt, r0=r0, p=p, c0=c0: e.matmul(
                ps[:, p, c0:512], lhsT=kaug[:, kt * 128:(kt + 1) * 128], rhs=qaug[:, r0 * 256 + c0:r0 * 256 + 512],
                start=True, stop=True), reads=[bk, bq], writes=[bps[p]])
            if wide and kt < KT0 - 4:
                sc.op("scalar", lambda e, p=p: e.activation(out=pt[p][:], in_=ps[:, p, :], func=AF.Exp, bias=nbias[:, 0:1],
                                                            scale=1.0), reads=[bps[p], bnb], writes=[bpt[p]])
                return
            j = (kt - (KT0 - 4)) if wide else (kt - (KT1 - 4) + 4)
            mc = 0 if wide else 256
            s = j % 2
            sc.op("vector", lambda e, p=p, j=j, s=s, mc=mc: e.tensor_tensor(out=stm[s][:], in0=ps[:, p, mc:mc + 256], in1=cm[:, j, :],
                                                                           op=ALU.add), reads=[bps[p], bmask], writes=[bstm[s]])
            sc.op("scalar", lambda e, p=p, s=s, mc=mc: e.activation(out=pt[p][:, mc:mc + 256], in_=stm[s][:], func=AF.Exp,
                                                                    bias=nbias[:, 0:1], scale=1.0),
                  reads=[bstm[s], bnb], writes=[bpt[p]])
            if wide:
                sc.op("scalar", lambda e, p=p: e.activation(out=pt[p][:, 256:512], in_=ps[:, p, 256:512], func=AF.Exp,
                                                            bias=nbias[:, 0:1], scale=1.0), reads=[bps[p], bnb], writes=[bpt[p]])

        def pv(i):
            m, kt, wide = tasks[i]
            KT1 = 8 * m + 8
            p = i % NB
            po = 4 + (m % 2)
            c0 = 0 if wide else 256
            sc.op("tensor", lambda e, kt=kt, p=p, po=po, c0=c0, KT1=KT1: e.matmul(
                ps[:, po, c0:512], lhsT=vaug[:, kt, :], rhs=pt[p][:, c0:512], start=(kt == 0), stop=(kt == KT1 - 1)),
                reads=[bv, bpt[p]], writes=[bps[po]])
            if kt < KT1 - 1:
                return
            for h2 in range(2):
                r = 2 * m + h2
                cs = slice(h2 * 256, (h2 + 1) * 256)
                sc.op("vector", lambda e, po=po, cs=cs: e.reciprocal(out=rec[:], in_=ps[64:128, po, cs]), reads=[bps[po]], writes=[brec])
                ys = r % 2
                sc.op("vector", lambda e, po=po, ys=ys, cs=cs: e.tensor_tensor(out=yst[ys][:], in0=ps[0:64, po, cs], in1=rec[:], op=ALU.mult),
                      reads=[bps[po], brec], writes=[byst[ys]])
                if fz is None:
                    sc.dma("sync", lambda e, u=u, r=r, ys=ys: e.dma_start(out=yo[u, :, r * 256:(r + 1) * 256], in_=yst[ys][:]),
                           reads=[byst[ys]], key=pfx + f"myo{ys}")
                else:
                    row0 = (r // 4) * 512 + u * 64
                    sc.dma("sync", lambda e, row0=row0, r=r, ys=ys: e.dma_start(
                        out=fz.Ysend[row0:row0 + 64, (r % 4) * 256:(r % 4 + 1) * 256], in_=yst[ys][:]),
                        reads=[byst[ys], fz.bYs], key=pfx + f"myo{ys}")

        for i in range(min(DEPTH, len(tasks))):
            qk(i)
        for i in range(len(tasks)):
            if i + DEPTH < len(tasks):
                qk(i + DEPTH)
            pv(i)
    return byst


class SimpleProg:
    def __init__(self):
        self.fused = None
        self.nc = bass.Bass("TRN2", target_bir_lowering=False)
        self.es = ExitStack()
        self.in_names = []
        self.out_names = []

    din = TokProg.din
    dout = TokProg.dout
    sb = TokProg.sb

    def finish(self, sc, outbufs):
        sc.final_wait("sync", outbufs)
        with self.nc.Block() as block:
            sc.emit(block)
        self.es.close()
        return self.nc


def build_moba(nu=3):
    P = SimpleProg()
    sc = Sched(P.nc, P.es)
    ob = moba_emit(P, sc, nu)
    return P, P.finish(sc, ob)


def rope_tables():
    inv = np.exp(np.float32(-np.log(500000.0)) * np.arange(0, 16, 2, dtype=np.float32) / np.float32(16)).astype(np.float32)
    ang = (np.arange(S, dtype=np.float32)[:, None] * inv[None, :]).astype(np.float32)
    cos = np.cos(ang).astype(np.float32).T
    sin = np.sin(ang).astype(np.float32).T
    tab = np.zeros((2, 16, S), np.float32)
    tab[0, 0:8] = cos
    tab[0, 8:16] = cos
    tab[1, 0:8] = -sin
    tab[1, 8:16] = sin
    return tab


def moba_unit_inputs(uq, uk, uv, half, tab):
    blocks = HALF_BLOCKS[half]
    qpos = np.concatenate([np.arange(b * 256, (b + 1) * 256) for b in blocks])
    qT = np.ascontiguousarray(uq[qpos].T)
    kT = np.ascontiguousarray(uk.T)
    sw = np.r_[8:16, 0:8]
    return dict(mq=qT, mqs=np.ascontiguousarray(qT[sw]), mk=kT, mks=np.ascontiguousarray(kT[sw]), mv=np.ascontiguousarray(uv),
                ropeq=np.ascontiguousarray(tab[:, :, qpos]))


def moba_const_inputs(half):
    blocks = HALF_BLOCKS[half]
    pm = np.zeros((MOBA_SLOTS, NBLK), np.float32)
    oh = np.zeros((MOBA_SLOTS, NBLK), np.float32)
    for r, b in enumerate(blocks):
        pm[r, b:] = -1e30
        oh[r, b] = 1.0
    kk = np.arange(128)[:, None]
    qq = np.arange(256)[None, :]
    M0 = np.where(kk <= qq, 0.0, NEG).astype(np.float32)
    M1 = np.where(kk + 128 <= qq, 0.0, NEG).astype(np.float32)
    Z = np.zeros((128, 256), np.float32)
    cm = np.zeros((2, 4, 128, 256), np.float32)
    for par in range(2):
        r = par
        b = blocks[r]
        if b == 2 * r + 1:
            cm[par] = np.stack([Z, Z, M0, M1])
        else:
            cm[par] = np.stack([M0, M1, Z, Z])
    pmb = np.ascontiguousarray(np.broadcast_to(pm.reshape(1, -1), (128, MOBA_SLOTS * NBLK)))
    ohb = np.ascontiguousarray(np.broadcast_to(oh.reshape(1, -1), (128, MOBA_SLOTS * NBLK)))
    return dict(pm=pmb, oh=ohb, cm=cm)


def moba_shared_inputs(tab):
    boh = np.zeros((32, S), np.float32)
    for n in range(32):
        boh[n, n * 256:(n + 1) * 256] = 1.0
    return dict(ropek=tab, boh=boh, ident=np.eye(128, dtype=np.float32))


CH = 32


def conv_emit(P, sc, pfx="", fz=None):
    nc, es = P.nc, P.es
    T = TOK
    if fz is None:
        uc = P.din(pfx + "uc", [512, T + CH])
        yc = P.dout(pfx + "ycT", [256, T])
    else:
        yc = fz.Yfull
        flag_d = P.din("cflag", [128, 1])
        flag = P.sb(pfx + "cflag_sb", [128, 1], F32)
    cw = P.din(pfx + "cw", [128, 2, 31])
    cp = P.din(pfx + "cp", [128, 2, 3])
    idb = P.din("cident", [128, 128])
    a_t = [P.sb(pfx + f"ca{c}", [128, T + CH], F32) for c in range(2)]
    g_t = [P.sb(pfx + f"cg{c}", [128, T + CH], F32) for c in range(2)]
    hg = [P.sb(pfx + f"chg{c}", [128, T + CH], BF16) for c in range(2)]
    dg = [P.sb(pfx + f"cdg{c}", [128, 31, 128], BF16) for c in range(2)]
    cws = P.sb(pfx + "cws", [128, 2, 31], F32)
    cps = P.sb(pfx + "cps", [128, 2, 3], F32)
    idt = P.sb(pfx + "cidt", [128, 128], F32)
    onesf = P.sb(pfx + "cones", [128, 128], F32)
    epsc = P.sb(pfx + "ceps", [128, 1], F32)
    hc = [P.sb(pfx + f"chc{c}", [128, 512], F32) for c in range(2)]
    sq = [P.sb(pfx + f"csq{c}", [128, 512], F32) for c in range(2)]
    mean = P.sb(pfx + "cmean", [128, 512], F32)
    msq = P.sb(pfx + "cmsq", [128, 512], F32)
    var = P.sb(pfx + "cvar", [128, 512], F32)
    rstd = P.sb(pfx + "crstd", [128, 512], F32)
    tt_ = [P.sb(pfx + f"ctt{c}", [128, 512], F32) for c in range(2)]
    yo = [P.sb(pfx + f"cyo{c}", [128, 512], F32) for c in range(2)]
    ps = es.enter_context(nc.psum_tensor(pfx + "cps_", [128, 4, 512], F32)) if fz is None else fz.ps
    ba = [Buf(), Buf()]
    bg = [Buf(), Buf()]
    bhg = [Buf(), Buf()]
    bdg = [Buf(), Buf()]
    bc, bhc, bsq = Buf(), [Buf(), Buf()], [Buf(), Buf()]
    bmean, bmsq, bvar, brstd = Buf(), Buf(), Buf(), Buf()
    btt = [Buf(), Buf()]
    byo = [Buf(), Buf()]
    bps = [Buf() for _ in range(4)]

    sc.dma("sync", lambda e: e.dma_start(out=cws[:], in_=cw[:, :, :]), writes=[bc], key=pfx + "cc")
    sc.dma("sync", lambda e: e.dma_start(out=cps[:], in_=cp[:, :, :]), writes=[bc], key=pfx + "cc")
    sc.dma("sync", lambda e: e.dma_start(out=idt[:], in_=idb[:, :]), writes=[bc], key=pfx + "cc")
    sc.op("vector", lambda e: e.memset(onesf[:], 1.0), writes=[bc])
    sc.op("vector", lambda e: e.memset(epsc[:], EPS), writes=[bc])
    if fz is not None:
        sc.dma("sync", lambda e: e.dma_start(out=flag[:], in_=flag_d[:, :]), writes=[bc], key=pfx + "cc")

    def prev_rows(e, row0):
        return fz.LH[row0:row0 + 128, :]

    for c in range(2):
        if fz is None:
            sc.dma("sync", lambda e, c=c: e.dma_start(out=a_t[c][:], in_=uc[c * 128:(c + 1) * 128, :]), writes=[ba[c]],
                   key=pfx + f"ca{c}")
            sc.dma("sync", lambda e, c=c: e.dma_start(out=g_t[c][:], in_=uc[256 + c * 128:256 + (c + 1) * 128, :]),
                   writes=[bg[c]], key=pfx + f"cg{c}")
        else:
            sc.dma("sync", lambda e, c=c: e.dma_start(out=a_t[c][:, CH:], in_=fz.Usend[c * 128:(c + 1) * 128, :]),
                   reads=[fz.bU], writes=[ba[c]], key=pfx + f"ca{c}")
            sc.dma("sync", lambda e, c=c: e.dma_start(out=a_t[c][:, 0:CH], in_=prev_rows(e, c * 128)),
                   reads=[fz.bUr], writes=[ba[c]], key=pfx + f"ca{c}")
            sc.dma("sync", lambda e, c=c: e.dma_start(out=g_t[c][:, CH:], in_=fz.Usend[256 + c * 128:256 + (c + 1) * 128, :]),
                   reads=[fz.bU], writes=[bg[c]], key=pfx + f"cg{c}")
            sc.dma("sync", lambda e, c=c: e.dma_start(out=g_t[c][:, 0:CH], in_=prev_rows(e, 256 + c * 128)),
                   reads=[fz.bUr], writes=[bg[c]], key=pfx + f"cg{c}")
        sc.op("scalar", lambda e, c=c: e.activation(out=g_t[c][:], in_=g_t[c][:], func=AF.Sigmoid),
              reads=[bg[c]], writes=[bg[c]])
        sc.op("vector", lambda e, c=c: e.tensor_tensor(out=hg[c][:], in0=a_t[c][:], in1=g_t[c][:], op=ALU.mult),
              reads=[ba[c], bg[c]], writes=[bhg[c]])
        if fz is not None:
            sc.op("vector", lambda e, c=c: e.tensor_scalar(out=hg[c][:, 0:CH], in0=hg[c][:, 0:CH], scalar1=flag[:, 0:1],
                                                           scalar2=None, op0=ALU.mult), reads=[bhg[c], bc], writes=[bhg[c]])
        for k in range(31):
            sc.op("gpsimd", lambda e, c=c, k=k: e.tensor_scalar(out=dg[c][:, k, :], in0=idt[:], scalar1=cws[:, c, k:k + 1],
                                                                scalar2=None, op0=ALU.mult), reads=[bc], writes=[bdg[c]])
    for tt in range(T // 512):
        for c in range(2):
            for k in range(31):
                o = tt * 512 + 2 + k
                sc.op("tensor", lambda e, c=c, k=k, o=o: e.matmul(ps[:, c, :], lhsT=dg[c][:, k, :], rhs=hg[c][:, o:o + 512],
                                                                 start=(k == 0), stop=(k == 30)),
                      reads=[bdg[c], bhg[c]], writes=[bps[c]], inc=(k == 30))
            sc.op("scalar", lambda e, c=c: e.activation(out=hc[c][:], in_=ps[:, c, :], func=AF.Identity, bias=cps[:, c, 0:1],
                                                        scale=1.0), reads=[bps[c], bc], writes=[bhc[c]])
            sc.op("scalar", lambda e, c=c: e.activation(out=sq[c][:], in_=hc[c][:], func=AF.Square), reads=[bhc[c]],
                  writes=[bsq[c]])
        for c in range(2):
            sc.op("tensor", lambda e, c=c: e.matmul(ps[:, 2, :], lhsT=onesf[:], rhs=hc[c][:], start=(c == 0), stop=(c == 1)),
                  reads=[bc, bhc[c]], writes=[bps[2]])
        for c in range(2):
            sc.op("tensor", lambda e, c=c: e.matmul(ps[:, 3, :], lhsT=onesf[:], rhs=sq[c][:], start=(c == 0), stop=(c == 1)),
                  reads=[bc, bsq[c]], writes=[bps[3]])
        sc.op("vector", lambda e: e.tensor_scalar(out=mean[:], in0=ps[:, 2, :], scalar1=1.0 / 256, scalar2=None, op0=ALU.mult),
              reads=[bps[2]], writes=[bmean])
        sc.op("vector", lambda e: e.tensor_tensor(out=msq[:], in0=mean[:], in1=mean[:], op=ALU.mult), reads=[bmean], writes=[bmsq])
        sc.op("vector", lambda e: e.scalar_tensor_tensor(out=var[:], in0=ps[:, 3, :], scalar=1.0 / 256, in1=msq[:],
                                                         op0=ALU.mult, op1=ALU.subtract), reads=[bps[3], bmsq], writes=[bvar])
        sc.op("scalar", lambda e: e.activation(out=var[:], in_=var[:], func=AF.Sqrt, bias=epsc[:, 0:1], scale=1.0),
              reads=[bvar, bc], writes=[bvar])
        sc.op("vector", lambda e: e.reciprocal(out=rstd[:], in_=var[:]), reads=[bvar], writes=[brstd])
        for c in range(2):
            sc.op("vector", lambda e, c=c: e.tensor_tensor(out=tt_[c][:], in0=hc[c][:], in1=mean[:], op=ALU.subtract),
                  reads=[bhc[c], bmean], writes=[btt[c]])
            sc.op("vector", lambda e, c=c: e.tensor_tensor(out=tt_[c][:], in0=tt_[c][:], in1=rstd[:], op=ALU.mult),
                  reads=[btt[c], brstd], writes=[btt[c]])
            sc.op("scalar", lambda e, c=c: e.activation(out=yo[c][:], in_=tt_[c][:], func=AF.Silu, bias=cps[:, c, 2:3],
                                                        scale=cps[:, c, 1:2]), reads=[btt[c], bc], writes=[byo[c]])
            sc.dma("sync", lambda e, c=c, tt=tt: e.dma_start(out=yc[c * 128:(c + 1) * 128, tt * 512:(tt + 1) * 512], in_=yo[c][:]),
                   reads=[byo[c]] + ([] if fz is None else [fz.bY]), key=pfx + f"cyo{c}")
    return byo


def build_conv():
    P = SimpleProg()
    sc = Sched(P.nc, P.es)
    ob = conv_emit(P, sc)
    return P, P.finish(sc, ob)


def conv_inputs(u_b, j, conv_w, conv_b, ln_g, ln_b):
    t0 = j * TOK
    uc = np.zeros((512, TOK + CH), np.float32)
    lo = max(0, t0 - CH)
    uc[:, CH - (t0 - lo):] = u_b[lo:t0 + TOK, 0:512].T
    lay = lambda v: np.ascontiguousarray(v.reshape(2, 128).T)
    cw = np.ascontiguousarray(conv_w.T.reshape(2, 128, 31).transpose(1, 0, 2))
    cp = np.ascontiguousarray(np.stack([lay(conv_b), lay(ln_g), lay(ln_b)], axis=-1))
    return dict(uc=uc, cw=cw, cp=cp, cident=np.eye(128, dtype=np.float32))


GC = 64
NCH = S // GC
GSEG = 16
AX = mybir.AxisListType


def gdn_emit(P, sc, nu, pfx="", fz=None):
    import os
    STOP = float(os.environ.get("GDN_STOP", "99"))
    nc, es = P.nc, P.es
    if fz is None:
        raw_d = P.din(pfx + "graw", [nu, 3, 64, S + 3])
        gz_d = P.din(pfx + "gz", [nu, S, 64])
        ga_d = P.din(pfx + "ga", [nu, 64, NCH])
        gb_d = P.din(pfx + "gb", [nu, 64, NCH])
        go_d = P.dout(pfx + "go", [nu, S, 64])
    gcw_d = P.din(pfx + "gcw", [nu, 64, 12])
    gpar_d = P.din(pfx + "gpar", [nu, 64, 2])
    gng_d = P.din(pfx + "gng", [nu, 64, 64])
    gcst_d = P.din("gcst", [3, 64, 64])

    def unit_h(e, u):
        return fz.dyn(e, "gpsimd", ("gh", u))

    f = lambda n, shp: P.sb(pfx + n, shp, F32)
    cst = f("gcst_sb", [64, 3, 64])
    TriB = f("gTriB", [64, 8, 64])
    MB = f("gMB", [64, 8, 64])
    IB = f("gIB", [64, 8, 64])
    ones64 = f("gones", [64, 64])
    epsg = f("geps", [64, 1])
    bcst = Buf()
    sc.dma("sync", lambda e: e.dma_start(out=cst[:], in_=gcst_d.rearrange("a p q -> p a q")), writes=[bcst], key=pfx + "gc")
    sc.op("vector", lambda e: e.memset(ones64[:], 1.0), writes=[bcst])
    sc.op("vector", lambda e: e.memset(epsg[:], EPS), writes=[bcst])
    for j in range(8):
        sc.op("vector", lambda e, j=j: e.tensor_copy(out=TriB[:, j, :], in_=cst[:, 0, :]), reads=[bcst], writes=[bcst])
        sc.op("vector", lambda e, j=j: e.tensor_copy(out=MB[:, j, :], in_=cst[:, 1, :]), reads=[bcst], writes=[bcst])
        sc.op("vector", lambda e, j=j: e.tensor_copy(out=IB[:, j, :], in_=cst[:, 2, :]), reads=[bcst], writes=[bcst])
    Tri = cst[:, 0, :]
    I64 = cst[:, 2, :]

    def bc_n(t, n0):
        return t[:, n0:n0 + 8].unsqueeze(2).to_broadcast([64, 8, 64])


    def emit_unit(u, sc):
        u2 = u % 2
        GB, SB0 = 4 * u2, 4 * u2 + 3
        f = lambda n, shp: P.sb(pfx + f"u{u}_" + n, shp, F32)
        gcw = f("gcw_sb", [64, 12])
        dgw = P.sb(pfx + f"u{u}_" + "gdgw", [64, 12, 64], BF16)
        par = f("gpar_sb", [64, 2])
        negA = f("gnegA", [64, 1])
        ngb = f("gngb", [64, 64])
        a_t = f("ga_sb", [64, NCH])
        b_t = f("gb_sb", [64, NCH])
        g_t = f("gg", [64, NCH])
        beta = f("gbeta", [64, NCH])
        gc = f("ggc", [64, NCH])
        egc = f("gegc", [64, NCH])
        eglb = f("geglb", [64, NCH])
        edec = f("gedec", [64, NCH])
        bgk = f("gbgk", [64, NCH])
        SEGT = S // GSEG
        SEGC = NCH // GSEG
        raw = [P.sb(pfx + f"u{u}_" + f"graw{i}", [64, 515], BF16) for i in range(2)]
        xa = [f(f"gxa{i}", [64, 512]) for i in range(2)]
        xq = f("gxq", [64, 512])
        rn = f("grn", [64, 512])
        qnT = f("gqnT", [64, SEGT])
        knT = f("gknT", [64, SEGT])
        Kt = f("gKt", [64, SEGC, 64])
        Vt = f("gVt", [64, SEGC, 64])
        oseg = f("goseg", [64, SEGC, 64])
        zseg = f("gzseg", [64, SEGC, 64])
        osq = f("gosq", [64, SEGC, 64])
        oss = f("goss", [64, SEGC])
        rhsD = f("grhsD", [64, 8, 64])
        ED = f("gED", [64, 8, 64])
        EDT = f("gEDT", [64, 8, 64])
        Lp = [f(f"gL{i}", [64, 8, 64]) for i in range(2)]
        Np = [f(f"gN{i}", [64, 8, 64]) for i in range(2)]
        Pm = f("gP", [64, 8, 64])
        Lb = [P.sb(pfx + f"u{u}_" + f"gLb{i}", [64, 8, 64], BF16) for i in range(2)]
        Nb = [P.sb(pfx + f"u{u}_" + f"gNb{i}", [64, 8, 64], BF16) for i in range(2)]
        Pb = P.sb(pfx + f"u{u}_" + "gPb", [64, 8, 64], BF16)
        bLb, bNb, bPb = [Buf(), Buf()], [Buf(), Buf()], Buf()
        Kbg = f("gKbg", [64, 8, 64])
        Vb = f("gVb", [64, 8, 64])
        kdec = f("gkdec", [64, 8, 64])
        u_sb = f("gu", [64, 8, 64])
        wT = f("gwT", [64, 8, 64])
        qkT = f("gqkT", [64, 8, 64])
        St = f("gS", [64, 64])
        vn = [f(f"gvn{i}", [64, 64]) for i in range(2)]
        As = [f(f"gAs{i}", [64, 64]) for i in range(2)]
        ps = es.enter_context(nc.psum_tensor(pfx + "gps", [64, 8, 512], F32)) if fz is None else fz.ps[0:64, :, :]

        B_ = lambda: Buf()
        bpar, bg = B_(), B_()
        braw = [B_(), B_()]
        bxa = [B_(), B_()]
        bxq, brn, bqn, bkn, bKt, bVt, boseg, bz, bosq, boss = (B_() for _ in range(10))
        brhsD, bED, bEDT, bP, bKbg, bVb, bkdec, bu, bwT, bqkT, bS = (B_() for _ in range(11))
        bL = [B_(), B_()]
        bN = [B_(), B_()]
        bvn = [B_(), B_()]
        bAs = [B_(), B_()]
        bps = [B_() for _ in range(8)]
        wk = [0]

        def nps():
            i = GB + wk[0] % 3
            wk[0] += 1
            return i

        sc.dma("sync", lambda e, u=u: e.dma_start(out=gcw[:], in_=gcw_d[u]), writes=[bpar], key=f"gu{u}" + "gp")
        sc.dma("sync", lambda e, u=u: e.dma_start(out=par[:], in_=gpar_d[u]), writes=[bpar], key=f"gu{u}" + "gp")
        sc.dma("sync", lambda e, u=u: e.dma_start(out=ngb[:], in_=gng_d[u]), writes=[bpar], key=f"gu{u}" + "gp")
        if fz is None:
            sc.dma("sync", lambda e, u=u: e.dma_start(out=a_t[:], in_=ga_d[u]), writes=[bpar], key=f"gu{u}" + "gp")
            sc.dma("sync", lambda e, u=u: e.dma_start(out=b_t[:], in_=gb_d[u]), writes=[bpar], key=f"gu{u}" + "gp")
        else:
            for rr in range(4):
                for (dst, ro) in ((a_t, 3200), (b_t, 3206)):
                    def absrc(e, u=u, rr=rr, ro=ro):
                        return fz.LAB[u, (0 if ro == 3200 else 1):(1 if ro == 3200 else 2), rr * TOK:(rr + 1) * TOK].rearrange(
                            "o (n s) -> s (o n)", s=64)
                    sc.dma("gpsimd", lambda e, dst=dst, rr=rr, absrc=absrc: e.dma_start(
                        out=dst[:, rr * 32:(rr + 1) * 32], in_=absrc(e), allow_slow_non_contiguous=True),
                        reads=[fz.bUr], writes=[bpar], key=f"gu{u}" + "gp")
        for k in range(12):
            sc.op("gpsimd", lambda e, k=k: e.tensor_scalar(out=dgw[:, k, :], in0=I64, scalar1=gcw[:, k:k + 1], scalar2=None,
                                                           op0=ALU.mult), reads=[bcst, bpar], writes=[bpar])
        sc.op("scalar", lambda e: e.activation(out=negA[:], in_=par[:, 0:1], func=AF.Exp), reads=[bpar], writes=[bg])
        sc.op("vector", lambda e: e.tensor_scalar(out=negA[:], in0=negA[:], scalar1=-1.0, scalar2=None, op0=ALU.mult),
              reads=[bg], writes=[bg])
        sc.op("scalar", lambda e: e.activation(out=g_t[:], in_=a_t[:], func=AF.Exp, bias=par[:, 1:2], scale=1.0),
              reads=[bpar, bg], writes=[bg])
        sc.op("scalar", lambda e: e.activation(out=g_t[:], in_=g_t[:], func=AF.Ln, bias=1.0, scale=1.0), reads=[bg], writes=[bg])
        sc.op("vector", lambda e: e.tensor_scalar(out=g_t[:], in0=g_t[:], scalar1=negA[:, 0:1], scalar2=None, op0=ALU.mult),
              reads=[bg], writes=[bg])
        sc.op("scalar", lambda e: e.activation(out=beta[:], in_=b_t[:], func=AF.Sigmoid), reads=[bpar, bg], writes=[bg])
        sc.op("tensor", lambda e: e.matmul(ps[:, GB, 0:NCH], lhsT=Tri, rhs=g_t[:], start=True, stop=True),
              reads=[bcst, bg], writes=[bps[GB]])
        sc.op("tensor", lambda e: e.matmul(ps[:, GB, NCH:2 * NCH], lhsT=ones64[:], rhs=g_t[:], start=True, stop=True),
              reads=[bcst, bg], writes=[bps[GB]])
        sc.op("vector", lambda e: e.tensor_copy(out=gc[:], in_=ps[:, GB, 0:NCH]), reads=[bps[GB], bg], writes=[bg])
        sc.op("vector", lambda e: e.tensor_copy(out=eglb[:], in_=ps[:, GB, NCH:2 * NCH]), reads=[bps[GB], bg], writes=[bg])
        sc.op("vector", lambda e: e.tensor_tensor(out=edec[:], in0=eglb[:], in1=gc[:], op=ALU.subtract), reads=[bg], writes=[bg])
        sc.op("scalar", lambda e: e.activation(out=egc[:], in_=gc[:], func=AF.Exp), reads=[bg], writes=[bg])
        sc.op("scalar", lambda e: e.activation(out=eglb[:], in_=eglb[:], func=AF.Exp), reads=[bg], writes=[bg])
        sc.op("scalar", lambda e: e.activation(out=edec[:], in_=edec[:], func=AF.Exp), reads=[bg], writes=[bg])
        sc.op("vector", lambda e: e.tensor_tensor(out=bgk[:], in0=beta[:], in1=egc[:], op=ALU.mult), reads=[bg], writes=[bg])
        sc.op("vector", lambda e: e.memset(St[:], 0.0), writes=[bS])
        if STOP <= 1:
            return [bS]

        for seg in range(GSEG):
            for tt in range(SEGT // 512):
                c0 = seg * SEGT + tt * 512
                for j in range(3):
                    ri = (tt * 3 + j) % 2
                    if fz is None:
                        sc.dma("gpsimd", lambda e, u=u, j=j, ri=ri, c0=c0: e.dma_start(out=raw[ri][:], in_=raw_d[u, j, :, c0:c0 + 515]),
                               writes=[braw[ri]], key=f"gu{u}" + f"graw{ri}")
                    else:
                        rr, t0 = c0 // TOK, c0 % TOK

                        def rsrc(e, rr_, ta, tb, u=u, j=j):
                            return fz.LG[u, j, :, rr_ * TOK + ta:rr_ * TOK + tb]
                        sc.dma("gpsimd", lambda e, ri=ri, rr=rr, t0=t0, rsrc=rsrc: e.dma_start(out=raw[ri][:, 3:515],
                                                                                            in_=rsrc(e, rr, t0, t0 + 512)),
                               reads=[fz.bUr], writes=[braw[ri]], key=f"gu{u}" + f"graw{ri}")
                        if t0 >= 3:
                            sc.dma("gpsimd", lambda e, ri=ri, rr=rr, t0=t0, rsrc=rsrc: e.dma_start(out=raw[ri][:, 0:3],
                                                                                                in_=rsrc(e, rr, t0 - 3, t0)),
                                   reads=[fz.bUr], writes=[braw[ri]], key=f"gu{u}" + f"graw{ri}")
                        elif rr > 0:
                            sc.dma("gpsimd", lambda e, ri=ri, rr=rr, rsrc=rsrc: e.dma_start(out=raw[ri][:, 0:3],
                                                                                         in_=rsrc(e, rr - 1, TOK - 3, TOK)),
                                   reads=[fz.bUr], writes=[braw[ri]], key=f"gu{u}" + f"graw{ri}")
                        else:
                            sc.op("vector", lambda e, ri=ri: e.memset(raw[ri][:, 0:3], 0.0), writes=[braw[ri]])
                    p1 = nps()
                    for k in range(4):
                        sc.op("tensor", lambda e, j=j, k=k, ri=ri, p1=p1: e.matmul(ps[:, p1, :], lhsT=dgw[:, j * 4 + k, :],
                                                                                 rhs=raw[ri][:, k:k + 512], start=(k == 0), stop=(k == 3)),
                              reads=[bpar, braw[ri]], writes=[bps[p1]], inc=(k == 3))
                    xi = j % 2
                    sc.op("scalar", lambda e, xi=xi, p1=p1: e.activation(out=xa[xi][:], in_=ps[:, p1, :], func=AF.Silu),
                          reads=[bps[p1]], writes=[bxa[xi]])
                    if j < 2:
                        sc.op("scalar", lambda e, xi=xi: e.activation(out=xq[:], in_=xa[xi][:], func=AF.Square),
                              reads=[bxa[xi]], writes=[bxq])
                        p2 = nps()
                        sc.op("tensor", lambda e, p2=p2: e.matmul(ps[:, p2, :], lhsT=ones64[:], rhs=xq[:], start=True, stop=True),
                              reads=[bcst, bxq], writes=[bps[p2]])
                        sc.op("scalar", lambda e, p2=p2: e.activation(out=rn[:], in_=ps[:, p2, :], func=AF.Sqrt, bias=epsg[:, 0:1],
                                                                      scale=1.0), reads=[bps[p2], bcst], writes=[brn])
                        sc.op("vector", lambda e: e.reciprocal(out=rn[:], in_=rn[:]), reads=[brn], writes=[brn])
                        dst, bd = (qnT, bqn) if j == 0 else (knT, bkn)
                        scl = 0.125 if j == 0 else 1.0
                        sc.op("vector", lambda e, xi=xi, dst=dst, tt=tt, scl=scl: e.scalar_tensor_tensor(
                            out=dst[:, tt * 512:(tt + 1) * 512], in0=xa[xi][:], scalar=scl, in1=rn[:], op0=ALU.mult, op1=ALU.mult),
                            reads=[bxa[xi], brn], writes=[bd])
                    if j >= 1:
                        srcT = knT[:, tt * 512:(tt + 1) * 512] if j == 1 else xa[xi][:]
                        bsrc = bkn if j == 1 else bxa[xi]
                        p3 = nps()
                        for cj in range(8):
                            sc.op("tensor", lambda e, srcT=srcT, cj=cj, p3=p3: e.transpose(ps[:, p3, cj * 64:(cj + 1) * 64],
                                                                                         srcT[:, cj * 64:(cj + 1) * 64], I64),
                                  reads=[bsrc, bcst], writes=[bps[p3]], inc=(cj == 7))
                        dstT, bdt = (Kt, bKt) if j == 1 else (Vt, bVt)
                        sc.op("vector", lambda e, dstT=dstT, tt=tt, p3=p3: e.tensor_copy(
                            out=dstT[:, tt * 8:(tt + 1) * 8, :], in_=ps[:, p3, :].rearrange("p (a b) -> p a b", b=64)),
                            reads=[bps[p3]], writes=[bdt])
            if fz is None:
                sc.dma("sync", lambda e, u=u, seg=seg: e.dma_start(
                    out=zseg[:], in_=gz_d[u, seg * SEGT:(seg + 1) * SEGT, :].rearrange("(n s) d -> s n d", s=64)),
                    writes=[bz], key=f"gu{u}" + "gz")
            else:
                def zsrc(e, u=u, seg=seg):
                    return fz.LZ[u, :, seg * SEGT:(seg + 1) * SEGT]
                sc.dma("gpsimd", lambda e, zsrc=zsrc: e.dma_start(out=zseg[:].rearrange("p a b -> p (a b)"), in_=zsrc(e)),
                       reads=[fz.bUr], writes=[bz], key=f"gu{u}" + "gz")
            if STOP <= 2:
                return [bz, bKt, bVt, bqn]
            for gi in range(SEGC // 8):
                l0 = gi * 8
                n0 = seg * SEGC + l0
                v3 = lambda t: t[:]
                pk, pd, pdt = nps(), nps(), nps()
                for j in range(8):
                    cs = slice((l0 + j) * 64, (l0 + j + 1) * 64)
                    sc.op("tensor", lambda e, j=j, cs=cs, pk=pk: e.matmul(ps[:, pk, j * 64:(j + 1) * 64], lhsT=knT[:, cs], rhs=knT[:, cs],
                                                                         start=True, stop=True), reads=[bkn], writes=[bps[pk]], inc=(j == 7))
                sc.op("vector", lambda e, n0=n0: e.tensor_tensor(out=rhsD[:], in0=MB[:], in1=bc_n(g_t, n0), op=ALU.mult),
                      reads=[bcst, bg], writes=[brhsD])
                if STOP <= 2.1:
                    return [brhsD, bps[pk]]
                sc.op("tensor", lambda e, pd=pd: e.matmul(ps[:, pd, :], lhsT=Tri, rhs=rhsD[:].rearrange("p a b -> p (a b)"),
                                                          start=True, stop=True), reads=[bcst, brhsD], writes=[bps[pd]])
                for j in range(8):
                    sc.op("tensor", lambda e, j=j, pdt=pdt: e.matmul(ps[:, pdt, j * 64:(j + 1) * 64], lhsT=rhsD[:, j, :], rhs=Tri,
                                                                    start=True, stop=True), reads=[bcst, brhsD], writes=[bps[pdt]], inc=(j == 7))
                r3 = lambda ap: ap.rearrange("p (a b) -> p a b", b=64)
                sc.op("scalar", lambda e, pd=pd: e.activation(out=ED[:], in_=r3(ps[:, pd, :]), func=AF.Exp), reads=[bps[pd]], writes=[bED])
                sc.op("scalar", lambda e, pdt=pdt: e.activation(out=EDT[:], in_=r3(ps[:, pdt, :]), func=AF.Exp), reads=[bps[pdt]], writes=[bEDT])
                if STOP <= 2.2:
                    return [bED, bEDT]
                sc.op("vector", lambda e, pk=pk: e.tensor_tensor(out=Lp[0][:], in0=r3(ps[:, pk, :]), in1=ED[:], op=ALU.mult),
                      reads=[bps[pk], bED], writes=[bL[0]])
                sc.op("vector", lambda e, n0=n0: e.tensor_tensor(out=Lp[0][:], in0=Lp[0][:], in1=bc_n(beta, n0), op=ALU.mult),
                      reads=[bL[0], bg], writes=[bL[0]])
                sc.op("vector", lambda e: e.tensor_tensor(out=Lp[0][:], in0=Lp[0][:], in1=MB[:], op=ALU.mult),
                      reads=[bL[0], bcst], writes=[bL[0]])
                if STOP <= 2.3:
                    return [bL[0]]
                pn = nps()
                for j in range(8):
                    sc.op("tensor", lambda e, j=j, pn=pn: e.matmul(ps[:, pn, j * 64:(j + 1) * 64], lhsT=Lp[0][:, j, :], rhs=I64,
                                                                  start=True, stop=True),
                          reads=[bL[0], bcst], writes=[bps[pn]], inc=(j == 7))
                sc.op("scalar", lambda e, pn=pn: e.copy(out=Np[0][:], in_=r3(ps[:, pn, :])), reads=[bps[pn]], writes=[bN[0]])
                sc.op("vector", lambda e: e.tensor_tensor(out=Pm[:], in0=IB[:], in1=Np[0][:], op=ALU.subtract),
                      reads=[bN[0], bcst], writes=[bP])
                if STOP <= 2.4:
                    return [bP, bN[0]]
                sc.op("gpsimd", lambda e: e.tensor_copy(out=Lb[0][:], in_=Lp[0][:]), reads=[bL[0]], writes=[bLb[0]])
                sc.op("gpsimd", lambda e: e.tensor_copy(out=Nb[0][:], in_=Np[0][:]), reads=[bN[0]], writes=[bNb[0]])
                sc.op("gpsimd", lambda e: e.tensor_copy(out=Pb[:], in_=Pm[:]), reads=[bP], writes=[bPb])
                cur = 0
                for lvl in range(5):
                    nxt = 1 - cur
                    pl = nps()
                    for j in range(8):
                        sc.op("tensor", lambda e, j=j, pl=pl, cur=cur: e.matmul(ps[:, pl, j * 64:(j + 1) * 64], lhsT=Nb[cur][:, j, :],
                                                                               rhs=Lb[cur][:, j, :], start=True, stop=True),
                              reads=[bNb[cur], bLb[cur]], writes=[bps[pl]], inc=(j == 7))
                    if lvl < 4:
                        pn2 = nps()
                        for j in range(8):
                            sc.op("tensor", lambda e, j=j, pn2=pn2, cur=cur: e.matmul(ps[:, pn2, j * 64:(j + 1) * 64], lhsT=Lb[cur][:, j, :],
                                                                                     rhs=Nb[cur][:, j, :], start=True, stop=True),
                                  reads=[bNb[cur], bLb[cur]], writes=[bps[pn2]], inc=(j == 7))
                    sc.op("scalar", lambda e, pl=pl, nxt=nxt: e.copy(out=Lb[nxt][:], in_=r3(ps[:, pl, :])), reads=[bps[pl]], writes=[bLb[nxt]])
                    if lvl < 4:
                        sc.op("vector", lambda e, pn2=pn2, nxt=nxt: e.tensor_copy(out=Nb[nxt][:], in_=r3(ps[:, pn2, :])),
                              reads=[bps[pn2]], writes=[bNb[nxt]])
                    pu = nps()
                    for j in range(8):
                        sc.op("tensor", lambda e, j=j, pu=pu, nxt=nxt: e.matmul(ps[:, pu, j * 64:(j + 1) * 64], lhsT=Lb[nxt][:, j, :],
                                                                               rhs=Pb[:, j, :], start=True, stop=True),
                              reads=[bLb[nxt], bPb], writes=[bps[pu]], inc=(j == 7))
                    sc.op("vector", lambda e, pu=pu: e.tensor_tensor(out=Pm[:], in0=Pm[:], in1=r3(ps[:, pu, :]), op=ALU.add),
                          reads=[bps[pu], bP], writes=[bP])
                    if lvl < 4:
                        sc.op("gpsimd", lambda e: e.tensor_copy(out=Pb[:], in_=Pm[:]), reads=[bP], writes=[bPb])
                    cur = nxt
                if STOP <= 2.5:
                    return [bP]
                sc.op("vector", lambda e, l0=l0, n0=n0: e.tensor_tensor(out=Kbg[:], in0=Kt[:, l0:l0 + 8, :], in1=bc_n(bgk, n0), op=ALU.mult),
                      reads=[bKt, bg], writes=[bKbg])
                sc.op("vector", lambda e, l0=l0, n0=n0: e.tensor_tensor(out=Vb[:], in0=Vt[:, l0:l0 + 8, :], in1=bc_n(beta, n0), op=ALU.mult),
                      reads=[bVt, bg], writes=[bVb])
                sc.op("vector", lambda e, l0=l0, n0=n0: e.tensor_tensor(out=kdec[:], in0=Kt[:, l0:l0 + 8, :], in1=bc_n(edec, n0), op=ALU.mult),
                      reads=[bKt, bg], writes=[bkdec])
                p_u, p_w, p_q = nps(), nps(), nps()
                for j in range(8):
                    sc.op("tensor", lambda e, j=j, p_u=p_u: e.matmul(ps[:, p_u, j * 64:(j + 1) * 64], lhsT=Pm[:, j, :], rhs=Vb[:, j, :],
                                                                    start=True, stop=True), reads=[bP, bVb], writes=[bps[p_u]], inc=(j == 7))
                for j in range(8):
                    sc.op("tensor", lambda e, j=j, p_w=p_w: e.matmul(ps[:, p_w, j * 64:(j + 1) * 64], lhsT=Kbg[:, j, :], rhs=Pm[:, j, :],
                                                                    start=True, stop=True), reads=[bP, bKbg], writes=[bps[p_w]], inc=(j == 7))
                for j in range(8):
                    cs = slice((l0 + j) * 64, (l0 + j + 1) * 64)
                    sc.op("tensor", lambda e, j=j, cs=cs, p_q=p_q: e.matmul(ps[:, p_q, j * 64:(j + 1) * 64], lhsT=knT[:, cs], rhs=qnT[:, cs],
                                                                           start=True, stop=True), reads=[bkn, bqn], writes=[bps[p_q]], inc=(j == 7))
                sc.op("scalar", lambda e, p_u=p_u: e.copy(out=u_sb[:], in_=r3(ps[:, p_u, :])), reads=[bps[p_u]], writes=[bu])
                sc.op("scalar", lambda e, p_w=p_w: e.copy(out=wT[:], in_=r3(ps[:, p_w, :])), reads=[bps[p_w]], writes=[bwT])
                sc.op("vector", lambda e, p_q=p_q: e.tensor_tensor(out=qkT[:], in0=r3(ps[:, p_q, :]), in1=EDT[:], op=ALU.mult),
                      reads=[bps[p_q], bEDT], writes=[bqkT])
                sc.op("vector", lambda e: e.tensor_tensor(out=qkT[:], in0=qkT[:], in1=TriB[:], op=ALU.mult), reads=[bqkT, bcst], writes=[bqkT])
                if STOP <= 3:
                    return [bqkT, bu, bwT]
                for j in range(8):
                    n = n0 + j
                    l = l0 + j
                    cs = slice(l * 64, (l + 1) * 64)
                    i2 = j % 2
                    bx_, by_ = SB0, SB0
                    sc.op("tensor", lambda e, j=j, bx_=bx_: e.matmul(ps[:, bx_, 0:64], lhsT=wT[:, j, :], rhs=St[:], start=True, stop=True),
                          reads=[bwT, bS], writes=[bps[bx_]])
                    sc.op("tensor", lambda e, cs=cs, by_=by_: e.matmul(ps[:, by_, 192:256], lhsT=qnT[:, cs], rhs=St[:], start=True, stop=True),
                          reads=[bqn, bS], writes=[bps[by_]])
                    sc.op("vector", lambda e, j=j, bx_=bx_, i2=i2: e.tensor_tensor(out=vn[i2][:], in0=u_sb[:, j, :], in1=ps[:, bx_, 0:64],
                                                                                 op=ALU.subtract), reads=[bu, bps[bx_]], writes=[bvn[i2]])
                    sc.op("vector", lambda e, by_=by_, i2=i2, n=n: e.tensor_scalar(out=As[i2][:], in0=ps[:, by_, 192:256], scalar1=egc[:, n:n + 1],
                                                                                  scalar2=None, op0=ALU.mult), reads=[bps[by_], bg], writes=[bAs[i2]])
                    sc.op("tensor", lambda e, j=j, bx_=bx_, i2=i2: e.matmul(ps[:, bx_, 64:128], lhsT=qkT[:, j, :], rhs=vn[i2][:],
                                                                          start=True, stop=True), reads=[bqkT, bvn[i2]], writes=[bps[bx_]], inc=False)
                    sc.op("tensor", lambda e, j=j, bx_=bx_, i2=i2: e.matmul(ps[:, bx_, 128:192], lhsT=kdec[:, j, :], rhs=vn[i2][:],
                                                                          start=True, stop=True), reads=[bkdec, bvn[i2]], writes=[bps[bx_]])
                    sc.op("vector", lambda e, bx_=bx_, n=n: e.scalar_tensor_tensor(out=St[:], in0=St[:], scalar=eglb[:, n:n + 1],
                                                                                  in1=ps[:, bx_, 128:192], op0=ALU.mult, op1=ALU.add),
                          reads=[bS, bg, bps[bx_]], writes=[bS])
                    sc.op("vector", lambda e, bx_=bx_, i2=i2, l=l: e.tensor_tensor(out=oseg[:, l, :], in0=As[i2][:], in1=ps[:, bx_, 64:128],
                                                                                 op=ALU.add), reads=[bAs[i2], bps[bx_]], writes=[boseg])
                if STOP <= 4:
                    return [boseg, bS]
            sc.op("gpsimd", lambda e: e.tensor_tensor(out=osq[:], in0=oseg[:], in1=oseg[:], op=ALU.mult), reads=[boseg], writes=[bosq])
            sc.op("vector", lambda e: e.tensor_reduce(out=oss[:], in_=osq[:], axis=AX.X, op=ALU.add), reads=[bosq], writes=[boss])
            sc.op("scalar", lambda e: e.activation(out=oss[:], in_=oss[:], func=AF.Sqrt, bias=epsg[:, 0:1], scale=1.0 / 64),
                  reads=[boss, bcst], writes=[boss])
            sc.op("vector", lambda e: e.reciprocal(out=oss[:], in_=oss[:]), reads=[boss], writes=[boss])
            sc.op("vector", lambda e: e.tensor_tensor(out=osq[:], in0=oseg[:], in1=oss[:].unsqueeze(2).to_broadcast([64, SEGC, 64]),
                                                      op=ALU.mult), reads=[boseg, boss], writes=[bosq])
            sc.op("gpsimd", lambda e: e.tensor_tensor(out=osq[:], in0=osq[:], in1=ngb[:].unsqueeze(1).to_broadcast([64, SEGC, 64]),
                                                      op=ALU.mult), reads=[bosq, bpar], writes=[bosq])
            sc.op("scalar", lambda e: e.activation(out=zseg[:], in_=zseg[:], func=AF.Silu), reads=[bz], writes=[bz])
            if fz is None:
                sc.op("vector", lambda e: e.tensor_tensor(out=osq[:], in0=osq[:], in1=zseg[:], op=ALU.mult), reads=[bosq, bz], writes=[bosq])
                sc.dma("sync", lambda e, u=u, seg=seg: e.dma_start(
                    out=go_d[u, seg * SEGT:(seg + 1) * SEGT, :].rearrange("(n s) d -> s n d", s=64), in_=osq[:]),
                    reads=[bosq], key=f"gu{u}" + "go")
            else:
                oT = oseg[:].rearrange("p a b -> p (a b)")
                zT = zseg[:].rearrange("p a b -> p (a b)")
                for g4 in range(SEGC // 8):
                    pt_ = nps()
                    for j in range(8):
                        sc.op("tensor", lambda e, j=j, g4=g4, pt_=pt_: e.transpose(ps[:, pt_, j * 64:(j + 1) * 64], osq[:, g4 * 8 + j, :], I64),
                              reads=[bosq, bcst], writes=[bps[pt_]], inc=(j == 7))
                    sc.op("vector", lambda e, g4=g4, pt_=pt_: e.tensor_tensor(out=oT[:, g4 * 512:(g4 + 1) * 512], in0=ps[:, pt_, :],
                                                                            in1=zT[:, g4 * 512:(g4 + 1) * 512], op=ALU.mult),
                          reads=[bps[pt_], bz, bosq], writes=[boseg])
                tok0 = seg * SEGT
                kblk, coff = tok0 // 1024, tok0 % 1024
                row0 = (kblk // 2) * 512 + 192 + (u * 2 + kblk % 2) * 64
                sc.dma("sync", lambda e, row0=row0, coff=coff: e.dma_start(out=fz.Ysend[row0:row0 + 64, coff:coff + SEGT],
                                                                          in_=oT[:, 0:SEGT]),
                       reads=[boseg, fz.bYs], key=f"gu{u}" + "go")
        return [bosq, boseg]

    class _Rec:
        def __init__(self):
            self.calls = []

        def op(self, *a, **k):
            self.calls.append(("op", a, k))

        def dma(self, *a, **k):
            self.calls.append(("dma", a, k))

    outs = []
    recs = []
    for u in range(nu):
        r = _Rec()
        outs += emit_unit(u, r)
        recs.append(r.calls)
    n = max(len(c) for c in recs)
    for i in range(n):
        for c in recs:
            if i < len(c):
                kind, a, k = c[i]
                getattr(sc, kind)(*a, **k)
    return outs


def build_gdn(nu=2):
    P = SimpleProg()
    sc = Sched(P.nc, P.es)
    ob = gdn_emit(P, sc, nu)
    return P, P.finish(sc, ob)


def gdn_const_inputs():
    i = np.arange(64)
    tri = (i[:, None] <= i[None, :]).astype(np.float32)
    ms = (i[:, None] > i[None, :]).astype(np.float32)
    return dict(gcst=np.stack([tri, ms, np.eye(64, dtype=np.float32)]))


def gdn_unit_inputs(ug, h, gdn_conv_w, a_log, dt_bias, norm_g):
    GW = 384
    raw = np.zeros((3, 64, S + 3), np.float32)
    cw = np.zeros((64, 12), np.float32)
    for j in range(3):
        cols = slice(j * GW + h * 64, j * GW + (h + 1) * 64)
        raw[j, :, 3:] = ug[:, cols].T
        cw[:, j * 4:(j + 1) * 4] = gdn_conv_w[:, cols].T
    z = np.ascontiguousarray(ug[:, 3 * GW + h * 64:3 * GW + (h + 1) * 64])
    a = np.ascontiguousarray(ug[:, 4 * GW + h].reshape(NCH, 64).T)
    b = np.ascontiguousarray(ug[:, 4 * GW + 6 + h].reshape(NCH, 64).T)
    par = np.zeros((64, 2), np.float32)
    par[:, 0] = a_log[h]
    par[:, 1] = dt_bias[h]
    ng = np.ascontiguousarray(np.broadcast_to(norm_g[None, :], (64, 64))).astype(np.float32)
    return dict(graw=raw, gcw=cw, gz=z, ga=a, gb=b, gpar=par, gng=ng)


def _lay(v):
    return np.ascontiguousarray(np.asarray(v, np.float32).reshape(-1, 128).T)


_PROGS = {}


def _prog(key, builder):
    if key not in _PROGS:
        _PROGS[key] = builder()
    return _PROGS[key]


def _run(nc, in_maps):
    res = run_bass_kernel_spmd(nc, in_maps, core_ids=list(range(NCORES)))
    return res.results


def _tok_launch(key, stages, inp, xT_list, yT_list=None):
    def mk():
        p = TokProg(stages)
        return p, p.build()
    P, nc = _prog(key, mk)
    maps = []
    for c in range(NCORES):
        b = c // 4
        m = {"xT": xT_list[c], "cT": _lay(inp["c"][b])}
        for name in P.in_names:
            if name in m:
                continue
            if name.startswith("yT"):
                m[name] = yT_list[c]
            elif name == "final_g":
                m[name] = _lay(inp["final_g"])
            else:
                base, l = name[:-1], int(name[-1])
                arr = np.asarray(inp[base][l], np.float32)
                if base == "b_ada" or base.startswith("ln_"):
                    arr = _lay(arr)
                m[name] = np.ascontiguousarray(arr)
        maps.append(m)
    return _run(nc, maps)


def _mixer(inp, l, u):
    y = np.zeros((B, S, D), np.float32)
    P, nc = _prog("conv", build_conv)
    maps = []
    for c in range(NCORES):
        b, j = c // 4, c % 4
        maps.append(conv_inputs(u[b], j, np.asarray(inp["conv_w"][l]), np.asarray(inp["conv_b"][l]),
                                np.asarray(inp["conv_ln_g"][l]), np.asarray(inp["conv_ln_b"][l])))
    res = _run(nc, maps)
    for c in range(NCORES):
        b, j = c // 4, c % 4
        y[b, j * TOK:(j + 1) * TOK, 0:256] = res[c]["ycT"].T
    P, nc = _prog("moba", lambda: build_moba(3))
    tab = rope_tables()
    shared = moba_shared_inputs(tab)
    consts = [moba_const_inputs(0), moba_const_inputs(1)]
    maps = []
    for c in range(NCORES):
        b, cc = c // 4, c % 4
        units = []
        for s in range(3):
            combo = 3 * cc + s
            h, half = combo // 2, combo % 2
            q = u[b, :, 512 + h * 64:512 + (h + 1) * 64]
            k = u[b, :, 512 + 384 + h * 64:512 + 384 + (h + 1) * 64]
            v = u[b, :, 512 + 768 + h * 64:512 + 768 + (h + 1) * 64]
            d = moba_unit_inputs(q, k, v, half, tab)
            d.update(consts[half])
            units.append(d)
        m = {k_: np.ascontiguousarray(np.stack([un[k_] for un in units])) for k_ in units[0]}
        m.update(shared)
        maps.append(m)
    res = _run(nc, maps)
    for c in range(NCORES):
        b, cc = c // 4, c % 4
        for s in range(3):
            combo = 3 * cc + s
            h, half = combo // 2, combo % 2
            qpos = np.concatenate([np.arange(bl * 256, (bl + 1) * 256) for bl in HALF_BLOCKS[half]])
            y[b, qpos, 256 + h * 64:256 + (h + 1) * 64] = res[c]["moT"][s].T
    P, nc = _prog("gdn", lambda: build_gdn(2))
    gconst = gdn_const_inputs()
    allu = [(b, h) for b in range(B) for h in range(6)]
    maps = []
    assign = []
    for c in range(NCORES):
        us = [allu[i] if i < len(allu) else allu[0] for i in (2 * c, 2 * c + 1)]
        assign.append([(i < len(allu)) for i in (2 * c, 2 * c + 1)])
        units = [gdn_unit_inputs(u[b, :, 512 + 1152:], h, np.asarray(inp["gdn_conv_w"][l]), np.asarray(inp["gdn_a_log"][l]),
                                 np.asarray(inp["gdn_dt_bias"][l]), np.asarray(inp["gdn_norm_g"][l])) for (b, h) in us]
        m = {k_: np.ascontiguousarray(np.stack([un[k_] for un in units])) for k_ in units[0]}
        m.update(gconst)
        maps.append(m)
    res = _run(nc, maps)
    for c in range(NCORES):
        for s in range(2):
            i = 2 * c + s
            if i < len(allu):
                b, h = allu[i]
                y[b, :, 640 + h * 64:640 + (h + 1) * 64] = res[c]["go"][s]
    return y


YROWS = 768 + 1024
RG = [[0, 1, 2, 3], [4, 5, 6, 7]]


def moba_unit(cc, su):
    return (cc, su) if su < 2 else (4 + cc // 2, cc % 2)


def moba_owner(h, half):
    return (h, half) if h < 4 else (2 * (h - 4) + half, 2)


class Fused:
    def __init__(self):
        self.nc = bass.Bass("TRN2", target_bir_lowering=False)
        self.es = ExitStack()
        self.cur = self.es
        self.dins = {}
        self.in_names = []
        self.out_names = []
        self.phase_i = 0
        self.load_x = False
        self.store_x = False
        self._dyn = {}

    def din(self, name, shape, dt=F32):
        if name not in self.dins:
            self.in_names.append(name)
            self.dins[name] = self.nc.dram_tensor(name, list(shape), dt, kind="ExternalInput").ap()
        return self.dins[name]

    def dout(self, name, shape, dt=F32):
        if name not in self.dins:
            self.out_names.append(name)
            self.dins[name] = self.nc.dram_tensor(name, list(shape), dt, kind="ExternalOutput").ap()
        return self.dins[name]

    AW = 36800

    def sb(self, name, shape, dt):
        p = shape[0]
        n = int(np.prod(shape[1:]))
        n32 = n if dt == F32 else (n + 1) // 2
        n32 = (n32 + 7) // 8 * 8
        off = self.aoff
        self.aoff += n32
        assert self.aoff <= self.AW, (name, self.aoff)
        v = self.arena[0:p, off:off + n32]
        if dt != F32:
            v = v.bitcast(dt)
        v = v[:, 0:n]
        if len(shape) == 3:
            v = v.rearrange("p (a b) -> p a b", a=shape[1])
        return v

    def dyn(self, e, engname, key):
        c = self._dyn.setdefault(engname, {})
        if "cc" not in c:
            c["cc"] = e.snap(e.partition_id() % 4)
        if key not in c:
            cc = c["cc"]
            doff = lambda h: (h // 2) * 512 + (h % 2) * 64
            v = {"c2048": lambda: cc * 2048, "prev": lambda: (cc + 3) % 4,
                 "D0": lambda: doff(cc), "D2": lambda: (cc // 2) * 64 + 1024, "mha2": lambda: cc % 2, "mhb2": lambda: 3 - cc % 2,
                 "gh1": lambda: (cc + 4) % 6, "Dg1": lambda: doff((cc + 4) % 6)}[key]()
            c[key] = e.snap(v)
        return c[key]

    def build(self):
        nc, es = self.nc, self.es
        sc = self.sc = Sched(nc, es)
        self.x = es.enter_context(nc.sbuf_tensor("x_res", [128, KC, TOK], F32))
        self.arena = es.enter_context(nc.sbuf_tensor("arena", [128, self.AW], F32))
        self.aoff = 0
        self.bx = [[Buf(f"x{c}_{t}") for t in range(TOK // 512)] for c in range(KC)]
        self.ps = es.enter_context(nc.psum_tensor("ps_all", [128, 8, 512], F32))
        NUC = (DIN + 127) // 128
        Usend_t = nc.dram_tensor("Usend", [NUC * 128, TOK], F32)
        Urecv_t = nc.dram_tensor("Urecv", [NUC * 512 + 128, TOK], F32)
        Ysend_t = nc.dram_tensor("Ysend", [2048, 1024], F32)
        Yrecv_t = nc.dram_tensor("Yrecv", [8192, 1024], F32)
        Yfull_t = nc.dram_tensor("Yfull", [D, TOK], F32)
        self.Usend, self.Urecv, self.Ysend, self.Yrecv, self.Yfull = (t.ap() for t in (Usend_t, Urecv_t, Ysend_t, Yrecv_t, Yfull_t))
        self.LK = nc.dram_tensor("LK", [3, 64, S], F32).ap()
        self.LV = nc.dram_tensor("LV", [3, 64, S], F32).ap()
        self.LQ = nc.dram_tensor("LQ", [3, 64, MOBA_SLOTS * 256], F32).ap()
        self.LQF = nc.dram_tensor("LQF", [3, 64, S], F32).ap()
        self.LG = nc.dram_tensor("LG", [2, 3, 64, S], F32).ap()
        self.LZ = nc.dram_tensor("LZ", [2, 64, S], F32).ap()
        self.LAB = nc.dram_tensor("LAB", [2, 2, S], F32).ap()
        self.LH = nc.dram_tensor("LH", [512, CH], F32).ap()
        self.Yloc = nc.dram_tensor("Yloc", [4, 512, 1024], F32).ap()
        self.uT_dst = self.Usend
        self.yT_src = self.Yfull
        self.bU, self.bUr, self.bYs, self.bYr, self.bY = Buf("U"), Buf("Ur"), Buf("Ys"), Buf("Yr"), Buf("Yf")
        self.bL, self.bLq, self.bYl = Buf("L"), Buf("Lq"), Buf("Yl")
        self.bUr_m, self.bUr_g = Buf("Ur_m"), Buf("Ur_g")
        outb = []

        def run_phase(fn):
            self.aoff = 0
            r = fn()
            sc.barrier(exclude=("agUm", "agUg"))
            self.phase_i += 1
            return r

        def tok_phase(stages, load_x=False, store_x=False, ag=True):
            def fn():
                self.load_x, self.store_x = load_x, store_x
                r = TokProg(stages, fused=self).build()
                if ag:
                    order = list(range(4, 13)) + list(range(13, NUC)) + list(range(0, 4))
                    waits = sc._deps("gpsimd", (), [self.bU, self.bUr_m, self.bUr_g])
                    sc.q["gpsimd"].append((waits, None, None))
                    for ci in order:
                        key = "agUm" if 4 <= ci < 13 else "agUg"
                        sc._get_dsem(key)
                        sc.cckeys.add(key)
                        sc.dcnt[key] += 1
                        sc.q["gpsimd"].append(([], (lambda e, ci=ci: e.collective_compute(
                            "AllGather", ALU.bypass, replica_groups=RG,
                            ins=[Usend_t.ap()[ci * 128:(ci + 1) * 128, :]], outs=[Urecv_t.ap()[ci * 512:(ci + 1) * 512, :]])),
                            ("c", key, sc.dcnt[key])))
                    for b, key in ((self.bUr_m, "agUm"), (self.bUr_g, "agUg"), (self.bU, "agUg")):
                        b.lw = ("c", key, sc.dcnt[key])
                        b.rd = {}
                return r
            return run_phase(fn)

        def y_exchange():
            for ci in range(8):
                sc.cc(lambda e, ci=ci: e.collective_compute(
                    "AllGather", ALU.bypass, replica_groups=RG,
                    ins=[Ysend_t.ap()[ci * 256:(ci + 1) * 256, :]], outs=[Yrecv_t.ap()[ci * 1024:(ci + 1) * 1024, :]]),
                    writes=[self.bYs, self.bYr], key="agY")
            LB = ([0, 3, 4, 7], [1, 2, 5, 6])
            for c2 in range(2):
                sc.dma("scalar", lambda e, c2=c2: e.dma_start(
                    out=self.Yloc[:, c2 * 256:(c2 + 1) * 256, :],
                    in_=self.Yrecv[c2 * 1024:c2 * 1024 + 7168, :][bass.ds(self.dyn(e, "scalar", "c2048"), 1024), :].rearrange(
                        "(r f) t -> r f t", r=4)),
                    reads=[self.bYr], writes=[self.bYl], key="yloc")
            for h in range(6):
                for half in range(2):
                    rs, su = moba_owner(h, half)
                    for q4 in range(4):
                        lb = LB[half][q4]
                        sc.dma("sync", lambda e, h=h, lb=lb, rs=rs, su=su, q4=q4: e.dma_start(
                            out=self.Yfull[256 + h * 64:256 + (h + 1) * 64, lb * 256:(lb + 1) * 256],
                            in_=self.Yloc[rs, su * 64:(su + 1) * 64, q4 * 256:(q4 + 1) * 256]),
                            reads=[self.bYl, self.bY], key="yasm")
            for h in range(6):
                rs, g = (h, 0) if h < 4 else (h - 4, 1)
                for kk in range(2):
                    r0 = 192 + (g * 2 + kk) * 64
                    sc.dma("sync", lambda e, h=h, kk=kk, rs=rs, r0=r0: e.dma_start(
                        out=self.Yfull[640 + h * 64:640 + (h + 1) * 64, kk * 1024:(kk + 1) * 1024],
                        in_=self.Yloc[rs, r0:r0 + 64, :]),
                        reads=[self.bYl, self.bY], key="yasm")

        def localize_m():
            Ur = self.Urecv
            rk = lambda ap: ap.rearrange("d (r t) -> d r t", r=4)
            LQF = self.LQF

            def blk(e, q, dkey, B, n=64):
                R0 = (B // 128) * 512 + B % 128
                R1 = min(R0 + 2048, NUC * 512 + 128)
                return Ur[R0:R1, :][bass.ds(self.dyn(e, q, dkey), 512), :].rearrange("(r f) t -> f r t", r=4)[0:n]

            for u in range(3):
                q = "sync" if u < 2 else "scalar"
                dk = "D0" if u < 2 else "D2"
                bqf = Buf()
                for (dst, B) in ((LQF, 512), (self.LK, 896), (self.LV, 1280)):
                    sc.dma(q, lambda e, u=u, q=q, dk=dk, dst=dst, B=B: e.dma_start(out=rk(dst[u]), in_=blk(e, q, dk, B)),
                           reads=[self.bUr_m], writes=[bqf if B == 512 else Buf()], key=(f"locqf{u}" if B == 512 else f"loc{q}{u}"))
                for ab in range(2):
                    dstq = self.LQ[u].rearrange("d (G ab i) -> d G ab i", G=8, ab=2)[:, :, ab:ab + 1, :]
                    srcv = LQF[u].rearrange("d (G b i) -> d G b i", G=8, b=4)
                    if u < 2:
                        b = u if ab == 0 else 3 - u
                        sc.dma(q, lambda e, dstq=dstq, srcv=srcv, b=b: e.dma_start(out=dstq, in_=srcv[:, :, b:b + 1, :]),
                               reads=[bqf], writes=[Buf()], key=f"locq{u}")
                    else:
                        kn = "mha2" if ab == 0 else "mhb2"
                        sc.dma(q, lambda e, dstq=dstq, srcv=srcv, kn=kn: e.dma_start(
                            out=dstq, in_=srcv[:, :, bass.ds(self.dyn(e, "scalar", kn), 1), :]),
                            reads=[bqf], writes=[Buf()], key=f"locq{u}")

        def localize_g():
            Ur = self.Urecv
            rk = lambda ap: ap.rearrange("d (r t) -> d r t", r=4)
            LQF = self.LQF

            def blk(e, q, dkey, B, n=64):
                R0 = (B // 128) * 512 + B % 128
                R1 = min(R0 + 2048, NUC * 512 + 128)
                return Ur[R0:R1, :][bass.ds(self.dyn(e, q, dkey), 512), :].rearrange("(r f) t -> f r t", r=4)[0:n]

            for u in range(2):
                dk, gk = ("D0", "cc") if u == 0 else ("Dg1", "gh1")
                for j in range(3):
                    sc.dma("gpsimd", lambda e, u=u, j=j, dk=dk: e.dma_start(
                        out=rk(self.LG[u, j]), in_=blk(e, "gpsimd", dk, 1664 + j * 384)),
                        reads=[self.bUr_g], writes=[Buf()], key="locg")
                q2 = "gpsimd" if u == 0 else "sync"
                sc.dma(q2, lambda e, u=u, dk=dk, q2=q2: e.dma_start(out=rk(self.LZ[u]), in_=blk(e, q2, dk, 2816)),
                       reads=[self.bUr_g], writes=[Buf()], key=f"locz{u}")
                for ab, B in ((0, 3200), (1, 3206)):
                    sc.dma(q2, lambda e, u=u, ab=ab, B=B, gk=gk, q2=q2: e.dma_start(
                        out=self.LAB[u, ab:ab + 1].rearrange("o (r t) -> o r t", r=4), in_=blk(e, q2, gk, B, 1)),
                        reads=[self.bUr_g], writes=[Buf()], key=f"locz{u}")
            sc.dma("scalar", lambda e: e.dma_start(
                out=self.LH.rearrange("(c f) t -> c f t", c=4),
                in_=Ur[0:2048, TOK - CH:TOK].rearrange("(c r f) t -> r c f t", r=4, f=128)[bass.ds(self.dyn(e, "scalar", "prev"), 1)]),
                reads=[self.bUr_g], writes=[Buf()], key="loch")

        def mixer(l):
            pfx = f"L{l}_"
            run_phase(localize_m)
            run_phase(lambda: moba_emit(self, sc, 3, pfx, fz=self))
            run_phase(localize_g)
            run_phase(lambda: conv_emit(self, sc, pfx, fz=self))

            def g():
                gdn_emit(self, sc, 2, pfx, fz=self)
                y_exchange()
            run_phase(g)

        zpad = self.din("zpad", [NUC * 128 - DIN, TOK])
        sc.dma("sync", lambda e: e.dma_start(out=self.Usend[DIN:NUC * 128, :], in_=zpad[:, :]), reads=[self.bU], key="zpad")
        tok_phase([("ffn1", 0), ("uproj", 0)], load_x=True)
        mixer(0)
        tok_phase([("wout", 0), ("ffn2", 0), ("ffn1", 1), ("uproj", 1)])
        mixer(1)
        outb = tok_phase([("wout", 1), ("ffn2", 1), ("final",)], store_x=True, ag=False)
        with nc.Block() as block:
            sc.emit(block)
        es.close()
        return nc


_FUSED = {}


def kernel(**inp):
    if "p" not in _FUSED:
        F = Fused()
        _FUSED["p"] = (F, F.build())
    F, nc = _FUSED["p"]
    x = np.asarray(inp["x"], np.float32)
    tab = rope_tables()
    shared = moba_shared_inputs(tab)
    mconst = [moba_const_inputs(0), moba_const_inputs(1)]
    gconst = gdn_const_inputs()
    wnames = ("w_ada", "ffn1_w_gate", "ffn1_w_up", "ffn1_w_down", "w_in", "w_out", "ffn2_w_gate", "ffn2_w_up", "ffn2_w_down")
    lnames = ("b_ada", "ln_ffn1_g", "ln_mix_g", "ln_ffn2_g")
    common = {}
    for l in range(2):
        for n in wnames:
            common[f"{n}{l}"] = np.ascontiguousarray(np.asarray(inp[n][l], np.float32))
        for n in lnames:
            common[f"{n}{l}"] = _lay(inp[n][l])
        cw = np.asarray(inp["conv_w"][l], np.float32)
        lay2 = lambda v: np.ascontiguousarray(np.asarray(v, np.float32).reshape(2, 128).T)
        common[f"L{l}_cw"] = np.ascontiguousarray(cw.T.reshape(2, 128, 31).transpose(1, 0, 2))
        common[f"L{l}_cp"] = np.ascontiguousarray(np.stack([lay2(inp["conv_b"][l]), lay2(inp["conv_ln_g"][l]),
                                                            lay2(inp["conv_ln_b"][l])], axis=-1))
    common["final_g"] = _lay(inp["final_g"])
    common["cident"] = np.eye(128, dtype=np.float32)
    common["zpad"] = np.zeros((((DIN + 127) // 128) * 128 - DIN, TOK), np.float32)
    common.update(shared)
    common.update(gconst)
    maps = []
    for c in range(NCORES):
        b, cc = c // 4, c % 4
        m = dict(common)
        m["xT"] = np.ascontiguousarray(x[b, cc * TOK:(cc + 1) * TOK].T)
        m["cT"] = _lay(inp["c"][b])
        m["cflag"] = np.full((128, 1), 0.0 if cc == 0 else 1.0, np.float32)
        units = []
        for su in range(3):
            half = moba_unit(cc, su)[1]
            qpos = np.concatenate([np.arange(bl * 256, (bl + 1) * 256) for bl in HALF_BLOCKS[half]])
            d = dict(mconst[half])
            d["ropeq"] = np.ascontiguousarray(tab[:, :, qpos])
            units.append(d)
        for k_ in units[0]:
            m[k_] = np.ascontiguousarray(np.stack([un[k_] for un in units]))
        for l in range(2):
            heads = [cc, (cc + 4) % 6]
            gw = np.asarray(inp["gdn_conv_w"][l], np.float32)
            gcw = np.zeros((2, 64, 12), np.float32)
            gpar = np.zeros((2, 64, 2), np.float32)
            gng = np.zeros((2, 64, 64), np.float32)
            for g, h in enumerate(heads):
                for j in range(3):
                    gcw[g, :, j * 4:(j + 1) * 4] = gw[:, j * 384 + h * 64:j * 384 + (h + 1) * 64].T
                gpar[g, :, 0] = np.asarray(inp["gdn_a_log"][l], np.float32)[h]
                gpar[g, :, 1] = np.asarray(inp["gdn_dt_bias"][l], np.float32)[h]
                gng[g] = np.asarray(inp["gdn_norm_g"][l], np.float32)[None, :]
            m[f"L{l}_gcw"], m[f"L{l}_gpar"], m[f"L{l}_gng"] = gcw, gpar, gng
        maps.append({k_: m[k_] for k_ in F.in_names})
    res = run_bass_kernel_spmd(nc, maps, core_ids=list(range(NCORES))).results
    out = np.zeros((B, S, D), np.float32)
    for c in range(NCORES):
        out[c // 4, (c % 4) * TOK:(c % 4 + 1) * TOK] = res[c]["xoT"].T
    return out
```

```python
import numpy as np
from contextlib import ExitStack
import concourse.bass as bass
import concourse.mybir as mybir
from concourse.bass_utils import run_bass_kernel_spmd

F32 = mybir.dt.float32
BF16 = mybir.dt.bfloat16
AF = mybir.ActivationFunctionType
ALU = mybir.AluOpType

D = 1024
KC = 8
DFF = 2816
FC = 22
DIN = 3212
B = 2
S = 8192
NCORES = 8
TOK = 2048
EPS = 1e-6

SAME_ENG_SYNC = True


class Buf:
    __slots__ = ("name", "lw", "rd")

    def __init__(self, name=""):
        self.name = name
        self.lw = None
        self.rd = {}


class Sched:
    ENGS = ("tensor", "vector", "scalar", "gpsimd", "sync")
    EPOCH = 20000

    def __init__(self, nc, es):
        self.nc = nc
        self.es = es
        self.q = {e: [] for e in self.ENGS}
        self.cnt = {e: 0 for e in self.ENGS}
        self.seen = {e: {} for e in self.ENGS}
        self.esem = {}
        self.dsem = {}
        self.dcnt = {}
        self.cckeys = set()

    def _get_esem(self, eng, epoch):
        k = (eng, epoch)
        if k not in self.esem:
            self.esem[k] = self.es.enter_context(self.nc.semaphore(f"se_{eng}_{epoch}"))
        return self.esem[k]

    def _get_dsem(self, key):
        if key not in self.dsem:
            self.dsem[key] = self.es.enter_context(self.nc.semaphore(f"sd_{key}"))
            self.dcnt[key] = 0
        return self.dsem[key]

    def _need(self, eng, tok, waits):
        if tok is None:
            return
        kind, k, val = tok
        if kind == "e":
            if k == eng and (eng == "tensor" or not SAME_ENG_SYNC):
                return
        key = (kind, k)
        if self.seen[eng].get(key, 0) >= val:
            return
        self.seen[eng][key] = val
        waits.append(tok)

    def _deps(self, eng, reads, writes):
        waits = []
        for b in reads:
            self._need(eng, b.lw, waits)
        for b in writes:
            self._need(eng, b.lw, waits)
            for k, v in b.rd.items():
                self._need(eng, (k[0], k[1], v), waits)
        return waits

    def _mark(self, tok, reads, writes):
        key = (tok[0], tok[1])
        for b in reads:
            if b.rd.get(key, 0) < tok[2]:
                b.rd[key] = tok[2]
        for b in writes:
            b.lw = tok
            b.rd = {}

    def op(self, eng, fn, reads=(), writes=(), inc=True):
        waits = self._deps(eng, reads, writes)
        idx = self.cnt[eng] + 1
        if inc:
            self.cnt[eng] = idx
        tok = ("e", eng, idx)
        self._mark(tok, reads, writes)
        self.q[eng].append((waits, fn, tok if inc else None))

    def dma(self, qeng, fn, reads=(), writes=(), key="d"):
        waits = self._deps(qeng, reads, writes)
        self._get_dsem(key)
        self.dcnt[key] += 1
        tok = ("d", key, 16 * self.dcnt[key])
        self._mark(tok, reads, writes)
        self.q[qeng].append((waits, fn, tok))

    def cc(self, fn, reads=(), writes=(), key="cc"):
        waits = self._deps("gpsimd", reads, writes)
        self._get_dsem(key)
        self.cckeys.add(key)
        self.dcnt[key] += 1
        tok = ("c", key, self.dcnt[key])
        self._mark(tok, reads, writes)
        self.q["gpsimd"].append((waits, fn, tok))

    def barrier(self, exclude=()):
        for e in self.ENGS:
            waits = []
            for e2 in self.ENGS:
                if e2 != e and self.cnt[e2] > 0:
                    self._need(e, ("e", e2, self.cnt[e2]), waits)
            for key, n in self.dcnt.items():
                if n > 0 and key not in exclude:
                    kind = "c" if key in self.cckeys else "d"
                    self._need(e, (kind, key, n if kind == "c" else 16 * n), waits)
            self.q[e].append((waits, None, None))

    def final_wait(self, eng, toks_bufs):
        waits = self._deps(eng, (), toks_bufs)
        self.q[eng].append((waits, None, None))

    def emit(self, block):
        nc = self.nc

        def run(engname):
            def body(eng):
                for waits, fn, tok in self.q[engname]:
                    for (kind, k, val) in waits:
                        if kind == "e":
                            epoch = (val - 1) // self.EPOCH
                            eng.wait_ge(self._get_esem(k, epoch), val - epoch * self.EPOCH)
                        else:
                            eng.wait_ge(self.dsem[k], val)
                    if fn is None:
                        continue
                    ins = fn(eng)
                    if tok is not None:
                        if tok[0] == "e":
                            epoch = (tok[2] - 1) // self.EPOCH
                            ins.then_inc(self._get_esem(tok[1], epoch), 1)
                        elif tok[0] == "c":
                            ins.then_inc(self.dsem[tok[1]])
                        else:
                            ins.then_inc(self.dsem[tok[1]], 16)
                self.q[engname] = []
            return body

        for e in self.ENGS:
            for ep in range((self.cnt[e] - 1) // self.EPOCH + 1 if self.cnt[e] else 0):
                self._get_esem(e, ep)
        block.tensor(run("tensor"))
        block.vector(run("vector"))
        block.scalar(run("scalar"))
        block.gpsimd(run("gpsimd"))
        block.sync(run("sync"))


class TokProg:
    def __init__(self, stages, tok=TOK, fused=None):
        self.stages = stages
        self.tok = tok
        self.fused = fused
        if fused is None:
            self.nc = bass.Bass("TRN2", target_bir_lowering=False)
            self.es = ExitStack()
        else:
            self.nc = fused.nc
            self.es = fused.es
        self.in_names = []
        self.out_names = []

    def din(self, name, shape, dt=F32):
        if self.fused is not None:
            return self.fused.din(name, shape, dt)
        self.in_names.append(name)
        return self.nc.dram_tensor(name, list(shape), dt, kind="ExternalInput").ap()

    def dout(self, name, shape, dt=F32):
        if self.fused is not None:
            return self.fused.dout(name, shape, dt)
        self.out_names.append(name)
        return self.nc.dram_tensor(name, list(shape), dt, kind="ExternalOutput").ap()

    def sb(self, name, shape, dt):
        if self.fused is not None:
            return self.fused.sb(name, shape, dt)
        return self.es.enter_context(self.nc.sbuf_tensor(name, list(shape), dt))

    def build(self):
        nc, es = self.nc, self.es
        fz = self.fused
        T = self.tok
        NH = T // 1024
        stages = self.stages
        layers = sorted({s[1] for s in stages if len(s) > 1})
        need_v = {}
        for s in stages:
            if s[0] == "ffn1":
                need_v.setdefault(s[1], set()).update([0, 1, 2])
            elif s[0] == "uproj":
                need_v.setdefault(s[1], set()).update([3, 4])
            elif s[0] == "wout":
                need_v.setdefault(s[1], set()).update([5])
            elif s[0] == "ffn2":
                need_v.setdefault(s[1], set()).update([6, 7, 8])

        xT_d = self.din("xT", [D, T]) if (fz is None or fz.load_x) else None
        cT_d = self.din("cT", [128, KC])
        W = {}
        for l in layers:
            W[("w_ada", l)] = self.din(f"w_ada{l}", [D, 9 * D])
            W[("b_ada", l)] = self.din(f"b_ada{l}", [128, 72])
        for s in stages:
            if s[0] in ("ffn1", "ffn2"):
                l = s[1]
                n = s[0]
                W[(n + "_g", l)] = self.din(f"ln_{n}_g{l}", [128, KC])
                W[(n + "_wg", l)] = self.din(f"{n}_w_gate{l}", [D, DFF])
                W[(n + "_wu", l)] = self.din(f"{n}_w_up{l}", [D, DFF])
                W[(n + "_wd", l)] = self.din(f"{n}_w_down{l}", [DFF, D])
            elif s[0] == "uproj":
                l = s[1]
                W[("mix_g", l)] = self.din(f"ln_mix_g{l}", [128, KC])
                W[("w_in", l)] = self.din(f"w_in{l}", [D, DIN])
                W[("uT", l)] = self.dout(f"uT{l}", [DIN, T]) if fz is None else fz.uT_dst
            elif s[0] == "wout":
                l = s[1]
                W[("w_out", l)] = self.din(f"w_out{l}", [D, D])
                W[("yT", l)] = self.din(f"yT{l}", [D, T]) if fz is None else fz.yT_src
            elif s[0] == "final":
                W[("final_g",)] = self.din("final_g", [128, KC])
        xo_d = self.dout("xoT", [D, T]) if (fz is None or fz.store_x) else None

        x = self.sb("x", [128, KC, T], F32) if fz is None else fz.x
        h = self.sb("h", [128, KC, 1024], BF16)
        act = self.sb("act", [128, FC, 1024], BF16)
        wd = self.sb("wd", [128, FC, D], BF16)
        NSLOT = 4
        SLOTW = 256
        wslot = [self.sb(f"ws{i}", [128, KC, SLOTW], BF16) for i in range(NSLOT)]
        tmpA = [self.sb(f"tmpA{i}", [128, 512], F32) for i in range(2)]
        tmpB = [self.sb(f"tmpB{i}", [128, 512], F32) for i in range(2)]
        sqb = [self.sb(f"sq{i}", [128, 512], BF16) for i in range(2)]
        rstd = self.sb("rstd", [128, 512], F32)
        ones = self.sb("ones", [128, 128], BF16)
        cT = self.sb("cT_sb", [128, KC], F32)
        cact = self.sb("cact", [128, KC], BF16)
        bada = {l: self.sb(f"bada{l}", [128, 72], F32) for l in layers}
        mod = {l: self.sb(f"mod{l}", [128, 72], F32) for l in layers}
        gains = {}
        for k in W:
            if k[0] in ("ffn1_g", "ffn2_g", "mix_g", "final_g"):
                gains[k] = self.sb("g_" + "_".join(map(str, k)), [128, KC], F32)
        coefA = {}
        coefG = {}
        ps = es.enter_context(nc.psum_tensor("ps", [128, 8, 512], F32)) if fz is None else fz.ps

        sc = Sched(nc, es) if fz is None else fz.sc
        bx = [[Buf(f"x{c}_{t}") for t in range(T // 512)] for c in range(KC)] if fz is None else fz.bx
        bU = [] if fz is None else [fz.bU]
        bY = [] if fz is None else [fz.bY]
        bh = [Buf(f"h{t}") for t in range(2)]
        bact = [[Buf(f"act{f}_{t}") for t in range(2)] for f in range(FC)]
        WD_PIECES = ((0, 6), (6, 12), (12, 17), (17, 22))
        bwd = [Buf(f"wd{i}") for i in range(4)]
        wd_piece = {}
        for i, (f0, f1) in enumerate(WD_PIECES):
            for f in range(f0, f1):
                wd_piece[f] = i
        bws = [Buf(f"ws{i}") for i in range(NSLOT)]
        btA = [Buf() for _ in range(2)]
        btB = [Buf() for _ in range(2)]
        bsq = [Buf() for _ in range(2)]
        brstd = Buf()
        bones = Buf()
        bps = [Buf(f"ps{i}") for i in range(8)]
        bmisc = Buf("misc")
        bmod = Buf("mod")

        if xT_d is not None:
            xT_v = xT_d.rearrange("(c p) t -> p c t", p=128)
            for c in range(KC):
                sc.dma("sync", lambda e, c=c: e.dma_start(out=x[:, c, :], in_=xT_v[:, c, :]),
                       writes=bx[c], key=f"x{c}")
        sc.dma("sync", lambda e: e.dma_start(out=cT[:], in_=cT_d[:, :]), writes=[bmisc], key="misc")
        for l in layers:
            sc.dma("sync", lambda e, l=l: e.dma_start(out=bada[l][:], in_=W[("b_ada", l)][:, :]),
                   writes=[bmisc], key="misc")
        for k, t in gains.items():
            sc.dma("sync", lambda e, k=k, t=t: e.dma_start(out=t[:], in_=W[k][:, :]), writes=[bmisc], key="misc")
        sc.op("vector", lambda e: e.memset(ones[:], 1.0), writes=[bones])
        sc.op("scalar", lambda e: e.activation(out=cact[:], in_=cT[:], func=AF.Silu), reads=[bmisc], writes=[bmod])

        wslot_i = [0]

        def next_slot():
            i = wslot_i[0] % NSLOT
            wslot_i[0] += 1
            return i

        def load_cols(Wd, c0, ncols, nk=KC):
            i = next_slot()
            src = Wd.rearrange("(k p) n -> p k n", p=128)
            sc.dma("gpsimd", lambda e, i=i: e.dma_start(out=wslot[i][:, 0:nk, 0:ncols], in_=src[:, :, c0:c0 + ncols]),
                   writes=[bws[i]], key=f"ws{i}")
            return i

        mod_ps = ps[:, 7, 0:72]
        for l in layers:
            for v in sorted(need_v[l]):
                for hh in range(4):
                    si = load_cols(W[("w_ada", l)], v * 1024 + hh * 256, 256)
                    for jj in range(2):
                        j = hh * 2 + jj
                        col = v * 8 + j
                        for kc in range(KC):
                            sc.op("tensor",
                                  lambda e, si=si, jj=jj, kc=kc, col=col: e.matmul(
                                      ps[:, 7, col:col + 1], lhsT=wslot[si][:, kc, jj * 128:(jj + 1) * 128],
                                      rhs=cact[:, kc:kc + 1], start=(kc == 0), stop=(kc == KC - 1)),
                                  reads=[bws[si], bmod], writes=[bps[7]], inc=(kc == KC - 1))
            for v in sorted(need_v[l]):
                sc.op("vector", lambda e, l=l, v=v: e.tensor_tensor(out=mod[l][:, v * 8:(v + 1) * 8], in0=ps[:, 7, v * 8:(v + 1) * 8],
                                                                    in1=bada[l][:, v * 8:(v + 1) * 8], op=ALU.add),
                      reads=[bps[7], bmisc], writes=[bmod])
            for (gk, vs, vg, half) in ((("ffn1_g", l), 1, 2, 0.5), (("mix_g", l), 4, None, None),
                                       (("ffn2_g", l), 7, 8, 0.5)):
                if gk in gains:
                    a = self.sb("cA_" + "_".join(map(str, gk)), [128, KC], F32)
                    coefA[gk] = a
                    sc.op("vector", lambda e, a=a, gk=gk, vs=vs, l=l: e.scalar_tensor_tensor(
                        out=a[:], in0=mod[l][:, vs * 8:vs * 8 + 8], scalar=1.0, in1=gains[gk][:],
                        op0=ALU.add, op1=ALU.mult), reads=[bmod, bmisc], writes=[bmod])
                    if vg is not None:
                        g = self.sb("cG_" + "_".join(map(str, gk)), [128, KC], F32)
                        coefG[gk] = g
                        sc.op("vector", lambda e, g=g, vg=vg, l=l: e.tensor_scalar(
                            out=g[:], in0=mod[l][:, vg * 8:vg * 8 + 8], scalar1=0.5, scalar2=None, op0=ALU.mult),
                            reads=[bmod], writes=[bmod])

        psi = [0]

        def next_ps(pool):
            i = pool[psi[0] % len(pool)]
            psi[0] += 1
            return i

        def norm_mod(half, A_ap, sh_ap):
            for tt in range(2):
                t0 = half * 1024 + tt * 512
                ti = t0 // 512
                pb = 6
                for c in range(KC):
                    s = c % 2
                    sc.op("scalar", lambda e, c=c, s=s, t0=t0: e.activation(out=sqb[s][:], in_=x[:, c, t0:t0 + 512],
                                                                            func=AF.Square),
                          reads=[bx[c][ti]], writes=[bsq[s]])
                    sc.op("tensor", lambda e, c=c, s=s: e.matmul(ps[:, pb, :], lhsT=ones[:], rhs=sqb[s][:],
                                                                 start=(c == 0), stop=(c == KC - 1)),
                          reads=[bones, bsq[s]], writes=[bps[pb]])
                sc.op("scalar", lambda e: e.activation(out=tmpA[0][:], in_=ps[:, pb, :], func=AF.Sqrt,
                                                       bias=eps_t[:, 0:1], scale=1.0 / D),
                      reads=[bps[pb], bmisc], writes=[btA[0]])
                sc.op("vector", lambda e: e.reciprocal(out=rstd[:], in_=tmpA[0][:]), reads=[btA[0]], writes=[brstd])
                for c in range(KC):
                    s = c % 2
                    sc.op("vector", lambda e, c=c, s=s, t0=t0: e.scalar_tensor_tensor(
                        out=tmpB[s][:], in0=x[:, c, t0:t0 + 512], scalar=A_ap[:, c:c + 1], in1=rstd[:],
                        op0=ALU.mult, op1=ALU.mult), reads=[bx[c][ti], brstd, bmod], writes=[btB[s]])
                    if sh_ap is not None:
                        sc.op("scalar", lambda e, c=c, s=s, tt=tt: e.activation(
                            out=h[:, c, tt * 512:(tt + 1) * 512], in_=tmpB[s][:], func=AF.Identity,
                            bias=sh_ap[:, c:c + 1], scale=1.0), reads=[btB[s], bmod], writes=[bh[tt]])

        def ffn(half, n, l):
            A = coefA[(n + "_g", l)]
            G = coefG[(n + "_g", l)]
            vsh = 0 if n == "ffn1" else 6
            sh = mod[l][:, vsh * 8:vsh * 8 + 8]
            norm_mod(half, A, sh)
            wdv = W[(n + "_wd", l)].rearrange("(f p) n -> p f n", p=128)
            for i, (f0, f1) in enumerate(WD_PIECES):
                sc.dma("gpsimd", lambda e, f0=f0, f1=f1: e.dma_start(out=wd[:, f0:f1, :], in_=wdv[:, f0:f1, :]),
                       writes=[bwd[i]], key=f"wd{i}")
            groups = [(g * 2, 2) for g in range(11)]
            loaded = {}

            def load_group(gi):
                f0, nf = groups[gi]
                loaded[gi] = (load_cols(W[(n + "_wg", l)], f0 * 128, nf * 128),
                              load_cols(W[(n + "_wu", l)], f0 * 128, nf * 128))
            load_group(0)
            for gi, (f0, nf) in enumerate(groups):
                if gi + 1 < len(groups):
                    load_group(gi + 1)
                sg, su = loaded[gi]
                for fi in range(nf):
                    f = f0 + fi
                    for tt in range(2):
                        pg = next_ps([0, 1])
                        pu = pg + 2
                        for kc in range(KC):
                            sc.op("tensor", lambda e, sg=sg, fi=fi, kc=kc, tt=tt, pg=pg: e.matmul(
                                ps[:, pg, :], lhsT=wslot[sg][:, kc, fi * 128:(fi + 1) * 128],
                                rhs=h[:, kc, tt * 512:(tt + 1) * 512], start=(kc == 0), stop=(kc == KC - 1)),
                                reads=[bws[sg], bh[tt]], writes=[bps[pg]], inc=(kc == KC - 1))
                        for kc in range(KC):
                            sc.op("tensor", lambda e, su=su, fi=fi, kc=kc, tt=tt, pu=pu: e.matmul(
                                ps[:, pu, :], lhsT=wslot[su][:, kc, fi * 128:(fi + 1) * 128],
                                rhs=h[:, kc, tt * 512:(tt + 1) * 512], start=(kc == 0), stop=(kc == KC - 1)),
                                reads=[bws[su], bh[tt]], writes=[bps[pu]], inc=(kc == KC - 1))
                        s = pg
                        sc.op("scalar", lambda e, s=s, pg=pg: e.activation(out=tmpA[s][:], in_=ps[:, pg, :],
                                                                           func=AF.Silu),
                              reads=[bps[pg]], writes=[btA[s]])
                        sc.op("vector", lambda e, s=s, pu=pu, f=f, tt=tt: e.tensor_tensor(
                            out=act[:, f, tt * 512:(tt + 1) * 512], in0=tmpA[s][:], in1=ps[:, pu, :], op=ALU.mult),
                            reads=[btA[s], bps[pu]], writes=[bact[f][tt]])
            for tt in range(2):
                t0 = half * 1024 + tt * 512
                ti = t0 // 512
                for d in range(KC):
                    pd = next_ps([4, 5])
                    for f in range(FC):
                        sc.op("tensor", lambda e, f=f, d=d, tt=tt, pd=pd: e.matmul(
                            ps[:, pd, :], lhsT=wd[:, f, d * 128:(d + 1) * 128], rhs=act[:, f, tt * 512:(tt + 1) * 512],
                            start=(f == 0), stop=(f == FC - 1)),
                            reads=[bwd[wd_piece[f]], bact[f][tt]], writes=[bps[pd]], inc=(f == FC - 1))
                    sc.op("vector", lambda e, d=d, t0=t0, pd=pd: e.scalar_tensor_tensor(
                        out=x[:, d, t0:t0 + 512], in0=ps[:, pd, :], scalar=G[:, d:d + 1], in1=x[:, d, t0:t0 + 512],
                        op0=ALU.mult, op1=ALU.add), reads=[bps[pd], bx[d][ti], bmod], writes=[bx[d][ti]])

        ostage = [self.sb(f"ost{i}", [128, 512], F32) for i in range(2)]
        bost = [Buf() for _ in range(2)]
        osi = [0]

        def uproj(half, l):
            A = coefA[("mix_g", l)]
            sh = mod[l][:, 3 * 8:3 * 8 + 8]
            norm_mod(half, A, sh)
            uT = W[("uT", l)]
            ngr = (DIN + 255) // 256
            loaded = {}

            def load_group(gi):
                c0 = gi * 256
                loaded[gi] = load_cols(W[("w_in", l)], c0, min(256, DIN - c0))
            load_group(0)
            for gi in range(ngr):
                if gi + 1 < ngr:
                    load_group(gi + 1)
                si = loaded[gi]
                c0 = gi * 256
                ncol = min(256, DIN - c0)
                for fi in range((ncol + 127) // 128):
                    m = min(128, ncol - fi * 128)
                    for tt in range(2):
                        t0 = half * 1024 + tt * 512
                        pg = next_ps([0, 1, 2, 3])
                        for kc in range(KC):
                            sc.op("tensor", lambda e, si=si, fi=fi, kc=kc, tt=tt, pg=pg, m=m: e.matmul(
                                ps[0:m, pg, :], lhsT=wslot[si][:, kc, fi * 128:fi * 128 + m],
                                rhs=h[:, kc, tt * 512:(tt + 1) * 512], start=(kc == 0), stop=(kc == KC - 1)),
                                reads=[bws[si], bh[tt]], writes=[bps[pg]], inc=(kc == KC - 1))
                        o = osi[0] % 2
                        osi[0] += 1
                        eng = "scalar" if o == 0 else "vector"
                        if eng == "scalar":
                            sc.op("scalar", lambda e, o=o, pg=pg, m=m: e.copy(out=ostage[o][0:m, :], in_=ps[0:m, pg, :]),
                                  reads=[bps[pg]], writes=[bost[o]])
                        else:
                            sc.op("vector", lambda e, o=o, pg=pg, m=m: e.tensor_copy(out=ostage[o][0:m, :],
                                                                                     in_=ps[0:m, pg, :]),
                                  reads=[bps[pg]], writes=[bost[o]])
                        r0 = c0 + fi * 128
                        sc.dma("sync", lambda e, o=o, m=m, r0=r0, t0=t0: e.dma_start(
                            out=uT[r0:r0 + m, t0:t0 + 512], in_=ostage[o][0:m, :]), reads=[bost[o]] + bU, key=f"ost{o}")

        ystage = [act[:, i * 8:(i + 1) * 8, 0:512] for i in range(2)]
        byst = [[bact[f][0] for f in range(i * 8, (i + 1) * 8)] for i in range(2)]

        def wout(half, l):
            yT = W[("yT", l)].rearrange("(c p) t -> p c t", p=128)
            wsl = [load_cols(W[("w_out", l)], q * 256, 256) for q in range(4)]
            G = mod[l][:, 5 * 8:5 * 8 + 8]
            for tt in range(2):
                t0 = half * 1024 + tt * 512
                ti = t0 // 512
                sc.dma("gpsimd", lambda e, tt=tt, t0=t0: e.dma_start(out=ystage[tt], in_=yT[:, :, t0:t0 + 512]),
                       reads=bY, writes=byst[tt], key=f"yst{tt}")
                for d in range(KC):
                    si = wsl[d // 2]
                    dj = d % 2
                    pd = next_ps([4, 5])
                    for kc in range(KC):
                        sc.op("tensor", lambda e, si=si, dj=dj, kc=kc, tt=tt, pd=pd: e.matmul(
                            ps[:, pd, :], lhsT=wslot[si][:, kc, dj * 128:(dj + 1) * 128], rhs=ystage[tt][:, kc, :],
                            start=(kc == 0), stop=(kc == KC - 1)),
                            reads=[bws[si]] + byst[tt], writes=[bps[pd]], inc=(kc == KC - 1))
                    sc.op("vector", lambda e, d=d, t0=t0, pd=pd: e.scalar_tensor_tensor(
                        out=x[:, d, t0:t0 + 512], in0=ps[:, pd, :], scalar=G[:, d:d + 1], in1=x[:, d, t0:t0 + 512],
                        op0=ALU.mult, op1=ALU.add), reads=[bps[pd], bx[d][ti], bmod], writes=[bx[d][ti]])

        def final(half):
            g = gains[("final_g",)]
            for tt in range(2):
                t0 = half * 1024 + tt * 512
                ti = t0 // 512
                pb = 6
                for c in range(KC):
                    s = c % 2
                    sc.op("scalar", lambda e, c=c, s=s, t0=t0: e.activation(out=sqb[s][:], in_=x[:, c, t0:t0 + 512],
                                                                            func=AF.Square),
                          reads=[bx[c][ti]], writes=[bsq[s]])
                    sc.op("tensor", lambda e, c=c, s=s: e.matmul(ps[:, pb, :], lhsT=ones[:], rhs=sqb[s][:],
                                                                 start=(c == 0), stop=(c == KC - 1)),
                          reads=[bones, bsq[s]], writes=[bps[pb]])
                sc.op("scalar", lambda e: e.activation(out=tmpA[0][:], in_=ps[:, pb, :], func=AF.Sqrt,
                                                       bias=eps_t[:, 0:1], scale=1.0 / D),
                      reads=[bps[pb], bmisc], writes=[btA[0]])
                sc.op("vector", lambda e: e.reciprocal(out=rstd[:], in_=tmpA[0][:]), reads=[btA[0]], writes=[brstd])
                for c in range(KC):
                    sc.op("vector", lambda e, c=c, t0=t0: e.scalar_tensor_tensor(
                        out=x[:, c, t0:t0 + 512], in0=x[:, c, t0:t0 + 512], scalar=g[:, c:c + 1], in1=rstd[:],
                        op0=ALU.mult, op1=ALU.mult), reads=[bx[c][ti], brstd, bmisc], writes=[bx[c][ti]])

        eps_t = self.sb("eps_t", [128, 1], F32)
        sc.op("vector", lambda e: e.memset(eps_t[:], EPS), writes=[bmisc])

        for half in range(NH):
            for s in stages:
                if s[0] in ("ffn1", "ffn2"):
                    ffn(half, s[0], s[1])
                elif s[0] == "uproj":
                    uproj(half, s[1])
                elif s[0] == "wout":
                    wout(half, s[1])
                elif s[0] == "final":
                    final(half)

        allb = []
        if xo_d is not None:
            xo_v = xo_d.rearrange("(c p) t -> p c t", p=128)
            for c in range(KC):
                sc.dma("sync", lambda e, c=c: e.dma_start(out=xo_v[:, c, :], in_=x[:, c, :]), reads=bx[c], key="xo")
                allb += bx[c]
        if fz is not None:
            return allb
        sc.final_wait("sync", allb + bost)

        with nc.Block() as block:
            sc.emit(block)
        es.close()
        return nc


NBLK = 32
MOBA_SLOTS = 16
HALF_BLOCKS = ([b for b in range(NBLK) if b % 4 in (0, 3)], [b for b in range(NBLK) if b % 4 in (1, 2)])
NEG = -30000.0


def moba_emit(P, sc, nu, pfx="", fz=None):
    nc, es = P.nc, P.es
    NQ = MOBA_SLOTS * 256
    if fz is None:
        mq = P.din(pfx + "mq", [nu, 64, NQ])
        mqs = P.din(pfx + "mqs", [nu, 16, NQ])
        mk = P.din(pfx + "mk", [nu, 64, S])
        mks = P.din(pfx + "mks", [nu, 16, S])
        mv = P.din(pfx + "mv", [nu, S, 64])
        yo = P.dout(pfx + "moT", [nu, 64, NQ])
    cq = P.din("ropeq", [nu, 2, 16, NQ])
    ck = P.din("ropek", [2, 16, S])
    pm_d = P.din("pm", [nu, 128, MOBA_SLOTS * NBLK])
    oh_d = P.din("oh", [nu, 128, MOBA_SLOTS * NBLK])
    cm_d = P.din("cm", [nu, 2, 4, 128, 256])
    boh_d = P.din("boh", [32, S])
    id_d = P.din("ident", [128, 128])

    qaug = P.sb(pfx + "qaug", [128, NQ], BF16)
    kaug = P.sb(pfx + "kaug", [128, S], BF16)
    vaug = P.sb(pfx + "vaug", [128, 64, 128], BF16)
    qf = P.sb(pfx + "qf", [64, NQ], F32)
    xt = [P.sb(pfx + f"xt{i}", [64, 1024], F32) for i in range(2)]
    xs = [P.sb(pfx + f"xs{i}", [16, 1024], F32) for i in range(2)]
    ct = [P.sb(pfx + f"ct{i}", [16, 2, 1024], F32) for i in range(2)]
    t16 = P.sb(pfx + "t16", [16, 1024], F32)
    sqf = P.sb(pfx + "sqf", [64, 1024], F32)
    kmean = P.sb(pfx + "kmean", [64, NBLK], F32)
    mx = P.sb(pfx + "mx", [128, 4], F32)
    nbias = P.sb(pfx + "nbias", [128, 1], F32)
    onesf = P.sb(pfx + "onesf", [64, 128], F32)
    ident = P.sb(pfx + "ident_sb", [128, 128], F32)
    pm = P.sb(pfx + "pm_sb", [128, MOBA_SLOTS * NBLK], F32)
    oh = P.sb(pfx + "oh_sb", [128, MOBA_SLOTS * NBLK], F32)
    cm = P.sb(pfx + "cm_sb", [128, 8, 256], F32)
    gs = P.sb(pfx + "gs", [128, NBLK], F32)
    g8 = P.sb(pfx + "g8", [128, 8], F32)
    m1 = P.sb(pfx + "m1", [128, NBLK], F32)
    m2 = P.sb(pfx + "m2", [128, NBLK], F32)
    stm = [P.sb(pfx + f"stm{i}", [128, 256], F32) for i in range(2)]
    pt = [P.sb(pfx + f"pt{i}", [128, 512], BF16) for i in range(4)]
    rec = P.sb(pfx + "rec", [64, 256], F32)
    yst = [P.sb(pfx + f"yst{i}", [64, 256], F32) for i in range(2)]
    ps = es.enter_context(nc.psum_tensor(pfx + "mps", [128, 8, 512], F32)) if fz is None else fz.ps
    if fz is not None:
        vt = P.sb(pfx + "vt", [64, 1024], F32)
        bvt = Buf()

    def rows6(e, u, r0, nr, cols):
        return fz.Urecv[r0:r0 + 5 * 64 + nr, cols][bass.ds(fz.dyn(e, "sync", ("mhr", u)), nr), :]

    bq, bk, bv, bqf = Buf(), Buf(), Buf(), Buf()
    bxt = [Buf(), Buf()]
    bxs = [Buf(), Buf()]
    bct = [Buf(), Buf()]
    bt16, bsqf, bkm, bmx, bnb, bconst, bmask = Buf(), Buf(), Buf(), Buf(), Buf(), Buf(), Buf()
    bgs, bg8, bm1, bm2 = Buf(), Buf(), Buf(), Buf()
    bstm = [Buf(), Buf()]
    bpt = [Buf(), Buf(), Buf(), Buf()]
    brec = Buf()
    byst = [Buf(), Buf()]
    bps = [Buf() for _ in range(8)]

    sc.dma("sync", lambda e: e.dma_start(out=ident[:], in_=id_d[:, :]), writes=[bconst], key=pfx + "mconst")
    sc.op("vector", lambda e: e.memset(onesf[:], 1.0), writes=[bconst])
    sc.op("vector", lambda e: e.memset(kaug[32:64, :], 0.0), writes=[bk])
    sc.op("vector", lambda e: e.memset(kaug[32:33, :], 1.0), writes=[bk])
    sc.dma("gpsimd", lambda e: e.dma_start(out=kaug[0:32, :], in_=boh_d[:, :]), writes=[bk], key=pfx + "mk0")
    sc.op("vector", lambda e: e.memset(qaug[32:64, :], 0.0), writes=[bq])
    sc.op("vector", lambda e: e.memset(vaug[:, :, 64:128], 1.0), writes=[bv])

    cnt = [0]
    for u in range(nu):
        sc.dma("sync", lambda e, u=u: e.dma_start(out=pm[:], in_=pm_d[u]), writes=[bmask], key=pfx + "mmask")
        sc.dma("sync", lambda e, u=u: e.dma_start(out=oh[:], in_=oh_d[u]), writes=[bmask], key=pfx + "mmask")
        sc.dma("sync", lambda e, u=u: e.dma_start(out=cm[:], in_=cm_d[u].rearrange("a k p q -> p (a k) q")),
               writes=[bmask], key=pfx + "mmask")
        if fz is None:
            for k0 in range(0, 64, 16):
                sc.dma("gpsimd", lambda e, u=u, k0=k0: e.dma_start(
                    out=vaug[:, k0:k0 + 16, 0:64], in_=mv[u].rearrange("(k p) d -> p k d", p=128)[:, k0:k0 + 16, :]),
                    writes=[bv], key=pfx + "mv")
        else:
            for c0 in range(0, S, 1024):
                rr, t0 = c0 // TOK, c0 % TOK

                def vsrc(e, u=u, c0=c0):
                    return fz.LV[u, :, c0:c0 + 1024]
                sc.dma("sync", lambda e, vsrc=vsrc: e.dma_start(out=vt[:], in_=vsrc(e)), reads=[fz.bUr], writes=[bvt],
                       key=pfx + "mvt")
                for cj in range(8):
                    sc.op("tensor", lambda e, cj=cj: e.transpose(ps[:, 6, cj * 64:(cj + 1) * 64], vt[:, cj * 128:(cj + 1) * 128],
                                                                 ident[0:64, 0:64]),
                          reads=[bvt, bconst], writes=[bps[6]], inc=(cj == 7))
                k0 = c0 // 128
                sc.op("vector", lambda e, k0=k0: e.tensor_copy(out=vaug[:, k0:k0 + 8, 0:64],
                                                               in_=ps[:, 6, :].rearrange("p (a b) -> p a b", b=64)),
                      reads=[bps[6]], writes=[bv])
        sc.op("vector", lambda e: e.memset(mx[:], 0.0), writes=[bmx])
        srcs_ = ((mk, mks, None, S), (mq, mqs, cq, NQ)) if fz is None else ((None, None, None, S), (None, None, cq, NQ))
        for which, (src, srcs, tab, ncols) in enumerate(srcs_):
            for c0 in range(0, ncols, 1024):
                i = cnt[0] % 2
                cnt[0] += 1
                if fz is None:
                    sc.dma("sync", lambda e, i=i, c0=c0, src=src, u=u: e.dma_start(out=xt[i][:], in_=src[u, :, c0:c0 + 1024]),
                           writes=[bxt[i]], key=pfx + f"mxt{i}")
                    sc.dma("sync", lambda e, i=i, c0=c0, srcs=srcs, u=u: e.dma_start(out=xs[i][:], in_=srcs[u, :, c0:c0 + 1024]),
                           writes=[bxs[i]], key=pfx + f"mxs{i}")
                elif which == 0:
                    rr, t0 = c0 // TOK, c0 % TOK

                    def ksrc(e, ro, nr, u=u, c0=c0):
                        return fz.LK[u, ro:ro + nr, c0:c0 + 1024]
                    sc.dma("sync", lambda e, i=i, ksrc=ksrc: e.dma_start(out=xt[i][:], in_=ksrc(e, 0, 64)),
                           reads=[fz.bUr], writes=[bxt[i]], key=pfx + f"mxt{i}")
                    sc.dma("sync", lambda e, i=i, ksrc=ksrc: e.dma_start(out=xs[i][0:8, :], in_=ksrc(e, 8, 8)),
                           reads=[fz.bUr], writes=[bxs[i]], key=pfx + f"mxs{i}")
                    sc.dma("sync", lambda e, i=i, ksrc=ksrc: e.dma_start(out=xs[i][8:16, :], in_=ksrc(e, 0, 8)),
                           reads=[fz.bUr], writes=[bxs[i]], key=pfx + f"mxs{i}")
                else:
                    def qsrc(e, ro, nr, u=u, c0=c0):
                        return fz.LQ[u, ro:ro + nr, c0:c0 + 1024]
                    sc.dma("sync", lambda e, i=i, qsrc=qsrc: e.dma_start(out=xt[i][:], in_=qsrc(e, 0, 64)),
                           reads=[fz.bUr], writes=[bxt[i]], key=pfx + f"mxt{i}")
                    sc.dma("sync", lambda e, i=i, qsrc=qsrc: e.dma_start(out=xs[i][0:8, :], in_=qsrc(e, 8, 8)),
                           reads=[fz.bUr], writes=[bxs[i]], key=pfx + f"mxs{i}")
                    sc.dma("sync", lambda e, i=i, qsrc=qsrc: e.dma_start(out=xs[i][8:16, :], in_=qsrc(e, 0, 8)),
                           reads=[fz.bUr], writes=[bxs[i]], key=pfx + f"mxs{i}")
                if which == 0:
                    sc.dma("sync", lambda e, i=i, c0=c0: e.dma_start(
                        out=ct[i][:], in_=ck[:, :, c0:c0 + 1024].rearrange("a p t -> p a t")),
                        writes=[bct[i]], key=pfx + f"mct{i}")
                else:
                    sc.dma("sync", lambda e, i=i, c0=c0, u=u: e.dma_start(
                        out=ct[i][:], in_=cq[u, :, :, c0:c0 + 1024].rearrange("a p t -> p a t")),
                        writes=[bct[i]], key=pfx + f"mct{i}")
                sc.op("vector", lambda e, i=i: e.tensor_tensor(out=t16[:], in0=xs[i][:], in1=ct[i][:, 1, :], op=ALU.mult),
                      reads=[bxs[i], bct[i]], writes=[bt16])
                sc.op("vector", lambda e, i=i: e.tensor_tensor(out=xt[i][0:16, :], in0=xt[i][0:16, :], in1=ct[i][:, 0, :],
                                                               op=ALU.mult), reads=[bxt[i], bct[i]], writes=[bxt[i]])
                sc.op("vector", lambda e, i=i: e.tensor_tensor(out=xt[i][0:16, :], in0=xt[i][0:16, :], in1=t16[:],
                                                               op=ALU.add), reads=[bxt[i], bt16], writes=[bxt[i]])
                sc.op("scalar", lambda e, i=i: e.activation(out=sqf[:], in_=xt[i][:], func=AF.Square),
                      reads=[bxt[i]], writes=[bsqf])
                for hh in range(2):
                    sc.op("tensor", lambda e, hh=hh: e.matmul(ps[:, 6, :], lhsT=onesf[:], rhs=sqf[:, hh * 512:(hh + 1) * 512],
                                                              start=True, stop=True), reads=[bconst, bsqf], writes=[bps[6]])
                    sc.op("vector", lambda e, which=which: e.tensor_reduce(out=mx[:, 2:3], in_=ps[:, 6, :], axis=mybir.AxisListType.X,
                                                                           op=ALU.max), reads=[bps[6]], writes=[bmx])
                    sc.op("vector", lambda e, which=which: e.tensor_tensor(out=mx[:, which:which + 1], in0=mx[:, which:which + 1],
                                                                           in1=mx[:, 2:3], op=ALU.max), reads=[bmx], writes=[bmx])
                if which == 0:
                    nb0 = c0 // 256
                    sc.op("vector", lambda e, i=i, nb0=nb0: e.tensor_reduce(
                        out=kmean[:, nb0:nb0 + 4], in_=xt[i][:].rearrange("p (n t) -> p n t", t=256),
                        axis=mybir.AxisListType.X, op=ALU.add), reads=[bxt[i]], writes=[bkm])
                    sc.op("scalar", lambda e, i=i, c0=c0: e.copy(out=kaug[64:128, c0:c0 + 1024], in_=xt[i][:]),
                          reads=[bxt[i]], writes=[bk])
                else:
                    sc.op("scalar", lambda e, i=i, c0=c0: e.mul(out=qaug[64:128, c0:c0 + 1024], in_=xt[i][:], mul=0.125),
                          reads=[bxt[i]], writes=[bq])
                    sc.op("vector", lambda e, i=i, c0=c0: e.tensor_copy(out=qf[:, c0:c0 + 1024], in_=xt[i][:]),
                          reads=[bxt[i]], writes=[bqf])
        sc.op("vector", lambda e: e.tensor_tensor(out=mx[:, 3:4], in0=mx[:, 0:1], in1=mx[:, 1:2], op=ALU.mult),
              reads=[bmx], writes=[bmx])
        sc.op("scalar", lambda e: e.activation(out=mx[:, 3:4], in_=mx[:, 3:4], func=AF.Sqrt), reads=[bmx], writes=[bmx])
        sc.op("vector", lambda e: e.tensor_scalar(out=nbias[:], in0=mx[:, 3:4], scalar1=-0.125, scalar2=None, op0=ALU.mult),
              reads=[bmx], writes=[bnb])
        for t in range(NQ // 128):
            r = t // 2
            sc.op("tensor", lambda e, t=t: e.matmul(ps[:, 7, 0:NBLK], lhsT=qf[:, t * 128:(t + 1) * 128], rhs=kmean[:],
                                                    start=True, stop=True), reads=[bqf, bkm], writes=[bps[7]])
            sc.op("vector", lambda e, r=r: e.tensor_tensor(out=gs[:], in0=ps[:, 7, 0:NBLK], in1=pm[:, r * NBLK:(r + 1) * NBLK],
                                                           op=ALU.add), reads=[bps[7], bmask], writes=[bgs])
            sc.op("vector", lambda e: e.max(out=g8[:], in_=gs[:]), reads=[bgs], writes=[bg8])
            sc.op("vector", lambda e: e.tensor_scalar(out=m1[:], in0=gs[:], scalar1=g8[:, 2:3], scalar2=None, op0=ALU.is_ge),
                  reads=[bgs, bg8], writes=[bm1])
            sc.op("vector", lambda e: e.tensor_scalar(out=m2[:], in0=gs[:], scalar1=-1e29, scalar2=None, op0=ALU.is_gt),
                  reads=[bgs], writes=[bm2])
            sc.op("vector", lambda e: e.tensor_tensor(out=m1[:], in0=m1[:], in1=m2[:], op=ALU.mult),
                  reads=[bm1, bm2], writes=[bm1])
            sc.op("vector", lambda e, r=r: e.tensor_tensor(out=m1[:], in0=m1[:], in1=oh[:, r * NBLK:(r + 1) * NBLK], op=ALU.add),
                  reads=[bm1, bmask], writes=[bm1])
            sc.op("vector", lambda e: e.tensor_scalar(out=m2[:], in0=m1[:], scalar1=-1.0, scalar2=-NEG, op0=ALU.add, op1=ALU.mult),
                  reads=[bm1], writes=[bm2])
            sc.op("tensor", lambda e: e.transpose(ps[0:NBLK, 7, 128:256], m2[:], ident[:]),
                  reads=[bm2, bconst], writes=[bps[7]])
            sc.op("vector", lambda e, t=t: e.tensor_copy(out=qaug[0:32, t * 128:(t + 1) * 128], in_=ps[0:NBLK, 7, 128:256]),
                  reads=[bps[7]], writes=[bq])
        tasks = []
        for m in range(MOBA_SLOTS // 2):
            KT0, KT1 = 8 * m + 4, 8 * m + 8
            tasks += [(m, kt, kt < KT0) for kt in range(KT1)]
        NB, DEPTH = 4, 3

        def qk(i):
            m, kt, wide = tasks[i]
            r0 = 2 * m
            KT0, KT1 = 8 * m + 4, 8 * m + 8
            p = i % NB
            c0 = 0 if wide else 256
            sc.op("tensor", lambda e, kt=kt, r0=r0, p=p, c0=c0: e.matmul(
                ps[:, p, c0:512], lhsT=kaug[:, kt * 128:(kt + 1) * 128], rhs=qaug[:, r0 * 256 + c0:r0 * 256 + 512],
                start=True, stop=True), reads=[bk, bq], writes=[bps[p]])
            if wide and kt < KT0 - 4:
                sc.op("scalar", lambda e, p=p: e.activation(out=pt[p][:], in_=ps[:, p, :], func=AF.Exp, bias=nbias[:, 0:1],
                                                            scale=1.0), reads=[bps[p], bnb], writes=[bpt[p]])
                return
            j = (kt - (KT0 - 4)) if wide else (kt - (KT1 - 4) + 4)
            mc = 0 if wide else 256
            s = j % 2
            sc.op("vector", lambda e, p=p, j=j, s=s, mc=mc: e.tensor_tensor(out=stm[s][:], in0=ps[:, p, mc:mc + 256], in1=cm[:, j, :],
                                                                           op=ALU.add), reads=[bps[p], bmask], writes=[bstm[s]])
            sc.op("scalar", lambda e, p=p, s=s, mc=mc: e.activation(out=pt[p][:, mc:mc + 256], in_=stm[s][:], func=AF.Exp,
                                                                    bias=nbias[:, 0:1], scale=1.0),
                  reads=[bstm[s], bnb], writes=[bpt[p]])
            if wide:
                sc.op("scalar", lambda e, p=p: e.activation(out=pt[p][:, 256:512], in_=ps[:, p, 256:512], func=AF.Exp,
                                                            bias=nbias[:, 0:1], scale=1.0), reads=[bps[p], bnb], writes=[bpt[p]])

        def pv(i):
            m, kt, wide = tasks[i]
            KT1 = 8 * m + 8
            p = i % NB
            po = 4 + (m % 2)
            c0 = 0 if wide else 256
            sc.op("tensor", lambda e, kt=kt, p=p, po=po, c0=c0, KT1=KT1: e.matmul(
                ps[:, po, c0:512], lhsT=vaug[:, kt, :], rhs=pt[p][:, c0:512], start=(kt == 0), stop=(kt == KT1 - 1)),
                reads=[bv, bpt[p]], writes=[bps[po]])
            if kt < KT1 - 1:
                return
            for h2 in range(2):
                r = 2 * m + h2
                cs = slice(h2 * 256, (h2 + 1) * 256)
                sc.op("vector", lambda e, po=po, cs=cs: e.reciprocal(out=rec[:], in_=ps[64:128, po, cs]), reads=[bps[po]], writes=[brec])
                ys = r % 2
                sc.op("vector", lambda e, po=po, ys=ys, cs=cs: e.tensor_tensor(out=yst[ys][:], in0=ps[0:64, po, cs], in1=rec[:], op=ALU.mult),
                      reads=[bps[po], brec], writes=[byst[ys]])
                if fz is None:
                    sc.dma("sync", lambda e, u=u, r=r, ys=ys: e.dma_start(out=yo[u, :, r * 256:(r + 1) * 256], in_=yst[ys][:]),
                           reads=[byst[ys]], key=pfx + f"myo{ys}")
                else:
                    row0 = (r // 4) * 512 + u * 64
                    sc.dma("sync", lambda e, row0=row0, r=r, ys=ys: e.dma_start(
                        out=fz.Ysend[row0:row0 + 64, (r % 4) * 256:(r % 4 + 1) * 256], in_=yst[ys][:]),
                        reads=[byst[ys], fz.bYs], key=pfx + f"myo{ys}")

        for i in range(min(DEPTH, len(tasks))):
            qk(i)
        for i in range(len(tasks)):
            if i + DEPTH < len(tasks):
                qk(i + DEPTH)
            pv(i)
    return byst


class SimpleProg:
    def __init__(self):
        self.fused = None
        self.nc = bass.Bass("TRN2", target_bir_lowering=False)
        self.es = ExitStack()
        self.in_names = []
        self.out_names = []

    din = TokProg.din
    dout = TokProg.dout
    sb = TokProg.sb

    def finish(self, sc, outbufs):
        sc.final_wait("sync", outbufs)
        with self.nc.Block() as block:
            sc.emit(block)
        self.es.close()
        return self.nc


def build_moba(nu=3):
    P = SimpleProg()
    sc = Sched(P.nc, P.es)
    ob = moba_emit(P, sc, nu)
    return P, P.finish(sc, ob)


def rope_tables():
    inv = np.exp(np.float32(-np.log(500000.0)) * np.arange(0, 16, 2, dtype=np.float32) / np.float32(16)).astype(np.float32)
    ang = (np.arange(S, dtype=np.float32)[:, None] * inv[None, :]).astype(np.float32)
    cos = np.cos(ang).astype(np.float32).T
    sin = np.sin(ang).astype(np.float32).T
    tab = np.zeros((2, 16, S), np.float32)
    tab[0, 0:8] = cos
    tab[0, 8:16] = cos
    tab[1, 0:8] = -sin
    tab[1, 8:16] = sin
    return tab


def moba_unit_inputs(uq, uk, uv, half, tab):
    blocks = HALF_BLOCKS[half]
    qpos = np.concatenate([np.arange(b * 256, (b + 1) * 256) for b in blocks])
    qT = np.ascontiguousarray(uq[qpos].T)
    kT = np.ascontiguousarray(uk.T)
    sw = np.r_[8:16, 0:8]
    return dict(mq=qT, mqs=np.ascontiguousarray(qT[sw]), mk=kT, mks=np.ascontiguousarray(kT[sw]), mv=np.ascontiguousarray(uv),
                ropeq=np.ascontiguousarray(tab[:, :, qpos]))


def moba_const_inputs(half):
    blocks = HALF_BLOCKS[half]
    pm = np.zeros((MOBA_SLOTS, NBLK), np.float32)
    oh = np.zeros((MOBA_SLOTS, NBLK), np.float32)
    for r, b in enumerate(blocks):
        pm[r, b:] = -1e30
        oh[r, b] = 1.0
    kk = np.arange(128)[:, None]
    qq = np.arange(256)[None, :]
    M0 = np.where(kk <= qq, 0.0, NEG).astype(np.float32)
    M1 = np.where(kk + 128 <= qq, 0.0, NEG).astype(np.float32)
    Z = np.zeros((128, 256), np.float32)
    cm = np.zeros((2, 4, 128, 256), np.float32)
    for par in range(2):
        r = par
        b = blocks[r]
        if b == 2 * r + 1:
            cm[par] = np.stack([Z, Z, M0, M1])
        else:
            cm[par] = np.stack([M0, M1, Z, Z])
    pmb = np.ascontiguousarray(np.broadcast_to(pm.reshape(1, -1), (128, MOBA_SLOTS * NBLK)))
    ohb = np.ascontiguousarray(np.broadcast_to(oh.reshape(1, -1), (128, MOBA_SLOTS * NBLK)))
    return dict(pm=pmb, oh=ohb, cm=cm)


def moba_shared_inputs(tab):
    boh = np.zeros((32, S), np.float32)
    for n in range(32):
        boh[n, n * 256:(n + 1) * 256] = 1.0
    return dict(ropek=tab, boh=boh, ident=np.eye(128, dtype=np.float32))


CH = 32


def conv_emit(P, sc, pfx="", fz=None):
    nc, es = P.nc, P.es
    T = TOK
    if fz is None:
        uc = P.din(pfx + "uc", [512, T + CH])
        yc = P.dout(pfx + "ycT", [256, T])
    else:
        yc = fz.Yfull
        flag_d = P.din("cflag", [128, 1])
        flag = P.sb(pfx + "cflag_sb", [128, 1], F32)
    cw = P.din(pfx + "cw", [128, 2, 31])
    cp = P.din(pfx + "cp", [128, 2, 3])
    idb = P.din("cident", [128, 128])
    a_t = [P.sb(pfx + f"ca{c}", [128, T + CH], F32) for c in range(2)]
    g_t = [P.sb(pfx + f"cg{c}", [128, T + CH], F32) for c in range(2)]
    hg = [P.sb(pfx + f"chg{c}", [128, T + CH], BF16) for c in range(2)]
    dg = [P.sb(pfx + f"cdg{c}", [128, 31, 128], BF16) for c in range(2)]
    cws = P.sb(pfx + "cws", [128, 2, 31], F32)
    cps = P.sb(pfx + "cps", [128, 2, 3], F32)
    idt = P.sb(pfx + "cidt", [128, 128], F32)
    onesf = P.sb(pfx + "cones", [128, 128], F32)
    epsc = P.sb(pfx + "ceps", [128, 1], F32)
    hc = [P.sb(pfx + f"chc{c}", [128, 512], F32) for c in range(2)]
    sq = [P.sb(pfx + f"csq{c}", [128, 512], F32) for c in range(2)]
    mean = P.sb(pfx + "cmean", [128, 512], F32)
    msq = P.sb(pfx + "cmsq", [128, 512], F32)
    var = P.sb(pfx + "cvar", [128, 512], F32)
    rstd = P.sb(pfx + "crstd", [128, 512], F32)
    tt_ = [P.sb(pfx + f"ctt{c}", [128, 512], F32) for c in range(2)]
    yo = [P.sb(pfx + f"cyo{c}", [128, 512], F32) for c in range(2)]
    ps = es.enter_context(nc.psum_tensor(pfx + "cps_", [128, 4, 512], F32)) if fz is None else fz.ps
    ba = [Buf(), Buf()]
    bg = [Buf(), Buf()]
    bhg = [Buf(), Buf()]
    bdg = [Buf(), Buf()]
    bc, bhc, bsq = Buf(), [Buf(), Buf()], [Buf(), Buf()]
    bmean, bmsq, bvar, brstd = Buf(), Buf(), Buf(), Buf()
    btt = [Buf(), Buf()]
    byo = [Buf(), Buf()]
    bps = [Buf() for _ in range(4)]

    sc.dma("sync", lambda e: e.dma_start(out=cws[:], in_=cw[:, :, :]), writes=[bc], key=pfx + "cc")
    sc.dma("sync", lambda e: e.dma_start(out=cps[:], in_=cp[:, :, :]), writes=[bc], key=pfx + "cc")
    sc.dma("sync", lambda e: e.dma_start(out=idt[:], in_=idb[:, :]), writes=[bc], key=pfx + "cc")
    sc.op("vector", lambda e: e.memset(onesf[:], 1.0), writes=[bc])
    sc.op("vector", lambda e: e.memset(epsc[:], EPS), writes=[bc])
    if fz is not None:
        sc.dma("sync", lambda e: e.dma_start(out=flag[:], in_=flag_d[:, :]), writes=[bc], key=pfx + "cc")

    def prev_rows(e, row0):
        return fz.LH[row0:row0 + 128, :]

    for c in range(2):
        if fz is None:
            sc.dma("sync", lambda e, c=c: e.dma_start(out=a_t[c][:], in_=uc[c * 128:(c + 1) * 128, :]), writes=[ba[c]],
                   key=pfx + f"ca{c}")
            sc.dma("sync", lambda e, c=c: e.dma_start(out=g_t[c][:], in_=uc[256 + c * 128:256 + (c + 1) * 128, :]),
                   writes=[bg[c]], key=pfx + f"cg{c}")
        else:
            sc.dma("sync", lambda e, c=c: e.dma_start(out=a_t[c][:, CH:], in_=fz.Usend[c * 128:(c + 1) * 128, :]),
                   reads=[fz.bU], writes=[ba[c]], key=pfx + f"ca{c}")
            sc.dma("sync", lambda e, c=c: e.dma_start(out=a_t[c][:, 0:CH], in_=prev_rows(e, c * 128)),
                   reads=[fz.bUr], writes=[ba[c]], key=pfx + f"ca{c}")
            sc.dma("sync", lambda e, c=c: e.dma_start(out=g_t[c][:, CH:], in_=fz.Usend[256 + c * 128:256 + (c + 1) * 128, :]),
                   reads=[fz.bU], writes=[bg[c]], key=pfx + f"cg{c}")
            sc.dma("sync", lambda e, c=c: e.dma_start(out=g_t[c][:, 0:CH], in_=prev_rows(e, 256 + c * 128)),
                   reads=[fz.bUr], writes=[bg[c]], key=pfx + f"cg{c}")
        sc.op("scalar", lambda e, c=c: e.activation(out=g_t[c][:], in_=g_t[c][:], func=AF.Sigmoid),
              reads=[bg[c]], writes=[bg[c]])
        sc.op("vector", lambda e, c=c: e.tensor_tensor(out=hg[c][:], in0=a_t[c][:], in1=g_t[c][:], op=ALU.mult),
              reads=[ba[c], bg[c]], writes=[bhg[c]])
        if fz is not None:
            sc.op("vector", lambda e, c=c: e.tensor_scalar(out=hg[c][:, 0:CH], in0=hg[c][:, 0:CH], scalar1=flag[:, 0:1],
                                                           scalar2=None, op0=ALU.mult), reads=[bhg[c], bc], writes=[bhg[c]])
        for k in range(31):
            sc.op("gpsimd", lambda e, c=c, k=k: e.tensor_scalar(out=dg[c][:, k, :], in0=idt[:], scalar1=cws[:, c, k:k + 1],
                                                                scalar2=None, op0=ALU.mult), reads=[bc], writes=[bdg[c]])
    for tt in range(T // 512):
        for c in range(2):
            for k in range(31):
                o = tt * 512 + 2 + k
                sc.op("tensor", lambda e, c=c, k=k, o=o: e.matmul(ps[:, c, :], lhsT=dg[c][:, k, :], rhs=hg[c][:, o:o + 512],
                                                                 start=(k == 0), stop=(k == 30)),
                      reads=[bdg[c], bhg[c]], writes=[bps[c]], inc=(k == 30))
            sc.op("scalar", lambda e, c=c: e.activation(out=hc[c][:], in_=ps[:, c, :], func=AF.Identity, bias=cps[:, c, 0:1],
                                                        scale=1.0), reads=[bps[c], bc], writes=[bhc[c]])
            sc.op("scalar", lambda e, c=c: e.activation(out=sq[c][:], in_=hc[c][:], func=AF.Square), reads=[bhc[c]],
                  writes=[bsq[c]])
        for c in range(2):
            sc.op("tensor", lambda e, c=c: e.matmul(ps[:, 2, :], lhsT=onesf[:], rhs=hc[c][:], start=(c == 0), stop=(c == 1)),
                  reads=[bc, bhc[c]], writes=[bps[2]])
        for c in range(2):
            sc.op("tensor", lambda e, c=c: e.matmul(ps[:, 3, :], lhsT=onesf[:], rhs=sq[c][:], start=(c == 0), stop=(c == 1)),
                  reads=[bc, bsq[c]], writes=[bps[3]])
        sc.op("vector", lambda e: e.tensor_scalar(out=mean[:], in0=ps[:, 2, :], scalar1=1.0 / 256, scalar2=None, op0=ALU.mult),
              reads=[bps[2]], writes=[bmean])
        sc.op("vector", lambda e: e.tensor_tensor(out=msq[:], in0=mean[:], in1=mean[:], op=ALU.mult), reads=[bmean], writes=[bmsq])
        sc.op("vector", lambda e: e.scalar_tensor_tensor(out=var[:], in0=ps[:, 3, :], scalar=1.0 / 256, in1=msq[:],
                                                         op0=ALU.mult, op1=ALU.subtract), reads=[bps[3], bmsq], writes=[bvar])
        sc.op("scalar", lambda e: e.activation(out=var[:], in_=var[:], func=AF.Sqrt, bias=epsc[:, 0:1], scale=1.0),
              reads=[bvar, bc], writes=[bvar])
        sc.op("vector", lambda e: e.reciprocal(out=rstd[:], in_=var[:]), reads=[bvar], writes=[brstd])
        for c in range(2):
            sc.op("vector", lambda e, c=c: e.tensor_tensor(out=tt_[c][:], in0=hc[c][:], in1=mean[:], op=ALU.subtract),
                  reads=[bhc[c], bmean], writes=[btt[c]])
            sc.op("vector", lambda e, c=c: e.tensor_tensor(out=tt_[c][:], in0=tt_[c][:], in1=rstd[:], op=ALU.mult),
                  reads=[btt[c], brstd], writes=[btt[c]])
            sc.op("scalar", lambda e, c=c: e.activation(out=yo[c][:], in_=tt_[c][:], func=AF.Silu, bias=cps[:, c, 2:3],
                                                        scale=cps[:, c, 1:2]), reads=[btt[c], bc], writes=[byo[c]])
            sc.dma("sync", lambda e, c=c, tt=tt: e.dma_start(out=yc[c * 128:(c + 1) * 128, tt * 512:(tt + 1) * 512], in_=yo[c][:]),
                   reads=[byo[c]] + ([] if fz is None else [fz.bY]), key=pfx + f"cyo{c}")
    return byo


def build_conv():
    P = SimpleProg()
    sc = Sched(P.nc, P.es)
    ob = conv_emit(P, sc)
    return P, P.finish(sc, ob)


def conv_inputs(u_b, j, conv_w, conv_b, ln_g, ln_b):
    t0 = j * TOK
    uc = np.zeros((512, TOK + CH), np.float32)
    lo = max(0, t0 - CH)
    uc[:, CH - (t0 - lo):] = u_b[lo:t0 + TOK, 0:512].T
    lay = lambda v: np.ascontiguousarray(v.reshape(2, 128).T)
    cw = np.ascontiguousarray(conv_w.T.reshape(2, 128, 31).transpose(1, 0, 2))
    cp = np.ascontiguousarray(np.stack([lay(conv_b), lay(ln_g), lay(ln_b)], axis=-1))
    return dict(uc=uc, cw=cw, cp=cp, cident=np.eye(128, dtype=np.float32))


GC = 64
NCH = S // GC
GSEG = 16
AX = mybir.AxisListType


def gdn_emit(P, sc, nu, pfx="", fz=None):
    import os
    STOP = float(os.environ.get("GDN_STOP", "99"))
    nc, es = P.nc, P.es
    if fz is None:
        raw_d = P.din(pfx + "graw", [nu, 3, 64, S + 3])
        gz_d = P.din(pfx + "gz", [nu, S, 64])
        ga_d = P.din(pfx + "ga", [nu, 64, NCH])
        gb_d = P.din(pfx + "gb", [nu, 64, NCH])
        go_d = P.dout(pfx + "go", [nu, S, 64])
    gcw_d = P.din(pfx + "gcw", [nu, 64, 12])
    gpar_d = P.din(pfx + "gpar", [nu, 64, 2])
    gng_d = P.din(pfx + "gng", [nu, 64, 64])
    gcst_d = P.din("gcst", [3, 64, 64])

    def unit_h(e, u):
        return fz.dyn(e, "gpsimd", ("gh", u))

    f = lambda n, shp: P.sb(pfx + n, shp, F32)
    cst = f("gcst_sb", [64, 3, 64])
    TriB = f("gTriB", [64, 8, 64])
    MB = f("gMB", [64, 8, 64])
    IB = f("gIB", [64, 8, 64])
    ones64 = f("gones", [64, 64])
    epsg = f("geps", [64, 1])
    bcst = Buf()
    sc.dma("sync", lambda e: e.dma_start(out=cst[:], in_=gcst_d.rearrange("a p q -> p a q")), writes=[bcst], key=pfx + "gc")
    sc.op("vector", lambda e: e.memset(ones64[:], 1.0), writes=[bcst])
    sc.op("vector", lambda e: e.memset(epsg[:], EPS), writes=[bcst])
    for j in range(8):
        sc.op("vector", lambda e, j=j: e.tensor_copy(out=TriB[:, j, :], in_=cst[:, 0, :]), reads=[bcst], writes=[bcst])
        sc.op("vector", lambda e, j=j: e.tensor_copy(out=MB[:, j, :], in_=cst[:, 1, :]), reads=[bcst], writes=[bcst])
        sc.op("vector", lambda e, j=j: e.tensor_copy(out=IB[:, j, :], in_=cst[:, 2, :]), reads=[bcst], writes=[bcst])
    Tri = cst[:, 0, :]
    I64 = cst[:, 2, :]

    def bc_n(t, n0):
        return t[:, n0:n0 + 8].unsqueeze(2).to_broadcast([64, 8, 64])


    def emit_unit(u, sc):
        u2 = u % 2
        GB, SB0 = 4 * u2, 4 * u2 + 3
        f = lambda n, shp: P.sb(pfx + f"u{u}_" + n, shp, F32)
        gcw = f("gcw_sb", [64, 12])
        dgw = P.sb(pfx + f"u{u}_" + "gdgw", [64, 12, 64], BF16)
        par = f("gpar_sb", [64, 2])
        negA = f("gnegA", [64, 1])
        ngb = f("gngb", [64, 64])
        a_t = f("ga_sb", [64, NCH])
        b_t = f("gb_sb", [64, NCH])
        g_t = f("gg", [64, NCH])
        beta = f("gbeta", [64, NCH])
        gc = f("ggc", [64, NCH])
        egc = f("gegc", [64, NCH])
        eglb = f("geglb", [64, NCH])
        edec = f("gedec", [64, NCH])
        bgk = f("gbgk", [64, NCH])
        SEGT = S // GSEG
        SEGC = NCH // GSEG
        raw = [P.sb(pfx + f"u{u}_" + f"graw{i}", [64, 515], BF16) for i in range(2)]
        xa = [f(f"gxa{i}", [64, 512]) for i in range(2)]
        xq = f("gxq", [64, 512])
        rn = f("grn", [64, 512])
        qnT = f("gqnT", [64, SEGT])
        knT = f("gknT", [64, SEGT])
        Kt = f("gKt", [64, SEGC, 64])
        Vt = f("gVt", [64, SEGC, 64])
        oseg = f("goseg", [64, SEGC, 64])
        zseg = f("gzseg", [64, SEGC, 64])
        osq = f("gosq", [64, SEGC, 64])
        oss = f("goss", [64, SEGC])
        rhsD = f("grhsD", [64, 8, 64])
        ED = f("gED", [64, 8, 64])
        EDT = f("gEDT", [64, 8, 64])
        Lp = [f(f"gL{i}", [64, 8, 64]) for i in range(2)]
        Np = [f(f"gN{i}", [64, 8, 64]) for i in range(2)]
        Pm = f("gP", [64, 8, 64])
        Lb = [P.sb(pfx + f"u{u}_" + f"gLb{i}", [64, 8, 64], BF16) for i in range(2)]
        Nb = [P.sb(pfx + f"u{u}_" + f"gNb{i}", [64, 8, 64], BF16) for i in range(2)]
        Pb = P.sb(pfx + f"u{u}_" + "gPb", [64, 8, 64], BF16)
        bLb, bNb, bPb = [Buf(), Buf()], [Buf(), Buf()], Buf()
        Kbg = f("gKbg", [64, 8, 64])
        Vb = f("gVb", [64, 8, 64])
        kdec = f("gkdec", [64, 8, 64])
        u_sb = f("gu", [64, 8, 64])
        wT = f("gwT", [64, 8, 64])
        qkT = f("gqkT", [64, 8, 64])
        St = f("gS", [64, 64])
        vn = [f(f"gvn{i}", [64, 64]) for i in range(2)]
        As = [f(f"gAs{i}", [64, 64]) for i in range(2)]
        ps = es.enter_context(nc.psum_tensor(pfx + "gps", [64, 8, 512], F32)) if fz is None else fz.ps[0:64, :, :]

        B_ = lambda: Buf()
        bpar, bg = B_(), B_()
        braw = [B_(), B_()]
        bxa = [B_(), B_()]
        bxq, brn, bqn, bkn, bKt, bVt, boseg, bz, bosq, boss = (B_() for _ in range(10))
        brhsD, bED, bEDT, bP, bKbg, bVb, bkdec, bu, bwT, bqkT, bS = (B_() for _ in range(11))
        bL = [B_(), B_()]
        bN = [B_(), B_()]
        bvn = [B_(), B_()]
        bAs = [B_(), B_()]
        bps = [B_() for _ in range(8)]
        wk = [0]

        def nps():
            i = GB + wk[0] % 3
            wk[0] += 1
            return i

        sc.dma("sync", lambda e, u=u: e.dma_start(out=gcw[:], in_=gcw_d[u]), writes=[bpar], key=f"gu{u}" + "gp")
        sc.dma("sync", lambda e, u=u: e.dma_start(out=par[:], in_=gpar_d[u]), writes=[bpar], key=f"gu{u}" + "gp")
        sc.dma("sync", lambda e, u=u: e.dma_start(out=ngb[:], in_=gng_d[u]), writes=[bpar], key=f"gu{u}" + "gp")
        if fz is None:
            sc.dma("sync", lambda e, u=u: e.dma_start(out=a_t[:], in_=ga_d[u]), writes=[bpar], key=f"gu{u}" + "gp")
            sc.dma("sync", lambda e, u=u: e.dma_start(out=b_t[:], in_=gb_d[u]), writes=[bpar], key=f"gu{u}" + "gp")
        else:
            for rr in range(4):
                for (dst, ro) in ((a_t, 3200), (b_t, 3206)):
                    def absrc(e, u=u, rr=rr, ro=ro):
                        return fz.LAB[u, (0 if ro == 3200 else 1):(1 if ro == 3200 else 2), rr * TOK:(rr + 1) * TOK].rearrange(
                            "o (n s) -> s (o n)", s=64)
                    sc.dma("gpsimd", lambda e, dst=dst, rr=rr, absrc=absrc: e.dma_start(
                        out=dst[:, rr * 32:(rr + 1) * 32], in_=absrc(e), allow_slow_non_contiguous=True),
                        reads=[fz.bUr], writes=[bpar], key=f"gu{u}" + "gp")
        for k in range(12):
            sc.op("gpsimd", lambda e, k=k: e.tensor_scalar(out=dgw[:, k, :], in0=I64, scalar1=gcw[:, k:k + 1], scalar2=None,
                                                           op0=ALU.mult), reads=[bcst, bpar], writes=[bpar])
        sc.op("scalar", lambda e: e.activation(out=negA[:], in_=par[:, 0:1], func=AF.Exp), reads=[bpar], writes=[bg])
        sc.op("vector", lambda e: e.tensor_scalar(out=negA[:], in0=negA[:], scalar1=-1.0, scalar2=None, op0=ALU.mult),
              reads=[bg], writes=[bg])
        sc.op("scalar", lambda e: e.activation(out=g_t[:], in_=a_t[:], func=AF.Exp, bias=par[:, 1:2], scale=1.0),
              reads=[bpar, bg], writes=[bg])
        sc.op("scalar", lambda e: e.activation(out=g_t[:], in_=g_t[:], func=AF.Ln, bias=1.0, scale=1.0), reads=[bg], writes=[bg])
        sc.op("vector", lambda e: e.tensor_scalar(out=g_t[:], in0=g_t[:], scalar1=negA[:, 0:1], scalar2=None, op0=ALU.mult),
              reads=[bg], writes=[bg])
        sc.op("scalar", lambda e: e.activation(out=beta[:], in_=b_t[:], func=AF.Sigmoid), reads=[bpar, bg], writes=[bg])
        sc.op("tensor", lambda e: e.matmul(ps[:, GB, 0:NCH], lhsT=Tri, rhs=g_t[:], start=True, stop=True),
              reads=[bcst, bg], writes=[bps[GB]])
        sc.op("tensor", lambda e: e.matmul(ps[:, GB, NCH:2 * NCH], lhsT=ones64[:], rhs=g_t[:], start=True, stop=True),
              reads=[bcst, bg], writes=[bps[GB]])
        sc.op("vector", lambda e: e.tensor_copy(out=gc[:], in_=ps[:, GB, 0:NCH]), reads=[bps[GB], bg], writes=[bg])
        sc.op("vector", lambda e: e.tensor_copy(out=eglb[:], in_=ps[:, GB, NCH:2 * NCH]), reads=[bps[GB], bg], writes=[bg])
        sc.op("vector", lambda e: e.tensor_tensor(out=edec[:], in0=eglb[:], in1=gc[:], op=ALU.subtract), reads=[bg], writes=[bg])
        sc.op("scalar", lambda e: e.activation(out=egc[:], in_=gc[:], func=AF.Exp), reads=[bg], writes=[bg])
        sc.op("scalar", lambda e: e.activation(out=eglb[:], in_=eglb[:], func=AF.Exp), reads=[bg], writes=[bg])
        sc.op("scalar", lambda e: e.activation(out=edec[:], in_=edec[:], func=AF.Exp), reads=[bg], writes=[bg])
        sc.op("vector", lambda e: e.tensor_tensor(out=bgk[:], in0=beta[:], in1=egc[:], op=ALU.mult), reads=[bg], writes=[bg])
        sc.op("vector", lambda e: e.memset(St[:], 0.0), writes=[bS])
        if STOP <= 1:
            return [bS]

        for seg in range(GSEG):
            for tt in range(SEGT // 512):
                c0 = seg * SEGT + tt * 512
                for j in range(3):
                    ri = (tt * 3 + j) % 2
                    if fz is None:
                        sc.dma("gpsimd", lambda e, u=u, j=j, ri=ri, c0=c0: e.dma_start(out=raw[ri][:], in_=raw_d[u, j, :, c0:c0 + 515]),
                               writes=[braw[ri]], key=f"gu{u}" + f"graw{ri}")
                    else:
                        rr, t0 = c0 // TOK, c0 % TOK

                        def rsrc(e, rr_, ta, tb, u=u, j=j):
                            return fz.LG[u, j, :, rr_ * TOK + ta:rr_ * TOK + tb]
                        sc.dma("gpsimd", lambda e, ri=ri, rr=rr, t0=t0, rsrc=rsrc: e.dma_start(out=raw[ri][:, 3:515],
                                                                                            in_=rsrc(e, rr, t0, t0 + 512)),
                               reads=[fz.bUr], writes=[braw[ri]], key=f"gu{u}" + f"graw{ri}")
                        if t0 >= 3:
                            sc.dma("gpsimd", lambda e, ri=ri, rr=rr, t0=t0, rsrc=rsrc: e.dma_start(out=raw[ri][:, 0:3],
                                                                                                in_=rsrc(e, rr, t0 - 3, t0)),
                                   reads=[fz.bUr], writes=[braw[ri]], key=f"gu{u}" + f"graw{ri}")
                        elif rr > 0:
                            sc.dma("gpsimd", lambda e, ri=ri, rr=rr, rsrc=rsrc: e.dma_start(out=raw[ri][:, 0:3],
                                                                                         in_=rsrc(e, rr - 1, TOK - 3, TOK)),
                                   reads=[fz.bUr], writes=[braw[ri]], key=f"gu{u}" + f"graw{ri}")
                        else:
                            sc.op("vector", lambda e, ri=ri: e.memset(raw[ri][:, 0:3], 0.0), writes=[braw[ri]])
                    p1 = nps()
                    for k in range(4):
                        sc.op("tensor", lambda e, j=j, k=k, ri=ri, p1=p1: e.matmul(ps[:, p1, :], lhsT=dgw[:, j * 4 + k, :],
                                                                                 rhs=raw[ri][:, k:k + 512], start=(k == 0), stop=(k == 3)),
                              reads=[bpar, braw[ri]], writes=[bps[p1]], inc=(k == 3))
                    xi = j % 2
                    sc.op("scalar", lambda e, xi=xi, p1=p1: e.activation(out=xa[xi][:], in_=ps[:, p1, :], func=AF.Silu),
                          reads=[bps[p1]], writes=[bxa[xi]])
                    if j < 2:
                        sc.op("scalar", lambda e, xi=xi: e.activation(out=xq[:], in_=xa[xi][:], func=AF.Square),
                              reads=[bxa[xi]], writes=[bxq])
                        p2 = nps()
                        sc.op("tensor", lambda e, p2=p2: e.matmul(ps[:, p2, :], lhsT=ones64[:], rhs=xq[:], start=True, stop=True),
                              reads=[bcst, bxq], writes=[bps[p2]])
                        sc.op("scalar", lambda e, p2=p2: e.activation(out=rn[:], in_=ps[:, p2, :], func=AF.Sqrt, bias=epsg[:, 0:1],
                                                                      scale=1.0), reads=[bps[p2], bcst], writes=[brn])
                        sc.op("vector", lambda e: e.reciprocal(out=rn[:], in_=rn[:]), reads=[brn], writes=[brn])
                        dst, bd = (qnT, bqn) if j == 0 else (knT, bkn)
                        scl = 0.125 if j == 0 else 1.0
                        sc.op("vector", lambda e, xi=xi, dst=dst, tt=tt, scl=scl: e.scalar_tensor_tensor(
                            out=dst[:, tt * 512:(tt + 1) * 512], in0=xa[xi][:], scalar=scl, in1=rn[:], op0=ALU.mult, op1=ALU.mult),
                            reads=[bxa[xi], brn], writes=[bd])
                    if j >= 1:
                        srcT = knT[:, tt * 512:(tt + 1) * 512] if j == 1 else xa[xi][:]
                        bsrc = bkn if j == 1 else bxa[xi]
                        p3 = nps()
                        for cj in range(8):
                            sc.op("tensor", lambda e, srcT=srcT, cj=cj, p3=p3: e.transpose(ps[:, p3, cj * 64:(cj + 1) * 64],
                                                                                         srcT[:, cj * 64:(cj + 1) * 64], I64),
                                  reads=[bsrc, bcst], writes=[bps[p3]], inc=(cj == 7))
                        dstT, bdt = (Kt, bKt) if j == 1 else (Vt, bVt)
                        sc.op("vector", lambda e, dstT=dstT, tt=tt, p3=p3: e.tensor_copy(
                            out=dstT[:, tt * 8:(tt + 1) * 8, :], in_=ps[:, p3, :].rearrange("p (a b) -> p a b", b=64)),
                            reads=[bps[p3]], writes=[bdt])
            if fz is None:
                sc.dma("sync", lambda e, u=u, seg=seg: e.dma_start(
                    out=zseg[:], in_=gz_d[u, seg * SEGT:(seg + 1) * SEGT, :].rearrange("(n s) d -> s n d", s=64)),
                    writes=[bz], key=f"gu{u}" + "gz")
            else:
                def zsrc(e, u=u, seg=seg):
                    return fz.LZ[u, :, seg * SEGT:(seg + 1) * SEGT]
                sc.dma("gpsimd", lambda e, zsrc=zsrc: e.dma_start(out=zseg[:].rearrange("p a b -> p (a b)"), in_=zsrc(e)),
                       reads=[fz.bUr], writes=[bz], key=f"gu{u}" + "gz")
            if STOP <= 2:
                return [bz, bKt, bVt, bqn]
            for gi in range(SEGC // 8):
                l0 = gi * 8
                n0 = seg * SEGC + l0
                v3 = lambda t: t[:]
                pk, pd, pdt = nps(), nps(), nps()
                for j in range(8):
                    cs = slice((l0 + j) * 64, (l0 + j + 1) * 64)
                    sc.op("tensor", lambda e, j=j, cs=cs, pk=pk: e.matmul(ps[:, pk, j * 64:(j + 1) * 64], lhsT=knT[:, cs], rhs=knT[:, cs],
                                                                         start=True, stop=True), reads=[bkn], writes=[bps[pk]], inc=(j == 7))
                sc.op("vector", lambda e, n0=n0: e.tensor_tensor(out=rhsD[:], in0=MB[:], in1=bc_n(g_t, n0), op=ALU.mult),
                      reads=[bcst, bg], writes=[brhsD])
                if STOP <= 2.1:
                    return [brhsD, bps[pk]]
                sc.op("tensor", lambda e, pd=pd: e.matmul(ps[:, pd, :], lhsT=Tri, rhs=rhsD[:].rearrange("p a b -> p (a b)"),
                                                          start=True, stop=True), reads=[bcst, brhsD], writes=[bps[pd]])
                for j in range(8):
                    sc.op("tensor", lambda e, j=j, pdt=pdt: e.matmul(ps[:, pdt, j * 64:(j + 1) * 64], lhsT=rhsD[:, j, :], rhs=Tri,
                                                                    start=True, stop=True), reads=[bcst, brhsD], writes=[bps[pdt]], inc=(j == 7))
                r3 = lambda ap: ap.rearrange("p (a b) -> p a b", b=64)
                sc.op("scalar", lambda e, pd=pd: e.activation(out=ED[:], in_=r3(ps[:, pd, :]), func=AF.Exp), reads=[bps[pd]], writes=[bED])
                sc.op("scalar", lambda e, pdt=pdt: e.activation(out=EDT[:], in_=r3(ps[:, pdt, :]), func=AF.Exp), reads=[bps[pdt]], writes=[bEDT])
                if STOP <= 2.2:
                    return [bED, bEDT]
                sc.op("vector", lambda e, pk=pk: e.tensor_tensor(out=Lp[0][:], in0=r3(ps[:, pk, :]), in1=ED[:], op=ALU.mult),
                      reads=[bps[pk], bED], writes=[bL[0]])
                sc.op("vector", lambda e, n0=n0: e.tensor_tensor(out=Lp[0][:], in0=Lp[0][:], in1=bc_n(beta, n0), op=ALU.mult),
                      reads=[bL[0], bg], writes=[bL[0]])
                sc.op("vector", lambda e: e.tensor_tensor(out=Lp[0][:], in0=Lp[0][:], in1=MB[:], op=ALU.mult),
                      reads=[bL[0], bcst], writes=[bL[0]])
                if STOP <= 2.3:
                    return [bL[0]]
                pn = nps()
                for j in range(8):
                    sc.op("tensor", lambda e, j=j, pn=pn: e.matmul(ps[:, pn, j * 64:(j + 1) * 64], lhsT=Lp[0][:, j, :], rhs=I64,
                                                                  start=True, stop=True),
                          reads=[bL[0], bcst], writes=[bps[pn]], inc=(j == 7))
                sc.op("scalar", lambda e, pn=pn: e.copy(out=Np[0][:], in_=r3(ps[:, pn, :])), reads=[bps[pn]], writes=[bN[0]])
                sc.op("vector", lambda e: e.tensor_tensor(out=Pm[:], in0=IB[:], in1=Np[0][:], op=ALU.subtract),
                      reads=[bN[0], bcst], writes=[bP])
                if STOP <= 2.4:
                    return [bP, bN[0]]
                sc.op("gpsimd", lambda e: e.tensor_copy(out=Lb[0][:], in_=Lp[0][:]), reads=[bL[0]], writes=[bLb[0]])
                sc.op("gpsimd", lambda e: e.tensor_copy(out=Nb[0][:], in_=Np[0][:]), reads=[bN[0]], writes=[bNb[0]])
                sc.op("gpsimd", lambda e: e.tensor_copy(out=Pb[:], in_=Pm[:]), reads=[bP], writes=[bPb])
                cur = 0
                for lvl in range(5):
                    nxt = 1 - cur
                    pl = nps()
                    for j in range(8):
                        sc.op("tensor", lambda e, j=j, pl=pl, cur=cur: e.matmul(ps[:, pl, j * 64:(j + 1) * 64], lhsT=Nb[cur][:, j, :],
                                                                               rhs=Lb[cur][:, j, :], start=True, stop=True),
                              reads=[bNb[cur], bLb[cur]], writes=[bps[pl]], inc=(j == 7))
                    if lvl < 4:
                        pn2 = nps()
                        for j in range(8):
                            sc.op("tensor", lambda e, j=j, pn2=pn2, cur=cur: e.matmul(ps[:, pn2, j * 64:(j + 1) * 64], lhsT=Lb[cur][:, j, :],
                                                                                     rhs=Nb[cur][:, j, :], start=True, stop=True),
                                  reads=[bNb[cur], bLb[cur]], writes=[bps[pn2]], inc=(j == 7))
                    sc.op("scalar", lambda e, pl=pl, nxt=nxt: e.copy(out=Lb[nxt][:], in_=r3(ps[:, pl, :])), reads=[bps[pl]], writes=[bLb[nxt]])
                    if lvl < 4:
                        sc.op("vector", lambda e, pn2=pn2, nxt=nxt: e.tensor_copy(out=Nb[nxt][:], in_=r3(ps[:, pn2, :])),
                              reads=[bps[pn2]], writes=[bNb[nxt]])
                    pu = nps()
                    for j in range(8):
                        sc.op("tensor", lambda e, j=j, pu=pu, nxt=nxt: e.matmul(ps[:, pu, j * 64:(j + 1) * 64], lhsT=Lb[nxt][:, j, :],
                                                                               rhs=Pb[:, j, :], start=True, stop=True),
                              reads=[bLb[nxt], bPb], writes=[bps[pu]], inc=(j == 7))
                    sc.op("vector", lambda e, pu=pu: e.tensor_tensor(out=Pm[:], in0=Pm[:], in1=r3(ps[:, pu, :]), op=ALU.add),
                          reads=[bps[pu], bP], writes=[bP])
                    if lvl < 4:
                        sc.op("gpsimd", lambda e: e.tensor_copy(out=Pb[:], in_=Pm[:]), reads=[bP], writes=[bPb])
                    cur = nxt
                if STOP <= 2.5:
                    return [bP]
                sc.op("vector", lambda e, l0=l0, n0=n0: e.tensor_tensor(out=Kbg[:], in0=Kt[:, l0:l0 + 8, :], in1=bc_n(bgk, n0), op=ALU.mult),
                      reads=[bKt, bg], writes=[bKbg])
                sc.op("vector", lambda e, l0=l0, n0=n0: e.tensor_tensor(out=Vb[:], in0=Vt[:, l0:l0 + 8, :], in1=bc_n(beta, n0), op=ALU.mult),
                      reads=[bVt, bg], writes=[bVb])
                sc.op("vector", lambda e, l0=l0, n0=n0: e.tensor_tensor(out=kdec[:], in0=Kt[:, l0:l0 + 8, :], in1=bc_n(edec, n0), op=ALU.mult),
                      reads=[bKt, bg], writes=[bkdec])
                p_u, p_w, p_q = nps(), nps(), nps()
                for j in range(8):
                    sc.op("tensor", lambda e, j=j, p_u=p_u: e.matmul(ps[:, p_u, j * 64:(j + 1) * 64], lhsT=Pm[:, j, :], rhs=Vb[:, j, :],
                                                                    start=True, stop=True), reads=[bP, bVb], writes=[bps[p_u]], inc=(j == 7))
                for j in range(8):
                    sc.op("tensor", lambda e, j=j, p_w=p_w: e.matmul(ps[:, p_w, j * 64:(j + 1) * 64], lhsT=Kbg[:, j, :], rhs=Pm[:, j, :],
                                                                    start=True, stop=True), reads=[bP, bKbg], writes=[bps[p_w]], inc=(j == 7))
                for j in range(8):
                    cs = slice((l0 + j) * 64, (l0 + j + 1) * 64)
                    sc.op("tensor", lambda e, j=j, cs=cs, p_q=p_q: e.matmul(ps[:, p_q, j * 64:(j + 1) * 64], lhsT=knT[:, cs], rhs=qnT[:, cs],
                                                                           start=True, stop=True), reads=[bkn, bqn], writes=[bps[p_q]], inc=(j == 7))
                sc.op("scalar", lambda e, p_u=p_u: e.copy(out=u_sb[:], in_=r3(ps[:, p_u, :])), reads=[bps[p_u]], writes=[bu])
                sc.op("scalar", lambda e, p_w=p_w: e.copy(out=wT[:], in_=r3(ps[:, p_w, :])), reads=[bps[p_w]], writes=[bwT])
                sc.op("vector", lambda e, p_q=p_q: e.tensor_tensor(out=qkT[:], in0=r3(ps[:, p_q, :]), in1=EDT[:], op=ALU.mult),
                      reads=[bps[p_q], bEDT], writes=[bqkT])
                sc.op("vector", lambda e: e.tensor_tensor(out=qkT[:], in0=qkT[:], in1=TriB[:], op=ALU.mult), reads=[bqkT, bcst], writes=[bqkT])
                if STOP <= 3:
                    return [bqkT, bu, bwT]
                for j in range(8):
                    n = n0 + j
                    l = l0 + j
                    cs = slice(l * 64, (l + 1) * 64)
                    i2 = j % 2
                    bx_, by_ = SB0, SB0
                    sc.op("tensor", lambda e, j=j, bx_=bx_: e.matmul(ps[:, bx_, 0:64], lhsT=wT[:, j, :], rhs=St[:], start=True, stop=True),
                          reads=[bwT, bS], writes=[bps[bx_]])
                    sc.op("tensor", lambda e, cs=cs, by_=by_: e.matmul(ps[:, by_, 192:256], lhsT=qnT[:, cs], rhs=St[:], start=True, stop=True),
                          reads=[bqn, bS], writes=[bps[by_]])
                    sc.op("vector", lambda e, j=j, bx_=bx_, i2=i2: e.tensor_tensor(out=vn[i2][:], in0=u_sb[:, j, :], in1=ps[:, bx_, 0:64],
                                                                                 op=ALU.subtract), reads=[bu, bps[bx_]], writes=[bvn[i2]])
                    sc.op("vector", lambda e, by_=by_, i2=i2, n=n: e.tensor_scalar(out=As[i2][:], in0=ps[:, by_, 192:256], scalar1=egc[:, n:n + 1],
                                                                                  scalar2=None, op0=ALU.mult), reads=[bps[by_], bg], writes=[bAs[i2]])
                    sc.op("tensor", lambda e, j=j, bx_=bx_, i2=i2: e.matmul(ps[:, bx_, 64:128], lhsT=qkT[:, j, :], rhs=vn[i2][:],
                                                                          start=True, stop=True), reads=[bqkT, bvn[i2]], writes=[bps[bx_]], inc=False)
                    sc.op("tensor", lambda e, j=j, bx_=bx_, i2=i2: e.matmul(ps[:, bx_, 128:192], lhsT=kdec[:, j, :], rhs=vn[i2][:],
                                                                          start=True, stop=True), reads=[bkdec, bvn[i2]], writes=[bps[bx_]])
                    sc.op("vector", lambda e, bx_=bx_, n=n: e.scalar_tensor_tensor(out=St[:], in0=St[:], scalar=eglb[:, n:n + 1],
                                                                                  in1=ps[:, bx_, 128:192], op0=ALU.mult, op1=ALU.add),
                          reads=[bS, bg, bps[bx_]], writes=[bS])
                    sc.op("vector", lambda e, bx_=bx_, i2=i2, l=l: e.tensor_tensor(out=oseg[:, l, :], in0=As[i2][:], in1=ps[:, bx_, 64:128],
                                                                                 op=ALU.add), reads=[bAs[i2], bps[bx_]], writes=[boseg])
                if STOP <= 4:
                    return [boseg, bS]
            sc.op("gpsimd", lambda e: e.tensor_tensor(out=osq[:], in0=oseg[:], in1=oseg[:], op=ALU.mult), reads=[boseg], writes=[bosq])
            sc.op("vector", lambda e: e.tensor_reduce(out=oss[:], in_=osq[:], axis=AX.X, op=ALU.add), reads=[bosq], writes=[boss])
            sc.op("scalar", lambda e: e.activation(out=oss[:], in_=oss[:], func=AF.Sqrt, bias=epsg[:, 0:1], scale=1.0 / 64),
                  reads=[boss, bcst], writes=[boss])
            sc.op("vector", lambda e: e.reciprocal(out=oss[:], in_=oss[:]), reads=[boss], writes=[boss])
            sc.op("vector", lambda e: e.tensor_tensor(out=osq[:], in0=oseg[:], in1=oss[:].unsqueeze(2).to_broadcast([64, SEGC, 64]),
                                                      op=ALU.mult), reads=[boseg, boss], writes=[bosq])
            sc.op("gpsimd", lambda e: e.tensor_tensor(out=osq[:], in0=osq[:], in1=ngb[:].unsqueeze(1).to_broadcast([64, SEGC, 64]),
                                                      op=ALU.mult), reads=[bosq, bpar], writes=[bosq])
            sc.op("scalar", lambda e: e.activation(out=zseg[:], in_=zseg[:], func=AF.Silu), reads=[bz], writes=[bz])
            if fz is None:
                sc.op("vector", lambda e: e.tensor_tensor(out=osq[:], in0=osq[:], in1=zseg[:], op=ALU.mult), reads=[bosq, bz], writes=[bosq])
                sc.dma("sync", lambda e, u=u, seg=seg: e.dma_start(
                    out=go_d[u, seg * SEGT:(seg + 1) * SEGT, :].rearrange("(n s) d -> s n d", s=64), in_=osq[:]),
                    reads=[bosq], key=f"gu{u}" + "go")
            else:
                oT = oseg[:].rearrange("p a b -> p (a b)")
                zT = zseg[:].rearrange("p a b -> p (a b)")
                for g4 in range(SEGC // 8):
                    pt_ = nps()
                    for j in range(8):
                        sc.op("tensor", lambda e, j=j, g4=g4, pt_=pt_: e.transpose(ps[:, pt_, j * 64:(j + 1) * 64], osq[:, g4 * 8 + j, :], I64),
                              reads=[bosq, bcst], writes=[bps[pt_]], inc=(j == 7))
                    sc.op("vector", lambda e, g4=g4, pt_=pt_: e.tensor_tensor(out=oT[:, g4 * 512:(g4 + 1) * 512], in0=ps[:, pt_, :],
                                                                            in1=zT[:, g4 * 512:(g4 + 1) * 512], op=ALU.mult),
                          reads=[bps[pt_], bz, bosq], writes=[boseg])
                tok0 = seg * SEGT
                kblk, coff = tok0 // 1024, tok0 % 1024
                row0 = (kblk // 2) * 512 + 192 + (u * 2 + kblk % 2) * 64
                sc.dma("sync", lambda e, row0=row0, coff=coff: e.dma_start(out=fz.Ysend[row0:row0 + 64, coff:coff + SEGT],
                                                                          in_=oT[:, 0:SEGT]),
                       reads=[boseg, fz.bYs], key=f"gu{u}" + "go")
        return [bosq, boseg]

    class _Rec:
        def __init__(self):
            self.calls = []

        def op(self, *a, **k):
            self.calls.append(("op", a, k))

        def dma(self, *a, **k):
            self.calls.append(("dma", a, k))

    outs = []
    recs = []
    for u in range(nu):
        r = _Rec()
        outs += emit_unit(u, r)
        recs.append(r.calls)
    n = max(len(c) for c in recs)
    for i in range(n):
        for c in recs:
            if i < len(c):
                kind, a, k = c[i]
                getattr(sc, kind)(*a, **k)
    return outs


def build_gdn(nu=2):
    P = SimpleProg()
    sc = Sched(P.nc, P.es)
    ob = gdn_emit(P, sc, nu)
    return P, P.finish(sc, ob)


def gdn_const_inputs():
    i = np.arange(64)
    tri = (i[:, None] <= i[None, :]).astype(np.float32)
    ms = (i[:, None] > i[None, :]).astype(np.float32)
    return dict(gcst=np.stack([tri, ms, np.eye(64, dtype=np.float32)]))


def gdn_unit_inputs(ug, h, gdn_conv_w, a_log, dt_bias, norm_g):
    GW = 384
    raw = np.zeros((3, 64, S + 3), np.float32)
    cw = np.zeros((64, 12), np.float32)
    for j in range(3):
        cols = slice(j * GW + h * 64, j * GW + (h + 1) * 64)
        raw[j, :, 3:] = ug[:, cols].T
        cw[:, j * 4:(j + 1) * 4] = gdn_conv_w[:, cols].T
    z = np.ascontiguousarray(ug[:, 3 * GW + h * 64:3 * GW + (h + 1) * 64])
    a = np.ascontiguousarray(ug[:, 4 * GW + h].reshape(NCH, 64).T)
    b = np.ascontiguousarray(ug[:, 4 * GW + 6 + h].reshape(NCH, 64).T)
    par = np.zeros((64, 2), np.float32)
    par[:, 0] = a_log[h]
    par[:, 1] = dt_bias[h]
    ng = np.ascontiguousarray(np.broadcast_to(norm_g[None, :], (64, 64))).astype(np.float32)
    return dict(graw=raw, gcw=cw, gz=z, ga=a, gb=b, gpar=par, gng=ng)


def _lay(v):
    return np.ascontiguousarray(np.asarray(v, np.float32).reshape(-1, 128).T)


_PROGS = {}


def _prog(key, builder):
    if key not in _PROGS:
        _PROGS[key] = builder()
    return _PROGS[key]


def _run(nc, in_maps):
    res = run_bass_kernel_spmd(nc, in_maps, core_ids=list(range(NCORES)))
    return res.results


def _tok_launch(key, stages, inp, xT_list, yT_list=None):
    def mk():
        p = TokProg(stages)
        return p, p.build()
    P, nc = _prog(key, mk)
    maps = []
    for c in range(NCORES):
        b = c // 4
        m = {"xT": xT_list[c], "cT": _lay(inp["c"][b])}
        for name in P.in_names:
            if name in m:
                continue
            if name.startswith("yT"):
                m[name] = yT_list[c]
            elif name == "final_g":
                m[name] = _lay(inp["final_g"])
            else:
                base, l = name[:-1], int(name[-1])
                arr = np.asarray(inp[base][l], np.float32)
                if base == "b_ada" or base.startswith("ln_"):
                    arr = _lay(arr)
                m[name] = np.ascontiguousarray(arr)
        maps.append(m)
    return _run(nc, maps)


def _mixer(inp, l, u):
    y = np.zeros((B, S, D), np.float32)
    P, nc = _prog("conv", build_conv)
    maps = []
    for c in range(NCORES):
        b, j = c // 4, c % 4
        maps.append(conv_inputs(u[b], j, np.asarray(inp["conv_w"][l]), np.asarray(inp["conv_b"][l]),
                                np.asarray(inp["conv_ln_g"][l]), np.asarray(inp["conv_ln_b"][l])))
    res = _run(nc, maps)
    for c in range(NCORES):
        b, j = c // 4, c % 4
        y[b, j * TOK:(j + 1) * TOK, 0:256] = res[c]["ycT"].T
    P, nc = _prog("moba", lambda: build_moba(3))
    tab = rope_tables()
    shared = moba_shared_inputs(tab)
    consts = [moba_const_inputs(0), moba_const_inputs(1)]
    maps = []
    for c in range(NCORES):
        b, cc = c // 4, c % 4
        units = []
        for s in range(3):
            combo = 3 * cc + s
            h, half = combo // 2, combo % 2
            q = u[b, :, 512 + h * 64:512 + (h + 1) * 64]
            k = u[b, :, 512 + 384 + h * 64:512 + 384 + (h + 1) * 64]
            v = u[b, :, 512 + 768 + h * 64:512 + 768 + (h + 1) * 64]
            d = moba_unit_inputs(q, k, v, half, tab)
            d.update(consts[half])
            units.append(d)
        m = {k_: np.ascontiguousarray(np.stack([un[k_] for un in units])) for k_ in units[0]}
        m.update(shared)
        maps.append(m)
    res = _run(nc, maps)
    for c in range(NCORES):
        b, cc = c // 4, c % 4
        for s in range(3):
            combo = 3 * cc + s
            h, half = combo // 2, combo % 2
            qpos = np.concatenate([np.arange(bl * 256, (bl + 1) * 256) for bl in HALF_BLOCKS[half]])
            y[b, qpos, 256 + h * 64:256 + (h + 1) * 64] = res[c]["moT"][s].T
    P, nc = _prog("gdn", lambda: build_gdn(2))
    gconst = gdn_const_inputs()
    allu = [(b, h) for b in range(B) for h in range(6)]
    maps = []
    assign = []
    for c in range(NCORES):
        us = [allu[i] if i < len(allu) else allu[0] for i in (2 * c, 2 * c + 1)]
        assign.append([(i < len(allu)) for i in (2 * c, 2 * c + 1)])
        units = [gdn_unit_inputs(u[b, :, 512 + 1152:], h, np.asarray(inp["gdn_conv_w"][l]), np.asarray(inp["gdn_a_log"][l]),
                                 np.asarray(inp["gdn_dt_bias"][l]), np.asarray(inp["gdn_norm_g"][l])) for (b, h) in us]
        m = {k_: np.ascontiguousarray(np.stack([un[k_] for un in units])) for k_ in units[0]}
        m.update(gconst)
        maps.append(m)
    res = _run(nc, maps)
    for c in range(NCORES):
        for s in range(2):
            i = 2 * c + s
            if i < len(allu):
                b, h = allu[i]
                y[b, :, 640 + h * 64:640 + (h + 1) * 64] = res[c]["go"][s]
    return y


YROWS = 768 + 1024
RG = [[0, 1, 2, 3], [4, 5, 6, 7]]


def moba_unit(cc, su):
    return (cc, su) if su < 2 else (4 + cc // 2, cc % 2)


def moba_owner(h, half):
    return (h, half) if h < 4 else (2 * (h - 4) + half, 2)


class Fused:
    def __init__(self):
        self.nc = bass.Bass("TRN2", target_bir_lowering=False)
        self.es = ExitStack()
        self.cur = self.es
        self.dins = {}
        self.in_names = []
        self.out_names = []
        self.phase_i = 0
        self.load_x = False
        self.store_x = False
        self._dyn = {}

    def din(self, name, shape, dt=F32):
        if name not in self.dins:
            self.in_names.append(name)
            self.dins[name] = self.nc.dram_tensor(name, list(shape), dt, kind="ExternalInput").ap()
        return self.dins[name]

    def dout(self, name, shape, dt=F32):
        if name not in self.dins:
            self.out_names.append(name)
            self.dins[name] = self.nc.dram_tensor(name, list(shape), dt, kind="ExternalOutput").ap()
        return self.dins[name]

    AW = 36800

    def sb(self, name, shape, dt):
        p = shape[0]
        n = int(np.prod(shape[1:]))
        n32 = n if dt == F32 else (n + 1) // 2
        n32 = (n32 + 7) // 8 * 8
        off = self.aoff
        self.aoff += n32
        assert self.aoff <= self.AW, (name, self.aoff)
        v = self.arena[0:p, off:off + n32]
        if dt != F32:
            v = v.bitcast(dt)
        v = v[:, 0:n]
        if len(shape) == 3:
            v = v.rearrange("p (a b) -> p a b", a=shape[1])
        return v

    def dyn(self, e, engname, key):
        c = self._dyn.setdefault(engname, {})
        if "cc" not in c:
            c["cc"] = e.snap(e.partition_id() % 4)
        if key not in c:
            cc = c["cc"]
            doff = lambda h: (h // 2) * 512 + (h % 2) * 64
            v = {"c2048": lambda: cc * 2048, "prev": lambda: (cc + 3) % 4,
                 "D0": lambda: doff(cc), "D2": lambda: (cc // 2) * 64 + 1024, "mha2": lambda: cc % 2, "mhb2": lambda: 3 - cc % 2,
                 "gh1": lambda: (cc + 4) % 6, "Dg1": lambda: doff((cc + 4) % 6)}[key]()
            c[key] = e.snap(v)
        return c[key]

    def build(self):
        nc, es = self.nc, self.es
        sc = self.sc = Sched(nc, es)
        self.x = es.enter_context(nc.sbuf_tensor("x_res", [128, KC, TOK], F32))
        self.arena = es.enter_context(nc.sbuf_tensor("arena", [128, self.AW], F32))
        self.aoff = 0
        self.bx = [[Buf(f"x{c}_{t}") for t in range(TOK // 512)] for c in range(KC)]
        self.ps = es.enter_context(nc.psum_tensor("ps_all", [128, 8, 512], F32))
        NUC = (DIN + 127) // 128
        Usend_t = nc.dram_tensor("Usend", [NUC * 128, TOK], F32)
        Urecv_t = nc.dram_tensor("Urecv", [NUC * 512 + 128, TOK], F32)
        Ysend_t = nc.dram_tensor("Ysend", [2048, 1024], F32)
        Yrecv_t = nc.dram_tensor("Yrecv", [8192, 1024], F32)
        Yfull_t = nc.dram_tensor("Yfull", [D, TOK], F32)
        self.Usend, self.Urecv, self.Ysend, self.Yrecv, self.Yfull = (t.ap() for t in (Usend_t, Urecv_t, Ysend_t, Yrecv_t, Yfull_t))
        self.LK = nc.dram_tensor("LK", [3, 64, S], F32).ap()
        self.LV = nc.dram_tensor("LV", [3, 64, S], F32).ap()
        self.LQ = nc.dram_tensor("LQ", [3, 64, MOBA_SLOTS * 256], F32).ap()
        self.LQF = nc.dram_tensor("LQF", [3, 64, S], F32).ap()
        self.LG = nc.dram_tensor("LG", [2, 3, 64, S], F32).ap()
        self.LZ = nc.dram_tensor("LZ", [2, 64, S], F32).ap()
        self.LAB = nc.dram_tensor("LAB", [2, 2, S], F32).ap()
        self.LH = nc.dram_tensor("LH", [512, CH], F32).ap()
        self.Yloc = nc.dram_tensor("Yloc", [4, 512, 1024], F32).ap()
        self.uT_dst = self.Usend
        self.yT_src = self.Yfull
        self.bU, self.bUr, self.bYs, self.bYr, self.bY = Buf("U"), Buf("Ur"), Buf("Ys"), Buf("Yr"), Buf("Yf")
        self.bL, self.bLq, self.bYl = Buf("L"), Buf("Lq"), Buf("Yl")
        self.bUr_m, self.bUr_g = Buf("Ur_m"), Buf("Ur_g")
        outb = []

        def run_phase(fn):
            self.aoff = 0
            r = fn()
            sc.barrier(exclude=("agUm", "agUg"))
            self.phase_i += 1
            return r

        def tok_phase(stages, load_x=False, store_x=False, ag=True):
            def fn():
                self.load_x, self.store_x = load_x, store_x
                r = TokProg(stages, fused=self).build()
                if ag:
                    order = list(range(4, 13)) + list(range(13, NUC)) + list(range(0, 4))
                    waits = sc._deps("gpsimd", (), [self.bU, self.bUr_m, self.bUr_g])
                    sc.q["gpsimd"].append((waits, None, None))
                    for ci in order:
                        key = "agUm" if 4 <= ci < 13 else "agUg"
                        sc._get_dsem(key)
                        sc.cckeys.add(key)
                        sc.dcnt[key] += 1
                        sc.q["gpsimd"].append(([], (lambda e, ci=ci: e.collective_compute(
                            "AllGather", ALU.bypass, replica_groups=RG,
                            ins=[Usend_t.ap()[ci * 128:(ci + 1) * 128, :]], outs=[Urecv_t.ap()[ci * 512:(ci + 1) * 512, :]])),
                            ("c", key, sc.dcnt[key])))
                    for b, key in ((self.bUr_m, "agUm"), (self.bUr_g, "agUg"), (self.bU, "agUg")):
                        b.lw = ("c", key, sc.dcnt[key])
                        b.rd = {}
                return r
            return run_phase(fn)

        def y_exchange():
            for ci in range(8):
                sc.cc(lambda e, ci=ci: e.collective_compute(
                    "AllGather", ALU.bypass, replica_groups=RG,
                    ins=[Ysend_t.ap()[ci * 256:(ci + 1) * 256, :]], outs=[Yrecv_t.ap()[ci * 1024:(ci + 1) * 1024, :]]),
                    writes=[self.bYs, self.bYr], key="agY")
            LB = ([0, 3, 4, 7], [1, 2, 5, 6])
            for c2 in range(2):
                sc.dma("scalar", lambda e, c2=c2: e.dma_start(
                    out=self.Yloc[:, c2 * 256:(c2 + 1) * 256, :],
                    in_=self.Yrecv[c2 * 1024:c2 * 1024 + 7168, :][bass.ds(self.dyn(e, "scalar", "c2048"), 1024), :].rearrange(
                        "(r f) t -> r f t", r=4)),
                    reads=[self.bYr], writes=[self.bYl], key="yloc")
            for h in range(6):
                for half in range(2):
                    rs, su = moba_owner(h, half)
                    for q4 in range(4):
                        lb = LB[half][q4]
                        sc.dma("sync", lambda e, h=h, lb=lb, rs=rs, su=su, q4=q4: e.dma_start(
                            out=self.Yfull[256 + h * 64:256 + (h + 1) * 64, lb * 256:(lb + 1) * 256],
                            in_=self.Yloc[rs, su * 64:(su + 1) * 64, q4 * 256:(q4 + 1) * 256]),
                            reads=[self.bYl, self.bY], key="yasm")
            for h in range(6):
                rs, g = (h, 0) if h < 4 else (h - 4, 1)
                for kk in range(2):
                    r0 = 192 + (g * 2 + kk) * 64
                    sc.dma("sync", lambda e, h=h, kk=kk, rs=rs, r0=r0: e.dma_start(
                        out=self.Yfull[640 + h * 64:640 + (h + 1) * 64, kk * 1024:(kk + 1) * 1024],
                        in_=self.Yloc[rs, r0:r0 + 64, :]),
                        reads=[self.bYl, self.bY], key="yasm")

        def localize_m():
            Ur = self.Urecv
            rk = lambda ap: ap.rearrange("d (r t) -> d r t", r=4)
            LQF = self.LQF

            def blk(e, q, dkey, B, n=64):
                R0 = (B // 128) * 512 + B % 128
                R1 = min(R0 + 2048, NUC * 512 + 128)
                return Ur[R0:R1, :][bass.ds(self.dyn(e, q, dkey), 512), :].rearrange("(r f) t -> f r t", r=4)[0:n]

            for u in range(3):
                q = "sync" if u < 2 else "scalar"
                dk = "D0" if u < 2 else "D2"
                bqf = Buf()
                for (dst, B) in ((LQF, 512), (self.LK, 896), (self.LV, 1280)):
                    sc.dma(q, lambda e, u=u, q=q, dk=dk, dst=dst, B=B: e.dma_start(out=rk(dst[u]), in_=blk(e, q, dk, B)),
                           reads=[self.bUr_m], writes=[bqf if B == 512 else Buf()], key=(f"locqf{u}" if B == 512 else f"loc{q}{u}"))
                for ab in range(2):
                    dstq = self.LQ[u].rearrange("d (G ab i) -> d G ab i", G=8, ab=2)[:, :, ab:ab + 1, :]
                    srcv = LQF[u].rearrange("d (G b i) -> d G b i", G=8, b=4)
                    if u < 2:
                        b = u if ab == 0 else 3 - u
                        sc.dma(q, lambda e, dstq=dstq, srcv=srcv, b=b: e.dma_start(out=dstq, in_=srcv[:, :, b:b + 1, :]),
                               reads=[bqf], writes=[Buf()], key=f"locq{u}")
                    else:
                        kn = "mha2" if ab == 0 else "mhb2"
                        sc.dma(q, lambda e, dstq=dstq, srcv=srcv, kn=kn: e.dma_start(
                            out=dstq, in_=srcv[:, :, bass.ds(self.dyn(e, "scalar", kn), 1), :]),
                            reads=[bqf], writes=[Buf()], key=f"locq{u}")

        def localize_g():
            Ur = self.Urecv
            rk = lambda ap: ap.rearrange("d (r t) -> d r t", r=4)
            LQF = self.LQF

            def blk(e, q, dkey, B, n=64):
                R0 = (B // 128) * 512 + B % 128
                R1 = min(R0 + 2048, NUC * 512 + 128)
                return Ur[R0:R1, :][bass.ds(self.dyn(e, q, dkey), 512), :].rearrange("(r f) t -> f r t", r=4)[0:n]

            for u in range(2):
                dk, gk = ("D0", "cc") if u == 0 else ("Dg1", "gh1")
                for j in range(3):
                    sc.dma("gpsimd", lambda e, u=u, j=j, dk=dk: e.dma_start(
                        out=rk(self.LG[u, j]), in_=blk(e, "gpsimd", dk, 1664 + j * 384)),
                        reads=[self.bUr_g], writes=[Buf()], key="locg")
                q2 = "gpsimd" if u == 0 else "sync"
                sc.dma(q2, lambda e, u=u, dk=dk, q2=q2: e.dma_start(out=rk(self.LZ[u]), in_=blk(e, q2, dk, 2816)),
                       reads=[self.bUr_g], writes=[Buf()], key=f"locz{u}")
                for ab, B in ((0, 3200), (1, 3206)):
                    sc.dma(q2, lambda e, u=u, ab=ab, B=B, gk=gk, q2=q2: e.dma_start(
                        out=self.LAB[u, ab:ab + 1].rearrange("o (r t) -> o r t", r=4), in_=blk(e, q2, gk, B, 1)),
                        reads=[self.bUr_g], writes=[Buf()], key=f"locz{u}")
            sc.dma("scalar", lambda e: e.dma_start(
                out=self.LH.rearrange("(c f) t -> c f t", c=4),
                in_=Ur[0:2048, TOK - CH:TOK].rearrange("(c r f) t -> r c f t", r=4, f=128)[bass.ds(self.dyn(e, "scalar", "prev"), 1)]),
                reads=[self.bUr_g], writes=[Buf()], key="loch")

        def mixer(l):
            pfx = f"L{l}_"
            run_phase(localize_m)

            def mg():
                moba_emit(self, sc, 3, pfx, fz=self)
                localize_g()
            run_phase(mg)
            run_phase(lambda: conv_emit(self, sc, pfx, fz=self))

            def g():
                gdn_emit(self, sc, 2, pfx, fz=self)
                y_exchange()
            run_phase(g)

        zpad = self.din("zpad", [NUC * 128 - DIN, TOK])
        sc.dma("sync", lambda e: e.dma_start(out=self.Usend[DIN:NUC * 128, :], in_=zpad[:, :]), reads=[self.bU], key="zpad")
        tok_phase([("ffn1", 0), ("uproj", 0)], load_x=True)
        mixer(0)
        tok_phase([("wout", 0), ("ffn2", 0), ("ffn1", 1), ("uproj", 1)])
        mixer(1)
        outb = tok_phase([("wout", 1), ("ffn2", 1), ("final",)], store_x=True, ag=False)
        with nc.Block() as block:
            sc.emit(block)
        es.close()
        return nc


_FUSED = {}


def kernel(**inp):
    if "p" not in _FUSED:
        F = Fused()
        _FUSED["p"] = (F, F.build())
    F, nc = _FUSED["p"]
    x = np.asarray(inp["x"], np.float32)
    tab = rope_tables()
    shared = moba_shared_inputs(tab)
    mconst = [moba_const_inputs(0), moba_const_inputs(1)]
    gconst = gdn_const_inputs()
    wnames = ("w_ada", "ffn1_w_gate", "ffn1_w_up", "ffn1_w_down", "w_in", "w_out", "ffn2_w_gate", "ffn2_w_up", "ffn2_w_down")
    lnames = ("b_ada", "ln_ffn1_g", "ln_mix_g", "ln_ffn2_g")
    common = {}
    for l in range(2):
        for n in wnames:
            common[f"{n}{l}"] = np.ascontiguousarray(np.asarray(inp[n][l], np.float32))
        for n in lnames:
            common[f"{n}{l}"] = _lay(inp[n][l])
        cw = np.asarray(inp["conv_w"][l], np.float32)
        lay2 = lambda v: np.ascontiguousarray(np.asarray(v, np.float32).reshape(2, 128).T)
        common[f"L{l}_cw"] = np.ascontiguousarray(cw.T.reshape(2, 128, 31).transpose(1, 0, 2))
        common[f"L{l}_cp"] = np.ascontiguousarray(np.stack([lay2(inp["conv_b"][l]), lay2(inp["conv_ln_g"][l]),
                                                            lay2(inp["conv_ln_b"][l])], axis=-1))
    common["final_g"] = _lay(inp["final_g"])
    common["cident"] = np.eye(128, dtype=np.float32)
    common["zpad"] = np.zeros((((DIN + 127) // 128) * 128 - DIN, TOK), np.float32)
    common.update(shared)
    common.update(gconst)
    maps = []
    for c in range(NCORES):
        b, cc = c // 4, c % 4
        m = dict(common)
        m["xT"] = np.ascontiguousarray(x[b, cc * TOK:(cc + 1) * TOK].T)
        m["cT"] = _lay(inp["c"][b])
        m["cflag"] = np.full((128, 1), 0.0 if cc == 0 else 1.0, np.float32)
        units = []
        for su in range(3):
            half = moba_unit(cc, su)[1]
            qpos = np.concatenate([np.arange(bl * 256, (bl + 1) * 256) for bl in HALF_BLOCKS[half]])
            d = dict(mconst[half])
            d["ropeq"] = np.ascontiguousarray(tab[:, :, qpos])
            units.append(d)
        for k_ in units[0]:
            m[k_] = np.ascontiguousarray(np.stack([un[k_] for un in units]))
        for l in range(2):
            heads = [cc, (cc + 4) % 6]
            gw = np.asarray(inp["gdn_conv_w"][l], np.float32)
            gcw = np.zeros((2, 64, 12), np.float32)
            gpar = np.zeros((2, 64, 2), np.float32)
            gng = np.zeros((2, 64, 64), np.float32)
            for g, h in enumerate(heads):
                for j in range(3):
                    gcw[g, :, j * 4:(j + 1) * 4] = gw[:, j * 384 + h * 64:j * 384 + (h + 1) * 64].T
                gpar[g, :, 0] = np.asarray(inp["gdn_a_log"][l], np.float32)[h]
                gpar[g, :, 1] = np.asarray(inp["gdn_dt_bias"][l], np.float32)[h]
                gng[g] = np.asarray(inp["gdn_norm_g"][l], np.float32)[None, :]
            m[f"L{l}_gcw"], m[f"L{l}_gpar"], m[f"L{l}_gng"] = gcw, gpar, gng
        maps.append({k_: m[k_] for k_ in F.in_names})
    res = run_bass_kernel_spmd(nc, maps, core_ids=list(range(NCORES))).results
    out = np.zeros((B, S, D), np.float32)
    for c in range(NCORES):
        out[c // 4, (c % 4) * TOK:(c % 4 + 1) * TOK] = res[c]["xoT"].T
    return out
```
